# Optimizing a Trainium2 kernel written in Bass

```python
import math
import jax, jax.numpy as jnp
from jax import lax
import numpy as np

D_MODEL = 1024
BATCH = 8
SEQ = 2048
DEPTH = 2

ATT_HEAD_DIM = 64
ATT_HEADS = D_MODEL // ATT_HEAD_DIM
ATT_WIDTH = ATT_HEADS * ATT_HEAD_DIM
ROT_DIM = ATT_HEAD_DIM // 4
ROPE_THETA = 500000.0
MOBA_BLOCK = 256
MOBA_TOPK = 3
MOBA_Q_CHUNK = 16

SSM_EXPAND = 2
D_INNER = SSM_EXPAND * D_MODEL
SSM_HEAD_DIM = 64
SSM_HEADS = D_INNER // SSM_HEAD_DIM
SSM_GROUPS = 8
SSM_STATE = 128
SSM_CONV = 4
SSM_CHUNK = 256
XBC_DIM = D_INNER + 2 * SSM_GROUPS * SSM_STATE

D_FF = 4 * D_MODEL

ALPHA = (2 * DEPTH) ** 0.25
BETA = (8 * DEPTH) ** -0.25
LN_EPS = 1e-5
RMS_EPS = 1e-5

IN_SPLITS = (ATT_WIDTH, ATT_WIDTH, ATT_WIDTH, D_INNER, XBC_DIM, SSM_HEADS, D_MODEL, D_MODEL)
IN_WIDTH = sum(IN_SPLITS)

kernel_name = 'moba_ssd_gated_hybrid_deepnorm'


def _split(a, sizes):
    offsets = [int(o) for o in np.cumsum(sizes)[:-1]]
    return jnp.split(a, offsets, axis=-1)


def _pad_seq(t, s_pad):
    widths = [(0, 0)] * t.ndim
    widths[1] = (0, s_pad - t.shape[1])
    return jnp.pad(t, widths)


def layer_norm(x, g, b):
    xf = x.astype(jnp.float32)
    mu = xf.mean(-1, keepdims=True)
    var = jnp.square(xf - mu).mean(-1, keepdims=True)
    return ((xf - mu) * lax.rsqrt(var + LN_EPS) * g.astype(jnp.float32) + b.astype(jnp.float32)).astype(x.dtype)


def partial_rope(t, positions):
    half = ROT_DIM // 2
    inv_freq = ROPE_THETA ** (-jnp.arange(0, ROT_DIM, 2, dtype=jnp.float32) / ROT_DIM)
    ang = positions.astype(jnp.float32)[..., None] * inv_freq
    cos = jnp.cos(ang)[:, :, None, :]
    sin = jnp.sin(ang)[:, :, None, :]
    tf = t.astype(jnp.float32)
    x1, x2 = tf[..., :half], tf[..., half:ROT_DIM]
    out = jnp.concatenate([x1 * cos - x2 * sin, x2 * cos + x1 * sin, tf[..., ROT_DIM:]], axis=-1)
    return out.astype(t.dtype)


def moba_attention(q, k, v):
    bsz, nh, s, hd = q.shape
    nblk = -(-s // MOBA_BLOCK)
    s_pad = nblk * MOBA_BLOCK
    pad = ((0, 0), (0, 0), (0, s_pad - s), (0, 0))
    q, k, v = jnp.pad(q, pad), jnp.pad(k, pad), jnp.pad(v, pad)
    kb = k.reshape(bsz, nh, nblk, MOBA_BLOCK, hd)
    vb = v.reshape(bsz, nh, nblk, MOBA_BLOCK, hd)
    k_mean = kb.astype(jnp.float32).mean(axis=3)
    gate = jnp.einsum('bhsd,bhnd->bhsn', q.astype(jnp.float32), k_mean)
    q_blk = jnp.arange(s_pad) // MOBA_BLOCK
    past = jnp.arange(nblk)[None, :] < q_blk[:, None]
    gate = jnp.where(past, gate, -jnp.inf)
    topk = min(MOBA_TOPK, nblk)
    _, sel = lax.top_k(gate, topk)
    nq = s_pad // MOBA_Q_CHUNK
    q_c = jnp.moveaxis(q.reshape(bsz, nh, nq, MOBA_Q_CHUNK, hd), 2, 0)
    sel_c = jnp.moveaxis(sel.reshape(bsz, nh, nq, MOBA_Q_CHUNK, topk), 2, 0)
    bi = jnp.arange(bsz)[:, None, None, None]
    hi = jnp.arange(nh)[None, :, None, None]
    scale = hd ** -0.5
    n_sel = topk * MOBA_BLOCK

    def attend(args):
        ci, qc, selc = args
        qpos = ci * MOBA_Q_CHUNK + jnp.arange(MOBA_Q_CHUNK)
        own = (ci * MOBA_Q_CHUNK) // MOBA_BLOCK
        k_sel = kb[bi, hi, selc]
        v_sel = vb[bi, hi, selc]
        k_own = lax.dynamic_index_in_dim(kb, own, axis=2, keepdims=False)
        v_own = lax.dynamic_index_in_dim(vb, own, axis=2, keepdims=False)
        s_sel = jnp.einsum('bhqd,bhqtkd->bhqtk', qc, k_sel).reshape(bsz, nh, MOBA_Q_CHUNK, n_sel)
        s_own = jnp.einsum('bhqd,bhkd->bhqk', qc, k_own)
        valid_sel = jnp.arange(topk)[None, :] < (qpos // MOBA_BLOCK)[:, None]
        valid_sel = jnp.repeat(valid_sel, MOBA_BLOCK, axis=1)
        valid_own = (own * MOBA_BLOCK + jnp.arange(MOBA_BLOCK))[None, :] <= qpos[:, None]
        valid = jnp.concatenate([valid_sel, valid_own], axis=1)
        scores = jnp.concatenate([s_sel, s_own], axis=-1).astype(jnp.float32) * scale
        probs = jax.nn.softmax(jnp.where(valid, scores, -jnp.inf), axis=-1)
        p_sel = probs[..., :n_sel].reshape(bsz, nh, MOBA_Q_CHUNK, topk, MOBA_BLOCK).astype(v.dtype)
        p_own = probs[..., n_sel:].astype(v.dtype)
        return (jnp.einsum('bhqtk,bhqtkd->bhqd', p_sel, v_sel)
                + jnp.einsum('bhqk,bhkd->bhqd', p_own, v_own))

    out = lax.map(attend, (jnp.arange(nq), q_c, sel_c))
    out = jnp.moveaxis(out, 0, 2).reshape(bsz, nh, s_pad, hd)
    return out[:, :, :s]


def causal_dwconv(u, w, bias):
    kw = w.shape[0]
    out = lax.conv_general_dilated(
        u, w[:, None, :].astype(u.dtype), window_strides=(1,), padding=[(kw - 1, 0)],
        dimension_numbers=('NWC', 'WIO', 'NWC'), feature_group_count=u.shape[-1])
    return out + bias.astype(u.dtype)


def ssd_chunked(xs, dt, a, bm, cm):
    f32 = jnp.float32
    bsz, l, h, p = xs.shape
    g, n = bm.shape[2], bm.shape[3]
    hg = h // g
    q = SSM_CHUNK
    nc = l // q
    xdt = (xs.astype(f32) * dt[..., None]).reshape(bsz, nc, q, g, hg, p)
    a_dt = (dt * a).reshape(bsz, nc, q, g, hg).transpose(0, 1, 3, 4, 2)
    a_cs = jnp.cumsum(a_dt, axis=-1)
    bc = bm.astype(f32).reshape(bsz, nc, q, g, n)
    cc = cm.astype(f32).reshape(bsz, nc, q, g, n)
    causal = jnp.tril(jnp.ones((q, q), dtype=bool))
    seg = a_cs[..., :, None] - a_cs[..., None, :]
    decay = jnp.where(causal, jnp.exp(jnp.where(causal, seg, 0.0)), 0.0)
    cb = jnp.einsum('bclgn,bcsgn->bcgls', cc, bc)
    y_diag = jnp.einsum('bcgkls,bcsgkp->bclgkp', cb[:, :, :, None] * decay, xdt)
    decay_states = jnp.exp(a_cs[..., -1:] - a_cs)
    xw = xdt * decay_states.transpose(0, 1, 4, 2, 3)[..., None]
    states = jnp.einsum('bcsgn,bcsgkp->bcgkpn', bc, xw)
    chunk_decay = jnp.exp(a_cs[..., -1])

    def step(carry, inp):
        st, dec = inp
        return carry * dec[..., None, None] + st, carry

    init = jnp.zeros((bsz, g, hg, p, n), f32)
    _, prev = lax.scan(step, init, (jnp.moveaxis(states, 1, 0), jnp.moveaxis(chunk_decay, 1, 0)))
    prev = jnp.moveaxis(prev, 0, 1)
    y_off = (jnp.einsum('bclgn,bcgkpn->bclgkp', cc, prev)
             * jnp.exp(a_cs).transpose(0, 1, 4, 2, 3)[..., None])
    return (y_diag + y_off).reshape(bsz, l, h, p)


def gated_rmsnorm(y, z, w):
    hf = y.astype(jnp.float32) * jax.nn.silu(z.astype(jnp.float32))
    hgp = hf.reshape(*hf.shape[:-1], SSM_GROUPS, D_INNER // SSM_GROUPS)
    hgp = hgp * lax.rsqrt(jnp.mean(jnp.square(hgp), axis=-1, keepdims=True) + RMS_EPS)
    return hgp.reshape(hf.shape) * w.astype(jnp.float32)


def mamba2_branch(z, xbc, dt_raw, conv_w, conv_b, dt_bias, a_log, d_skip, ssm_norm_w):
    bsz, s, _ = xbc.shape
    xbc = jax.nn.silu(causal_dwconv(xbc, conv_w, conv_b))
    xs, bm, cm = _split(xbc, (D_INNER, SSM_GROUPS * SSM_STATE, SSM_GROUPS * SSM_STATE))
    dt = jax.nn.softplus(dt_raw.astype(jnp.float32) + dt_bias.astype(jnp.float32))
    a = -jnp.exp(a_log.astype(jnp.float32))
    s_pad = -(-s // SSM_CHUNK) * SSM_CHUNK
    xh = xs.reshape(bsz, s, SSM_HEADS, SSM_HEAD_DIM)
    y = ssd_chunked(_pad_seq(xh, s_pad), _pad_seq(dt, s_pad), a,
                    _pad_seq(bm.reshape(bsz, s, SSM_GROUPS, SSM_STATE), s_pad),
                    _pad_seq(cm.reshape(bsz, s, SSM_GROUPS, SSM_STATE), s_pad))[:, :s]
    y = y + xh.astype(jnp.float32) * d_skip.astype(jnp.float32)[:, None]
    return gated_rmsnorm(y.reshape(bsz, s, D_INNER), z, ssm_norm_w).astype(xbc.dtype)


def hybrid_mixer(x, positions, w_in, conv_w, conv_b, dt_bias, a_log, d_skip, ssm_norm_w,
                 w_attn_proj, w_ssm_proj, w_out):
    bsz, s, _ = x.shape
    q, k, v, z, xbc, dt_raw, g_att, g_ssm = _split(x @ w_in, IN_SPLITS)
    heads = (bsz, s, ATT_HEADS, ATT_HEAD_DIM)
    q = partial_rope(q.reshape(heads), positions).transpose(0, 2, 1, 3)
    k = partial_rope(k.reshape(heads), positions).transpose(0, 2, 1, 3)
    v = v.reshape(heads).transpose(0, 2, 1, 3)
    att = moba_attention(q, k, v).transpose(0, 2, 1, 3).reshape(bsz, s, ATT_WIDTH)
    y_ssm = mamba2_branch(z, xbc, dt_raw, conv_w, conv_b, dt_bias, a_log, d_skip, ssm_norm_w)
    merged = (jax.nn.sigmoid(g_att) * (att @ w_attn_proj)
              + jax.nn.sigmoid(g_ssm) * (y_ssm @ w_ssm_proj))
    return merged @ w_out


def squared_relu_mlp(h, w_up, w_down):
    return jnp.square(jax.nn.relu(h @ w_up)) @ w_down


def setup_inputs(seed: int = 0) -> dict:
    key = jax.random.key(seed)
    ks = jax.random.split(key, 20)
    f32 = jnp.float32

    def nrm(k, shape, scale):
        return jax.random.normal(k, shape, f32) * scale

    x = jax.random.normal(ks[0], (BATCH, SEQ, D_MODEL), f32)
    start = jax.random.randint(ks[1], (BATCH, 1), 0, 1024)
    positions = (start + jnp.arange(SEQ)[None, :]).astype(jnp.int32)
    ln1_g = 1.0 + nrm(ks[2], (DEPTH, D_MODEL), 0.02)
    ln1_b = nrm(ks[3], (DEPTH, D_MODEL), 0.02)
    w_in = nrm(ks[4], (DEPTH, D_MODEL, IN_WIDTH), D_MODEL ** -0.5)
    conv_w = nrm(ks[5], (DEPTH, SSM_CONV, XBC_DIM), SSM_CONV ** -0.5)
    conv_b = nrm(ks[6], (DEPTH, XBC_DIM), 0.01)
    u = jax.random.uniform(ks[7], (DEPTH, SSM_HEADS), f32)
    dt0 = jnp.exp(u * (math.log(0.1) - math.log(0.001)) + math.log(0.001))
    dt_bias = dt0 + jnp.log(-jnp.expm1(-dt0))
    a_log = jnp.log(jax.random.uniform(ks[8], (DEPTH, SSM_HEADS), f32, 1.0, 16.0))
    d_skip = 1.0 + nrm(ks[9], (DEPTH, SSM_HEADS), 0.1)
    ssm_norm_w = 1.0 + nrm(ks[10], (DEPTH, D_INNER), 0.02)
    w_attn_proj = nrm(ks[11], (DEPTH, ATT_WIDTH, D_MODEL), BETA * ATT_WIDTH ** -0.5)
    w_ssm_proj = nrm(ks[12], (DEPTH, D_INNER, D_MODEL), BETA * D_INNER ** -0.5)
    w_out = nrm(ks[13], (DEPTH, D_MODEL, D_MODEL), BETA * D_MODEL ** -0.5)
    ln2_g = 1.0 + nrm(ks[14], (DEPTH, D_MODEL), 0.02)
    ln2_b = nrm(ks[15], (DEPTH, D_MODEL), 0.02)
    w_up = nrm(ks[16], (DEPTH, D_MODEL, D_FF), D_MODEL ** -0.5)
    w_down = nrm(ks[17], (DEPTH, D_FF, D_MODEL), BETA * D_FF ** -0.5)
    return {'x': x, 'positions': positions, 'ln1_g': ln1_g, 'ln1_b': ln1_b, 'w_in': w_in,
            'conv_w': conv_w, 'conv_b': conv_b, 'dt_bias': dt_bias, 'a_log': a_log,
            'd_skip': d_skip, 'ssm_norm_w': ssm_norm_w, 'w_attn_proj': w_attn_proj,
            'w_ssm_proj': w_ssm_proj, 'w_out': w_out, 'ln2_g': ln2_g, 'ln2_b': ln2_b,
            'w_up': w_up, 'w_down': w_down}


def reference(x, positions, ln1_g, ln1_b, w_in, conv_w, conv_b, dt_bias, a_log, d_skip,
              ssm_norm_w, w_attn_proj, w_ssm_proj, w_out, ln2_g, ln2_b, w_up, w_down):
    for i in range(DEPTH):
        mix = hybrid_mixer(x, positions, w_in[i], conv_w[i], conv_b[i], dt_bias[i], a_log[i],
                           d_skip[i], ssm_norm_w[i], w_attn_proj[i], w_ssm_proj[i], w_out[i])
        x = layer_norm(ALPHA * x + mix, ln1_g[i], ln1_b[i])
        x = layer_norm(ALPHA * x + squared_relu_mlp(x, w_up[i], w_down[i]), ln2_g[i], ln2_b[i])
    return x
```

```python
import math
import numpy as np
import concourse.bass as bass
import concourse.mybir as mybir
from concourse.bass_utils import run_bass_kernel_spmd

F32 = mybir.dt.float32
BF16 = mybir.dt.bfloat16
I32 = mybir.dt.int32
AF = mybir.ActivationFunctionType
ALU = mybir.AluOpType
AX = mybir.AxisListType

D = 1024
SEQ = 2048
DEPTH = 2
NCORES = 8
IN_W = 11296
ALPHA = (2 * DEPTH) ** 0.25
LN_EPS = 1e-5
RMS_EPS = 1e-5
NEG = -1.0e5
PI = math.pi

DEBUG = {}
STOP_AFTER = None
NO_BARRIER = False
VARIANT = 0


class Sched:
    ENG = ('pe', 'act', 'dve', 'pool', 'sp')

    def __init__(self, nc, n_dma_sems=24):
        self.nc = nc
        self.sem = {e: nc.alloc_semaphore('s_' + e) for e in self.ENG}
        self.cnt = {e: 0 for e in self.ENG}
        self.dsem = [nc.alloc_semaphore('d%d' % i) for i in range(n_dma_sems)]
        self.dcnt = [0] * n_dma_sems
        self.drr = {'sp': 0, 'pool': 0}
        self.dpool = {'sp': list(range(0, n_dma_sems // 2)), 'pool': list(range(n_dma_sems // 2, n_dma_sems))}
        self.waited = {}
        self.last_w = {}
        self.readers = {}
        self.q = {e: [] for e in self.ENG}

    def barrier(self):
        if NO_BARRIER is True:
            return
        self.epoch = {}
        scr = self.bar_scratch
        d_sp = {('dma', i): self.dcnt[i] for i in self.dpool['sp'] if self.dcnt[i] > 0}
        d_pl = {('dma', i): self.dcnt[i] for i in self.dpool['pool'] if self.dcnt[i] > 0}
        self._token('act', d_sp, lambda e: e.memzero(scr[:, 0:1]))
        t1 = ('act', self.cnt['act'])
        d_pl[t1[0]] = t1[1]
        self._token('pool', d_pl, lambda e: e.memset(scr[:, 1:2], 0.0))
        d3 = {e: c for e, c in self.cnt.items() if c > 0}
        self._token('dve', d3, lambda e: e.memset(scr[:, 2:3], 0.0))
        self.epoch = {'dve': self.cnt['dve']}

    def _token(self, e, deps, emit):
        waits = self._waits(e, deps)
        self.cnt[e] += 1
        sem = self.sem[e]

        def thunk(eng):
            for s, v in waits:
                eng.wait_ge(s, v)
            emit(eng).then_inc(sem, 1)
        self.q[e].append(thunk)

    def _deps(self, reads, writes):
        deps = dict(getattr(self, 'epoch', {}))

        def add(src, idx):
            if deps.get(src, 0) < idx:
                deps[src] = idx
        for k in reads:
            lw = self.last_w.get(k)
            if lw is not None:
                add(*lw)
        for k in writes:
            lw = self.last_w.get(k)
            if lw is not None:
                add(*lw)
            for src, idx in self.readers.get(k, {}).items():
                add(src, idx)
        return deps

    def _waits(self, e, deps):
        out = []
        self.maxw = getattr(self, 'maxw', {})
        for src, idx in deps.items():
            if self.waited.get((e, src), 0) >= idx:
                continue
            self.waited[(e, src)] = idx
            if isinstance(src, tuple):
                out.append((self.dsem[src[1]], 16 * idx))
            else:
                out.append((self.sem[src], idx))
        self.maxw[len(out)] = self.maxw.get(len(out), 0) + 1
        return out

    def _record(self, token, reads, writes):
        src, idx = token
        for k in reads:
            r = self.readers.setdefault(k, {})
            if r.get(src, 0) < idx:
                r[src] = idx
        for k in writes:
            self.last_w[k] = (src, idx)
            self.readers[k] = {}

    @staticmethod
    def _bank(k):
        if isinstance(k, tuple) and k and k[0] == 'PA':
            return ('BANK', k[1])
        if isinstance(k, tuple) and k and k[0] == 'PB':
            return ('BANK', 4 + k[1])
        if k in ('PC', 'PCg', 'PCo'):
            return ('BANK', 6)
        if k == 'PD':
            return ('BANK', 7)
        return None

    def _canon(self, reads, writes):
        r2, w2 = [], []
        for k in reads:
            b = self._bank(k)
            if b is None:
                r2.append(k)
            elif b not in w2:
                w2.append(b)
        for k in writes:
            b = self._bank(k)
            if b is None:
                w2.append(k)
            elif b not in w2:
                w2.append(b)
        return r2, w2

    def op(self, e, reads, writes, emit):
        reads, writes = self._canon(reads, writes)
        deps = self._deps(reads, writes)
        waits = self._waits(e, deps)
        self.cnt[e] += 1
        idx = self.cnt[e]
        sem = self.sem[e]

        def thunk(eng):
            for s, v in waits:
                eng.wait_ge(s, v)
            emit(eng).then_inc(sem, 1)
        self.q[e].append(thunk)
        self._record((e, idx), reads, writes)

    def dma(self, e, reads, writes, emit):
        pool = self.dpool[e]
        s = pool[self.drr[e] % len(pool)]
        self.drr[e] += 1
        src = ('dma', s)
        deps = self._deps(reads, writes)
        if self.dcnt[s] > 0 and deps.get(src, 0) < self.dcnt[s]:
            deps[src] = self.dcnt[s]
        waits = self._waits(e, deps)
        self.dcnt[s] += 1
        idx = self.dcnt[s]
        sem = self.dsem[s]

        def thunk(eng):
            for sm, v in waits:
                eng.wait_ge(sm, v)
            emit(eng).then_inc(sem, 16)
        self.q[e].append(thunk)
        self._record((src, idx), reads, writes)

    def alias(self, old_keys, new_keys):
        acc = {}
        for k in old_keys:
            lw = self.last_w.get(k)
            if lw is not None and acc.get(lw[0], 0) < lw[1]:
                acc[lw[0]] = lw[1]
            for src, idx in self.readers.get(k, {}).items():
                if acc.get(src, 0) < idx:
                    acc[src] = idx
        for k in new_keys:
            r = self.readers.setdefault(k, {})
            for src, idx in acc.items():
                if r.get(src, 0) < idx:
                    r[src] = idx

    def finish(self, final_keys):
        self.barrier()
        deps = self._deps(final_keys, [])
        waits = self._waits('sp', deps)

        def thunk(eng):
            for s, v in waits:
                eng.wait_ge(s, v)
        self.q['sp'].append(thunk)
        nc = self.nc
        q = self.q
        with nc.Block() as block:
            @block.sync
            def _(eng):
                for t in q['sp']:
                    t(eng)

            @block.tensor
            def _(eng):
                for t in q['pe']:
                    t(eng)

            @block.scalar
            def _(eng):
                for t in q['act']:
                    t(eng)

            @block.vector
            def _(eng):
                for t in q['dve']:
                    t(eng)

            @block.gpsimd
            def _(eng):
                for t in q['pool']:
                    t(eng)


C_ID, C_TRI, C_ONES, C_RM, C_INVF, C_SGN, C_NEGP, C_FLOOR, NC_CONST = 0, 128, 256, 384, 512, 513, 514, 578, 648


def make_consts():
    c = np.zeros((128, NC_CONST), np.float32)
    c[:, C_ID:C_ID + 128] = np.eye(128, dtype=np.float32)
    c[:, C_TRI:C_TRI + 128] = np.triu(np.ones((128, 128), np.float32))
    c[:, C_ONES:C_ONES + 128] = 1.0
    rm = np.zeros((128, 128), np.float32)
    inv = (500000.0 ** (-np.arange(0, 16, 2, dtype=np.float32) / 16)).astype(np.float32)
    for blk in range(2):
        for d in range(16):
            src = d + 8 if d < 8 else d - 8
            rm[blk * 64 + src, blk * 64 + d] = 1.0
    c[:, C_RM:C_RM + 128] = rm
    for p in range(128):
        d = p % 64
        if d < 16:
            c[p, C_INVF] = inv[d % 8]
            c[p, C_SGN] = -1.0 if d < 8 else 1.0
    for qt in range(8):
        blk = (8 + qt) // 2
        for j in range(8):
            c[:, C_NEGP + qt * 8 + j] = 0.0 if j < blk else -1.0e30
            c[:, C_FLOOR + qt * 8 + j] = NEG if j < blk else 0.0
    e8 = np.zeros((8, SEQ), np.float32)
    for j in range(8):
        e8[j, j * 256:(j + 1) * 256] = 1.0
    return c, e8


def prep_weights(inp):
    out = {}
    L = DEPTH

    def kmaj(w):
        K, N = w.shape
        return w.reshape(K // 128, 128, N).transpose(1, 0, 2)

    w_in = inp['w_in']
    wqkv = np.empty((L, 8, 128, 8, 384), np.float32)
    wssm = np.empty((L, 8, 128, 8, 768), np.float32)
    wdt = np.empty((L, 128, 8, 32), np.float32)
    wc1 = np.empty((L, 8, 128, 5120), np.float32)
    wout = np.empty((L, 128, 8, 1024), np.float32)
    wup = np.empty((L, 8, 128, 8, 512), np.float32)
    wdn = np.empty((L, 8, 128, 4, 1024), np.float32)
    cw = np.empty((L, 128, 8, 4, 5), np.float32)
    for l in range(L):
        wi = kmaj(w_in[l])
        for hp in range(8):
            wqkv[l, hp, :, :, 0:128] = wi[:, :, hp * 128:(hp + 1) * 128]
            wqkv[l, hp, :, :, 128:256] = wi[:, :, 1024 + hp * 128:1024 + (hp + 1) * 128]
            wqkv[l, hp, :, :, 256:384] = wi[:, :, 2048 + hp * 128:2048 + (hp + 1) * 128]
        for g in range(8):
            wssm[l, g, :, :, 0:256] = wi[:, :, 3072 + g * 256:3072 + (g + 1) * 256]
            wssm[l, g, :, :, 256:512] = wi[:, :, 5120 + g * 256:5120 + (g + 1) * 256]
            wssm[l, g, :, :, 512:640] = wi[:, :, 7168 + g * 128:7168 + (g + 1) * 128]
            wssm[l, g, :, :, 640:768] = wi[:, :, 8192 + g * 128:8192 + (g + 1) * 128]
        wdt[l] = wi[:, :, 9216:9248]
        wap = kmaj(inp['w_attn_proj'][l])
        wsp = kmaj(inp['w_ssm_proj'][l])
        for fc in range(8):
            blk = np.empty((128, 8, 256), np.float32)
            blk[:, :, 0:128] = wi[:, :, 9248 + fc * 128:9248 + (fc + 1) * 128]
            blk[:, :, 128:256] = wi[:, :, 10272 + fc * 128:10272 + (fc + 1) * 128]
            wc1[l, fc, :, 0:2048] = blk.reshape(128, 2048)
            wc1[l, fc, :, 2048:3072] = wap[:, :, fc * 128:(fc + 1) * 128].reshape(128, 1024)
            wc1[l, fc, :, 3072:5120] = wsp[:, :, fc * 128:(fc + 1) * 128].reshape(128, 2048)
        wout[l] = kmaj(inp['w_out'][l])
        wu = kmaj(inp['w_up'][l])
        wd = kmaj(inp['w_down'][l])
        for gI in range(8):
            wup[l, gI] = wu[:, :, gI * 512:(gI + 1) * 512]
            wdn[l, gI] = wd[:, gI * 4:(gI + 1) * 4, :]
        cwl = inp['conv_w'][l]
        cbl = inp['conv_b'][l]
        for g in range(8):
            offs = [g * 256, g * 256 + 128, 2048 + g * 128, 3072 + g * 128]
            for ci, o in enumerate(offs):
                cw[l, :, g, ci, 0:4] = cwl[:, o:o + 128].T
                cw[l, :, g, ci, 4] = cbl[o:o + 128]
    out['wqkv'] = wqkv.reshape(L, 8, 128, 3072)
    out['wssm'] = wssm.reshape(L, 8, 128, 6144)
    out['wdt'] = wdt.reshape(L, 128, 256)
    out['wc1'] = wc1
    out['wout'] = wout.reshape(L, 128, 8192)
    out['wup'] = wup.reshape(L, 8, 128, 4096)
    out['wdn'] = wdn.reshape(L, 8, 128, 4096)
    out['cw'] = cw.reshape(L, 128, 160)
    small = np.concatenate([inp['dt_bias'], inp['a_log'], inp['d_skip']], axis=1)
    out['small'] = np.ascontiguousarray(small.reshape(L, 1, 96))
    out['normw'] = np.ascontiguousarray(inp['ssm_norm_w'].reshape(L, 1, 2048))
    lnp = np.stack([inp['ln1_g'], inp['ln1_b'], inp['ln2_g'], inp['ln2_b']], axis=1)
    out['lnp'] = np.ascontiguousarray(lnp.reshape(L, 1, 4096))
    return {k: np.ascontiguousarray(v, dtype=np.float32) for k, v in out.items()}


class Arena:
    def __init__(self, base_ap, segments):
        self.base = base_ap
        self.segs = [list(x) for x in segments]

    def alloc(self, shape, dtype=F32):
        P = shape[0]
        n = 1
        for d in shape[1:]:
            n *= d
        per = 4 if dtype in (F32, I32) else 2
        ncols = (n * per + 3) // 4
        ncols = (ncols + 7) // 8 * 8
        for sg in self.segs:
            if sg[1] - sg[0] >= ncols:
                off = sg[0]
                sg[0] += ncols
                ap = self.base[0:P, off:off + (n * per + 3) // 4]
                if dtype != F32:
                    ap = ap.bitcast(dtype)
                if len(shape) == 3:
                    ap = ap.rearrange("p (a b) -> p a b", a=shape[1])
                elif len(shape) == 4:
                    ap = ap.rearrange("p (a b c) -> p a b c", a=shape[1], b=shape[2])
                return ap
        raise RuntimeError("arena out of memory for %s" % (shape,))


class _Stop(Exception):
    pass


class Builder:
    def __init__(self):
        nc = bass.Bass("TRN2", target_bir_lowering=False)
        self.nc = nc
        self.S = Sched(nc)
        self.rec = None
        S_ = self.S
        S_._op, S_._dma = S_.op, S_.dma

        def _rop(*a):
            if self.rec is not None:
                self.rec.append(lambda: S_._op(*a))
            else:
                S_._op(*a)

        def _rdma(*a):
            if self.rec is not None:
                self.rec.append(lambda: S_._dma(*a))
            else:
                S_._dma(*a)
        S_.op, S_.dma = _rop, _rdma
        L = DEPTH
        dt = lambda n, s, d=F32, k="ExternalInput": nc.dram_tensor(n, s, d, kind=k).ap()
        self.x_in = dt("x", [SEQ, D])
        self.pos_in = dt("pos", [1, SEQ], I32)
        self.consts_in = dt("consts", [128, NC_CONST])
        self.e8_in = dt("e8", [8, SEQ])
        self.wqkv = dt("wqkv", [L, 8, 128, 3072])
        self.wssm = dt("wssm", [L, 8, 128, 6144])
        self.wdt = dt("wdt", [L, 128, 256])
        self.wc1 = dt("wc1", [L, 8, 128, 5120])
        self.wout = dt("wout", [L, 128, 8192])
        self.wup = dt("wup", [L, 8, 128, 4096])
        self.wdn = dt("wdn", [L, 8, 128, 4096])
        self.cw = dt("cw", [L, 128, 160])
        self.small = dt("small", [L, 1, 96])
        self.normw = dt("normw", [L, 1, 2048])
        self.lnp = dt("lnp", [L, 1, 4096])
        self.out = dt("out", [SEQ, D], F32, "ExternalOutput")
        self.x1 = dt("x1s", [SEQ, D], F32, "Internal")
        self.hres = dt("hres", [SEQ, D], F32, "Internal")
        self.dbg = {}
        for name, shape in DEBUG.items():
            self.dbg[name] = dt("dbg_" + name, list(shape), F32, "ExternalOutput")
        self.final_keys = []

    def sb(self, name, shape, dtype=F32):
        return self.ar.alloc(list(shape), dtype)

    def phase(self, segs):
        self.S.barrier()
        self.ar = Arena(self.arena, segs)

    def mm(self, reads, writes, out, pairs, transpose=False):
        def emit(e):
            n = len(pairs)
            ins = None
            for i, (l, r) in enumerate(pairs):
                ins = e.matmul(out, lhsT=l, rhs=r, start=(i == 0), stop=(i == n - 1))
            return ins
        self.S.op('pe', reads, writes, emit)

    def mmseq(self, reads, writes, groups):
        def emit(e):
            ins = None
            for out, pairs in groups:
                n = len(pairs)
                for i, (l, r) in enumerate(pairs):
                    ins = e.matmul(out, lhsT=l, rhs=r, start=(i == 0), stop=(i == n - 1))
            return ins
        self.S.op('pe', reads, writes, emit)

    def tps(self, reads, writes, items, ident):
        def emit(e):
            ins = None
            for o, i in items:
                ins = e.transpose(o, i, ident)
            return ins
        self.S.op('pe', reads, writes, emit)

    def dump(self, name, ap, keys):
        if name in self.dbg:
            if not isinstance(keys, list):
                keys = [keys]
            self.S.dma('sp', keys, [('dbg', name)], lambda e: e.dma_start(out=self.dbg[name], in_=ap))
            self.final_keys.append(('dbg', name))

    def build(self):
        nc, S = self.nc, self.S
        sb = self.sb
        nbytes = int(nc.sbuf_bytes_remaining) - 256
        NA = nbytes // 4 // 8 * 8
        self.arena = nc.alloc_sbuf_tensor("arena", [128, NA], F32)[:]
        self.R_P = (0, 11264)
        self.R_B1 = (11264, 19456)
        self.R_B2 = (19456, 35840)
        self.R_B3 = (35840, 44032)
        self.R_W = (44032, NA)
        assert NA - 44032 > 8000, NA
        self.ar = Arena(self.arena, [self.R_P])
        self.S.bar_scratch = sb("barscr", [128, 8])
        cst = sb("cst", [128, NC_CONST])
        self.cst = cst
        S.dma('sp', [], ['cst'], lambda e: e.dma_start(out=cst[:], in_=self.consts_in))
        self.ident_f = cst[:, C_ID:C_ID + 128]
        self.tri_f = cst[:, C_TRI:C_TRI + 128]
        self.ones_f = cst[:, C_ONES:C_ONES + 128]
        cbf = sb("cbf", [128, 512], BF16)
        S.op('dve', ['cst'], ['cbf'], lambda e: e.tensor_copy(cbf[:], cst[:, 0:512]))
        self.ident_b = cbf[:, 0:128]
        self.tri_b = cbf[:, 128:256]
        self.rm_b = cbf[:, 384:512]
        self.nh = sb("nh", [128, 8])
        S.op('pool', [], ['nh'], lambda e: e.memset(self.nh[:], -0.5))
        self.ropeC = sb("ropeC", [128, SEQ], BF16)
        self.ropeS = sb("ropeS", [128, SEQ], BF16)
        self.xT = sb("xT", [128, 8, SEQ], BF16)
        A = self.arena
        self.B1 = A[:, self.R_B1[0]:self.R_B1[1]].bitcast(BF16).rearrange("p (c t) -> p c t", c=8)
        self.B2 = A[:, self.R_B2[0]:self.R_B2[1]]
        self.B3 = A[:, self.R_B3[0]:self.R_B3[1]].bitcast(BF16).rearrange("p (c t) -> p c t", c=8)
        self.yT = self.B2.bitcast(BF16).rearrange("p (c t) -> p c t", c=16)
        self.acc = self.B2.rearrange("p (t f) -> p t f", t=16)
        self.ar = Arena(self.arena, [self.R_B2])
        self.PA = nc.alloc_psum_tensor("PA", [128, 2048], F32)
        self.PB = nc.alloc_psum_tensor("PB", [128, 1024], F32)
        self.PC = nc.alloc_psum_tensor("PC", [128, 512], F32)
        self.PD = nc.alloc_psum_tensor("PD", [128, 512], F32)
        self.rope_tables()
        if STOP_AFTER == 'rope':
            return self.finish()
        self.load_xT()
        if STOP_AFTER == 'xT':
            self.dbg_tile('xT0', self.xT[:, 3, 1024:1536], self.xT_keys(), 512)
            return self.finish()
        try:
            for l in range(DEPTH):
                self.layer(l)
                if STOP_AFTER == 'layer0':
                    break
        except _Stop:
            pass
        return self.finish()

    def finish(self):
        self.S.finish(self.final_keys)
        return self.nc

    def rope_tables(self):
        S, sb = self.S, self.sb
        cst = None
        posi = sb("posi", [128, SEQ], I32)
        S.dma('sp', [], ['posi'], lambda e: e.dma_start(out=posi[:], in_=self.pos_in.broadcast_to([128, SEQ])))
        ang = sb("ang", [128, SEQ])
        tmp = sb("rtmp", [128, SEQ])
        tmi = sb("rtmi", [128, SEQ], I32)
        rc = sb("rc", [128, SEQ])
        rs = sb("rs", [128, SEQ])
        invf = self.cst[:, C_INVF:C_INVF + 1]
        sgn = self.cst[:, C_SGN:C_SGN + 1]
        S.op('dve', ['posi'], ['ang'], lambda e: e.tensor_copy(ang[:], posi[:]))
        S.op('dve', ['ang', 'cst'], ['ang'], lambda e: e.tensor_scalar(ang[:], ang[:], invf, None, ALU.mult))
        for which, dst in ((0, rs), (1, rc)):
            off = 0.0 if which == 0 else PI / 2
            S.op('dve', ['ang'], ['rtmp'], lambda e, off=off: e.tensor_scalar(tmp[:], ang[:], off, 1.0 / (2 * PI), ALU.add, ALU.mult))
            S.op('dve', ['rtmp'], ['rtmi'], lambda e: e.tensor_copy(tmi[:], tmp[:]))
            S.op('dve', ['rtmi'], ['rtmp'], lambda e: e.tensor_copy(tmp[:], tmi[:]))
            S.op('dve', ['rtmp', 'ang'], ['rtmp'], lambda e: e.scalar_tensor_tensor(tmp[:], in0=tmp[:], scalar=-2 * PI, in1=ang[:], op0=ALU.mult, op1=ALU.add))
            S.op('dve', ['rtmp'], [('rope', which)], lambda e, off=off, dst=dst: e.tensor_scalar(dst[:], tmp[:], off, None, ALU.add))
            S.op('dve', [('rope', which)], ['rtmp'], lambda e, dst=dst: e.tensor_scalar(tmp[:], dst[:], PI, -2 * PI, ALU.is_gt, ALU.mult))
            S.op('dve', ['rtmp', ('rope', which)], [('rope', which)], lambda e, dst=dst: e.tensor_tensor(dst[:], dst[:], tmp[:], ALU.add))
            S.op('dve', [('rope', which)], ['rtmp'], lambda e, dst=dst: e.tensor_scalar(tmp[:], dst[:], -PI, 2 * PI, ALU.is_lt, ALU.mult))
            S.op('dve', ['rtmp', ('rope', which)], [('rope', which)], lambda e, dst=dst: e.tensor_tensor(dst[:], dst[:], tmp[:], ALU.add))
            S.op('dve', [('rope', which)], [('rope', which)], lambda e, dst=dst: e.tensor_scalar(dst[:], dst[:], 3.1415925, -3.1415925, ALU.min, ALU.max))
            S.op('act', [('rope', which)], [('rope', which)], lambda e, dst=dst: e.activation(dst[:], dst[:], AF.Sin))
        S.op('dve', [('rope', 0), 'cst'], [('rope', 0)], lambda e: e.tensor_scalar(self.ropeS[:], rs[:], sgn, None, ALU.mult))
        S.op('dve', [('rope', 1)], [('rope', 1)], lambda e: e.tensor_copy(self.ropeC[:], rc[:]))
        self.dump('ropeC', rc[:], ('rope', 1))
        self.dump('ropeS', rs[:], ('rope', 0))

    def load_xT(self):
        S, sb = self.S, self.sb
        self.xt_buf = [sb("xtile%d" % i, [128, D]) for i in range(2)]
        for tt in range(16):
            xt = self.xt_buf[tt % 2]
            k = ('xtile', tt % 2)
            S.dma('sp', [], [k], lambda e, xt=xt, tt=tt: e.dma_start(out=xt[:], in_=self.x_in[tt * 128:(tt + 1) * 128, :]))
            self.to_featmajor(xt, k, self.xT, 'xT', tt)

    def to_featmajor(self, tile, tkey, dstT, dkey, tt):
        S = self.S
        for half in range(2):
            ps = self.PA[:, half * 512:(half + 1) * 512]
            pk = ('PA', half)
            self.tps([tkey, 'cst'], [pk], [(ps[:, j * 128:(j + 1) * 128], tile[:, (half * 4 + j) * 128:(half * 4 + j + 1) * 128]) for j in range(4)], self.ident_f)
            eng = 'act' if half == 0 else 'dve'
            if eng == 'act':
                S.op('act', [pk], [(dkey, tt)], lambda e, ps=ps, half=half: e.activation(
                    dstT[:, half * 4:(half + 1) * 4, tt * 128:(tt + 1) * 128], ps.rearrange("p (c t) -> p c t", c=4), AF.Copy))
            else:
                S.op('dve', [pk], [(dkey, tt)], lambda e, ps=ps, half=half: e.tensor_copy(
                    dstT[:, half * 4:(half + 1) * 4, tt * 128:(tt + 1) * 128], ps.rearrange("p (c t) -> p c t", c=4)))

    def layer(self, l):
        self.attention(l)
        if STOP_AFTER == 'att':
            raise _Stop()
        self.ssd(l)
        if STOP_AFTER == 'ssd':
            raise _Stop()
        self.merge(l)
        if STOP_AFTER == 'merge':
            raise _Stop()
        self.mix_ln1(l)
        if STOP_AFTER == 'ln1':
            raise _Stop()
        self.ffn(l)

    def psum_epoch(self):
        keys = [('PA', i) for i in range(4)] + [('PB', 0), ('PB', 1), ('PB', 0, 0), ('PB', 0, 1)] + \
               [('PB', 1, h) for h in range(4)] + ['PC', 'PCg', 'PCo', 'PD']
        self.S.alias(keys, keys)

    def dbg_qk(self, h):
        S = self.S
        if 'qa0' in self.dbg:
            qa, ka = self.qa[h], self.ka[h]
            dq = self.sb("dbgq", [128, SEQ])
            dk = self.sb("dbgk", [128, SEQ])
            S.op('dve', [('qa', h)], ['dbgq'], lambda e: e.tensor_copy(dq[0:72, :], qa[0:72, :]))
            S.op('dve', [('ka', h)], ['dbgk'], lambda e: e.tensor_copy(dk[0:72, :], ka[0:72, :]))
            self.dump('qa0', dq[0:72, :], 'dbgq')
            self.dump('ka0', dk[0:72, :], 'dbgk')

    def dbg_tile(self, name, ap, keys, ncols):
        if name in self.dbg:
            t = self.sb("dbg_" + name, [128, ncols])
            self.S.op('dve', keys, ['dbgt_' + name], lambda e: e.tensor_copy(t[:], ap))
            self.dump(name, t[:], 'dbgt_' + name)

    def xT_keys(self):
        return [('xT', tt) for tt in range(16)]

    def attention(self, l):
        self.psum_epoch()
        S, sb, nc = self.S, self.sb, self.nc
        self.phase([self.R_W, self.R_B2, (self.R_B3[0] + 7168, self.R_B3[1])])
        if True:
            self.wqkv_b = [sb("wqkv%d" % i, [128, 8, 384], BF16) for i in range(2)]
            self.qa = [sb("qa%d" % i, [72, SEQ], BF16) for i in range(4)]
            self.ka = [sb("ka%d" % i, [72, SEQ], BF16) for i in range(4)]
            self.va = [sb("va%d" % i, [128, 16, 65], BF16) for i in range(4)]
            self.qbf = [sb("qbf%d" % i, [128, 512], BF16) for i in range(2)]
            self.t1 = [sb("t1_%d" % i, [128, 512]) for i in range(2)]
            self.t2 = [sb("t2_%d" % i, [128, 512]) for i in range(2)]
            self.atm = [sb("atm%d" % i, [128, 16, 128], BF16) for i in range(2)]
            self.km = sb("km", [64, 8])
            self.kmh = sb("kmh", [64, 8], BF16)
            self.kml = sb("kml", [64, 8], BF16)
            self.gm = sb("gm", [128, 64])
            self.m8 = sb("m8", [128, 8, 8])
            self.lt = sb("lt", [128, 64])
            self.bpad = sb("bpad", [128, 8, 72], BF16)
            self.rden = sb("rden", [128, 4])
            self.NPT = 28
            for i in range(4):
                S.op('pool', [], [('qa', i)], lambda e, i=i: e.memset(self.qa[i][64:72, :], 0.0))
                S.dma('pool', [], [('ka', i)], lambda e, i=i: e.dma_start(out=self.ka[i][64:72, :], in_=self.e8_in))
                S.op('pool', [], [('va', i)], lambda e, i=i: e.memset(self.va[i][:, :, 64:65], 1.0))
            S.op('pool', [], ['bpad'], lambda e: e.memset(self.bpad[:], 0.0))
            if STOP_AFTER == 'att_init':
                self.dbg_qk(0)
                raise _Stop()
        PT = [self.B3[:, i // 4, (i % 4) * 512:(i % 4 + 1) * 512] for i in range(self.NPT)]
        attT = self.B1
        negp = self.cst[:, C_NEGP:C_NEGP + 64]
        floorb = self.cst[:, C_FLOOR:C_FLOOR + 64]
        pt_rr = [0]
        PDb = self.PD[:].bitcast(BF16)
        xk = self.xT_keys()

        def load_w(hp):
            wb = self.wqkv_b[hp % 2]
            S.dma('pool', [], [('wqkv', hp % 2, 0)], lambda e, wb=wb, hp=hp: e.dma_start(
                out=wb[:], in_=self.wqkv[l, hp].rearrange("p (k n) -> p k n", k=8), max_dma_last_dim=4096))
        load_w(0)
        for hp in range(8):
            wb = self.wqkv_b[hp % 2]
            wk = ('wqkv', hp % 2, 0)
            wk1 = wk2 = wk3 = wk
            if STOP_AFTER == 'att_w0':
                self.dbg_tile('wb0', wb[:].rearrange("p a b -> p (a b)"), [wk, wk1, wk2, wk3], 3072)
                raise _Stop()
            if hp + 1 < 8:
                load_w(hp + 1)
            if STOP_AFTER == 'att_w1':
                self.dbg_tile('wb0', wb[:].rearrange("p a b -> p (a b)"), [wk, wk1, wk2, wk3], 3072)
                raise _Stop()
            par = (hp % 2) * 2
            hA, hB = par, par + 1
            cnt = 0
            for which in (1, 0):
                dst = self.ka if which == 1 else self.qa
                dn = 'ka' if which == 1 else 'qa'
                for tc in range(4):
                    bank = cnt % 4
                    cnt += 1
                    ps = self.PA[:, bank * 512:(bank + 1) * 512]
                    pk = ('PA', bank)
                    cols = slice(tc * 512, (tc + 1) * 512)
                    def stopat(ch, ap=None, keys=None):
                        if STOP_AFTER == 'att_qk1' + ch:
                            if ap is not None:
                                self.dbg_tile('probe', ap, keys, 512)
                            raise _Stop()
                    self.mm([wk, wk1, wk2, wk3] + xk[tc * 4:(tc + 1) * 4], [pk], ps,
                            [(wb[:, kc, which * 128:(which + 1) * 128], self.xT[:, kc, cols]) for kc in range(8)])
                    stopat('a', ps, [pk])
                    qb = self.qbf[cnt % 2]
                    qk_ = ('qbf', cnt % 2)
                    if VARIANT != 6:
                        S.op('act', [pk], [qk_], lambda e, qb=qb, ps=ps: e.activation(qb[:], ps, AF.Copy))
                    stopat('b', qb[:], [qk_])
                    t1 = self.t1[cnt % 2]
                    t2 = self.t2[cnt % 2]
                    k1 = ('t1', cnt % 2)
                    k2 = ('t2', cnt % 2)
                    if VARIANT == 4:
                        t1 = self.sb("t1x", [128, 512])
                    if VARIANT in (5, 8):
                        S.op('dve', [('rope', 1)], [k1], lambda e, t1=t1, ps=ps, cols=cols: e.tensor_copy(t1[:], self.ropeC[:, cols]))
                    elif VARIANT in (1, 4):
                        S.op('dve', [pk, ('rope', 1)], [k1], lambda e, t1=t1, ps=ps, cols=cols: e.tensor_copy(t1[:], ps))
                    elif VARIANT == 2:
                        S.op('dve', [pk, ('rope', 1)], [k1], lambda e, t1=t1, ps=ps, cols=cols: e.tensor_copy(t1[:], self.ropeC[:, cols]))
                    elif VARIANT == 3:
                        S.op('dve', [pk, ('rope', 1)], [k1], lambda e, t1=t1, ps=ps, cols=cols: e.tensor_tensor(t1[:], ps, self.cst[:, 0:512], ALU.mult))
                    else:
                        S.op('dve', [pk, ('rope', 1)], [k1], lambda e, t1=t1, ps=ps, cols=cols: e.tensor_tensor(t1[:], ps, self.ropeC[:, cols], ALU.mult))
                    stopat('c', qb[:] if VARIANT == 7 else t1[:], [qk_, k1] if VARIANT in (7, 8) else [k1])
                    pr = self.PB[:, (cnt % 2) * 512:(cnt % 2 + 1) * 512]
                    prk = ('PB', cnt % 2)
                    self.mm([qk_, 'cbf'], [prk], pr, [(self.rm_b, qb[:])])
                    stopat('d', pr, [prk])
                    S.op('dve', [prk, ('rope', 0)], [k2], lambda e, t2=t2, pr=pr, cols=cols: e.tensor_tensor(t2[:], pr, self.ropeS[:, cols], ALU.mult))
                    stopat('e', t2[:], [k2])
                    S.op('dve', [k1, k2], [(dn, hA)], lambda e, t1=t1, t2=t2, dst=dst, cols=cols, hA=hA: e.tensor_tensor(dst[hA][0:64, cols], t1[0:64, :], t2[0:64, :], ALU.add))
                    stopat('f', dst[hA][0:64, cols], [(dn, hA)])
                    S.op('dve', [k1, k2], [(dn, hB)], lambda e, t1=t1, t2=t2, dst=dst, cols=cols, hB=hB: e.tensor_tensor(dst[hB][0:64, cols], t1[64:128, :], t2[64:128, :], ALU.add))
                    stopat('g', dst[hB][0:64, cols], [(dn, hB)])
            if STOP_AFTER == 'att_qk':
                raise _Stop()
            for tq in range(4):
                bank = cnt % 4
                cnt += 1
                ps = self.PA[:, bank * 512:(bank + 1) * 512]
                pk = ('PA', bank)
                self.mmseq([wk, wk1, wk2, wk3] + xk[tq * 4:(tq + 1) * 4], [pk],
                           [(ps[:, j * 128:(j + 1) * 128],
                             [(self.xT[:, kc, (tq * 4 + j) * 128:(tq * 4 + j + 1) * 128], wb[:, kc, 256:384]) for kc in range(8)])
                            for j in range(4)])
                psv = ps.rearrange("p (t c) -> p t c", t=4)
                if hp == 0 and tq == 0:
                    self.dbg_tile('psv', ps, [pk], 512)
                    self.dbg_tile('wb0', wb[:].rearrange("p a b -> p (a b)"), [wk, wk1, wk2, wk3], 3072)
                S.op('act', [pk], [('va', hA)], lambda e, psv=psv, tq=tq, hA=hA: e.activation(self.va[hA][:, tq * 4:(tq + 1) * 4, 0:64], psv[:, :, 0:64], AF.Copy))
                S.op('act', [pk], [('va', hB)], lambda e, psv=psv, tq=tq, hB=hB: e.activation(self.va[hB][:, tq * 4:(tq + 1) * 4, 0:64], psv[:, :, 64:128], AF.Copy))
            if STOP_AFTER == 'att_proj':
                self.dbg_tile('va0', self.va[0][:].rearrange("p a b -> p (a b)"), [('va', 0)], 1040)
                self.dbg_qk(hA)
                raise _Stop()
            atm = self.atm[hp % 2]
            ak = ('atm', hp % 2)
            for hh, hbuf in ((0, hA), (1, hB)):
                qa, ka, va = self.qa[hbuf], self.ka[hbuf], self.va[hbuf]
                qk, kk, vk = ('qa', hbuf), ('ka', hbuf), ('va', hbuf)
                S.op('dve', [kk], ['km'], lambda e, ka=ka: e.tensor_reduce(self.km[:], ka[0:64, :].rearrange("p (b t) -> p b t", b=8), AX.X, ALU.add))
                S.op('dve', ['km'], ['kmh'], lambda e: e.tensor_scalar(self.kmh[:], self.km[:], 1.0 / 256, None, ALU.mult))
                S.op('dve', ['km', 'kmh'], ['kml'], lambda e: e.scalar_tensor_tensor(self.kml[:], in0=self.km[:], scalar=1.0 / 256, in1=self.kmh[:], op0=ALU.mult, op1=ALU.subtract))
                pg = self.PC[:, 448:512]
                self.mmseq([qk, 'kmh', 'kml'], ['PCg'],
                           [(pg[:, t * 8:(t + 1) * 8], [(qa[0:64, (8 + t) * 128:(9 + t) * 128], self.kmh[:]), (qa[0:64, (8 + t) * 128:(9 + t) * 128], self.kml[:])]) for t in range(8)])
                S.op('dve', ['PCg', 'cst'], ['gm'], lambda e, pg=pg: e.tensor_tensor(self.gm[:], pg, negp, ALU.add))

                S.op('dve', ['gm'], ['m8'], lambda e: sel_max(e, self))
                S.op('dve', ['gm', 'm8'], ['lt'], lambda e: sel_lt(e, self))
                S.op('dve', ['lt', 'cst'], ['bpad'], lambda e: e.tensor_tensor(self.bpad[:, :, 64:72], self.lt[:].rearrange("p (t j) -> p t j", t=8), floorb.rearrange("p (t j) -> p t j", t=8), ALU.max))
                self.tps(['bpad', 'cbf'], ['PD'], [(PDb[0:72, t * 128:(t + 1) * 128], self.bpad[:, t, :]) for t in range(8)], self.ident_b)
                S.op('act', ['PD'], [qk], lambda e, qa=qa: e.activation(qa[64:72, 1024:2048], PDb[64:72, :], AF.Copy))
                if STOP_AFTER == 'att_gate':
                    self.dbg_qk(hA)
                    raise _Stop()
                slots = {}

                def stageA(qc):
                    sl = []
                    for kt in range(4 * qc + 4):
                        c0 = max(0, kt * 128 - qc * 512)
                        bank = kt % 2
                        ps = self.PB[:, bank * 512:(bank + 1) * 512]
                        pk = ('PB', bank)
                        self.mm([kk, qk], [pk], ps[:, c0:512], [(ka[0:72, kt * 128:(kt + 1) * 128], qa[0:72, qc * 512 + c0:(qc + 1) * 512])])
                        si = pt_rr[0]
                        pt_rr[0] = (pt_rr[0] + 1) % self.NPT
                        sl.append(si)
                        pt = PT[si]
                        S.op('act', [pk], [('PT', si)], lambda e, pt=pt, ps=ps, c0=c0: e.activation(pt[:, c0:512], ps[:, c0:512], AF.Exp, scale=0.125))
                        if kt >= 4 * qc:
                            S.op('dve', [('PT', si), 'cbf'], [('PT', si)], lambda e, pt=pt, c0=c0: e.tensor_tensor(pt[:, c0:c0 + 128], pt[:, c0:c0 + 128], self.tri_b, ALU.mult))
                    slots[qc] = sl

                def stageB(qc):
                    sl = slots[qc]
                    groups = []
                    for qi in range(4):
                        qt = 4 * qc + qi
                        groups.append((self.PC[:, qi * 65:(qi + 1) * 65],
                                       [(PT[sl[kt]][:, qi * 128:(qi + 1) * 128], va[:, kt, :]) for kt in range(qt + 1)]))
                    self.mmseq([('PT', s_) for s_ in sl] + [vk], ['PCo'], groups)
                    pv = self.PC[:, 0:260].rearrange("p (q c) -> p q c", q=4)
                    S.op('dve', ['PCo'], ['rden'], lambda e, pv=pv: e.reciprocal(self.rden[:], pv[:, :, 64]))

                    def norm(e, atm=atm, hh=hh, qc=qc):
                        ins = None
                        for qi in range(4):
                            ins = e.tensor_scalar(atm[:, 4 * qc + qi, hh * 64:(hh + 1) * 64], self.PC[:, qi * 65:qi * 65 + 64], self.rden[:, qi:qi + 1], None, ALU.mult)
                        return ins
                    S.op('dve', ['PCo', 'rden'], [ak], norm)

                stageA(0)
                for qc in range(4):
                    if qc + 1 < 4:
                        stageA(qc + 1)
                    stageB(qc)
            if hp == 0:
                self.dbg_tile('va0', self.va[0][:].rearrange("p a b -> p (a b)"), [('va', 0)], 1040)
                self.dbg_tile('atm0', atm[:].rearrange("p a b -> p (a b)"), [ak], 2048)
            for half in range(2):
                self.tps([ak, 'cbf'], ['PD'], [(PDb[:, t * 128:(t + 1) * 128], atm[:, half * 8 + t, :]) for t in range(8)], self.ident_b)
                S.op('act', ['PD'], [('attT', hp, half)], lambda e, half=half, hp=hp: e.activation(attT[:, hp, half * 1024:(half + 1) * 1024], PDb, AF.Copy))
        if 'attT' in self.dbg:
            self.dbgf = self.sb("dbgf", [128, 512])
            for hp in range(8):
                for q4 in range(4):
                    S.op('dve', [('attT', hp, 0), ('attT', hp, 1), ('dbg', 'attT')], ['dbgf'], lambda e, hp=hp, q4=q4: e.tensor_copy(self.dbgf[:, 0:512], attT[:, hp, q4 * 512:(q4 + 1) * 512]))
                    S.dma('sp', ['dbgf'], [('dbg', 'attT')], lambda e, hp=hp, q4=q4: e.dma_start(out=self.dbg['attT'][hp * 128:(hp + 1) * 128, q4 * 512:(q4 + 1) * 512], in_=self.dbgf[:, 0:512]))
            self.final_keys.append(('dbg', 'attT'))

    def ssd(self, l):
        self.psum_epoch()
        S, sb, nc = self.S, self.sb, self.nc
        self.phase([self.R_W, self.R_B3])
        if True:
            self.wssm_b = [sb("wssm0", [128, 8, 768], BF16)] * 2
            self.wdt_b = sb("wdtb", [128, 8, 32], BF16)
            self.smallb = sb("smallb", [128, 96])
            self.cwb = sb("cwb", [128, 8, 4, 5])
            self.dt_all = sb("dt_all", [128, 16, 32])
            self.adt_all = sb("adt_all", [128, 16, 32])
            self.acs_all = sb("acs_all", [128, 16, 32])
            self.dst_all = sb("dst_all", [128, 16, 32])
            self.ea_all = sb("ea_all", [128, 16, 32])
            self.lastb = sb("lastb", [128, 8, 32])
            self.cdb = sb("cdb", [128, 8, 32])
            self.a_b = sb("a_b", [128, 32])
            self.uext = [sb("uext%d" % i, [128, 4, 259]) for i in range(2)]
            self.xh = sb("xh", [128, 4, 256])
            self.tnh = sb("tnh", [128, 4, 256])
            self.bcb2 = [sb("bcb%d" % i, [128, 2, 256], BF16) for i in range(2)]
            self.bcf = sb("bcf", [128, 256])
            self.xs_tok2 = [sb("xs_tok%d" % i, [128, 2, 256], BF16) for i in range(2)]
            self.xdt = sb("xdt", [128, 2, 256], BF16)
            self.xw2 = [sb("xw%d" % i, [128, 2, 256], BF16) for i in range(2)]
            self.btok2 = [sb("btok%d" % i, [128, 2, 128], BF16) for i in range(2)]
            self.cb3 = sb("cb3", [128, 3, 128])
            self.dtmp = [sb("dtmp%d" % i, [128, 3, 128]) for i in range(2)]
            self.mt = [sb("mt%d" % i, [128, 3, 128], BF16) for i in range(2)]
            self.prev_f = sb("prev_f", [128, 256])
            self.prev_b = sb("prev_b", [128, 256], BF16)
            self.tz = sb("tz", [128, 2, 256])
            self.yv = sb("yv", [128, 2, 256])
            self.yt2 = sb("yt2", [128, 2, 256])
            self.ss = sb("ss", [128, 2])
            self.rstd = sb("rstd", [128, 2])
            self.nwb = [sb("nwb%d" % i, [128, 256]) for i in range(2)]
            self.xsf = self.xh[:, 0:2, :]
            self.zs = self.tz
            self.h2 = self.yv
            self.yn = self.yv
            self.junk = self.tnh[:, 0, :]
        xk = self.xT_keys()
        S.dma('sp', [], ['smallb'], lambda e: e.dma_start(out=self.smallb[:], in_=self.small[l].broadcast_to([128, 96])))
        S.dma('sp', [], ['cwb'], lambda e: e.dma_start(out=self.cwb[:], in_=self.cw[l].rearrange("p (g c k) -> p g c k", g=8, c=4)))
        S.dma('pool', [], ['wdtb'], lambda e: e.dma_start(out=self.wdt_b[:], in_=self.wdt[l].rearrange("p (k n) -> p k n", k=8)))
        S.op('pool', ['cwb'], ['cwb'], lambda e: e.tensor_scalar(self.cwb[:], self.cwb[:], 0.5, None, ALU.mult))
        dtb = self.smallb[:, 0:32]
        alog = self.smallb[:, 32:64]
        dsk = self.smallb[:, 64:96]
        S.op('act', ['smallb'], ['a_b'], lambda e: e.activation(self.a_b[:], alog, AF.Exp))
        S.op('dve', ['a_b'], ['a_b'], lambda e: e.tensor_scalar(self.a_b[:], self.a_b[:], -1.0, None, ALU.mult))
        psd = self.PA[:, 0:512]
        self.mmseq(['wdtb'] + xk, [('PA', 0)],
                   [(psd[:, tt * 32:(tt + 1) * 32], [(self.xT[:, kc, tt * 128:(tt + 1) * 128], self.wdt_b[:, kc, :]) for kc in range(8)]) for tt in range(16)])
        dtf = self.dt_all[:].rearrange("p t h -> p (t h)")
        psd3 = psd.rearrange("p (t h) -> p t h", t=16)
        S.op('dve', [('PA', 0), 'smallb'], ['dt_all'], lambda e: e.tensor_tensor(self.dt_all[:], psd3, dtb.unsqueeze(1).broadcast_to([128, 16, 32]), ALU.add))
        S.op('act', ['dt_all'], ['dt_all'], lambda e: e.activation(dtf, dtf, AF.Exp))
        S.op('act', ['dt_all'], ['dt_all'], lambda e: e.activation(dtf, dtf, AF.Ln, bias=1.0))
        S.op('dve', ['dt_all', 'a_b'], ['adt_all'], lambda e: e.tensor_tensor(self.adt_all[:], self.dt_all[:], self.a_b[:].unsqueeze(1).broadcast_to([128, 16, 32]), ALU.mult))
        psa = self.PA[:, 512:1024]
        psl = self.PA[:, 1024:1280]
        groups = []
        for c in range(8):
            a0 = self.adt_all[:, 2 * c, :]
            a1 = self.adt_all[:, 2 * c + 1, :]
            groups.append((psa[:, (2 * c) * 32:(2 * c + 1) * 32], [(self.tri_f, a0)]))
            groups.append((psa[:, (2 * c + 1) * 32:(2 * c + 2) * 32], [(self.ones_f, a0), (self.tri_f, a1)]))
        self.mmseq(['adt_all', 'cst'], [('PA', 1)], groups)
        self.mmseq(['adt_all', 'cst'], [('PA', 2)],
                   [(psl[:, c * 32:(c + 1) * 32], [(self.ones_f, self.adt_all[:, 2 * c, :]), (self.ones_f, self.adt_all[:, 2 * c + 1, :])]) for c in range(8)])
        acsf = self.acs_all[:].rearrange("p t h -> p (t h)")
        S.op('dve', [('PA', 1)], ['acs_all'], lambda e: e.tensor_copy(acsf, psa))
        S.op('dve', [('PA', 2)], ['lastb'], lambda e: e.tensor_copy(self.lastb[:].rearrange("p c h -> p (c h)"), psl))
        S.op('dve', ['lastb', 'acs_all'], ['dst_all'], lambda e: e.tensor_tensor(
            self.dst_all[:].rearrange("p (c s) h -> p c s h", c=8), self.lastb[:].unsqueeze(2).broadcast_to([128, 8, 2, 32]),
            self.acs_all[:].rearrange("p (c s) h -> p c s h", c=8), ALU.subtract))
        dstf = self.dst_all[:].rearrange("p t h -> p (t h)")
        S.op('act', ['dst_all'], ['dst_all'], lambda e: e.activation(dstf, dstf, AF.Exp))
        S.op('dve', ['dst_all', 'dt_all'], ['dst_all'], lambda e: e.tensor_tensor(self.dst_all[:], self.dst_all[:], self.dt_all[:], ALU.mult))
        S.op('act', ['acs_all'], ['ea_all'], lambda e: e.activation(self.ea_all[:].rearrange("p t h -> p (t h)"), acsf, AF.Exp))
        S.op('act', ['lastb'], ['cdb'], lambda e: e.activation(self.cdb[:].rearrange("p c h -> p (c h)"), self.lastb[:].rearrange("p c h -> p (c h)"), AF.Exp))
        self.dump('dt_all', self.dt_all[:].rearrange("p t h -> p (t h)"), 'dt_all')
        self.dump('acs_all', acsf, 'acs_all')

        P0 = self.PA[:, 0:512]
        P1 = self.PA[:, 512:1024]
        P2 = self.PA[:, 1024:1536]
        P3 = self.PA[:, 1536:2048]
        P4 = self.PB[:, 0:512]
        P5 = self.PB[:, 512:1024]
        P6 = self.PC[:, 0:512]
        P7 = self.PD[:, 0:512]
        uprev = None
        def load_g(g):
            wb = self.wssm_b[g % 2]
            S.dma('pool', [], [('wssm', 0, 0)], lambda e, wb=wb, g=g: e.dma_start(
                out=wb[:], in_=self.wssm[l, g].rearrange("p (k n) -> p k n", k=8), max_dma_last_dim=4096))
            nw = self.nwb[g % 2]
            S.dma('sp', [], [('nwb', g % 2)], lambda e, nw=nw, g=g: e.dma_start(out=nw[:], in_=self.normw[l][:, g * 256:(g + 1) * 256].broadcast_to([128, 256])))
        load_g(0)
        for g in range(8):
            wb = self.wssm_b[g % 2]
            wk = ('wssm', 0, 0)
            wkall = [('wssm', 0, 0)]
            nw = self.nwb[g % 2]
            nk = ('nwb', g % 2)
            if g > 0:
                load_g(g)
            hs = slice(4 * g, 4 * g + 4)
            prevB = None
            for c in range(8):
                cols = slice(c * 256, (c + 1) * 256)
                sl = c % 2
                bcb, xs_tok, xw, btok = self.bcb2[sl], self.xs_tok2[sl], self.xw2[sl], self.btok2[sl]
                kbcb, kxs, kxw, kbt = ('bcb', sl), ('xs_tok', sl), ('xw', sl), ('btok', sl)
                F1, F2, Bq = [], [], []
                self.rec = F1
                xkc = xk[2 * c:2 * c + 2]
                ue = self.uext[c % 2]
                uk = ('uext', c % 2)
                self.mmseq(wkall + xkc, [('PA', 0)],
                           [(P0[:, j * 256:(j + 1) * 256], [(wb[:, kc, 256 + j * 128:256 + (j + 1) * 128], self.xT[:, kc, cols]) for kc in range(8)]) for j in range(2)])
                self.mmseq(wkall + xkc, [('PA', 1)],
                           [(P1[:, j * 256:(j + 1) * 256], [(wb[:, kc, 512 + j * 128:512 + (j + 1) * 128], self.xT[:, kc, cols]) for kc in range(8)]) for j in range(2)])
                if c == 0:
                    S.op('pool', [], [uk], lambda e, bcb=bcb, xs_tok=xs_tok, xw=xw, btok=btok, ue=ue: e.memset(ue[:, :, 0:3], 0.0))
                else:
                    up = self.uext[(c - 1) % 2]
                    S.op('pool', [('uext', (c - 1) % 2)], [uk], lambda e, bcb=bcb, xs_tok=xs_tok, xw=xw, btok=btok, ue=ue, up=up: e.tensor_copy(ue[:, :, 0:3], up[:, :, 256:259]))
                S.op('act', [('PA', 0)], [uk], lambda e, bcb=bcb, xs_tok=xs_tok, xw=xw, btok=btok, ue=ue: e.activation(ue[:, 0:2, 3:259], P0.rearrange("p (j t) -> p j t", j=2), AF.Copy))
                S.op('act', [('PA', 1)], [uk], lambda e, bcb=bcb, xs_tok=xs_tok, xw=xw, btok=btok, ue=ue: e.activation(ue[:, 2:4, 3:259], P1.rearrange("p (j t) -> p j t", j=2), AF.Copy))
                for ci in range(4):
                    eng = 'dve'
                    w = self.cwb[:, g, ci, :]
                    o = self.xh[:, ci, :]
                    S.op(eng, [uk, 'cwb'], [('xh', ci)], lambda e, bcb=bcb, xs_tok=xs_tok, xw=xw, btok=btok, ci=ci, ue=ue, w=w, o=o: e.tensor_scalar(o, ue[:, ci, 0:256], w[:, 0:1], w[:, 4:5], ALU.mult, ALU.add))
                    for k in range(1, 4):
                        if eng == 'dve':
                            S.op('dve', [uk, 'cwb', ('xh', ci)], [('xh', ci)], lambda e, bcb=bcb, xs_tok=xs_tok, xw=xw, btok=btok, ci=ci, ue=ue, w=w, o=o, k=k: e.scalar_tensor_tensor(
                                o, in0=ue[:, ci, k:k + 256], scalar=w[:, k:k + 1], in1=o, op0=ALU.mult, op1=ALU.add))
                        else:
                            S.op('pool', [uk, 'cwb'], [('tnh', ci)], lambda e, bcb=bcb, xs_tok=xs_tok, xw=xw, btok=btok, ci=ci, ue=ue, w=w, k=k: e.tensor_scalar(self.tnh[:, ci, :], ue[:, ci, k:k + 256], w[:, k:k + 1], None, ALU.mult))
                            S.op('pool', [('tnh', ci), ('xh', ci)], [('xh', ci)], lambda e, bcb=bcb, xs_tok=xs_tok, xw=xw, btok=btok, ci=ci, o=o: e.tensor_tensor(o, o, self.tnh[:, ci, :], ALU.add))
                S.op('act', [('xh', ci) for ci in range(4)], [('tnh', ci) for ci in range(4)], lambda e, bcb=bcb, xs_tok=xs_tok, xw=xw, btok=btok: e.activation(self.tnh[:], self.xh[:], AF.Tanh))
                S.op('dve', [('tnh', 0), ('tnh', 1), ('xh', 0), ('xh', 1)], [('xh', 0), ('xh', 1)], lambda e, bcb=bcb, xs_tok=xs_tok, xw=xw, btok=btok: e.scalar_tensor_tensor(self.xsf[:], in0=self.tnh[:, 0:2, :], scalar=1.0, in1=self.xh[:, 0:2, :], op0=ALU.add, op1=ALU.mult))
                S.op('dve', [('tnh', 2), ('tnh', 3), ('xh', 2), ('xh', 3)], [kbcb], lambda e, bcb=bcb, xs_tok=xs_tok, xw=xw, btok=btok: e.scalar_tensor_tensor(bcb[:], in0=self.tnh[:, 2:4, :], scalar=1.0, in1=self.xh[:, 2:4, :], op0=ALU.add, op1=ALU.mult))
                S.op('dve', [('tnh', 2), ('xh', 2)], ['bcf'], lambda e, bcb=bcb, xs_tok=xs_tok, xw=xw, btok=btok: e.scalar_tensor_tensor(self.bcf[:], in0=self.tnh[:, 2, :], scalar=1.0, in1=self.xh[:, 2, :], op0=ALU.add, op1=ALU.mult))
                self.tps([('xh', 0), ('xh', 1), 'cst'], [('PA', 0)],
                         [(P0[:, si * 256 + j * 128:si * 256 + (j + 1) * 128], self.xsf[:, j, si * 128:(si + 1) * 128]) for si in range(2) for j in range(2)], self.ident_f)
                self.tps(['bcf', 'cst'], [('PA', 1)],
                         [(P1[:, si * 128:(si + 1) * 128], self.bcf[:, si * 128:(si + 1) * 128]) for si in range(2)], self.ident_f)
                P0v = P0.rearrange("p (s h d) -> p s h d", s=2, h=4)
                S.op('act', [('PA', 0)], [kxs], lambda e, bcb=bcb, xs_tok=xs_tok, xw=xw, btok=btok: e.activation(xs_tok[:].rearrange("p s c -> p (s c)"), P0, AF.Copy))
                S.op('dve', [('PA', 0), 'dt_all'], ['xdt'], lambda e, bcb=bcb, xs_tok=xs_tok, xw=xw, btok=btok, c=c, hs=hs: e.tensor_tensor(
                    self.xdt[:].rearrange("p s (h d) -> p s h d", h=4), P0v,
                    self.dt_all[:, 2 * c:2 * c + 2, hs].unsqueeze(3).broadcast_to([128, 2, 4, 64]), ALU.mult))
                S.op('dve', [('PA', 0), 'dst_all'], [kxw], lambda e, bcb=bcb, xs_tok=xs_tok, xw=xw, btok=btok, c=c, hs=hs: e.tensor_tensor(
                    xw[:].rearrange("p s (h d) -> p s h d", h=4), P0v,
                    self.dst_all[:, 2 * c:2 * c + 2, hs].unsqueeze(3).broadcast_to([128, 2, 4, 64]), ALU.mult))
                S.op('act', [('PA', 1)], [kbt], lambda e, bcb=bcb, xs_tok=xs_tok, xw=xw, btok=btok: e.activation(btok[:].rearrange("p s n -> p (s n)"), P1[:, 0:256], AF.Copy))
                self.mmseq([kbcb], [('PA', 3)],
                           [(P3[:, 0:256], [(bcb[:, 0, 0:128], bcb[:, 1, :])]),
                            (P3[:, 384:512], [(bcb[:, 0, 128:256], bcb[:, 1, 128:256])])])
                S.op('dve', [('PA', 3), 'cst'], ['cb3'], lambda e, bcb=bcb, xs_tok=xs_tok, xw=xw, btok=btok: e.tensor_tensor(self.cb3[:, 0, :], P3[:, 0:128], self.tri_f, ALU.mult))
                S.op('act', [('PA', 3)], ['cb3'], lambda e, bcb=bcb, xs_tok=xs_tok, xw=xw, btok=btok: e.activation(self.cb3[:, 1, :], P3[:, 128:256], AF.Copy))
                S.op('dve', [('PA', 3), 'cst', 'cb3'], ['cb3'], lambda e, bcb=bcb, xs_tok=xs_tok, xw=xw, btok=btok: e.tensor_tensor(self.cb3[:, 2, :], P3[:, 384:512], self.tri_f, ALU.mult))
                self.rec = F2
                for hh in range(4):
                    hg = 4 * g + hh
                    bank = hh % 2
                    pb = P4[:, bank * 256:(bank + 1) * 256]
                    pbk = ('PB', 0, bank)
                    a0 = self.adt_all[:, 2 * c, hg:hg + 1].broadcast_to([128, 128])
                    a1 = self.adt_all[:, 2 * c + 1, hg:hg + 1].broadcast_to([128, 128])
                    self.mmseq(['adt_all', 'cst'], [pbk],
                               [(pb[:, 0:128], [(a0, self.tri_f)]),
                                (pb[:, 128:256], [(a0, self.ones_f), (a1, self.tri_f)])])
                    dtm = self.dtmp[hh % 2]
                    dk = ('dtmp', hh % 2)
                    mt = self.mt[hh % 2]
                    mk = ('mt', hh % 2)
                    ac0 = self.acs_all[:, 2 * c, hg:hg + 1]
                    ac1 = self.acs_all[:, 2 * c + 1, hg:hg + 1]

                    def dec(e, dtm=dtm, pb=pb, ac0=ac0, ac1=ac1):
                        e.tensor_scalar(dtm[:, 0, :], pb[:, 0:128], ac0, 0.0, ALU.subtract, ALU.min)
                        e.tensor_scalar(dtm[:, 1, :], pb[:, 128:256], ac0, 0.0, ALU.subtract, ALU.min)
                        return e.tensor_scalar(dtm[:, 2, :], pb[:, 128:256], ac1, 0.0, ALU.subtract, ALU.min)
                    S.op('dve', [pbk, 'acs_all'], [dk], dec)
                    S.op('act', [dk], [dk], lambda e, bcb=bcb, xs_tok=xs_tok, xw=xw, btok=btok, dtm=dtm: e.activation(dtm[:].rearrange("p a b -> p (a b)"), dtm[:].rearrange("p a b -> p (a b)"), AF.Exp))
                    S.op('dve', [dk, 'cb3'], [mk], lambda e, bcb=bcb, xs_tok=xs_tok, xw=xw, btok=btok, dtm=dtm, mt=mt: e.tensor_tensor(mt[:], dtm[:], self.cb3[:], ALU.mult))
                    self.mmseq([mk, 'xdt'], [('PB', 1, hh)],
                               [(P5[:, hh * 64:(hh + 1) * 64], [(mt[:, 0, :], self.xdt[:, 0, hh * 64:(hh + 1) * 64])]),
                                (P5[:, 256 + hh * 64:256 + (hh + 1) * 64], [(mt[:, 1, :], self.xdt[:, 0, hh * 64:(hh + 1) * 64]), (mt[:, 2, :], self.xdt[:, 1, hh * 64:(hh + 1) * 64])])])
                self.rec = Bq
                self.mmseq(wkall + xkc, [('PA', 2)],
                           [(P2[:, si * 256:(si + 1) * 256], [(self.xT[:, kc, c * 256 + si * 128:c * 256 + (si + 1) * 128], wb[:, kc, 0:256]) for kc in range(8)]) for si in range(2)])
                S.op('act', [('PA', 2)], ['tz'], lambda e, bcb=bcb, xs_tok=xs_tok, xw=xw, btok=btok: e.activation(self.tz[:].rearrange("p s c -> p (s c)"), P2, AF.Tanh, scale=0.5))
                S.op('dve', ['tz', ('PA', 2)], ['tz'], lambda e, bcb=bcb, xs_tok=xs_tok, xw=xw, btok=btok: e.scalar_tensor_tensor(self.zs[:].rearrange("p s c -> p (s c)"), in0=self.tz[:].rearrange("p s c -> p (s c)"), scalar=1.0, in1=P2, op0=ALU.add, op1=ALU.mult))
                p5k = [('PB', 1, hh) for hh in range(4)]
                if c > 0:
                    self.mmseq([kbcb, 'prev_b'], ['PC'],
                               [(P6[:, si * 256:(si + 1) * 256], [(bcb[:, 1, si * 128:(si + 1) * 128], self.prev_b[:])]) for si in range(2)])
                    S.op('dve', ['PC', 'ea_all'], ['yt2'], lambda e, bcb=bcb, xs_tok=xs_tok, xw=xw, btok=btok, c=c, hs=hs: e.tensor_tensor(
                        self.yt2[:].rearrange("p s (h d) -> p s h d", h=4), P6.rearrange("p (s h d) -> p s h d", s=2, h=4),
                        self.ea_all[:, 2 * c:2 * c + 2, hs].unsqueeze(3).broadcast_to([128, 2, 4, 64]), ALU.mult))
                if c < 7:
                    self.mm([kbt, kxw], [('PB', 0, 0)], P4[:, 0:256], [(btok[:, si, :], xw[:, si, :]) for si in range(2)])
                    if c == 0:
                        S.op('act', [('PB', 0, 0)], ['prev_f'], lambda e, bcb=bcb, xs_tok=xs_tok, xw=xw, btok=btok: e.activation(self.prev_f[:], P4[:, 0:256], AF.Copy))
                    else:
                        S.op('dve', ['prev_f', 'cdb'], ['prev_f'], lambda e, bcb=bcb, xs_tok=xs_tok, xw=xw, btok=btok, c=c, hs=hs: e.tensor_tensor(
                            self.prev_f[:].rearrange("p (h d) -> p h d", h=4), self.prev_f[:].rearrange("p (h d) -> p h d", h=4),
                            self.cdb[:, c, hs].unsqueeze(2).broadcast_to([128, 4, 64]), ALU.mult))
                        S.op('dve', ['prev_f', ('PB', 0, 0)], ['prev_f'], lambda e, bcb=bcb, xs_tok=xs_tok, xw=xw, btok=btok: e.tensor_tensor(self.prev_f[:], self.prev_f[:], P4[:, 0:256], ALU.add))
                    S.op('act', ['prev_f'], ['prev_b'], lambda e, bcb=bcb, xs_tok=xs_tok, xw=xw, btok=btok: e.activation(self.prev_b[:], self.prev_f[:], AF.Copy))
                S.op('pool', [kxs, 'smallb'], ['yv'], lambda e, bcb=bcb, xs_tok=xs_tok, xw=xw, btok=btok, hs=hs: e.tensor_tensor(
                    self.yv[:].rearrange("p s (h d) -> p s h d", h=4), xs_tok[:].rearrange("p s (h d) -> p s h d", h=4),
                    dsk[:, hs].unsqueeze(1).unsqueeze(3).broadcast_to([128, 2, 4, 64]), ALU.mult))
                S.op('dve', p5k + ['yv'], ['yv'], lambda e, bcb=bcb, xs_tok=xs_tok, xw=xw, btok=btok: e.tensor_tensor(self.yv[:].rearrange("p s c -> p (s c)"), P5, self.yv[:].rearrange("p s c -> p (s c)"), ALU.add))
                if c > 0:
                    S.op('dve', ['yv', 'yt2'], ['yv'], lambda e, bcb=bcb, xs_tok=xs_tok, xw=xw, btok=btok: e.tensor_tensor(self.yv[:], self.yv[:], self.yt2[:], ALU.add))
                S.op('dve', ['yv', 'tz'], ['yv'], lambda e, bcb=bcb, xs_tok=xs_tok, xw=xw, btok=btok: e.tensor_tensor(self.h2[:], self.yv[:], self.zs[:], ALU.mult))
                for si in range(2):
                    S.op('act', ['yv'], [('tnh', 0), ('ss', si)], lambda e, bcb=bcb, xs_tok=xs_tok, xw=xw, btok=btok, si=si: e.activation(self.junk[:], self.h2[:, si, :], AF.Square, accum_out=self.ss[:, si:si + 1]))
                S.op('dve', [('ss', 0), ('ss', 1)], ['rstd'], lambda e, bcb=bcb, xs_tok=xs_tok, xw=xw, btok=btok: e.tensor_scalar(self.rstd[:], self.ss[:], 1.0 / 256, 4 * RMS_EPS, ALU.mult, ALU.add))
                S.op('pool', ['rstd', 'nh'], ['rstd'], lambda e, bcb=bcb, xs_tok=xs_tok, xw=xw, btok=btok: e.tensor_tensor(self.rstd[:], self.rstd[:], self.nh[:, 0:2], ALU.pow))
                for si in range(2):
                    S.op('dve', ['yv', 'rstd', nk], ['yv'], lambda e, bcb=bcb, xs_tok=xs_tok, xw=xw, btok=btok, si=si, nw=nw: e.scalar_tensor_tensor(
                        self.yn[:, si, :], in0=self.h2[:, si, :], scalar=self.rstd[:, si:si + 1], in1=nw[:], op0=ALU.mult, op1=ALU.mult))
                self.tps(['yv', 'cst'], ['PD'],
                         [(P7[:, j * 256 + si * 128:j * 256 + (si + 1) * 128], self.yn[:, si, j * 128:(j + 1) * 128]) for j in range(2) for si in range(2)], self.ident_f)
                S.op('act', ['PD'], [('yT', g, c)], lambda e, bcb=bcb, xs_tok=xs_tok, xw=xw, btok=btok, g=g, cols=cols: e.activation(self.yT[:, 2 * g:2 * g + 2, cols], P7.rearrange("p (j t) -> p j t", j=2), AF.Copy))
                self.rec = None
                if prevB is None:
                    for t in F1:
                        t()
                else:
                    k = max(1, -(-len(F1) // max(1, len(prevB))))
                    bi = 0
                    for i, t in enumerate(F1):
                        t()
                        if (i + 1) % k == 0 and bi < len(prevB):
                            prevB[bi]()
                            bi += 1
                    while bi < len(prevB):
                        prevB[bi]()
                        bi += 1
                for t in F2:
                    t()
                prevB = Bq
            for t in prevB:
                t()
        if 'yT' in self.dbg:
            if True:
                self.dbgf = self.sb("dbgf", [128, 512])
            for kc in range(16):
                for q4 in range(4):
                    S.op('dve', [('yT', g, c) for g in range(8) for c in range(8)] + [('dbg', 'yT')], ['dbgf'], lambda e, kc=kc, q4=q4: e.tensor_copy(self.dbgf[:, 0:512], self.yT[:, kc, q4 * 512:(q4 + 1) * 512]))
                    S.dma('sp', ['dbgf'], [('dbg', 'yT')], lambda e, kc=kc, q4=q4: e.dma_start(out=self.dbg['yT'][kc * 128:(kc + 1) * 128, q4 * 512:(q4 + 1) * 512], in_=self.dbgf[:, 0:512]))
            self.final_keys.append(('dbg', 'yT'))

    def merge(self, l):
        self.psum_epoch()
        S, sb = self.S, self.sb
        self.phase([self.R_W])
        if True:
            self.wc1_b = [sb("wc1b%d" % i, [128, 5120], BF16) for i in range(2)]
            self.ta = [sb("ta%d" % i, [128, 512]) for i in range(2)]
            self.tb = [sb("tb%d" % i, [128, 512]) for i in range(2)]
        mT = self.B3
        xk = self.xT_keys()
        attk = [('attT', hp, h) for hp in range(8) for h in range(2)]
        yk = [('yT', g, c) for g in range(8) for c in range(8)]
        it = 0
        def load_c(fc):
            wb = self.wc1_b[fc % 2]
            S.dma('pool', [], [('wc1', fc % 2, 0)], lambda e, wb=wb, fc=fc: e.dma_start(
                out=wb[:].rearrange("p (a b) -> p a b", a=5), in_=self.wc1[l, fc].rearrange("p (a b) -> p a b", a=5), max_dma_last_dim=4096))
        load_c(0)
        for fc in range(8):
            wb = self.wc1_b[fc % 2]
            wks = [('wc1', fc % 2, 0)]
            if fc + 1 < 8:
                load_c(fc + 1)
            wg = wb[:, 0:2048].rearrange("p (k n) -> p k n", k=8)
            wa = wb[:, 2048:3072].rearrange("p (k n) -> p k n", k=8)
            ws = wb[:, 3072:5120].rearrange("p (k n) -> p k n", k=16)
            for tc in range(4):
                cols = slice(tc * 512, (tc + 1) * 512)
                if it % 2 == 0:
                    pga, pgs, ppa, pps = (self.PA[:, i * 512:(i + 1) * 512] for i in range(4))
                    bk = [('PA', 0), ('PA', 1), ('PA', 2), ('PA', 3)]
                else:
                    pga, pgs, ppa, pps = self.PB[:, 0:512], self.PB[:, 512:1024], self.PC[:, 0:512], self.PD[:, 0:512]
                    bk = [('PB', 0), ('PB', 1), 'PC', 'PD']
                self.mm(wks + xk[tc * 4:(tc + 1) * 4], [bk[0]], pga, [(wg[:, kc, 0:128], self.xT[:, kc, cols]) for kc in range(8)])
                self.mm(wks + xk[tc * 4:(tc + 1) * 4], [bk[1]], pgs, [(wg[:, kc, 128:256], self.xT[:, kc, cols]) for kc in range(8)])
                self.mm(wks + attk, [bk[2]], ppa, [(wa[:, kc, :], self.B1[:, kc, cols]) for kc in range(8)])
                self.mm(wks + yk, [bk[3]], pps, [(ws[:, kc, :], self.yT[:, kc, cols]) for kc in range(16)])
                ta, tb = self.ta[it % 2], self.tb[it % 2]
                tak, tbk = ('ta', it % 2), ('tb', it % 2)
                it += 1
                S.op('act', [bk[0]], [tak], lambda e, ta=ta, pga=pga: e.activation(ta[:], pga, AF.Tanh, scale=0.5))
                S.op('act', [bk[1]], [tbk], lambda e, tb=tb, pgs=pgs: e.activation(tb[:], pgs, AF.Tanh, scale=0.5))
                S.op('dve', [tak, bk[2]], [tak], lambda e, ta=ta, ppa=ppa: e.scalar_tensor_tensor(ta[:], in0=ta[:], scalar=1.0, in1=ppa, op0=ALU.add, op1=ALU.mult))
                S.op('dve', [tbk, bk[3]], [tbk], lambda e, tb=tb, pps=pps: e.scalar_tensor_tensor(tb[:], in0=tb[:], scalar=1.0, in1=pps, op0=ALU.add, op1=ALU.mult))
                S.op('dve', [tak, tbk], [('mergedT', fc, tc)], lambda e, ta=ta, tb=tb, fc=fc, cols=cols: e.tensor_tensor(mT[:, fc, cols], ta[:], tb[:], ALU.add))

    def layernorm(self, t, tk, g_b, b_b, gk, i):
        S = self.S
        st = self.ln_st[i % 2]
        mv = self.ln_mv[i % 2]
        sk = ('lnst', i % 2)

        def stats(e):
            e.bn_stats(st[:, 0, :], t[:, 0:512])
            return e.bn_stats(st[:, 1, :], t[:, 512:1024])
        S.op('dve', [tk], [sk], stats)
        S.op('dve', [sk], [sk], lambda e: e.bn_aggr(mv[:, 0:2], st[:].rearrange("p a b -> p (a b)")))
        S.op('dve', [sk], [sk], lambda e: e.tensor_scalar(mv[:, 2:3], mv[:, 1:2], LN_EPS, None, ALU.add))
        S.op('pool', [sk, 'nh'], [sk], lambda e: e.tensor_tensor(mv[:, 3:4], mv[:, 2:3], self.nh[:, 0:1], ALU.pow))
        S.op('dve', [tk, sk], [tk], lambda e: e.tensor_scalar(t[:], t[:], mv[:, 0:1], mv[:, 3:4], ALU.subtract, ALU.mult))
        S.op('dve', [tk, gk], [tk], lambda e: e.tensor_tensor(t[:], t[:], g_b, ALU.mult))
        S.op('dve', [tk, gk], [tk], lambda e: e.tensor_tensor(t[:], t[:], b_b, ALU.add))

    def mix_ln1(self, l):
        self.psum_epoch()
        S, sb = self.S, self.sb
        self.phase([self.R_W, self.R_B2])
        if True:
            self.wout_b = sb("woutb", [128, 8, 1024], BF16)
            self.lnpb = sb("lnpb", [128, 4096])
            self.res = [sb("res%d" % i, [128, D]) for i in range(2)]
            self.ln_st = [sb("lnst%d" % i, [128, 2, 6]) for i in range(2)]
            self.ln_mv = [sb("lnmv%d" % i, [128, 4]) for i in range(2)]
        S.dma('pool', [], [('wout', 0)], lambda e: e.dma_start(
            out=self.wout_b[:], in_=self.wout[l].rearrange("p (k n) -> p k n", k=8), max_dma_last_dim=4096))
        lnpb1 = self.lnpb
        S.dma('sp', [], ['lnpb'], lambda e: e.dma_start(out=lnpb1[:], in_=self.lnp[l].broadcast_to([128, 4096])))
        src = self.x_in if l == 0 else self.x1
        srck = 'x_in' if l == 0 else 'x1'
        mk = [('mergedT', fc, tc) for fc in range(8) for tc in range(4)]
        wk = [('wout', 0)]
        for tt in range(16):
            r = self.res[tt % 2]
            rk = ('res', tt % 2)
            S.dma('sp', [(srck, tt)], [rk], lambda e, r=r, tt=tt: e.dma_start(out=r[:], in_=src[tt * 128:(tt + 1) * 128, :]))
            S.op('act', [rk], [rk], lambda e, r=r: e.activation(r[:], r[:], AF.Copy, scale=float(ALPHA)))
            if tt % 2 == 0:
                pm = self.PB[:, 0:1024]
                rkeys = [('PB', 0), ('PB', 1)]
            else:
                pm = self.PA[:, 0:1024]
                rkeys = [('PA', 0), ('PA', 1)]
            self.mmseq(mk + wk, rkeys,
                       [(pm[:, half * 512:(half + 1) * 512], [(self.B3[:, kc, tt * 128:(tt + 1) * 128], self.wout_b[:, kc, half * 512:(half + 1) * 512]) for kc in range(8)]) for half in range(2)])
            S.op('dve', rkeys + [rk], [rk], lambda e, r=r, pm=pm: e.scalar_tensor_tensor(r[:], in0=pm, scalar=0.5, in1=r[:], op0=ALU.mult, op1=ALU.add))
            self.layernorm(r, rk, self.lnpb[:, 0:1024], self.lnpb[:, 1024:2048], 'lnpb', tt)
            S.dma('sp', [rk], [('hres', tt)], lambda e, r=r, tt=tt: e.dma_start(out=self.hres[tt * 128:(tt + 1) * 128, :], in_=r[:]))
            self.to_featmajor2(r, rk, self.B1, 'hT', tt)
        if 'hT' in self.dbg:
            if True:
                self.dbgf = self.sb("dbgf", [128, 512])
            for kc in range(8):
                for q4 in range(4):
                    S.op('dve', [('hT', tt) for tt in range(16)] + [('dbg', 'hT')], ['dbgf'], lambda e, kc=kc, q4=q4: e.tensor_copy(self.dbgf[:, 0:512], self.B1[:, kc, q4 * 512:(q4 + 1) * 512]))
                    S.dma('sp', ['dbgf'], [('dbg', 'hT')], lambda e, kc=kc, q4=q4: e.dma_start(out=self.dbg['hT'][kc * 128:(kc + 1) * 128, q4 * 512:(q4 + 1) * 512], in_=self.dbgf[:, 0:512]))
            self.final_keys.append(('dbg', 'hT'))

    def to_featmajor2(self, tile, tkey, dstT, dkey, tt):
        S = self.S
        for half in range(2):
            ps = self.PC[:, 0:512] if half == 0 else self.PD[:, 0:512]
            pk = 'PC' if half == 0 else 'PD'
            self.tps([tkey, 'cst'], [pk], [(ps[:, j * 128:(j + 1) * 128], tile[:, (half * 4 + j) * 128:(half * 4 + j + 1) * 128]) for j in range(4)], self.ident_f)
            S.op('act', [pk], [(dkey, tt)], lambda e, ps=ps, half=half: e.activation(
                dstT[:, half * 4:(half + 1) * 4, tt * 128:(tt + 1) * 128], ps.rearrange("p (c t) -> p c t", c=4), AF.Copy))

    def ffn(self, l):
        self.psum_epoch()
        S, sb = self.S, self.sb
        self.phase([self.R_W, (self.R_B3[0] + 4096, self.R_B3[1])])
        if True:
            self.wup_b = [sb("wupb%d" % i, [128, 8, 512], BF16) for i in range(2)]
            self.wdn_b = [sb("wdnb%d" % i, [128, 4, 1024], BF16) for i in range(2)]
            self.rl = [sb("rl%d" % i, [128, 512]) for i in range(2)]
            self.ln_st = [sb("lnst%d" % i, [128, 2, 6]) for i in range(2)]
            self.ln_mv = [sb("lnmv%d" % i, [128, 4]) for i in range(2)]
        uT = self.B3
        hk = [('hT', tt) for tt in range(16)]
        it = 0
        def load_f(gi):
            wu = self.wup_b[gi % 2]
            wd = self.wdn_b[gi % 2]
            wuk, wdk = ('wup', gi % 2), ('wdn', gi % 2)
            S.dma('pool', [], [(wuk, 0)], lambda e, wu=wu, gi=gi: e.dma_start(
                out=wu[:], in_=self.wup[l, gi].rearrange("p (k n) -> p k n", k=8), max_dma_last_dim=4096))
            S.dma('pool', [], [(wdk, 0)], lambda e, wd=wd, gi=gi: e.dma_start(
                out=wd[:], in_=self.wdn[l, gi].rearrange("p (k n) -> p k n", k=4), max_dma_last_dim=4096))
        load_f(0)
        for gi in range(8):
            wu = self.wup_b[gi % 2]
            wd = self.wdn_b[gi % 2]
            wuk, wdk = ('wup', gi % 2), ('wdn', gi % 2)
            if gi + 1 < 8:
                load_f(gi + 1)
            wuks = [(wuk, 0)]
            wdks = [(wdk, 0)]
            for j in range(4):
                for tc in range(4):
                    bank = it % 4
                    ps = [self.PA[:, 0:512], self.PA[:, 512:1024], self.PC[:, 0:512], self.PD[:, 0:512]][bank]
                    pk = [('PA', 0), ('PA', 1), 'PC', 'PD'][bank]
                    cols = slice(tc * 512, (tc + 1) * 512)
                    self.mm(wuks + hk[tc * 4:(tc + 1) * 4], [pk], ps, [(wu[:, kc, j * 128:(j + 1) * 128], self.B1[:, kc, cols]) for kc in range(8)])
                    rl = self.rl[it % 2]
                    rlk = ('rl', it % 2)
                    S.op('act', [pk], [rlk], lambda e, rl=rl, ps=ps: e.activation(rl[:], ps, AF.Relu))
                    eng = 'dve'
                    S.op(eng, [rlk], [('uT', j)], lambda e, rl=rl, j=j, cols=cols: e.tensor_tensor(uT[:, j, cols], rl[:], rl[:], ALU.mult))
                    it += 1
            for tt in range(16):
                if tt % 2 == 0:
                    pm = self.PB[:, 0:1024]
                    rkeys = [('PB', 0), ('PB', 1)]
                else:
                    pm = self.PA[:, 1024:2048]
                    rkeys = [('PA', 2), ('PA', 3)]
                self.mmseq([('uT', j) for j in range(4)] + wdks, rkeys,
                           [(pm[:, half * 512:(half + 1) * 512], [(uT[:, j, tt * 128:(tt + 1) * 128], wd[:, j, half * 512:(half + 1) * 512]) for j in range(4)]) for half in range(2)])
                if gi == 0:
                    S.op('act', rkeys, [('acc', tt)], lambda e, tt=tt, pm=pm: e.activation(self.acc[:, tt, :], pm, AF.Copy))
                else:
                    S.op('dve', rkeys + [('acc', tt)], [('acc', tt)], lambda e, tt=tt, pm=pm: e.tensor_tensor(self.acc[:, tt, :], pm, self.acc[:, tt, :], ALU.add))
        S.barrier()
        self.res = [self.wup_b[i][:, 0:4, :].rearrange("p a b -> p (a b)").bitcast(F32) for i in range(2)]
        self.lnpb = self.wdn_b[0][:].rearrange("p a b -> p (a b)").bitcast(F32)
        lnpb2 = self.lnpb
        S.dma('sp', [], ['lnpb'], lambda e: e.dma_start(out=lnpb2, in_=self.lnp[l][:, 2048:4096].broadcast_to([128, 2048])))
        dst = self.out if l == DEPTH - 1 else self.x1
        dstk = 'out' if l == DEPTH - 1 else 'x1'
        for tt in range(16):
            r = self.res[tt % 2]
            rk = ('res', tt % 2)
            S.dma('sp', [('hres', tt)], [rk], lambda e, r=r, tt=tt: e.dma_start(out=r[:], in_=self.hres[tt * 128:(tt + 1) * 128, :]))
            S.op('dve', [rk, ('acc', tt)], [rk], lambda e, r=r, tt=tt: e.scalar_tensor_tensor(r[:], in0=r[:], scalar=float(ALPHA), in1=self.acc[:, tt, :], op0=ALU.mult, op1=ALU.add))
            self.layernorm(r, rk, self.lnpb[:, 0:1024], self.lnpb[:, 1024:2048], 'lnpb', tt)
            S.dma('sp', [rk], [(dstk, tt)], lambda e, r=r, tt=tt: e.dma_start(out=dst[tt * 128:(tt + 1) * 128, :], in_=r[:]))
            if l == DEPTH - 1:
                self.final_keys.append((dstk, tt))
            else:
                self.to_featmajor2(r, rk, self.xT, 'xT', tt)


def sel_max(e, b):
    ins = None
    for t in range(8):
        ins = e.max(b.m8[:, t, :], b.gm[:, t * 8:(t + 1) * 8])
    return ins


def sel_lt(e, b):
    ins = None
    for t in range(8):
        ins = e.tensor_scalar(b.lt[:, t * 8:(t + 1) * 8], b.gm[:, t * 8:(t + 1) * 8], b.m8[:, t, 2:3], NEG, ALU.is_lt, ALU.mult)
    return ins


_CACHE = {}


def kernel(**inputs):
    inp = {k: np.asarray(v) for k, v in inputs.items()}
    w = prep_weights(inp)
    consts, e8 = make_consts()
    x = np.ascontiguousarray(inp['x'], dtype=np.float32)
    pos = np.ascontiguousarray(inp['positions'], dtype=np.int32)
    nc = Builder().build()
    in_maps = []
    for b in range(NCORES):
        m = {"x": x[b], "pos": pos[b:b + 1], "consts": consts, "e8": e8}
        m.update(w)
        in_maps.append(m)
    res = run_bass_kernel_spmd(nc, in_maps, core_ids=list(range(NCORES)))
    out = np.stack([np.asarray(r["out"], dtype=np.float32) for r in res.results], axis=0)
    return out
```

```python
import math
import numpy as np
import concourse.bass as bass
import concourse.mybir as mybir
from concourse.bass_utils import run_bass_kernel_spmd

F32 = mybir.dt.float32
BF16 = mybir.dt.bfloat16
I32 = mybir.dt.int32
AF = mybir.ActivationFunctionType
ALU = mybir.AluOpType
AX = mybir.AxisListType

D = 1024
SEQ = 2048
DEPTH = 2
NCORES = 8
IN_W = 11296
ALPHA = (2 * DEPTH) ** 0.25
LN_EPS = 1e-5
RMS_EPS = 1e-5
NEG = -1.0e5
PI = math.pi

DEBUG = {}
STOP_AFTER = None
NO_BARRIER = False
VARIANT = 0


class Sched:
    ENG = ('pe', 'act', 'dve', 'pool', 'sp')

    def __init__(self, nc, n_dma_sems=24):
        self.nc = nc
        self.sem = {e: nc.alloc_semaphore('s_' + e) for e in self.ENG}
        self.cnt = {e: 0 for e in self.ENG}
        self.dsem = [nc.alloc_semaphore('d%d' % i) for i in range(n_dma_sems)]
        self.dcnt = [0] * n_dma_sems
        self.drr = {'sp': 0, 'pool': 0}
        self.dpool = {'sp': list(range(0, n_dma_sems // 2)), 'pool': list(range(n_dma_sems // 2, n_dma_sems))}
        self.waited = {}
        self.last_w = {}
        self.readers = {}
        self.q = {e: [] for e in self.ENG}

    def barrier(self):
        if NO_BARRIER is True:
            return
        self.epoch = {}
        scr = self.bar_scratch
        d_sp = {('dma', i): self.dcnt[i] for i in self.dpool['sp'] if self.dcnt[i] > 0}
        d_pl = {('dma', i): self.dcnt[i] for i in self.dpool['pool'] if self.dcnt[i] > 0}
        self._token('act', d_sp, lambda e: e.memzero(scr[:, 0:1]))
        t1 = ('act', self.cnt['act'])
        d_pl[t1[0]] = t1[1]
        self._token('pool', d_pl, lambda e: e.memset(scr[:, 1:2], 0.0))
        d3 = {e: c for e, c in self.cnt.items() if c > 0}
        self._token('dve', d3, lambda e: e.memset(scr[:, 2:3], 0.0))
        self.epoch = {'dve': self.cnt['dve']}

    def _token(self, e, deps, emit):
        waits = self._waits(e, deps)
        self.cnt[e] += 1
        sem = self.sem[e]

        def thunk(eng):
            for s, v in waits:
                eng.wait_ge(s, v)
            emit(eng).then_inc(sem, 1)
        self.q[e].append(thunk)

    def _deps(self, reads, writes):
        deps = dict(getattr(self, 'epoch', {}))

        def add(src, idx):
            if deps.get(src, 0) < idx:
                deps[src] = idx
        for k in reads:
            lw = self.last_w.get(k)
            if lw is not None:
                add(*lw)
        for k in writes:
            lw = self.last_w.get(k)
            if lw is not None:
                add(*lw)
            for src, idx in self.readers.get(k, {}).items():
                add(src, idx)
        return deps

    def _waits(self, e, deps):
        out = []
        self.maxw = getattr(self, 'maxw', {})
        for src, idx in deps.items():
            if self.waited.get((e, src), 0) >= idx:
                continue
            self.waited[(e, src)] = idx
            if isinstance(src, tuple):
                out.append((self.dsem[src[1]], 16 * idx))
            else:
                out.append((self.sem[src], idx))
        self.maxw[len(out)] = self.maxw.get(len(out), 0) + 1
        return out

    def _record(self, token, reads, writes):
        src, idx = token
        for k in reads:
            r = self.readers.setdefault(k, {})
            if r.get(src, 0) < idx:
                r[src] = idx
        for k in writes:
            self.last_w[k] = (src, idx)
            self.readers[k] = {}

    @staticmethod
    def _bank(k):
        if isinstance(k, tuple) and k and k[0] == 'PA':
            return ('BANK', k[1])
        if isinstance(k, tuple) and k and k[0] == 'PB':
            return ('BANK', 4 + k[1])
        if k in ('PC', 'PCg', 'PCo'):
            return ('BANK', 6)
        if k == 'PD':
            return ('BANK', 7)
        return None

    def _canon(self, reads, writes):
        r2, w2 = [], []
        for k in reads:
            b = self._bank(k)
            if b is None:
                r2.append(k)
            elif b not in w2:
                w2.append(b)
        for k in writes:
            b = self._bank(k)
            if b is None:
                w2.append(k)
            elif b not in w2:
                w2.append(b)
        return r2, w2

    def op(self, e, reads, writes, emit):
        reads, writes = self._canon(reads, writes)
        deps = self._deps(reads, writes)
        waits = self._waits(e, deps)
        self.cnt[e] += 1
        idx = self.cnt[e]
        sem = self.sem[e]

        def thunk(eng):
            for s, v in waits:
                eng.wait_ge(s, v)
            emit(eng).then_inc(sem, 1)
        self.q[e].append(thunk)
        self._record((e, idx), reads, writes)

    def dma(self, e, reads, writes, emit):
        pool = self.dpool[e]
        s = pool[self.drr[e] % len(pool)]
        self.drr[e] += 1
        src = ('dma', s)
        deps = self._deps(reads, writes)
        if self.dcnt[s] > 0 and deps.get(src, 0) < self.dcnt[s]:
            deps[src] = self.dcnt[s]
        waits = self._waits(e, deps)
        self.dcnt[s] += 1
        idx = self.dcnt[s]
        sem = self.dsem[s]

        def thunk(eng):
            for sm, v in waits:
                eng.wait_ge(sm, v)
            emit(eng).then_inc(sem, 16)
        self.q[e].append(thunk)
        self._record((src, idx), reads, writes)

    def alias(self, old_keys, new_keys):
        acc = {}
        for k in old_keys:
            lw = self.last_w.get(k)
            if lw is not None and acc.get(lw[0], 0) < lw[1]:
                acc[lw[0]] = lw[1]
            for src, idx in self.readers.get(k, {}).items():
                if acc.get(src, 0) < idx:
                    acc[src] = idx
        for k in new_keys:
            r = self.readers.setdefault(k, {})
            for src, idx in acc.items():
                if r.get(src, 0) < idx:
                    r[src] = idx

    def finish(self, final_keys):
        self.barrier()
        deps = self._deps(final_keys, [])
        waits = self._waits('sp', deps)

        def thunk(eng):
            for s, v in waits:
                eng.wait_ge(s, v)
        self.q['sp'].append(thunk)
        nc = self.nc
        q = self.q
        with nc.Block() as block:
            @block.sync
            def _(eng):
                for t in q['sp']:
                    t(eng)

            @block.tensor
            def _(eng):
                for t in q['pe']:
                    t(eng)

            @block.scalar
            def _(eng):
                for t in q['act']:
                    t(eng)

            @block.vector
            def _(eng):
                for t in q['dve']:
                    t(eng)

            @block.gpsimd
            def _(eng):
                for t in q['pool']:
                    t(eng)


C_ID, C_TRI, C_ONES, C_RM, C_INVF, C_SGN, C_NEGP, C_FLOOR, NC_CONST = 0, 128, 256, 384, 512, 513, 514, 578, 648


def make_consts():
    c = np.zeros((128, NC_CONST), np.float32)
    c[:, C_ID:C_ID + 128] = np.eye(128, dtype=np.float32)
    c[:, C_TRI:C_TRI + 128] = np.triu(np.ones((128, 128), np.float32))
    c[:, C_ONES:C_ONES + 128] = 1.0
    rm = np.zeros((128, 128), np.float32)
    inv = (500000.0 ** (-np.arange(0, 16, 2, dtype=np.float32) / 16)).astype(np.float32)
    for blk in range(2):
        for d in range(16):
            src = d + 8 if d < 8 else d - 8
            rm[blk * 64 + src, blk * 64 + d] = 1.0
    c[:, C_RM:C_RM + 128] = rm
    for p in range(128):
        d = p % 64
        if d < 16:
            c[p, C_INVF] = inv[d % 8]
            c[p, C_SGN] = -1.0 if d < 8 else 1.0
    for qt in range(8):
        blk = (8 + qt) // 2
        for j in range(8):
            c[:, C_NEGP + qt * 8 + j] = 0.0 if j < blk else -1.0e30
            c[:, C_FLOOR + qt * 8 + j] = NEG if j < blk else 0.0
    e8 = np.zeros((8, SEQ), np.float32)
    for j in range(8):
        e8[j, j * 256:(j + 1) * 256] = 1.0
    return c, e8


def prep_weights(inp):
    out = {}
    L = DEPTH

    def kmaj(w):
        K, N = w.shape
        return w.reshape(K // 128, 128, N).transpose(1, 0, 2)

    w_in = inp['w_in']
    wqkv = np.empty((L, 8, 128, 8, 384), np.float32)
    wssm = np.empty((L, 8, 128, 8, 768), np.float32)
    wdt = np.empty((L, 128, 8, 32), np.float32)
    wc1 = np.empty((L, 8, 128, 5120), np.float32)
    wout = np.empty((L, 128, 8, 1024), np.float32)
    wup = np.empty((L, 8, 128, 8, 512), np.float32)
    wdn = np.empty((L, 8, 128, 4, 1024), np.float32)
    cw = np.empty((L, 128, 8, 4, 5), np.float32)
    for l in range(L):
        wi = kmaj(w_in[l])
        for hp in range(8):
            wqkv[l, hp, :, :, 0:128] = wi[:, :, hp * 128:(hp + 1) * 128]
            wqkv[l, hp, :, :, 128:256] = wi[:, :, 1024 + hp * 128:1024 + (hp + 1) * 128]
            wqkv[l, hp, :, :, 256:384] = wi[:, :, 2048 + hp * 128:2048 + (hp + 1) * 128]
        for g in range(8):
            wssm[l, g, :, :, 0:256] = wi[:, :, 3072 + g * 256:3072 + (g + 1) * 256]
            wssm[l, g, :, :, 256:512] = wi[:, :, 5120 + g * 256:5120 + (g + 1) * 256]
            wssm[l, g, :, :, 512:640] = wi[:, :, 7168 + g * 128:7168 + (g + 1) * 128]
            wssm[l, g, :, :, 640:768] = wi[:, :, 8192 + g * 128:8192 + (g + 1) * 128]
        wdt[l] = wi[:, :, 9216:9248]
        wap = kmaj(inp['w_attn_proj'][l])
        wsp = kmaj(inp['w_ssm_proj'][l])
        for fc in range(8):
            blk = np.empty((128, 8, 256), np.float32)
            blk[:, :, 0:128] = wi[:, :, 9248 + fc * 128:9248 + (fc + 1) * 128]
            blk[:, :, 128:256] = wi[:, :, 10272 + fc * 128:10272 + (fc + 1) * 128]
            wc1[l, fc, :, 0:2048] = blk.reshape(128, 2048)
            wc1[l, fc, :, 2048:3072] = wap[:, :, fc * 128:(fc + 1) * 128].reshape(128, 1024)
            wc1[l, fc, :, 3072:5120] = wsp[:, :, fc * 128:(fc + 1) * 128].reshape(128, 2048)
        wout[l] = kmaj(inp['w_out'][l])
        wu = kmaj(inp['w_up'][l])
        wd = kmaj(inp['w_down'][l])
        for gI in range(8):
            wup[l, gI] = wu[:, :, gI * 512:(gI + 1) * 512]
            wdn[l, gI] = wd[:, gI * 4:(gI + 1) * 4, :]
        cwl = inp['conv_w'][l]
        cbl = inp['conv_b'][l]
        for g in range(8):
            offs = [g * 256, g * 256 + 128, 2048 + g * 128, 3072 + g * 128]
            for ci, o in enumerate(offs):
                cw[l, :, g, ci, 0:4] = cwl[:, o:o + 128].T
                cw[l, :, g, ci, 4] = cbl[o:o + 128]
    out['wqkv'] = wqkv.reshape(L, 8, 128, 3072)
    out['wssm'] = wssm.reshape(L, 8, 128, 6144)
    out['wdt'] = wdt.reshape(L, 128, 256)
    out['wc1'] = wc1
    out['wout'] = wout.reshape(L, 128, 8192)
    out['wup'] = wup.reshape(L, 8, 128, 4096)
    out['wdn'] = wdn.reshape(L, 8, 128, 4096)
    out['cw'] = cw.reshape(L, 128, 160)
    small = np.concatenate([inp['dt_bias'], inp['a_log'], inp['d_skip']], axis=1)
    out['small'] = np.ascontiguousarray(small.reshape(L, 1, 96))
    out['normw'] = np.ascontiguousarray(inp['ssm_norm_w'].reshape(L, 1, 2048))
    lnp = np.stack([inp['ln1_g'], inp['ln1_b'], inp['ln2_g'], inp['ln2_b']], axis=1)
    out['lnp'] = np.ascontiguousarray(lnp.reshape(L, 1, 4096))
    return {k: np.ascontiguousarray(v, dtype=np.float32) for k, v in out.items()}


class Arena:
    def __init__(self, base_ap, segments):
        self.base = base_ap
        self.segs = [list(x) for x in segments]

    def alloc(self, shape, dtype=F32):
        P = shape[0]
        n = 1
        for d in shape[1:]:
            n *= d
        per = 4 if dtype in (F32, I32) else 2
        ncols = (n * per + 3) // 4
        ncols = (ncols + 7) // 8 * 8
        for sg in self.segs:
            if sg[1] - sg[0] >= ncols:
                off = sg[0]
                sg[0] += ncols
                ap = self.base[0:P, off:off + (n * per + 3) // 4]
                if dtype != F32:
                    ap = ap.bitcast(dtype)
                if len(shape) == 3:
                    ap = ap.rearrange("p (a b) -> p a b", a=shape[1])
                elif len(shape) == 4:
                    ap = ap.rearrange("p (a b c) -> p a b c", a=shape[1], b=shape[2])
                return ap
        raise RuntimeError("arena out of memory for %s" % (shape,))


class _Stop(Exception):
    pass


class Builder:
    def __init__(self):
        nc = bass.Bass("TRN2", target_bir_lowering=False)
        self.nc = nc
        self.S = Sched(nc)
        self.rec = None
        S_ = self.S
        S_._op, S_._dma = S_.op, S_.dma

        def _rop(*a):
            if self.rec is not None:
                self.rec.append(lambda: S_._op(*a))
            else:
                S_._op(*a)

        def _rdma(*a):
            if self.rec is not None:
                self.rec.append(lambda: S_._dma(*a))
            else:
                S_._dma(*a)
        S_.op, S_.dma = _rop, _rdma
        L = DEPTH
        dt = lambda n, s, d=F32, k="ExternalInput": nc.dram_tensor(n, s, d, kind=k).ap()
        self.x_in = dt("x", [SEQ, D])
        self.pos_in = dt("pos", [1, SEQ], I32)
        self.consts_in = dt("consts", [128, NC_CONST])
        self.e8_in = dt("e8", [8, SEQ])
        self.wqkv = dt("wqkv", [L, 8, 128, 3072])
        self.wssm = dt("wssm", [L, 8, 128, 6144])
        self.wdt = dt("wdt", [L, 128, 256])
        self.wc1 = dt("wc1", [L, 8, 128, 5120])
        self.wout = dt("wout", [L, 128, 8192])
        self.wup = dt("wup", [L, 8, 128, 4096])
        self.wdn = dt("wdn", [L, 8, 128, 4096])
        self.cw = dt("cw", [L, 128, 160])
        self.small = dt("small", [L, 1, 96])
        self.normw = dt("normw", [L, 1, 2048])
        self.lnp = dt("lnp", [L, 1, 4096])
        self.out = dt("out", [SEQ, D], F32, "ExternalOutput")
        self.x1 = dt("x1s", [SEQ, D], F32, "Internal")
        self.hres = dt("hres", [SEQ, D], F32, "Internal")
        self.dbg = {}
        for name, shape in DEBUG.items():
            self.dbg[name] = dt("dbg_" + name, list(shape), F32, "ExternalOutput")
        self.final_keys = []

    def sb(self, name, shape, dtype=F32):
        return self.ar.alloc(list(shape), dtype)

    def phase(self, segs):
        self.S.barrier()
        self.ar = Arena(self.arena, segs)

    def mm(self, reads, writes, out, pairs, transpose=False):
        def emit(e):
            n = len(pairs)
            ins = None
            for i, (l, r) in enumerate(pairs):
                ins = e.matmul(out, lhsT=l, rhs=r, start=(i == 0), stop=(i == n - 1))
            return ins
        self.S.op('pe', reads, writes, emit)

    def mmseq(self, reads, writes, groups):
        def emit(e):
            ins = None
            for out, pairs in groups:
                n = len(pairs)
                for i, (l, r) in enumerate(pairs):
                    ins = e.matmul(out, lhsT=l, rhs=r, start=(i == 0), stop=(i == n - 1))
            return ins
        self.S.op('pe', reads, writes, emit)

    def tps(self, reads, writes, items, ident):
        def emit(e):
            ins = None
            for o, i in items:
                ins = e.transpose(o, i, ident)
            return ins
        self.S.op('pe', reads, writes, emit)

    def dump(self, name, ap, keys):
        if name in self.dbg:
            if not isinstance(keys, list):
                keys = [keys]
            self.S.dma('sp', keys, [('dbg', name)], lambda e: e.dma_start(out=self.dbg[name], in_=ap))
            self.final_keys.append(('dbg', name))

    def build(self):
        nc, S = self.nc, self.S
        sb = self.sb
        nbytes = int(nc.sbuf_bytes_remaining) - 256
        NA = nbytes // 4 // 8 * 8
        self.arena = nc.alloc_sbuf_tensor("arena", [128, NA], F32)[:]
        self.R_P = (0, 11264)
        self.R_B1 = (11264, 19456)
        self.R_B2 = (19456, 35840)
        self.R_B3 = (35840, 44032)
        self.R_W = (44032, NA)
        assert NA - 44032 > 8000, NA
        self.ar = Arena(self.arena, [self.R_P])
        self.S.bar_scratch = sb("barscr", [128, 8])
        cst = sb("cst", [128, NC_CONST])
        self.cst = cst
        S.dma('sp', [], ['cst'], lambda e: e.dma_start(out=cst[:], in_=self.consts_in))
        self.ident_f = cst[:, C_ID:C_ID + 128]
        self.tri_f = cst[:, C_TRI:C_TRI + 128]
        self.ones_f = cst[:, C_ONES:C_ONES + 128]
        cbf = sb("cbf", [128, 512], BF16)
        S.op('dve', ['cst'], ['cbf'], lambda e: e.tensor_copy(cbf[:], cst[:, 0:512]))
        self.ident_b = cbf[:, 0:128]
        self.tri_b = cbf[:, 128:256]
        self.rm_b = cbf[:, 384:512]
        self.nh = sb("nh", [128, 8])
        S.op('pool', [], ['nh'], lambda e: e.memset(self.nh[:], -0.5))
        self.ropeC = sb("ropeC", [128, SEQ], BF16)
        self.ropeS = sb("ropeS", [128, SEQ], BF16)
        self.xT = sb("xT", [128, 8, SEQ], BF16)
        A = self.arena
        self.B1 = A[:, self.R_B1[0]:self.R_B1[1]].bitcast(BF16).rearrange("p (c t) -> p c t", c=8)
        self.B2 = A[:, self.R_B2[0]:self.R_B2[1]]
        self.B3 = A[:, self.R_B3[0]:self.R_B3[1]].bitcast(BF16).rearrange("p (c t) -> p c t", c=8)
        self.yT = self.B2.bitcast(BF16).rearrange("p (c t) -> p c t", c=16)
        self.acc = self.B2.rearrange("p (t f) -> p t f", t=16)
        self.ar = Arena(self.arena, [self.R_B2])
        self.PA = nc.alloc_psum_tensor("PA", [128, 2048], F32)
        self.PB = nc.alloc_psum_tensor("PB", [128, 1024], F32)
        self.PC = nc.alloc_psum_tensor("PC", [128, 512], F32)
        self.PD = nc.alloc_psum_tensor("PD", [128, 512], F32)
        self.rope_tables()
        if STOP_AFTER == 'rope':
            return self.finish()
        self.load_xT()
        if STOP_AFTER == 'xT':
            self.dbg_tile('xT0', self.xT[:, 3, 1024:1536], self.xT_keys(), 512)
            return self.finish()
        try:
            for l in range(DEPTH):
                self.layer(l)
                if STOP_AFTER == 'layer0':
                    break
        except _Stop:
            pass
        return self.finish()

    def finish(self):
        self.S.finish(self.final_keys)
        return self.nc

    def rope_tables(self):
        S, sb = self.S, self.sb
        cst = None
        posi = sb("posi", [128, SEQ], I32)
        S.dma('sp', [], ['posi'], lambda e: e.dma_start(out=posi[:], in_=self.pos_in.broadcast_to([128, SEQ])))
        ang = sb("ang", [128, SEQ])
        tmp = sb("rtmp", [128, SEQ])
        tmi = sb("rtmi", [128, SEQ], I32)
        rc = sb("rc", [128, SEQ])
        rs = sb("rs", [128, SEQ])
        invf = self.cst[:, C_INVF:C_INVF + 1]
        sgn = self.cst[:, C_SGN:C_SGN + 1]
        S.op('dve', ['posi'], ['ang'], lambda e: e.tensor_copy(ang[:], posi[:]))
        S.op('dve', ['ang', 'cst'], ['ang'], lambda e: e.tensor_scalar(ang[:], ang[:], invf, None, ALU.mult))
        for which, dst in ((0, rs), (1, rc)):
            off = 0.0 if which == 0 else PI / 2
            S.op('dve', ['ang'], ['rtmp'], lambda e, off=off: e.tensor_scalar(tmp[:], ang[:], off, 1.0 / (2 * PI), ALU.add, ALU.mult))
            S.op('dve', ['rtmp'], ['rtmi'], lambda e: e.tensor_copy(tmi[:], tmp[:]))
            S.op('dve', ['rtmi'], ['rtmp'], lambda e: e.tensor_copy(tmp[:], tmi[:]))
            S.op('dve', ['rtmp', 'ang'], ['rtmp'], lambda e: e.scalar_tensor_tensor(tmp[:], in0=tmp[:], scalar=-2 * PI, in1=ang[:], op0=ALU.mult, op1=ALU.add))
            S.op('dve', ['rtmp'], [('rope', which)], lambda e, off=off, dst=dst: e.tensor_scalar(dst[:], tmp[:], off, None, ALU.add))
            S.op('dve', [('rope', which)], ['rtmp'], lambda e, dst=dst: e.tensor_scalar(tmp[:], dst[:], PI, -2 * PI, ALU.is_gt, ALU.mult))
            S.op('dve', ['rtmp', ('rope', which)], [('rope', which)], lambda e, dst=dst: e.tensor_tensor(dst[:], dst[:], tmp[:], ALU.add))
            S.op('dve', [('rope', which)], ['rtmp'], lambda e, dst=dst: e.tensor_scalar(tmp[:], dst[:], -PI, 2 * PI, ALU.is_lt, ALU.mult))
            S.op('dve', ['rtmp', ('rope', which)], [('rope', which)], lambda e, dst=dst: e.tensor_tensor(dst[:], dst[:], tmp[:], ALU.add))
            S.op('dve', [('rope', which)], [('rope', which)], lambda e, dst=dst: e.tensor_scalar(dst[:], dst[:], 3.1415925, -3.1415925, ALU.min, ALU.max))
            S.op('act', [('rope', which)], [('rope', which)], lambda e, dst=dst: e.activation(dst[:], dst[:], AF.Sin))
        S.op('dve', [('rope', 0), 'cst'], [('rope', 0)], lambda e: e.tensor_scalar(self.ropeS[:], rs[:], sgn, None, ALU.mult))
        S.op('dve', [('rope', 1)], [('rope', 1)], lambda e: e.tensor_copy(self.ropeC[:], rc[:]))
        self.dump('ropeC', rc[:], ('rope', 1))
        self.dump('ropeS', rs[:], ('rope', 0))

    def load_xT(self):
        S, sb = self.S, self.sb
        self.xt_buf = [sb("xtile%d" % i, [128, D]) for i in range(2)]
        for tt in range(16):
            xt = self.xt_buf[tt % 2]
            k = ('xtile', tt % 2)
            S.dma('sp', [], [k], lambda e, xt=xt, tt=tt: e.dma_start(out=xt[:], in_=self.x_in[tt * 128:(tt + 1) * 128, :]))
            self.to_featmajor(xt, k, self.xT, 'xT', tt)

    def to_featmajor(self, tile, tkey, dstT, dkey, tt):
        S = self.S
        for half in range(2):
            ps = self.PA[:, half * 512:(half + 1) * 512]
            pk = ('PA', half)
            self.tps([tkey, 'cst'], [pk], [(ps[:, j * 128:(j + 1) * 128], tile[:, (half * 4 + j) * 128:(half * 4 + j + 1) * 128]) for j in range(4)], self.ident_f)
            eng = 'act' if half == 0 else 'dve'
            if eng == 'act':
                S.op('act', [pk], [(dkey, tt)], lambda e, ps=ps, half=half: e.activation(
                    dstT[:, half * 4:(half + 1) * 4, tt * 128:(tt + 1) * 128], ps.rearrange("p (c t) -> p c t", c=4), AF.Copy))
            else:
                S.op('dve', [pk], [(dkey, tt)], lambda e, ps=ps, half=half: e.tensor_copy(
                    dstT[:, half * 4:(half + 1) * 4, tt * 128:(tt + 1) * 128], ps.rearrange("p (c t) -> p c t", c=4)))

    def layer(self, l):
        self.attention(l)
        if STOP_AFTER == 'att':
            raise _Stop()
        self.ssd(l)
        if STOP_AFTER == 'ssd':
            raise _Stop()
        self.merge(l)
        if STOP_AFTER == 'merge':
            raise _Stop()
        self.mix_ln1(l)
        if STOP_AFTER == 'ln1':
            raise _Stop()
        self.ffn(l)

    def psum_epoch(self):
        keys = [('PA', i) for i in range(4)] + [('PB', 0), ('PB', 1), ('PB', 0, 0), ('PB', 0, 1)] + \
               [('PB', 1, h) for h in range(4)] + ['PC', 'PCg', 'PCo', 'PD']
        self.S.alias(keys, keys)

    def dbg_qk(self, h):
        S = self.S
        if 'qa0' in self.dbg:
            qa, ka = self.qa[h], self.ka[h]
            dq = self.sb("dbgq", [128, SEQ])
            dk = self.sb("dbgk", [128, SEQ])
            S.op('dve', [('qa', h)], ['dbgq'], lambda e: e.tensor_copy(dq[0:72, :], qa[0:72, :]))
            S.op('dve', [('ka', h)], ['dbgk'], lambda e: e.tensor_copy(dk[0:72, :], ka[0:72, :]))
            self.dump('qa0', dq[0:72, :], 'dbgq')
            self.dump('ka0', dk[0:72, :], 'dbgk')

    def dbg_tile(self, name, ap, keys, ncols):
        if name in self.dbg:
            t = self.sb("dbg_" + name, [128, ncols])
            self.S.op('dve', keys, ['dbgt_' + name], lambda e: e.tensor_copy(t[:], ap))
            self.dump(name, t[:], 'dbgt_' + name)

    def xT_keys(self):
        return [('xT', tt) for tt in range(16)]

    def attention(self, l):
        self.psum_epoch()
        S, sb, nc = self.S, self.sb, self.nc
        self.phase([self.R_W, self.R_B2, (self.R_B3[0] + 7168, self.R_B3[1])])
        if True:
            self.wqkv_b = [sb("wqkv%d" % i, [128, 8, 384], BF16) for i in range(2)]
            self.qa = [sb("qa%d" % i, [72, SEQ], BF16) for i in range(4)]
            self.ka = [sb("ka%d" % i, [72, SEQ], BF16) for i in range(4)]
            self.va = [sb("va%d" % i, [128, 16, 65], BF16) for i in range(4)]
            self.qbf = [sb("qbf%d" % i, [128, 512], BF16) for i in range(2)]
            self.t1 = [sb("t1_%d" % i, [128, 512]) for i in range(2)]
            self.t2 = [sb("t2_%d" % i, [128, 512]) for i in range(2)]
            self.atm = [sb("atm%d" % i, [128, 16, 128], BF16) for i in range(2)]
            self.km = sb("km", [64, 8])
            self.kmh = sb("kmh", [64, 8], BF16)
            self.kml = sb("kml", [64, 8], BF16)
            self.gm = sb("gm", [128, 64])
            self.m8 = sb("m8", [128, 8, 8])
            self.lt = sb("lt", [128, 64])
            self.bpad = sb("bpad", [128, 8, 72], BF16)
            self.rden = sb("rden", [128, 4])
            self.NPT = 28
            for i in range(4):
                S.op('pool', [], [('qa', i)], lambda e, i=i: e.memset(self.qa[i][64:72, :], 0.0))
                S.dma('pool', [], [('ka', i)], lambda e, i=i: e.dma_start(out=self.ka[i][64:72, :], in_=self.e8_in))
                S.op('pool', [], [('va', i)], lambda e, i=i: e.memset(self.va[i][:, :, 64:65], 1.0))
            S.op('pool', [], ['bpad'], lambda e: e.memset(self.bpad[:], 0.0))
            if STOP_AFTER == 'att_init':
                self.dbg_qk(0)
                raise _Stop()
        PT = [self.B3[:, i // 4, (i % 4) * 512:(i % 4 + 1) * 512] for i in range(self.NPT)]
        attT = self.B1
        negp = self.cst[:, C_NEGP:C_NEGP + 64]
        floorb = self.cst[:, C_FLOOR:C_FLOOR + 64]
        pt_rr = [0]
        PDb = self.PD[:].bitcast(BF16)
        xk = self.xT_keys()

        def load_w(hp):
            wb = self.wqkv_b[hp % 2]
            S.dma('pool', [], [('wqkv', hp % 2, 0)], lambda e, wb=wb, hp=hp: e.dma_start(
                out=wb[:], in_=self.wqkv[l, hp].rearrange("p (k n) -> p k n", k=8), max_dma_last_dim=4096))
        load_w(0)
        for hp in range(8):
            wb = self.wqkv_b[hp % 2]
            wk = ('wqkv', hp % 2, 0)
            wk1 = wk2 = wk3 = wk
            if STOP_AFTER == 'att_w0':
                self.dbg_tile('wb0', wb[:].rearrange("p a b -> p (a b)"), [wk, wk1, wk2, wk3], 3072)
                raise _Stop()
            if hp + 1 < 8:
                load_w(hp + 1)
            if STOP_AFTER == 'att_w1':
                self.dbg_tile('wb0', wb[:].rearrange("p a b -> p (a b)"), [wk, wk1, wk2, wk3], 3072)
                raise _Stop()
            par = (hp % 2) * 2
            hA, hB = par, par + 1
            cnt = 0
            for which in (1, 0):
                dst = self.ka if which == 1 else self.qa
                dn = 'ka' if which == 1 else 'qa'
                for tc in range(4):
                    bank = cnt % 4
                    cnt += 1
                    ps = self.PA[:, bank * 512:(bank + 1) * 512]
                    pk = ('PA', bank)
                    cols = slice(tc * 512, (tc + 1) * 512)
                    def stopat(ch, ap=None, keys=None):
                        if STOP_AFTER == 'att_qk1' + ch:
                            if ap is not None:
                                self.dbg_tile('probe', ap, keys, 512)
                            raise _Stop()
                    self.mm([wk, wk1, wk2, wk3] + xk[tc * 4:(tc + 1) * 4], [pk], ps,
                            [(wb[:, kc, which * 128:(which + 1) * 128], self.xT[:, kc, cols]) for kc in range(8)])
                    stopat('a', ps, [pk])
                    qb = self.qbf[cnt % 2]
                    qk_ = ('qbf', cnt % 2)
                    if VARIANT != 6:
                        S.op('act', [pk], [qk_], lambda e, qb=qb, ps=ps: e.activation(qb[:], ps, AF.Copy))
                    stopat('b', qb[:], [qk_])
                    t1 = self.t1[cnt % 2]
                    t2 = self.t2[cnt % 2]
                    k1 = ('t1', cnt % 2)
                    k2 = ('t2', cnt % 2)
                    if VARIANT == 4:
                        t1 = self.sb("t1x", [128, 512])
                    if VARIANT in (5, 8):
                        S.op('dve', [('rope', 1)], [k1], lambda e, t1=t1, ps=ps, cols=cols: e.tensor_copy(t1[:], self.ropeC[:, cols]))
                    elif VARIANT in (1, 4):
                        S.op('dve', [pk, ('rope', 1)], [k1], lambda e, t1=t1, ps=ps, cols=cols: e.tensor_copy(t1[:], ps))
                    elif VARIANT == 2:
                        S.op('dve', [pk, ('rope', 1)], [k1], lambda e, t1=t1, ps=ps, cols=cols: e.tensor_copy(t1[:], self.ropeC[:, cols]))
                    elif VARIANT == 3:
                        S.op('dve', [pk, ('rope', 1)], [k1], lambda e, t1=t1, ps=ps, cols=cols: e.tensor_tensor(t1[:], ps, self.cst[:, 0:512], ALU.mult))
                    else:
                        S.op('dve', [pk, ('rope', 1)], [k1], lambda e, t1=t1, ps=ps, cols=cols: e.tensor_tensor(t1[:], ps, self.ropeC[:, cols], ALU.mult))
                    stopat('c', qb[:] if VARIANT == 7 else t1[:], [qk_, k1] if VARIANT in (7, 8) else [k1])
                    pr = self.PB[:, (cnt % 2) * 512:(cnt % 2 + 1) * 512]
                    prk = ('PB', cnt % 2)
                    self.mm([qk_, 'cbf'], [prk], pr, [(self.rm_b, qb[:])])
                    stopat('d', pr, [prk])
                    S.op('dve', [prk, ('rope', 0)], [k2], lambda e, t2=t2, pr=pr, cols=cols: e.tensor_tensor(t2[:], pr, self.ropeS[:, cols], ALU.mult))
                    stopat('e', t2[:], [k2])
                    S.op('dve', [k1, k2], [(dn, hA)], lambda e, t1=t1, t2=t2, dst=dst, cols=cols, hA=hA: e.tensor_tensor(dst[hA][0:64, cols], t1[0:64, :], t2[0:64, :], ALU.add))
                    stopat('f', dst[hA][0:64, cols], [(dn, hA)])
                    S.op('dve', [k1, k2], [(dn, hB)], lambda e, t1=t1, t2=t2, dst=dst, cols=cols, hB=hB: e.tensor_tensor(dst[hB][0:64, cols], t1[64:128, :], t2[64:128, :], ALU.add))
                    stopat('g', dst[hB][0:64, cols], [(dn, hB)])
            if STOP_AFTER == 'att_qk':
                raise _Stop()
            for tq in range(4):
                bank = cnt % 4
                cnt += 1
                ps = self.PA[:, bank * 512:(bank + 1) * 512]
                pk = ('PA', bank)
                self.mmseq([wk, wk1, wk2, wk3] + xk[tq * 4:(tq + 1) * 4], [pk],
                           [(ps[:, j * 128:(j + 1) * 128],
                             [(self.xT[:, kc, (tq * 4 + j) * 128:(tq * 4 + j + 1) * 128], wb[:, kc, 256:384]) for kc in range(8)])
                            for j in range(4)])
                psv = ps.rearrange("p (t c) -> p t c", t=4)
                if hp == 0 and tq == 0:
                    self.dbg_tile('psv', ps, [pk], 512)
                    self.dbg_tile('wb0', wb[:].rearrange("p a b -> p (a b)"), [wk, wk1, wk2, wk3], 3072)
                S.op('act', [pk], [('va', hA)], lambda e, psv=psv, tq=tq, hA=hA: e.activation(self.va[hA][:, tq * 4:(tq + 1) * 4, 0:64], psv[:, :, 0:64], AF.Copy))
                S.op('act', [pk], [('va', hB)], lambda e, psv=psv, tq=tq, hB=hB: e.activation(self.va[hB][:, tq * 4:(tq + 1) * 4, 0:64], psv[:, :, 64:128], AF.Copy))
            if STOP_AFTER == 'att_proj':
                self.dbg_tile('va0', self.va[0][:].rearrange("p a b -> p (a b)"), [('va', 0)], 1040)
                self.dbg_qk(hA)
                raise _Stop()
            atm = self.atm[hp % 2]
            ak = ('atm', hp % 2)
            for hh, hbuf in ((0, hA), (1, hB)):
                qa, ka, va = self.qa[hbuf], self.ka[hbuf], self.va[hbuf]
                qk, kk, vk = ('qa', hbuf), ('ka', hbuf), ('va', hbuf)
                S.op('dve', [kk], ['km'], lambda e, ka=ka: e.tensor_reduce(self.km[:], ka[0:64, :].rearrange("p (b t) -> p b t", b=8), AX.X, ALU.add))
                S.op('dve', ['km'], ['kmh'], lambda e: e.tensor_scalar(self.kmh[:], self.km[:], 1.0 / 256, None, ALU.mult))
                S.op('dve', ['km', 'kmh'], ['kml'], lambda e: e.scalar_tensor_tensor(self.kml[:], in0=self.km[:], scalar=1.0 / 256, in1=self.kmh[:], op0=ALU.mult, op1=ALU.subtract))
                pg = self.PC[:, 448:512]
                self.mmseq([qk, 'kmh', 'kml'], ['PCg'],
                           [(pg[:, t * 8:(t + 1) * 8], [(qa[0:64, (8 + t) * 128:(9 + t) * 128], self.kmh[:]), (qa[0:64, (8 + t) * 128:(9 + t) * 128], self.kml[:])]) for t in range(8)])
                S.op('dve', ['PCg', 'cst'], ['gm'], lambda e, pg=pg: e.tensor_tensor(self.gm[:], pg, negp, ALU.add))

                S.op('dve', ['gm'], ['m8'], lambda e: sel_max(e, self))
                S.op('dve', ['gm', 'm8'], ['lt'], lambda e: sel_lt(e, self))
                S.op('dve', ['lt', 'cst'], ['bpad'], lambda e: e.tensor_tensor(self.bpad[:, :, 64:72], self.lt[:].rearrange("p (t j) -> p t j", t=8), floorb.rearrange("p (t j) -> p t j", t=8), ALU.max))
                self.tps(['bpad', 'cbf'], ['PD'], [(PDb[0:72, t * 128:(t + 1) * 128], self.bpad[:, t, :]) for t in range(8)], self.ident_b)
                S.op('act', ['PD'], [qk], lambda e, qa=qa: e.activation(qa[64:72, 1024:2048], PDb[64:72, :], AF.Copy))
                if STOP_AFTER == 'att_gate':
                    self.dbg_qk(hA)
                    raise _Stop()
                slots = {}

                def stageA(qc):
                    sl = []
                    for kt in range(4 * qc + 4):
                        c0 = max(0, kt * 128 - qc * 512)
                        bank = kt % 2
                        ps = self.PB[:, bank * 512:(bank + 1) * 512]
                        pk = ('PB', bank)
                        self.mm([kk, qk], [pk], ps[:, c0:512], [(ka[0:72, kt * 128:(kt + 1) * 128], qa[0:72, qc * 512 + c0:(qc + 1) * 512])])
                        si = pt_rr[0]
                        pt_rr[0] = (pt_rr[0] + 1) % self.NPT
                        sl.append(si)
                        pt = PT[si]
                        S.op('act', [pk], [('PT', si)], lambda e, pt=pt, ps=ps, c0=c0: e.activation(pt[:, c0:512], ps[:, c0:512], AF.Exp, scale=0.125))
                        if kt >= 4 * qc:
                            S.op('dve', [('PT', si), 'cbf'], [('PT', si)], lambda e, pt=pt, c0=c0: e.tensor_tensor(pt[:, c0:c0 + 128], pt[:, c0:c0 + 128], self.tri_b, ALU.mult))
                    slots[qc] = sl

                def stageB(qc):
                    sl = slots[qc]
                    groups = []
                    for qi in range(4):
                        qt = 4 * qc + qi
                        groups.append((self.PC[:, qi * 65:(qi + 1) * 65],
                                       [(PT[sl[kt]][:, qi * 128:(qi + 1) * 128], va[:, kt, :]) for kt in range(qt + 1)]))
                    self.mmseq([('PT', s_) for s_ in sl] + [vk], ['PCo'], groups)
                    pv = self.PC[:, 0:260].rearrange("p (q c) -> p q c", q=4)
                    S.op('dve', ['PCo'], ['rden'], lambda e, pv=pv: e.reciprocal(self.rden[:], pv[:, :, 64]))

                    def norm(e, atm=atm, hh=hh, qc=qc):
                        ins = None
                        for qi in range(4):
                            ins = e.tensor_scalar(atm[:, 4 * qc + qi, hh * 64:(hh + 1) * 64], self.PC[:, qi * 65:qi * 65 + 64], self.rden[:, qi:qi + 1], None, ALU.mult)
                        return ins
                    S.op('dve', ['PCo', 'rden'], [ak], norm)

                stageA(0)
                for qc in range(4):
                    if qc + 1 < 4:
                        stageA(qc + 1)
                    stageB(qc)
            if hp == 0:
                self.dbg_tile('va0', self.va[0][:].rearrange("p a b -> p (a b)"), [('va', 0)], 1040)
                self.dbg_tile('atm0', atm[:].rearrange("p a b -> p (a b)"), [ak], 2048)
            for half in range(2):
                self.tps([ak, 'cbf'], ['PD'], [(PDb[:, t * 128:(t + 1) * 128], atm[:, half * 8 + t, :]) for t in range(8)], self.ident_b)
                S.op('act', ['PD'], [('attT', hp, half)], lambda e, half=half, hp=hp: e.activation(attT[:, hp, half * 1024:(half + 1) * 1024], PDb, AF.Copy))
        if 'attT' in self.dbg:
            self.dbgf = self.sb("dbgf", [128, 512])
            for hp in range(8):
                for q4 in range(4):
                    S.op('dve', [('attT', hp, 0), ('attT', hp, 1), ('dbg', 'attT')], ['dbgf'], lambda e, hp=hp, q4=q4: e.tensor_copy(self.dbgf[:, 0:512], attT[:, hp, q4 * 512:(q4 + 1) * 512]))
                    S.dma('sp', ['dbgf'], [('dbg', 'attT')], lambda e, hp=hp, q4=q4: e.dma_start(out=self.dbg['attT'][hp * 128:(hp + 1) * 128, q4 * 512:(q4 + 1) * 512], in_=self.dbgf[:, 0:512]))
            self.final_keys.append(('dbg', 'attT'))

    def ssd(self, l):
        self.psum_epoch()
        S, sb, nc = self.S, self.sb, self.nc
        self.phase([self.R_W, self.R_B3])
        if True:
            self.wssm_b = [sb("wssm0", [128, 8, 768], BF16)] * 2
            self.wdt_b = sb("wdtb", [128, 8, 32], BF16)
            self.smallb = sb("smallb", [128, 96])
            self.cwb = sb("cwb", [128, 8, 4, 5])
            self.dt_all = sb("dt_all", [128, 16, 32])
            self.adt_all = sb("adt_all", [128, 16, 32])
            self.acs_all = sb("acs_all", [128, 16, 32])
            self.dst_all = sb("dst_all", [128, 16, 32])
            self.ea_all = sb("ea_all", [128, 16, 32])
            self.lastb = sb("lastb", [128, 8, 32])
            self.cdb = sb("cdb", [128, 8, 32])
            self.a_b = sb("a_b", [128, 32])
            self.uext = [sb("uext%d" % i, [128, 4, 259]) for i in range(2)]
            self.xh = sb("xh", [128, 4, 256])
            self.tnh = sb("tnh", [128, 4, 256])
            self.bcb2 = [sb("bcb%d" % i, [128, 2, 256], BF16) for i in range(2)]
            self.bcf = sb("bcf", [128, 256])
            self.xs_tok2 = [sb("xs_tok%d" % i, [128, 2, 256], BF16) for i in range(2)]
            self.xdt = sb("xdt", [128, 2, 256], BF16)
            self.xw2 = [sb("xw%d" % i, [128, 2, 256], BF16) for i in range(2)]
            self.btok2 = [sb("btok%d" % i, [128, 2, 128], BF16) for i in range(2)]
            self.cb3 = sb("cb3", [128, 3, 128])
            self.dtmp = [sb("dtmp%d" % i, [128, 3, 128]) for i in range(2)]
            self.mt = [sb("mt%d" % i, [128, 3, 128], BF16) for i in range(2)]
            self.prev_f = sb("prev_f", [128, 256])
            self.prev_b = sb("prev_b", [128, 256], BF16)
            self.tz = sb("tz", [128, 2, 256])
            self.yv = sb("yv", [128, 2, 256])
            self.yt2 = sb("yt2", [128, 2, 256])
            self.ss = sb("ss", [128, 2])
            self.rstd = sb("rstd", [128, 2])
            self.nwb = [sb("nwb%d" % i, [128, 256]) for i in range(2)]
            self.xsf = self.xh[:, 0:2, :]
            self.zs = self.tz
            self.h2 = self.yv
            self.yn = self.yv
            self.junk = self.tnh[:, 0, :]
        xk = self.xT_keys()
        S.dma('sp', [], ['smallb'], lambda e: e.dma_start(out=self.smallb[:], in_=self.small[l].broadcast_to([128, 96])))
        S.dma('sp', [], ['cwb'], lambda e: e.dma_start(out=self.cwb[:], in_=self.cw[l].rearrange("p (g c k) -> p g c k", g=8, c=4)))
        S.dma('pool', [], ['wdtb'], lambda e: e.dma_start(out=self.wdt_b[:], in_=self.wdt[l].rearrange("p (k n) -> p k n", k=8)))
        S.op('pool', ['cwb'], ['cwb'], lambda e: e.tensor_scalar(self.cwb[:], self.cwb[:], 0.5, None, ALU.mult))
        dtb = self.smallb[:, 0:32]
        alog = self.smallb[:, 32:64]
        dsk = self.smallb[:, 64:96]
        S.op('act', ['smallb'], ['a_b'], lambda e: e.activation(self.a_b[:], alog, AF.Exp))
        S.op('dve', ['a_b'], ['a_b'], lambda e: e.tensor_scalar(self.a_b[:], self.a_b[:], -1.0, None, ALU.mult))
        psd = self.PA[:, 0:512]
        self.mmseq(['wdtb'] + xk, [('PA', 0)],
                   [(psd[:, tt * 32:(tt + 1) * 32], [(self.xT[:, kc, tt * 128:(tt + 1) * 128], self.wdt_b[:, kc, :]) for kc in range(8)]) for tt in range(16)])
        dtf = self.dt_all[:].rearrange("p t h -> p (t h)")
        psd3 = psd.rearrange("p (t h) -> p t h", t=16)
        S.op('dve', [('PA', 0), 'smallb'], ['dt_all'], lambda e: e.tensor_tensor(self.dt_all[:], psd3, dtb.unsqueeze(1).broadcast_to([128, 16, 32]), ALU.add))
        S.op('act', ['dt_all'], ['dt_all'], lambda e: e.activation(dtf, dtf, AF.Exp))
        S.op('act', ['dt_all'], ['dt_all'], lambda e: e.activation(dtf, dtf, AF.Ln, bias=1.0))
        S.op('dve', ['dt_all', 'a_b'], ['adt_all'], lambda e: e.tensor_tensor(self.adt_all[:], self.dt_all[:], self.a_b[:].unsqueeze(1).broadcast_to([128, 16, 32]), ALU.mult))
        psa = self.PA[:, 512:1024]
        psl = self.PA[:, 1024:1280]
        groups = []
        for c in range(8):
            a0 = self.adt_all[:, 2 * c, :]
            a1 = self.adt_all[:, 2 * c + 1, :]
            groups.append((psa[:, (2 * c) * 32:(2 * c + 1) * 32], [(self.tri_f, a0)]))
            groups.append((psa[:, (2 * c + 1) * 32:(2 * c + 2) * 32], [(self.ones_f, a0), (self.tri_f, a1)]))
        self.mmseq(['adt_all', 'cst'], [('PA', 1)], groups)
        self.mmseq(['adt_all', 'cst'], [('PA', 2)],
                   [(psl[:, c * 32:(c + 1) * 32], [(self.ones_f, self.adt_all[:, 2 * c, :]), (self.ones_f, self.adt_all[:, 2 * c + 1, :])]) for c in range(8)])
        acsf = self.acs_all[:].rearrange("p t h -> p (t h)")
        S.op('dve', [('PA', 1)], ['acs_all'], lambda e: e.tensor_copy(acsf, psa))
        S.op('dve', [('PA', 2)], ['lastb'], lambda e: e.tensor_copy(self.lastb[:].rearrange("p c h -> p (c h)"), psl))
        S.op('dve', ['lastb', 'acs_all'], ['dst_all'], lambda e: e.tensor_tensor(
            self.dst_all[:].rearrange("p (c s) h -> p c s h", c=8), self.lastb[:].unsqueeze(2).broadcast_to([128, 8, 2, 32]),
            self.acs_all[:].rearrange("p (c s) h -> p c s h", c=8), ALU.subtract))
        dstf = self.dst_all[:].rearrange("p t h -> p (t h)")
        S.op('act', ['dst_all'], ['dst_all'], lambda e: e.activation(dstf, dstf, AF.Exp))
        S.op('dve', ['dst_all', 'dt_all'], ['dst_all'], lambda e: e.tensor_tensor(self.dst_all[:], self.dst_all[:], self.dt_all[:], ALU.mult))
        S.op('act', ['acs_all'], ['ea_all'], lambda e: e.activation(self.ea_all[:].rearrange("p t h -> p (t h)"), acsf, AF.Exp))
        S.op('act', ['lastb'], ['cdb'], lambda e: e.activation(self.cdb[:].rearrange("p c h -> p (c h)"), self.lastb[:].rearrange("p c h -> p (c h)"), AF.Exp))
        self.dump('dt_all', self.dt_all[:].rearrange("p t h -> p (t h)"), 'dt_all')
        self.dump('acs_all', acsf, 'acs_all')

        P0 = self.PA[:, 0:512]
        P1 = self.PA[:, 512:1024]
        P2 = self.PA[:, 1024:1536]
        P3 = self.PA[:, 1536:2048]
        P4 = self.PB[:, 0:512]
        P5 = self.PB[:, 512:1024]
        P6 = self.PC[:, 0:512]
        P7 = self.PD[:, 0:512]
        uprev = None
        def load_g(g):
            wb = self.wssm_b[g % 2]
            S.dma('pool', [], [('wssm', 0, 0)], lambda e, wb=wb, g=g: e.dma_start(
                out=wb[:], in_=self.wssm[l, g].rearrange("p (k n) -> p k n", k=8), max_dma_last_dim=4096))
            nw = self.nwb[g % 2]
            S.dma('sp', [], [('nwb', g % 2)], lambda e, nw=nw, g=g: e.dma_start(out=nw[:], in_=self.normw[l][:, g * 256:(g + 1) * 256].broadcast_to([128, 256])))
        load_g(0)
        for g in range(8):
            wb = self.wssm_b[g % 2]
            wk = ('wssm', 0, 0)
            wkall = [('wssm', 0, 0)]
            nw = self.nwb[g % 2]
            nk = ('nwb', g % 2)
            if g > 0:
                load_g(g)
            hs = slice(4 * g, 4 * g + 4)
            prevB = None
            for c in range(8):
                cols = slice(c * 256, (c + 1) * 256)
                sl = c % 2
                bcb, xs_tok, xw, btok = self.bcb2[sl], self.xs_tok2[sl], self.xw2[sl], self.btok2[sl]
                kbcb, kxs, kxw, kbt = ('bcb', sl), ('xs_tok', sl), ('xw', sl), ('btok', sl)
                F1, F2, Bq = [], [], []
                self.rec = F1
                xkc = xk[2 * c:2 * c + 2]
                ue = self.uext[c % 2]
                uk = ('uext', c % 2)
                self.mmseq(wkall + xkc, [('PA', 0)],
                           [(P0[:, j * 256:(j + 1) * 256], [(wb[:, kc, 256 + j * 128:256 + (j + 1) * 128], self.xT[:, kc, cols]) for kc in range(8)]) for j in range(2)])
                self.mmseq(wkall + xkc, [('PA', 1)],
                           [(P1[:, j * 256:(j + 1) * 256], [(wb[:, kc, 512 + j * 128:512 + (j + 1) * 128], self.xT[:, kc, cols]) for kc in range(8)]) for j in range(2)])
                if c == 0:
                    S.op('pool', [], [uk], lambda e, bcb=bcb, xs_tok=xs_tok, xw=xw, btok=btok, ue=ue: e.memset(ue[:, :, 0:3], 0.0))
                else:
                    up = self.uext[(c - 1) % 2]
                    S.op('pool', [('uext', (c - 1) % 2)], [uk], lambda e, bcb=bcb, xs_tok=xs_tok, xw=xw, btok=btok, ue=ue, up=up: e.tensor_copy(ue[:, :, 0:3], up[:, :, 256:259]))
                S.op('act', [('PA', 0)], [uk], lambda e, bcb=bcb, xs_tok=xs_tok, xw=xw, btok=btok, ue=ue: e.activation(ue[:, 0:2, 3:259], P0.rearrange("p (j t) -> p j t", j=2), AF.Copy))
                S.op('act', [('PA', 1)], [uk], lambda e, bcb=bcb, xs_tok=xs_tok, xw=xw, btok=btok, ue=ue: e.activation(ue[:, 2:4, 3:259], P1.rearrange("p (j t) -> p j t", j=2), AF.Copy))
                for ci in range(4):
                    eng = 'dve'
                    w = self.cwb[:, g, ci, :]
                    o = self.xh[:, ci, :]
                    S.op(eng, [uk, 'cwb'], [('xh', ci)], lambda e, bcb=bcb, xs_tok=xs_tok, xw=xw, btok=btok, ci=ci, ue=ue, w=w, o=o: e.tensor_scalar(o, ue[:, ci, 0:256], w[:, 0:1], w[:, 4:5], ALU.mult, ALU.add))
                    for k in range(1, 4):
                        if eng == 'dve':
                            S.op('dve', [uk, 'cwb', ('xh', ci)], [('xh', ci)], lambda e, bcb=bcb, xs_tok=xs_tok, xw=xw, btok=btok, ci=ci, ue=ue, w=w, o=o, k=k: e.scalar_tensor_tensor(
                                o, in0=ue[:, ci, k:k + 256], scalar=w[:, k:k + 1], in1=o, op0=ALU.mult, op1=ALU.add))
                        else:
                            S.op('pool', [uk, 'cwb'], [('tnh', ci)], lambda e, bcb=bcb, xs_tok=xs_tok, xw=xw, btok=btok, ci=ci, ue=ue, w=w, k=k: e.tensor_scalar(self.tnh[:, ci, :], ue[:, ci, k:k + 256], w[:, k:k + 1], None, ALU.mult))
                            S.op('pool', [('tnh', ci), ('xh', ci)], [('xh', ci)], lambda e, bcb=bcb, xs_tok=xs_tok, xw=xw, btok=btok, ci=ci, o=o: e.tensor_tensor(o, o, self.tnh[:, ci, :], ALU.add))
                S.op('act', [('xh', ci) for ci in range(4)], [('tnh', ci) for ci in range(4)], lambda e, bcb=bcb, xs_tok=xs_tok, xw=xw, btok=btok: e.activation(self.tnh[:], self.xh[:], AF.Tanh))
                S.op('dve', [('tnh', 0), ('tnh', 1), ('xh', 0), ('xh', 1)], [('xh', 0), ('xh', 1)], lambda e, bcb=bcb, xs_tok=xs_tok, xw=xw, btok=btok: e.scalar_tensor_tensor(self.xsf[:], in0=self.tnh[:, 0:2, :], scalar=1.0, in1=self.xh[:, 0:2, :], op0=ALU.add, op1=ALU.mult))
                S.op('dve', [('tnh', 2), ('tnh', 3), ('xh', 2), ('xh', 3)], [kbcb], lambda e, bcb=bcb, xs_tok=xs_tok, xw=xw, btok=btok: e.scalar_tensor_tensor(bcb[:], in0=self.tnh[:, 2:4, :], scalar=1.0, in1=self.xh[:, 2:4, :], op0=ALU.add, op1=ALU.mult))
                S.op('dve', [('tnh', 2), ('xh', 2)], ['bcf'], lambda e, bcb=bcb, xs_tok=xs_tok, xw=xw, btok=btok: e.scalar_tensor_tensor(self.bcf[:], in0=self.tnh[:, 2, :], scalar=1.0, in1=self.xh[:, 2, :], op0=ALU.add, op1=ALU.mult))
                self.tps([('xh', 0), ('xh', 1), 'cst'], [('PA', 0)],
                         [(P0[:, si * 256 + j * 128:si * 256 + (j + 1) * 128], self.xsf[:, j, si * 128:(si + 1) * 128]) for si in range(2) for j in range(2)], self.ident_f)
                self.tps(['bcf', 'cst'], [('PA', 1)],
                         [(P1[:, si * 128:(si + 1) * 128], self.bcf[:, si * 128:(si + 1) * 128]) for si in range(2)], self.ident_f)
                P0v = P0.rearrange("p (s h d) -> p s h d", s=2, h=4)
                S.op('act', [('PA', 0)], [kxs], lambda e, bcb=bcb, xs_tok=xs_tok, xw=xw, btok=btok: e.activation(xs_tok[:].rearrange("p s c -> p (s c)"), P0, AF.Copy))
                S.op('dve', [('PA', 0), 'dt_all'], ['xdt'], lambda e, bcb=bcb, xs_tok=xs_tok, xw=xw, btok=btok, c=c, hs=hs: e.tensor_tensor(
                    self.xdt[:].rearrange("p s (h d) -> p s h d", h=4), P0v,
                    self.dt_all[:, 2 * c:2 * c + 2, hs].unsqueeze(3).broadcast_to([128, 2, 4, 64]), ALU.mult))
                S.op('dve', [('PA', 0), 'dst_all'], [kxw], lambda e, bcb=bcb, xs_tok=xs_tok, xw=xw, btok=btok, c=c, hs=hs: e.tensor_tensor(
                    xw[:].rearrange("p s (h d) -> p s h d", h=4), P0v,
                    self.dst_all[:, 2 * c:2 * c + 2, hs].unsqueeze(3).broadcast_to([128, 2, 4, 64]), ALU.mult))
                S.op('act', [('PA', 1)], [kbt], lambda e, bcb=bcb, xs_tok=xs_tok, xw=xw, btok=btok: e.activation(btok[:].rearrange("p s n -> p (s n)"), P1[:, 0:256], AF.Copy))
                self.mmseq([kbcb], [('PA', 3)],
                           [(P3[:, 0:256], [(bcb[:, 0, 0:128], bcb[:, 1, :])]),
                            (P3[:, 384:512], [(bcb[:, 0, 128:256], bcb[:, 1, 128:256])])])
                S.op('dve', [('PA', 3), 'cst'], ['cb3'], lambda e, bcb=bcb, xs_tok=xs_tok, xw=xw, btok=btok: e.tensor_tensor(self.cb3[:, 0, :], P3[:, 0:128], self.tri_f, ALU.mult))
                S.op('act', [('PA', 3)], ['cb3'], lambda e, bcb=bcb, xs_tok=xs_tok, xw=xw, btok=btok: e.activation(self.cb3[:, 1, :], P3[:, 128:256], AF.Copy))
                S.op('dve', [('PA', 3), 'cst', 'cb3'], ['cb3'], lambda e, bcb=bcb, xs_tok=xs_tok, xw=xw, btok=btok: e.tensor_tensor(self.cb3[:, 2, :], P3[:, 384:512], self.tri_f, ALU.mult))
                self.rec = F2
                def bc_mm(hh):
                    hg = 4 * g + hh
                    pb = (P4 if hh % 2 == 0 else P3)[:, 0:256]
                    pbk = ('PB', 0, 0) if hh % 2 == 0 else ('PA', 3)
                    a0 = self.adt_all[:, 2 * c, hg:hg + 1].broadcast_to([128, 128])
                    a1 = self.adt_all[:, 2 * c + 1, hg:hg + 1].broadcast_to([128, 128])
                    self.mmseq(['adt_all', 'cst'], [pbk],
                               [(pb[:, 0:128], [(a0, self.tri_f)]),
                                (pb[:, 128:256], [(a0, self.ones_f), (a1, self.tri_f)])])
                    return pb, pbk
                nxt = bc_mm(0)
                for hh in range(4):
                    hg = 4 * g + hh
                    pb, pbk = nxt
                    if hh + 1 < 4:
                        nxt = bc_mm(hh + 1)
                    dtm = self.dtmp[hh % 2]
                    dk = ('dtmp', hh % 2)
                    mt = self.mt[hh % 2]
                    mk = ('mt', hh % 2)
                    ac0 = self.acs_all[:, 2 * c, hg:hg + 1]
                    ac1 = self.acs_all[:, 2 * c + 1, hg:hg + 1]

                    def dec(e, dtm=dtm, pb=pb, ac0=ac0, ac1=ac1):
                        e.tensor_scalar(dtm[:, 0, :], pb[:, 0:128], ac0, 0.0, ALU.subtract, ALU.min)
                        e.tensor_scalar(dtm[:, 1, :], pb[:, 128:256], ac0, 0.0, ALU.subtract, ALU.min)
                        return e.tensor_scalar(dtm[:, 2, :], pb[:, 128:256], ac1, 0.0, ALU.subtract, ALU.min)
                    S.op('dve', [pbk, 'acs_all'], [dk], dec)
                    S.op('act', [dk], [dk], lambda e, dtm=dtm: e.activation(dtm[:].rearrange("p a b -> p (a b)"), dtm[:].rearrange("p a b -> p (a b)"), AF.Exp))
                    S.op('dve', [dk, 'cb3'], [mk], lambda e, dtm=dtm, mt=mt: e.tensor_tensor(mt[:], dtm[:], self.cb3[:], ALU.mult))
                    self.mmseq([mk, 'xdt'], [('PB', 1, hh)],
                               [(P5[:, hh * 64:(hh + 1) * 64], [(mt[:, 0, :], self.xdt[:, 0, hh * 64:(hh + 1) * 64])]),
                                (P5[:, 256 + hh * 64:256 + (hh + 1) * 64], [(mt[:, 1, :], self.xdt[:, 0, hh * 64:(hh + 1) * 64]), (mt[:, 2, :], self.xdt[:, 1, hh * 64:(hh + 1) * 64])])])
                self.rec = Bq
                self.mmseq(wkall + xkc, [('PA', 2)],
                           [(P2[:, si * 256:(si + 1) * 256], [(self.xT[:, kc, c * 256 + si * 128:c * 256 + (si + 1) * 128], wb[:, kc, 0:256]) for kc in range(8)]) for si in range(2)])
                S.op('act', [('PA', 2)], ['tz'], lambda e, bcb=bcb, xs_tok=xs_tok, xw=xw, btok=btok: e.activation(self.tz[:].rearrange("p s c -> p (s c)"), P2, AF.Tanh, scale=0.5))
                S.op('dve', ['tz', ('PA', 2)], ['tz'], lambda e, bcb=bcb, xs_tok=xs_tok, xw=xw, btok=btok: e.scalar_tensor_tensor(self.zs[:].rearrange("p s c -> p (s c)"), in0=self.tz[:].rearrange("p s c -> p (s c)"), scalar=1.0, in1=P2, op0=ALU.add, op1=ALU.mult))
                p5k = [('PB', 1, hh) for hh in range(4)]
                if c > 0:
                    self.mmseq([kbcb, 'prev_b'], ['PC'],
                               [(P6[:, si * 256:(si + 1) * 256], [(bcb[:, 1, si * 128:(si + 1) * 128], self.prev_b[:])]) for si in range(2)])
                    S.op('dve', ['PC', 'ea_all'], ['yt2'], lambda e, bcb=bcb, xs_tok=xs_tok, xw=xw, btok=btok, c=c, hs=hs: e.tensor_tensor(
                        self.yt2[:].rearrange("p s (h d) -> p s h d", h=4), P6.rearrange("p (s h d) -> p s h d", s=2, h=4),
                        self.ea_all[:, 2 * c:2 * c + 2, hs].unsqueeze(3).broadcast_to([128, 2, 4, 64]), ALU.mult))
                if c < 7:
                    self.mm([kbt, kxw], [('PB', 0, 0)], P4[:, 0:256], [(btok[:, si, :], xw[:, si, :]) for si in range(2)])
                    if c == 0:
                        S.op('act', [('PB', 0, 0)], ['prev_f'], lambda e, bcb=bcb, xs_tok=xs_tok, xw=xw, btok=btok: e.activation(self.prev_f[:], P4[:, 0:256], AF.Copy))
                    else:
                        S.op('dve', ['prev_f', 'cdb'], ['prev_f'], lambda e, bcb=bcb, xs_tok=xs_tok, xw=xw, btok=btok, c=c, hs=hs: e.tensor_tensor(
                            self.prev_f[:].rearrange("p (h d) -> p h d", h=4), self.prev_f[:].rearrange("p (h d) -> p h d", h=4),
                            self.cdb[:, c, hs].unsqueeze(2).broadcast_to([128, 4, 64]), ALU.mult))
                        S.op('dve', ['prev_f', ('PB', 0, 0)], ['prev_f'], lambda e, bcb=bcb, xs_tok=xs_tok, xw=xw, btok=btok: e.tensor_tensor(self.prev_f[:], self.prev_f[:], P4[:, 0:256], ALU.add))
                    S.op('act', ['prev_f'], ['prev_b'], lambda e, bcb=bcb, xs_tok=xs_tok, xw=xw, btok=btok: e.activation(self.prev_b[:], self.prev_f[:], AF.Copy))
                S.op('pool', [kxs, 'smallb'], ['yv'], lambda e, bcb=bcb, xs_tok=xs_tok, xw=xw, btok=btok, hs=hs: e.tensor_tensor(
                    self.yv[:].rearrange("p s (h d) -> p s h d", h=4), xs_tok[:].rearrange("p s (h d) -> p s h d", h=4),
                    dsk[:, hs].unsqueeze(1).unsqueeze(3).broadcast_to([128, 2, 4, 64]), ALU.mult))
                S.op('dve', p5k + ['yv'], ['yv'], lambda e, bcb=bcb, xs_tok=xs_tok, xw=xw, btok=btok: e.tensor_tensor(self.yv[:].rearrange("p s c -> p (s c)"), P5, self.yv[:].rearrange("p s c -> p (s c)"), ALU.add))
                if c > 0:
                    S.op('dve', ['yv', 'yt2'], ['yv'], lambda e, bcb=bcb, xs_tok=xs_tok, xw=xw, btok=btok: e.tensor_tensor(self.yv[:], self.yv[:], self.yt2[:], ALU.add))
                S.op('dve', ['yv', 'tz'], ['yv'], lambda e, bcb=bcb, xs_tok=xs_tok, xw=xw, btok=btok: e.tensor_tensor(self.h2[:], self.yv[:], self.zs[:], ALU.mult))
                for si in range(2):
                    S.op('act', ['yv'], [('tnh', 0), ('ss', si)], lambda e, bcb=bcb, xs_tok=xs_tok, xw=xw, btok=btok, si=si: e.activation(self.junk[:], self.h2[:, si, :], AF.Square, accum_out=self.ss[:, si:si + 1]))
                S.op('dve', [('ss', 0), ('ss', 1)], ['rstd'], lambda e, bcb=bcb, xs_tok=xs_tok, xw=xw, btok=btok: e.tensor_scalar(self.rstd[:], self.ss[:], 1.0 / 256, 4 * RMS_EPS, ALU.mult, ALU.add))
                S.op('pool', ['rstd', 'nh'], ['rstd'], lambda e, bcb=bcb, xs_tok=xs_tok, xw=xw, btok=btok: e.tensor_tensor(self.rstd[:], self.rstd[:], self.nh[:, 0:2], ALU.pow))
                for si in range(2):
                    S.op('dve', ['yv', 'rstd', nk], ['yv'], lambda e, bcb=bcb, xs_tok=xs_tok, xw=xw, btok=btok, si=si, nw=nw: e.scalar_tensor_tensor(
                        self.yn[:, si, :], in0=self.h2[:, si, :], scalar=self.rstd[:, si:si + 1], in1=nw[:], op0=ALU.mult, op1=ALU.mult))
                self.tps(['yv', 'cst'], ['PD'],
                         [(P7[:, j * 256 + si * 128:j * 256 + (si + 1) * 128], self.yn[:, si, j * 128:(j + 1) * 128]) for j in range(2) for si in range(2)], self.ident_f)
                S.op('act', ['PD'], [('yT', g, c)], lambda e, bcb=bcb, xs_tok=xs_tok, xw=xw, btok=btok, g=g, cols=cols: e.activation(self.yT[:, 2 * g:2 * g + 2, cols], P7.rearrange("p (j t) -> p j t", j=2), AF.Copy))
                self.rec = None
                if prevB is None:
                    for t in F1:
                        t()
                else:
                    k = max(1, -(-len(F1) // max(1, len(prevB))))
                    bi = 0
                    for i, t in enumerate(F1):
                        t()
                        if (i + 1) % k == 0 and bi < len(prevB):
                            prevB[bi]()
                            bi += 1
                    while bi < len(prevB):
                        prevB[bi]()
                        bi += 1
                for t in F2:
                    t()
                prevB = Bq
            for t in prevB:
                t()
        if 'yT' in self.dbg:
            if True:
                self.dbgf = self.sb("dbgf", [128, 512])
            for kc in range(16):
                for q4 in range(4):
                    S.op('dve', [('yT', g, c) for g in range(8) for c in range(8)] + [('dbg', 'yT')], ['dbgf'], lambda e, kc=kc, q4=q4: e.tensor_copy(self.dbgf[:, 0:512], self.yT[:, kc, q4 * 512:(q4 + 1) * 512]))
                    S.dma('sp', ['dbgf'], [('dbg', 'yT')], lambda e, kc=kc, q4=q4: e.dma_start(out=self.dbg['yT'][kc * 128:(kc + 1) * 128, q4 * 512:(q4 + 1) * 512], in_=self.dbgf[:, 0:512]))
            self.final_keys.append(('dbg', 'yT'))

    def merge(self, l):
        self.psum_epoch()
        S, sb = self.S, self.sb
        self.phase([self.R_W])
        if True:
            self.wc1_b = [sb("wc1b%d" % i, [128, 5120], BF16) for i in range(2)]
            self.ta = [sb("ta%d" % i, [128, 512]) for i in range(2)]
            self.tb = [sb("tb%d" % i, [128, 512]) for i in range(2)]
        mT = self.B3
        xk = self.xT_keys()
        attk = [('attT', hp, h) for hp in range(8) for h in range(2)]
        yk = [('yT', g, c) for g in range(8) for c in range(8)]
        it = 0
        def load_c(fc):
            wb = self.wc1_b[fc % 2]
            S.dma('pool', [], [('wc1', fc % 2, 0)], lambda e, wb=wb, fc=fc: e.dma_start(
                out=wb[:].rearrange("p (a b) -> p a b", a=5), in_=self.wc1[l, fc].rearrange("p (a b) -> p a b", a=5), max_dma_last_dim=4096))
        load_c(0)
        for fc in range(8):
            wb = self.wc1_b[fc % 2]
            wks = [('wc1', fc % 2, 0)]
            if fc + 1 < 8:
                load_c(fc + 1)
            wg = wb[:, 0:2048].rearrange("p (k n) -> p k n", k=8)
            wa = wb[:, 2048:3072].rearrange("p (k n) -> p k n", k=8)
            ws = wb[:, 3072:5120].rearrange("p (k n) -> p k n", k=16)
            for tc in range(4):
                cols = slice(tc * 512, (tc + 1) * 512)
                if it % 2 == 0:
                    pga, pgs, ppa, pps = (self.PA[:, i * 512:(i + 1) * 512] for i in range(4))
                    bk = [('PA', 0), ('PA', 1), ('PA', 2), ('PA', 3)]
                else:
                    pga, pgs, ppa, pps = self.PB[:, 0:512], self.PB[:, 512:1024], self.PC[:, 0:512], self.PD[:, 0:512]
                    bk = [('PB', 0), ('PB', 1), 'PC', 'PD']
                self.mm(wks + xk[tc * 4:(tc + 1) * 4], [bk[0]], pga, [(wg[:, kc, 0:128], self.xT[:, kc, cols]) for kc in range(8)])
                self.mm(wks + xk[tc * 4:(tc + 1) * 4], [bk[1]], pgs, [(wg[:, kc, 128:256], self.xT[:, kc, cols]) for kc in range(8)])
                self.mm(wks + attk, [bk[2]], ppa, [(wa[:, kc, :], self.B1[:, kc, cols]) for kc in range(8)])
                self.mm(wks + yk, [bk[3]], pps, [(ws[:, kc, :], self.yT[:, kc, cols]) for kc in range(16)])
                ta, tb = self.ta[it % 2], self.tb[it % 2]
                tak, tbk = ('ta', it % 2), ('tb', it % 2)
                it += 1
                S.op('act', [bk[0]], [tak], lambda e, ta=ta, pga=pga: e.activation(ta[:], pga, AF.Tanh, scale=0.5))
                S.op('act', [bk[1]], [tbk], lambda e, tb=tb, pgs=pgs: e.activation(tb[:], pgs, AF.Tanh, scale=0.5))
                S.op('dve', [tak, bk[2]], [tak], lambda e, ta=ta, ppa=ppa: e.scalar_tensor_tensor(ta[:], in0=ta[:], scalar=1.0, in1=ppa, op0=ALU.add, op1=ALU.mult))
                S.op('dve', [tbk, bk[3]], [tbk], lambda e, tb=tb, pps=pps: e.scalar_tensor_tensor(tb[:], in0=tb[:], scalar=1.0, in1=pps, op0=ALU.add, op1=ALU.mult))
                S.op('pool', [tak, tbk], [('mergedT', fc, tc)], lambda e, ta=ta, tb=tb, fc=fc, cols=cols: e.tensor_tensor(mT[:, fc, cols], ta[:], tb[:], ALU.add))

    def layernorm(self, t, tk, g_b, b_b, gk, i):
        S = self.S
        st = self.ln_st[i % 2]
        mv = self.ln_mv[i % 2]
        sk = ('lnst', i % 2)

        def stats(e):
            e.bn_stats(st[:, 0, :], t[:, 0:512])
            return e.bn_stats(st[:, 1, :], t[:, 512:1024])
        S.op('dve', [tk], [sk], stats)
        S.op('dve', [sk], [sk], lambda e: e.bn_aggr(mv[:, 0:2], st[:].rearrange("p a b -> p (a b)")))
        S.op('dve', [sk], [sk], lambda e: e.tensor_scalar(mv[:, 2:3], mv[:, 1:2], LN_EPS, None, ALU.add))
        S.op('pool', [sk, 'nh'], [sk], lambda e: e.tensor_tensor(mv[:, 3:4], mv[:, 2:3], self.nh[:, 0:1], ALU.pow))
        S.op('dve', [tk, sk], [tk], lambda e: e.tensor_scalar(t[:], t[:], mv[:, 0:1], mv[:, 3:4], ALU.subtract, ALU.mult))
        S.op('pool', [tk, gk], [tk], lambda e: e.tensor_tensor(t[:], t[:], g_b, ALU.mult))
        S.op('dve', [tk, gk], [tk], lambda e: e.tensor_tensor(t[:], t[:], b_b, ALU.add))

    def mix_ln1(self, l):
        self.psum_epoch()
        S, sb = self.S, self.sb
        self.phase([self.R_W, self.R_B2])
        if True:
            self.wout_b = sb("woutb", [128, 8, 1024], BF16)
            self.lnpb = sb("lnpb", [128, 4096])
            self.res = [sb("res%d" % i, [128, D]) for i in range(2)]
            self.ln_st = [sb("lnst%d" % i, [128, 2, 6]) for i in range(2)]
            self.ln_mv = [sb("lnmv%d" % i, [128, 4]) for i in range(2)]
        S.dma('pool', [], [('wout', 0)], lambda e: e.dma_start(
            out=self.wout_b[:], in_=self.wout[l].rearrange("p (k n) -> p k n", k=8), max_dma_last_dim=4096))
        lnpb1 = self.lnpb
        S.dma('sp', [], ['lnpb'], lambda e: e.dma_start(out=lnpb1[:], in_=self.lnp[l].broadcast_to([128, 4096])))
        src = self.x_in if l == 0 else self.x1
        srck = 'x_in' if l == 0 else 'x1'
        mk = [('mergedT', fc, tc) for fc in range(8) for tc in range(4)]
        wk = [('wout', 0)]
        for tt in range(16):
            r = self.res[tt % 2]
            rk = ('res', tt % 2)
            S.dma('sp', [(srck, tt)], [rk], lambda e, r=r, tt=tt: e.dma_start(out=r[:], in_=src[tt * 128:(tt + 1) * 128, :]))
            S.op('act', [rk], [rk], lambda e, r=r: e.activation(r[:], r[:], AF.Copy, scale=float(ALPHA)))
            if tt % 2 == 0:
                pm = self.PB[:, 0:1024]
                rkeys = [('PB', 0), ('PB', 1)]
            else:
                pm = self.PA[:, 0:1024]
                rkeys = [('PA', 0), ('PA', 1)]
            self.mmseq(mk + wk, rkeys,
                       [(pm[:, half * 512:(half + 1) * 512], [(self.B3[:, kc, tt * 128:(tt + 1) * 128], self.wout_b[:, kc, half * 512:(half + 1) * 512]) for kc in range(8)]) for half in range(2)])
            S.op('dve', rkeys + [rk], [rk], lambda e, r=r, pm=pm: e.scalar_tensor_tensor(r[:], in0=pm, scalar=0.5, in1=r[:], op0=ALU.mult, op1=ALU.add))
            self.layernorm(r, rk, self.lnpb[:, 0:1024], self.lnpb[:, 1024:2048], 'lnpb', tt)
            S.dma('sp', [rk], [('hres', tt)], lambda e, r=r, tt=tt: e.dma_start(out=self.hres[tt * 128:(tt + 1) * 128, :], in_=r[:]))
            self.to_featmajor2(r, rk, self.B1, 'hT', tt)
        if 'hT' in self.dbg:
            if True:
                self.dbgf = self.sb("dbgf", [128, 512])
            for kc in range(8):
                for q4 in range(4):
                    S.op('dve', [('hT', tt) for tt in range(16)] + [('dbg', 'hT')], ['dbgf'], lambda e, kc=kc, q4=q4: e.tensor_copy(self.dbgf[:, 0:512], self.B1[:, kc, q4 * 512:(q4 + 1) * 512]))
                    S.dma('sp', ['dbgf'], [('dbg', 'hT')], lambda e, kc=kc, q4=q4: e.dma_start(out=self.dbg['hT'][kc * 128:(kc + 1) * 128, q4 * 512:(q4 + 1) * 512], in_=self.dbgf[:, 0:512]))
            self.final_keys.append(('dbg', 'hT'))

    def to_featmajor2(self, tile, tkey, dstT, dkey, tt):
        S = self.S
        for half in range(2):
            ps = self.PC[:, 0:512] if half == 0 else self.PD[:, 0:512]
            pk = 'PC' if half == 0 else 'PD'
            self.tps([tkey, 'cst'], [pk], [(ps[:, j * 128:(j + 1) * 128], tile[:, (half * 4 + j) * 128:(half * 4 + j + 1) * 128]) for j in range(4)], self.ident_f)
            S.op('act', [pk], [(dkey, tt)], lambda e, ps=ps, half=half: e.activation(
                dstT[:, half * 4:(half + 1) * 4, tt * 128:(tt + 1) * 128], ps.rearrange("p (c t) -> p c t", c=4), AF.Copy))

    def ffn(self, l):
        self.psum_epoch()
        S, sb = self.S, self.sb
        self.phase([self.R_W, (self.R_B3[0] + 4096, self.R_B3[1])])
        if True:
            self.wup_b = [sb("wupb%d" % i, [128, 8, 512], BF16) for i in range(2)]
            self.wdn_b = [sb("wdnb%d" % i, [128, 4, 1024], BF16) for i in range(2)]
            self.rl = [sb("rl%d" % i, [128, 512]) for i in range(2)]
            self.ln_st = [sb("lnst%d" % i, [128, 2, 6]) for i in range(2)]
            self.ln_mv = [sb("lnmv%d" % i, [128, 4]) for i in range(2)]
        uT = self.B3
        hk = [('hT', tt) for tt in range(16)]
        it = 0
        def load_f(gi):
            wu = self.wup_b[gi % 2]
            wd = self.wdn_b[gi % 2]
            wuk, wdk = ('wup', gi % 2), ('wdn', gi % 2)
            S.dma('pool', [], [(wuk, 0)], lambda e, wu=wu, gi=gi: e.dma_start(
                out=wu[:], in_=self.wup[l, gi].rearrange("p (k n) -> p k n", k=8), max_dma_last_dim=4096))
            S.dma('pool', [], [(wdk, 0)], lambda e, wd=wd, gi=gi: e.dma_start(
                out=wd[:], in_=self.wdn[l, gi].rearrange("p (k n) -> p k n", k=4), max_dma_last_dim=4096))
        load_f(0)
        for gi in range(8):
            wu = self.wup_b[gi % 2]
            wd = self.wdn_b[gi % 2]
            wuk, wdk = ('wup', gi % 2), ('wdn', gi % 2)
            if gi + 1 < 8:
                load_f(gi + 1)
            wuks = [(wuk, 0)]
            wdks = [(wdk, 0)]
            for j in range(4):
                for tc in range(4):
                    bank = it % 4
                    ps = [self.PA[:, 0:512], self.PA[:, 512:1024], self.PC[:, 0:512], self.PD[:, 0:512]][bank]
                    pk = [('PA', 0), ('PA', 1), 'PC', 'PD'][bank]
                    cols = slice(tc * 512, (tc + 1) * 512)
                    self.mm(wuks + hk[tc * 4:(tc + 1) * 4], [pk], ps, [(wu[:, kc, j * 128:(j + 1) * 128], self.B1[:, kc, cols]) for kc in range(8)])
                    rl = self.rl[it % 2]
                    rlk = ('rl', it % 2)
                    S.op('act', [pk], [rlk], lambda e, rl=rl, ps=ps: e.activation(rl[:], ps, AF.Relu))
                    eng = 'dve'
                    S.op(eng, [rlk], [('uT', j)], lambda e, rl=rl, j=j, cols=cols: e.tensor_tensor(uT[:, j, cols], rl[:], rl[:], ALU.mult))
                    it += 1
            for tt in range(16):
                if tt % 2 == 0:
                    pm = self.PB[:, 0:1024]
                    rkeys = [('PB', 0), ('PB', 1)]
                else:
                    pm = self.PA[:, 1024:2048]
                    rkeys = [('PA', 2), ('PA', 3)]
                self.mmseq([('uT', j) for j in range(4)] + wdks, rkeys,
                           [(pm[:, half * 512:(half + 1) * 512], [(uT[:, j, tt * 128:(tt + 1) * 128], wd[:, j, half * 512:(half + 1) * 512]) for j in range(4)]) for half in range(2)])
                if gi == 0:
                    S.op('act', rkeys, [('acc', tt)], lambda e, tt=tt, pm=pm: e.activation(self.acc[:, tt, :], pm, AF.Copy))
                else:
                    S.op('dve', rkeys + [('acc', tt)], [('acc', tt)], lambda e, tt=tt, pm=pm: e.tensor_tensor(self.acc[:, tt, :], pm, self.acc[:, tt, :], ALU.add))
        S.barrier()
        self.res = [self.wup_b[i][:, 0:4, :].rearrange("p a b -> p (a b)").bitcast(F32) for i in range(2)]
        self.lnpb = self.wdn_b[0][:].rearrange("p a b -> p (a b)").bitcast(F32)
        lnpb2 = self.lnpb
        S.dma('sp', [], ['lnpb'], lambda e: e.dma_start(out=lnpb2, in_=self.lnp[l][:, 2048:4096].broadcast_to([128, 2048])))
        dst = self.out if l == DEPTH - 1 else self.x1
        dstk = 'out' if l == DEPTH - 1 else 'x1'
        for tt in range(16):
            r = self.res[tt % 2]
            rk = ('res', tt % 2)
            S.dma('sp', [('hres', tt)], [rk], lambda e, r=r, tt=tt: e.dma_start(out=r[:], in_=self.hres[tt * 128:(tt + 1) * 128, :]))
            S.op('dve', [rk, ('acc', tt)], [rk], lambda e, r=r, tt=tt: e.scalar_tensor_tensor(r[:], in0=r[:], scalar=float(ALPHA), in1=self.acc[:, tt, :], op0=ALU.mult, op1=ALU.add))
            self.layernorm(r, rk, self.lnpb[:, 0:1024], self.lnpb[:, 1024:2048], 'lnpb', tt)
            S.dma('sp', [rk], [(dstk, tt)], lambda e, r=r, tt=tt: e.dma_start(out=dst[tt * 128:(tt + 1) * 128, :], in_=r[:]))
            if l == DEPTH - 1:
                self.final_keys.append((dstk, tt))
            else:
                self.to_featmajor2(r, rk, self.xT, 'xT', tt)


def sel_max(e, b):
    ins = None
    for t in range(8):
        ins = e.max(b.m8[:, t, :], b.gm[:, t * 8:(t + 1) * 8])
    return ins


def sel_lt(e, b):
    ins = None
    for t in range(8):
        ins = e.tensor_scalar(b.lt[:, t * 8:(t + 1) * 8], b.gm[:, t * 8:(t + 1) * 8], b.m8[:, t, 2:3], NEG, ALU.is_lt, ALU.mult)
    return ins


_CACHE = {}


def kernel(**inputs):
    inp = {k: np.asarray(v) for k, v in inputs.items()}
    w = prep_weights(inp)
    consts, e8 = make_consts()
    x = np.ascontiguousarray(inp['x'], dtype=np.float32)
    pos = np.ascontiguousarray(inp['positions'], dtype=np.int32)
    nc = Builder().build()
    in_maps = []
    for b in range(NCORES):
        m = {"x": x[b], "pos": pos[b:b + 1], "consts": consts, "e8": e8}
        m.update(w)
        in_maps.append(m)
    res = run_bass_kernel_spmd(nc, in_maps, core_ids=list(range(NCORES)))
    out = np.stack([np.asarray(r["out"], dtype=np.float32) for r in res.results], axis=0)
    return out
```

```python
import math
import numpy as np
import concourse.bass as bass
import concourse.mybir as mybir
from concourse.bass_utils import run_bass_kernel_spmd

F32 = mybir.dt.float32
BF16 = mybir.dt.bfloat16
I32 = mybir.dt.int32
AF = mybir.ActivationFunctionType
ALU = mybir.AluOpType
AX = mybir.AxisListType

D = 1024
SEQ = 2048
DEPTH = 2
NCORES = 8
IN_W = 11296
ALPHA = (2 * DEPTH) ** 0.25
LN_EPS = 1e-5
RMS_EPS = 1e-5
NEG = -1.0e5
PI = math.pi

DEBUG = {}
STOP_AFTER = None
NO_BARRIER = False
VARIANT = 0


class Sched:
    ENG = ('pe', 'act', 'dve', 'pool', 'sp')

    def __init__(self, nc, n_dma_sems=24):
        self.nc = nc
        self.sem = {e: nc.alloc_semaphore('s_' + e) for e in self.ENG}
        self.cnt = {e: 0 for e in self.ENG}
        self.dsem = [nc.alloc_semaphore('d%d' % i) for i in range(n_dma_sems)]
        self.dcnt = [0] * n_dma_sems
        self.drr = {'sp': 0, 'pool': 0}
        self.dpool = {'sp': list(range(0, n_dma_sems // 2)), 'pool': list(range(n_dma_sems // 2, n_dma_sems))}
        self.waited = {}
        self.last_w = {}
        self.readers = {}
        self.q = {e: [] for e in self.ENG}

    def barrier(self):
        if NO_BARRIER is True:
            return
        self.epoch = {}
        scr = self.bar_scratch
        d_sp = {('dma', i): self.dcnt[i] for i in self.dpool['sp'] if self.dcnt[i] > 0}
        d_pl = {('dma', i): self.dcnt[i] for i in self.dpool['pool'] if self.dcnt[i] > 0}
        self._token('act', d_sp, lambda e: e.memzero(scr[:, 0:1]))
        t1 = ('act', self.cnt['act'])
        d_pl[t1[0]] = t1[1]
        self._token('pool', d_pl, lambda e: e.memset(scr[:, 1:2], 0.0))
        d3 = {e: c for e, c in self.cnt.items() if c > 0}
        self._token('dve', d3, lambda e: e.memset(scr[:, 2:3], 0.0))
        self.epoch = {'dve': self.cnt['dve']}

    def _token(self, e, deps, emit):
        waits = self._waits(e, deps)
        self.cnt[e] += 1
        sem = self.sem[e]

        def thunk(eng):
            for s, v in waits:
                eng.wait_ge(s, v)
            emit(eng).then_inc(sem, 1)
        self.q[e].append(thunk)

    def _deps(self, reads, writes):
        deps = dict(getattr(self, 'epoch', {}))

        def add(src, idx):
            if deps.get(src, 0) < idx:
                deps[src] = idx
        for k in reads:
            lw = self.last_w.get(k)
            if lw is not None:
                add(*lw)
        for k in writes:
            lw = self.last_w.get(k)
            if lw is not None:
                add(*lw)
            for src, idx in self.readers.get(k, {}).items():
                add(src, idx)
        return deps

    def _waits(self, e, deps):
        out = []
        self.maxw = getattr(self, 'maxw', {})
        for src, idx in deps.items():
            if self.waited.get((e, src), 0) >= idx:
                continue
            self.waited[(e, src)] = idx
            if isinstance(src, tuple):
                out.append((self.dsem[src[1]], 16 * idx))
            else:
                out.append((self.sem[src], idx))
        self.maxw[len(out)] = self.maxw.get(len(out), 0) + 1
        return out

    def _record(self, token, reads, writes):
        src, idx = token
        for k in reads:
            r = self.readers.setdefault(k, {})
            if r.get(src, 0) < idx:
                r[src] = idx
        for k in writes:
            self.last_w[k] = (src, idx)
            self.readers[k] = {}

    @staticmethod
    def _bank(k):
        if isinstance(k, tuple) and k and k[0] == 'PA':
            return ('BANK', k[1])
        if isinstance(k, tuple) and k and k[0] == 'PB':
            return ('BANK', 4 + k[1])
        if k in ('PC', 'PCg', 'PCo'):
            return ('BANK', 6)
        if k == 'PD':
            return ('BANK', 7)
        return None

    def _canon(self, reads, writes):
        r2, w2 = [], []
        for k in reads:
            b = self._bank(k)
            if b is None:
                r2.append(k)
            elif b not in w2:
                w2.append(b)
        for k in writes:
            b = self._bank(k)
            if b is None:
                w2.append(k)
            elif b not in w2:
                w2.append(b)
        return r2, w2

    def op(self, e, reads, writes, emit):
        reads, writes = self._canon(reads, writes)
        deps = self._deps(reads, writes)
        waits = self._waits(e, deps)
        self.cnt[e] += 1
        idx = self.cnt[e]
        sem = self.sem[e]

        def thunk(eng):
            for s, v in waits:
                eng.wait_ge(s, v)
            emit(eng).then_inc(sem, 1)
        self.q[e].append(thunk)
        self._record((e, idx), reads, writes)

    def dma(self, e, reads, writes, emit):
        pool = self.dpool[e]
        s = pool[self.drr[e] % len(pool)]
        self.drr[e] += 1
        src = ('dma', s)
        deps = self._deps(reads, writes)
        if self.dcnt[s] > 0 and deps.get(src, 0) < self.dcnt[s]:
            deps[src] = self.dcnt[s]
        waits = self._waits(e, deps)
        self.dcnt[s] += 1
        idx = self.dcnt[s]
        sem = self.dsem[s]

        def thunk(eng):
            for sm, v in waits:
                eng.wait_ge(sm, v)
            emit(eng).then_inc(sem, 16)
        self.q[e].append(thunk)
        self._record((src, idx), reads, writes)

    def alias(self, old_keys, new_keys):
        acc = {}
        for k in old_keys:
            lw = self.last_w.get(k)
            if lw is not None and acc.get(lw[0], 0) < lw[1]:
                acc[lw[0]] = lw[1]
            for src, idx in self.readers.get(k, {}).items():
                if acc.get(src, 0) < idx:
                    acc[src] = idx
        for k in new_keys:
            r = self.readers.setdefault(k, {})
            for src, idx in acc.items():
                if r.get(src, 0) < idx:
                    r[src] = idx

    def finish(self, final_keys):
        self.barrier()
        deps = self._deps(final_keys, [])
        waits = self._waits('sp', deps)

        def thunk(eng):
            for s, v in waits:
                eng.wait_ge(s, v)
        self.q['sp'].append(thunk)
        nc = self.nc
        q = self.q
        with nc.Block() as block:
            @block.sync
            def _(eng):
                for t in q['sp']:
                    t(eng)

            @block.tensor
            def _(eng):
                for t in q['pe']:
                    t(eng)

            @block.scalar
            def _(eng):
                for t in q['act']:
                    t(eng)

            @block.vector
            def _(eng):
                for t in q['dve']:
                    t(eng)

            @block.gpsimd
            def _(eng):
                for t in q['pool']:
                    t(eng)


C_ID, C_TRI, C_ONES, C_RM, C_INVF, C_SGN, C_NEGP, C_FLOOR, NC_CONST = 0, 128, 256, 384, 512, 513, 514, 578, 648


def make_consts():
    c = np.zeros((128, NC_CONST), np.float32)
    c[:, C_ID:C_ID + 128] = np.eye(128, dtype=np.float32)
    c[:, C_TRI:C_TRI + 128] = np.triu(np.ones((128, 128), np.float32))
    c[:, C_ONES:C_ONES + 128] = 1.0
    rm = np.zeros((128, 128), np.float32)
    inv = (500000.0 ** (-np.arange(0, 16, 2, dtype=np.float32) / 16)).astype(np.float32)
    for blk in range(2):
        for d in range(16):
            src = d + 8 if d < 8 else d - 8
            rm[blk * 64 + src, blk * 64 + d] = 1.0
    c[:, C_RM:C_RM + 128] = rm
    for p in range(128):
        d = p % 64
        if d < 16:
            c[p, C_INVF] = inv[d % 8]
            c[p, C_SGN] = -1.0 if d < 8 else 1.0
    for qt in range(8):
        blk = (8 + qt) // 2
        for j in range(8):
            c[:, C_NEGP + qt * 8 + j] = 0.0 if j < blk else -1.0e30
            c[:, C_FLOOR + qt * 8 + j] = NEG if j < blk else 0.0
    e8 = np.zeros((8, SEQ), np.float32)
    for j in range(8):
        e8[j, j * 256:(j + 1) * 256] = 1.0
    return c, e8


def prep_weights(inp):
    out = {}
    L = DEPTH

    def kmaj(w):
        K, N = w.shape
        return w.reshape(K // 128, 128, N).transpose(1, 0, 2)

    w_in = inp['w_in']
    wqkv = np.empty((L, 8, 128, 8, 384), np.float32)
    wssm = np.empty((L, 8, 128, 8, 768), np.float32)
    wdt = np.empty((L, 128, 8, 32), np.float32)
    wc1 = np.empty((L, 8, 128, 5120), np.float32)
    wout = np.empty((L, 128, 8, 1024), np.float32)
    wup = np.empty((L, 8, 128, 8, 512), np.float32)
    wdn = np.empty((L, 8, 128, 4, 1024), np.float32)
    cw = np.empty((L, 128, 8, 4, 5), np.float32)
    for l in range(L):
        wi = kmaj(w_in[l])
        for hp in range(8):
            wqkv[l, hp, :, :, 0:128] = wi[:, :, hp * 128:(hp + 1) * 128]
            wqkv[l, hp, :, :, 128:256] = wi[:, :, 1024 + hp * 128:1024 + (hp + 1) * 128]
            wqkv[l, hp, :, :, 256:384] = wi[:, :, 2048 + hp * 128:2048 + (hp + 1) * 128]
        for g in range(8):
            wssm[l, g, :, :, 0:256] = wi[:, :, 3072 + g * 256:3072 + (g + 1) * 256]
            wssm[l, g, :, :, 256:512] = wi[:, :, 5120 + g * 256:5120 + (g + 1) * 256]
            wssm[l, g, :, :, 512:640] = wi[:, :, 7168 + g * 128:7168 + (g + 1) * 128]
            wssm[l, g, :, :, 640:768] = wi[:, :, 8192 + g * 128:8192 + (g + 1) * 128]
        wdt[l] = wi[:, :, 9216:9248]
        wap = kmaj(inp['w_attn_proj'][l])
        wsp = kmaj(inp['w_ssm_proj'][l])
        for fc in range(8):
            blk = np.empty((128, 8, 256), np.float32)
            blk[:, :, 0:128] = wi[:, :, 9248 + fc * 128:9248 + (fc + 1) * 128]
            blk[:, :, 128:256] = wi[:, :, 10272 + fc * 128:10272 + (fc + 1) * 128]
            wc1[l, fc, :, 0:2048] = blk.reshape(128, 2048)
            wc1[l, fc, :, 2048:3072] = wap[:, :, fc * 128:(fc + 1) * 128].reshape(128, 1024)
            wc1[l, fc, :, 3072:5120] = wsp[:, :, fc * 128:(fc + 1) * 128].reshape(128, 2048)
        wout[l] = kmaj(inp['w_out'][l])
        wu = kmaj(inp['w_up'][l])
        wd = kmaj(inp['w_down'][l])
        for gI in range(8):
            wup[l, gI] = wu[:, :, gI * 512:(gI + 1) * 512]
            wdn[l, gI] = wd[:, gI * 4:(gI + 1) * 4, :]
        cwl = inp['conv_w'][l]
        cbl = inp['conv_b'][l]
        for g in range(8):
            offs = [g * 256, g * 256 + 128, 2048 + g * 128, 3072 + g * 128]
            for ci, o in enumerate(offs):
                cw[l, :, g, ci, 0:4] = cwl[:, o:o + 128].T
                cw[l, :, g, ci, 4] = cbl[o:o + 128]
    out['wqkv'] = wqkv.reshape(L, 8, 128, 3072)
    out['wssm'] = wssm.reshape(L, 8, 128, 6144)
    out['wdt'] = wdt.reshape(L, 128, 256)
    out['wc1'] = wc1
    out['wout'] = wout.reshape(L, 128, 8192)
    out['wup'] = wup.reshape(L, 8, 128, 4096)
    out['wdn'] = wdn.reshape(L, 8, 128, 4096)
    out['cw'] = cw.reshape(L, 128, 160)
    small = np.concatenate([inp['dt_bias'], inp['a_log'], inp['d_skip']], axis=1)
    out['small'] = np.ascontiguousarray(small.reshape(L, 1, 96))
    out['normw'] = np.ascontiguousarray(inp['ssm_norm_w'].reshape(L, 1, 2048))
    lnp = np.stack([inp['ln1_g'], inp['ln1_b'], inp['ln2_g'], inp['ln2_b']], axis=1)
    out['lnp'] = np.ascontiguousarray(lnp.reshape(L, 1, 4096))
    return {k: np.ascontiguousarray(v, dtype=np.float32) for k, v in out.items()}


class Arena:
    def __init__(self, base_ap, segments):
        self.base = base_ap
        self.segs = [list(x) for x in segments]

    def alloc(self, shape, dtype=F32):
        P = shape[0]
        n = 1
        for d in shape[1:]:
            n *= d
        per = 4 if dtype in (F32, I32) else 2
        ncols = (n * per + 3) // 4
        ncols = (ncols + 7) // 8 * 8
        for sg in self.segs:
            if sg[1] - sg[0] >= ncols:
                off = sg[0]
                sg[0] += ncols
                ap = self.base[0:P, off:off + (n * per + 3) // 4]
                if dtype != F32:
                    ap = ap.bitcast(dtype)
                if len(shape) == 3:
                    ap = ap.rearrange("p (a b) -> p a b", a=shape[1])
                elif len(shape) == 4:
                    ap = ap.rearrange("p (a b c) -> p a b c", a=shape[1], b=shape[2])
                return ap
        raise RuntimeError("arena out of memory for %s" % (shape,))


class _Stop(Exception):
    pass


class Builder:
    def __init__(self):
        nc = bass.Bass("TRN2", target_bir_lowering=False)
        self.nc = nc
        self.S = Sched(nc)
        self.rec = None
        S_ = self.S
        S_._op, S_._dma = S_.op, S_.dma

        def _rop(*a):
            if self.rec is not None:
                self.rec.append(lambda: S_._op(*a))
            else:
                S_._op(*a)

        def _rdma(*a):
            if self.rec is not None:
                self.rec.append(lambda: S_._dma(*a))
            else:
                S_._dma(*a)
        S_.op, S_.dma = _rop, _rdma
        L = DEPTH
        dt = lambda n, s, d=F32, k="ExternalInput": nc.dram_tensor(n, s, d, kind=k).ap()
        self.x_in = dt("x", [SEQ, D])
        self.pos_in = dt("pos", [1, SEQ], I32)
        self.consts_in = dt("consts", [128, NC_CONST])
        self.e8_in = dt("e8", [8, SEQ])
        self.wqkv = dt("wqkv", [L, 8, 128, 3072])
        self.wssm = dt("wssm", [L, 8, 128, 6144])
        self.wdt = dt("wdt", [L, 128, 256])
        self.wc1 = dt("wc1", [L, 8, 128, 5120])
        self.wout = dt("wout", [L, 128, 8192])
        self.wup = dt("wup", [L, 8, 128, 4096])
        self.wdn = dt("wdn", [L, 8, 128, 4096])
        self.cw = dt("cw", [L, 128, 160])
        self.small = dt("small", [L, 1, 96])
        self.normw = dt("normw", [L, 1, 2048])
        self.lnp = dt("lnp", [L, 1, 4096])
        self.out = dt("out", [SEQ, D], F32, "ExternalOutput")
        self.x1 = dt("x1s", [SEQ, D], F32, "Internal")
        self.hres = dt("hres", [SEQ, D], F32, "Internal")
        self.dbg = {}
        for name, shape in DEBUG.items():
            self.dbg[name] = dt("dbg_" + name, list(shape), F32, "ExternalOutput")
        self.final_keys = []

    def sb(self, name, shape, dtype=F32):
        return self.ar.alloc(list(shape), dtype)

    def phase(self, segs):
        self.S.barrier()
        self.ar = Arena(self.arena, segs)

    def mm(self, reads, writes, out, pairs, transpose=False):
        def emit(e):
            n = len(pairs)
            ins = None
            for i, (l, r) in enumerate(pairs):
                ins = e.matmul(out, lhsT=l, rhs=r, start=(i == 0), stop=(i == n - 1))
            return ins
        self.S.op('pe', reads, writes, emit)

    def mmseq(self, reads, writes, groups):
        def emit(e):
            ins = None
            for out, pairs in groups:
                n = len(pairs)
                for i, (l, r) in enumerate(pairs):
                    ins = e.matmul(out, lhsT=l, rhs=r, start=(i == 0), stop=(i == n - 1))
            return ins
        self.S.op('pe', reads, writes, emit)

    def tps(self, reads, writes, items, ident):
        def emit(e):
            ins = None
            for o, i in items:
                ins = e.transpose(o, i, ident)
            return ins
        self.S.op('pe', reads, writes, emit)

    def dump(self, name, ap, keys):
        if name in self.dbg:
            if not isinstance(keys, list):
                keys = [keys]
            self.S.dma('sp', keys, [('dbg', name)], lambda e: e.dma_start(out=self.dbg[name], in_=ap))
            self.final_keys.append(('dbg', name))

    def build(self):
        nc, S = self.nc, self.S
        sb = self.sb
        nbytes = int(nc.sbuf_bytes_remaining) - 256
        NA = nbytes // 4 // 8 * 8
        self.arena = nc.alloc_sbuf_tensor("arena", [128, NA], F32)[:]
        self.R_P = (0, 11264)
        self.R_B1 = (11264, 19456)
        self.R_B2 = (19456, 35840)
        self.R_B3 = (35840, 44032)
        self.R_W = (44032, NA)
        assert NA - 44032 > 8000, NA
        self.ar = Arena(self.arena, [self.R_P])
        self.S.bar_scratch = sb("barscr", [128, 8])
        cst = sb("cst", [128, NC_CONST])
        self.cst = cst
        S.dma('sp', [], ['cst'], lambda e: e.dma_start(out=cst[:], in_=self.consts_in))
        self.ident_f = cst[:, C_ID:C_ID + 128]
        self.tri_f = cst[:, C_TRI:C_TRI + 128]
        self.ones_f = cst[:, C_ONES:C_ONES + 128]
        cbf = sb("cbf", [128, 512], BF16)
        S.op('dve', ['cst'], ['cbf'], lambda e: e.tensor_copy(cbf[:], cst[:, 0:512]))
        self.ident_b = cbf[:, 0:128]
        self.tri_b = cbf[:, 128:256]
        self.rm_b = cbf[:, 384:512]
        self.nh = sb("nh", [128, 8])
        S.op('pool', [], ['nh'], lambda e: e.memset(self.nh[:], -0.5))
        self.ropeC = sb("ropeC", [128, SEQ], BF16)
        self.ropeS = sb("ropeS", [128, SEQ], BF16)
        self.xT = sb("xT", [128, 8, SEQ], BF16)
        A = self.arena
        self.B1 = A[:, self.R_B1[0]:self.R_B1[1]].bitcast(BF16).rearrange("p (c t) -> p c t", c=8)
        self.B2 = A[:, self.R_B2[0]:self.R_B2[1]]
        self.B3 = A[:, self.R_B3[0]:self.R_B3[1]].bitcast(BF16).rearrange("p (c t) -> p c t", c=8)
        self.yT = self.B2.bitcast(BF16).rearrange("p (c t) -> p c t", c=16)
        self.acc = self.B2.rearrange("p (t f) -> p t f", t=16)
        self.ar = Arena(self.arena, [self.R_B2])
        self.PA = nc.alloc_psum_tensor("PA", [128, 2048], F32)
        self.PB = nc.alloc_psum_tensor("PB", [128, 1024], F32)
        self.PC = nc.alloc_psum_tensor("PC", [128, 512], F32)
        self.PD = nc.alloc_psum_tensor("PD", [128, 512], F32)
        self.rope_tables()
        if STOP_AFTER == 'rope':
            return self.finish()
        self.load_xT()
        if STOP_AFTER == 'xT':
            self.dbg_tile('xT0', self.xT[:, 3, 1024:1536], self.xT_keys(), 512)
            return self.finish()
        try:
            for l in range(DEPTH):
                self.layer(l)
                if STOP_AFTER == 'layer0':
                    break
        except _Stop:
            pass
        return self.finish()

    def finish(self):
        self.S.finish(self.final_keys)
        return self.nc

    def rope_tables(self):
        S, sb = self.S, self.sb
        cst = None
        posi = sb("posi", [128, SEQ], I32)
        S.dma('sp', [], ['posi'], lambda e: e.dma_start(out=posi[:], in_=self.pos_in.broadcast_to([128, SEQ])))
        ang = sb("ang", [128, SEQ])
        tmp = sb("rtmp", [128, SEQ])
        tmi = sb("rtmi", [128, SEQ], I32)
        rc = sb("rc", [128, SEQ])
        rs = sb("rs", [128, SEQ])
        invf = self.cst[:, C_INVF:C_INVF + 1]
        sgn = self.cst[:, C_SGN:C_SGN + 1]
        S.op('dve', ['posi'], ['ang'], lambda e: e.tensor_copy(ang[:], posi[:]))
        S.op('dve', ['ang', 'cst'], ['ang'], lambda e: e.tensor_scalar(ang[:], ang[:], invf, None, ALU.mult))
        for which, dst in ((0, rs), (1, rc)):
            off = 0.0 if which == 0 else PI / 2
            S.op('dve', ['ang'], ['rtmp'], lambda e, off=off: e.tensor_scalar(tmp[:], ang[:], off, 1.0 / (2 * PI), ALU.add, ALU.mult))
            S.op('dve', ['rtmp'], ['rtmi'], lambda e: e.tensor_copy(tmi[:], tmp[:]))
            S.op('dve', ['rtmi'], ['rtmp'], lambda e: e.tensor_copy(tmp[:], tmi[:]))
            S.op('dve', ['rtmp', 'ang'], ['rtmp'], lambda e: e.scalar_tensor_tensor(tmp[:], in0=tmp[:], scalar=-2 * PI, in1=ang[:], op0=ALU.mult, op1=ALU.add))
            S.op('dve', ['rtmp'], [('rope', which)], lambda e, off=off, dst=dst: e.tensor_scalar(dst[:], tmp[:], off, None, ALU.add))
            S.op('dve', [('rope', which)], ['rtmp'], lambda e, dst=dst: e.tensor_scalar(tmp[:], dst[:], PI, -2 * PI, ALU.is_gt, ALU.mult))
            S.op('dve', ['rtmp', ('rope', which)], [('rope', which)], lambda e, dst=dst: e.tensor_tensor(dst[:], dst[:], tmp[:], ALU.add))
            S.op('dve', [('rope', which)], ['rtmp'], lambda e, dst=dst: e.tensor_scalar(tmp[:], dst[:], -PI, 2 * PI, ALU.is_lt, ALU.mult))
            S.op('dve', ['rtmp', ('rope', which)], [('rope', which)], lambda e, dst=dst: e.tensor_tensor(dst[:], dst[:], tmp[:], ALU.add))
            S.op('dve', [('rope', which)], [('rope', which)], lambda e, dst=dst: e.tensor_scalar(dst[:], dst[:], 3.1415925, -3.1415925, ALU.min, ALU.max))
            S.op('act', [('rope', which)], [('rope', which)], lambda e, dst=dst: e.activation(dst[:], dst[:], AF.Sin))
        S.op('dve', [('rope', 0), 'cst'], [('rope', 0)], lambda e: e.tensor_scalar(self.ropeS[:], rs[:], sgn, None, ALU.mult))
        S.op('dve', [('rope', 1)], [('rope', 1)], lambda e: e.tensor_copy(self.ropeC[:], rc[:]))
        self.dump('ropeC', rc[:], ('rope', 1))
        self.dump('ropeS', rs[:], ('rope', 0))

    def load_xT(self):
        S, sb = self.S, self.sb
        self.xt_buf = [sb("xtile%d" % i, [128, D]) for i in range(2)]
        for tt in range(16):
            xt = self.xt_buf[tt % 2]
            k = ('xtile', tt % 2)
            S.dma('sp', [], [k], lambda e, xt=xt, tt=tt: e.dma_start(out=xt[:], in_=self.x_in[tt * 128:(tt + 1) * 128, :]))
            self.to_featmajor(xt, k, self.xT, 'xT', tt)

    def to_featmajor(self, tile, tkey, dstT, dkey, tt):
        S = self.S
        for half in range(2):
            ps = self.PA[:, half * 512:(half + 1) * 512]
            pk = ('PA', half)
            self.tps([tkey, 'cst'], [pk], [(ps[:, j * 128:(j + 1) * 128], tile[:, (half * 4 + j) * 128:(half * 4 + j + 1) * 128]) for j in range(4)], self.ident_f)
            eng = 'act' if half == 0 else 'dve'
            if eng == 'act':
                S.op('act', [pk], [(dkey, tt)], lambda e, ps=ps, half=half: e.activation(
                    dstT[:, half * 4:(half + 1) * 4, tt * 128:(tt + 1) * 128], ps.rearrange("p (c t) -> p c t", c=4), AF.Copy))
            else:
                S.op('dve', [pk], [(dkey, tt)], lambda e, ps=ps, half=half: e.tensor_copy(
                    dstT[:, half * 4:(half + 1) * 4, tt * 128:(tt + 1) * 128], ps.rearrange("p (c t) -> p c t", c=4)))

    def layer(self, l):
        self.attention(l)
        if STOP_AFTER == 'att':
            raise _Stop()
        self.ssd(l)
        if STOP_AFTER == 'ssd':
            raise _Stop()
        self.merge(l)
        if STOP_AFTER == 'merge':
            raise _Stop()
        self.mix_ln1(l)
        if STOP_AFTER == 'ln1':
            raise _Stop()
        self.ffn(l)

    def psum_epoch(self):
        keys = [('PA', i) for i in range(4)] + [('PB', 0), ('PB', 1), ('PB', 0, 0), ('PB', 0, 1)] + \
               [('PB', 1, h) for h in range(4)] + ['PC', 'PCg', 'PCo', 'PD']
        self.S.alias(keys, keys)

    def dbg_qk(self, h):
        S = self.S
        if 'qa0' in self.dbg:
            qa, ka = self.qa[h], self.ka[h]
            dq = self.sb("dbgq", [128, SEQ])
            dk = self.sb("dbgk", [128, SEQ])
            S.op('dve', [('qa', h)], ['dbgq'], lambda e: e.tensor_copy(dq[0:72, :], qa[0:72, :]))
            S.op('dve', [('ka', h)], ['dbgk'], lambda e: e.tensor_copy(dk[0:72, :], ka[0:72, :]))
            self.dump('qa0', dq[0:72, :], 'dbgq')
            self.dump('ka0', dk[0:72, :], 'dbgk')

    def dbg_tile(self, name, ap, keys, ncols):
        if name in self.dbg:
            t = self.sb("dbg_" + name, [128, ncols])
            self.S.op('dve', keys, ['dbgt_' + name], lambda e: e.tensor_copy(t[:], ap))
            self.dump(name, t[:], 'dbgt_' + name)

    def xT_keys(self):
        return [('xT', tt) for tt in range(16)]

    def attention(self, l):
        self.psum_epoch()
        S, sb, nc = self.S, self.sb, self.nc
        self.phase([self.R_W, self.R_B2, (self.R_B3[0] + 7168, self.R_B3[1])])
        if True:
            self.wqkv_b = [sb("wqkv%d" % i, [128, 8, 384], BF16) for i in range(2)]
            self.qa = [sb("qa%d" % i, [72, SEQ], BF16) for i in range(4)]
            self.ka = [sb("ka%d" % i, [72, SEQ], BF16) for i in range(4)]
            self.va = [sb("va%d" % i, [128, 16, 65], BF16) for i in range(4)]
            self.qbf = [sb("qbf%d" % i, [128, 512], BF16) for i in range(2)]
            self.t1 = [sb("t1_%d" % i, [128, 512]) for i in range(2)]
            self.t2 = [sb("t2_%d" % i, [128, 512]) for i in range(2)]
            self.atm = [sb("atm%d" % i, [128, 16, 128], BF16) for i in range(2)]
            self.km = sb("km", [64, 8])
            self.kmh = sb("kmh", [64, 8], BF16)
            self.kml = sb("kml", [64, 8], BF16)
            self.gm = sb("gm", [128, 64])
            self.m8 = sb("m8", [128, 8, 8])
            self.lt = sb("lt", [128, 64])
            self.bpad = sb("bpad", [128, 8, 72], BF16)
            self.rden = sb("rden", [128, 4])
            self.NPT = 28
            for i in range(4):
                S.op('pool', [], [('qa', i)], lambda e, i=i: e.memset(self.qa[i][64:72, :], 0.0))
                S.dma('pool', [], [('ka', i)], lambda e, i=i: e.dma_start(out=self.ka[i][64:72, :], in_=self.e8_in))
                S.op('pool', [], [('va', i)], lambda e, i=i: e.memset(self.va[i][:, :, 64:65], 1.0))
            S.op('pool', [], ['bpad'], lambda e: e.memset(self.bpad[:], 0.0))
            if STOP_AFTER == 'att_init':
                self.dbg_qk(0)
                raise _Stop()
        PT = [self.B3[:, i // 4, (i % 4) * 512:(i % 4 + 1) * 512] for i in range(self.NPT)]
        attT = self.B1
        negp = self.cst[:, C_NEGP:C_NEGP + 64]
        floorb = self.cst[:, C_FLOOR:C_FLOOR + 64]
        pt_rr = [0]
        PDb = self.PD[:].bitcast(BF16)
        xk = self.xT_keys()

        def load_w(hp):
            wb = self.wqkv_b[hp % 2]
            S.dma('pool', [], [('wqkv', hp % 2, 0)], lambda e, wb=wb, hp=hp: e.dma_start(
                out=wb[:], in_=self.wqkv[l, hp].rearrange("p (k n) -> p k n", k=8), max_dma_last_dim=4096))
        load_w(0)
        for hp in range(8):
            wb = self.wqkv_b[hp % 2]
            wk = ('wqkv', hp % 2, 0)
            wk1 = wk2 = wk3 = wk
            if STOP_AFTER == 'att_w0':
                self.dbg_tile('wb0', wb[:].rearrange("p a b -> p (a b)"), [wk, wk1, wk2, wk3], 3072)
                raise _Stop()
            if hp + 1 < 8:
                load_w(hp + 1)
            if STOP_AFTER == 'att_w1':
                self.dbg_tile('wb0', wb[:].rearrange("p a b -> p (a b)"), [wk, wk1, wk2, wk3], 3072)
                raise _Stop()
            par = (hp % 2) * 2
            hA, hB = par, par + 1
            cnt = 0
            for which in (1, 0):
                dst = self.ka if which == 1 else self.qa
                dn = 'ka' if which == 1 else 'qa'
                for tc in range(4):
                    bank = cnt % 4
                    cnt += 1
                    ps = self.PA[:, bank * 512:(bank + 1) * 512]
                    pk = ('PA', bank)
                    cols = slice(tc * 512, (tc + 1) * 512)
                    def stopat(ch, ap=None, keys=None):
                        if STOP_AFTER == 'att_qk1' + ch:
                            if ap is not None:
                                self.dbg_tile('probe', ap, keys, 512)
                            raise _Stop()
                    self.mm([wk, wk1, wk2, wk3] + xk[tc * 4:(tc + 1) * 4], [pk], ps,
                            [(wb[:, kc, which * 128:(which + 1) * 128], self.xT[:, kc, cols]) for kc in range(8)])
                    stopat('a', ps, [pk])
                    qb = self.qbf[cnt % 2]
                    qk_ = ('qbf', cnt % 2)
                    if VARIANT != 6:
                        S.op('act', [pk], [qk_], lambda e, qb=qb, ps=ps: e.activation(qb[:], ps, AF.Copy))
                    stopat('b', qb[:], [qk_])
                    t1 = self.t1[cnt % 2]
                    t2 = self.t2[cnt % 2]
                    k1 = ('t1', cnt % 2)
                    k2 = ('t2', cnt % 2)
                    if VARIANT == 4:
                        t1 = self.sb("t1x", [128, 512])
                    if VARIANT in (5, 8):
                        S.op('dve', [('rope', 1)], [k1], lambda e, t1=t1, ps=ps, cols=cols: e.tensor_copy(t1[:], self.ropeC[:, cols]))
                    elif VARIANT in (1, 4):
                        S.op('dve', [pk, ('rope', 1)], [k1], lambda e, t1=t1, ps=ps, cols=cols: e.tensor_copy(t1[:], ps))
                    elif VARIANT == 2:
                        S.op('dve', [pk, ('rope', 1)], [k1], lambda e, t1=t1, ps=ps, cols=cols: e.tensor_copy(t1[:], self.ropeC[:, cols]))
                    elif VARIANT == 3:
                        S.op('dve', [pk, ('rope', 1)], [k1], lambda e, t1=t1, ps=ps, cols=cols: e.tensor_tensor(t1[:], ps, self.cst[:, 0:512], ALU.mult))
                    else:
                        S.op('dve', [pk, ('rope', 1)], [k1], lambda e, t1=t1, ps=ps, cols=cols: e.tensor_tensor(t1[:], ps, self.ropeC[:, cols], ALU.mult))
                    stopat('c', qb[:] if VARIANT == 7 else t1[:], [qk_, k1] if VARIANT in (7, 8) else [k1])
                    pr = self.PB[:, (cnt % 2) * 512:(cnt % 2 + 1) * 512]
                    prk = ('PB', cnt % 2)
                    self.mm([qk_, 'cbf'], [prk], pr, [(self.rm_b, qb[:])])
                    stopat('d', pr, [prk])
                    S.op('dve', [prk, ('rope', 0)], [k2], lambda e, t2=t2, pr=pr, cols=cols: e.tensor_tensor(t2[:], pr, self.ropeS[:, cols], ALU.mult))
                    stopat('e', t2[:], [k2])
                    S.op('dve', [k1, k2], [(dn, hA)], lambda e, t1=t1, t2=t2, dst=dst, cols=cols, hA=hA: e.tensor_tensor(dst[hA][0:64, cols], t1[0:64, :], t2[0:64, :], ALU.add))
                    stopat('f', dst[hA][0:64, cols], [(dn, hA)])
                    S.op('dve', [k1, k2], [(dn, hB)], lambda e, t1=t1, t2=t2, dst=dst, cols=cols, hB=hB: e.tensor_tensor(dst[hB][0:64, cols], t1[64:128, :], t2[64:128, :], ALU.add))
                    stopat('g', dst[hB][0:64, cols], [(dn, hB)])
            if STOP_AFTER == 'att_qk':
                raise _Stop()
            for tq in range(4):
                bank = cnt % 4
                cnt += 1
                ps = self.PA[:, bank * 512:(bank + 1) * 512]
                pk = ('PA', bank)
                self.mmseq([wk, wk1, wk2, wk3] + xk[tq * 4:(tq + 1) * 4], [pk],
                           [(ps[:, j * 128:(j + 1) * 128],
                             [(self.xT[:, kc, (tq * 4 + j) * 128:(tq * 4 + j + 1) * 128], wb[:, kc, 256:384]) for kc in range(8)])
                            for j in range(4)])
                psv = ps.rearrange("p (t c) -> p t c", t=4)
                if hp == 0 and tq == 0:
                    self.dbg_tile('psv', ps, [pk], 512)
                    self.dbg_tile('wb0', wb[:].rearrange("p a b -> p (a b)"), [wk, wk1, wk2, wk3], 3072)
                S.op('act', [pk], [('va', hA)], lambda e, psv=psv, tq=tq, hA=hA: e.activation(self.va[hA][:, tq * 4:(tq + 1) * 4, 0:64], psv[:, :, 0:64], AF.Copy))
                S.op('act', [pk], [('va', hB)], lambda e, psv=psv, tq=tq, hB=hB: e.activation(self.va[hB][:, tq * 4:(tq + 1) * 4, 0:64], psv[:, :, 64:128], AF.Copy))
            if STOP_AFTER == 'att_proj':
                self.dbg_tile('va0', self.va[0][:].rearrange("p a b -> p (a b)"), [('va', 0)], 1040)
                self.dbg_qk(hA)
                raise _Stop()
            atm = self.atm[hp % 2]
            ak = ('atm', hp % 2)
            for hh, hbuf in ((0, hA), (1, hB)):
                qa, ka, va = self.qa[hbuf], self.ka[hbuf], self.va[hbuf]
                qk, kk, vk = ('qa', hbuf), ('ka', hbuf), ('va', hbuf)
                S.op('dve', [kk], ['km'], lambda e, ka=ka: e.tensor_reduce(self.km[:], ka[0:64, :].rearrange("p (b t) -> p b t", b=8), AX.X, ALU.add))
                S.op('dve', ['km'], ['kmh'], lambda e: e.tensor_scalar(self.kmh[:], self.km[:], 1.0 / 256, None, ALU.mult))
                S.op('dve', ['km', 'kmh'], ['kml'], lambda e: e.scalar_tensor_tensor(self.kml[:], in0=self.km[:], scalar=1.0 / 256, in1=self.kmh[:], op0=ALU.mult, op1=ALU.subtract))
                pg = self.PC[:, 448:512]
                self.mmseq([qk, 'kmh', 'kml'], ['PCg'],
                           [(pg[:, t * 8:(t + 1) * 8], [(qa[0:64, (8 + t) * 128:(9 + t) * 128], self.kmh[:]), (qa[0:64, (8 + t) * 128:(9 + t) * 128], self.kml[:])]) for t in range(8)])
                S.op('dve', ['PCg', 'cst'], ['gm'], lambda e, pg=pg: e.tensor_tensor(self.gm[:], pg, negp, ALU.add))

                S.op('dve', ['gm'], ['m8'], lambda e: sel_max(e, self))
                S.op('dve', ['gm', 'm8'], ['lt'], lambda e: sel_lt(e, self))
                S.op('dve', ['lt', 'cst'], ['bpad'], lambda e: e.tensor_tensor(self.bpad[:, :, 64:72], self.lt[:].rearrange("p (t j) -> p t j", t=8), floorb.rearrange("p (t j) -> p t j", t=8), ALU.max))
                self.tps(['bpad', 'cbf'], ['PD'], [(PDb[0:72, t * 128:(t + 1) * 128], self.bpad[:, t, :]) for t in range(8)], self.ident_b)
                S.op('act', ['PD'], [qk], lambda e, qa=qa: e.activation(qa[64:72, 1024:2048], PDb[64:72, :], AF.Copy))
                if STOP_AFTER == 'att_gate':
                    self.dbg_qk(hA)
                    raise _Stop()
                slots = {}

                def stageA(qc):
                    sl = []
                    for kt in range(4 * qc + 4):
                        c0 = max(0, kt * 128 - qc * 512)
                        bank = kt % 2
                        ps = self.PB[:, bank * 512:(bank + 1) * 512]
                        pk = ('PB', bank)
                        self.mm([kk, qk], [pk], ps[:, c0:512], [(ka[0:72, kt * 128:(kt + 1) * 128], qa[0:72, qc * 512 + c0:(qc + 1) * 512])])
                        si = pt_rr[0]
                        pt_rr[0] = (pt_rr[0] + 1) % self.NPT
                        sl.append(si)
                        pt = PT[si]
                        S.op('act', [pk], [('PT', si)], lambda e, pt=pt, ps=ps, c0=c0: e.activation(pt[:, c0:512], ps[:, c0:512], AF.Exp, scale=0.125))
                        if kt >= 4 * qc:
                            S.op('dve', [('PT', si), 'cbf'], [('PT', si)], lambda e, pt=pt, c0=c0: e.tensor_tensor(pt[:, c0:c0 + 128], pt[:, c0:c0 + 128], self.tri_b, ALU.mult))
                    slots[qc] = sl

                def stageB(qc):
                    sl = slots[qc]
                    groups = []
                    for qi in range(4):
                        qt = 4 * qc + qi
                        groups.append((self.PC[:, qi * 65:(qi + 1) * 65],
                                       [(PT[sl[kt]][:, qi * 128:(qi + 1) * 128], va[:, kt, :]) for kt in range(qt + 1)]))
                    self.mmseq([('PT', s_) for s_ in sl] + [vk], ['PCo'], groups)
                    pv = self.PC[:, 0:260].rearrange("p (q c) -> p q c", q=4)
                    S.op('dve', ['PCo'], ['rden'], lambda e, pv=pv: e.reciprocal(self.rden[:], pv[:, :, 64]))

                    def norm(e, atm=atm, hh=hh, qc=qc):
                        ins = None
                        for qi in range(4):
                            ins = e.tensor_scalar(atm[:, 4 * qc + qi, hh * 64:(hh + 1) * 64], self.PC[:, qi * 65:qi * 65 + 64], self.rden[:, qi:qi + 1], None, ALU.mult)
                        return ins
                    S.op('dve', ['PCo', 'rden'], [ak], norm)

                stageA(0)
                for qc in range(4):
                    if qc + 1 < 4:
                        stageA(qc + 1)
                    stageB(qc)
            if hp == 0:
                self.dbg_tile('va0', self.va[0][:].rearrange("p a b -> p (a b)"), [('va', 0)], 1040)
                self.dbg_tile('atm0', atm[:].rearrange("p a b -> p (a b)"), [ak], 2048)
            for half in range(2):
                self.tps([ak, 'cbf'], ['PD'], [(PDb[:, t * 128:(t + 1) * 128], atm[:, half * 8 + t, :]) for t in range(8)], self.ident_b)
                S.op('act', ['PD'], [('attT', hp, half)], lambda e, half=half, hp=hp: e.activation(attT[:, hp, half * 1024:(half + 1) * 1024], PDb, AF.Copy))
        if 'attT' in self.dbg:
            self.dbgf = self.sb("dbgf", [128, 512])
            for hp in range(8):
                for q4 in range(4):
                    S.op('dve', [('attT', hp, 0), ('attT', hp, 1), ('dbg', 'attT')], ['dbgf'], lambda e, hp=hp, q4=q4: e.tensor_copy(self.dbgf[:, 0:512], attT[:, hp, q4 * 512:(q4 + 1) * 512]))
                    S.dma('sp', ['dbgf'], [('dbg', 'attT')], lambda e, hp=hp, q4=q4: e.dma_start(out=self.dbg['attT'][hp * 128:(hp + 1) * 128, q4 * 512:(q4 + 1) * 512], in_=self.dbgf[:, 0:512]))
            self.final_keys.append(('dbg', 'attT'))

    def ssd(self, l):
        self.psum_epoch()
        S, sb, nc = self.S, self.sb, self.nc
        self.phase([self.R_W, self.R_B3])
        if True:
            self.wssm_b = [sb("wssm0", [128, 8, 768], BF16)] * 2
            self.wdt_b = sb("wdtb", [128, 8, 32], BF16)
            self.smallb = sb("smallb", [128, 96])
            self.cwb = sb("cwb", [128, 8, 4, 5])
            self.dt_all = sb("dt_all", [128, 16, 32])
            self.adt_all = sb("adt_all", [128, 16, 32])
            self.acs_all = sb("acs_all", [128, 16, 32])
            self.dst_all = sb("dst_all", [128, 16, 32])
            self.ea_all = sb("ea_all", [128, 16, 32])
            self.lastb = sb("lastb", [128, 8, 32])
            self.cdb = sb("cdb", [128, 8, 32])
            self.a_b = sb("a_b", [128, 32])
            self.uext = [sb("uext%d" % i, [128, 4, 259]) for i in range(2)]
            self.xh = sb("xh", [128, 4, 256])
            self.tnh = sb("tnh", [128, 4, 256])
            self.bcb2 = [sb("bcb%d" % i, [128, 2, 256], BF16) for i in range(2)]
            self.bcf = sb("bcf", [128, 256])
            self.xs_tok2 = [sb("xs_tok%d" % i, [128, 2, 256], BF16) for i in range(2)]
            self.xdt2 = [sb("xdt%d" % i, [128, 2, 256], BF16) for i in range(2)]
            self.xw2 = [sb("xw%d" % i, [128, 2, 256], BF16) for i in range(2)]
            self.btok2 = [sb("btok%d" % i, [128, 2, 128], BF16) for i in range(2)]
            self.cb32 = [sb("cb3_%d" % i, [128, 3, 128]) for i in range(2)]
            self.dtmp = [sb("dtmp%d" % i, [128, 3, 128]) for i in range(2)]
            self.mt = [sb("mt%d" % i, [128, 3, 128], BF16) for i in range(2)]
            self.prev_f = sb("prev_f", [128, 256])
            self.prev_b = sb("prev_b", [128, 256], BF16)
            self.tz = sb("tz", [128, 2, 256])
            self.yv = sb("yv", [128, 2, 256])
            self.ss = sb("ss", [128, 2])
            self.rstd = sb("rstd", [128, 2])
            self.nwb = [sb("nwb%d" % i, [128, 256]) for i in range(2)]
            self.xsf = self.xh[:, 0:2, :]
            self.zs = self.tz
            self.h2 = self.yv
            self.yn = self.yv
            self.junk = sb("junk", [128, 256], BF16)
        xk = self.xT_keys()
        S.dma('sp', [], ['smallb'], lambda e: e.dma_start(out=self.smallb[:], in_=self.small[l].broadcast_to([128, 96])))
        S.dma('sp', [], ['cwb'], lambda e: e.dma_start(out=self.cwb[:], in_=self.cw[l].rearrange("p (g c k) -> p g c k", g=8, c=4)))
        S.dma('pool', [], ['wdtb'], lambda e: e.dma_start(out=self.wdt_b[:], in_=self.wdt[l].rearrange("p (k n) -> p k n", k=8)))
        S.op('pool', ['cwb'], ['cwb'], lambda e: e.tensor_scalar(self.cwb[:], self.cwb[:], 0.5, None, ALU.mult))
        dtb = self.smallb[:, 0:32]
        alog = self.smallb[:, 32:64]
        dsk = self.smallb[:, 64:96]
        S.op('act', ['smallb'], ['a_b'], lambda e: e.activation(self.a_b[:], alog, AF.Exp))
        S.op('dve', ['a_b'], ['a_b'], lambda e: e.tensor_scalar(self.a_b[:], self.a_b[:], -1.0, None, ALU.mult))
        psd = self.PA[:, 0:512]
        self.mmseq(['wdtb'] + xk, [('PA', 0)],
                   [(psd[:, tt * 32:(tt + 1) * 32], [(self.xT[:, kc, tt * 128:(tt + 1) * 128], self.wdt_b[:, kc, :]) for kc in range(8)]) for tt in range(16)])
        dtf = self.dt_all[:].rearrange("p t h -> p (t h)")
        psd3 = psd.rearrange("p (t h) -> p t h", t=16)
        S.op('dve', [('PA', 0), 'smallb'], ['dt_all'], lambda e: e.tensor_tensor(self.dt_all[:], psd3, dtb.unsqueeze(1).broadcast_to([128, 16, 32]), ALU.add))
        S.op('act', ['dt_all'], ['dt_all'], lambda e: e.activation(dtf, dtf, AF.Exp))
        S.op('act', ['dt_all'], ['dt_all'], lambda e: e.activation(dtf, dtf, AF.Ln, bias=1.0))
        S.op('dve', ['dt_all', 'a_b'], ['adt_all'], lambda e: e.tensor_tensor(self.adt_all[:], self.dt_all[:], self.a_b[:].unsqueeze(1).broadcast_to([128, 16, 32]), ALU.mult))
        psa = self.PA[:, 512:1024]
        psl = self.PA[:, 1024:1280]
        groups = []
        for c in range(8):
            a0 = self.adt_all[:, 2 * c, :]
            a1 = self.adt_all[:, 2 * c + 1, :]
            groups.append((psa[:, (2 * c) * 32:(2 * c + 1) * 32], [(self.tri_f, a0)]))
            groups.append((psa[:, (2 * c + 1) * 32:(2 * c + 2) * 32], [(self.ones_f, a0), (self.tri_f, a1)]))
        self.mmseq(['adt_all', 'cst'], [('PA', 1)], groups)
        self.mmseq(['adt_all', 'cst'], [('PA', 2)],
                   [(psl[:, c * 32:(c + 1) * 32], [(self.ones_f, self.adt_all[:, 2 * c, :]), (self.ones_f, self.adt_all[:, 2 * c + 1, :])]) for c in range(8)])
        acsf = self.acs_all[:].rearrange("p t h -> p (t h)")
        S.op('dve', [('PA', 1)], ['acs_all'], lambda e: e.tensor_copy(acsf, psa))
        S.op('dve', [('PA', 2)], ['lastb'], lambda e: e.tensor_copy(self.lastb[:].rearrange("p c h -> p (c h)"), psl))
        S.op('dve', ['lastb', 'acs_all'], ['dst_all'], lambda e: e.tensor_tensor(
            self.dst_all[:].rearrange("p (c s) h -> p c s h", c=8), self.lastb[:].unsqueeze(2).broadcast_to([128, 8, 2, 32]),
            self.acs_all[:].rearrange("p (c s) h -> p c s h", c=8), ALU.subtract))
        dstf = self.dst_all[:].rearrange("p t h -> p (t h)")
        S.op('act', ['dst_all'], ['dst_all'], lambda e: e.activation(dstf, dstf, AF.Exp))
        S.op('dve', ['dst_all', 'dt_all'], ['dst_all'], lambda e: e.tensor_tensor(self.dst_all[:], self.dst_all[:], self.dt_all[:], ALU.mult))
        S.op('act', ['acs_all'], ['ea_all'], lambda e: e.activation(self.ea_all[:].rearrange("p t h -> p (t h)"), acsf, AF.Exp))
        S.op('act', ['lastb'], ['cdb'], lambda e: e.activation(self.cdb[:].rearrange("p c h -> p (c h)"), self.lastb[:].rearrange("p c h -> p (c h)"), AF.Exp))
        self.dump('dt_all', self.dt_all[:].rearrange("p t h -> p (t h)"), 'dt_all')
        self.dump('acs_all', acsf, 'acs_all')

        P0 = self.PA[:, 0:512]
        P1 = self.PA[:, 512:1024]
        P2 = self.PA[:, 1024:1536]
        P3 = self.PA[:, 1536:2048]
        P4 = self.PB[:, 0:512]
        P5 = self.PB[:, 512:1024]
        P6 = self.PC[:, 0:512]
        P7 = self.PD[:, 0:512]
        uprev = None
        def load_g(g):
            wb = self.wssm_b[g % 2]
            S.dma('pool', [], [('wssm', 0, 0)], lambda e, wb=wb, g=g: e.dma_start(
                out=wb[:], in_=self.wssm[l, g].rearrange("p (k n) -> p k n", k=8), max_dma_last_dim=4096))
            nw = self.nwb[g % 2]
            S.dma('sp', [], [('nwb', g % 2)], lambda e, nw=nw, g=g: e.dma_start(out=nw[:], in_=self.normw[l][:, g * 256:(g + 1) * 256].broadcast_to([128, 256])))
        load_g(0)
        for g in range(8):
            wb = self.wssm_b[g % 2]
            wk = ('wssm', 0, 0)
            wkall = [('wssm', 0, 0)]
            nw = self.nwb[g % 2]
            nk = ('nwb', g % 2)
            if g > 0:
                load_g(g)
            hs = slice(4 * g, 4 * g + 4)
            prevB = None
            for c in range(8):
                cols = slice(c * 256, (c + 1) * 256)
                sl = c % 2
                bcb, xs_tok, xw, btok = self.bcb2[sl], self.xs_tok2[sl], self.xw2[sl], self.btok2[sl]
                kbcb, kxs, kxw, kbt = ('bcb', sl), ('xs_tok', sl), ('xw', sl), ('btok', sl)
                xdt, cb3 = self.xdt2[sl], self.cb32[sl]
                kxdt, kcb3 = ('xdt', sl), ('cb3', sl)
                F1, F2, Bq = [], [], []
                self.rec = F1
                xkc = xk[2 * c:2 * c + 2]
                ue = self.uext[c % 2]
                uk = ('uext', c % 2)
                self.mmseq(wkall + xkc, [('PA', 0)],
                           [(P0[:, j * 256:(j + 1) * 256], [(wb[:, kc, 256 + j * 128:256 + (j + 1) * 128], self.xT[:, kc, cols]) for kc in range(8)]) for j in range(2)])
                self.mmseq(wkall + xkc, [('PA', 1)],
                           [(P1[:, j * 256:(j + 1) * 256], [(wb[:, kc, 512 + j * 128:512 + (j + 1) * 128], self.xT[:, kc, cols]) for kc in range(8)]) for j in range(2)])
                if c == 0:
                    S.op('pool', [], [uk], lambda e, bcb=bcb, xs_tok=xs_tok, xw=xw, btok=btok, xdt=xdt, cb3=cb3, ue=ue: e.memset(ue[:, :, 0:3], 0.0))
                else:
                    up = self.uext[(c - 1) % 2]
                    S.op('pool', [('uext', (c - 1) % 2)], [uk], lambda e, bcb=bcb, xs_tok=xs_tok, xw=xw, btok=btok, xdt=xdt, cb3=cb3, ue=ue, up=up: e.tensor_copy(ue[:, :, 0:3], up[:, :, 256:259]))
                S.op('act', [('PA', 0)], [uk], lambda e, bcb=bcb, xs_tok=xs_tok, xw=xw, btok=btok, xdt=xdt, cb3=cb3, ue=ue: e.activation(ue[:, 0:2, 3:259], P0.rearrange("p (j t) -> p j t", j=2), AF.Copy))
                S.op('act', [('PA', 1)], [uk], lambda e, bcb=bcb, xs_tok=xs_tok, xw=xw, btok=btok, xdt=xdt, cb3=cb3, ue=ue: e.activation(ue[:, 2:4, 3:259], P1.rearrange("p (j t) -> p j t", j=2), AF.Copy))
                for ci in range(4):
                    eng = 'dve'
                    w = self.cwb[:, g, ci, :]
                    o = self.xh[:, ci, :]
                    S.op(eng, [uk, 'cwb'], [('xh', ci)], lambda e, bcb=bcb, xs_tok=xs_tok, xw=xw, btok=btok, xdt=xdt, cb3=cb3, ci=ci, ue=ue, w=w, o=o: e.tensor_scalar(o, ue[:, ci, 0:256], w[:, 0:1], w[:, 4:5], ALU.mult, ALU.add))
                    for k in range(1, 4):
                        if eng == 'dve':
                            S.op('dve', [uk, 'cwb', ('xh', ci)], [('xh', ci)], lambda e, bcb=bcb, xs_tok=xs_tok, xw=xw, btok=btok, xdt=xdt, cb3=cb3, ci=ci, ue=ue, w=w, o=o, k=k: e.scalar_tensor_tensor(
                                o, in0=ue[:, ci, k:k + 256], scalar=w[:, k:k + 1], in1=o, op0=ALU.mult, op1=ALU.add))
                        else:
                            S.op('pool', [uk, 'cwb'], [('tnh', ci)], lambda e, bcb=bcb, xs_tok=xs_tok, xw=xw, btok=btok, xdt=xdt, cb3=cb3, ci=ci, ue=ue, w=w, k=k: e.tensor_scalar(self.tnh[:, ci, :], ue[:, ci, k:k + 256], w[:, k:k + 1], None, ALU.mult))
                            S.op('pool', [('tnh', ci), ('xh', ci)], [('xh', ci)], lambda e, bcb=bcb, xs_tok=xs_tok, xw=xw, btok=btok, xdt=xdt, cb3=cb3, ci=ci, o=o: e.tensor_tensor(o, o, self.tnh[:, ci, :], ALU.add))
                S.op('act', [('xh', ci) for ci in range(4)], [('tnh', ci) for ci in range(4)], lambda e, bcb=bcb, xs_tok=xs_tok, xw=xw, btok=btok, xdt=xdt, cb3=cb3: e.activation(self.tnh[:], self.xh[:], AF.Tanh))
                S.op('dve', [('tnh', 0), ('tnh', 1), ('xh', 0), ('xh', 1)], [('xh', 0), ('xh', 1)], lambda e, bcb=bcb, xs_tok=xs_tok, xw=xw, btok=btok, xdt=xdt, cb3=cb3: e.scalar_tensor_tensor(self.xsf[:], in0=self.tnh[:, 0:2, :], scalar=1.0, in1=self.xh[:, 0:2, :], op0=ALU.add, op1=ALU.mult))
                S.op('dve', [('tnh', 2), ('tnh', 3), ('xh', 2), ('xh', 3)], [kbcb], lambda e, bcb=bcb, xs_tok=xs_tok, xw=xw, btok=btok, xdt=xdt, cb3=cb3: e.scalar_tensor_tensor(bcb[:], in0=self.tnh[:, 2:4, :], scalar=1.0, in1=self.xh[:, 2:4, :], op0=ALU.add, op1=ALU.mult))
                S.op('dve', [('tnh', 2), ('xh', 2)], ['bcf'], lambda e, bcb=bcb, xs_tok=xs_tok, xw=xw, btok=btok, xdt=xdt, cb3=cb3: e.scalar_tensor_tensor(self.bcf[:], in0=self.tnh[:, 2, :], scalar=1.0, in1=self.xh[:, 2, :], op0=ALU.add, op1=ALU.mult))
                self.tps([('xh', 0), ('xh', 1), 'cst'], [('PA', 0)],
                         [(P0[:, si * 256 + j * 128:si * 256 + (j + 1) * 128], self.xsf[:, j, si * 128:(si + 1) * 128]) for si in range(2) for j in range(2)], self.ident_f)
                self.tps(['bcf', 'cst'], [('PA', 1)],
                         [(P1[:, si * 128:(si + 1) * 128], self.bcf[:, si * 128:(si + 1) * 128]) for si in range(2)], self.ident_f)
                P0v = P0.rearrange("p (s h d) -> p s h d", s=2, h=4)
                S.op('act', [('PA', 0)], [kxs], lambda e, bcb=bcb, xs_tok=xs_tok, xw=xw, btok=btok, xdt=xdt, cb3=cb3: e.activation(xs_tok[:].rearrange("p s c -> p (s c)"), P0, AF.Copy))
                S.op('dve', [('PA', 0), 'dt_all'], [kxdt], lambda e, bcb=bcb, xs_tok=xs_tok, xw=xw, btok=btok, xdt=xdt, cb3=cb3, c=c, hs=hs: e.tensor_tensor(
                    xdt[:].rearrange("p s (h d) -> p s h d", h=4), P0v,
                    self.dt_all[:, 2 * c:2 * c + 2, hs].unsqueeze(3).broadcast_to([128, 2, 4, 64]), ALU.mult))
                S.op('dve', [('PA', 0), 'dst_all'], [kxw], lambda e, bcb=bcb, xs_tok=xs_tok, xw=xw, btok=btok, xdt=xdt, cb3=cb3, c=c, hs=hs: e.tensor_tensor(
                    xw[:].rearrange("p s (h d) -> p s h d", h=4), P0v,
                    self.dst_all[:, 2 * c:2 * c + 2, hs].unsqueeze(3).broadcast_to([128, 2, 4, 64]), ALU.mult))
                S.op('act', [('PA', 1)], [kbt], lambda e, bcb=bcb, xs_tok=xs_tok, xw=xw, btok=btok, xdt=xdt, cb3=cb3: e.activation(btok[:].rearrange("p s n -> p (s n)"), P1[:, 0:256], AF.Copy))
                self.mmseq([kbcb], [('PA', 3)],
                           [(P3[:, 0:256], [(bcb[:, 0, 0:128], bcb[:, 1, :])]),
                            (P3[:, 384:512], [(bcb[:, 0, 128:256], bcb[:, 1, 128:256])])])
                S.op('dve', [('PA', 3), 'cst'], [kcb3], lambda e, bcb=bcb, xs_tok=xs_tok, xw=xw, btok=btok, xdt=xdt, cb3=cb3: e.tensor_tensor(cb3[:, 0, :], P3[:, 0:128], self.tri_f, ALU.mult))
                S.op('act', [('PA', 3)], [kcb3], lambda e, bcb=bcb, xs_tok=xs_tok, xw=xw, btok=btok, xdt=xdt, cb3=cb3: e.activation(cb3[:, 1, :], P3[:, 128:256], AF.Copy))
                S.op('dve', [('PA', 3), 'cst', kcb3], [kcb3], lambda e, bcb=bcb, xs_tok=xs_tok, xw=xw, btok=btok, xdt=xdt, cb3=cb3: e.tensor_tensor(cb3[:, 2, :], P3[:, 384:512], self.tri_f, ALU.mult))
                self.rec = F2
                def bc_mm(hh):
                    hg = 4 * g + hh
                    pb = (P4 if hh % 2 == 0 else P6)[:, 0:256]
                    pbk = ('PB', 0, 0) if hh % 2 == 0 else 'PC'
                    a0 = self.adt_all[:, 2 * c, hg:hg + 1].broadcast_to([128, 128])
                    a1 = self.adt_all[:, 2 * c + 1, hg:hg + 1].broadcast_to([128, 128])
                    self.mmseq(['adt_all', 'cst'], [pbk],
                               [(pb[:, 0:128], [(a0, self.tri_f)]),
                                (pb[:, 128:256], [(a0, self.ones_f), (a1, self.tri_f)])])
                    return pb, pbk
                nxt = bc_mm(0)
                for hh in range(4):
                    hg = 4 * g + hh
                    pb, pbk = nxt
                    if hh + 1 < 4:
                        nxt = bc_mm(hh + 1)
                    dtm = self.dtmp[hh % 2]
                    dk = ('dtmp', hh % 2)
                    mt = self.mt[hh % 2]
                    mk = ('mt', hh % 2)
                    ac0 = self.acs_all[:, 2 * c, hg:hg + 1]
                    ac1 = self.acs_all[:, 2 * c + 1, hg:hg + 1]

                    def dec(e, dtm=dtm, pb=pb, ac0=ac0, ac1=ac1):
                        e.tensor_scalar(dtm[:, 0, :], pb[:, 0:128], ac0, 0.0, ALU.subtract, ALU.min)
                        e.tensor_scalar(dtm[:, 1, :], pb[:, 128:256], ac0, 0.0, ALU.subtract, ALU.min)
                        return e.tensor_scalar(dtm[:, 2, :], pb[:, 128:256], ac1, 0.0, ALU.subtract, ALU.min)
                    S.op('dve', [pbk, 'acs_all'], [dk], dec)
                    S.op('act', [dk], [dk], lambda e, dtm=dtm: e.activation(dtm[:].rearrange("p a b -> p (a b)"), dtm[:].rearrange("p a b -> p (a b)"), AF.Exp))
                    S.op('dve', [dk, kcb3], [mk], lambda e, dtm=dtm, mt=mt, cb3=cb3: e.tensor_tensor(mt[:], dtm[:], cb3[:], ALU.mult))
                    self.mmseq([mk, kxdt], [('PB', 1, hh)],
                               [(P5[:, hh * 64:(hh + 1) * 64], [(mt[:, 0, :], xdt[:, 0, hh * 64:(hh + 1) * 64])]),
                                (P5[:, 256 + hh * 64:256 + (hh + 1) * 64], [(mt[:, 1, :], xdt[:, 0, hh * 64:(hh + 1) * 64]), (mt[:, 2, :], xdt[:, 1, hh * 64:(hh + 1) * 64])])])
                self.rec = Bq
                self.mmseq(wkall + xkc, [('PA', 2)],
                           [(P2[:, si * 256:(si + 1) * 256], [(self.xT[:, kc, c * 256 + si * 128:c * 256 + (si + 1) * 128], wb[:, kc, 0:256]) for kc in range(8)]) for si in range(2)])
                S.op('act', [('PA', 2)], ['tz'], lambda e, bcb=bcb, xs_tok=xs_tok, xw=xw, btok=btok, xdt=xdt, cb3=cb3: e.activation(self.tz[:].rearrange("p s c -> p (s c)"), P2, AF.Tanh, scale=0.5))
                S.op('dve', ['tz', ('PA', 2)], ['tz'], lambda e, bcb=bcb, xs_tok=xs_tok, xw=xw, btok=btok, xdt=xdt, cb3=cb3: e.scalar_tensor_tensor(self.zs[:].rearrange("p s c -> p (s c)"), in0=self.tz[:].rearrange("p s c -> p (s c)"), scalar=1.0, in1=P2, op0=ALU.add, op1=ALU.mult))
                p5k = [('PB', 1, hh) for hh in range(4)]
                if c > 0:
                    self.mmseq([kbcb, 'prev_b'], [('PA', 2)],
                               [(P2[:, si * 256:(si + 1) * 256], [(bcb[:, 1, si * 128:(si + 1) * 128], self.prev_b[:])]) for si in range(2)])
                if c < 7:
                    self.mm([kbt, kxw], [('PB', 0, 0)], P4[:, 0:256], [(btok[:, si, :], xw[:, si, :]) for si in range(2)])
                    if c == 0:
                        S.op('act', [('PB', 0, 0)], ['prev_f'], lambda e, bcb=bcb, xs_tok=xs_tok, xw=xw, btok=btok, xdt=xdt, cb3=cb3: e.activation(self.prev_f[:], P4[:, 0:256], AF.Copy))
                    else:
                        S.op('dve', ['prev_f', 'cdb'], ['prev_f'], lambda e, bcb=bcb, xs_tok=xs_tok, xw=xw, btok=btok, xdt=xdt, cb3=cb3, c=c, hs=hs: e.tensor_tensor(
                            self.prev_f[:].rearrange("p (h d) -> p h d", h=4), self.prev_f[:].rearrange("p (h d) -> p h d", h=4),
                            self.cdb[:, c, hs].unsqueeze(2).broadcast_to([128, 4, 64]), ALU.mult))
                        S.op('dve', ['prev_f', ('PB', 0, 0)], ['prev_f'], lambda e, bcb=bcb, xs_tok=xs_tok, xw=xw, btok=btok, xdt=xdt, cb3=cb3: e.tensor_tensor(self.prev_f[:], self.prev_f[:], P4[:, 0:256], ALU.add))
                    S.op('act', ['prev_f'], ['prev_b'], lambda e, bcb=bcb, xs_tok=xs_tok, xw=xw, btok=btok, xdt=xdt, cb3=cb3: e.activation(self.prev_b[:], self.prev_f[:], AF.Copy))
                S.op('pool', [kxs, 'smallb'], ['yv'], lambda e, bcb=bcb, xs_tok=xs_tok, xw=xw, btok=btok, xdt=xdt, cb3=cb3, hs=hs: e.tensor_tensor(
                    self.yv[:].rearrange("p s (h d) -> p s h d", h=4), xs_tok[:].rearrange("p s (h d) -> p s h d", h=4),
                    dsk[:, hs].unsqueeze(1).unsqueeze(3).broadcast_to([128, 2, 4, 64]), ALU.mult))
                S.op('dve', p5k + ['yv'], ['yv'], lambda e, bcb=bcb, xs_tok=xs_tok, xw=xw, btok=btok, xdt=xdt, cb3=cb3: e.tensor_tensor(self.yv[:].rearrange("p s c -> p (s c)"), P5, self.yv[:].rearrange("p s c -> p (s c)"), ALU.add))
                if c > 0:
                    def yoff(e, c=c, g=g):
                        ins = None
                        for si in range(2):
                            for hh in range(4):
                                o = self.yv[:, si, hh * 64:(hh + 1) * 64]
                                ins = e.scalar_tensor_tensor(o, in0=P2[:, si * 256 + hh * 64:si * 256 + (hh + 1) * 64],
                                                             scalar=self.ea_all[:, 2 * c + si, 4 * g + hh:4 * g + hh + 1], in1=o, op0=ALU.mult, op1=ALU.add)
                        return ins
                    S.op('dve', [('PA', 2), 'ea_all', 'yv'], ['yv'], yoff)
                S.op('dve', ['yv', 'tz'], ['yv'], lambda e, bcb=bcb, xs_tok=xs_tok, xw=xw, btok=btok, xdt=xdt, cb3=cb3: e.tensor_tensor(self.h2[:], self.yv[:], self.zs[:], ALU.mult))
                for si in range(2):
                    S.op('act', ['yv'], ['junk', ('ss', si)], lambda e, bcb=bcb, xs_tok=xs_tok, xw=xw, btok=btok, xdt=xdt, cb3=cb3, si=si: e.activation(self.junk[:], self.h2[:, si, :], AF.Square, accum_out=self.ss[:, si:si + 1]))
                S.op('dve', [('ss', 0), ('ss', 1)], ['rstd'], lambda e, bcb=bcb, xs_tok=xs_tok, xw=xw, btok=btok, xdt=xdt, cb3=cb3: e.tensor_scalar(self.rstd[:], self.ss[:], 1.0 / 256, 4 * RMS_EPS, ALU.mult, ALU.add))
                S.op('pool', ['rstd', 'nh'], ['rstd'], lambda e, bcb=bcb, xs_tok=xs_tok, xw=xw, btok=btok, xdt=xdt, cb3=cb3: e.tensor_tensor(self.rstd[:], self.rstd[:], self.nh[:, 0:2], ALU.pow))
                for si in range(2):
                    S.op('dve', ['yv', 'rstd', nk], ['yv'], lambda e, bcb=bcb, xs_tok=xs_tok, xw=xw, btok=btok, xdt=xdt, cb3=cb3, si=si, nw=nw: e.scalar_tensor_tensor(
                        self.yn[:, si, :], in0=self.h2[:, si, :], scalar=self.rstd[:, si:si + 1], in1=nw[:], op0=ALU.mult, op1=ALU.mult))
                self.tps(['yv', 'cst'], ['PD'],
                         [(P7[:, j * 256 + si * 128:j * 256 + (si + 1) * 128], self.yn[:, si, j * 128:(j + 1) * 128]) for j in range(2) for si in range(2)], self.ident_f)
                S.op('act', ['PD'], [('yT', g, c)], lambda e, bcb=bcb, xs_tok=xs_tok, xw=xw, btok=btok, xdt=xdt, cb3=cb3, g=g, cols=cols: e.activation(self.yT[:, 2 * g:2 * g + 2, cols], P7.rearrange("p (j t) -> p j t", j=2), AF.Copy))
                self.rec = None
                X = F2 + Bq
                if prevB is None:
                    for t in F1:
                        t()
                else:
                    n1, n2 = len(F1), len(prevB)
                    i1 = i2 = 0
                    while i1 < n1 or i2 < n2:
                        if i2 >= n2 or (i1 < n1 and i1 * n2 <= i2 * n1):
                            F1[i1]()
                            i1 += 1
                        else:
                            prevB[i2]()
                            i2 += 1
                prevB = X
            for t in prevB:
                t()
        if 'yT' in self.dbg:
            if True:
                self.dbgf = self.sb("dbgf", [128, 512])
            for kc in range(16):
                for q4 in range(4):
                    S.op('dve', [('yT', g, c) for g in range(8) for c in range(8)] + [('dbg', 'yT')], ['dbgf'], lambda e, kc=kc, q4=q4: e.tensor_copy(self.dbgf[:, 0:512], self.yT[:, kc, q4 * 512:(q4 + 1) * 512]))
                    S.dma('sp', ['dbgf'], [('dbg', 'yT')], lambda e, kc=kc, q4=q4: e.dma_start(out=self.dbg['yT'][kc * 128:(kc + 1) * 128, q4 * 512:(q4 + 1) * 512], in_=self.dbgf[:, 0:512]))
            self.final_keys.append(('dbg', 'yT'))

    def merge(self, l):
        self.psum_epoch()
        S, sb = self.S, self.sb
        self.phase([self.R_W])
        if True:
            self.wc1_b = [sb("wc1b%d" % i, [128, 5120], BF16) for i in range(2)]
            self.ta = [sb("ta%d" % i, [128, 512]) for i in range(2)]
            self.tb = [sb("tb%d" % i, [128, 512]) for i in range(2)]
        mT = self.B3
        xk = self.xT_keys()
        attk = [('attT', hp, h) for hp in range(8) for h in range(2)]
        yk = [('yT', g, c) for g in range(8) for c in range(8)]
        it = 0
        def load_c(fc):
            wb = self.wc1_b[fc % 2]
            S.dma('pool', [], [('wc1', fc % 2, 0)], lambda e, wb=wb, fc=fc: e.dma_start(
                out=wb[:].rearrange("p (a b) -> p a b", a=5), in_=self.wc1[l, fc].rearrange("p (a b) -> p a b", a=5), max_dma_last_dim=4096))
        load_c(0)
        for fc in range(8):
            wb = self.wc1_b[fc % 2]
            wks = [('wc1', fc % 2, 0)]
            if fc + 1 < 8:
                load_c(fc + 1)
            wg = wb[:, 0:2048].rearrange("p (k n) -> p k n", k=8)
            wa = wb[:, 2048:3072].rearrange("p (k n) -> p k n", k=8)
            ws = wb[:, 3072:5120].rearrange("p (k n) -> p k n", k=16)
            for tc in range(4):
                cols = slice(tc * 512, (tc + 1) * 512)
                if it % 2 == 0:
                    pga, pgs, ppa, pps = (self.PA[:, i * 512:(i + 1) * 512] for i in range(4))
                    bk = [('PA', 0), ('PA', 1), ('PA', 2), ('PA', 3)]
                else:
                    pga, pgs, ppa, pps = self.PB[:, 0:512], self.PB[:, 512:1024], self.PC[:, 0:512], self.PD[:, 0:512]
                    bk = [('PB', 0), ('PB', 1), 'PC', 'PD']
                self.mm(wks + xk[tc * 4:(tc + 1) * 4], [bk[0]], pga, [(wg[:, kc, 0:128], self.xT[:, kc, cols]) for kc in range(8)])
                self.mm(wks + xk[tc * 4:(tc + 1) * 4], [bk[1]], pgs, [(wg[:, kc, 128:256], self.xT[:, kc, cols]) for kc in range(8)])
                self.mm(wks + attk, [bk[2]], ppa, [(wa[:, kc, :], self.B1[:, kc, cols]) for kc in range(8)])
                self.mm(wks + yk, [bk[3]], pps, [(ws[:, kc, :], self.yT[:, kc, cols]) for kc in range(16)])
                ta, tb = self.ta[it % 2], self.tb[it % 2]
                tak, tbk = ('ta', it % 2), ('tb', it % 2)
                it += 1
                S.op('act', [bk[0]], [tak], lambda e, ta=ta, pga=pga: e.activation(ta[:], pga, AF.Tanh, scale=0.5))
                S.op('act', [bk[1]], [tbk], lambda e, tb=tb, pgs=pgs: e.activation(tb[:], pgs, AF.Tanh, scale=0.5))
                S.op('dve', [tak, bk[2]], [tak], lambda e, ta=ta, ppa=ppa: e.scalar_tensor_tensor(ta[:], in0=ta[:], scalar=1.0, in1=ppa, op0=ALU.add, op1=ALU.mult))
                S.op('dve', [tbk, bk[3]], [tbk], lambda e, tb=tb, pps=pps: e.scalar_tensor_tensor(tb[:], in0=tb[:], scalar=1.0, in1=pps, op0=ALU.add, op1=ALU.mult))
                S.op('pool', [tak, tbk], [('mergedT', fc, tc)], lambda e, ta=ta, tb=tb, fc=fc, cols=cols: e.tensor_tensor(mT[:, fc, cols], ta[:], tb[:], ALU.add))

    def layernorm(self, t, tk, g_b, b_b, gk, i):
        S = self.S
        st = self.ln_st[i % 2]
        mv = self.ln_mv[i % 2]
        sk = ('lnst', i % 2)

        def stats(e):
            e.bn_stats(st[:, 0, :], t[:, 0:512])
            return e.bn_stats(st[:, 1, :], t[:, 512:1024])
        S.op('dve', [tk], [sk], stats)
        S.op('dve', [sk], [sk], lambda e: e.bn_aggr(mv[:, 0:2], st[:].rearrange("p a b -> p (a b)")))
        S.op('dve', [sk], [sk], lambda e: e.tensor_scalar(mv[:, 2:3], mv[:, 1:2], LN_EPS, None, ALU.add))
        S.op('pool', [sk, 'nh'], [sk], lambda e: e.tensor_tensor(mv[:, 3:4], mv[:, 2:3], self.nh[:, 0:1], ALU.pow))
        S.op('dve', [tk, sk], [tk], lambda e: e.tensor_scalar(t[:], t[:], mv[:, 0:1], mv[:, 3:4], ALU.subtract, ALU.mult))
        S.op('pool', [tk, gk], [tk], lambda e: e.tensor_tensor(t[:], t[:], g_b, ALU.mult))
        S.op('dve', [tk, gk], [tk], lambda e: e.tensor_tensor(t[:], t[:], b_b, ALU.add))

    def mix_ln1(self, l):
        self.psum_epoch()
        S, sb = self.S, self.sb
        self.phase([self.R_W, self.R_B2])
        if True:
            self.wout_b = sb("woutb", [128, 8, 1024], BF16)
            self.lnpb = sb("lnpb", [128, 4096])
            self.res = [sb("res%d" % i, [128, D]) for i in range(2)]
            self.ln_st = [sb("lnst%d" % i, [128, 2, 6]) for i in range(2)]
            self.ln_mv = [sb("lnmv%d" % i, [128, 4]) for i in range(2)]
        S.dma('pool', [], [('wout', 0)], lambda e: e.dma_start(
            out=self.wout_b[:], in_=self.wout[l].rearrange("p (k n) -> p k n", k=8), max_dma_last_dim=4096))
        lnpb1 = self.lnpb
        S.dma('sp', [], ['lnpb'], lambda e: e.dma_start(out=lnpb1[:], in_=self.lnp[l].broadcast_to([128, 4096])))
        src = self.x_in if l == 0 else self.x1
        srck = 'x_in' if l == 0 else 'x1'
        mk = [('mergedT', fc, tc) for fc in range(8) for tc in range(4)]
        wk = [('wout', 0)]
        for tt in range(16):
            r = self.res[tt % 2]
            rk = ('res', tt % 2)
            S.dma('sp', [(srck, tt)], [rk], lambda e, r=r, tt=tt: e.dma_start(out=r[:], in_=src[tt * 128:(tt + 1) * 128, :]))
            S.op('act', [rk], [rk], lambda e, r=r: e.activation(r[:], r[:], AF.Copy, scale=float(ALPHA)))
            if tt % 2 == 0:
                pm = self.PB[:, 0:1024]
                rkeys = [('PB', 0), ('PB', 1)]
            else:
                pm = self.PA[:, 0:1024]
                rkeys = [('PA', 0), ('PA', 1)]
            self.mmseq(mk + wk, rkeys,
                       [(pm[:, half * 512:(half + 1) * 512], [(self.B3[:, kc, tt * 128:(tt + 1) * 128], self.wout_b[:, kc, half * 512:(half + 1) * 512]) for kc in range(8)]) for half in range(2)])
            S.op('dve', rkeys + [rk], [rk], lambda e, r=r, pm=pm: e.scalar_tensor_tensor(r[:], in0=pm, scalar=0.5, in1=r[:], op0=ALU.mult, op1=ALU.add))
            self.layernorm(r, rk, self.lnpb[:, 0:1024], self.lnpb[:, 1024:2048], 'lnpb', tt)
            S.dma('sp', [rk], [('hres', tt)], lambda e, r=r, tt=tt: e.dma_start(out=self.hres[tt * 128:(tt + 1) * 128, :], in_=r[:]))
            self.to_featmajor2(r, rk, self.B1, 'hT', tt)
        if 'hT' in self.dbg:
            if True:
                self.dbgf = self.sb("dbgf", [128, 512])
            for kc in range(8):
                for q4 in range(4):
                    S.op('dve', [('hT', tt) for tt in range(16)] + [('dbg', 'hT')], ['dbgf'], lambda e, kc=kc, q4=q4: e.tensor_copy(self.dbgf[:, 0:512], self.B1[:, kc, q4 * 512:(q4 + 1) * 512]))
                    S.dma('sp', ['dbgf'], [('dbg', 'hT')], lambda e, kc=kc, q4=q4: e.dma_start(out=self.dbg['hT'][kc * 128:(kc + 1) * 128, q4 * 512:(q4 + 1) * 512], in_=self.dbgf[:, 0:512]))
            self.final_keys.append(('dbg', 'hT'))

    def to_featmajor2(self, tile, tkey, dstT, dkey, tt):
        S = self.S
        for half in range(2):
            ps = self.PC[:, 0:512] if half == 0 else self.PD[:, 0:512]
            pk = 'PC' if half == 0 else 'PD'
            self.tps([tkey, 'cst'], [pk], [(ps[:, j * 128:(j + 1) * 128], tile[:, (half * 4 + j) * 128:(half * 4 + j + 1) * 128]) for j in range(4)], self.ident_f)
            S.op('act', [pk], [(dkey, tt)], lambda e, ps=ps, half=half: e.activation(
                dstT[:, half * 4:(half + 1) * 4, tt * 128:(tt + 1) * 128], ps.rearrange("p (c t) -> p c t", c=4), AF.Copy))

    def ffn(self, l):
        self.psum_epoch()
        S, sb = self.S, self.sb
        self.phase([self.R_W, (self.R_B3[0] + 4096, self.R_B3[1])])
        if True:
            self.wup_b = [sb("wupb%d" % i, [128, 8, 512], BF16) for i in range(2)]
            self.wdn_b = [sb("wdnb%d" % i, [128, 4, 1024], BF16) for i in range(2)]
            self.rl = [sb("rl%d" % i, [128, 512]) for i in range(2)]
            self.ln_st = [sb("lnst%d" % i, [128, 2, 6]) for i in range(2)]
            self.ln_mv = [sb("lnmv%d" % i, [128, 4]) for i in range(2)]
        uT = self.B3
        hk = [('hT', tt) for tt in range(16)]
        it = 0
        def load_f(gi):
            wu = self.wup_b[gi % 2]
            wd = self.wdn_b[gi % 2]
            wuk, wdk = ('wup', gi % 2), ('wdn', gi % 2)
            S.dma('pool', [], [(wuk, 0)], lambda e, wu=wu, gi=gi: e.dma_start(
                out=wu[:], in_=self.wup[l, gi].rearrange("p (k n) -> p k n", k=8), max_dma_last_dim=4096))
            S.dma('pool', [], [(wdk, 0)], lambda e, wd=wd, gi=gi: e.dma_start(
                out=wd[:], in_=self.wdn[l, gi].rearrange("p (k n) -> p k n", k=4), max_dma_last_dim=4096))
        load_f(0)
        for gi in range(8):
            wu = self.wup_b[gi % 2]
            wd = self.wdn_b[gi % 2]
            wuk, wdk = ('wup', gi % 2), ('wdn', gi % 2)
            if gi + 1 < 8:
                load_f(gi + 1)
            wuks = [(wuk, 0)]
            wdks = [(wdk, 0)]
            for j in range(4):
                for tc in range(4):
                    bank = it % 4
                    ps = [self.PA[:, 0:512], self.PA[:, 512:1024], self.PC[:, 0:512], self.PD[:, 0:512]][bank]
                    pk = [('PA', 0), ('PA', 1), 'PC', 'PD'][bank]
                    cols = slice(tc * 512, (tc + 1) * 512)
                    self.mm(wuks + hk[tc * 4:(tc + 1) * 4], [pk], ps, [(wu[:, kc, j * 128:(j + 1) * 128], self.B1[:, kc, cols]) for kc in range(8)])
                    rl = self.rl[it % 2]
                    rlk = ('rl', it % 2)
                    S.op('act', [pk], [rlk], lambda e, rl=rl, ps=ps: e.activation(rl[:], ps, AF.Relu))
                    eng = 'dve'
                    S.op(eng, [rlk], [('uT', j)], lambda e, rl=rl, j=j, cols=cols: e.tensor_tensor(uT[:, j, cols], rl[:], rl[:], ALU.mult))
                    it += 1
            for tt in range(16):
                if tt % 2 == 0:
                    pm = self.PB[:, 0:1024]
                    rkeys = [('PB', 0), ('PB', 1)]
                else:
                    pm = self.PA[:, 1024:2048]
                    rkeys = [('PA', 2), ('PA', 3)]
                self.mmseq([('uT', j) for j in range(4)] + wdks, rkeys,
                           [(pm[:, half * 512:(half + 1) * 512], [(uT[:, j, tt * 128:(tt + 1) * 128], wd[:, j, half * 512:(half + 1) * 512]) for j in range(4)]) for half in range(2)])
                if gi == 0:
                    S.op('act', rkeys, [('acc', tt)], lambda e, tt=tt, pm=pm: e.activation(self.acc[:, tt, :], pm, AF.Copy))
                else:
                    S.op('dve', rkeys + [('acc', tt)], [('acc', tt)], lambda e, tt=tt, pm=pm: e.tensor_tensor(self.acc[:, tt, :], pm, self.acc[:, tt, :], ALU.add))
        S.barrier()
        self.res = [self.wup_b[i][:, 0:4, :].rearrange("p a b -> p (a b)").bitcast(F32) for i in range(2)]
        self.lnpb = self.wdn_b[0][:].rearrange("p a b -> p (a b)").bitcast(F32)
        lnpb2 = self.lnpb
        S.dma('sp', [], ['lnpb'], lambda e: e.dma_start(out=lnpb2, in_=self.lnp[l][:, 2048:4096].broadcast_to([128, 2048])))
        dst = self.out if l == DEPTH - 1 else self.x1
        dstk = 'out' if l == DEPTH - 1 else 'x1'
        for tt in range(16):
            r = self.res[tt % 2]
            rk = ('res', tt % 2)
            S.dma('sp', [('hres', tt)], [rk], lambda e, r=r, tt=tt: e.dma_start(out=r[:], in_=self.hres[tt * 128:(tt + 1) * 128, :]))
            S.op('dve', [rk, ('acc', tt)], [rk], lambda e, r=r, tt=tt: e.scalar_tensor_tensor(r[:], in0=r[:], scalar=float(ALPHA), in1=self.acc[:, tt, :], op0=ALU.mult, op1=ALU.add))
            self.layernorm(r, rk, self.lnpb[:, 0:1024], self.lnpb[:, 1024:2048], 'lnpb', tt)
            S.dma('sp', [rk], [(dstk, tt)], lambda e, r=r, tt=tt: e.dma_start(out=dst[tt * 128:(tt + 1) * 128, :], in_=r[:]))
            if l == DEPTH - 1:
                self.final_keys.append((dstk, tt))
            else:
                self.to_featmajor2(r, rk, self.xT, 'xT', tt)


def sel_max(e, b):
    ins = None
    for t in range(8):
        ins = e.max(b.m8[:, t, :], b.gm[:, t * 8:(t + 1) * 8])
    return ins


def sel_lt(e, b):
    ins = None
    for t in range(8):
        ins = e.tensor_scalar(b.lt[:, t * 8:(t + 1) * 8], b.gm[:, t * 8:(t + 1) * 8], b.m8[:, t, 2:3], NEG, ALU.is_lt, ALU.mult)
    return ins


_CACHE = {}


def kernel(**inputs):
    inp = {k: np.asarray(v) for k, v in inputs.items()}
    w = prep_weights(inp)
    consts, e8 = make_consts()
    x = np.ascontiguousarray(inp['x'], dtype=np.float32)
    pos = np.ascontiguousarray(inp['positions'], dtype=np.int32)
    nc = Builder().build()
    in_maps = []
    for b in range(NCORES):
        m = {"x": x[b], "pos": pos[b:b + 1], "consts": consts, "e8": e8}
        m.update(w)
        in_maps.append(m)
    res = run_bass_kernel_spmd(nc, in_maps, core_ids=list(range(NCORES)))
    out = np.stack([np.asarray(r["out"], dtype=np.float32) for r in res.results], axis=0)
    return out
```

```python
import math
import numpy as np
import concourse.bass as bass
import concourse.mybir as mybir
from concourse.bass_utils import run_bass_kernel_spmd

F32 = mybir.dt.float32
BF16 = mybir.dt.bfloat16
I32 = mybir.dt.int32
AF = mybir.ActivationFunctionType
ALU = mybir.AluOpType
AX = mybir.AxisListType

D = 1024
SEQ = 2048
DEPTH = 2
NCORES = 8
IN_W = 11296
ALPHA = (2 * DEPTH) ** 0.25
LN_EPS = 1e-5
RMS_EPS = 1e-5
NEG = -1.0e5
PI = math.pi

DEBUG = {}
STOP_AFTER = None
NO_BARRIER = False
VARIANT = 0


class Sched:
    ENG = ('pe', 'act', 'dve', 'pool', 'sp')

    def __init__(self, nc, n_dma_sems=24):
        self.nc = nc
        self.sem = {e: nc.alloc_semaphore('s_' + e) for e in self.ENG}
        self.cnt = {e: 0 for e in self.ENG}
        self.dsem = [nc.alloc_semaphore('d%d' % i) for i in range(n_dma_sems)]
        self.dcnt = [0] * n_dma_sems
        self.drr = {'sp': 0, 'pool': 0}
        self.dpool = {'sp': list(range(0, n_dma_sems // 2)), 'pool': list(range(n_dma_sems // 2, n_dma_sems))}
        self.waited = {}
        self.last_w = {}
        self.readers = {}
        self.q = {e: [] for e in self.ENG}

    def barrier(self):
        if NO_BARRIER is True:
            return
        self.epoch = {}
        scr = self.bar_scratch
        d_sp = {('dma', i): self.dcnt[i] for i in self.dpool['sp'] if self.dcnt[i] > 0}
        d_pl = {('dma', i): self.dcnt[i] for i in self.dpool['pool'] if self.dcnt[i] > 0}
        self._token('act', d_sp, lambda e: e.memzero(scr[:, 0:1]))
        t1 = ('act', self.cnt['act'])
        d_pl[t1[0]] = t1[1]
        self._token('pool', d_pl, lambda e: e.memset(scr[:, 1:2], 0.0))
        d3 = {e: c for e, c in self.cnt.items() if c > 0}
        self._token('dve', d3, lambda e: e.memset(scr[:, 2:3], 0.0))
        self.epoch = {'dve': self.cnt['dve']}

    def _token(self, e, deps, emit):
        waits = self._waits(e, deps)
        self.cnt[e] += 1
        sem = self.sem[e]

        def thunk(eng):
            for s, v in waits:
                eng.wait_ge(s, v)
            emit(eng).then_inc(sem, 1)
        self.q[e].append(thunk)

    def _deps(self, reads, writes):
        deps = dict(getattr(self, 'epoch', {}))

        def add(src, idx):
            if deps.get(src, 0) < idx:
                deps[src] = idx
        for k in reads:
            lw = self.last_w.get(k)
            if lw is not None:
                add(*lw)
        for k in writes:
            lw = self.last_w.get(k)
            if lw is not None:
                add(*lw)
            for src, idx in self.readers.get(k, {}).items():
                add(src, idx)
        return deps

    def _waits(self, e, deps):
        out = []
        self.maxw = getattr(self, 'maxw', {})
        for src, idx in deps.items():
            if self.waited.get((e, src), 0) >= idx:
                continue
            self.waited[(e, src)] = idx
            if isinstance(src, tuple):
                out.append((self.dsem[src[1]], 16 * idx))
            else:
                out.append((self.sem[src], idx))
        self.maxw[len(out)] = self.maxw.get(len(out), 0) + 1
        return out

    def _record(self, token, reads, writes):
        src, idx = token
        for k in reads:
            r = self.readers.setdefault(k, {})
            if r.get(src, 0) < idx:
                r[src] = idx
        for k in writes:
            self.last_w[k] = (src, idx)
            self.readers[k] = {}

    @staticmethod
    def _bank(k):
        if isinstance(k, tuple) and k and k[0] == 'PA':
            return ('BANK', k[1])
        if isinstance(k, tuple) and k and k[0] == 'PB':
            return ('BANK', 4 + k[1])
        if k in ('PC', 'PCg', 'PCo'):
            return ('BANK', 6)
        if k == 'PD':
            return ('BANK', 7)
        return None

    def _canon(self, reads, writes):
        r2, w2 = [], []
        for k in reads:
            b = self._bank(k)
            if b is None:
                r2.append(k)
            elif b not in w2:
                w2.append(b)
        for k in writes:
            b = self._bank(k)
            if b is None:
                w2.append(k)
            elif b not in w2:
                w2.append(b)
        return r2, w2

    def op(self, e, reads, writes, emit):
        reads, writes = self._canon(reads, writes)
        deps = self._deps(reads, writes)
        waits = self._waits(e, deps)
        self.cnt[e] += 1
        idx = self.cnt[e]
        sem = self.sem[e]

        def thunk(eng):
            for s, v in waits:
                eng.wait_ge(s, v)
            emit(eng).then_inc(sem, 1)
        self.q[e].append(thunk)
        self._record((e, idx), reads, writes)

    def dma(self, e, reads, writes, emit):
        pool = self.dpool[e]
        s = pool[self.drr[e] % len(pool)]
        self.drr[e] += 1
        src = ('dma', s)
        deps = self._deps(reads, writes)
        if self.dcnt[s] > 0 and deps.get(src, 0) < self.dcnt[s]:
            deps[src] = self.dcnt[s]
        waits = self._waits(e, deps)
        self.dcnt[s] += 1
        idx = self.dcnt[s]
        sem = self.dsem[s]

        def thunk(eng):
            for sm, v in waits:
                eng.wait_ge(sm, v)
            emit(eng).then_inc(sem, 16)
        self.q[e].append(thunk)
        self._record((src, idx), reads, writes)

    def alias(self, old_keys, new_keys):
        acc = {}
        for k in old_keys:
            lw = self.last_w.get(k)
            if lw is not None and acc.get(lw[0], 0) < lw[1]:
                acc[lw[0]] = lw[1]
            for src, idx in self.readers.get(k, {}).items():
                if acc.get(src, 0) < idx:
                    acc[src] = idx
        for k in new_keys:
            r = self.readers.setdefault(k, {})
            for src, idx in acc.items():
                if r.get(src, 0) < idx:
                    r[src] = idx

    def finish(self, final_keys):
        self.barrier()
        deps = self._deps(final_keys, [])
        waits = self._waits('sp', deps)

        def thunk(eng):
            for s, v in waits:
                eng.wait_ge(s, v)
        self.q['sp'].append(thunk)
        nc = self.nc
        q = self.q
        with nc.Block() as block:
            @block.sync
            def _(eng):
                for t in q['sp']:
                    t(eng)

            @block.tensor
            def _(eng):
                for t in q['pe']:
                    t(eng)

            @block.scalar
            def _(eng):
                for t in q['act']:
                    t(eng)

            @block.vector
            def _(eng):
                for t in q['dve']:
                    t(eng)

            @block.gpsimd
            def _(eng):
                for t in q['pool']:
                    t(eng)


C_ID, C_TRI, C_ONES, C_RM, C_INVF, C_SGN, C_NEGP, C_FLOOR, NC_CONST = 0, 128, 256, 384, 512, 513, 514, 578, 648


def make_consts():
    c = np.zeros((128, NC_CONST), np.float32)
    c[:, C_ID:C_ID + 128] = np.eye(128, dtype=np.float32)
    c[:, C_TRI:C_TRI + 128] = np.triu(np.ones((128, 128), np.float32))
    c[:, C_ONES:C_ONES + 128] = 1.0
    rm = np.zeros((128, 128), np.float32)
    inv = (500000.0 ** (-np.arange(0, 16, 2, dtype=np.float32) / 16)).astype(np.float32)
    for blk in range(2):
        for d in range(16):
            src = d + 8 if d < 8 else d - 8
            rm[blk * 64 + src, blk * 64 + d] = 1.0
    c[:, C_RM:C_RM + 128] = rm
    for p in range(128):
        d = p % 64
        if d < 16:
            c[p, C_INVF] = inv[d % 8]
            c[p, C_SGN] = -1.0 if d < 8 else 1.0
    for qt in range(8):
        blk = (8 + qt) // 2
        for j in range(8):
            c[:, C_NEGP + qt * 8 + j] = 0.0 if j < blk else -1.0e30
            c[:, C_FLOOR + qt * 8 + j] = NEG if j < blk else 0.0
    e8 = np.zeros((8, SEQ), np.float32)
    for j in range(8):
        e8[j, j * 256:(j + 1) * 256] = 1.0
    return c, e8


def prep_weights(inp):
    out = {}
    L = DEPTH

    def kmaj(w):
        K, N = w.shape
        return w.reshape(K // 128, 128, N).transpose(1, 0, 2)

    w_in = inp['w_in']
    wqkv = np.empty((L, 8, 128, 8, 384), np.float32)
    wssm = np.empty((L, 8, 128, 8, 768), np.float32)
    wdt = np.empty((L, 128, 8, 32), np.float32)
    wc1 = np.empty((L, 8, 128, 5120), np.float32)
    wout = np.empty((L, 128, 8, 1024), np.float32)
    wup = np.empty((L, 8, 128, 8, 512), np.float32)
    wdn = np.empty((L, 8, 128, 4, 1024), np.float32)
    cw = np.empty((L, 128, 8, 4, 5), np.float32)
    for l in range(L):
        wi = kmaj(w_in[l])
        for hp in range(8):
            wqkv[l, hp, :, :, 0:128] = wi[:, :, hp * 128:(hp + 1) * 128]
            wqkv[l, hp, :, :, 128:256] = wi[:, :, 1024 + hp * 128:1024 + (hp + 1) * 128]
            wqkv[l, hp, :, :, 256:384] = wi[:, :, 2048 + hp * 128:2048 + (hp + 1) * 128]
        for g in range(8):
            wssm[l, g, :, :, 0:256] = wi[:, :, 3072 + g * 256:3072 + (g + 1) * 256]
            wssm[l, g, :, :, 256:512] = wi[:, :, 5120 + g * 256:5120 + (g + 1) * 256]
            wssm[l, g, :, :, 512:640] = wi[:, :, 7168 + g * 128:7168 + (g + 1) * 128]
            wssm[l, g, :, :, 640:768] = wi[:, :, 8192 + g * 128:8192 + (g + 1) * 128]
        wdt[l] = wi[:, :, 9216:9248]
        wap = kmaj(inp['w_attn_proj'][l])
        wsp = kmaj(inp['w_ssm_proj'][l])
        for fc in range(8):
            blk = np.empty((128, 8, 256), np.float32)
            blk[:, :, 0:128] = wi[:, :, 9248 + fc * 128:9248 + (fc + 1) * 128]
            blk[:, :, 128:256] = wi[:, :, 10272 + fc * 128:10272 + (fc + 1) * 128]
            wc1[l, fc, :, 0:2048] = blk.reshape(128, 2048)
            wc1[l, fc, :, 2048:3072] = wap[:, :, fc * 128:(fc + 1) * 128].reshape(128, 1024)
            wc1[l, fc, :, 3072:5120] = wsp[:, :, fc * 128:(fc + 1) * 128].reshape(128, 2048)
        wout[l] = kmaj(inp['w_out'][l])
        wu = kmaj(inp['w_up'][l])
        wd = kmaj(inp['w_down'][l])
        for gI in range(8):
            wup[l, gI] = wu[:, :, gI * 512:(gI + 1) * 512]
            wdn[l, gI] = wd[:, gI * 4:(gI + 1) * 4, :]
        cwl = inp['conv_w'][l]
        cbl = inp['conv_b'][l]
        for g in range(8):
            offs = [g * 256, g * 256 + 128, 2048 + g * 128, 3072 + g * 128]
            for ci, o in enumerate(offs):
                cw[l, :, g, ci, 0:4] = cwl[:, o:o + 128].T
                cw[l, :, g, ci, 4] = cbl[o:o + 128]
    out['wqkv'] = wqkv.reshape(L, 8, 128, 3072)
    out['wssm'] = wssm.reshape(L, 8, 128, 6144)
    out['wdt'] = wdt.reshape(L, 128, 256)
    out['wc1'] = wc1
    out['wout'] = wout.reshape(L, 128, 8192)
    out['wup'] = wup.reshape(L, 8, 128, 4096)
    out['wdn'] = wdn.reshape(L, 8, 128, 4096)
    out['cw'] = cw.reshape(L, 128, 160)
    small = np.concatenate([inp['dt_bias'], inp['a_log'], inp['d_skip']], axis=1)
    out['small'] = np.ascontiguousarray(small.reshape(L, 1, 96))
    out['normw'] = np.ascontiguousarray(inp['ssm_norm_w'].reshape(L, 1, 2048))
    lnp = np.stack([inp['ln1_g'], inp['ln1_b'], inp['ln2_g'], inp['ln2_b']], axis=1)
    out['lnp'] = np.ascontiguousarray(lnp.reshape(L, 1, 4096))
    return {k: np.ascontiguousarray(v, dtype=np.float32) for k, v in out.items()}


class Arena:
    def __init__(self, base_ap, segments):
        self.base = base_ap
        self.segs = [list(x) for x in segments]

    def alloc(self, shape, dtype=F32):
        P = shape[0]
        n = 1
        for d in shape[1:]:
            n *= d
        per = 4 if dtype in (F32, I32) else 2
        ncols = (n * per + 3) // 4
        ncols = (ncols + 7) // 8 * 8
        for sg in self.segs:
            if sg[1] - sg[0] >= ncols:
                off = sg[0]
                sg[0] += ncols
                ap = self.base[0:P, off:off + (n * per + 3) // 4]
                if dtype != F32:
                    ap = ap.bitcast(dtype)
                if len(shape) == 3:
                    ap = ap.rearrange("p (a b) -> p a b", a=shape[1])
                elif len(shape) == 4:
                    ap = ap.rearrange("p (a b c) -> p a b c", a=shape[1], b=shape[2])
                return ap
        raise RuntimeError("arena out of memory for %s" % (shape,))


class _Stop(Exception):
    pass


class Builder:
    def __init__(self):
        nc = bass.Bass("TRN2", target_bir_lowering=False)
        self.nc = nc
        self.S = Sched(nc)
        self.rec = None
        S_ = self.S
        S_._op, S_._dma = S_.op, S_.dma

        def _rop(*a):
            if self.rec is not None:
                self.rec.append(lambda: S_._op(*a))
            else:
                S_._op(*a)

        def _rdma(*a):
            if self.rec is not None:
                self.rec.append(lambda: S_._dma(*a))
            else:
                S_._dma(*a)
        S_.op, S_.dma = _rop, _rdma
        L = DEPTH
        dt = lambda n, s, d=F32, k="ExternalInput": nc.dram_tensor(n, s, d, kind=k).ap()
        self.x_in = dt("x", [SEQ, D])
        self.pos_in = dt("pos", [1, SEQ], I32)
        self.consts_in = dt("consts", [128, NC_CONST])
        self.e8_in = dt("e8", [8, SEQ])
        self.wqkv = dt("wqkv", [L, 8, 128, 3072])
        self.wssm = dt("wssm", [L, 8, 128, 6144])
        self.wdt = dt("wdt", [L, 128, 256])
        self.wc1 = dt("wc1", [L, 8, 128, 5120])
        self.wout = dt("wout", [L, 128, 8192])
        self.wup = dt("wup", [L, 8, 128, 4096])
        self.wdn = dt("wdn", [L, 8, 128, 4096])
        self.cw = dt("cw", [L, 128, 160])
        self.small = dt("small", [L, 1, 96])
        self.normw = dt("normw", [L, 1, 2048])
        self.lnp = dt("lnp", [L, 1, 4096])
        self.out = dt("out", [SEQ, D], F32, "ExternalOutput")
        self.x1 = dt("x1s", [SEQ, D], F32, "Internal")
        self.hres = dt("hres", [SEQ, D], F32, "Internal")
        self.dbg = {}
        for name, shape in DEBUG.items():
            self.dbg[name] = dt("dbg_" + name, list(shape), F32, "ExternalOutput")
        self.final_keys = []

    def sb(self, name, shape, dtype=F32):
        return self.ar.alloc(list(shape), dtype)

    def phase(self, segs):
        self.S.barrier()
        self.ar = Arena(self.arena, segs)

    def mm(self, reads, writes, out, pairs, transpose=False):
        def emit(e):
            n = len(pairs)
            ins = None
            for i, (l, r) in enumerate(pairs):
                ins = e.matmul(out, lhsT=l, rhs=r, start=(i == 0), stop=(i == n - 1))
            return ins
        self.S.op('pe', reads, writes, emit)

    def mmseq(self, reads, writes, groups):
        def emit(e):
            ins = None
            for out, pairs in groups:
                n = len(pairs)
                for i, (l, r) in enumerate(pairs):
                    ins = e.matmul(out, lhsT=l, rhs=r, start=(i == 0), stop=(i == n - 1))
            return ins
        self.S.op('pe', reads, writes, emit)

    def tps(self, reads, writes, items, ident):
        def emit(e):
            ins = None
            for o, i in items:
                ins = e.transpose(o, i, ident)
            return ins
        self.S.op('pe', reads, writes, emit)

    def dump(self, name, ap, keys):
        if name in self.dbg:
            if not isinstance(keys, list):
                keys = [keys]
            self.S.dma('sp', keys, [('dbg', name)], lambda e: e.dma_start(out=self.dbg[name], in_=ap))
            self.final_keys.append(('dbg', name))

    def build(self):
        nc, S = self.nc, self.S
        sb = self.sb
        nbytes = int(nc.sbuf_bytes_remaining) - 256
        NA = nbytes // 4 // 8 * 8
        self.arena = nc.alloc_sbuf_tensor("arena", [128, NA], F32)[:]
        self.R_P = (0, 11264)
        self.R_B1 = (11264, 19456)
        self.R_B2 = (19456, 35840)
        self.R_B3 = (35840, 44032)
        self.R_W = (44032, NA)
        assert NA - 44032 > 8000, NA
        self.ar = Arena(self.arena, [self.R_P])
        self.S.bar_scratch = sb("barscr", [128, 8])
        cst = sb("cst", [128, NC_CONST])
        self.cst = cst
        S.dma('sp', [], ['cst'], lambda e: e.dma_start(out=cst[:], in_=self.consts_in))
        self.ident_f = cst[:, C_ID:C_ID + 128]
        self.tri_f = cst[:, C_TRI:C_TRI + 128]
        self.ones_f = cst[:, C_ONES:C_ONES + 128]
        cbf = sb("cbf", [128, 512], BF16)
        S.op('dve', ['cst'], ['cbf'], lambda e: e.tensor_copy(cbf[:], cst[:, 0:512]))
        self.ident_b = cbf[:, 0:128]
        self.tri_b = cbf[:, 128:256]
        self.rm_b = cbf[:, 384:512]
        self.nh = sb("nh", [128, 8])
        S.op('pool', [], ['nh'], lambda e: e.memset(self.nh[:], -0.5))
        self.ropeC = sb("ropeC", [128, SEQ], BF16)
        self.ropeS = sb("ropeS", [128, SEQ], BF16)
        self.xT = sb("xT", [128, 8, SEQ], BF16)
        A = self.arena
        self.B1 = A[:, self.R_B1[0]:self.R_B1[1]].bitcast(BF16).rearrange("p (c t) -> p c t", c=8)
        self.B2 = A[:, self.R_B2[0]:self.R_B2[1]]
        self.B3 = A[:, self.R_B3[0]:self.R_B3[1]].bitcast(BF16).rearrange("p (c t) -> p c t", c=8)
        self.yT = self.B2.bitcast(BF16).rearrange("p (c t) -> p c t", c=16)
        self.acc = self.B2.rearrange("p (t f) -> p t f", t=16)
        self.ar = Arena(self.arena, [self.R_B2])
        self.PA = nc.alloc_psum_tensor("PA", [128, 2048], F32)
        self.PB = nc.alloc_psum_tensor("PB", [128, 1024], F32)
        self.PC = nc.alloc_psum_tensor("PC", [128, 512], F32)
        self.PD = nc.alloc_psum_tensor("PD", [128, 512], F32)
        self.rope_tables()
        if STOP_AFTER == 'rope':
            return self.finish()
        self.load_xT()
        if STOP_AFTER == 'xT':
            self.dbg_tile('xT0', self.xT[:, 3, 1024:1536], self.xT_keys(), 512)
            return self.finish()
        try:
            for l in range(DEPTH):
                self.layer(l)
                if STOP_AFTER == 'layer0':
                    break
        except _Stop:
            pass
        return self.finish()

    def finish(self):
        self.S.finish(self.final_keys)
        return self.nc

    def rope_tables(self):
        S, sb = self.S, self.sb
        cst = None
        posi = sb("posi", [128, SEQ], I32)
        S.dma('sp', [], ['posi'], lambda e: e.dma_start(out=posi[:], in_=self.pos_in.broadcast_to([128, SEQ])))
        ang = sb("ang", [128, SEQ])
        tmp = sb("rtmp", [128, SEQ])
        tmi = sb("rtmi", [128, SEQ], I32)
        rc = sb("rc", [128, SEQ])
        rs = sb("rs", [128, SEQ])
        invf = self.cst[:, C_INVF:C_INVF + 1]
        sgn = self.cst[:, C_SGN:C_SGN + 1]
        S.op('dve', ['posi'], ['ang'], lambda e: e.tensor_copy(ang[:], posi[:]))
        S.op('dve', ['ang', 'cst'], ['ang'], lambda e: e.tensor_scalar(ang[:], ang[:], invf, None, ALU.mult))
        for which, dst in ((0, rs), (1, rc)):
            off = 0.0 if which == 0 else PI / 2
            S.op('dve', ['ang'], ['rtmp'], lambda e, off=off: e.tensor_scalar(tmp[:], ang[:], off, 1.0 / (2 * PI), ALU.add, ALU.mult))
            S.op('dve', ['rtmp'], ['rtmi'], lambda e: e.tensor_copy(tmi[:], tmp[:]))
            S.op('dve', ['rtmi'], ['rtmp'], lambda e: e.tensor_copy(tmp[:], tmi[:]))
            S.op('dve', ['rtmp', 'ang'], ['rtmp'], lambda e: e.scalar_tensor_tensor(tmp[:], in0=tmp[:], scalar=-2 * PI, in1=ang[:], op0=ALU.mult, op1=ALU.add))
            S.op('dve', ['rtmp'], [('rope', which)], lambda e, off=off, dst=dst: e.tensor_scalar(dst[:], tmp[:], off, None, ALU.add))
            S.op('dve', [('rope', which)], ['rtmp'], lambda e, dst=dst: e.tensor_scalar(tmp[:], dst[:], PI, -2 * PI, ALU.is_gt, ALU.mult))
            S.op('dve', ['rtmp', ('rope', which)], [('rope', which)], lambda e, dst=dst: e.tensor_tensor(dst[:], dst[:], tmp[:], ALU.add))
            S.op('dve', [('rope', which)], ['rtmp'], lambda e, dst=dst: e.tensor_scalar(tmp[:], dst[:], -PI, 2 * PI, ALU.is_lt, ALU.mult))
            S.op('dve', ['rtmp', ('rope', which)], [('rope', which)], lambda e, dst=dst: e.tensor_tensor(dst[:], dst[:], tmp[:], ALU.add))
            S.op('dve', [('rope', which)], [('rope', which)], lambda e, dst=dst: e.tensor_scalar(dst[:], dst[:], 3.1415925, -3.1415925, ALU.min, ALU.max))
            S.op('act', [('rope', which)], [('rope', which)], lambda e, dst=dst: e.activation(dst[:], dst[:], AF.Sin))
        S.op('dve', [('rope', 0), 'cst'], [('rope', 0)], lambda e: e.tensor_scalar(self.ropeS[:], rs[:], sgn, None, ALU.mult))
        S.op('dve', [('rope', 1)], [('rope', 1)], lambda e: e.tensor_copy(self.ropeC[:], rc[:]))
        self.dump('ropeC', rc[:], ('rope', 1))
        self.dump('ropeS', rs[:], ('rope', 0))

    def load_xT(self):
        S, sb = self.S, self.sb
        self.xt_buf = [sb("xtile%d" % i, [128, D]) for i in range(2)]
        for tt in range(16):
            xt = self.xt_buf[tt % 2]
            k = ('xtile', tt % 2)
            S.dma('sp', [], [k], lambda e, xt=xt, tt=tt: e.dma_start(out=xt[:], in_=self.x_in[tt * 128:(tt + 1) * 128, :]))
            self.to_featmajor(xt, k, self.xT, 'xT', tt)

    def to_featmajor(self, tile, tkey, dstT, dkey, tt):
        S = self.S
        for half in range(2):
            ps = self.PA[:, half * 512:(half + 1) * 512]
            pk = ('PA', half)
            self.tps([tkey, 'cst'], [pk], [(ps[:, j * 128:(j + 1) * 128], tile[:, (half * 4 + j) * 128:(half * 4 + j + 1) * 128]) for j in range(4)], self.ident_f)
            eng = 'act' if half == 0 else 'dve'
            if eng == 'act':
                S.op('act', [pk], [(dkey, tt)], lambda e, ps=ps, half=half: e.activation(
                    dstT[:, half * 4:(half + 1) * 4, tt * 128:(tt + 1) * 128], ps.rearrange("p (c t) -> p c t", c=4), AF.Copy))
            else:
                S.op('dve', [pk], [(dkey, tt)], lambda e, ps=ps, half=half: e.tensor_copy(
                    dstT[:, half * 4:(half + 1) * 4, tt * 128:(tt + 1) * 128], ps.rearrange("p (c t) -> p c t", c=4)))

    def layer(self, l):
        self.attention(l)
        if STOP_AFTER == 'att':
            raise _Stop()
        self.ssd(l)
        if STOP_AFTER == 'ssd':
            raise _Stop()
        self.merge(l)
        if STOP_AFTER == 'merge':
            raise _Stop()
        self.mix_ln1(l)
        if STOP_AFTER == 'ln1':
            raise _Stop()
        self.ffn(l)

    def psum_epoch(self):
        keys = [('PA', i) for i in range(4)] + [('PB', 0), ('PB', 1), ('PB', 0, 0), ('PB', 0, 1)] + \
               [('PB', 1, h) for h in range(4)] + ['PC', 'PCg', 'PCo', 'PD']
        self.S.alias(keys, keys)

    def dbg_qk(self, h):
        S = self.S
        if 'qa0' in self.dbg:
            qa, ka = self.qa[h], self.ka[h]
            dq = self.sb("dbgq", [128, SEQ])
            dk = self.sb("dbgk", [128, SEQ])
            S.op('dve', [('qa', h)], ['dbgq'], lambda e: e.tensor_copy(dq[0:72, :], qa[0:72, :]))
            S.op('dve', [('ka', h)], ['dbgk'], lambda e: e.tensor_copy(dk[0:72, :], ka[0:72, :]))
            self.dump('qa0', dq[0:72, :], 'dbgq')
            self.dump('ka0', dk[0:72, :], 'dbgk')

    def dbg_tile(self, name, ap, keys, ncols):
        if name in self.dbg:
            t = self.sb("dbg_" + name, [128, ncols])
            self.S.op('dve', keys, ['dbgt_' + name], lambda e: e.tensor_copy(t[:], ap))
            self.dump(name, t[:], 'dbgt_' + name)

    def xT_keys(self):
        return [('xT', tt) for tt in range(16)]

    def attention(self, l):
        self.psum_epoch()
        S, sb, nc = self.S, self.sb, self.nc
        self.phase([self.R_W, self.R_B2, (self.R_B3[0] + 7168, self.R_B3[1])])
        if True:
            self.wqkv_b = [sb("wqkv%d" % i, [128, 8, 384], BF16) for i in range(2)]
            self.qa = [sb("qa%d" % i, [72, SEQ], BF16) for i in range(4)]
            self.ka = [sb("ka%d" % i, [72, SEQ], BF16) for i in range(4)]
            self.va = [sb("va%d" % i, [128, 16, 65], BF16) for i in range(4)]
            self.qbf = [sb("qbf%d" % i, [128, 512], BF16) for i in range(2)]
            self.t1 = [sb("t1_%d" % i, [128, 512]) for i in range(2)]
            self.t2 = [sb("t2_%d" % i, [128, 512]) for i in range(2)]
            self.atm = [sb("atm%d" % i, [128, 16, 128], BF16) for i in range(2)]
            self.km = sb("km", [64, 8])
            self.kmh = sb("kmh", [64, 8], BF16)
            self.kml = sb("kml", [64, 8], BF16)
            self.gm = sb("gm", [128, 64])
            self.m8 = sb("m8", [128, 8, 8])
            self.lt = sb("lt", [128, 64])
            self.bpad = sb("bpad", [128, 8, 72], BF16)
            self.rden = sb("rden", [128, 4])
            self.NPT = 28
            for i in range(4):
                S.op('pool', [], [('qa', i)], lambda e, i=i: e.memset(self.qa[i][64:72, :], 0.0))
                S.dma('pool', [], [('ka', i)], lambda e, i=i: e.dma_start(out=self.ka[i][64:72, :], in_=self.e8_in))
                S.op('pool', [], [('va', i)], lambda e, i=i: e.memset(self.va[i][:, :, 64:65], 1.0))
            S.op('pool', [], ['bpad'], lambda e: e.memset(self.bpad[:], 0.0))
            if STOP_AFTER == 'att_init':
                self.dbg_qk(0)
                raise _Stop()
        PT = [self.B3[:, i // 4, (i % 4) * 512:(i % 4 + 1) * 512] for i in range(self.NPT)]
        attT = self.B1
        negp = self.cst[:, C_NEGP:C_NEGP + 64]
        floorb = self.cst[:, C_FLOOR:C_FLOOR + 64]
        pt_rr = [0]
        PDb = self.PD[:].bitcast(BF16)
        xk = self.xT_keys()

        def load_w(hp):
            wb = self.wqkv_b[hp % 2]
            S.dma('pool', [], [('wqkv', hp % 2, 0)], lambda e, wb=wb, hp=hp: e.dma_start(
                out=wb[:], in_=self.wqkv[l, hp].rearrange("p (k n) -> p k n", k=8), max_dma_last_dim=4096))
        load_w(0)
        Wl = [[] for _ in range(9)]
        Pl = [[] for _ in range(9)]
        Cl = [[] for _ in range(9)]
        for hp in range(8):
            wb = self.wqkv_b[hp % 2]
            wk = ('wqkv', hp % 2, 0)
            wk1 = wk2 = wk3 = wk
            self.rec = Wl[hp + 1]
            if STOP_AFTER == 'att_w0':
                self.dbg_tile('wb0', wb[:].rearrange("p a b -> p (a b)"), [wk, wk1, wk2, wk3], 3072)
                raise _Stop()
            if hp + 1 < 8:
                load_w(hp + 1)
            if STOP_AFTER == 'att_w1':
                self.dbg_tile('wb0', wb[:].rearrange("p a b -> p (a b)"), [wk, wk1, wk2, wk3], 3072)
                raise _Stop()
            par = (hp % 2) * 2
            hA, hB = par, par + 1
            self.rec = Pl[hp]
            cnt = 0
            for which in (1, 0):
                dst = self.ka if which == 1 else self.qa
                dn = 'ka' if which == 1 else 'qa'
                for tc in range(4):
                    bank = cnt % 4
                    cnt += 1
                    ps = self.PA[:, bank * 512:(bank + 1) * 512]
                    pk = ('PA', bank)
                    cols = slice(tc * 512, (tc + 1) * 512)
                    def stopat(ch, ap=None, keys=None):
                        if STOP_AFTER == 'att_qk1' + ch:
                            if ap is not None:
                                self.dbg_tile('probe', ap, keys, 512)
                            raise _Stop()
                    self.mm([wk, wk1, wk2, wk3] + xk[tc * 4:(tc + 1) * 4], [pk], ps,
                            [(wb[:, kc, which * 128:(which + 1) * 128], self.xT[:, kc, cols]) for kc in range(8)])
                    stopat('a', ps, [pk])
                    qb = self.qbf[cnt % 2]
                    qk_ = ('qbf', cnt % 2)
                    if VARIANT != 6:
                        S.op('act', [pk], [qk_], lambda e, qb=qb, ps=ps: e.activation(qb[:], ps, AF.Copy))
                    stopat('b', qb[:], [qk_])
                    t1 = self.t1[cnt % 2]
                    t2 = self.t2[cnt % 2]
                    k1 = ('t1', cnt % 2)
                    k2 = ('t2', cnt % 2)
                    if VARIANT == 4:
                        t1 = self.sb("t1x", [128, 512])
                    if VARIANT in (5, 8):
                        S.op('dve', [('rope', 1)], [k1], lambda e, t1=t1, ps=ps, cols=cols: e.tensor_copy(t1[:], self.ropeC[:, cols]))
                    elif VARIANT in (1, 4):
                        S.op('dve', [pk, ('rope', 1)], [k1], lambda e, t1=t1, ps=ps, cols=cols: e.tensor_copy(t1[:], ps))
                    elif VARIANT == 2:
                        S.op('dve', [pk, ('rope', 1)], [k1], lambda e, t1=t1, ps=ps, cols=cols: e.tensor_copy(t1[:], self.ropeC[:, cols]))
                    elif VARIANT == 3:
                        S.op('dve', [pk, ('rope', 1)], [k1], lambda e, t1=t1, ps=ps, cols=cols: e.tensor_tensor(t1[:], ps, self.cst[:, 0:512], ALU.mult))
                    else:
                        S.op('dve', [pk, ('rope', 1)], [k1], lambda e, t1=t1, ps=ps, cols=cols: e.tensor_tensor(t1[:], ps, self.ropeC[:, cols], ALU.mult))
                    stopat('c', qb[:] if VARIANT == 7 else t1[:], [qk_, k1] if VARIANT in (7, 8) else [k1])
                    rb = (bank + 2) % 4
                    pr = self.PA[:, rb * 512:(rb + 1) * 512]
                    prk = ('PA', rb)
                    self.mm([qk_, 'cbf'], [prk], pr, [(self.rm_b, qb[:])])
                    stopat('d', pr, [prk])
                    S.op('dve', [prk, ('rope', 0)], [k2], lambda e, t2=t2, pr=pr, cols=cols: e.tensor_tensor(t2[:], pr, self.ropeS[:, cols], ALU.mult))
                    stopat('e', t2[:], [k2])
                    S.op('dve', [k1, k2], [(dn, hA)], lambda e, t1=t1, t2=t2, dst=dst, cols=cols, hA=hA: e.tensor_tensor(dst[hA][0:64, cols], t1[0:64, :], t2[0:64, :], ALU.add))
                    stopat('f', dst[hA][0:64, cols], [(dn, hA)])
                    S.op('dve', [k1, k2], [(dn, hB)], lambda e, t1=t1, t2=t2, dst=dst, cols=cols, hB=hB: e.tensor_tensor(dst[hB][0:64, cols], t1[64:128, :], t2[64:128, :], ALU.add))
                    stopat('g', dst[hB][0:64, cols], [(dn, hB)])
            if STOP_AFTER == 'att_qk':
                raise _Stop()
            for tq in range(4):
                bank = cnt % 4
                cnt += 1
                ps = self.PA[:, bank * 512:(bank + 1) * 512]
                pk = ('PA', bank)
                self.mmseq([wk, wk1, wk2, wk3] + xk[tq * 4:(tq + 1) * 4], [pk],
                           [(ps[:, j * 128:(j + 1) * 128],
                             [(self.xT[:, kc, (tq * 4 + j) * 128:(tq * 4 + j + 1) * 128], wb[:, kc, 256:384]) for kc in range(8)])
                            for j in range(4)])
                psv = ps.rearrange("p (t c) -> p t c", t=4)
                if hp == 0 and tq == 0:
                    self.dbg_tile('psv', ps, [pk], 512)
                    self.dbg_tile('wb0', wb[:].rearrange("p a b -> p (a b)"), [wk, wk1, wk2, wk3], 3072)
                S.op('act', [pk], [('va', hA)], lambda e, psv=psv, tq=tq, hA=hA: e.activation(self.va[hA][:, tq * 4:(tq + 1) * 4, 0:64], psv[:, :, 0:64], AF.Copy))
                S.op('act', [pk], [('va', hB)], lambda e, psv=psv, tq=tq, hB=hB: e.activation(self.va[hB][:, tq * 4:(tq + 1) * 4, 0:64], psv[:, :, 64:128], AF.Copy))
            if STOP_AFTER == 'att_proj':
                self.dbg_tile('va0', self.va[0][:].rearrange("p a b -> p (a b)"), [('va', 0)], 1040)
                self.dbg_qk(hA)
                raise _Stop()
            self.rec = Cl[hp]
            atm = self.atm[hp % 2]
            ak = ('atm', hp % 2)
            for hh, hbuf in ((0, hA), (1, hB)):
                qa, ka, va = self.qa[hbuf], self.ka[hbuf], self.va[hbuf]
                qk, kk, vk = ('qa', hbuf), ('ka', hbuf), ('va', hbuf)
                S.op('dve', [kk], ['km'], lambda e, ka=ka: e.tensor_reduce(self.km[:], ka[0:64, :].rearrange("p (b t) -> p b t", b=8), AX.X, ALU.add))
                S.op('dve', ['km'], ['kmh'], lambda e: e.tensor_scalar(self.kmh[:], self.km[:], 1.0 / 256, None, ALU.mult))
                S.op('dve', ['km', 'kmh'], ['kml'], lambda e: e.scalar_tensor_tensor(self.kml[:], in0=self.km[:], scalar=1.0 / 256, in1=self.kmh[:], op0=ALU.mult, op1=ALU.subtract))
                pg = self.PC[:, 448:512]
                self.mmseq([qk, 'kmh', 'kml'], ['PCg'],
                           [(pg[:, t * 8:(t + 1) * 8], [(qa[0:64, (8 + t) * 128:(9 + t) * 128], self.kmh[:]), (qa[0:64, (8 + t) * 128:(9 + t) * 128], self.kml[:])]) for t in range(8)])
                S.op('dve', ['PCg', 'cst'], ['gm'], lambda e, pg=pg: e.tensor_tensor(self.gm[:], pg, negp, ALU.add))

                S.op('dve', ['gm'], ['m8'], lambda e: sel_max(e, self))
                S.op('dve', ['gm', 'm8'], ['lt'], lambda e: sel_lt(e, self))
                S.op('dve', ['lt', 'cst'], ['bpad'], lambda e: e.tensor_tensor(self.bpad[:, :, 64:72], self.lt[:].rearrange("p (t j) -> p t j", t=8), floorb.rearrange("p (t j) -> p t j", t=8), ALU.max))
                self.tps(['bpad', 'cbf'], ['PD'], [(PDb[0:72, t * 128:(t + 1) * 128], self.bpad[:, t, :]) for t in range(8)], self.ident_b)
                S.op('act', ['PD'], [qk], lambda e, qa=qa: e.activation(qa[64:72, 1024:2048], PDb[64:72, :], AF.Copy))
                if STOP_AFTER == 'att_gate':
                    self.dbg_qk(hA)
                    raise _Stop()
                slots = {}

                def stageA(qc):
                    sl = []
                    for kt in range(4 * qc + 4):
                        c0 = max(0, kt * 128 - qc * 512)
                        bank = kt % 2
                        ps = self.PB[:, bank * 512:(bank + 1) * 512]
                        pk = ('PB', bank)
                        self.mm([kk, qk], [pk], ps[:, c0:512], [(ka[0:72, kt * 128:(kt + 1) * 128], qa[0:72, qc * 512 + c0:(qc + 1) * 512])])
                        si = pt_rr[0]
                        pt_rr[0] = (pt_rr[0] + 1) % self.NPT
                        sl.append(si)
                        pt = PT[si]
                        S.op('act', [pk], [('PT', si)], lambda e, pt=pt, ps=ps, c0=c0: e.activation(pt[:, c0:512], ps[:, c0:512], AF.Exp, scale=0.125))
                        if kt >= 4 * qc:
                            S.op('dve', [('PT', si), 'cbf'], [('PT', si)], lambda e, pt=pt, c0=c0: e.tensor_tensor(pt[:, c0:c0 + 128], pt[:, c0:c0 + 128], self.tri_b, ALU.mult))
                    slots[qc] = sl

                def stageB(qc):
                    sl = slots[qc]
                    groups = []
                    for qi in range(4):
                        qt = 4 * qc + qi
                        groups.append((self.PC[:, qi * 65:(qi + 1) * 65],
                                       [(PT[sl[kt]][:, qi * 128:(qi + 1) * 128], va[:, kt, :]) for kt in range(qt + 1)]))
                    self.mmseq([('PT', s_) for s_ in sl] + [vk], ['PCo'], groups)
                    pv = self.PC[:, 0:260].rearrange("p (q c) -> p q c", q=4)
                    S.op('dve', ['PCo'], ['rden'], lambda e, pv=pv: e.reciprocal(self.rden[:], pv[:, :, 64]))

                    def norm(e, atm=atm, hh=hh, qc=qc):
                        ins = None
                        for qi in range(4):
                            ins = e.tensor_scalar(atm[:, 4 * qc + qi, hh * 64:(hh + 1) * 64], self.PC[:, qi * 65:qi * 65 + 64], self.rden[:, qi:qi + 1], None, ALU.mult)
                        return ins
                    S.op('dve', ['PCo', 'rden'], [ak], norm)

                stageA(0)
                for qc in range(4):
                    if qc + 1 < 4:
                        stageA(qc + 1)
                    stageB(qc)
            if hp == 0:
                self.dbg_tile('va0', self.va[0][:].rearrange("p a b -> p (a b)"), [('va', 0)], 1040)
                self.dbg_tile('atm0', atm[:].rearrange("p a b -> p (a b)"), [ak], 2048)
            for half in range(2):
                self.tps([ak, 'cbf'], ['PD'], [(PDb[:, t * 128:(t + 1) * 128], atm[:, half * 8 + t, :]) for t in range(8)], self.ident_b)
                S.op('act', ['PD'], [('attT', hp, half)], lambda e, half=half, hp=hp: e.activation(attT[:, hp, half * 1024:(half + 1) * 1024], PDb, AF.Copy))
            self.rec = None
        for t in Pl[0]:
            t()
        for hp in range(8):
            for t in Wl[hp + 1]:
                t()
            A, Bn = Cl[hp], Pl[hp + 1]
            n1, n2 = len(A), len(Bn)
            i1 = i2 = 0
            while i1 < n1 or i2 < n2:
                if i2 >= n2 or (i1 < n1 and i1 * n2 <= i2 * n1):
                    A[i1]()
                    i1 += 1
                else:
                    Bn[i2]()
                    i2 += 1
        if 'attT' in self.dbg:
            self.dbgf = self.sb("dbgf", [128, 512])
            for hp in range(8):
                for q4 in range(4):
                    S.op('dve', [('attT', hp, 0), ('attT', hp, 1), ('dbg', 'attT')], ['dbgf'], lambda e, hp=hp, q4=q4: e.tensor_copy(self.dbgf[:, 0:512], attT[:, hp, q4 * 512:(q4 + 1) * 512]))
                    S.dma('sp', ['dbgf'], [('dbg', 'attT')], lambda e, hp=hp, q4=q4: e.dma_start(out=self.dbg['attT'][hp * 128:(hp + 1) * 128, q4 * 512:(q4 + 1) * 512], in_=self.dbgf[:, 0:512]))
            self.final_keys.append(('dbg', 'attT'))

    def ssd(self, l):
        self.psum_epoch()
        S, sb, nc = self.S, self.sb, self.nc
        self.phase([self.R_W, self.R_B3])
        if True:
            self.wssm_b = [sb("wssm0", [128, 8, 768], BF16)] * 2
            self.wdt_b = sb("wdtb", [128, 8, 32], BF16)
            self.smallb = sb("smallb", [128, 96])
            self.cwb = sb("cwb", [128, 8, 4, 5])
            self.dt_all = sb("dt_all", [128, 16, 32])
            self.adt_all = sb("adt_all", [128, 16, 32])
            self.acs_all = sb("acs_all", [128, 16, 32])
            self.dst_all = sb("dst_all", [128, 16, 32])
            self.ea_all = sb("ea_all", [128, 16, 32])
            self.lastb = sb("lastb", [128, 8, 32])
            self.cdb = sb("cdb", [128, 8, 32])
            self.a_b = sb("a_b", [128, 32])
            self.uext = [sb("uext%d" % i, [128, 4, 259]) for i in range(2)]
            self.xh = sb("xh", [128, 4, 256])
            self.tnh = sb("tnh", [128, 4, 256])
            self.bcb2 = [sb("bcb%d" % i, [128, 2, 256], BF16) for i in range(2)]
            self.bcf = sb("bcf", [128, 256])
            self.xs_tok2 = [sb("xs_tok%d" % i, [128, 2, 256], BF16) for i in range(2)]
            self.xdt2 = [sb("xdt%d" % i, [128, 2, 256], BF16) for i in range(2)]
            self.xw2 = [sb("xw%d" % i, [128, 2, 256], BF16) for i in range(2)]
            self.btok2 = [sb("btok%d" % i, [128, 2, 128], BF16) for i in range(2)]
            self.cb32 = [sb("cb3_%d" % i, [128, 3, 128]) for i in range(2)]
            self.dtmp = [sb("dtmp%d" % i, [128, 3, 128]) for i in range(2)]
            self.mt = [sb("mt%d" % i, [128, 3, 128], BF16) for i in range(2)]
            self.prev_f = sb("prev_f", [128, 256])
            self.prev_b = sb("prev_b", [128, 256], BF16)
            self.tz = sb("tz", [128, 2, 256])
            self.yv = sb("yv", [128, 2, 256])
            self.ss = sb("ss", [128, 2])
            self.rstd = sb("rstd", [128, 2])
            self.nwb = [sb("nwb%d" % i, [128, 256]) for i in range(2)]
            self.xsf = self.xh[:, 0:2, :]
            self.zs = self.tz
            self.h2 = self.yv
            self.yn = self.yv
            self.junk = sb("junk", [128, 256], BF16)
        xk = self.xT_keys()
        S.dma('sp', [], ['smallb'], lambda e: e.dma_start(out=self.smallb[:], in_=self.small[l].broadcast_to([128, 96])))
        S.dma('sp', [], ['cwb'], lambda e: e.dma_start(out=self.cwb[:], in_=self.cw[l].rearrange("p (g c k) -> p g c k", g=8, c=4)))
        S.dma('pool', [], ['wdtb'], lambda e: e.dma_start(out=self.wdt_b[:], in_=self.wdt[l].rearrange("p (k n) -> p k n", k=8)))
        S.op('pool', ['cwb'], ['cwb'], lambda e: e.tensor_scalar(self.cwb[:], self.cwb[:], 0.5, None, ALU.mult))
        dtb = self.smallb[:, 0:32]
        alog = self.smallb[:, 32:64]
        dsk = self.smallb[:, 64:96]
        S.op('act', ['smallb'], ['a_b'], lambda e: e.activation(self.a_b[:], alog, AF.Exp))
        S.op('dve', ['a_b'], ['a_b'], lambda e: e.tensor_scalar(self.a_b[:], self.a_b[:], -1.0, None, ALU.mult))
        psd = self.PA[:, 0:512]
        self.mmseq(['wdtb'] + xk, [('PA', 0)],
                   [(psd[:, tt * 32:(tt + 1) * 32], [(self.xT[:, kc, tt * 128:(tt + 1) * 128], self.wdt_b[:, kc, :]) for kc in range(8)]) for tt in range(16)])
        dtf = self.dt_all[:].rearrange("p t h -> p (t h)")
        psd3 = psd.rearrange("p (t h) -> p t h", t=16)
        S.op('dve', [('PA', 0), 'smallb'], ['dt_all'], lambda e: e.tensor_tensor(self.dt_all[:], psd3, dtb.unsqueeze(1).broadcast_to([128, 16, 32]), ALU.add))
        S.op('act', ['dt_all'], ['dt_all'], lambda e: e.activation(dtf, dtf, AF.Exp))
        S.op('act', ['dt_all'], ['dt_all'], lambda e: e.activation(dtf, dtf, AF.Ln, bias=1.0))
        S.op('dve', ['dt_all', 'a_b'], ['adt_all'], lambda e: e.tensor_tensor(self.adt_all[:], self.dt_all[:], self.a_b[:].unsqueeze(1).broadcast_to([128, 16, 32]), ALU.mult))
        psa = self.PA[:, 512:1024]
        psl = self.PA[:, 1024:1280]
        groups = []
        for c in range(8):
            a0 = self.adt_all[:, 2 * c, :]
            a1 = self.adt_all[:, 2 * c + 1, :]
            groups.append((psa[:, (2 * c) * 32:(2 * c + 1) * 32], [(self.tri_f, a0)]))
            groups.append((psa[:, (2 * c + 1) * 32:(2 * c + 2) * 32], [(self.ones_f, a0), (self.tri_f, a1)]))
        self.mmseq(['adt_all', 'cst'], [('PA', 1)], groups)
        self.mmseq(['adt_all', 'cst'], [('PA', 2)],
                   [(psl[:, c * 32:(c + 1) * 32], [(self.ones_f, self.adt_all[:, 2 * c, :]), (self.ones_f, self.adt_all[:, 2 * c + 1, :])]) for c in range(8)])
        acsf = self.acs_all[:].rearrange("p t h -> p (t h)")
        S.op('dve', [('PA', 1)], ['acs_all'], lambda e: e.tensor_copy(acsf, psa))
        S.op('dve', [('PA', 2)], ['lastb'], lambda e: e.tensor_copy(self.lastb[:].rearrange("p c h -> p (c h)"), psl))
        S.op('dve', ['lastb', 'acs_all'], ['dst_all'], lambda e: e.tensor_tensor(
            self.dst_all[:].rearrange("p (c s) h -> p c s h", c=8), self.lastb[:].unsqueeze(2).broadcast_to([128, 8, 2, 32]),
            self.acs_all[:].rearrange("p (c s) h -> p c s h", c=8), ALU.subtract))
        dstf = self.dst_all[:].rearrange("p t h -> p (t h)")
        S.op('act', ['dst_all'], ['dst_all'], lambda e: e.activation(dstf, dstf, AF.Exp))
        S.op('dve', ['dst_all', 'dt_all'], ['dst_all'], lambda e: e.tensor_tensor(self.dst_all[:], self.dst_all[:], self.dt_all[:], ALU.mult))
        S.op('act', ['acs_all'], ['ea_all'], lambda e: e.activation(self.ea_all[:].rearrange("p t h -> p (t h)"), acsf, AF.Exp))
        S.op('act', ['lastb'], ['cdb'], lambda e: e.activation(self.cdb[:].rearrange("p c h -> p (c h)"), self.lastb[:].rearrange("p c h -> p (c h)"), AF.Exp))
        self.dump('dt_all', self.dt_all[:].rearrange("p t h -> p (t h)"), 'dt_all')
        self.dump('acs_all', acsf, 'acs_all')

        P0 = self.PA[:, 0:512]
        P1 = self.PA[:, 512:1024]
        P2 = self.PA[:, 1024:1536]
        P3 = self.PA[:, 1536:2048]
        P4 = self.PB[:, 0:512]
        P5 = self.PB[:, 512:1024]
        P6 = self.PC[:, 0:512]
        P7 = self.PD[:, 0:512]
        uprev = None
        def load_g(g):
            wb = self.wssm_b[g % 2]
            S.dma('pool', [], [('wssm', 0, 0)], lambda e, wb=wb, g=g: e.dma_start(
                out=wb[:], in_=self.wssm[l, g].rearrange("p (k n) -> p k n", k=8), max_dma_last_dim=4096))
            nw = self.nwb[g % 2]
            S.dma('sp', [], [('nwb', g % 2)], lambda e, nw=nw, g=g: e.dma_start(out=nw[:], in_=self.normw[l][:, g * 256:(g + 1) * 256].broadcast_to([128, 256])))
        load_g(0)
        for g in range(8):
            wb = self.wssm_b[g % 2]
            wk = ('wssm', 0, 0)
            wkall = [('wssm', 0, 0)]
            nw = self.nwb[g % 2]
            nk = ('nwb', g % 2)
            if g > 0:
                load_g(g)
            hs = slice(4 * g, 4 * g + 4)
            prevB = None
            for c in range(8):
                cols = slice(c * 256, (c + 1) * 256)
                sl = c % 2
                bcb, xs_tok, xw, btok = self.bcb2[sl], self.xs_tok2[sl], self.xw2[sl], self.btok2[sl]
                kbcb, kxs, kxw, kbt = ('bcb', sl), ('xs_tok', sl), ('xw', sl), ('btok', sl)
                xdt, cb3 = self.xdt2[sl], self.cb32[sl]
                kxdt, kcb3 = ('xdt', sl), ('cb3', sl)
                F1, F2, Bq = [], [], []
                self.rec = F1
                xkc = xk[2 * c:2 * c + 2]
                ue = self.uext[c % 2]
                uk = ('uext', c % 2)
                self.mmseq(wkall + xkc, [('PA', 0)],
                           [(P0[:, j * 256:(j + 1) * 256], [(wb[:, kc, 256 + j * 128:256 + (j + 1) * 128], self.xT[:, kc, cols]) for kc in range(8)]) for j in range(2)])
                self.mmseq(wkall + xkc, [('PA', 1)],
                           [(P1[:, j * 256:(j + 1) * 256], [(wb[:, kc, 512 + j * 128:512 + (j + 1) * 128], self.xT[:, kc, cols]) for kc in range(8)]) for j in range(2)])
                if c == 0:
                    S.op('pool', [], [uk], lambda e, bcb=bcb, xs_tok=xs_tok, xw=xw, btok=btok, xdt=xdt, cb3=cb3, ue=ue: e.memset(ue[:, :, 0:3], 0.0))
                else:
                    up = self.uext[(c - 1) % 2]
                    S.op('pool', [('uext', (c - 1) % 2)], [uk], lambda e, bcb=bcb, xs_tok=xs_tok, xw=xw, btok=btok, xdt=xdt, cb3=cb3, ue=ue, up=up: e.tensor_copy(ue[:, :, 0:3], up[:, :, 256:259]))
                S.op('act', [('PA', 0)], [uk], lambda e, bcb=bcb, xs_tok=xs_tok, xw=xw, btok=btok, xdt=xdt, cb3=cb3, ue=ue: e.activation(ue[:, 0:2, 3:259], P0.rearrange("p (j t) -> p j t", j=2), AF.Copy))
                S.op('act', [('PA', 1)], [uk], lambda e, bcb=bcb, xs_tok=xs_tok, xw=xw, btok=btok, xdt=xdt, cb3=cb3, ue=ue: e.activation(ue[:, 2:4, 3:259], P1.rearrange("p (j t) -> p j t", j=2), AF.Copy))
                for ci in range(4):
                    eng = 'dve'
                    w = self.cwb[:, g, ci, :]
                    o = self.xh[:, ci, :]
                    S.op(eng, [uk, 'cwb'], [('xh', ci)], lambda e, bcb=bcb, xs_tok=xs_tok, xw=xw, btok=btok, xdt=xdt, cb3=cb3, ci=ci, ue=ue, w=w, o=o: e.tensor_scalar(o, ue[:, ci, 0:256], w[:, 0:1], w[:, 4:5], ALU.mult, ALU.add))
                    for k in range(1, 4):
                        if eng == 'dve':
                            S.op('dve', [uk, 'cwb', ('xh', ci)], [('xh', ci)], lambda e, bcb=bcb, xs_tok=xs_tok, xw=xw, btok=btok, xdt=xdt, cb3=cb3, ci=ci, ue=ue, w=w, o=o, k=k: e.scalar_tensor_tensor(
                                o, in0=ue[:, ci, k:k + 256], scalar=w[:, k:k + 1], in1=o, op0=ALU.mult, op1=ALU.add))
                        else:
                            S.op('pool', [uk, 'cwb'], [('tnh', ci)], lambda e, bcb=bcb, xs_tok=xs_tok, xw=xw, btok=btok, xdt=xdt, cb3=cb3, ci=ci, ue=ue, w=w, k=k: e.tensor_scalar(self.tnh[:, ci, :], ue[:, ci, k:k + 256], w[:, k:k + 1], None, ALU.mult))
                            S.op('pool', [('tnh', ci), ('xh', ci)], [('xh', ci)], lambda e, bcb=bcb, xs_tok=xs_tok, xw=xw, btok=btok, xdt=xdt, cb3=cb3, ci=ci, o=o: e.tensor_tensor(o, o, self.tnh[:, ci, :], ALU.add))
                S.op('act', [('xh', ci) for ci in range(4)], [('tnh', ci) for ci in range(4)], lambda e, bcb=bcb, xs_tok=xs_tok, xw=xw, btok=btok, xdt=xdt, cb3=cb3: e.activation(self.tnh[:], self.xh[:], AF.Tanh))
                S.op('dve', [('tnh', 0), ('tnh', 1), ('xh', 0), ('xh', 1)], [('xh', 0), ('xh', 1)], lambda e, bcb=bcb, xs_tok=xs_tok, xw=xw, btok=btok, xdt=xdt, cb3=cb3: e.scalar_tensor_tensor(self.xsf[:], in0=self.tnh[:, 0:2, :], scalar=1.0, in1=self.xh[:, 0:2, :], op0=ALU.add, op1=ALU.mult))
                S.op('dve', [('tnh', 2), ('tnh', 3), ('xh', 2), ('xh', 3)], [kbcb], lambda e, bcb=bcb, xs_tok=xs_tok, xw=xw, btok=btok, xdt=xdt, cb3=cb3: e.scalar_tensor_tensor(bcb[:], in0=self.tnh[:, 2:4, :], scalar=1.0, in1=self.xh[:, 2:4, :], op0=ALU.add, op1=ALU.mult))
                S.op('dve', [('tnh', 2), ('xh', 2)], ['bcf'], lambda e, bcb=bcb, xs_tok=xs_tok, xw=xw, btok=btok, xdt=xdt, cb3=cb3: e.scalar_tensor_tensor(self.bcf[:], in0=self.tnh[:, 2, :], scalar=1.0, in1=self.xh[:, 2, :], op0=ALU.add, op1=ALU.mult))
                self.tps([('xh', 0), ('xh', 1), 'cst'], [('PA', 0)],
                         [(P0[:, si * 256 + j * 128:si * 256 + (j + 1) * 128], self.xsf[:, j, si * 128:(si + 1) * 128]) for si in range(2) for j in range(2)], self.ident_f)
                self.tps(['bcf', 'cst'], [('PA', 1)],
                         [(P1[:, si * 128:(si + 1) * 128], self.bcf[:, si * 128:(si + 1) * 128]) for si in range(2)], self.ident_f)
                P0v = P0.rearrange("p (s h d) -> p s h d", s=2, h=4)
                S.op('act', [('PA', 0)], [kxs], lambda e, bcb=bcb, xs_tok=xs_tok, xw=xw, btok=btok, xdt=xdt, cb3=cb3: e.activation(xs_tok[:].rearrange("p s c -> p (s c)"), P0, AF.Copy))
                S.op('dve', [('PA', 0), 'dt_all'], [kxdt], lambda e, bcb=bcb, xs_tok=xs_tok, xw=xw, btok=btok, xdt=xdt, cb3=cb3, c=c, hs=hs: e.tensor_tensor(
                    xdt[:].rearrange("p s (h d) -> p s h d", h=4), P0v,
                    self.dt_all[:, 2 * c:2 * c + 2, hs].unsqueeze(3).broadcast_to([128, 2, 4, 64]), ALU.mult))
                S.op('dve', [('PA', 0), 'dst_all'], [kxw], lambda e, bcb=bcb, xs_tok=xs_tok, xw=xw, btok=btok, xdt=xdt, cb3=cb3, c=c, hs=hs: e.tensor_tensor(
                    xw[:].rearrange("p s (h d) -> p s h d", h=4), P0v,
                    self.dst_all[:, 2 * c:2 * c + 2, hs].unsqueeze(3).broadcast_to([128, 2, 4, 64]), ALU.mult))
                S.op('act', [('PA', 1)], [kbt], lambda e, bcb=bcb, xs_tok=xs_tok, xw=xw, btok=btok, xdt=xdt, cb3=cb3: e.activation(btok[:].rearrange("p s n -> p (s n)"), P1[:, 0:256], AF.Copy))
                self.mmseq([kbcb], [('PA', 3)],
                           [(P3[:, 0:256], [(bcb[:, 0, 0:128], bcb[:, 1, :])]),
                            (P3[:, 384:512], [(bcb[:, 0, 128:256], bcb[:, 1, 128:256])])])
                S.op('dve', [('PA', 3), 'cst'], [kcb3], lambda e, bcb=bcb, xs_tok=xs_tok, xw=xw, btok=btok, xdt=xdt, cb3=cb3: e.tensor_tensor(cb3[:, 0, :], P3[:, 0:128], self.tri_f, ALU.mult))
                S.op('act', [('PA', 3)], [kcb3], lambda e, bcb=bcb, xs_tok=xs_tok, xw=xw, btok=btok, xdt=xdt, cb3=cb3: e.activation(cb3[:, 1, :], P3[:, 128:256], AF.Copy))
                S.op('dve', [('PA', 3), 'cst', kcb3], [kcb3], lambda e, bcb=bcb, xs_tok=xs_tok, xw=xw, btok=btok, xdt=xdt, cb3=cb3: e.tensor_tensor(cb3[:, 2, :], P3[:, 384:512], self.tri_f, ALU.mult))
                self.rec = F2
                def bc_mm(hh):
                    hg = 4 * g + hh
                    pb = (P4 if hh % 2 == 0 else P6)[:, 0:256]
                    pbk = ('PB', 0, 0) if hh % 2 == 0 else 'PC'
                    a0 = self.adt_all[:, 2 * c, hg:hg + 1].broadcast_to([128, 128])
                    a1 = self.adt_all[:, 2 * c + 1, hg:hg + 1].broadcast_to([128, 128])
                    self.mmseq(['adt_all', 'cst'], [pbk],
                               [(pb[:, 0:128], [(a0, self.tri_f)]),
                                (pb[:, 128:256], [(a0, self.ones_f), (a1, self.tri_f)])])
                    return pb, pbk
                nxt = bc_mm(0)
                for hh in range(4):
                    hg = 4 * g + hh
                    pb, pbk = nxt
                    if hh + 1 < 4:
                        nxt = bc_mm(hh + 1)
                    dtm = self.dtmp[hh % 2]
                    dk = ('dtmp', hh % 2)
                    mt = self.mt[hh % 2]
                    mk = ('mt', hh % 2)
                    ac0 = self.acs_all[:, 2 * c, hg:hg + 1]
                    ac1 = self.acs_all[:, 2 * c + 1, hg:hg + 1]

                    def dec(e, dtm=dtm, pb=pb, ac0=ac0, ac1=ac1):
                        e.tensor_scalar(dtm[:, 0, :], pb[:, 0:128], ac0, 0.0, ALU.subtract, ALU.min)
                        e.tensor_scalar(dtm[:, 1, :], pb[:, 128:256], ac0, 0.0, ALU.subtract, ALU.min)
                        return e.tensor_scalar(dtm[:, 2, :], pb[:, 128:256], ac1, 0.0, ALU.subtract, ALU.min)
                    S.op('dve', [pbk, 'acs_all'], [dk], dec)
                    S.op('act', [dk], [dk], lambda e, dtm=dtm: e.activation(dtm[:].rearrange("p a b -> p (a b)"), dtm[:].rearrange("p a b -> p (a b)"), AF.Exp))
                    S.op('dve', [dk, kcb3], [mk], lambda e, dtm=dtm, mt=mt, cb3=cb3: e.tensor_tensor(mt[:], dtm[:], cb3[:], ALU.mult))
                    self.mmseq([mk, kxdt], [('PB', 1, hh)],
                               [(P5[:, hh * 64:(hh + 1) * 64], [(mt[:, 0, :], xdt[:, 0, hh * 64:(hh + 1) * 64])]),
                                (P5[:, 256 + hh * 64:256 + (hh + 1) * 64], [(mt[:, 1, :], xdt[:, 0, hh * 64:(hh + 1) * 64]), (mt[:, 2, :], xdt[:, 1, hh * 64:(hh + 1) * 64])])])
                self.rec = Bq
                self.mmseq(wkall + xkc, [('PA', 2)],
                           [(P2[:, si * 256:(si + 1) * 256], [(self.xT[:, kc, c * 256 + si * 128:c * 256 + (si + 1) * 128], wb[:, kc, 0:256]) for kc in range(8)]) for si in range(2)])
                S.op('act', [('PA', 2)], ['tz'], lambda e, bcb=bcb, xs_tok=xs_tok, xw=xw, btok=btok, xdt=xdt, cb3=cb3: e.activation(self.tz[:].rearrange("p s c -> p (s c)"), P2, AF.Tanh, scale=0.5))
                S.op('dve', ['tz', ('PA', 2)], ['tz'], lambda e, bcb=bcb, xs_tok=xs_tok, xw=xw, btok=btok, xdt=xdt, cb3=cb3: e.scalar_tensor_tensor(self.zs[:].rearrange("p s c -> p (s c)"), in0=self.tz[:].rearrange("p s c -> p (s c)"), scalar=1.0, in1=P2, op0=ALU.add, op1=ALU.mult))
                p5k = [('PB', 1, hh) for hh in range(4)]
                if c > 0:
                    self.mmseq([kbcb, 'prev_b'], [('PA', 2)],
                               [(P2[:, si * 256:(si + 1) * 256], [(bcb[:, 1, si * 128:(si + 1) * 128], self.prev_b[:])]) for si in range(2)])
                if c < 7:
                    self.mm([kbt, kxw], [('PB', 0, 0)], P4[:, 0:256], [(btok[:, si, :], xw[:, si, :]) for si in range(2)])
                    if c == 0:
                        S.op('act', [('PB', 0, 0)], ['prev_f'], lambda e, bcb=bcb, xs_tok=xs_tok, xw=xw, btok=btok, xdt=xdt, cb3=cb3: e.activation(self.prev_f[:], P4[:, 0:256], AF.Copy))
                    else:
                        S.op('dve', ['prev_f', 'cdb'], ['prev_f'], lambda e, bcb=bcb, xs_tok=xs_tok, xw=xw, btok=btok, xdt=xdt, cb3=cb3, c=c, hs=hs: e.tensor_tensor(
                            self.prev_f[:].rearrange("p (h d) -> p h d", h=4), self.prev_f[:].rearrange("p (h d) -> p h d", h=4),
                            self.cdb[:, c, hs].unsqueeze(2).broadcast_to([128, 4, 64]), ALU.mult))
                        S.op('dve', ['prev_f', ('PB', 0, 0)], ['prev_f'], lambda e, bcb=bcb, xs_tok=xs_tok, xw=xw, btok=btok, xdt=xdt, cb3=cb3: e.tensor_tensor(self.prev_f[:], self.prev_f[:], P4[:, 0:256], ALU.add))
                    S.op('act', ['prev_f'], ['prev_b'], lambda e, bcb=bcb, xs_tok=xs_tok, xw=xw, btok=btok, xdt=xdt, cb3=cb3: e.activation(self.prev_b[:], self.prev_f[:], AF.Copy))
                S.op('pool', [kxs, 'smallb'], ['yv'], lambda e, bcb=bcb, xs_tok=xs_tok, xw=xw, btok=btok, xdt=xdt, cb3=cb3, hs=hs: e.tensor_tensor(
                    self.yv[:].rearrange("p s (h d) -> p s h d", h=4), xs_tok[:].rearrange("p s (h d) -> p s h d", h=4),
                    dsk[:, hs].unsqueeze(1).unsqueeze(3).broadcast_to([128, 2, 4, 64]), ALU.mult))
                S.op('dve', p5k + ['yv'], ['yv'], lambda e, bcb=bcb, xs_tok=xs_tok, xw=xw, btok=btok, xdt=xdt, cb3=cb3: e.tensor_tensor(self.yv[:].rearrange("p s c -> p (s c)"), P5, self.yv[:].rearrange("p s c -> p (s c)"), ALU.add))
                if c > 0:
                    def yoff(e, c=c, g=g):
                        ins = None
                        for si in range(2):
                            for hh in range(4):
                                o = self.yv[:, si, hh * 64:(hh + 1) * 64]
                                ins = e.scalar_tensor_tensor(o, in0=P2[:, si * 256 + hh * 64:si * 256 + (hh + 1) * 64],
                                                             scalar=self.ea_all[:, 2 * c + si, 4 * g + hh:4 * g + hh + 1], in1=o, op0=ALU.mult, op1=ALU.add)
                        return ins
                    S.op('dve', [('PA', 2), 'ea_all', 'yv'], ['yv'], yoff)
                S.op('dve', ['yv', 'tz'], ['yv'], lambda e, bcb=bcb, xs_tok=xs_tok, xw=xw, btok=btok, xdt=xdt, cb3=cb3: e.tensor_tensor(self.h2[:], self.yv[:], self.zs[:], ALU.mult))
                for si in range(2):
                    S.op('act', ['yv'], ['junk', ('ss', si)], lambda e, bcb=bcb, xs_tok=xs_tok, xw=xw, btok=btok, xdt=xdt, cb3=cb3, si=si: e.activation(self.junk[:], self.h2[:, si, :], AF.Square, accum_out=self.ss[:, si:si + 1]))
                S.op('dve', [('ss', 0), ('ss', 1)], ['rstd'], lambda e, bcb=bcb, xs_tok=xs_tok, xw=xw, btok=btok, xdt=xdt, cb3=cb3: e.tensor_scalar(self.rstd[:], self.ss[:], 1.0 / 256, 4 * RMS_EPS, ALU.mult, ALU.add))
                S.op('pool', ['rstd', 'nh'], ['rstd'], lambda e, bcb=bcb, xs_tok=xs_tok, xw=xw, btok=btok, xdt=xdt, cb3=cb3: e.tensor_tensor(self.rstd[:], self.rstd[:], self.nh[:, 0:2], ALU.pow))
                for si in range(2):
                    S.op('dve', ['yv', 'rstd', nk], ['yv'], lambda e, bcb=bcb, xs_tok=xs_tok, xw=xw, btok=btok, xdt=xdt, cb3=cb3, si=si, nw=nw: e.scalar_tensor_tensor(
                        self.yn[:, si, :], in0=self.h2[:, si, :], scalar=self.rstd[:, si:si + 1], in1=nw[:], op0=ALU.mult, op1=ALU.mult))
                self.tps(['yv', 'cst'], ['PD'],
                         [(P7[:, j * 256 + si * 128:j * 256 + (si + 1) * 128], self.yn[:, si, j * 128:(j + 1) * 128]) for j in range(2) for si in range(2)], self.ident_f)
                S.op('act', ['PD'], [('yT', g, c)], lambda e, bcb=bcb, xs_tok=xs_tok, xw=xw, btok=btok, xdt=xdt, cb3=cb3, g=g, cols=cols: e.activation(self.yT[:, 2 * g:2 * g + 2, cols], P7.rearrange("p (j t) -> p j t", j=2), AF.Copy))
                self.rec = None
                X = F2 + Bq
                if prevB is None:
                    for t in F1:
                        t()
                else:
                    n1, n2 = len(F1), len(prevB)
                    i1 = i2 = 0
                    while i1 < n1 or i2 < n2:
                        if i2 >= n2 or (i1 < n1 and i1 * n2 <= i2 * n1):
                            F1[i1]()
                            i1 += 1
                        else:
                            prevB[i2]()
                            i2 += 1
                prevB = X
            for t in prevB:
                t()
        if 'yT' in self.dbg:
            if True:
                self.dbgf = self.sb("dbgf", [128, 512])
            for kc in range(16):
                for q4 in range(4):
                    S.op('dve', [('yT', g, c) for g in range(8) for c in range(8)] + [('dbg', 'yT')], ['dbgf'], lambda e, kc=kc, q4=q4: e.tensor_copy(self.dbgf[:, 0:512], self.yT[:, kc, q4 * 512:(q4 + 1) * 512]))
                    S.dma('sp', ['dbgf'], [('dbg', 'yT')], lambda e, kc=kc, q4=q4: e.dma_start(out=self.dbg['yT'][kc * 128:(kc + 1) * 128, q4 * 512:(q4 + 1) * 512], in_=self.dbgf[:, 0:512]))
            self.final_keys.append(('dbg', 'yT'))

    def merge(self, l):
        self.psum_epoch()
        S, sb = self.S, self.sb
        self.phase([self.R_W])
        if True:
            self.wc1_b = [sb("wc1b%d" % i, [128, 5120], BF16) for i in range(2)]
            self.ta = [sb("ta%d" % i, [128, 512]) for i in range(2)]
            self.tb = [sb("tb%d" % i, [128, 512]) for i in range(2)]
        mT = self.B3
        xk = self.xT_keys()
        attk = [('attT', hp, h) for hp in range(8) for h in range(2)]
        yk = [('yT', g, c) for g in range(8) for c in range(8)]
        it = 0
        def load_c(fc):
            wb = self.wc1_b[fc % 2]
            S.dma('pool', [], [('wc1', fc % 2, 0)], lambda e, wb=wb, fc=fc: e.dma_start(
                out=wb[:].rearrange("p (a b) -> p a b", a=5), in_=self.wc1[l, fc].rearrange("p (a b) -> p a b", a=5), max_dma_last_dim=4096))
        load_c(0)
        for fc in range(8):
            wb = self.wc1_b[fc % 2]
            wks = [('wc1', fc % 2, 0)]
            if fc + 1 < 8:
                load_c(fc + 1)
            wg = wb[:, 0:2048].rearrange("p (k n) -> p k n", k=8)
            wa = wb[:, 2048:3072].rearrange("p (k n) -> p k n", k=8)
            ws = wb[:, 3072:5120].rearrange("p (k n) -> p k n", k=16)
            for tc in range(4):
                cols = slice(tc * 512, (tc + 1) * 512)
                if it % 2 == 0:
                    pga, pgs, ppa, pps = (self.PA[:, i * 512:(i + 1) * 512] for i in range(4))
                    bk = [('PA', 0), ('PA', 1), ('PA', 2), ('PA', 3)]
                else:
                    pga, pgs, ppa, pps = self.PB[:, 0:512], self.PB[:, 512:1024], self.PC[:, 0:512], self.PD[:, 0:512]
                    bk = [('PB', 0), ('PB', 1), 'PC', 'PD']
                self.mm(wks + xk[tc * 4:(tc + 1) * 4], [bk[0]], pga, [(wg[:, kc, 0:128], self.xT[:, kc, cols]) for kc in range(8)])
                self.mm(wks + xk[tc * 4:(tc + 1) * 4], [bk[1]], pgs, [(wg[:, kc, 128:256], self.xT[:, kc, cols]) for kc in range(8)])
                self.mm(wks + attk, [bk[2]], ppa, [(wa[:, kc, :], self.B1[:, kc, cols]) for kc in range(8)])
                self.mm(wks + yk, [bk[3]], pps, [(ws[:, kc, :], self.yT[:, kc, cols]) for kc in range(16)])
                ta, tb = self.ta[it % 2], self.tb[it % 2]
                tak, tbk = ('ta', it % 2), ('tb', it % 2)
                it += 1
                S.op('act', [bk[0]], [tak], lambda e, ta=ta, pga=pga: e.activation(ta[:], pga, AF.Tanh, scale=0.5))
                S.op('act', [bk[1]], [tbk], lambda e, tb=tb, pgs=pgs: e.activation(tb[:], pgs, AF.Tanh, scale=0.5))
                S.op('dve', [tak, bk[2]], [tak], lambda e, ta=ta, ppa=ppa: e.scalar_tensor_tensor(ta[:], in0=ta[:], scalar=1.0, in1=ppa, op0=ALU.add, op1=ALU.mult))
                S.op('dve', [tbk, bk[3]], [tbk], lambda e, tb=tb, pps=pps: e.scalar_tensor_tensor(tb[:], in0=tb[:], scalar=1.0, in1=pps, op0=ALU.add, op1=ALU.mult))
                S.op('pool', [tak, tbk], [('mergedT', fc, tc)], lambda e, ta=ta, tb=tb, fc=fc, cols=cols: e.tensor_tensor(mT[:, fc, cols], ta[:], tb[:], ALU.add))

    def layernorm(self, t, tk, g_b, b_b, gk, i):
        S = self.S
        st = self.ln_st[i % 2]
        mv = self.ln_mv[i % 2]
        sk = ('lnst', i % 2)

        def stats(e):
            e.bn_stats(st[:, 0, :], t[:, 0:512])
            return e.bn_stats(st[:, 1, :], t[:, 512:1024])
        S.op('dve', [tk], [sk], stats)
        S.op('dve', [sk], [sk], lambda e: e.bn_aggr(mv[:, 0:2], st[:].rearrange("p a b -> p (a b)")))
        S.op('dve', [sk], [sk], lambda e: e.tensor_scalar(mv[:, 2:3], mv[:, 1:2], LN_EPS, None, ALU.add))
        S.op('pool', [sk, 'nh'], [sk], lambda e: e.tensor_tensor(mv[:, 3:4], mv[:, 2:3], self.nh[:, 0:1], ALU.pow))
        S.op('dve', [tk, sk], [tk], lambda e: e.tensor_scalar(t[:], t[:], mv[:, 0:1], mv[:, 3:4], ALU.subtract, ALU.mult))
        S.op('pool', [tk, gk], [tk], lambda e: e.tensor_tensor(t[:], t[:], g_b, ALU.mult))
        S.op('dve', [tk, gk], [tk], lambda e: e.tensor_tensor(t[:], t[:], b_b, ALU.add))

    def mix_ln1(self, l):
        self.psum_epoch()
        S, sb = self.S, self.sb
        self.phase([self.R_W, self.R_B2])
        if True:
            self.wout_b = sb("woutb", [128, 8, 1024], BF16)
            self.lnpb = sb("lnpb", [128, 4096])
            self.res = [sb("res%d" % i, [128, D]) for i in range(2)]
            self.ln_st = [sb("lnst%d" % i, [128, 2, 6]) for i in range(2)]
            self.ln_mv = [sb("lnmv%d" % i, [128, 4]) for i in range(2)]
        S.dma('pool', [], [('wout', 0)], lambda e: e.dma_start(
            out=self.wout_b[:], in_=self.wout[l].rearrange("p (k n) -> p k n", k=8), max_dma_last_dim=4096))
        lnpb1 = self.lnpb
        S.dma('sp', [], ['lnpb'], lambda e: e.dma_start(out=lnpb1[:], in_=self.lnp[l].broadcast_to([128, 4096])))
        src = self.x_in if l == 0 else self.x1
        srck = 'x_in' if l == 0 else 'x1'
        mk = [('mergedT', fc, tc) for fc in range(8) for tc in range(4)]
        wk = [('wout', 0)]
        for tt in range(16):
            r = self.res[tt % 2]
            rk = ('res', tt % 2)
            S.dma('sp', [(srck, tt)], [rk], lambda e, r=r, tt=tt: e.dma_start(out=r[:], in_=src[tt * 128:(tt + 1) * 128, :]))
            S.op('act', [rk], [rk], lambda e, r=r: e.activation(r[:], r[:], AF.Copy, scale=float(ALPHA)))
            if tt % 2 == 0:
                pm = self.PB[:, 0:1024]
                rkeys = [('PB', 0), ('PB', 1)]
            else:
                pm = self.PA[:, 0:1024]
                rkeys = [('PA', 0), ('PA', 1)]
            self.mmseq(mk + wk, rkeys,
                       [(pm[:, half * 512:(half + 1) * 512], [(self.B3[:, kc, tt * 128:(tt + 1) * 128], self.wout_b[:, kc, half * 512:(half + 1) * 512]) for kc in range(8)]) for half in range(2)])
            S.op('dve', rkeys + [rk], [rk], lambda e, r=r, pm=pm: e.scalar_tensor_tensor(r[:], in0=pm, scalar=0.5, in1=r[:], op0=ALU.mult, op1=ALU.add))
            self.layernorm(r, rk, self.lnpb[:, 0:1024], self.lnpb[:, 1024:2048], 'lnpb', tt)
            S.dma('sp', [rk], [('hres', tt)], lambda e, r=r, tt=tt: e.dma_start(out=self.hres[tt * 128:(tt + 1) * 128, :], in_=r[:]))
            self.to_featmajor2(r, rk, self.B1, 'hT', tt)
        if 'hT' in self.dbg:
            if True:
                self.dbgf = self.sb("dbgf", [128, 512])
            for kc in range(8):
                for q4 in range(4):
                    S.op('dve', [('hT', tt) for tt in range(16)] + [('dbg', 'hT')], ['dbgf'], lambda e, kc=kc, q4=q4: e.tensor_copy(self.dbgf[:, 0:512], self.B1[:, kc, q4 * 512:(q4 + 1) * 512]))
                    S.dma('sp', ['dbgf'], [('dbg', 'hT')], lambda e, kc=kc, q4=q4: e.dma_start(out=self.dbg['hT'][kc * 128:(kc + 1) * 128, q4 * 512:(q4 + 1) * 512], in_=self.dbgf[:, 0:512]))
            self.final_keys.append(('dbg', 'hT'))

    def to_featmajor2(self, tile, tkey, dstT, dkey, tt):
        S = self.S
        for half in range(2):
            ps = self.PC[:, 0:512] if half == 0 else self.PD[:, 0:512]
            pk = 'PC' if half == 0 else 'PD'
            self.tps([tkey, 'cst'], [pk], [(ps[:, j * 128:(j + 1) * 128], tile[:, (half * 4 + j) * 128:(half * 4 + j + 1) * 128]) for j in range(4)], self.ident_f)
            S.op('act', [pk], [(dkey, tt)], lambda e, ps=ps, half=half: e.activation(
                dstT[:, half * 4:(half + 1) * 4, tt * 128:(tt + 1) * 128], ps.rearrange("p (c t) -> p c t", c=4), AF.Copy))

    def ffn(self, l):
        self.psum_epoch()
        S, sb = self.S, self.sb
        self.phase([self.R_W, (self.R_B3[0] + 4096, self.R_B3[1])])
        if True:
            self.wup_b = [sb("wupb%d" % i, [128, 8, 512], BF16) for i in range(2)]
            self.wdn_b = [sb("wdnb%d" % i, [128, 4, 1024], BF16) for i in range(2)]
            self.rl = [sb("rl%d" % i, [128, 512]) for i in range(2)]
            self.ln_st = [sb("lnst%d" % i, [128, 2, 6]) for i in range(2)]
            self.ln_mv = [sb("lnmv%d" % i, [128, 4]) for i in range(2)]
        uT = self.B3
        hk = [('hT', tt) for tt in range(16)]
        it = 0
        def load_f(gi):
            wu = self.wup_b[gi % 2]
            wd = self.wdn_b[gi % 2]
            wuk, wdk = ('wup', gi % 2), ('wdn', gi % 2)
            S.dma('pool', [], [(wuk, 0)], lambda e, wu=wu, gi=gi: e.dma_start(
                out=wu[:], in_=self.wup[l, gi].rearrange("p (k n) -> p k n", k=8), max_dma_last_dim=4096))
            S.dma('pool', [], [(wdk, 0)], lambda e, wd=wd, gi=gi: e.dma_start(
                out=wd[:], in_=self.wdn[l, gi].rearrange("p (k n) -> p k n", k=4), max_dma_last_dim=4096))
        load_f(0)
        for gi in range(8):
            wu = self.wup_b[gi % 2]
            wd = self.wdn_b[gi % 2]
            wuk, wdk = ('wup', gi % 2), ('wdn', gi % 2)
            if gi + 1 < 8:
                load_f(gi + 1)
            wuks = [(wuk, 0)]
            wdks = [(wdk, 0)]
            for j in range(4):
                for tc in range(4):
                    bank = it % 4
                    ps = [self.PA[:, 0:512], self.PA[:, 512:1024], self.PC[:, 0:512], self.PD[:, 0:512]][bank]
                    pk = [('PA', 0), ('PA', 1), 'PC', 'PD'][bank]
                    cols = slice(tc * 512, (tc + 1) * 512)
                    self.mm(wuks + hk[tc * 4:(tc + 1) * 4], [pk], ps, [(wu[:, kc, j * 128:(j + 1) * 128], self.B1[:, kc, cols]) for kc in range(8)])
                    rl = self.rl[it % 2]
                    rlk = ('rl', it % 2)
                    S.op('act', [pk], [rlk], lambda e, rl=rl, ps=ps: e.activation(rl[:], ps, AF.Relu))
                    eng = 'dve'
                    S.op(eng, [rlk], [('uT', j)], lambda e, rl=rl, j=j, cols=cols: e.tensor_tensor(uT[:, j, cols], rl[:], rl[:], ALU.mult))
                    it += 1
            for tt in range(16):
                if tt % 2 == 0:
                    pm = self.PB[:, 0:1024]
                    rkeys = [('PB', 0), ('PB', 1)]
                else:
                    pm = self.PA[:, 1024:2048]
                    rkeys = [('PA', 2), ('PA', 3)]
                self.mmseq([('uT', j) for j in range(4)] + wdks, rkeys,
                           [(pm[:, half * 512:(half + 1) * 512], [(uT[:, j, tt * 128:(tt + 1) * 128], wd[:, j, half * 512:(half + 1) * 512]) for j in range(4)]) for half in range(2)])
                if gi == 0:
                    S.op('act', rkeys, [('acc', tt)], lambda e, tt=tt, pm=pm: e.activation(self.acc[:, tt, :], pm, AF.Copy))
                else:
                    S.op('dve', rkeys + [('acc', tt)], [('acc', tt)], lambda e, tt=tt, pm=pm: e.tensor_tensor(self.acc[:, tt, :], pm, self.acc[:, tt, :], ALU.add))
        S.barrier()
        self.res = [self.wup_b[i][:, 0:4, :].rearrange("p a b -> p (a b)").bitcast(F32) for i in range(2)]
        self.lnpb = self.wdn_b[0][:].rearrange("p a b -> p (a b)").bitcast(F32)
        lnpb2 = self.lnpb
        S.dma('sp', [], ['lnpb'], lambda e: e.dma_start(out=lnpb2, in_=self.lnp[l][:, 2048:4096].broadcast_to([128, 2048])))
        dst = self.out if l == DEPTH - 1 else self.x1
        dstk = 'out' if l == DEPTH - 1 else 'x1'
        for tt in range(16):
            r = self.res[tt % 2]
            rk = ('res', tt % 2)
            S.dma('sp', [('hres', tt)], [rk], lambda e, r=r, tt=tt: e.dma_start(out=r[:], in_=self.hres[tt * 128:(tt + 1) * 128, :]))
            S.op('dve', [rk, ('acc', tt)], [rk], lambda e, r=r, tt=tt: e.scalar_tensor_tensor(r[:], in0=r[:], scalar=float(ALPHA), in1=self.acc[:, tt, :], op0=ALU.mult, op1=ALU.add))
            self.layernorm(r, rk, self.lnpb[:, 0:1024], self.lnpb[:, 1024:2048], 'lnpb', tt)
            S.dma('sp', [rk], [(dstk, tt)], lambda e, r=r, tt=tt: e.dma_start(out=dst[tt * 128:(tt + 1) * 128, :], in_=r[:]))
            if l == DEPTH - 1:
                self.final_keys.append((dstk, tt))
            else:
                self.to_featmajor2(r, rk, self.xT, 'xT', tt)


def sel_max(e, b):
    ins = None
    for t in range(8):
        ins = e.max(b.m8[:, t, :], b.gm[:, t * 8:(t + 1) * 8])
    return ins


def sel_lt(e, b):
    ins = None
    for t in range(8):
        ins = e.tensor_scalar(b.lt[:, t * 8:(t + 1) * 8], b.gm[:, t * 8:(t + 1) * 8], b.m8[:, t, 2:3], NEG, ALU.is_lt, ALU.mult)
    return ins


_CACHE = {}


def kernel(**inputs):
    inp = {k: np.asarray(v) for k, v in inputs.items()}
    w = prep_weights(inp)
    consts, e8 = make_consts()
    x = np.ascontiguousarray(inp['x'], dtype=np.float32)
    pos = np.ascontiguousarray(inp['positions'], dtype=np.int32)
    nc = Builder().build()
    in_maps = []
    for b in range(NCORES):
        m = {"x": x[b], "pos": pos[b:b + 1], "consts": consts, "e8": e8}
        m.update(w)
        in_maps.append(m)
    res = run_bass_kernel_spmd(nc, in_maps, core_ids=list(range(NCORES)))
    out = np.stack([np.asarray(r["out"], dtype=np.float32) for r in res.results], axis=0)
    return out
```

```python
import math
import numpy as np
import concourse.bass as bass
import concourse.mybir as mybir
from concourse.bass_utils import run_bass_kernel_spmd

F32 = mybir.dt.float32
BF16 = mybir.dt.bfloat16
I32 = mybir.dt.int32
AF = mybir.ActivationFunctionType
ALU = mybir.AluOpType
AX = mybir.AxisListType

D = 1024
SEQ = 2048
DEPTH = 2
NCORES = 8
IN_W = 11296
ALPHA = (2 * DEPTH) ** 0.25
LN_EPS = 1e-5
RMS_EPS = 1e-5
NEG = -1.0e5
PI = math.pi

DEBUG = {}
STOP_AFTER = None
NO_BARRIER = False
VARIANT = 0


class Sched:
    ENG = ('pe', 'act', 'dve', 'pool', 'sp')

    def __init__(self, nc, n_dma_sems=24):
        self.nc = nc
        self.sem = {e: nc.alloc_semaphore('s_' + e) for e in self.ENG}
        self.cnt = {e: 0 for e in self.ENG}
        self.dsem = [nc.alloc_semaphore('d%d' % i) for i in range(n_dma_sems)]
        self.dcnt = [0] * n_dma_sems
        self.drr = {'sp': 0, 'pool': 0}
        self.dpool = {'sp': list(range(0, n_dma_sems // 2)), 'pool': list(range(n_dma_sems // 2, n_dma_sems))}
        self.waited = {}
        self.last_w = {}
        self.readers = {}
        self.q = {e: [] for e in self.ENG}

    def barrier(self):
        if NO_BARRIER is True:
            return
        self.epoch = {}
        scr = self.bar_scratch
        d_sp = {('dma', i): self.dcnt[i] for i in self.dpool['sp'] if self.dcnt[i] > 0}
        d_pl = {('dma', i): self.dcnt[i] for i in self.dpool['pool'] if self.dcnt[i] > 0}
        self._token('act', d_sp, lambda e: e.memzero(scr[:, 0:1]))
        t1 = ('act', self.cnt['act'])
        d_pl[t1[0]] = t1[1]
        self._token('pool', d_pl, lambda e: e.memset(scr[:, 1:2], 0.0))
        d3 = {e: c for e, c in self.cnt.items() if c > 0}
        self._token('dve', d3, lambda e: e.memset(scr[:, 2:3], 0.0))
        self.epoch = {'dve': self.cnt['dve']}

    def _token(self, e, deps, emit):
        waits = self._waits(e, deps)
        self.cnt[e] += 1
        sem = self.sem[e]

        def thunk(eng):
            for s, v in waits:
                eng.wait_ge(s, v)
            emit(eng).then_inc(sem, 1)
        self.q[e].append(thunk)

    def _deps(self, reads, writes):
        deps = dict(getattr(self, 'epoch', {}))

        def add(src, idx):
            if deps.get(src, 0) < idx:
                deps[src] = idx
        for k in reads:
            lw = self.last_w.get(k)
            if lw is not None:
                add(*lw)
        for k in writes:
            lw = self.last_w.get(k)
            if lw is not None:
                add(*lw)
            for src, idx in self.readers.get(k, {}).items():
                add(src, idx)
        return deps

    def _waits(self, e, deps):
        out = []
        self.maxw = getattr(self, 'maxw', {})
        for src, idx in deps.items():
            if self.waited.get((e, src), 0) >= idx:
                continue
            self.waited[(e, src)] = idx
            if isinstance(src, tuple):
                out.append((self.dsem[src[1]], 16 * idx))
            else:
                out.append((self.sem[src], idx))
        self.maxw[len(out)] = self.maxw.get(len(out), 0) + 1
        return out

    def _record(self, token, reads, writes):
        src, idx = token
        for k in reads:
            r = self.readers.setdefault(k, {})
            if r.get(src, 0) < idx:
                r[src] = idx
        for k in writes:
            self.last_w[k] = (src, idx)
            self.readers[k] = {}

    @staticmethod
    def _bank(k):
        if isinstance(k, tuple) and k and k[0] == 'PA':
            return ('BANK', k[1])
        if isinstance(k, tuple) and k and k[0] == 'PB':
            return ('BANK', 4 + k[1])
        if k in ('PC', 'PCg', 'PCo'):
            return ('BANK', 6)
        if k == 'PD':
            return ('BANK', 7)
        return None

    def _canon(self, reads, writes):
        r2, w2 = [], []
        for k in reads:
            b = self._bank(k)
            if b is None:
                r2.append(k)
            elif b not in w2:
                w2.append(b)
        for k in writes:
            b = self._bank(k)
            if b is None:
                w2.append(k)
            elif b not in w2:
                w2.append(b)
        return r2, w2

    def op(self, e, reads, writes, emit):
        reads, writes = self._canon(reads, writes)
        deps = self._deps(reads, writes)
        waits = self._waits(e, deps)
        self.cnt[e] += 1
        idx = self.cnt[e]
        sem = self.sem[e]

        def thunk(eng):
            for s, v in waits:
                eng.wait_ge(s, v)
            emit(eng).then_inc(sem, 1)
        self.q[e].append(thunk)
        self._record((e, idx), reads, writes)

    def dma(self, e, reads, writes, emit):
        pool = self.dpool[e]
        s = pool[self.drr[e] % len(pool)]
        self.drr[e] += 1
        src = ('dma', s)
        deps = self._deps(reads, writes)
        if self.dcnt[s] > 0 and deps.get(src, 0) < self.dcnt[s]:
            deps[src] = self.dcnt[s]
        waits = self._waits(e, deps)
        self.dcnt[s] += 1
        idx = self.dcnt[s]
        sem = self.dsem[s]

        def thunk(eng):
            for sm, v in waits:
                eng.wait_ge(sm, v)
            emit(eng).then_inc(sem, 16)
        self.q[e].append(thunk)
        self._record((src, idx), reads, writes)

    def alias(self, old_keys, new_keys):
        acc = {}
        for k in old_keys:
            lw = self.last_w.get(k)
            if lw is not None and acc.get(lw[0], 0) < lw[1]:
                acc[lw[0]] = lw[1]
            for src, idx in self.readers.get(k, {}).items():
                if acc.get(src, 0) < idx:
                    acc[src] = idx
        for k in new_keys:
            r = self.readers.setdefault(k, {})
            for src, idx in acc.items():
                if r.get(src, 0) < idx:
                    r[src] = idx

    def finish(self, final_keys):
        self.barrier()
        deps = self._deps(final_keys, [])
        waits = self._waits('sp', deps)

        def thunk(eng):
            for s, v in waits:
                eng.wait_ge(s, v)
        self.q['sp'].append(thunk)
        nc = self.nc
        q = self.q
        with nc.Block() as block:
            @block.sync
            def _(eng):
                for t in q['sp']:
                    t(eng)

            @block.tensor
            def _(eng):
                for t in q['pe']:
                    t(eng)

            @block.scalar
            def _(eng):
                for t in q['act']:
                    t(eng)

            @block.vector
            def _(eng):
                for t in q['dve']:
                    t(eng)

            @block.gpsimd
            def _(eng):
                for t in q['pool']:
                    t(eng)


C_ID, C_TRI, C_ONES, C_RM, C_INVF, C_SGN, C_NEGP, C_FLOOR, NC_CONST = 0, 128, 256, 384, 512, 513, 514, 578, 648


def make_consts():
    c = np.zeros((128, NC_CONST), np.float32)
    c[:, C_ID:C_ID + 128] = np.eye(128, dtype=np.float32)
    c[:, C_TRI:C_TRI + 128] = np.triu(np.ones((128, 128), np.float32))
    c[:, C_ONES:C_ONES + 128] = 1.0
    rm = np.zeros((128, 128), np.float32)
    inv = (500000.0 ** (-np.arange(0, 16, 2, dtype=np.float32) / 16)).astype(np.float32)
    for blk in range(2):
        for d in range(16):
            src = d + 8 if d < 8 else d - 8
            rm[blk * 64 + src, blk * 64 + d] = 1.0
    c[:, C_RM:C_RM + 128] = rm
    for p in range(128):
        d = p % 64
        if d < 16:
            c[p, C_INVF] = inv[d % 8]
            c[p, C_SGN] = -1.0 if d < 8 else 1.0
    for qt in range(8):
        blk = (8 + qt) // 2
        for j in range(8):
            c[:, C_NEGP + qt * 8 + j] = 0.0 if j < blk else -1.0e30
            c[:, C_FLOOR + qt * 8 + j] = NEG if j < blk else 0.0
    e8 = np.zeros((8, SEQ), np.float32)
    for j in range(8):
        e8[j, j * 256:(j + 1) * 256] = 1.0
    return c, e8


def prep_weights(inp):
    out = {}
    L = DEPTH

    def kmaj(w):
        K, N = w.shape
        return w.reshape(K // 128, 128, N).transpose(1, 0, 2)

    w_in = inp['w_in']
    wqkv = np.empty((L, 8, 128, 8, 384), np.float32)
    wssm = np.empty((L, 8, 128, 8, 768), np.float32)
    wdt = np.empty((L, 128, 8, 32), np.float32)
    wc1 = np.empty((L, 8, 128, 5120), np.float32)
    wout = np.empty((L, 128, 8, 1024), np.float32)
    wup = np.empty((L, 8, 128, 8, 512), np.float32)
    wdn = np.empty((L, 8, 128, 4, 1024), np.float32)
    cw = np.empty((L, 128, 8, 4, 5), np.float32)
    for l in range(L):
        wi = kmaj(w_in[l])
        for hp in range(8):
            wqkv[l, hp, :, :, 0:128] = wi[:, :, hp * 128:(hp + 1) * 128]
            wqkv[l, hp, :, :, 128:256] = wi[:, :, 1024 + hp * 128:1024 + (hp + 1) * 128]
            wqkv[l, hp, :, :, 256:384] = wi[:, :, 2048 + hp * 128:2048 + (hp + 1) * 128]
        for g in range(8):
            wssm[l, g, :, :, 0:256] = wi[:, :, 3072 + g * 256:3072 + (g + 1) * 256]
            wssm[l, g, :, :, 256:512] = wi[:, :, 5120 + g * 256:5120 + (g + 1) * 256]
            wssm[l, g, :, :, 512:640] = wi[:, :, 7168 + g * 128:7168 + (g + 1) * 128]
            wssm[l, g, :, :, 640:768] = wi[:, :, 8192 + g * 128:8192 + (g + 1) * 128]
        wdt[l] = wi[:, :, 9216:9248]
        wap = kmaj(inp['w_attn_proj'][l])
        wsp = kmaj(inp['w_ssm_proj'][l])
        for fc in range(8):
            blk = np.empty((128, 8, 256), np.float32)
            blk[:, :, 0:128] = wi[:, :, 9248 + fc * 128:9248 + (fc + 1) * 128]
            blk[:, :, 128:256] = wi[:, :, 10272 + fc * 128:10272 + (fc + 1) * 128]
            wc1[l, fc, :, 0:2048] = blk.reshape(128, 2048)
            wc1[l, fc, :, 2048:3072] = wap[:, :, fc * 128:(fc + 1) * 128].reshape(128, 1024)
            wc1[l, fc, :, 3072:5120] = wsp[:, :, fc * 128:(fc + 1) * 128].reshape(128, 2048)
        wout[l] = kmaj(inp['w_out'][l])
        wu = kmaj(inp['w_up'][l])
        wd = kmaj(inp['w_down'][l])
        for gI in range(8):
            wup[l, gI] = wu[:, :, gI * 512:(gI + 1) * 512]
            wdn[l, gI] = wd[:, gI * 4:(gI + 1) * 4, :]
        cwl = inp['conv_w'][l]
        cbl = inp['conv_b'][l]
        for g in range(8):
            offs = [g * 256, g * 256 + 128, 2048 + g * 128, 3072 + g * 128]
            for ci, o in enumerate(offs):
                cw[l, :, g, ci, 0:4] = cwl[:, o:o + 128].T
                cw[l, :, g, ci, 4] = cbl[o:o + 128]
    out['wqkv'] = wqkv.reshape(L, 8, 128, 3072)
    out['wssm'] = wssm.reshape(L, 8, 128, 6144)
    out['wdt'] = wdt.reshape(L, 128, 256)
    out['wc1'] = wc1
    out['wout'] = wout.reshape(L, 128, 8192)
    out['wup'] = wup.reshape(L, 8, 128, 4096)
    out['wdn'] = wdn.reshape(L, 8, 128, 4096)
    out['cw'] = cw.reshape(L, 128, 160)
    small = np.concatenate([inp['dt_bias'], inp['a_log'], inp['d_skip']], axis=1)
    out['small'] = np.ascontiguousarray(small.reshape(L, 1, 96))
    out['normw'] = np.ascontiguousarray(inp['ssm_norm_w'].reshape(L, 1, 2048))
    lnp = np.stack([inp['ln1_g'], inp['ln1_b'], inp['ln2_g'], inp['ln2_b']], axis=1)
    out['lnp'] = np.ascontiguousarray(lnp.reshape(L, 1, 4096))
    return {k: np.ascontiguousarray(v, dtype=np.float32) for k, v in out.items()}


class Arena:
    def __init__(self, base_ap, segments):
        self.base = base_ap
        self.segs = [list(x) for x in segments]

    def alloc(self, shape, dtype=F32):
        P = shape[0]
        n = 1
        for d in shape[1:]:
            n *= d
        per = 4 if dtype in (F32, I32) else 2
        ncols = (n * per + 3) // 4
        ncols = (ncols + 7) // 8 * 8
        for sg in self.segs:
            if sg[1] - sg[0] >= ncols:
                off = sg[0]
                sg[0] += ncols
                ap = self.base[0:P, off:off + (n * per + 3) // 4]
                if dtype != F32:
                    ap = ap.bitcast(dtype)
                if len(shape) == 3:
                    ap = ap.rearrange("p (a b) -> p a b", a=shape[1])
                elif len(shape) == 4:
                    ap = ap.rearrange("p (a b c) -> p a b c", a=shape[1], b=shape[2])
                return ap
        raise RuntimeError("arena out of memory for %s" % (shape,))


class _Stop(Exception):
    pass


class Builder:
    def __init__(self):
        nc = bass.Bass("TRN2", target_bir_lowering=False)
        self.nc = nc
        self.S = Sched(nc)
        self.rec = None
        S_ = self.S
        S_._op, S_._dma = S_.op, S_.dma

        def _rop(*a):
            if self.rec is not None:
                self.rec.append(lambda: S_._op(*a))
            else:
                S_._op(*a)

        def _rdma(*a):
            if self.rec is not None:
                self.rec.append(lambda: S_._dma(*a))
            else:
                S_._dma(*a)
        S_.op, S_.dma = _rop, _rdma
        L = DEPTH
        dt = lambda n, s, d=F32, k="ExternalInput": nc.dram_tensor(n, s, d, kind=k).ap()
        self.x_in = dt("x", [SEQ, D])
        self.pos_in = dt("pos", [1, SEQ], I32)
        self.consts_in = dt("consts", [128, NC_CONST])
        self.e8_in = dt("e8", [8, SEQ])
        self.wqkv = dt("wqkv", [L, 8, 128, 3072])
        self.wssm = dt("wssm", [L, 8, 128, 6144])
        self.wdt = dt("wdt", [L, 128, 256])
        self.wc1 = dt("wc1", [L, 8, 128, 5120])
        self.wout = dt("wout", [L, 128, 8192])
        self.wup = dt("wup", [L, 8, 128, 4096])
        self.wdn = dt("wdn", [L, 8, 128, 4096])
        self.cw = dt("cw", [L, 128, 160])
        self.small = dt("small", [L, 1, 96])
        self.normw = dt("normw", [L, 1, 2048])
        self.lnp = dt("lnp", [L, 1, 4096])
        self.out = dt("out", [SEQ, D], F32, "ExternalOutput")
        self.x1 = dt("x1s", [SEQ, D], F32, "Internal")
        self.hres = dt("hres", [SEQ, D], F32, "Internal")
        self.dbg = {}
        for name, shape in DEBUG.items():
            self.dbg[name] = dt("dbg_" + name, list(shape), F32, "ExternalOutput")
        self.final_keys = []

    def sb(self, name, shape, dtype=F32):
        return self.ar.alloc(list(shape), dtype)

    def phase(self, segs):
        self.S.barrier()
        self.ar = Arena(self.arena, segs)

    def mm(self, reads, writes, out, pairs, transpose=False):
        def emit(e):
            n = len(pairs)
            ins = None
            for i, (l, r) in enumerate(pairs):
                ins = e.matmul(out, lhsT=l, rhs=r, start=(i == 0), stop=(i == n - 1))
            return ins
        self.S.op('pe', reads, writes, emit)

    def mmseq(self, reads, writes, groups):
        def emit(e):
            ins = None
            for out, pairs in groups:
                n = len(pairs)
                for i, (l, r) in enumerate(pairs):
                    ins = e.matmul(out, lhsT=l, rhs=r, start=(i == 0), stop=(i == n - 1))
            return ins
        self.S.op('pe', reads, writes, emit)

    def tps(self, reads, writes, items, ident):
        def emit(e):
            ins = None
            for o, i in items:
                ins = e.transpose(o, i, ident)
            return ins
        self.S.op('pe', reads, writes, emit)

    def dump(self, name, ap, keys):
        if name in self.dbg:
            if not isinstance(keys, list):
                keys = [keys]
            self.S.dma('sp', keys, [('dbg', name)], lambda e: e.dma_start(out=self.dbg[name], in_=ap))
            self.final_keys.append(('dbg', name))

    def build(self):
        nc, S = self.nc, self.S
        sb = self.sb
        nbytes = int(nc.sbuf_bytes_remaining) - 256
        NA = nbytes // 4 // 8 * 8
        self.arena = nc.alloc_sbuf_tensor("arena", [128, NA], F32)[:]
        self.R_P = (0, 11264)
        self.R_B1 = (11264, 19456)
        self.R_B2 = (19456, 35840)
        self.R_B3 = (35840, 44032)
        self.R_W = (44032, NA)
        assert NA - 44032 > 8000, NA
        self.ar = Arena(self.arena, [self.R_P])
        self.S.bar_scratch = sb("barscr", [128, 8])
        cst = sb("cst", [128, NC_CONST])
        self.cst = cst
        S.dma('sp', [], ['cst'], lambda e: e.dma_start(out=cst[:], in_=self.consts_in))
        self.ident_f = cst[:, C_ID:C_ID + 128]
        self.tri_f = cst[:, C_TRI:C_TRI + 128]
        self.ones_f = cst[:, C_ONES:C_ONES + 128]
        cbf = sb("cbf", [128, 512], BF16)
        S.op('dve', ['cst'], ['cbf'], lambda e: e.tensor_copy(cbf[:], cst[:, 0:512]))
        self.ident_b = cbf[:, 0:128]
        self.tri_b = cbf[:, 128:256]
        self.rm_b = cbf[:, 384:512]
        self.nh = sb("nh", [128, 8])
        S.op('pool', [], ['nh'], lambda e: e.memset(self.nh[:], -0.5))
        self.ropeC = sb("ropeC", [128, SEQ], BF16)
        self.ropeS = sb("ropeS", [128, SEQ], BF16)
        self.xT = sb("xT", [128, 8, SEQ], BF16)
        A = self.arena
        self.B1 = A[:, self.R_B1[0]:self.R_B1[1]].bitcast(BF16).rearrange("p (c t) -> p c t", c=8)
        self.B2 = A[:, self.R_B2[0]:self.R_B2[1]]
        self.B3 = A[:, self.R_B3[0]:self.R_B3[1]].bitcast(BF16).rearrange("p (c t) -> p c t", c=8)
        self.yT = self.B2.bitcast(BF16).rearrange("p (c t) -> p c t", c=16)
        self.acc = self.B2.rearrange("p (t f) -> p t f", t=16)
        self.ar = Arena(self.arena, [self.R_B2])
        self.PA = nc.alloc_psum_tensor("PA", [128, 2048], F32)
        self.PB = nc.alloc_psum_tensor("PB", [128, 1024], F32)
        self.PC = nc.alloc_psum_tensor("PC", [128, 512], F32)
        self.PD = nc.alloc_psum_tensor("PD", [128, 512], F32)
        self.rope_tables()
        if STOP_AFTER == 'rope':
            return self.finish()
        self.load_xT()
        if STOP_AFTER == 'xT':
            self.dbg_tile('xT0', self.xT[:, 3, 1024:1536], self.xT_keys(), 512)
            return self.finish()
        try:
            for l in range(DEPTH):
                self.layer(l)
                if STOP_AFTER == 'layer0':
                    break
        except _Stop:
            pass
        return self.finish()

    def finish(self):
        self.S.finish(self.final_keys)
        return self.nc

    def rope_tables(self):
        S, sb = self.S, self.sb
        cst = None
        posi = sb("posi", [128, SEQ], I32)
        S.dma('sp', [], ['posi'], lambda e: e.dma_start(out=posi[:], in_=self.pos_in.broadcast_to([128, SEQ])))
        ang = sb("ang", [128, SEQ])
        tmp = sb("rtmp", [128, SEQ])
        tmi = sb("rtmi", [128, SEQ], I32)
        rc = sb("rc", [128, SEQ])
        rs = sb("rs", [128, SEQ])
        invf = self.cst[:, C_INVF:C_INVF + 1]
        sgn = self.cst[:, C_SGN:C_SGN + 1]
        S.op('dve', ['posi'], ['ang'], lambda e: e.tensor_copy(ang[:], posi[:]))
        S.op('dve', ['ang', 'cst'], ['ang'], lambda e: e.tensor_scalar(ang[:], ang[:], invf, None, ALU.mult))
        for which, dst in ((0, rs), (1, rc)):
            off = 0.0 if which == 0 else PI / 2
            S.op('dve', ['ang'], ['rtmp'], lambda e, off=off: e.tensor_scalar(tmp[:], ang[:], off, 1.0 / (2 * PI), ALU.add, ALU.mult))
            S.op('dve', ['rtmp'], ['rtmi'], lambda e: e.tensor_copy(tmi[:], tmp[:]))
            S.op('dve', ['rtmi'], ['rtmp'], lambda e: e.tensor_copy(tmp[:], tmi[:]))
            S.op('dve', ['rtmp', 'ang'], ['rtmp'], lambda e: e.scalar_tensor_tensor(tmp[:], in0=tmp[:], scalar=-2 * PI, in1=ang[:], op0=ALU.mult, op1=ALU.add))
            S.op('dve', ['rtmp'], [('rope', which)], lambda e, off=off, dst=dst: e.tensor_scalar(dst[:], tmp[:], off, None, ALU.add))
            S.op('dve', [('rope', which)], ['rtmp'], lambda e, dst=dst: e.tensor_scalar(tmp[:], dst[:], PI, -2 * PI, ALU.is_gt, ALU.mult))
            S.op('dve', ['rtmp', ('rope', which)], [('rope', which)], lambda e, dst=dst: e.tensor_tensor(dst[:], dst[:], tmp[:], ALU.add))
            S.op('dve', [('rope', which)], ['rtmp'], lambda e, dst=dst: e.tensor_scalar(tmp[:], dst[:], -PI, 2 * PI, ALU.is_lt, ALU.mult))
            S.op('dve', ['rtmp', ('rope', which)], [('rope', which)], lambda e, dst=dst: e.tensor_tensor(dst[:], dst[:], tmp[:], ALU.add))
            S.op('dve', [('rope', which)], [('rope', which)], lambda e, dst=dst: e.tensor_scalar(dst[:], dst[:], 3.1415925, -3.1415925, ALU.min, ALU.max))
            S.op('act', [('rope', which)], [('rope', which)], lambda e, dst=dst: e.activation(dst[:], dst[:], AF.Sin))
        S.op('dve', [('rope', 0), 'cst'], [('rope', 0)], lambda e: e.tensor_scalar(self.ropeS[:], rs[:], sgn, None, ALU.mult))
        S.op('dve', [('rope', 1)], [('rope', 1)], lambda e: e.tensor_copy(self.ropeC[:], rc[:]))
        self.dump('ropeC', rc[:], ('rope', 1))
        self.dump('ropeS', rs[:], ('rope', 0))

    def load_xT(self):
        S, sb = self.S, self.sb
        self.xt_buf = [sb("xtile%d" % i, [128, D]) for i in range(2)]
        for tt in range(16):
            xt = self.xt_buf[tt % 2]
            k = ('xtile', tt % 2)
            S.dma('sp', [], [k], lambda e, xt=xt, tt=tt: e.dma_start(out=xt[:], in_=self.x_in[tt * 128:(tt + 1) * 128, :]))
            self.to_featmajor(xt, k, self.xT, 'xT', tt)

    def to_featmajor(self, tile, tkey, dstT, dkey, tt):
        S = self.S
        for half in range(2):
            ps = self.PA[:, half * 512:(half + 1) * 512]
            pk = ('PA', half)
            self.tps([tkey, 'cst'], [pk], [(ps[:, j * 128:(j + 1) * 128], tile[:, (half * 4 + j) * 128:(half * 4 + j + 1) * 128]) for j in range(4)], self.ident_f)
            eng = 'act' if half == 0 else 'dve'
            if eng == 'act':
                S.op('act', [pk], [(dkey, tt)], lambda e, ps=ps, half=half: e.activation(
                    dstT[:, half * 4:(half + 1) * 4, tt * 128:(tt + 1) * 128], ps.rearrange("p (c t) -> p c t", c=4), AF.Copy))
            else:
                S.op('dve', [pk], [(dkey, tt)], lambda e, ps=ps, half=half: e.tensor_copy(
                    dstT[:, half * 4:(half + 1) * 4, tt * 128:(tt + 1) * 128], ps.rearrange("p (c t) -> p c t", c=4)))

    def layer(self, l):
        self.attention(l)
        if STOP_AFTER == 'att':
            raise _Stop()
        self.ssd(l)
        if STOP_AFTER == 'ssd':
            raise _Stop()
        self.merge(l)
        if STOP_AFTER == 'merge':
            raise _Stop()
        self.mix_ln1(l)
        if STOP_AFTER == 'ln1':
            raise _Stop()
        self.ffn(l)

    def psum_epoch(self):
        keys = [('PA', i) for i in range(4)] + [('PB', 0), ('PB', 1), ('PB', 0, 0), ('PB', 0, 1)] + \
               [('PB', 1, h) for h in range(4)] + ['PC', 'PCg', 'PCo', 'PD']
        self.S.alias(keys, keys)

    def dbg_qk(self, h):
        S = self.S
        if 'qa0' in self.dbg:
            qa, ka = self.qa[h], self.ka[h]
            dq = self.sb("dbgq", [128, SEQ])
            dk = self.sb("dbgk", [128, SEQ])
            S.op('dve', [('qa', h)], ['dbgq'], lambda e: e.tensor_copy(dq[0:72, :], qa[0:72, :]))
            S.op('dve', [('ka', h)], ['dbgk'], lambda e: e.tensor_copy(dk[0:72, :], ka[0:72, :]))
            self.dump('qa0', dq[0:72, :], 'dbgq')
            self.dump('ka0', dk[0:72, :], 'dbgk')

    def dbg_tile(self, name, ap, keys, ncols):
        if name in self.dbg:
            t = self.sb("dbg_" + name, [128, ncols])
            self.S.op('dve', keys, ['dbgt_' + name], lambda e: e.tensor_copy(t[:], ap))
            self.dump(name, t[:], 'dbgt_' + name)

    def xT_keys(self):
        return [('xT', tt) for tt in range(16)]

    def attention(self, l):
        self.psum_epoch()
        S, sb, nc = self.S, self.sb, self.nc
        self.phase([self.R_W, self.R_B2, (self.R_B3[0] + 7168, self.R_B3[1])])
        if True:
            self.wqkv_b = [sb("wqkv%d" % i, [128, 8, 384], BF16) for i in range(2)]
            self.qa = [sb("qa%d" % i, [72, SEQ], BF16) for i in range(4)]
            self.ka = [sb("ka%d" % i, [72, SEQ], BF16) for i in range(4)]
            self.va = [sb("va%d" % i, [128, 16, 65], BF16) for i in range(4)]
            self.qbf = [sb("qbf%d" % i, [128, 512], BF16) for i in range(2)]
            self.t1 = [sb("t1_%d" % i, [128, 512]) for i in range(2)]
            self.t2 = [sb("t2_%d" % i, [128, 512]) for i in range(2)]
            self.atm = [sb("atm%d" % i, [128, 16, 128], BF16) for i in range(2)]
            self.km = sb("km", [64, 8])
            self.kmh = sb("kmh", [64, 8], BF16)
            self.kml = sb("kml", [64, 8], BF16)
            self.gm = sb("gm", [128, 64])
            self.m8 = sb("m8", [128, 8, 8])
            self.lt = sb("lt", [128, 64])
            self.bpad = sb("bpad", [128, 8, 72], BF16)
            self.rden = sb("rden", [128, 4])
            self.NPT = 28
            for i in range(4):
                S.op('pool', [], [('qa', i)], lambda e, i=i: e.memset(self.qa[i][64:72, :], 0.0))
                S.dma('pool', [], [('ka', i)], lambda e, i=i: e.dma_start(out=self.ka[i][64:72, :], in_=self.e8_in))
                S.op('pool', [], [('va', i)], lambda e, i=i: e.memset(self.va[i][:, :, 64:65], 1.0))
            S.op('pool', [], ['bpad'], lambda e: e.memset(self.bpad[:], 0.0))
            if STOP_AFTER == 'att_init':
                self.dbg_qk(0)
                raise _Stop()
        PT = [self.B3[:, i // 4, (i % 4) * 512:(i % 4 + 1) * 512] for i in range(self.NPT)]
        attT = self.B1
        negp = self.cst[:, C_NEGP:C_NEGP + 64]
        floorb = self.cst[:, C_FLOOR:C_FLOOR + 64]
        pt_rr = [0]
        PDb = self.PD[:].bitcast(BF16)
        xk = self.xT_keys()

        def load_w(hp):
            wb = self.wqkv_b[hp % 2]
            S.dma('pool', [], [('wqkv', hp % 2, 0)], lambda e, wb=wb, hp=hp: e.dma_start(
                out=wb[:], in_=self.wqkv[l, hp].rearrange("p (k n) -> p k n", k=8), max_dma_last_dim=4096))
        load_w(0)
        Wl = [[] for _ in range(9)]
        Pl = [[] for _ in range(9)]
        Cl = [[] for _ in range(9)]
        for hp in range(8):
            wb = self.wqkv_b[hp % 2]
            wk = ('wqkv', hp % 2, 0)
            wk1 = wk2 = wk3 = wk
            self.rec = Wl[hp + 1]
            if STOP_AFTER == 'att_w0':
                self.dbg_tile('wb0', wb[:].rearrange("p a b -> p (a b)"), [wk, wk1, wk2, wk3], 3072)
                raise _Stop()
            if hp + 1 < 8:
                load_w(hp + 1)
            if STOP_AFTER == 'att_w1':
                self.dbg_tile('wb0', wb[:].rearrange("p a b -> p (a b)"), [wk, wk1, wk2, wk3], 3072)
                raise _Stop()
            par = (hp % 2) * 2
            hA, hB = par, par + 1
            self.rec = Pl[hp]
            cnt = 0
            for which in (1, 0):
                dst = self.ka if which == 1 else self.qa
                dn = 'ka' if which == 1 else 'qa'
                for tc in range(4):
                    bank = cnt % 4
                    cnt += 1
                    ps = self.PA[:, bank * 512:(bank + 1) * 512]
                    pk = ('PA', bank)
                    cols = slice(tc * 512, (tc + 1) * 512)
                    def stopat(ch, ap=None, keys=None):
                        if STOP_AFTER == 'att_qk1' + ch:
                            if ap is not None:
                                self.dbg_tile('probe', ap, keys, 512)
                            raise _Stop()
                    self.mm([wk, wk1, wk2, wk3] + xk[tc * 4:(tc + 1) * 4], [pk], ps,
                            [(wb[:, kc, which * 128:(which + 1) * 128], self.xT[:, kc, cols]) for kc in range(8)])
                    stopat('a', ps, [pk])
                    qb = self.qbf[cnt % 2]
                    qk_ = ('qbf', cnt % 2)
                    if VARIANT != 6:
                        S.op('act', [pk], [qk_], lambda e, qb=qb, ps=ps: e.activation(qb[:], ps, AF.Copy))
                    stopat('b', qb[:], [qk_])
                    t1 = self.t1[cnt % 2]
                    t2 = self.t2[cnt % 2]
                    k1 = ('t1', cnt % 2)
                    k2 = ('t2', cnt % 2)
                    if VARIANT == 4:
                        t1 = self.sb("t1x", [128, 512])
                    if VARIANT in (5, 8):
                        S.op('dve', [('rope', 1)], [k1], lambda e, t1=t1, ps=ps, cols=cols: e.tensor_copy(t1[:], self.ropeC[:, cols]))
                    elif VARIANT in (1, 4):
                        S.op('dve', [pk, ('rope', 1)], [k1], lambda e, t1=t1, ps=ps, cols=cols: e.tensor_copy(t1[:], ps))
                    elif VARIANT == 2:
                        S.op('dve', [pk, ('rope', 1)], [k1], lambda e, t1=t1, ps=ps, cols=cols: e.tensor_copy(t1[:], self.ropeC[:, cols]))
                    elif VARIANT == 3:
                        S.op('dve', [pk, ('rope', 1)], [k1], lambda e, t1=t1, ps=ps, cols=cols: e.tensor_tensor(t1[:], ps, self.cst[:, 0:512], ALU.mult))
                    else:
                        S.op('dve', [pk, ('rope', 1)], [k1], lambda e, t1=t1, ps=ps, cols=cols: e.tensor_tensor(t1[:], ps, self.ropeC[:, cols], ALU.mult))
                    stopat('c', qb[:] if VARIANT == 7 else t1[:], [qk_, k1] if VARIANT in (7, 8) else [k1])
                    rb = (bank + 2) % 4
                    pr = self.PA[:, rb * 512:(rb + 1) * 512]
                    prk = ('PA', rb)
                    self.mm([qk_, 'cbf'], [prk], pr, [(self.rm_b, qb[:])])
                    stopat('d', pr, [prk])
                    S.op('dve', [prk, ('rope', 0)], [k2], lambda e, t2=t2, pr=pr, cols=cols: e.tensor_tensor(t2[:], pr, self.ropeS[:, cols], ALU.mult))
                    stopat('e', t2[:], [k2])
                    S.op('dve', [k1, k2], [(dn, hA)], lambda e, t1=t1, t2=t2, dst=dst, cols=cols, hA=hA: e.tensor_tensor(dst[hA][0:64, cols], t1[0:64, :], t2[0:64, :], ALU.add))
                    stopat('f', dst[hA][0:64, cols], [(dn, hA)])
                    S.op('dve', [k1, k2], [(dn, hB)], lambda e, t1=t1, t2=t2, dst=dst, cols=cols, hB=hB: e.tensor_tensor(dst[hB][0:64, cols], t1[64:128, :], t2[64:128, :], ALU.add))
                    stopat('g', dst[hB][0:64, cols], [(dn, hB)])
            if STOP_AFTER == 'att_qk':
                raise _Stop()
            for tq in range(4):
                bank = cnt % 4
                cnt += 1
                ps = self.PA[:, bank * 512:(bank + 1) * 512]
                pk = ('PA', bank)
                self.mmseq([wk, wk1, wk2, wk3] + xk[tq * 4:(tq + 1) * 4], [pk],
                           [(ps[:, j * 128:(j + 1) * 128],
                             [(self.xT[:, kc, (tq * 4 + j) * 128:(tq * 4 + j + 1) * 128], wb[:, kc, 256:384]) for kc in range(8)])
                            for j in range(4)])
                psv = ps.rearrange("p (t c) -> p t c", t=4)
                if hp == 0 and tq == 0:
                    self.dbg_tile('psv', ps, [pk], 512)
                    self.dbg_tile('wb0', wb[:].rearrange("p a b -> p (a b)"), [wk, wk1, wk2, wk3], 3072)
                S.op('act', [pk], [('va', hA)], lambda e, psv=psv, tq=tq, hA=hA: e.activation(self.va[hA][:, tq * 4:(tq + 1) * 4, 0:64], psv[:, :, 0:64], AF.Copy))
                S.op('act', [pk], [('va', hB)], lambda e, psv=psv, tq=tq, hB=hB: e.activation(self.va[hB][:, tq * 4:(tq + 1) * 4, 0:64], psv[:, :, 64:128], AF.Copy))
            if STOP_AFTER == 'att_proj':
                self.dbg_tile('va0', self.va[0][:].rearrange("p a b -> p (a b)"), [('va', 0)], 1040)
                self.dbg_qk(hA)
                raise _Stop()
            self.rec = Cl[hp]
            atm = self.atm[hp % 2]
            ak = ('atm', hp % 2)
            for hh, hbuf in ((0, hA), (1, hB)):
                qa, ka, va = self.qa[hbuf], self.ka[hbuf], self.va[hbuf]
                qk, kk, vk = ('qa', hbuf), ('ka', hbuf), ('va', hbuf)
                S.op('dve', [kk], ['km'], lambda e, ka=ka: e.tensor_reduce(self.km[:], ka[0:64, :].rearrange("p (b t) -> p b t", b=8), AX.X, ALU.add))
                S.op('dve', ['km'], ['kmh'], lambda e: e.tensor_scalar(self.kmh[:], self.km[:], 1.0 / 256, None, ALU.mult))
                S.op('dve', ['km', 'kmh'], ['kml'], lambda e: e.scalar_tensor_tensor(self.kml[:], in0=self.km[:], scalar=1.0 / 256, in1=self.kmh[:], op0=ALU.mult, op1=ALU.subtract))
                pg = self.PC[:, 448:512]
                self.mmseq([qk, 'kmh', 'kml'], ['PCg'],
                           [(pg[:, t * 8:(t + 1) * 8], [(qa[0:64, (8 + t) * 128:(9 + t) * 128], self.kmh[:]), (qa[0:64, (8 + t) * 128:(9 + t) * 128], self.kml[:])]) for t in range(8)])
                S.op('dve', ['PCg', 'cst'], ['gm'], lambda e, pg=pg: e.tensor_tensor(self.gm[:], pg, negp, ALU.add))

                S.op('dve', ['gm'], ['m8'], lambda e: sel_max(e, self))
                S.op('dve', ['gm', 'm8'], ['lt'], lambda e: sel_lt(e, self))
                S.op('dve', ['lt', 'cst'], ['bpad'], lambda e: e.tensor_tensor(self.bpad[:, :, 64:72], self.lt[:].rearrange("p (t j) -> p t j", t=8), floorb.rearrange("p (t j) -> p t j", t=8), ALU.max))
                self.tps(['bpad', 'cbf'], ['PD'], [(PDb[0:72, t * 128:(t + 1) * 128], self.bpad[:, t, :]) for t in range(8)], self.ident_b)
                S.op('act', ['PD'], [qk], lambda e, qa=qa: e.activation(qa[64:72, 1024:2048], PDb[64:72, :], AF.Copy))
                if STOP_AFTER == 'att_gate':
                    self.dbg_qk(hA)
                    raise _Stop()
                slots = {}

                def stageA(qc):
                    sl = []
                    for kt in range(4 * qc + 4):
                        c0 = max(0, kt * 128 - qc * 512)
                        bank = kt % 2
                        ps = self.PB[:, bank * 512:(bank + 1) * 512]
                        pk = ('PB', bank)
                        self.mm([kk, qk], [pk], ps[:, c0:512], [(ka[0:72, kt * 128:(kt + 1) * 128], qa[0:72, qc * 512 + c0:(qc + 1) * 512])])
                        si = pt_rr[0]
                        pt_rr[0] = (pt_rr[0] + 1) % self.NPT
                        sl.append(si)
                        pt = PT[si]
                        S.op('act', [pk], [('PT', si)], lambda e, pt=pt, ps=ps, c0=c0: e.activation(pt[:, c0:512], ps[:, c0:512], AF.Exp, scale=0.125))
                        if kt >= 4 * qc:
                            S.op('dve', [('PT', si), 'cbf'], [('PT', si)], lambda e, pt=pt, c0=c0: e.tensor_tensor(pt[:, c0:c0 + 128], pt[:, c0:c0 + 128], self.tri_b, ALU.mult))
                    slots[qc] = sl

                def stageB(qc):
                    sl = slots[qc]
                    groups = []
                    for qi in range(4):
                        qt = 4 * qc + qi
                        groups.append((self.PC[:, qi * 65:(qi + 1) * 65],
                                       [(PT[sl[kt]][:, qi * 128:(qi + 1) * 128], va[:, kt, :]) for kt in range(qt + 1)]))
                    self.mmseq([('PT', s_) for s_ in sl] + [vk], ['PCo'], groups)
                    pv = self.PC[:, 0:260].rearrange("p (q c) -> p q c", q=4)
                    S.op('dve', ['PCo'], ['rden'], lambda e, pv=pv: e.reciprocal(self.rden[:], pv[:, :, 64]))

                    def norm(e, atm=atm, hh=hh, qc=qc):
                        ins = None
                        for qi in range(4):
                            ins = e.tensor_scalar(atm[:, 4 * qc + qi, hh * 64:(hh + 1) * 64], self.PC[:, qi * 65:qi * 65 + 64], self.rden[:, qi:qi + 1], None, ALU.mult)
                        return ins
                    S.op('dve', ['PCo', 'rden'], [ak], norm)

                stageA(0)
                for qc in range(4):
                    if qc + 1 < 4:
                        stageA(qc + 1)
                    stageB(qc)
            if hp == 0:
                self.dbg_tile('va0', self.va[0][:].rearrange("p a b -> p (a b)"), [('va', 0)], 1040)
                self.dbg_tile('atm0', atm[:].rearrange("p a b -> p (a b)"), [ak], 2048)
            for half in range(2):
                self.tps([ak, 'cbf'], ['PD'], [(PDb[:, t * 128:(t + 1) * 128], atm[:, half * 8 + t, :]) for t in range(8)], self.ident_b)
                S.op('act', ['PD'], [('attT', hp, half)], lambda e, half=half, hp=hp: e.activation(attT[:, hp, half * 1024:(half + 1) * 1024], PDb, AF.Copy))
            self.rec = None
        for t in Pl[0]:
            t()
        for hp in range(8):
            for t in Wl[hp + 1]:
                t()
            A, Bn = Cl[hp], Pl[hp + 1]
            n1, n2 = len(A), len(Bn)
            i1 = i2 = 0
            while i1 < n1 or i2 < n2:
                if i2 >= n2 or (i1 < n1 and i1 * n2 <= i2 * n1):
                    A[i1]()
                    i1 += 1
                else:
                    Bn[i2]()
                    i2 += 1
        if 'attT' in self.dbg:
            self.dbgf = self.sb("dbgf", [128, 512])
            for hp in range(8):
                for q4 in range(4):
                    S.op('dve', [('attT', hp, 0), ('attT', hp, 1), ('dbg', 'attT')], ['dbgf'], lambda e, hp=hp, q4=q4: e.tensor_copy(self.dbgf[:, 0:512], attT[:, hp, q4 * 512:(q4 + 1) * 512]))
                    S.dma('sp', ['dbgf'], [('dbg', 'attT')], lambda e, hp=hp, q4=q4: e.dma_start(out=self.dbg['attT'][hp * 128:(hp + 1) * 128, q4 * 512:(q4 + 1) * 512], in_=self.dbgf[:, 0:512]))
            self.final_keys.append(('dbg', 'attT'))

    def ssd(self, l):
        self.psum_epoch()
        S, sb, nc = self.S, self.sb, self.nc
        self.phase([self.R_W, self.R_B3])
        if True:
            self.wssm_b = [sb("wssm0", [128, 8, 768], BF16)] * 2
            self.wdt_b = sb("wdtb", [128, 8, 32], BF16)
            self.smallb = sb("smallb", [128, 96])
            self.cwb = sb("cwb", [128, 8, 4, 5])
            self.dt_all = sb("dt_all", [128, 16, 32])
            self.adt_all = sb("adt_all", [128, 16, 32])
            self.acs_all = sb("acs_all", [128, 16, 32])
            self.dst_all = sb("dst_all", [128, 16, 32])
            self.ea_all = sb("ea_all", [128, 16, 32])
            self.lastb = sb("lastb", [128, 8, 32])
            self.cdb = sb("cdb", [128, 8, 32])
            self.a_b = sb("a_b", [128, 32])
            self.uext = [sb("uext%d" % i, [128, 4, 259], BF16) for i in range(2)]
            self.dg = sb("dg", [128, 4, 4, 128], BF16)
            self.xh = sb("xh", [128, 4, 256])
            self.tnh = sb("tnh", [128, 4, 256])
            self.bcb2 = [sb("bcb%d" % i, [128, 2, 256], BF16) for i in range(2)]
            self.bcf = sb("bcf", [128, 256])
            self.xs_tok2 = [sb("xs_tok%d" % i, [128, 2, 256], BF16) for i in range(2)]
            self.xdt2 = [sb("xdt%d" % i, [128, 2, 256], BF16) for i in range(2)]
            self.xw2 = [sb("xw%d" % i, [128, 2, 256], BF16) for i in range(2)]
            self.btok2 = [sb("btok%d" % i, [128, 2, 128], BF16) for i in range(2)]
            self.cb32 = [sb("cb3_%d" % i, [128, 3, 128]) for i in range(2)]
            self.dtmp = [sb("dtmp%d" % i, [128, 3, 128]) for i in range(2)]
            self.mt = [sb("mt%d" % i, [128, 3, 128], BF16) for i in range(2)]
            self.prev_f = sb("prev_f", [128, 256])
            self.prev_b = sb("prev_b", [128, 256], BF16)
            self.tz = sb("tz", [128, 2, 256])
            self.yv = sb("yv", [128, 2, 256])
            self.ss = sb("ss", [128, 2])
            self.rstd = sb("rstd", [128, 2])
            self.nwb = [sb("nwb%d" % i, [128, 256]) for i in range(2)]
            self.xsf = self.xh[:, 0:2, :]
            self.zs = self.tz
            self.h2 = self.yv
            self.yn = self.yv
            self.junk = sb("junk", [128, 256], BF16)
        xk = self.xT_keys()
        S.dma('sp', [], ['smallb'], lambda e: e.dma_start(out=self.smallb[:], in_=self.small[l].broadcast_to([128, 96])))
        S.dma('sp', [], ['cwb'], lambda e: e.dma_start(out=self.cwb[:], in_=self.cw[l].rearrange("p (g c k) -> p g c k", g=8, c=4)))
        S.dma('pool', [], ['wdtb'], lambda e: e.dma_start(out=self.wdt_b[:], in_=self.wdt[l].rearrange("p (k n) -> p k n", k=8)))
        S.op('pool', ['cwb'], ['cwb'], lambda e: e.tensor_scalar(self.cwb[:], self.cwb[:], 0.5, None, ALU.mult))
        dtb = self.smallb[:, 0:32]
        alog = self.smallb[:, 32:64]
        dsk = self.smallb[:, 64:96]
        S.op('act', ['smallb'], ['a_b'], lambda e: e.activation(self.a_b[:], alog, AF.Exp))
        S.op('dve', ['a_b'], ['a_b'], lambda e: e.tensor_scalar(self.a_b[:], self.a_b[:], -1.0, None, ALU.mult))
        psd = self.PA[:, 0:512]
        self.mmseq(['wdtb'] + xk, [('PA', 0)],
                   [(psd[:, tt * 32:(tt + 1) * 32], [(self.xT[:, kc, tt * 128:(tt + 1) * 128], self.wdt_b[:, kc, :]) for kc in range(8)]) for tt in range(16)])
        dtf = self.dt_all[:].rearrange("p t h -> p (t h)")
        psd3 = psd.rearrange("p (t h) -> p t h", t=16)
        S.op('dve', [('PA', 0), 'smallb'], ['dt_all'], lambda e: e.tensor_tensor(self.dt_all[:], psd3, dtb.unsqueeze(1).broadcast_to([128, 16, 32]), ALU.add))
        S.op('act', ['dt_all'], ['dt_all'], lambda e: e.activation(dtf, dtf, AF.Exp))
        S.op('act', ['dt_all'], ['dt_all'], lambda e: e.activation(dtf, dtf, AF.Ln, bias=1.0))
        S.op('dve', ['dt_all', 'a_b'], ['adt_all'], lambda e: e.tensor_tensor(self.adt_all[:], self.dt_all[:], self.a_b[:].unsqueeze(1).broadcast_to([128, 16, 32]), ALU.mult))
        psa = self.PA[:, 512:1024]
        psl = self.PA[:, 1024:1280]
        groups = []
        for c in range(8):
            a0 = self.adt_all[:, 2 * c, :]
            a1 = self.adt_all[:, 2 * c + 1, :]
            groups.append((psa[:, (2 * c) * 32:(2 * c + 1) * 32], [(self.tri_f, a0)]))
            groups.append((psa[:, (2 * c + 1) * 32:(2 * c + 2) * 32], [(self.ones_f, a0), (self.tri_f, a1)]))
        self.mmseq(['adt_all', 'cst'], [('PA', 1)], groups)
        self.mmseq(['adt_all', 'cst'], [('PA', 2)],
                   [(psl[:, c * 32:(c + 1) * 32], [(self.ones_f, self.adt_all[:, 2 * c, :]), (self.ones_f, self.adt_all[:, 2 * c + 1, :])]) for c in range(8)])
        acsf = self.acs_all[:].rearrange("p t h -> p (t h)")
        S.op('dve', [('PA', 1)], ['acs_all'], lambda e: e.tensor_copy(acsf, psa))
        S.op('dve', [('PA', 2)], ['lastb'], lambda e: e.tensor_copy(self.lastb[:].rearrange("p c h -> p (c h)"), psl))
        S.op('dve', ['lastb', 'acs_all'], ['dst_all'], lambda e: e.tensor_tensor(
            self.dst_all[:].rearrange("p (c s) h -> p c s h", c=8), self.lastb[:].unsqueeze(2).broadcast_to([128, 8, 2, 32]),
            self.acs_all[:].rearrange("p (c s) h -> p c s h", c=8), ALU.subtract))
        dstf = self.dst_all[:].rearrange("p t h -> p (t h)")
        S.op('act', ['dst_all'], ['dst_all'], lambda e: e.activation(dstf, dstf, AF.Exp))
        S.op('dve', ['dst_all', 'dt_all'], ['dst_all'], lambda e: e.tensor_tensor(self.dst_all[:], self.dst_all[:], self.dt_all[:], ALU.mult))
        S.op('act', ['acs_all'], ['ea_all'], lambda e: e.activation(self.ea_all[:].rearrange("p t h -> p (t h)"), acsf, AF.Exp))
        S.op('act', ['lastb'], ['cdb'], lambda e: e.activation(self.cdb[:].rearrange("p c h -> p (c h)"), self.lastb[:].rearrange("p c h -> p (c h)"), AF.Exp))
        self.dump('dt_all', self.dt_all[:].rearrange("p t h -> p (t h)"), 'dt_all')
        self.dump('acs_all', acsf, 'acs_all')

        P0 = self.PA[:, 0:512]
        P1 = self.PA[:, 512:1024]
        P2 = self.PA[:, 1024:1536]
        P3 = self.PA[:, 1536:2048]
        P4 = self.PB[:, 0:512]
        P5 = self.PB[:, 512:1024]
        P6 = self.PC[:, 0:512]
        P7 = self.PD[:, 0:512]
        uprev = None
        def load_g(g):
            wb = self.wssm_b[g % 2]
            S.dma('pool', [], [('wssm', 0, 0)], lambda e, wb=wb, g=g: e.dma_start(
                out=wb[:], in_=self.wssm[l, g].rearrange("p (k n) -> p k n", k=8), max_dma_last_dim=4096))
            nw = self.nwb[g % 2]
            S.dma('sp', [], [('nwb', g % 2)], lambda e, nw=nw, g=g: e.dma_start(out=nw[:], in_=self.normw[l][:, g * 256:(g + 1) * 256].broadcast_to([128, 256])))
        load_g(0)
        for g in range(8):
            wb = self.wssm_b[g % 2]
            wk = ('wssm', 0, 0)
            wkall = [('wssm', 0, 0)]
            nw = self.nwb[g % 2]
            nk = ('nwb', g % 2)
            if g > 0:
                load_g(g)
            hs = slice(4 * g, 4 * g + 4)

            def build_dg(e, g=g):
                ins = None
                for ci in range(4):
                    for k in range(4):
                        ins = e.tensor_scalar(self.dg[:, ci, k, :], self.ident_b, self.cwb[:, g, ci, k:k + 1], None, ALU.mult)
                return ins
            S.op('dve', ['cwb', 'cbf'], ['dg'], build_dg)
            prevB = None
            for c in range(8):
                cols = slice(c * 256, (c + 1) * 256)
                sl = c % 2
                bcb, xs_tok, xw, btok = self.bcb2[sl], self.xs_tok2[sl], self.xw2[sl], self.btok2[sl]
                kbcb, kxs, kxw, kbt = ('bcb', sl), ('xs_tok', sl), ('xw', sl), ('btok', sl)
                xdt, cb3 = self.xdt2[sl], self.cb32[sl]
                kxdt, kcb3 = ('xdt', sl), ('cb3', sl)
                F1, F2, Bq = [], [], []
                self.rec = F1
                xkc = xk[2 * c:2 * c + 2]
                ue = self.uext[c % 2]
                uk = ('uext', c % 2)
                self.mmseq(wkall + xkc, [('PA', 0)],
                           [(P0[:, j * 256:(j + 1) * 256], [(wb[:, kc, 256 + j * 128:256 + (j + 1) * 128], self.xT[:, kc, cols]) for kc in range(8)]) for j in range(2)])
                self.mmseq(wkall + xkc, [('PA', 1)],
                           [(P1[:, j * 256:(j + 1) * 256], [(wb[:, kc, 512 + j * 128:512 + (j + 1) * 128], self.xT[:, kc, cols]) for kc in range(8)]) for j in range(2)])
                if c == 0:
                    S.op('pool', [], [uk], lambda e, bcb=bcb, xs_tok=xs_tok, xw=xw, btok=btok, xdt=xdt, cb3=cb3, ue=ue: e.memset(ue[:, :, 0:3], 0.0))
                else:
                    up = self.uext[(c - 1) % 2]
                    S.op('pool', [('uext', (c - 1) % 2)], [uk], lambda e, bcb=bcb, xs_tok=xs_tok, xw=xw, btok=btok, xdt=xdt, cb3=cb3, ue=ue, up=up: e.tensor_copy(ue[:, :, 0:3], up[:, :, 256:259]))
                S.op('act', [('PA', 0)], [uk], lambda e, bcb=bcb, xs_tok=xs_tok, xw=xw, btok=btok, xdt=xdt, cb3=cb3, ue=ue: e.activation(ue[:, 0:2, 3:259], P0.rearrange("p (j t) -> p j t", j=2), AF.Copy))
                S.op('act', [('PA', 1)], [uk], lambda e, bcb=bcb, xs_tok=xs_tok, xw=xw, btok=btok, xdt=xdt, cb3=cb3, ue=ue: e.activation(ue[:, 2:4, 3:259], P1.rearrange("p (j t) -> p j t", j=2), AF.Copy))
                self.mmseq([uk, 'dg'], [('PA', 0)],
                           [(P0[:, j * 256:(j + 1) * 256], [(self.dg[:, j, k, :], ue[:, j, k:k + 256]) for k in range(4)]) for j in range(2)])
                self.mmseq([uk, 'dg'], [('PA', 1)],
                           [(P1[:, j * 256:(j + 1) * 256], [(self.dg[:, 2 + j, k, :], ue[:, 2 + j, k:k + 256]) for k in range(4)]) for j in range(2)])
                for ci in range(4):
                    w = self.cwb[:, g, ci, :]
                    pc = (P0 if ci < 2 else P1)[:, (ci % 2) * 256:(ci % 2 + 1) * 256]
                    pk = ('PA', 0 if ci < 2 else 1)
                    S.op('act', [pk, 'cwb'], [('xh', ci)], lambda e, ci=ci, w=w, pc=pc: e.activation(self.xh[:, ci, :], pc, AF.Identity, bias=w[:, 4:5]))
                    S.op('act', [pk, 'cwb'], [('tnh', ci)], lambda e, ci=ci, w=w, pc=pc: e.activation(self.tnh[:, ci, :], pc, AF.Tanh, bias=w[:, 4:5]))
                S.op('dve', [('tnh', 0), ('tnh', 1), ('xh', 0), ('xh', 1)], [('xh', 0), ('xh', 1)], lambda e, bcb=bcb, xs_tok=xs_tok, xw=xw, btok=btok, xdt=xdt, cb3=cb3: e.scalar_tensor_tensor(self.xsf[:], in0=self.tnh[:, 0:2, :], scalar=1.0, in1=self.xh[:, 0:2, :], op0=ALU.add, op1=ALU.mult))
                S.op('dve', [('tnh', 2), ('tnh', 3), ('xh', 2), ('xh', 3)], [kbcb], lambda e, bcb=bcb, xs_tok=xs_tok, xw=xw, btok=btok, xdt=xdt, cb3=cb3: e.scalar_tensor_tensor(bcb[:], in0=self.tnh[:, 2:4, :], scalar=1.0, in1=self.xh[:, 2:4, :], op0=ALU.add, op1=ALU.mult))
                S.op('dve', [('tnh', 2), ('xh', 2)], ['bcf'], lambda e, bcb=bcb, xs_tok=xs_tok, xw=xw, btok=btok, xdt=xdt, cb3=cb3: e.scalar_tensor_tensor(self.bcf[:], in0=self.tnh[:, 2, :], scalar=1.0, in1=self.xh[:, 2, :], op0=ALU.add, op1=ALU.mult))
                self.tps([('xh', 0), ('xh', 1), 'cst'], [('PA', 0)],
                         [(P0[:, si * 256 + j * 128:si * 256 + (j + 1) * 128], self.xsf[:, j, si * 128:(si + 1) * 128]) for si in range(2) for j in range(2)], self.ident_f)
                self.tps(['bcf', 'cst'], [('PA', 1)],
                         [(P1[:, si * 128:(si + 1) * 128], self.bcf[:, si * 128:(si + 1) * 128]) for si in range(2)], self.ident_f)
                P0v = P0.rearrange("p (s h d) -> p s h d", s=2, h=4)
                S.op('act', [('PA', 0)], [kxs], lambda e, bcb=bcb, xs_tok=xs_tok, xw=xw, btok=btok, xdt=xdt, cb3=cb3: e.activation(xs_tok[:].rearrange("p s c -> p (s c)"), P0, AF.Copy))
                S.op('dve', [('PA', 0), 'dt_all'], [kxdt], lambda e, bcb=bcb, xs_tok=xs_tok, xw=xw, btok=btok, xdt=xdt, cb3=cb3, c=c, hs=hs: e.tensor_tensor(
                    xdt[:].rearrange("p s (h d) -> p s h d", h=4), P0v,
                    self.dt_all[:, 2 * c:2 * c + 2, hs].unsqueeze(3).broadcast_to([128, 2, 4, 64]), ALU.mult))
                S.op('dve', [('PA', 0), 'dst_all'], [kxw], lambda e, bcb=bcb, xs_tok=xs_tok, xw=xw, btok=btok, xdt=xdt, cb3=cb3, c=c, hs=hs: e.tensor_tensor(
                    xw[:].rearrange("p s (h d) -> p s h d", h=4), P0v,
                    self.dst_all[:, 2 * c:2 * c + 2, hs].unsqueeze(3).broadcast_to([128, 2, 4, 64]), ALU.mult))
                S.op('act', [('PA', 1)], [kbt], lambda e, bcb=bcb, xs_tok=xs_tok, xw=xw, btok=btok, xdt=xdt, cb3=cb3: e.activation(btok[:].rearrange("p s n -> p (s n)"), P1[:, 0:256], AF.Copy))
                self.mmseq([kbcb], [('PA', 3)],
                           [(P3[:, 0:256], [(bcb[:, 0, 0:128], bcb[:, 1, :])]),
                            (P3[:, 384:512], [(bcb[:, 0, 128:256], bcb[:, 1, 128:256])])])
                S.op('dve', [('PA', 3), 'cst'], [kcb3], lambda e, bcb=bcb, xs_tok=xs_tok, xw=xw, btok=btok, xdt=xdt, cb3=cb3: e.tensor_tensor(cb3[:, 0, :], P3[:, 0:128], self.tri_f, ALU.mult))
                S.op('act', [('PA', 3)], [kcb3], lambda e, bcb=bcb, xs_tok=xs_tok, xw=xw, btok=btok, xdt=xdt, cb3=cb3: e.activation(cb3[:, 1, :], P3[:, 128:256], AF.Copy))
                S.op('dve', [('PA', 3), 'cst', kcb3], [kcb3], lambda e, bcb=bcb, xs_tok=xs_tok, xw=xw, btok=btok, xdt=xdt, cb3=cb3: e.tensor_tensor(cb3[:, 2, :], P3[:, 384:512], self.tri_f, ALU.mult))
                self.rec = F2
                def bc_mm(hh):
                    hg = 4 * g + hh
                    pb = (P4 if hh % 2 == 0 else P6)[:, 0:256]
                    pbk = ('PB', 0, 0) if hh % 2 == 0 else 'PC'
                    a0 = self.adt_all[:, 2 * c, hg:hg + 1].broadcast_to([128, 128])
                    a1 = self.adt_all[:, 2 * c + 1, hg:hg + 1].broadcast_to([128, 128])
                    self.mmseq(['adt_all', 'cst'], [pbk],
                               [(pb[:, 0:128], [(a0, self.tri_f)]),
                                (pb[:, 128:256], [(a0, self.ones_f), (a1, self.tri_f)])])
                    return pb, pbk
                nxt = bc_mm(0)
                for hh in range(4):
                    hg = 4 * g + hh
                    pb, pbk = nxt
                    if hh + 1 < 4:
                        nxt = bc_mm(hh + 1)
                    dtm = self.dtmp[hh % 2]
                    dk = ('dtmp', hh % 2)
                    mt = self.mt[hh % 2]
                    mk = ('mt', hh % 2)
                    ac0 = self.acs_all[:, 2 * c, hg:hg + 1]
                    ac1 = self.acs_all[:, 2 * c + 1, hg:hg + 1]

                    def dec(e, dtm=dtm, pb=pb, ac0=ac0, ac1=ac1):
                        e.tensor_scalar(dtm[:, 0, :], pb[:, 0:128], ac0, 0.0, ALU.subtract, ALU.min)
                        e.tensor_scalar(dtm[:, 1, :], pb[:, 128:256], ac0, 0.0, ALU.subtract, ALU.min)
                        return e.tensor_scalar(dtm[:, 2, :], pb[:, 128:256], ac1, 0.0, ALU.subtract, ALU.min)
                    S.op('dve', [pbk, 'acs_all'], [dk], dec)
                    S.op('act', [dk], [dk], lambda e, dtm=dtm: e.activation(dtm[:].rearrange("p a b -> p (a b)"), dtm[:].rearrange("p a b -> p (a b)"), AF.Exp))
                    S.op('dve', [dk, kcb3], [mk], lambda e, dtm=dtm, mt=mt, cb3=cb3: e.tensor_tensor(mt[:], dtm[:], cb3[:], ALU.mult))
                    self.mmseq([mk, kxdt], [('PB', 1, hh)],
                               [(P5[:, hh * 64:(hh + 1) * 64], [(mt[:, 0, :], xdt[:, 0, hh * 64:(hh + 1) * 64])]),
                                (P5[:, 256 + hh * 64:256 + (hh + 1) * 64], [(mt[:, 1, :], xdt[:, 0, hh * 64:(hh + 1) * 64]), (mt[:, 2, :], xdt[:, 1, hh * 64:(hh + 1) * 64])])])
                self.rec = Bq
                self.mmseq(wkall + xkc, [('PA', 2)],
                           [(P2[:, si * 256:(si + 1) * 256], [(self.xT[:, kc, c * 256 + si * 128:c * 256 + (si + 1) * 128], wb[:, kc, 0:256]) for kc in range(8)]) for si in range(2)])
                S.op('act', [('PA', 2)], ['tz'], lambda e, bcb=bcb, xs_tok=xs_tok, xw=xw, btok=btok, xdt=xdt, cb3=cb3: e.activation(self.tz[:].rearrange("p s c -> p (s c)"), P2, AF.Tanh, scale=0.5))
                S.op('dve', ['tz', ('PA', 2)], ['tz'], lambda e, bcb=bcb, xs_tok=xs_tok, xw=xw, btok=btok, xdt=xdt, cb3=cb3: e.scalar_tensor_tensor(self.zs[:].rearrange("p s c -> p (s c)"), in0=self.tz[:].rearrange("p s c -> p (s c)"), scalar=1.0, in1=P2, op0=ALU.add, op1=ALU.mult))
                p5k = [('PB', 1, hh) for hh in range(4)]
                if c > 0:
                    self.mmseq([kbcb, 'prev_b'], [('PA', 2)],
                               [(P2[:, si * 256:(si + 1) * 256], [(bcb[:, 1, si * 128:(si + 1) * 128], self.prev_b[:])]) for si in range(2)])
                if c < 7:
                    self.mm([kbt, kxw], [('PB', 0, 0)], P4[:, 0:256], [(btok[:, si, :], xw[:, si, :]) for si in range(2)])
                    if c == 0:
                        S.op('act', [('PB', 0, 0)], ['prev_f'], lambda e, bcb=bcb, xs_tok=xs_tok, xw=xw, btok=btok, xdt=xdt, cb3=cb3: e.activation(self.prev_f[:], P4[:, 0:256], AF.Copy))
                    else:
                        S.op('dve', ['prev_f', 'cdb'], ['prev_f'], lambda e, bcb=bcb, xs_tok=xs_tok, xw=xw, btok=btok, xdt=xdt, cb3=cb3, c=c, hs=hs: e.tensor_tensor(
                            self.prev_f[:].rearrange("p (h d) -> p h d", h=4), self.prev_f[:].rearrange("p (h d) -> p h d", h=4),
                            self.cdb[:, c, hs].unsqueeze(2).broadcast_to([128, 4, 64]), ALU.mult))
                        S.op('dve', ['prev_f', ('PB', 0, 0)], ['prev_f'], lambda e, bcb=bcb, xs_tok=xs_tok, xw=xw, btok=btok, xdt=xdt, cb3=cb3: e.tensor_tensor(self.prev_f[:], self.prev_f[:], P4[:, 0:256], ALU.add))
                    S.op('act', ['prev_f'], ['prev_b'], lambda e, bcb=bcb, xs_tok=xs_tok, xw=xw, btok=btok, xdt=xdt, cb3=cb3: e.activation(self.prev_b[:], self.prev_f[:], AF.Copy))
                S.op('pool', [kxs, 'smallb'], ['yv'], lambda e, bcb=bcb, xs_tok=xs_tok, xw=xw, btok=btok, xdt=xdt, cb3=cb3, hs=hs: e.tensor_tensor(
                    self.yv[:].rearrange("p s (h d) -> p s h d", h=4), xs_tok[:].rearrange("p s (h d) -> p s h d", h=4),
                    dsk[:, hs].unsqueeze(1).unsqueeze(3).broadcast_to([128, 2, 4, 64]), ALU.mult))
                S.op('dve', p5k + ['yv'], ['yv'], lambda e, bcb=bcb, xs_tok=xs_tok, xw=xw, btok=btok, xdt=xdt, cb3=cb3: e.tensor_tensor(self.yv[:].rearrange("p s c -> p (s c)"), P5, self.yv[:].rearrange("p s c -> p (s c)"), ALU.add))
                if c > 0:
                    def yoff(e, c=c, g=g):
                        ins = None
                        for si in range(2):
                            for hh in range(4):
                                o = self.yv[:, si, hh * 64:(hh + 1) * 64]
                                ins = e.scalar_tensor_tensor(o, in0=P2[:, si * 256 + hh * 64:si * 256 + (hh + 1) * 64],
                                                             scalar=self.ea_all[:, 2 * c + si, 4 * g + hh:4 * g + hh + 1], in1=o, op0=ALU.mult, op1=ALU.add)
                        return ins
                    S.op('dve', [('PA', 2), 'ea_all', 'yv'], ['yv'], yoff)
                S.op('dve', ['yv', 'tz'], ['yv'], lambda e, bcb=bcb, xs_tok=xs_tok, xw=xw, btok=btok, xdt=xdt, cb3=cb3: e.tensor_tensor(self.h2[:], self.yv[:], self.zs[:], ALU.mult))
                for si in range(2):
                    S.op('act', ['yv'], ['junk', ('ss', si)], lambda e, bcb=bcb, xs_tok=xs_tok, xw=xw, btok=btok, xdt=xdt, cb3=cb3, si=si: e.activation(self.junk[:], self.h2[:, si, :], AF.Square, accum_out=self.ss[:, si:si + 1]))
                S.op('dve', [('ss', 0), ('ss', 1)], ['rstd'], lambda e, bcb=bcb, xs_tok=xs_tok, xw=xw, btok=btok, xdt=xdt, cb3=cb3: e.tensor_scalar(self.rstd[:], self.ss[:], 1.0 / 256, 4 * RMS_EPS, ALU.mult, ALU.add))
                S.op('pool', ['rstd', 'nh'], ['rstd'], lambda e, bcb=bcb, xs_tok=xs_tok, xw=xw, btok=btok, xdt=xdt, cb3=cb3: e.tensor_tensor(self.rstd[:], self.rstd[:], self.nh[:, 0:2], ALU.pow))
                for si in range(2):
                    S.op('dve', ['yv', 'rstd', nk], ['yv'], lambda e, bcb=bcb, xs_tok=xs_tok, xw=xw, btok=btok, xdt=xdt, cb3=cb3, si=si, nw=nw: e.scalar_tensor_tensor(
                        self.yn[:, si, :], in0=self.h2[:, si, :], scalar=self.rstd[:, si:si + 1], in1=nw[:], op0=ALU.mult, op1=ALU.mult))
                self.tps(['yv', 'cst'], ['PD'],
                         [(P7[:, j * 256 + si * 128:j * 256 + (si + 1) * 128], self.yn[:, si, j * 128:(j + 1) * 128]) for j in range(2) for si in range(2)], self.ident_f)
                S.op('act', ['PD'], [('yT', g, c)], lambda e, bcb=bcb, xs_tok=xs_tok, xw=xw, btok=btok, xdt=xdt, cb3=cb3, g=g, cols=cols: e.activation(self.yT[:, 2 * g:2 * g + 2, cols], P7.rearrange("p (j t) -> p j t", j=2), AF.Copy))
                self.rec = None
                X = F2 + Bq
                if prevB is None:
                    for t in F1:
                        t()
                else:
                    n1, n2 = len(F1), len(prevB)
                    i1 = i2 = 0
                    while i1 < n1 or i2 < n2:
                        if i2 >= n2 or (i1 < n1 and i1 * n2 <= i2 * n1):
                            F1[i1]()
                            i1 += 1
                        else:
                            prevB[i2]()
                            i2 += 1
                prevB = X
            for t in prevB:
                t()
        if 'yT' in self.dbg:
            if True:
                self.dbgf = self.sb("dbgf", [128, 512])
            for kc in range(16):
                for q4 in range(4):
                    S.op('dve', [('yT', g, c) for g in range(8) for c in range(8)] + [('dbg', 'yT')], ['dbgf'], lambda e, kc=kc, q4=q4: e.tensor_copy(self.dbgf[:, 0:512], self.yT[:, kc, q4 * 512:(q4 + 1) * 512]))
                    S.dma('sp', ['dbgf'], [('dbg', 'yT')], lambda e, kc=kc, q4=q4: e.dma_start(out=self.dbg['yT'][kc * 128:(kc + 1) * 128, q4 * 512:(q4 + 1) * 512], in_=self.dbgf[:, 0:512]))
            self.final_keys.append(('dbg', 'yT'))

    def merge(self, l):
        self.psum_epoch()
        S, sb = self.S, self.sb
        self.phase([self.R_W])
        if True:
            self.wc1_b = [sb("wc1b%d" % i, [128, 5120], BF16) for i in range(2)]
            self.ta = [sb("ta%d" % i, [128, 512]) for i in range(2)]
            self.tb = [sb("tb%d" % i, [128, 512]) for i in range(2)]
        mT = self.B3
        xk = self.xT_keys()
        attk = [('attT', hp, h) for hp in range(8) for h in range(2)]
        yk = [('yT', g, c) for g in range(8) for c in range(8)]
        it = 0
        def load_c(fc):
            wb = self.wc1_b[fc % 2]
            S.dma('pool', [], [('wc1', fc % 2, 0)], lambda e, wb=wb, fc=fc: e.dma_start(
                out=wb[:].rearrange("p (a b) -> p a b", a=5), in_=self.wc1[l, fc].rearrange("p (a b) -> p a b", a=5), max_dma_last_dim=4096))
        load_c(0)
        for fc in range(8):
            wb = self.wc1_b[fc % 2]
            wks = [('wc1', fc % 2, 0)]
            if fc + 1 < 8:
                load_c(fc + 1)
            wg = wb[:, 0:2048].rearrange("p (k n) -> p k n", k=8)
            wa = wb[:, 2048:3072].rearrange("p (k n) -> p k n", k=8)
            ws = wb[:, 3072:5120].rearrange("p (k n) -> p k n", k=16)
            for tc in range(4):
                cols = slice(tc * 512, (tc + 1) * 512)
                if it % 2 == 0:
                    pga, pgs, ppa, pps = (self.PA[:, i * 512:(i + 1) * 512] for i in range(4))
                    bk = [('PA', 0), ('PA', 1), ('PA', 2), ('PA', 3)]
                else:
                    pga, pgs, ppa, pps = self.PB[:, 0:512], self.PB[:, 512:1024], self.PC[:, 0:512], self.PD[:, 0:512]
                    bk = [('PB', 0), ('PB', 1), 'PC', 'PD']
                self.mm(wks + xk[tc * 4:(tc + 1) * 4], [bk[0]], pga, [(wg[:, kc, 0:128], self.xT[:, kc, cols]) for kc in range(8)])
                self.mm(wks + xk[tc * 4:(tc + 1) * 4], [bk[1]], pgs, [(wg[:, kc, 128:256], self.xT[:, kc, cols]) for kc in range(8)])
                self.mm(wks + attk, [bk[2]], ppa, [(wa[:, kc, :], self.B1[:, kc, cols]) for kc in range(8)])
                self.mm(wks + yk, [bk[3]], pps, [(ws[:, kc, :], self.yT[:, kc, cols]) for kc in range(16)])
                ta, tb = self.ta[it % 2], self.tb[it % 2]
                tak, tbk = ('ta', it % 2), ('tb', it % 2)
                it += 1
                S.op('act', [bk[0]], [tak], lambda e, ta=ta, pga=pga: e.activation(ta[:], pga, AF.Tanh, scale=0.5))
                S.op('act', [bk[1]], [tbk], lambda e, tb=tb, pgs=pgs: e.activation(tb[:], pgs, AF.Tanh, scale=0.5))
                S.op('dve', [tak, bk[2]], [tak], lambda e, ta=ta, ppa=ppa: e.scalar_tensor_tensor(ta[:], in0=ta[:], scalar=1.0, in1=ppa, op0=ALU.add, op1=ALU.mult))
                S.op('dve', [tbk, bk[3]], [tbk], lambda e, tb=tb, pps=pps: e.scalar_tensor_tensor(tb[:], in0=tb[:], scalar=1.0, in1=pps, op0=ALU.add, op1=ALU.mult))
                S.op('pool', [tak, tbk], [('mergedT', fc, tc)], lambda e, ta=ta, tb=tb, fc=fc, cols=cols: e.tensor_tensor(mT[:, fc, cols], ta[:], tb[:], ALU.add))

    def layernorm(self, t, tk, g_b, b_b, gk, i):
        S = self.S
        st = self.ln_st[i % 2]
        mv = self.ln_mv[i % 2]
        sk = ('lnst', i % 2)

        def stats(e):
            e.bn_stats(st[:, 0, :], t[:, 0:512])
            return e.bn_stats(st[:, 1, :], t[:, 512:1024])
        S.op('dve', [tk], [sk], stats)
        S.op('dve', [sk], [sk], lambda e: e.bn_aggr(mv[:, 0:2], st[:].rearrange("p a b -> p (a b)")))
        S.op('dve', [sk], [sk], lambda e: e.tensor_scalar(mv[:, 2:3], mv[:, 1:2], LN_EPS, None, ALU.add))
        S.op('pool', [sk, 'nh'], [sk], lambda e: e.tensor_tensor(mv[:, 3:4], mv[:, 2:3], self.nh[:, 0:1], ALU.pow))
        S.op('dve', [tk, sk], [tk], lambda e: e.tensor_scalar(t[:], t[:], mv[:, 0:1], mv[:, 3:4], ALU.subtract, ALU.mult))
        S.op('pool', [tk, gk], [tk], lambda e: e.tensor_tensor(t[:], t[:], g_b, ALU.mult))
        S.op('dve', [tk, gk], [tk], lambda e: e.tensor_tensor(t[:], t[:], b_b, ALU.add))

    def mix_ln1(self, l):
        self.psum_epoch()
        S, sb = self.S, self.sb
        self.phase([self.R_W, self.R_B2])
        if True:
            self.wout_b = sb("woutb", [128, 8, 1024], BF16)
            self.lnpb = sb("lnpb", [128, 4096])
            self.res = [sb("res%d" % i, [128, D]) for i in range(2)]
            self.ln_st = [sb("lnst%d" % i, [128, 2, 6]) for i in range(2)]
            self.ln_mv = [sb("lnmv%d" % i, [128, 4]) for i in range(2)]
        S.dma('pool', [], [('wout', 0)], lambda e: e.dma_start(
            out=self.wout_b[:], in_=self.wout[l].rearrange("p (k n) -> p k n", k=8), max_dma_last_dim=4096))
        lnpb1 = self.lnpb
        S.dma('sp', [], ['lnpb'], lambda e: e.dma_start(out=lnpb1[:], in_=self.lnp[l].broadcast_to([128, 4096])))
        src = self.x_in if l == 0 else self.x1
        srck = 'x_in' if l == 0 else 'x1'
        mk = [('mergedT', fc, tc) for fc in range(8) for tc in range(4)]
        wk = [('wout', 0)]
        for tt in range(16):
            r = self.res[tt % 2]
            rk = ('res', tt % 2)
            S.dma('sp', [(srck, tt)], [rk], lambda e, r=r, tt=tt: e.dma_start(out=r[:], in_=src[tt * 128:(tt + 1) * 128, :]))
            S.op('act', [rk], [rk], lambda e, r=r: e.activation(r[:], r[:], AF.Copy, scale=float(ALPHA)))
            if tt % 2 == 0:
                pm = self.PB[:, 0:1024]
                rkeys = [('PB', 0), ('PB', 1)]
            else:
                pm = self.PA[:, 0:1024]
                rkeys = [('PA', 0), ('PA', 1)]
            self.mmseq(mk + wk, rkeys,
                       [(pm[:, half * 512:(half + 1) * 512], [(self.B3[:, kc, tt * 128:(tt + 1) * 128], self.wout_b[:, kc, half * 512:(half + 1) * 512]) for kc in range(8)]) for half in range(2)])
            S.op('dve', rkeys + [rk], [rk], lambda e, r=r, pm=pm: e.scalar_tensor_tensor(r[:], in0=pm, scalar=0.5, in1=r[:], op0=ALU.mult, op1=ALU.add))
            self.layernorm(r, rk, self.lnpb[:, 0:1024], self.lnpb[:, 1024:2048], 'lnpb', tt)
            S.dma('sp', [rk], [('hres', tt)], lambda e, r=r, tt=tt: e.dma_start(out=self.hres[tt * 128:(tt + 1) * 128, :], in_=r[:]))
            self.to_featmajor2(r, rk, self.B1, 'hT', tt)
        if 'hT' in self.dbg:
            if True:
                self.dbgf = self.sb("dbgf", [128, 512])
            for kc in range(8):
                for q4 in range(4):
                    S.op('dve', [('hT', tt) for tt in range(16)] + [('dbg', 'hT')], ['dbgf'], lambda e, kc=kc, q4=q4: e.tensor_copy(self.dbgf[:, 0:512], self.B1[:, kc, q4 * 512:(q4 + 1) * 512]))
                    S.dma('sp', ['dbgf'], [('dbg', 'hT')], lambda e, kc=kc, q4=q4: e.dma_start(out=self.dbg['hT'][kc * 128:(kc + 1) * 128, q4 * 512:(q4 + 1) * 512], in_=self.dbgf[:, 0:512]))
            self.final_keys.append(('dbg', 'hT'))

    def to_featmajor2(self, tile, tkey, dstT, dkey, tt):
        S = self.S
        for half in range(2):
            ps = self.PC[:, 0:512] if half == 0 else self.PD[:, 0:512]
            pk = 'PC' if half == 0 else 'PD'
            self.tps([tkey, 'cst'], [pk], [(ps[:, j * 128:(j + 1) * 128], tile[:, (half * 4 + j) * 128:(half * 4 + j + 1) * 128]) for j in range(4)], self.ident_f)
            S.op('act', [pk], [(dkey, tt)], lambda e, ps=ps, half=half: e.activation(
                dstT[:, half * 4:(half + 1) * 4, tt * 128:(tt + 1) * 128], ps.rearrange("p (c t) -> p c t", c=4), AF.Copy))

    def ffn(self, l):
        self.psum_epoch()
        S, sb = self.S, self.sb
        self.phase([self.R_W, (self.R_B3[0] + 4096, self.R_B3[1])])
        if True:
            self.wup_b = [sb("wupb%d" % i, [128, 8, 512], BF16) for i in range(2)]
            self.wdn_b = [sb("wdnb%d" % i, [128, 4, 1024], BF16) for i in range(2)]
            self.rl = [sb("rl%d" % i, [128, 512]) for i in range(2)]
            self.ln_st = [sb("lnst%d" % i, [128, 2, 6]) for i in range(2)]
            self.ln_mv = [sb("lnmv%d" % i, [128, 4]) for i in range(2)]
        uT = self.B3
        hk = [('hT', tt) for tt in range(16)]
        it = 0
        def load_f(gi):
            wu = self.wup_b[gi % 2]
            wd = self.wdn_b[gi % 2]
            wuk, wdk = ('wup', gi % 2), ('wdn', gi % 2)
            S.dma('pool', [], [(wuk, 0)], lambda e, wu=wu, gi=gi: e.dma_start(
                out=wu[:], in_=self.wup[l, gi].rearrange("p (k n) -> p k n", k=8), max_dma_last_dim=4096))
            S.dma('pool', [], [(wdk, 0)], lambda e, wd=wd, gi=gi: e.dma_start(
                out=wd[:], in_=self.wdn[l, gi].rearrange("p (k n) -> p k n", k=4), max_dma_last_dim=4096))
        load_f(0)
        for gi in range(8):
            wu = self.wup_b[gi % 2]
            wd = self.wdn_b[gi % 2]
            wuk, wdk = ('wup', gi % 2), ('wdn', gi % 2)
            if gi + 1 < 8:
                load_f(gi + 1)
            wuks = [(wuk, 0)]
            wdks = [(wdk, 0)]
            for j in range(4):
                for tc in range(4):
                    bank = it % 4
                    ps = [self.PA[:, 0:512], self.PA[:, 512:1024], self.PC[:, 0:512], self.PD[:, 0:512]][bank]
                    pk = [('PA', 0), ('PA', 1), 'PC', 'PD'][bank]
                    cols = slice(tc * 512, (tc + 1) * 512)
                    self.mm(wuks + hk[tc * 4:(tc + 1) * 4], [pk], ps, [(wu[:, kc, j * 128:(j + 1) * 128], self.B1[:, kc, cols]) for kc in range(8)])
                    rl = self.rl[it % 2]
                    rlk = ('rl', it % 2)
                    S.op('act', [pk], [rlk], lambda e, rl=rl, ps=ps: e.activation(rl[:], ps, AF.Relu))
                    eng = 'dve'
                    S.op(eng, [rlk], [('uT', j)], lambda e, rl=rl, j=j, cols=cols: e.tensor_tensor(uT[:, j, cols], rl[:], rl[:], ALU.mult))
                    it += 1
            for tt in range(16):
                if tt % 2 == 0:
                    pm = self.PB[:, 0:1024]
                    rkeys = [('PB', 0), ('PB', 1)]
                else:
                    pm = self.PA[:, 1024:2048]
                    rkeys = [('PA', 2), ('PA', 3)]
                self.mmseq([('uT', j) for j in range(4)] + wdks, rkeys,
                           [(pm[:, half * 512:(half + 1) * 512], [(uT[:, j, tt * 128:(tt + 1) * 128], wd[:, j, half * 512:(half + 1) * 512]) for j in range(4)]) for half in range(2)])
                if gi == 0:
                    S.op('act', rkeys, [('acc', tt)], lambda e, tt=tt, pm=pm: e.activation(self.acc[:, tt, :], pm, AF.Copy))
                else:
                    S.op('dve', rkeys + [('acc', tt)], [('acc', tt)], lambda e, tt=tt, pm=pm: e.tensor_tensor(self.acc[:, tt, :], pm, self.acc[:, tt, :], ALU.add))
        S.barrier()
        self.res = [self.wup_b[i][:, 0:4, :].rearrange("p a b -> p (a b)").bitcast(F32) for i in range(2)]
        self.lnpb = self.wdn_b[0][:].rearrange("p a b -> p (a b)").bitcast(F32)
        lnpb2 = self.lnpb
        S.dma('sp', [], ['lnpb'], lambda e: e.dma_start(out=lnpb2, in_=self.lnp[l][:, 2048:4096].broadcast_to([128, 2048])))
        dst = self.out if l == DEPTH - 1 else self.x1
        dstk = 'out' if l == DEPTH - 1 else 'x1'
        for tt in range(16):
            r = self.res[tt % 2]
            rk = ('res', tt % 2)
            S.dma('sp', [('hres', tt)], [rk], lambda e, r=r, tt=tt: e.dma_start(out=r[:], in_=self.hres[tt * 128:(tt + 1) * 128, :]))
            S.op('dve', [rk, ('acc', tt)], [rk], lambda e, r=r, tt=tt: e.scalar_tensor_tensor(r[:], in0=r[:], scalar=float(ALPHA), in1=self.acc[:, tt, :], op0=ALU.mult, op1=ALU.add))
            self.layernorm(r, rk, self.lnpb[:, 0:1024], self.lnpb[:, 1024:2048], 'lnpb', tt)
            S.dma('sp', [rk], [(dstk, tt)], lambda e, r=r, tt=tt: e.dma_start(out=dst[tt * 128:(tt + 1) * 128, :], in_=r[:]))
            if l == DEPTH - 1:
                self.final_keys.append((dstk, tt))
            else:
                self.to_featmajor2(r, rk, self.xT, 'xT', tt)


def sel_max(e, b):
    ins = None
    for t in range(8):
        ins = e.max(b.m8[:, t, :], b.gm[:, t * 8:(t + 1) * 8])
    return ins


def sel_lt(e, b):
    ins = None
    for t in range(8):
        ins = e.tensor_scalar(b.lt[:, t * 8:(t + 1) * 8], b.gm[:, t * 8:(t + 1) * 8], b.m8[:, t, 2:3], NEG, ALU.is_lt, ALU.mult)
    return ins


_CACHE = {}


def kernel(**inputs):
    inp = {k: np.asarray(v) for k, v in inputs.items()}
    w = prep_weights(inp)
    consts, e8 = make_consts()
    x = np.ascontiguousarray(inp['x'], dtype=np.float32)
    pos = np.ascontiguousarray(inp['positions'], dtype=np.int32)
    nc = Builder().build()
    in_maps = []
    for b in range(NCORES):
        m = {"x": x[b], "pos": pos[b:b + 1], "consts": consts, "e8": e8}
        m.update(w)
        in_maps.append(m)
    res = run_bass_kernel_spmd(nc, in_maps, core_ids=list(range(NCORES)))
    out = np.stack([np.asarray(r["out"], dtype=np.float32) for r in res.results], axis=0)
    return out
```

```python
import math
import numpy as np
import concourse.bass as bass
import concourse.mybir as mybir
from concourse.bass_utils import run_bass_kernel_spmd

F32 = mybir.dt.float32
BF16 = mybir.dt.bfloat16
I32 = mybir.dt.int32
AF = mybir.ActivationFunctionType
ALU = mybir.AluOpType
AX = mybir.AxisListType

D = 1024
SEQ = 2048
DEPTH = 2
NCORES = 8
IN_W = 11296
ALPHA = (2 * DEPTH) ** 0.25
LN_EPS = 1e-5
RMS_EPS = 1e-5
NEG = -1.0e5
PI = math.pi

DEBUG = {}
STOP_AFTER = None
NO_BARRIER = False
VARIANT = 0


class Sched:
    ENG = ('pe', 'act', 'dve', 'pool', 'sp')

    def __init__(self, nc, n_dma_sems=24):
        self.nc = nc
        self.sem = {e: nc.alloc_semaphore('s_' + e) for e in self.ENG}
        self.cnt = {e: 0 for e in self.ENG}
        self.dsem = [nc.alloc_semaphore('d%d' % i) for i in range(n_dma_sems)]
        self.dcnt = [0] * n_dma_sems
        self.drr = {'sp': 0, 'pool': 0}
        self.dpool = {'sp': list(range(0, n_dma_sems // 2)), 'pool': list(range(n_dma_sems // 2, n_dma_sems))}
        self.waited = {}
        self.last_w = {}
        self.readers = {}
        self.q = {e: [] for e in self.ENG}

    def barrier(self):
        if NO_BARRIER is True:
            return
        self.epoch = {}
        scr = self.bar_scratch
        d_sp = {('dma', i): self.dcnt[i] for i in self.dpool['sp'] if self.dcnt[i] > 0}
        d_pl = {('dma', i): self.dcnt[i] for i in self.dpool['pool'] if self.dcnt[i] > 0}
        self._token('act', d_sp, lambda e: e.memzero(scr[:, 0:1]))
        t1 = ('act', self.cnt['act'])
        d_pl[t1[0]] = t1[1]
        self._token('pool', d_pl, lambda e: e.memset(scr[:, 1:2], 0.0))
        d3 = {e: c for e, c in self.cnt.items() if c > 0}
        self._token('dve', d3, lambda e: e.memset(scr[:, 2:3], 0.0))
        self.epoch = {'dve': self.cnt['dve']}

    def _token(self, e, deps, emit):
        waits = self._waits(e, deps)
        self.cnt[e] += 1
        sem = self.sem[e]

        def thunk(eng):
            for s, v in waits:
                eng.wait_ge(s, v)
            emit(eng).then_inc(sem, 1)
        self.q[e].append(thunk)

    def _deps(self, reads, writes):
        deps = dict(getattr(self, 'epoch', {}))

        def add(src, idx):
            if deps.get(src, 0) < idx:
                deps[src] = idx
        for k in reads:
            lw = self.last_w.get(k)
            if lw is not None:
                add(*lw)
        for k in writes:
            lw = self.last_w.get(k)
            if lw is not None:
                add(*lw)
            for src, idx in self.readers.get(k, {}).items():
                add(src, idx)
        return deps

    def _waits(self, e, deps):
        out = []
        self.maxw = getattr(self, 'maxw', {})
        for src, idx in deps.items():
            if self.waited.get((e, src), 0) >= idx:
                continue
            self.waited[(e, src)] = idx
            if isinstance(src, tuple):
                out.append((self.dsem[src[1]], 16 * idx))
            else:
                out.append((self.sem[src], idx))
        self.maxw[len(out)] = self.maxw.get(len(out), 0) + 1
        return out

    def _record(self, token, reads, writes):
        src, idx = token
        for k in reads:
            r = self.readers.setdefault(k, {})
            if r.get(src, 0) < idx:
                r[src] = idx
        for k in writes:
            self.last_w[k] = (src, idx)
            self.readers[k] = {}

    @staticmethod
    def _bank(k):
        if isinstance(k, tuple) and k and k[0] == 'PA':
            return ('BANK', k[1])
        if isinstance(k, tuple) and k and k[0] == 'PB':
            return ('BANK', 4 + k[1])
        if k in ('PC', 'PCg', 'PCo'):
            return ('BANK', 6)
        if k == 'PD':
            return ('BANK', 7)
        return None

    def _canon(self, reads, writes):
        r2, w2 = [], []
        for k in reads:
            b = self._bank(k)
            if b is None:
                r2.append(k)
            elif b not in w2:
                w2.append(b)
        for k in writes:
            b = self._bank(k)
            if b is None:
                w2.append(k)
            elif b not in w2:
                w2.append(b)
        return r2, w2

    def op(self, e, reads, writes, emit):
        reads, writes = self._canon(reads, writes)
        deps = self._deps(reads, writes)
        waits = self._waits(e, deps)
        self.cnt[e] += 1
        idx = self.cnt[e]
        sem = self.sem[e]

        def thunk(eng):
            for s, v in waits:
                eng.wait_ge(s, v)
            emit(eng).then_inc(sem, 1)
        self.q[e].append(thunk)
        self._record((e, idx), reads, writes)

    def dma(self, e, reads, writes, emit):
        pool = self.dpool[e]
        s = pool[self.drr[e] % len(pool)]
        self.drr[e] += 1
        src = ('dma', s)
        deps = self._deps(reads, writes)
        if self.dcnt[s] > 0 and deps.get(src, 0) < self.dcnt[s]:
            deps[src] = self.dcnt[s]
        waits = self._waits(e, deps)
        self.dcnt[s] += 1
        idx = self.dcnt[s]
        sem = self.dsem[s]

        def thunk(eng):
            for sm, v in waits:
                eng.wait_ge(sm, v)
            emit(eng).then_inc(sem, 16)
        self.q[e].append(thunk)
        self._record((src, idx), reads, writes)

    def alias(self, old_keys, new_keys):
        acc = {}
        for k in old_keys:
            lw = self.last_w.get(k)
            if lw is not None and acc.get(lw[0], 0) < lw[1]:
                acc[lw[0]] = lw[1]
            for src, idx in self.readers.get(k, {}).items():
                if acc.get(src, 0) < idx:
                    acc[src] = idx
        for k in new_keys:
            r = self.readers.setdefault(k, {})
            for src, idx in acc.items():
                if r.get(src, 0) < idx:
                    r[src] = idx

    def finish(self, final_keys):
        self.barrier()
        deps = self._deps(final_keys, [])
        waits = self._waits('sp', deps)

        def thunk(eng):
            for s, v in waits:
                eng.wait_ge(s, v)
        self.q['sp'].append(thunk)
        nc = self.nc
        q = self.q
        with nc.Block() as block:
            @block.sync
            def _(eng):
                for t in q['sp']:
                    t(eng)

            @block.tensor
            def _(eng):
                for t in q['pe']:
                    t(eng)

            @block.scalar
            def _(eng):
                for t in q['act']:
                    t(eng)

            @block.vector
            def _(eng):
                for t in q['dve']:
                    t(eng)

            @block.gpsimd
            def _(eng):
                for t in q['pool']:
                    t(eng)


C_ID, C_TRI, C_ONES, C_RM, C_INVF, C_SGN, C_NEGP, C_FLOOR, NC_CONST = 0, 128, 256, 384, 512, 513, 514, 578, 648


def make_consts():
    c = np.zeros((128, NC_CONST), np.float32)
    c[:, C_ID:C_ID + 128] = np.eye(128, dtype=np.float32)
    c[:, C_TRI:C_TRI + 128] = np.triu(np.ones((128, 128), np.float32))
    c[:, C_ONES:C_ONES + 128] = 1.0
    rm = np.zeros((128, 128), np.float32)
    inv = (500000.0 ** (-np.arange(0, 16, 2, dtype=np.float32) / 16)).astype(np.float32)
    for blk in range(2):
        for d in range(16):
            src = d + 8 if d < 8 else d - 8
            rm[blk * 64 + src, blk * 64 + d] = 1.0
    c[:, C_RM:C_RM + 128] = rm
    for p in range(128):
        d = p % 64
        if d < 16:
            c[p, C_INVF] = inv[d % 8]
            c[p, C_SGN] = -1.0 if d < 8 else 1.0
    for qt in range(8):
        blk = (8 + qt) // 2
        for j in range(8):
            c[:, C_NEGP + qt * 8 + j] = 0.0 if j < blk else -1.0e30
            c[:, C_FLOOR + qt * 8 + j] = NEG if j < blk else 0.0
    e8 = np.zeros((8, SEQ), np.float32)
    for j in range(8):
        e8[j, j * 256:(j + 1) * 256] = 1.0
    return c, e8


def prep_weights(inp):
    out = {}
    L = DEPTH

    def kmaj(w):
        K, N = w.shape
        return w.reshape(K // 128, 128, N).transpose(1, 0, 2)

    w_in = inp['w_in']
    wqkv = np.empty((L, 8, 128, 8, 384), np.float32)
    wssm = np.empty((L, 8, 128, 8, 768), np.float32)
    wdt = np.empty((L, 128, 8, 32), np.float32)
    wc1 = np.empty((L, 8, 128, 5120), np.float32)
    wout = np.empty((L, 128, 8, 1024), np.float32)
    wup = np.empty((L, 8, 128, 8, 512), np.float32)
    wdn = np.empty((L, 8, 128, 4, 1024), np.float32)
    cw = np.empty((L, 128, 8, 4, 5), np.float32)
    for l in range(L):
        wi = kmaj(w_in[l])
        for hp in range(8):
            wqkv[l, hp, :, :, 0:128] = wi[:, :, hp * 128:(hp + 1) * 128]
            wqkv[l, hp, :, :, 128:256] = wi[:, :, 1024 + hp * 128:1024 + (hp + 1) * 128]
            wqkv[l, hp, :, :, 256:384] = wi[:, :, 2048 + hp * 128:2048 + (hp + 1) * 128]
        for g in range(8):
            wssm[l, g, :, :, 0:256] = wi[:, :, 3072 + g * 256:3072 + (g + 1) * 256]
            wssm[l, g, :, :, 256:512] = wi[:, :, 5120 + g * 256:5120 + (g + 1) * 256]
            wssm[l, g, :, :, 512:640] = wi[:, :, 7168 + g * 128:7168 + (g + 1) * 128]
            wssm[l, g, :, :, 640:768] = wi[:, :, 8192 + g * 128:8192 + (g + 1) * 128]
        wdt[l] = wi[:, :, 9216:9248]
        wap = kmaj(inp['w_attn_proj'][l])
        wsp = kmaj(inp['w_ssm_proj'][l])
        for fc in range(8):
            blk = np.empty((128, 8, 256), np.float32)
            blk[:, :, 0:128] = wi[:, :, 9248 + fc * 128:9248 + (fc + 1) * 128]
            blk[:, :, 128:256] = wi[:, :, 10272 + fc * 128:10272 + (fc + 1) * 128]
            wc1[l, fc, :, 0:2048] = blk.reshape(128, 2048)
            wc1[l, fc, :, 2048:3072] = wap[:, :, fc * 128:(fc + 1) * 128].reshape(128, 1024)
            wc1[l, fc, :, 3072:5120] = wsp[:, :, fc * 128:(fc + 1) * 128].reshape(128, 2048)
        wout[l] = kmaj(inp['w_out'][l])
        wu = kmaj(inp['w_up'][l])
        wd = kmaj(inp['w_down'][l])
        for gI in range(8):
            wup[l, gI] = wu[:, :, gI * 512:(gI + 1) * 512]
            wdn[l, gI] = wd[:, gI * 4:(gI + 1) * 4, :]
        cwl = inp['conv_w'][l]
        cbl = inp['conv_b'][l]
        for g in range(8):
            offs = [g * 256, g * 256 + 128, 2048 + g * 128, 3072 + g * 128]
            for ci, o in enumerate(offs):
                cw[l, :, g, ci, 0:4] = cwl[:, o:o + 128].T
                cw[l, :, g, ci, 4] = cbl[o:o + 128]
    out['wqkv'] = wqkv.reshape(L, 8, 128, 3072)
    out['wssm'] = wssm.reshape(L, 8, 128, 6144)
    out['wdt'] = wdt.reshape(L, 128, 256)
    out['wc1'] = wc1
    out['wout'] = wout.reshape(L, 128, 8192)
    out['wup'] = wup.reshape(L, 8, 128, 4096)
    out['wdn'] = wdn.reshape(L, 8, 128, 4096)
    out['cw'] = cw.reshape(L, 128, 160)
    small = np.concatenate([inp['dt_bias'], inp['a_log'], inp['d_skip']], axis=1)
    out['small'] = np.ascontiguousarray(small.reshape(L, 1, 96))
    out['normw'] = np.ascontiguousarray(inp['ssm_norm_w'].reshape(L, 1, 2048))
    lnp = np.stack([inp['ln1_g'], inp['ln1_b'], inp['ln2_g'], inp['ln2_b']], axis=1)
    out['lnp'] = np.ascontiguousarray(lnp.reshape(L, 1, 4096))
    return {k: np.ascontiguousarray(v, dtype=np.float32) for k, v in out.items()}


class Arena:
    def __init__(self, base_ap, segments):
        self.base = base_ap
        self.segs = [list(x) for x in segments]

    def alloc(self, shape, dtype=F32):
        P = shape[0]
        n = 1
        for d in shape[1:]:
            n *= d
        per = 4 if dtype in (F32, I32) else 2
        ncols = (n * per + 3) // 4
        ncols = (ncols + 7) // 8 * 8
        for sg in self.segs:
            if sg[1] - sg[0] >= ncols:
                off = sg[0]
                sg[0] += ncols
                ap = self.base[0:P, off:off + (n * per + 3) // 4]
                if dtype != F32:
                    ap = ap.bitcast(dtype)
                if len(shape) == 3:
                    ap = ap.rearrange("p (a b) -> p a b", a=shape[1])
                elif len(shape) == 4:
                    ap = ap.rearrange("p (a b c) -> p a b c", a=shape[1], b=shape[2])
                return ap
        raise RuntimeError("arena out of memory for %s" % (shape,))


class _Stop(Exception):
    pass


class Builder:
    def __init__(self):
        nc = bass.Bass("TRN2", target_bir_lowering=False)
        self.nc = nc
        self.S = Sched(nc)
        self.rec = None
        S_ = self.S
        S_._op, S_._dma = S_.op, S_.dma

        def _rop(*a):
            if self.rec is not None:
                self.rec.append(lambda: S_._op(*a))
            else:
                S_._op(*a)

        def _rdma(*a):
            if self.rec is not None:
                self.rec.append(lambda: S_._dma(*a))
            else:
                S_._dma(*a)
        S_.op, S_.dma = _rop, _rdma
        L = DEPTH
        dt = lambda n, s, d=F32, k="ExternalInput": nc.dram_tensor(n, s, d, kind=k).ap()
        self.x_in = dt("x", [SEQ, D])
        self.pos_in = dt("pos", [1, SEQ], I32)
        self.consts_in = dt("consts", [128, NC_CONST])
        self.e8_in = dt("e8", [8, SEQ])
        self.wqkv = dt("wqkv", [L, 8, 128, 3072])
        self.wssm = dt("wssm", [L, 8, 128, 6144])
        self.wdt = dt("wdt", [L, 128, 256])
        self.wc1 = dt("wc1", [L, 8, 128, 5120])
        self.wout = dt("wout", [L, 128, 8192])
        self.wup = dt("wup", [L, 8, 128, 4096])
        self.wdn = dt("wdn", [L, 8, 128, 4096])
        self.cw = dt("cw", [L, 128, 160])
        self.small = dt("small", [L, 1, 96])
        self.normw = dt("normw", [L, 1, 2048])
        self.lnp = dt("lnp", [L, 1, 4096])
        self.out = dt("out", [SEQ, D], F32, "ExternalOutput")
        self.x1 = dt("x1s", [SEQ, D], F32, "Internal")
        self.hres = dt("hres", [SEQ, D], F32, "Internal")
        self.dbg = {}
        for name, shape in DEBUG.items():
            self.dbg[name] = dt("dbg_" + name, list(shape), F32, "ExternalOutput")
        self.final_keys = []

    def sb(self, name, shape, dtype=F32):
        return self.ar.alloc(list(shape), dtype)

    def phase(self, segs):
        self.S.barrier()
        self.ar = Arena(self.arena, segs)

    def mm(self, reads, writes, out, pairs, transpose=False):
        def emit(e):
            n = len(pairs)
            ins = None
            for i, (l, r) in enumerate(pairs):
                ins = e.matmul(out, lhsT=l, rhs=r, start=(i == 0), stop=(i == n - 1))
            return ins
        self.S.op('pe', reads, writes, emit)

    def mmseq(self, reads, writes, groups):
        def emit(e):
            ins = None
            for out, pairs in groups:
                n = len(pairs)
                for i, (l, r) in enumerate(pairs):
                    ins = e.matmul(out, lhsT=l, rhs=r, start=(i == 0), stop=(i == n - 1))
            return ins
        self.S.op('pe', reads, writes, emit)

    def tps(self, reads, writes, items, ident):
        def emit(e):
            ins = None
            for o, i in items:
                ins = e.transpose(o, i, ident)
            return ins
        self.S.op('pe', reads, writes, emit)

    def dump(self, name, ap, keys):
        if name in self.dbg:
            if not isinstance(keys, list):
                keys = [keys]
            self.S.dma('sp', keys, [('dbg', name)], lambda e: e.dma_start(out=self.dbg[name], in_=ap))
            self.final_keys.append(('dbg', name))

    def build(self):
        nc, S = self.nc, self.S
        sb = self.sb
        nbytes = int(nc.sbuf_bytes_remaining) - 256
        NA = nbytes // 4 // 8 * 8
        self.arena = nc.alloc_sbuf_tensor("arena", [128, NA], F32)[:]
        self.R_P = (0, 11264)
        self.R_B1 = (11264, 19456)
        self.R_B2 = (19456, 35840)
        self.R_B3 = (35840, 44032)
        self.R_W = (44032, NA)
        assert NA - 44032 > 8000, NA
        self.ar = Arena(self.arena, [self.R_P])
        self.S.bar_scratch = sb("barscr", [128, 8])
        cst = sb("cst", [128, NC_CONST])
        self.cst = cst
        S.dma('sp', [], ['cst'], lambda e: e.dma_start(out=cst[:], in_=self.consts_in))
        self.ident_f = cst[:, C_ID:C_ID + 128]
        self.tri_f = cst[:, C_TRI:C_TRI + 128]
        self.ones_f = cst[:, C_ONES:C_ONES + 128]
        cbf = sb("cbf", [128, 512], BF16)
        S.op('dve', ['cst'], ['cbf'], lambda e: e.tensor_copy(cbf[:], cst[:, 0:512]))
        self.ident_b = cbf[:, 0:128]
        self.tri_b = cbf[:, 128:256]
        self.rm_b = cbf[:, 384:512]
        self.nh = sb("nh", [128, 8])
        S.op('pool', [], ['nh'], lambda e: e.memset(self.nh[:], -0.5))
        self.ropeC = sb("ropeC", [128, SEQ], BF16)
        self.ropeS = sb("ropeS", [128, SEQ], BF16)
        self.xT = sb("xT", [128, 8, SEQ], BF16)
        A = self.arena
        self.B1 = A[:, self.R_B1[0]:self.R_B1[1]].bitcast(BF16).rearrange("p (c t) -> p c t", c=8)
        self.B2 = A[:, self.R_B2[0]:self.R_B2[1]]
        self.B3 = A[:, self.R_B3[0]:self.R_B3[1]].bitcast(BF16).rearrange("p (c t) -> p c t", c=8)
        self.yT = self.B2.bitcast(BF16).rearrange("p (c t) -> p c t", c=16)
        self.acc = self.B2.rearrange("p (t f) -> p t f", t=16)
        self.ar = Arena(self.arena, [self.R_B2])
        self.PA = nc.alloc_psum_tensor("PA", [128, 2048], F32)
        self.PB = nc.alloc_psum_tensor("PB", [128, 1024], F32)
        self.PC = nc.alloc_psum_tensor("PC", [128, 512], F32)
        self.PD = nc.alloc_psum_tensor("PD", [128, 512], F32)
        self.rope_tables()
        if STOP_AFTER == 'rope':
            return self.finish()
        self.load_xT()
        if STOP_AFTER == 'xT':
            self.dbg_tile('xT0', self.xT[:, 3, 1024:1536], self.xT_keys(), 512)
            return self.finish()
        try:
            for l in range(DEPTH):
                self.layer(l)
                if STOP_AFTER == 'layer0':
                    break
        except _Stop:
            pass
        return self.finish()

    def finish(self):
        self.S.finish(self.final_keys)
        return self.nc

    def rope_tables(self):
        S, sb = self.S, self.sb
        cst = None
        posi = sb("posi", [128, SEQ], I32)
        S.dma('sp', [], ['posi'], lambda e: e.dma_start(out=posi[:], in_=self.pos_in.broadcast_to([128, SEQ])))
        ang = sb("ang", [128, SEQ])
        tmp = sb("rtmp", [128, SEQ])
        tmi = sb("rtmi", [128, SEQ], I32)
        rc = sb("rc", [128, SEQ])
        rs = sb("rs", [128, SEQ])
        invf = self.cst[:, C_INVF:C_INVF + 1]
        sgn = self.cst[:, C_SGN:C_SGN + 1]
        S.op('dve', ['posi'], ['ang'], lambda e: e.tensor_copy(ang[:], posi[:]))
        S.op('dve', ['ang', 'cst'], ['ang'], lambda e: e.tensor_scalar(ang[:], ang[:], invf, None, ALU.mult))
        for which, dst in ((0, rs), (1, rc)):
            off = 0.0 if which == 0 else PI / 2
            S.op('dve', ['ang'], ['rtmp'], lambda e, off=off: e.tensor_scalar(tmp[:], ang[:], off, 1.0 / (2 * PI), ALU.add, ALU.mult))
            S.op('dve', ['rtmp'], ['rtmi'], lambda e: e.tensor_copy(tmi[:], tmp[:]))
            S.op('dve', ['rtmi'], ['rtmp'], lambda e: e.tensor_copy(tmp[:], tmi[:]))
            S.op('dve', ['rtmp', 'ang'], ['rtmp'], lambda e: e.scalar_tensor_tensor(tmp[:], in0=tmp[:], scalar=-2 * PI, in1=ang[:], op0=ALU.mult, op1=ALU.add))
            S.op('dve', ['rtmp'], [('rope', which)], lambda e, off=off, dst=dst: e.tensor_scalar(dst[:], tmp[:], off, None, ALU.add))
            S.op('dve', [('rope', which)], ['rtmp'], lambda e, dst=dst: e.tensor_scalar(tmp[:], dst[:], PI, -2 * PI, ALU.is_gt, ALU.mult))
            S.op('dve', ['rtmp', ('rope', which)], [('rope', which)], lambda e, dst=dst: e.tensor_tensor(dst[:], dst[:], tmp[:], ALU.add))
            S.op('dve', [('rope', which)], ['rtmp'], lambda e, dst=dst: e.tensor_scalar(tmp[:], dst[:], -PI, 2 * PI, ALU.is_lt, ALU.mult))
            S.op('dve', ['rtmp', ('rope', which)], [('rope', which)], lambda e, dst=dst: e.tensor_tensor(dst[:], dst[:], tmp[:], ALU.add))
            S.op('dve', [('rope', which)], [('rope', which)], lambda e, dst=dst: e.tensor_scalar(dst[:], dst[:], 3.1415925, -3.1415925, ALU.min, ALU.max))
            S.op('act', [('rope', which)], [('rope', which)], lambda e, dst=dst: e.activation(dst[:], dst[:], AF.Sin))
        S.op('dve', [('rope', 0), 'cst'], [('rope', 0)], lambda e: e.tensor_scalar(self.ropeS[:], rs[:], sgn, None, ALU.mult))
        S.op('dve', [('rope', 1)], [('rope', 1)], lambda e: e.tensor_copy(self.ropeC[:], rc[:]))
        self.dump('ropeC', rc[:], ('rope', 1))
        self.dump('ropeS', rs[:], ('rope', 0))

    def load_xT(self):
        S, sb = self.S, self.sb
        self.xt_buf = [sb("xtile%d" % i, [128, D]) for i in range(2)]
        for tt in range(16):
            xt = self.xt_buf[tt % 2]
            k = ('xtile', tt % 2)
            S.dma('sp', [], [k], lambda e, xt=xt, tt=tt: e.dma_start(out=xt[:], in_=self.x_in[tt * 128:(tt + 1) * 128, :]))
            self.to_featmajor(xt, k, self.xT, 'xT', tt)

    def to_featmajor(self, tile, tkey, dstT, dkey, tt):
        S = self.S
        for half in range(2):
            ps = self.PA[:, half * 512:(half + 1) * 512]
            pk = ('PA', half)
            self.tps([tkey, 'cst'], [pk], [(ps[:, j * 128:(j + 1) * 128], tile[:, (half * 4 + j) * 128:(half * 4 + j + 1) * 128]) for j in range(4)], self.ident_f)
            eng = 'act' if half == 0 else 'dve'
            if eng == 'act':
                S.op('act', [pk], [(dkey, tt)], lambda e, ps=ps, half=half: e.activation(
                    dstT[:, half * 4:(half + 1) * 4, tt * 128:(tt + 1) * 128], ps.rearrange("p (c t) -> p c t", c=4), AF.Copy))
            else:
                S.op('dve', [pk], [(dkey, tt)], lambda e, ps=ps, half=half: e.tensor_copy(
                    dstT[:, half * 4:(half + 1) * 4, tt * 128:(tt + 1) * 128], ps.rearrange("p (c t) -> p c t", c=4)))

    def layer(self, l):
        self.attention(l)
        if STOP_AFTER == 'att':
            raise _Stop()
        self.ssd(l)
        if STOP_AFTER == 'ssd':
            raise _Stop()
        self.merge(l)
        if STOP_AFTER == 'merge':
            raise _Stop()
        self.mix_ln1(l)
        if STOP_AFTER == 'ln1':
            raise _Stop()
        self.ffn(l)

    def psum_epoch(self):
        keys = [('PA', i) for i in range(4)] + [('PB', 0), ('PB', 1), ('PB', 0, 0), ('PB', 0, 1)] + \
               [('PB', 1, h) for h in range(4)] + ['PC', 'PCg', 'PCo', 'PD']
        self.S.alias(keys, keys)

    def dbg_qk(self, h):
        S = self.S
        if 'qa0' in self.dbg:
            qa, ka = self.qa[h], self.ka[h]
            dq = self.sb("dbgq", [128, SEQ])
            dk = self.sb("dbgk", [128, SEQ])
            S.op('dve', [('qa', h)], ['dbgq'], lambda e: e.tensor_copy(dq[0:72, :], qa[0:72, :]))
            S.op('dve', [('ka', h)], ['dbgk'], lambda e: e.tensor_copy(dk[0:72, :], ka[0:72, :]))
            self.dump('qa0', dq[0:72, :], 'dbgq')
            self.dump('ka0', dk[0:72, :], 'dbgk')

    def dbg_tile(self, name, ap, keys, ncols):
        if name in self.dbg:
            t = self.sb("dbg_" + name, [128, ncols])
            self.S.op('dve', keys, ['dbgt_' + name], lambda e: e.tensor_copy(t[:], ap))
            self.dump(name, t[:], 'dbgt_' + name)

    def xT_keys(self):
        return [('xT', tt) for tt in range(16)]

    def attention(self, l):
        self.psum_epoch()
        S, sb, nc = self.S, self.sb, self.nc
        self.phase([self.R_W, self.R_B2, (self.R_B3[0] + 7168, self.R_B3[1])])
        if True:
            self.wqkv_b = [sb("wqkv%d" % i, [128, 8, 384], BF16) for i in range(2)]
            self.qa = [sb("qa%d" % i, [72, SEQ], BF16) for i in range(4)]
            self.ka = [sb("ka%d" % i, [72, SEQ], BF16) for i in range(4)]
            self.va = [sb("va%d" % i, [128, 16, 65], BF16) for i in range(4)]
            self.qbf = [sb("qbf%d" % i, [128, 512], BF16) for i in range(2)]
            self.t1 = [sb("t1_%d" % i, [128, 512]) for i in range(2)]
            self.t2 = [sb("t2_%d" % i, [128, 512]) for i in range(2)]
            self.atm = [sb("atm%d" % i, [128, 16, 128], BF16) for i in range(2)]
            self.km = sb("km", [64, 8])
            self.kmh = sb("kmh", [64, 8], BF16)
            self.kml = sb("kml", [64, 8], BF16)
            self.gm = sb("gm", [128, 64])
            self.m8 = sb("m8", [128, 8, 8])
            self.lt = sb("lt", [128, 64])
            self.bpad = sb("bpad", [128, 8, 72], BF16)
            self.rden = sb("rden", [128, 4])
            self.NPT = 28
            for i in range(4):
                S.op('pool', [], [('qa', i)], lambda e, i=i: e.memset(self.qa[i][64:72, :], 0.0))
                S.dma('pool', [], [('ka', i)], lambda e, i=i: e.dma_start(out=self.ka[i][64:72, :], in_=self.e8_in))
                S.op('pool', [], [('va', i)], lambda e, i=i: e.memset(self.va[i][:, :, 64:65], 1.0))
            S.op('pool', [], ['bpad'], lambda e: e.memset(self.bpad[:], 0.0))
            if STOP_AFTER == 'att_init':
                self.dbg_qk(0)
                raise _Stop()
        PT = [self.B3[:, i // 4, (i % 4) * 512:(i % 4 + 1) * 512] for i in range(self.NPT)]
        attT = self.B1
        negp = self.cst[:, C_NEGP:C_NEGP + 64]
        floorb = self.cst[:, C_FLOOR:C_FLOOR + 64]
        pt_rr = [0]
        PDb = self.PD[:].bitcast(BF16)
        xk = self.xT_keys()

        def load_w(hp):
            wb = self.wqkv_b[hp % 2]
            S.dma('pool', [], [('wqkv', hp % 2, 0)], lambda e, wb=wb, hp=hp: e.dma_start(
                out=wb[:], in_=self.wqkv[l, hp].rearrange("p (k n) -> p k n", k=8), max_dma_last_dim=4096))
        load_w(0)
        Wl = [[] for _ in range(9)]
        Pl = [[] for _ in range(9)]
        Cl = [[] for _ in range(9)]
        for hp in range(8):
            wb = self.wqkv_b[hp % 2]
            wk = ('wqkv', hp % 2, 0)
            wk1 = wk2 = wk3 = wk
            self.rec = Wl[hp + 1]
            if STOP_AFTER == 'att_w0':
                self.dbg_tile('wb0', wb[:].rearrange("p a b -> p (a b)"), [wk, wk1, wk2, wk3], 3072)
                raise _Stop()
            if hp + 1 < 8:
                load_w(hp + 1)
            if STOP_AFTER == 'att_w1':
                self.dbg_tile('wb0', wb[:].rearrange("p a b -> p (a b)"), [wk, wk1, wk2, wk3], 3072)
                raise _Stop()
            par = (hp % 2) * 2
            hA, hB = par, par + 1
            self.rec = Pl[hp]
            cnt = 0
            for which in (1, 0):
                dst = self.ka if which == 1 else self.qa
                dn = 'ka' if which == 1 else 'qa'
                for tc in range(4):
                    bank = cnt % 4
                    cnt += 1
                    ps = self.PA[:, bank * 512:(bank + 1) * 512]
                    pk = ('PA', bank)
                    cols = slice(tc * 512, (tc + 1) * 512)
                    def stopat(ch, ap=None, keys=None):
                        if STOP_AFTER == 'att_qk1' + ch:
                            if ap is not None:
                                self.dbg_tile('probe', ap, keys, 512)
                            raise _Stop()
                    self.mm([wk, wk1, wk2, wk3] + xk[tc * 4:(tc + 1) * 4], [pk], ps,
                            [(wb[:, kc, which * 128:(which + 1) * 128], self.xT[:, kc, cols]) for kc in range(8)])
                    stopat('a', ps, [pk])
                    qb = self.qbf[cnt % 2]
                    qk_ = ('qbf', cnt % 2)
                    if VARIANT != 6:
                        S.op('act', [pk], [qk_], lambda e, qb=qb, ps=ps: e.activation(qb[:], ps, AF.Copy))
                    stopat('b', qb[:], [qk_])
                    t1 = self.t1[cnt % 2]
                    t2 = self.t2[cnt % 2]
                    k1 = ('t1', cnt % 2)
                    k2 = ('t2', cnt % 2)
                    if VARIANT == 4:
                        t1 = self.sb("t1x", [128, 512])
                    if VARIANT in (5, 8):
                        S.op('dve', [('rope', 1)], [k1], lambda e, t1=t1, ps=ps, cols=cols: e.tensor_copy(t1[:], self.ropeC[:, cols]))
                    elif VARIANT in (1, 4):
                        S.op('dve', [pk, ('rope', 1)], [k1], lambda e, t1=t1, ps=ps, cols=cols: e.tensor_copy(t1[:], ps))
                    elif VARIANT == 2:
                        S.op('dve', [pk, ('rope', 1)], [k1], lambda e, t1=t1, ps=ps, cols=cols: e.tensor_copy(t1[:], self.ropeC[:, cols]))
                    elif VARIANT == 3:
                        S.op('dve', [pk, ('rope', 1)], [k1], lambda e, t1=t1, ps=ps, cols=cols: e.tensor_tensor(t1[:], ps, self.cst[:, 0:512], ALU.mult))
                    else:
                        S.op('dve', [pk, ('rope', 1)], [k1], lambda e, t1=t1, ps=ps, cols=cols: e.tensor_tensor(t1[:], ps, self.ropeC[:, cols], ALU.mult))
                    stopat('c', qb[:] if VARIANT == 7 else t1[:], [qk_, k1] if VARIANT in (7, 8) else [k1])
                    rb = (bank + 2) % 4
                    pr = self.PA[:, rb * 512:(rb + 1) * 512]
                    prk = ('PA', rb)
                    self.mm([qk_, 'cbf'], [prk], pr, [(self.rm_b, qb[:])])
                    stopat('d', pr, [prk])
                    S.op('dve', [prk, ('rope', 0)], [k2], lambda e, t2=t2, pr=pr, cols=cols: e.tensor_tensor(t2[:], pr, self.ropeS[:, cols], ALU.mult))
                    stopat('e', t2[:], [k2])
                    S.op('dve', [k1, k2], [(dn, hA)], lambda e, t1=t1, t2=t2, dst=dst, cols=cols, hA=hA: e.tensor_tensor(dst[hA][0:64, cols], t1[0:64, :], t2[0:64, :], ALU.add))
                    stopat('f', dst[hA][0:64, cols], [(dn, hA)])
                    S.op('dve', [k1, k2], [(dn, hB)], lambda e, t1=t1, t2=t2, dst=dst, cols=cols, hB=hB: e.tensor_tensor(dst[hB][0:64, cols], t1[64:128, :], t2[64:128, :], ALU.add))
                    stopat('g', dst[hB][0:64, cols], [(dn, hB)])
            if STOP_AFTER == 'att_qk':
                raise _Stop()
            for tq in range(4):
                bank = cnt % 4
                cnt += 1
                ps = self.PA[:, bank * 512:(bank + 1) * 512]
                pk = ('PA', bank)
                self.mmseq([wk, wk1, wk2, wk3] + xk[tq * 4:(tq + 1) * 4], [pk],
                           [(ps[:, j * 128:(j + 1) * 128],
                             [(self.xT[:, kc, (tq * 4 + j) * 128:(tq * 4 + j + 1) * 128], wb[:, kc, 256:384]) for kc in range(8)])
                            for j in range(4)])
                psv = ps.rearrange("p (t c) -> p t c", t=4)
                if hp == 0 and tq == 0:
                    self.dbg_tile('psv', ps, [pk], 512)
                    self.dbg_tile('wb0', wb[:].rearrange("p a b -> p (a b)"), [wk, wk1, wk2, wk3], 3072)
                S.op('act', [pk], [('va', hA)], lambda e, psv=psv, tq=tq, hA=hA: e.activation(self.va[hA][:, tq * 4:(tq + 1) * 4, 0:64], psv[:, :, 0:64], AF.Copy))
                S.op('act', [pk], [('va', hB)], lambda e, psv=psv, tq=tq, hB=hB: e.activation(self.va[hB][:, tq * 4:(tq + 1) * 4, 0:64], psv[:, :, 64:128], AF.Copy))
            if STOP_AFTER == 'att_proj':
                self.dbg_tile('va0', self.va[0][:].rearrange("p a b -> p (a b)"), [('va', 0)], 1040)
                self.dbg_qk(hA)
                raise _Stop()
            self.rec = Cl[hp]
            atm = self.atm[hp % 2]
            ak = ('atm', hp % 2)
            for hh, hbuf in ((0, hA), (1, hB)):
                qa, ka, va = self.qa[hbuf], self.ka[hbuf], self.va[hbuf]
                qk, kk, vk = ('qa', hbuf), ('ka', hbuf), ('va', hbuf)
                S.op('dve', [kk], ['km'], lambda e, ka=ka: e.tensor_reduce(self.km[:], ka[0:64, :].rearrange("p (b t) -> p b t", b=8), AX.X, ALU.add))
                S.op('dve', ['km'], ['kmh'], lambda e: e.tensor_scalar(self.kmh[:], self.km[:], 1.0 / 256, None, ALU.mult))
                S.op('dve', ['km', 'kmh'], ['kml'], lambda e: e.scalar_tensor_tensor(self.kml[:], in0=self.km[:], scalar=1.0 / 256, in1=self.kmh[:], op0=ALU.mult, op1=ALU.subtract))
                pg = self.PC[:, 448:512]
                self.mmseq([qk, 'kmh', 'kml'], ['PCg'],
                           [(pg[:, t * 8:(t + 1) * 8], [(qa[0:64, (8 + t) * 128:(9 + t) * 128], self.kmh[:]), (qa[0:64, (8 + t) * 128:(9 + t) * 128], self.kml[:])]) for t in range(8)])
                S.op('dve', ['PCg', 'cst'], ['gm'], lambda e, pg=pg: e.tensor_tensor(self.gm[:], pg, negp, ALU.add))

                S.op('dve', ['gm'], ['m8'], lambda e: sel_max(e, self))
                S.op('dve', ['gm', 'm8'], ['lt'], lambda e: sel_lt(e, self))
                S.op('dve', ['lt', 'cst'], ['bpad'], lambda e: e.tensor_tensor(self.bpad[:, :, 64:72], self.lt[:].rearrange("p (t j) -> p t j", t=8), floorb.rearrange("p (t j) -> p t j", t=8), ALU.max))
                self.tps(['bpad', 'cbf'], ['PD'], [(PDb[0:72, t * 128:(t + 1) * 128], self.bpad[:, t, :]) for t in range(8)], self.ident_b)
                S.op('act', ['PD'], [qk], lambda e, qa=qa: e.activation(qa[64:72, 1024:2048], PDb[64:72, :], AF.Copy))
                if STOP_AFTER == 'att_gate':
                    self.dbg_qk(hA)
                    raise _Stop()
                slots = {}

                def stageA(qc):
                    sl = []
                    for kt in range(4 * qc + 4):
                        c0 = max(0, kt * 128 - qc * 512)
                        bank = kt % 2
                        ps = self.PB[:, bank * 512:(bank + 1) * 512]
                        pk = ('PB', bank)
                        self.mm([kk, qk], [pk], ps[:, c0:512], [(ka[0:72, kt * 128:(kt + 1) * 128], qa[0:72, qc * 512 + c0:(qc + 1) * 512])])
                        si = pt_rr[0]
                        pt_rr[0] = (pt_rr[0] + 1) % self.NPT
                        sl.append(si)
                        pt = PT[si]
                        S.op('act', [pk], [('PT', si)], lambda e, pt=pt, ps=ps, c0=c0: e.activation(pt[:, c0:512], ps[:, c0:512], AF.Exp, scale=0.125))
                        if kt >= 4 * qc:
                            S.op('dve', [('PT', si), 'cbf'], [('PT', si)], lambda e, pt=pt, c0=c0: e.tensor_tensor(pt[:, c0:c0 + 128], pt[:, c0:c0 + 128], self.tri_b, ALU.mult))
                    slots[qc] = sl

                def stageB(qc):
                    sl = slots[qc]
                    groups = []
                    for qi in range(4):
                        qt = 4 * qc + qi
                        groups.append((self.PC[:, qi * 65:(qi + 1) * 65],
                                       [(PT[sl[kt]][:, qi * 128:(qi + 1) * 128], va[:, kt, :]) for kt in range(qt + 1)]))
                    self.mmseq([('PT', s_) for s_ in sl] + [vk], ['PCo'], groups)
                    pv = self.PC[:, 0:260].rearrange("p (q c) -> p q c", q=4)
                    S.op('dve', ['PCo'], ['rden'], lambda e, pv=pv: e.reciprocal(self.rden[:], pv[:, :, 64]))

                    def norm(e, atm=atm, hh=hh, qc=qc):
                        ins = None
                        for qi in range(4):
                            ins = e.tensor_scalar(atm[:, 4 * qc + qi, hh * 64:(hh + 1) * 64], self.PC[:, qi * 65:qi * 65 + 64], self.rden[:, qi:qi + 1], None, ALU.mult)
                        return ins
                    S.op('dve', ['PCo', 'rden'], [ak], norm)

                stageA(0)
                for qc in range(4):
                    if qc + 1 < 4:
                        stageA(qc + 1)
                    stageB(qc)
            if hp == 0:
                self.dbg_tile('va0', self.va[0][:].rearrange("p a b -> p (a b)"), [('va', 0)], 1040)
                self.dbg_tile('atm0', atm[:].rearrange("p a b -> p (a b)"), [ak], 2048)
            for half in range(2):
                self.tps([ak, 'cbf'], ['PD'], [(PDb[:, t * 128:(t + 1) * 128], atm[:, half * 8 + t, :]) for t in range(8)], self.ident_b)
                S.op('act', ['PD'], [('attT', hp, half)], lambda e, half=half, hp=hp: e.activation(attT[:, hp, half * 1024:(half + 1) * 1024], PDb, AF.Copy))
            self.rec = None
        for t in Pl[0]:
            t()
        for hp in range(8):
            for t in Wl[hp + 1]:
                t()
            A, Bn = Cl[hp], Pl[hp + 1]
            n1, n2 = len(A), len(Bn)
            i1 = i2 = 0
            while i1 < n1 or i2 < n2:
                if i2 >= n2 or (i1 < n1 and i1 * n2 <= i2 * n1):
                    A[i1]()
                    i1 += 1
                else:
                    Bn[i2]()
                    i2 += 1
        if 'attT' in self.dbg:
            self.dbgf = self.sb("dbgf", [128, 512])
            for hp in range(8):
                for q4 in range(4):
                    S.op('dve', [('attT', hp, 0), ('attT', hp, 1), ('dbg', 'attT')], ['dbgf'], lambda e, hp=hp, q4=q4: e.tensor_copy(self.dbgf[:, 0:512], attT[:, hp, q4 * 512:(q4 + 1) * 512]))
                    S.dma('sp', ['dbgf'], [('dbg', 'attT')], lambda e, hp=hp, q4=q4: e.dma_start(out=self.dbg['attT'][hp * 128:(hp + 1) * 128, q4 * 512:(q4 + 1) * 512], in_=self.dbgf[:, 0:512]))
            self.final_keys.append(('dbg', 'attT'))

    def ssd(self, l):
        self.psum_epoch()
        S, sb, nc = self.S, self.sb, self.nc
        self.phase([self.R_W, self.R_B3])
        if True:
            self.wssm_b = [sb("wssm0", [128, 8, 768], BF16)] * 2
            self.wdt_b = sb("wdtb", [128, 8, 32], BF16)
            self.smallb = sb("smallb", [128, 96])
            self.cwb = sb("cwb", [128, 8, 4, 5])
            self.dt_all = sb("dt_all", [128, 16, 32])
            self.adt_all = sb("adt_all", [128, 16, 32])
            self.acs_all = sb("acs_all", [128, 16, 32])
            self.dst_all = sb("dst_all", [128, 16, 32])
            self.ea_all = sb("ea_all", [128, 16, 32])
            self.lastb = sb("lastb", [128, 8, 32])
            self.cdb = sb("cdb", [128, 8, 32])
            self.a_b = sb("a_b", [128, 32])
            self.uext = [sb("uext%d" % i, [128, 4, 259], BF16) for i in range(2)]
            self.dg = sb("dg", [128, 4, 4, 128], BF16)
            self.xh = sb("xh", [128, 4, 256])
            self.tnh = sb("tnh", [128, 4, 256])
            self.bcb2 = [sb("bcb%d" % i, [128, 2, 256], BF16) for i in range(2)]
            self.bcf = sb("bcf", [128, 256])
            self.xs_tok2 = [sb("xs_tok%d" % i, [128, 2, 256], BF16) for i in range(2)]
            self.xdt2 = [sb("xdt%d" % i, [128, 2, 256], BF16) for i in range(2)]
            self.xw2 = [sb("xw%d" % i, [128, 2, 256], BF16) for i in range(2)]
            self.btok2 = [sb("btok%d" % i, [128, 2, 128], BF16) for i in range(2)]
            self.cb32 = [sb("cb3_%d" % i, [128, 3, 128]) for i in range(2)]
            self.dtmp = [sb("dtmp%d" % i, [128, 3, 128]) for i in range(2)]
            self.mt = [sb("mt%d" % i, [128, 3, 128], BF16) for i in range(2)]
            self.prev_f = sb("prev_f", [128, 256])
            self.prev_b = sb("prev_b", [128, 256], BF16)
            self.tz = sb("tz", [128, 2, 256])
            self.yv = sb("yv", [128, 2, 256])
            self.ss = sb("ss", [128, 2])
            self.rstd = sb("rstd", [128, 2])
            self.nwb = [sb("nwb%d" % i, [128, 256]) for i in range(2)]
            self.xsf = self.xh[:, 0:2, :]
            self.zs = self.tz
            self.h2 = self.yv
            self.yn = self.yv
            self.junk = sb("junk", [128, 256], BF16)
        xk = self.xT_keys()
        S.dma('sp', [], ['smallb'], lambda e: e.dma_start(out=self.smallb[:], in_=self.small[l].broadcast_to([128, 96])))
        S.dma('sp', [], ['cwb'], lambda e: e.dma_start(out=self.cwb[:], in_=self.cw[l].rearrange("p (g c k) -> p g c k", g=8, c=4)))
        S.dma('pool', [], ['wdtb'], lambda e: e.dma_start(out=self.wdt_b[:], in_=self.wdt[l].rearrange("p (k n) -> p k n", k=8)))
        S.op('pool', ['cwb'], ['cwb'], lambda e: e.tensor_scalar(self.cwb[:], self.cwb[:], 0.5, None, ALU.mult))
        dtb = self.smallb[:, 0:32]
        alog = self.smallb[:, 32:64]
        dsk = self.smallb[:, 64:96]
        S.op('act', ['smallb'], ['a_b'], lambda e: e.activation(self.a_b[:], alog, AF.Exp))
        S.op('dve', ['a_b'], ['a_b'], lambda e: e.tensor_scalar(self.a_b[:], self.a_b[:], -1.0, None, ALU.mult))
        psd = self.PA[:, 0:512]
        self.mmseq(['wdtb'] + xk, [('PA', 0)],
                   [(psd[:, tt * 32:(tt + 1) * 32], [(self.xT[:, kc, tt * 128:(tt + 1) * 128], self.wdt_b[:, kc, :]) for kc in range(8)]) for tt in range(16)])
        dtf = self.dt_all[:].rearrange("p t h -> p (t h)")
        psd3 = psd.rearrange("p (t h) -> p t h", t=16)
        S.op('dve', [('PA', 0), 'smallb'], ['dt_all'], lambda e: e.tensor_tensor(self.dt_all[:], psd3, dtb.unsqueeze(1).broadcast_to([128, 16, 32]), ALU.add))
        S.op('act', ['dt_all'], ['dt_all'], lambda e: e.activation(dtf, dtf, AF.Exp))
        S.op('act', ['dt_all'], ['dt_all'], lambda e: e.activation(dtf, dtf, AF.Ln, bias=1.0))
        S.op('dve', ['dt_all', 'a_b'], ['adt_all'], lambda e: e.tensor_tensor(self.adt_all[:], self.dt_all[:], self.a_b[:].unsqueeze(1).broadcast_to([128, 16, 32]), ALU.mult))
        psa = self.PA[:, 512:1024]
        psl = self.PA[:, 1024:1280]
        groups = []
        for c in range(8):
            a0 = self.adt_all[:, 2 * c, :]
            a1 = self.adt_all[:, 2 * c + 1, :]
            groups.append((psa[:, (2 * c) * 32:(2 * c + 1) * 32], [(self.tri_f, a0)]))
            groups.append((psa[:, (2 * c + 1) * 32:(2 * c + 2) * 32], [(self.ones_f, a0), (self.tri_f, a1)]))
        self.mmseq(['adt_all', 'cst'], [('PA', 1)], groups)
        self.mmseq(['adt_all', 'cst'], [('PA', 2)],
                   [(psl[:, c * 32:(c + 1) * 32], [(self.ones_f, self.adt_all[:, 2 * c, :]), (self.ones_f, self.adt_all[:, 2 * c + 1, :])]) for c in range(8)])
        acsf = self.acs_all[:].rearrange("p t h -> p (t h)")
        S.op('dve', [('PA', 1)], ['acs_all'], lambda e: e.tensor_copy(acsf, psa))
        S.op('dve', [('PA', 2)], ['lastb'], lambda e: e.tensor_copy(self.lastb[:].rearrange("p c h -> p (c h)"), psl))
        S.op('dve', ['lastb', 'acs_all'], ['dst_all'], lambda e: e.tensor_tensor(
            self.dst_all[:].rearrange("p (c s) h -> p c s h", c=8), self.lastb[:].unsqueeze(2).broadcast_to([128, 8, 2, 32]),
            self.acs_all[:].rearrange("p (c s) h -> p c s h", c=8), ALU.subtract))
        dstf = self.dst_all[:].rearrange("p t h -> p (t h)")
        S.op('act', ['dst_all'], ['dst_all'], lambda e: e.activation(dstf, dstf, AF.Exp))
        S.op('dve', ['dst_all', 'dt_all'], ['dst_all'], lambda e: e.tensor_tensor(self.dst_all[:], self.dst_all[:], self.dt_all[:], ALU.mult))
        S.op('act', ['acs_all'], ['ea_all'], lambda e: e.activation(self.ea_all[:].rearrange("p t h -> p (t h)"), acsf, AF.Exp))
        S.op('act', ['lastb'], ['cdb'], lambda e: e.activation(self.cdb[:].rearrange("p c h -> p (c h)"), self.lastb[:].rearrange("p c h -> p (c h)"), AF.Exp))
        self.dump('dt_all', self.dt_all[:].rearrange("p t h -> p (t h)"), 'dt_all')
        self.dump('acs_all', acsf, 'acs_all')

        P0 = self.PA[:, 0:512]
        P1 = self.PA[:, 512:1024]
        P2 = self.PA[:, 1024:1536]
        P3 = self.PA[:, 1536:2048]
        P4 = self.PB[:, 0:512]
        P5 = self.PB[:, 512:1024]
        P6 = self.PC[:, 0:512]
        P7 = self.PD[:, 0:512]
        uprev = None
        def load_g(g):
            wb = self.wssm_b[g % 2]
            S.dma('pool', [], [('wssm', 0, 0)], lambda e, wb=wb, g=g: e.dma_start(
                out=wb[:], in_=self.wssm[l, g].rearrange("p (k n) -> p k n", k=8), max_dma_last_dim=4096))
            nw = self.nwb[g % 2]
            S.dma('sp', [], [('nwb', g % 2)], lambda e, nw=nw, g=g: e.dma_start(out=nw[:], in_=self.normw[l][:, g * 256:(g + 1) * 256].broadcast_to([128, 256])))
        load_g(0)
        for g in range(8):
            wb = self.wssm_b[g % 2]
            wk = ('wssm', 0, 0)
            wkall = [('wssm', 0, 0)]
            nw = self.nwb[g % 2]
            nk = ('nwb', g % 2)
            if g > 0:
                load_g(g)
            hs = slice(4 * g, 4 * g + 4)

            def build_dg(e, g=g):
                ins = None
                for ci in range(4):
                    for k in range(4):
                        ins = e.tensor_scalar(self.dg[:, ci, k, :], self.ident_b, self.cwb[:, g, ci, k:k + 1], None, ALU.mult)
                return ins
            S.op('dve', ['cwb', 'cbf'], ['dg'], build_dg)
            prevB = None
            for c in range(8):
                cols = slice(c * 256, (c + 1) * 256)
                sl = c % 2
                bcb, xs_tok, xw, btok = self.bcb2[sl], self.xs_tok2[sl], self.xw2[sl], self.btok2[sl]
                kbcb, kxs, kxw, kbt = ('bcb', sl), ('xs_tok', sl), ('xw', sl), ('btok', sl)
                xdt, cb3 = self.xdt2[sl], self.cb32[sl]
                kxdt, kcb3 = ('xdt', sl), ('cb3', sl)
                F1, F2, Bq = [], [], []
                self.rec = F1
                xkc = xk[2 * c:2 * c + 2]
                ue = self.uext[c % 2]
                uk = ('uext', c % 2)
                self.mmseq(wkall + xkc, [('PA', 0)],
                           [(P0[:, j * 256:(j + 1) * 256], [(wb[:, kc, 256 + j * 128:256 + (j + 1) * 128], self.xT[:, kc, cols]) for kc in range(8)]) for j in range(2)])
                self.mmseq(wkall + xkc, [('PA', 1)],
                           [(P1[:, j * 256:(j + 1) * 256], [(wb[:, kc, 512 + j * 128:512 + (j + 1) * 128], self.xT[:, kc, cols]) for kc in range(8)]) for j in range(2)])
                if c == 0:
                    S.op('pool', [], [uk], lambda e, bcb=bcb, xs_tok=xs_tok, xw=xw, btok=btok, xdt=xdt, cb3=cb3, ue=ue: e.memset(ue[:, :, 0:3], 0.0))
                else:
                    up = self.uext[(c - 1) % 2]
                    S.op('pool', [('uext', (c - 1) % 2)], [uk], lambda e, bcb=bcb, xs_tok=xs_tok, xw=xw, btok=btok, xdt=xdt, cb3=cb3, ue=ue, up=up: e.tensor_copy(ue[:, :, 0:3], up[:, :, 256:259]))
                S.op('act', [('PA', 0)], [uk], lambda e, bcb=bcb, xs_tok=xs_tok, xw=xw, btok=btok, xdt=xdt, cb3=cb3, ue=ue: e.activation(ue[:, 0:2, 3:259], P0.rearrange("p (j t) -> p j t", j=2), AF.Copy))
                S.op('act', [('PA', 1)], [uk], lambda e, bcb=bcb, xs_tok=xs_tok, xw=xw, btok=btok, xdt=xdt, cb3=cb3, ue=ue: e.activation(ue[:, 2:4, 3:259], P1.rearrange("p (j t) -> p j t", j=2), AF.Copy))
                self.mmseq([uk, 'dg'], [('PA', 0)],
                           [(P0[:, j * 256:(j + 1) * 256], [(self.dg[:, j, k, :], ue[:, j, k:k + 256]) for k in range(4)]) for j in range(2)])
                self.mmseq([uk, 'dg'], [('PA', 1)],
                           [(P1[:, j * 256:(j + 1) * 256], [(self.dg[:, 2 + j, k, :], ue[:, 2 + j, k:k + 256]) for k in range(4)]) for j in range(2)])
                for ci in range(4):
                    w = self.cwb[:, g, ci, :]
                    pc = (P0 if ci < 2 else P1)[:, (ci % 2) * 256:(ci % 2 + 1) * 256]
                    pk = ('PA', 0 if ci < 2 else 1)
                    S.op('act', [pk, 'cwb'], [('xh', ci)], lambda e, ci=ci, w=w, pc=pc: e.activation(self.xh[:, ci, :], pc, AF.Identity, bias=w[:, 4:5]))
                    S.op('act', [pk, 'cwb'], [('tnh', ci)], lambda e, ci=ci, w=w, pc=pc: e.activation(self.tnh[:, ci, :], pc, AF.Tanh, bias=w[:, 4:5]))
                S.op('dve', [('tnh', 0), ('tnh', 1), ('xh', 0), ('xh', 1)], [('xh', 0), ('xh', 1)], lambda e, bcb=bcb, xs_tok=xs_tok, xw=xw, btok=btok, xdt=xdt, cb3=cb3: e.scalar_tensor_tensor(self.xsf[:], in0=self.tnh[:, 0:2, :], scalar=1.0, in1=self.xh[:, 0:2, :], op0=ALU.add, op1=ALU.mult))
                S.op('dve', [('tnh', 2), ('tnh', 3), ('xh', 2), ('xh', 3)], [kbcb], lambda e, bcb=bcb, xs_tok=xs_tok, xw=xw, btok=btok, xdt=xdt, cb3=cb3: e.scalar_tensor_tensor(bcb[:], in0=self.tnh[:, 2:4, :], scalar=1.0, in1=self.xh[:, 2:4, :], op0=ALU.add, op1=ALU.mult))
                S.op('dve', [('tnh', 2), ('xh', 2)], ['bcf'], lambda e, bcb=bcb, xs_tok=xs_tok, xw=xw, btok=btok, xdt=xdt, cb3=cb3: e.scalar_tensor_tensor(self.bcf[:], in0=self.tnh[:, 2, :], scalar=1.0, in1=self.xh[:, 2, :], op0=ALU.add, op1=ALU.mult))
                self.tps([('xh', 0), ('xh', 1), 'cst'], [('PA', 0)],
                         [(P0[:, si * 256 + j * 128:si * 256 + (j + 1) * 128], self.xsf[:, j, si * 128:(si + 1) * 128]) for si in range(2) for j in range(2)], self.ident_f)
                self.tps(['bcf', 'cst'], [('PA', 1)],
                         [(P1[:, si * 128:(si + 1) * 128], self.bcf[:, si * 128:(si + 1) * 128]) for si in range(2)], self.ident_f)
                P0v = P0.rearrange("p (s h d) -> p s h d", s=2, h=4)
                S.op('act', [('PA', 0)], [kxs], lambda e, bcb=bcb, xs_tok=xs_tok, xw=xw, btok=btok, xdt=xdt, cb3=cb3: e.activation(xs_tok[:].rearrange("p s c -> p (s c)"), P0, AF.Copy))
                S.op('dve', [('PA', 0), 'dt_all'], [kxdt], lambda e, bcb=bcb, xs_tok=xs_tok, xw=xw, btok=btok, xdt=xdt, cb3=cb3, c=c, hs=hs: e.tensor_tensor(
                    xdt[:].rearrange("p s (h d) -> p s h d", h=4), P0v,
                    self.dt_all[:, 2 * c:2 * c + 2, hs].unsqueeze(3).broadcast_to([128, 2, 4, 64]), ALU.mult))
                S.op('dve', [('PA', 0), 'dst_all'], [kxw], lambda e, bcb=bcb, xs_tok=xs_tok, xw=xw, btok=btok, xdt=xdt, cb3=cb3, c=c, hs=hs: e.tensor_tensor(
                    xw[:].rearrange("p s (h d) -> p s h d", h=4), P0v,
                    self.dst_all[:, 2 * c:2 * c + 2, hs].unsqueeze(3).broadcast_to([128, 2, 4, 64]), ALU.mult))
                S.op('act', [('PA', 1)], [kbt], lambda e, bcb=bcb, xs_tok=xs_tok, xw=xw, btok=btok, xdt=xdt, cb3=cb3: e.activation(btok[:].rearrange("p s n -> p (s n)"), P1[:, 0:256], AF.Copy))
                self.mmseq([kbcb], [('PA', 3)],
                           [(P3[:, 0:256], [(bcb[:, 0, 0:128], bcb[:, 1, :])]),
                            (P3[:, 384:512], [(bcb[:, 0, 128:256], bcb[:, 1, 128:256])])])
                S.op('dve', [('PA', 3), 'cst'], [kcb3], lambda e, bcb=bcb, xs_tok=xs_tok, xw=xw, btok=btok, xdt=xdt, cb3=cb3: e.tensor_tensor(cb3[:, 0, :], P3[:, 0:128], self.tri_f, ALU.mult))
                S.op('act', [('PA', 3)], [kcb3], lambda e, bcb=bcb, xs_tok=xs_tok, xw=xw, btok=btok, xdt=xdt, cb3=cb3: e.activation(cb3[:, 1, :], P3[:, 128:256], AF.Copy))
                S.op('dve', [('PA', 3), 'cst', kcb3], [kcb3], lambda e, bcb=bcb, xs_tok=xs_tok, xw=xw, btok=btok, xdt=xdt, cb3=cb3: e.tensor_tensor(cb3[:, 2, :], P3[:, 384:512], self.tri_f, ALU.mult))
                self.rec = F2
                def bc_mm(hh):
                    hg = 4 * g + hh
                    pb = (P4 if hh % 2 == 0 else P6)[:, 0:256]
                    pbk = ('PB', 0, 0) if hh % 2 == 0 else 'PC'
                    a0 = self.adt_all[:, 2 * c, hg:hg + 1].broadcast_to([128, 128])
                    a1 = self.adt_all[:, 2 * c + 1, hg:hg + 1].broadcast_to([128, 128])
                    self.mmseq(['adt_all', 'cst'], [pbk],
                               [(pb[:, 0:128], [(a0, self.tri_f)]),
                                (pb[:, 128:256], [(a0, self.ones_f), (a1, self.tri_f)])])
                    return pb, pbk
                nxt = bc_mm(0)
                for hh in range(4):
                    hg = 4 * g + hh
                    pb, pbk = nxt
                    if hh + 1 < 4:
                        nxt = bc_mm(hh + 1)
                    dtm = self.dtmp[hh % 2]
                    dk = ('dtmp', hh % 2)
                    mt = self.mt[hh % 2]
                    mk = ('mt', hh % 2)
                    ac0 = self.acs_all[:, 2 * c, hg:hg + 1]
                    ac1 = self.acs_all[:, 2 * c + 1, hg:hg + 1]

                    def dec(e, dtm=dtm, pb=pb, ac0=ac0, ac1=ac1):
                        e.tensor_scalar(dtm[:, 0, :], pb[:, 0:128], ac0, 0.0, ALU.subtract, ALU.min)
                        e.tensor_scalar(dtm[:, 1, :], pb[:, 128:256], ac0, 0.0, ALU.subtract, ALU.min)
                        return e.tensor_scalar(dtm[:, 2, :], pb[:, 128:256], ac1, 0.0, ALU.subtract, ALU.min)
                    S.op('dve', [pbk, 'acs_all'], [dk], dec)
                    S.op('act', [dk], [dk], lambda e, dtm=dtm: e.activation(dtm[:].rearrange("p a b -> p (a b)"), dtm[:].rearrange("p a b -> p (a b)"), AF.Exp))
                    S.op('dve', [dk, kcb3], [mk], lambda e, dtm=dtm, mt=mt, cb3=cb3: e.tensor_tensor(mt[:], dtm[:], cb3[:], ALU.mult))
                    self.mmseq([mk, kxdt], [('PB', 1, hh)],
                               [(P5[:, hh * 64:(hh + 1) * 64], [(mt[:, 0, :], xdt[:, 0, hh * 64:(hh + 1) * 64])]),
                                (P5[:, 256 + hh * 64:256 + (hh + 1) * 64], [(mt[:, 1, :], xdt[:, 0, hh * 64:(hh + 1) * 64]), (mt[:, 2, :], xdt[:, 1, hh * 64:(hh + 1) * 64])])])
                self.rec = Bq
                self.mmseq(wkall + xkc, [('PA', 2)],
                           [(P2[:, si * 256:(si + 1) * 256], [(self.xT[:, kc, c * 256 + si * 128:c * 256 + (si + 1) * 128], wb[:, kc, 0:256]) for kc in range(8)]) for si in range(2)])
                S.op('act', [('PA', 2)], ['tz'], lambda e, bcb=bcb, xs_tok=xs_tok, xw=xw, btok=btok, xdt=xdt, cb3=cb3: e.activation(self.tz[:].rearrange("p s c -> p (s c)"), P2, AF.Tanh, scale=0.5))
                S.op('dve', ['tz', ('PA', 2)], ['tz'], lambda e, bcb=bcb, xs_tok=xs_tok, xw=xw, btok=btok, xdt=xdt, cb3=cb3: e.scalar_tensor_tensor(self.zs[:].rearrange("p s c -> p (s c)"), in0=self.tz[:].rearrange("p s c -> p (s c)"), scalar=1.0, in1=P2, op0=ALU.add, op1=ALU.mult))
                p5k = [('PB', 1, hh) for hh in range(4)]
                if c > 0:
                    self.mmseq([kbcb, 'prev_b'], [('PA', 2)],
                               [(P2[:, si * 256:(si + 1) * 256], [(bcb[:, 1, si * 128:(si + 1) * 128], self.prev_b[:])]) for si in range(2)])
                if c < 7:
                    self.mm([kbt, kxw], [('PB', 0, 0)], P4[:, 0:256], [(btok[:, si, :], xw[:, si, :]) for si in range(2)])
                    if c == 0:
                        S.op('act', [('PB', 0, 0)], ['prev_f'], lambda e, bcb=bcb, xs_tok=xs_tok, xw=xw, btok=btok, xdt=xdt, cb3=cb3: e.activation(self.prev_f[:], P4[:, 0:256], AF.Copy))
                    else:
                        S.op('dve', ['prev_f', 'cdb'], ['prev_f'], lambda e, bcb=bcb, xs_tok=xs_tok, xw=xw, btok=btok, xdt=xdt, cb3=cb3, c=c, hs=hs: e.tensor_tensor(
                            self.prev_f[:].rearrange("p (h d) -> p h d", h=4), self.prev_f[:].rearrange("p (h d) -> p h d", h=4),
                            self.cdb[:, c, hs].unsqueeze(2).broadcast_to([128, 4, 64]), ALU.mult))
                        S.op('dve', ['prev_f', ('PB', 0, 0)], ['prev_f'], lambda e, bcb=bcb, xs_tok=xs_tok, xw=xw, btok=btok, xdt=xdt, cb3=cb3: e.tensor_tensor(self.prev_f[:], self.prev_f[:], P4[:, 0:256], ALU.add))
                    S.op('act', ['prev_f'], ['prev_b'], lambda e, bcb=bcb, xs_tok=xs_tok, xw=xw, btok=btok, xdt=xdt, cb3=cb3: e.activation(self.prev_b[:], self.prev_f[:], AF.Copy))
                S.op('pool', [kxs, 'smallb'], ['yv'], lambda e, bcb=bcb, xs_tok=xs_tok, xw=xw, btok=btok, xdt=xdt, cb3=cb3, hs=hs: e.tensor_tensor(
                    self.yv[:].rearrange("p s (h d) -> p s h d", h=4), xs_tok[:].rearrange("p s (h d) -> p s h d", h=4),
                    dsk[:, hs].unsqueeze(1).unsqueeze(3).broadcast_to([128, 2, 4, 64]), ALU.mult))
                S.op('dve', p5k + ['yv'], ['yv'], lambda e, bcb=bcb, xs_tok=xs_tok, xw=xw, btok=btok, xdt=xdt, cb3=cb3: e.tensor_tensor(self.yv[:].rearrange("p s c -> p (s c)"), P5, self.yv[:].rearrange("p s c -> p (s c)"), ALU.add))
                if c > 0:
                    def yoff(e, c=c, g=g):
                        ins = None
                        for si in range(2):
                            for hh in range(4):
                                o = self.yv[:, si, hh * 64:(hh + 1) * 64]
                                ins = e.scalar_tensor_tensor(o, in0=P2[:, si * 256 + hh * 64:si * 256 + (hh + 1) * 64],
                                                             scalar=self.ea_all[:, 2 * c + si, 4 * g + hh:4 * g + hh + 1], in1=o, op0=ALU.mult, op1=ALU.add)
                        return ins
                    S.op('dve', [('PA', 2), 'ea_all', 'yv'], ['yv'], yoff)
                S.op('dve', ['yv', 'tz'], ['yv'], lambda e, bcb=bcb, xs_tok=xs_tok, xw=xw, btok=btok, xdt=xdt, cb3=cb3: e.tensor_tensor(self.h2[:], self.yv[:], self.zs[:], ALU.mult))
                for si in range(2):
                    S.op('act', ['yv'], ['junk', ('ss', si)], lambda e, bcb=bcb, xs_tok=xs_tok, xw=xw, btok=btok, xdt=xdt, cb3=cb3, si=si: e.activation(self.junk[:], self.h2[:, si, :], AF.Square, accum_out=self.ss[:, si:si + 1]))
                S.op('dve', [('ss', 0), ('ss', 1)], ['rstd'], lambda e, bcb=bcb, xs_tok=xs_tok, xw=xw, btok=btok, xdt=xdt, cb3=cb3: e.tensor_scalar(self.rstd[:], self.ss[:], 1.0 / 256, 4 * RMS_EPS, ALU.mult, ALU.add))
                S.op('pool', ['rstd', 'nh'], ['rstd'], lambda e, bcb=bcb, xs_tok=xs_tok, xw=xw, btok=btok, xdt=xdt, cb3=cb3: e.tensor_tensor(self.rstd[:], self.rstd[:], self.nh[:, 0:2], ALU.pow))
                for si in range(2):
                    S.op('dve', ['yv', 'rstd', nk], ['yv'], lambda e, bcb=bcb, xs_tok=xs_tok, xw=xw, btok=btok, xdt=xdt, cb3=cb3, si=si, nw=nw: e.scalar_tensor_tensor(
                        self.yn[:, si, :], in0=self.h2[:, si, :], scalar=self.rstd[:, si:si + 1], in1=nw[:], op0=ALU.mult, op1=ALU.mult))
                self.tps(['yv', 'cst'], ['PD'],
                         [(P7[:, j * 256 + si * 128:j * 256 + (si + 1) * 128], self.yn[:, si, j * 128:(j + 1) * 128]) for j in range(2) for si in range(2)], self.ident_f)
                S.op('act', ['PD'], [('yT', g, c)], lambda e, bcb=bcb, xs_tok=xs_tok, xw=xw, btok=btok, xdt=xdt, cb3=cb3, g=g, cols=cols: e.activation(self.yT[:, 2 * g:2 * g + 2, cols], P7.rearrange("p (j t) -> p j t", j=2), AF.Copy))
                self.rec = None
                X = F2 + Bq
                if prevB is None:
                    for t in F1:
                        t()
                else:
                    n1, n2 = len(F1), len(prevB)
                    i1 = i2 = 0
                    while i1 < n1 or i2 < n2:
                        if i2 >= n2 or (i1 < n1 and i1 * n2 <= i2 * n1):
                            F1[i1]()
                            i1 += 1
                        else:
                            prevB[i2]()
                            i2 += 1
                prevB = X
            for t in prevB:
                t()
        if 'yT' in self.dbg:
            if True:
                self.dbgf = self.sb("dbgf", [128, 512])
            for kc in range(16):
                for q4 in range(4):
                    S.op('dve', [('yT', g, c) for g in range(8) for c in range(8)] + [('dbg', 'yT')], ['dbgf'], lambda e, kc=kc, q4=q4: e.tensor_copy(self.dbgf[:, 0:512], self.yT[:, kc, q4 * 512:(q4 + 1) * 512]))
                    S.dma('sp', ['dbgf'], [('dbg', 'yT')], lambda e, kc=kc, q4=q4: e.dma_start(out=self.dbg['yT'][kc * 128:(kc + 1) * 128, q4 * 512:(q4 + 1) * 512], in_=self.dbgf[:, 0:512]))
            self.final_keys.append(('dbg', 'yT'))

    def merge(self, l):
        self.psum_epoch()
        S, sb = self.S, self.sb
        self.phase([self.R_W])
        if True:
            self.wc1_b = [sb("wc1b%d" % i, [128, 5120], BF16) for i in range(2)]
            self.ta = [sb("ta%d" % i, [128, 512]) for i in range(2)]
            self.tb = [sb("tb%d" % i, [128, 512]) for i in range(2)]
        mT = self.B3
        xk = self.xT_keys()
        attk = [('attT', hp, h) for hp in range(8) for h in range(2)]
        yk = [('yT', g, c) for g in range(8) for c in range(8)]
        it = 0
        def load_c(fc):
            wb = self.wc1_b[fc % 2]
            S.dma('pool', [], [('wc1', fc % 2, 0)], lambda e, wb=wb, fc=fc: e.dma_start(
                out=wb[:].rearrange("p (a b) -> p a b", a=5), in_=self.wc1[l, fc].rearrange("p (a b) -> p a b", a=5), max_dma_last_dim=4096))
        load_c(0)
        for fc in range(8):
            wb = self.wc1_b[fc % 2]
            wks = [('wc1', fc % 2, 0)]
            if fc + 1 < 8:
                load_c(fc + 1)
            wg = wb[:, 0:2048].rearrange("p (k n) -> p k n", k=8)
            wa = wb[:, 2048:3072].rearrange("p (k n) -> p k n", k=8)
            ws = wb[:, 3072:5120].rearrange("p (k n) -> p k n", k=16)
            for tc in range(4):
                cols = slice(tc * 512, (tc + 1) * 512)
                if it % 2 == 0:
                    pga, pgs, ppa, pps = (self.PA[:, i * 512:(i + 1) * 512] for i in range(4))
                    bk = [('PA', 0), ('PA', 1), ('PA', 2), ('PA', 3)]
                else:
                    pga, pgs, ppa, pps = self.PB[:, 0:512], self.PB[:, 512:1024], self.PC[:, 0:512], self.PD[:, 0:512]
                    bk = [('PB', 0), ('PB', 1), 'PC', 'PD']
                self.mm(wks + xk[tc * 4:(tc + 1) * 4], [bk[0]], pga, [(wg[:, kc, 0:128], self.xT[:, kc, cols]) for kc in range(8)])
                self.mm(wks + xk[tc * 4:(tc + 1) * 4], [bk[1]], pgs, [(wg[:, kc, 128:256], self.xT[:, kc, cols]) for kc in range(8)])
                self.mm(wks + attk, [bk[2]], ppa, [(wa[:, kc, :], self.B1[:, kc, cols]) for kc in range(8)])
                self.mm(wks + yk, [bk[3]], pps, [(ws[:, kc, :], self.yT[:, kc, cols]) for kc in range(16)])
                ta, tb = self.ta[it % 2], self.tb[it % 2]
                tak, tbk = ('ta', it % 2), ('tb', it % 2)
                it += 1
                S.op('act', [bk[0]], [tak], lambda e, ta=ta, pga=pga: e.activation(ta[:], pga, AF.Tanh, scale=0.5))
                S.op('act', [bk[1]], [tbk], lambda e, tb=tb, pgs=pgs: e.activation(tb[:], pgs, AF.Tanh, scale=0.5))
                S.op('dve', [tak, bk[2]], [tak], lambda e, ta=ta, ppa=ppa: e.scalar_tensor_tensor(ta[:], in0=ta[:], scalar=1.0, in1=ppa, op0=ALU.add, op1=ALU.mult))
                S.op('dve', [tbk, bk[3]], [tbk], lambda e, tb=tb, pps=pps: e.scalar_tensor_tensor(tb[:], in0=tb[:], scalar=1.0, in1=pps, op0=ALU.add, op1=ALU.mult))
                S.op('pool', [tak, tbk], [('mergedT', fc, tc)], lambda e, ta=ta, tb=tb, fc=fc, cols=cols: e.tensor_tensor(mT[:, fc, cols], ta[:], tb[:], ALU.add))

    def layernorm(self, t, tk, g_b, b_b, gk, i):
        S = self.S
        st = self.ln_st[i % 4]
        mv = self.ln_mv[i % 4]
        sk = ('lnst', i % 4)

        def stats(e):
            e.bn_stats(st[:, 0, :], t[:, 0:512])
            return e.bn_stats(st[:, 1, :], t[:, 512:1024])
        S.op('dve', [tk], [sk], stats)
        S.op('dve', [sk], [sk], lambda e: e.bn_aggr(mv[:, 0:2], st[:].rearrange("p a b -> p (a b)")))
        S.op('dve', [sk], [sk], lambda e: e.tensor_scalar(mv[:, 2:3], mv[:, 1:2], LN_EPS, None, ALU.add))
        S.op('pool', [sk, 'nh'], [sk], lambda e: e.tensor_tensor(mv[:, 3:4], mv[:, 2:3], self.nh[:, 0:1], ALU.pow))
        S.op('dve', [tk, sk], [tk], lambda e: e.tensor_scalar(t[:], t[:], mv[:, 0:1], mv[:, 3:4], ALU.subtract, ALU.mult))
        S.op('pool', [tk, gk], [tk], lambda e: e.tensor_tensor(t[:], t[:], g_b, ALU.mult))
        S.op('dve', [tk, gk], [tk], lambda e: e.tensor_tensor(t[:], t[:], b_b, ALU.add))

    def mix_ln1(self, l):
        self.psum_epoch()
        S, sb = self.S, self.sb
        self.phase([self.R_W, self.R_B2])
        if True:
            self.wout_b = sb("woutb", [128, 8, 1024], BF16)
            self.lnpb = sb("lnpb", [128, 4096])
            self.res = [sb("res%d" % i, [128, D]) for i in range(4)]
            self.ln_st = [sb("lnst%d" % i, [128, 2, 6]) for i in range(4)]
            self.ln_mv = [sb("lnmv%d" % i, [128, 4]) for i in range(4)]
        S.dma('pool', [], [('wout', 0)], lambda e: e.dma_start(
            out=self.wout_b[:], in_=self.wout[l].rearrange("p (k n) -> p k n", k=8), max_dma_last_dim=4096))
        lnpb1 = self.lnpb
        S.dma('sp', [], ['lnpb'], lambda e: e.dma_start(out=lnpb1[:], in_=self.lnp[l].broadcast_to([128, 4096])))
        src = self.x_in if l == 0 else self.x1
        srck = 'x_in' if l == 0 else 'x1'
        mk = [('mergedT', fc, tc) for fc in range(8) for tc in range(4)]
        wk = [('wout', 0)]
        for tt in range(16):
            r = self.res[tt % 4]
            rk = ('res', tt % 4)
            S.dma('sp', [(srck, tt)], [rk], lambda e, r=r, tt=tt: e.dma_start(out=r[:], in_=src[tt * 128:(tt + 1) * 128, :]))
            S.op('act', [rk], [rk], lambda e, r=r: e.activation(r[:], r[:], AF.Copy, scale=float(ALPHA)))
            if tt % 2 == 0:
                pm = self.PB[:, 0:1024]
                rkeys = [('PB', 0), ('PB', 1)]
            else:
                pm = self.PA[:, 0:1024]
                rkeys = [('PA', 0), ('PA', 1)]
            self.mmseq(mk + wk, rkeys,
                       [(pm[:, half * 512:(half + 1) * 512], [(self.B3[:, kc, tt * 128:(tt + 1) * 128], self.wout_b[:, kc, half * 512:(half + 1) * 512]) for kc in range(8)]) for half in range(2)])
            S.op('dve', rkeys + [rk], [rk], lambda e, r=r, pm=pm: e.scalar_tensor_tensor(r[:], in0=pm, scalar=0.5, in1=r[:], op0=ALU.mult, op1=ALU.add))
            self.layernorm(r, rk, self.lnpb[:, 0:1024], self.lnpb[:, 1024:2048], 'lnpb', tt)
            S.dma('sp', [rk], [('hres', tt)], lambda e, r=r, tt=tt: e.dma_start(out=self.hres[tt * 128:(tt + 1) * 128, :], in_=r[:]))
            self.to_featmajor2(r, rk, self.B1, 'hT', tt)
        if 'hT' in self.dbg:
            if True:
                self.dbgf = self.sb("dbgf", [128, 512])
            for kc in range(8):
                for q4 in range(4):
                    S.op('dve', [('hT', tt) for tt in range(16)] + [('dbg', 'hT')], ['dbgf'], lambda e, kc=kc, q4=q4: e.tensor_copy(self.dbgf[:, 0:512], self.B1[:, kc, q4 * 512:(q4 + 1) * 512]))
                    S.dma('sp', ['dbgf'], [('dbg', 'hT')], lambda e, kc=kc, q4=q4: e.dma_start(out=self.dbg['hT'][kc * 128:(kc + 1) * 128, q4 * 512:(q4 + 1) * 512], in_=self.dbgf[:, 0:512]))
            self.final_keys.append(('dbg', 'hT'))

    def to_featmajor2(self, tile, tkey, dstT, dkey, tt):
        S = self.S
        for half in range(2):
            ps = self.PC[:, 0:512] if half == 0 else self.PD[:, 0:512]
            pk = 'PC' if half == 0 else 'PD'
            self.tps([tkey, 'cst'], [pk], [(ps[:, j * 128:(j + 1) * 128], tile[:, (half * 4 + j) * 128:(half * 4 + j + 1) * 128]) for j in range(4)], self.ident_f)
            S.op('act', [pk], [(dkey, tt)], lambda e, ps=ps, half=half: e.activation(
                dstT[:, half * 4:(half + 1) * 4, tt * 128:(tt + 1) * 128], ps.rearrange("p (c t) -> p c t", c=4), AF.Copy))

    def ffn(self, l):
        self.psum_epoch()
        S, sb = self.S, self.sb
        self.phase([self.R_W, (self.R_B3[0] + 4096, self.R_B3[1])])
        if True:
            self.wup_b = [sb("wupb%d" % i, [128, 8, 512], BF16) for i in range(2)]
            self.wdn_b = [sb("wdnb%d" % i, [128, 4, 1024], BF16) for i in range(2)]
            self.rl = [sb("rl%d" % i, [128, 512]) for i in range(2)]
            self.ln_st = [sb("lnst%d" % i, [128, 2, 6]) for i in range(4)]
            self.ln_mv = [sb("lnmv%d" % i, [128, 4]) for i in range(4)]
        uT = self.B3
        hk = [('hT', tt) for tt in range(16)]
        it = 0
        def load_f(gi):
            wu = self.wup_b[gi % 2]
            wd = self.wdn_b[gi % 2]
            wuk, wdk = ('wup', gi % 2), ('wdn', gi % 2)
            S.dma('pool', [], [(wuk, 0)], lambda e, wu=wu, gi=gi: e.dma_start(
                out=wu[:], in_=self.wup[l, gi].rearrange("p (k n) -> p k n", k=8), max_dma_last_dim=4096))
            S.dma('pool', [], [(wdk, 0)], lambda e, wd=wd, gi=gi: e.dma_start(
                out=wd[:], in_=self.wdn[l, gi].rearrange("p (k n) -> p k n", k=4), max_dma_last_dim=4096))
        load_f(0)
        for gi in range(8):
            wu = self.wup_b[gi % 2]
            wd = self.wdn_b[gi % 2]
            wuk, wdk = ('wup', gi % 2), ('wdn', gi % 2)
            if gi + 1 < 8:
                load_f(gi + 1)
            wuks = [(wuk, 0)]
            wdks = [(wdk, 0)]
            for j in range(4):
                for tc in range(4):
                    bank = it % 4
                    ps = [self.PA[:, 0:512], self.PA[:, 512:1024], self.PC[:, 0:512], self.PD[:, 0:512]][bank]
                    pk = [('PA', 0), ('PA', 1), 'PC', 'PD'][bank]
                    cols = slice(tc * 512, (tc + 1) * 512)
                    self.mm(wuks + hk[tc * 4:(tc + 1) * 4], [pk], ps, [(wu[:, kc, j * 128:(j + 1) * 128], self.B1[:, kc, cols]) for kc in range(8)])
                    rl = self.rl[it % 2]
                    rlk = ('rl', it % 2)
                    S.op('act', [pk], [rlk], lambda e, rl=rl, ps=ps: e.activation(rl[:], ps, AF.Relu))
                    eng = 'dve'
                    S.op(eng, [rlk], [('uT', j)], lambda e, rl=rl, j=j, cols=cols: e.tensor_tensor(uT[:, j, cols], rl[:], rl[:], ALU.mult))
                    it += 1
            for tt in range(16):
                if tt % 2 == 0:
                    pm = self.PB[:, 0:1024]
                    rkeys = [('PB', 0), ('PB', 1)]
                else:
                    pm = self.PA[:, 1024:2048]
                    rkeys = [('PA', 2), ('PA', 3)]
                self.mmseq([('uT', j) for j in range(4)] + wdks, rkeys,
                           [(pm[:, half * 512:(half + 1) * 512], [(uT[:, j, tt * 128:(tt + 1) * 128], wd[:, j, half * 512:(half + 1) * 512]) for j in range(4)]) for half in range(2)])
                if gi == 0:
                    S.op('act', rkeys, [('acc', tt)], lambda e, tt=tt, pm=pm: e.activation(self.acc[:, tt, :], pm, AF.Copy))
                else:
                    S.op('dve', rkeys + [('acc', tt)], [('acc', tt)], lambda e, tt=tt, pm=pm: e.tensor_tensor(self.acc[:, tt, :], pm, self.acc[:, tt, :], ALU.add))
        S.barrier()
        self.res = [self.wup_b[i][:, 4 * j:4 * j + 4, :].rearrange("p a b -> p (a b)").bitcast(F32) for i in range(2) for j in range(2)]
        self.lnpb = self.wdn_b[0][:].rearrange("p a b -> p (a b)").bitcast(F32)
        lnpb2 = self.lnpb
        S.dma('sp', [], ['lnpb'], lambda e: e.dma_start(out=lnpb2, in_=self.lnp[l][:, 2048:4096].broadcast_to([128, 2048])))
        dst = self.out if l == DEPTH - 1 else self.x1
        dstk = 'out' if l == DEPTH - 1 else 'x1'
        for tt in range(16):
            r = self.res[tt % 4]
            rk = ('res', tt % 4)
            S.dma('sp', [('hres', tt)], [rk], lambda e, r=r, tt=tt: e.dma_start(out=r[:], in_=self.hres[tt * 128:(tt + 1) * 128, :]))
            S.op('dve', [rk, ('acc', tt)], [rk], lambda e, r=r, tt=tt: e.scalar_tensor_tensor(r[:], in0=r[:], scalar=float(ALPHA), in1=self.acc[:, tt, :], op0=ALU.mult, op1=ALU.add))
            self.layernorm(r, rk, self.lnpb[:, 0:1024], self.lnpb[:, 1024:2048], 'lnpb', tt)
            S.dma('sp', [rk], [(dstk, tt)], lambda e, r=r, tt=tt: e.dma_start(out=dst[tt * 128:(tt + 1) * 128, :], in_=r[:]))
            if l == DEPTH - 1:
                self.final_keys.append((dstk, tt))
            else:
                self.to_featmajor2(r, rk, self.xT, 'xT', tt)


def sel_max(e, b):
    ins = None
    for t in range(8):
        ins = e.max(b.m8[:, t, :], b.gm[:, t * 8:(t + 1) * 8])
    return ins


def sel_lt(e, b):
    ins = None
    for t in range(8):
        ins = e.tensor_scalar(b.lt[:, t * 8:(t + 1) * 8], b.gm[:, t * 8:(t + 1) * 8], b.m8[:, t, 2:3], NEG, ALU.is_lt, ALU.mult)
    return ins


_CACHE = {}


def kernel(**inputs):
    inp = {k: np.asarray(v) for k, v in inputs.items()}
    w = prep_weights(inp)
    consts, e8 = make_consts()
    x = np.ascontiguousarray(inp['x'], dtype=np.float32)
    pos = np.ascontiguousarray(inp['positions'], dtype=np.int32)
    nc = Builder().build()
    in_maps = []
    for b in range(NCORES):
        m = {"x": x[b], "pos": pos[b:b + 1], "consts": consts, "e8": e8}
        m.update(w)
        in_maps.append(m)
    res = run_bass_kernel_spmd(nc, in_maps, core_ids=list(range(NCORES)))
    out = np.stack([np.asarray(r["out"], dtype=np.float32) for r in res.results], axis=0)
    return out
```

```python
import math
import numpy as np
import concourse.bass as bass
import concourse.mybir as mybir
from concourse.bass_utils import run_bass_kernel_spmd

F32 = mybir.dt.float32
BF16 = mybir.dt.bfloat16
I32 = mybir.dt.int32
AF = mybir.ActivationFunctionType
ALU = mybir.AluOpType
AX = mybir.AxisListType

D = 1024
SEQ = 2048
DEPTH = 2
NCORES = 8
IN_W = 11296
ALPHA = (2 * DEPTH) ** 0.25
LN_EPS = 1e-5
RMS_EPS = 1e-5
NEG = -1.0e5
PI = math.pi

DEBUG = {}
STOP_AFTER = None
NO_BARRIER = False
VARIANT = 0


class Sched:
    ENG = ('pe', 'act', 'dve', 'pool', 'sp')

    def __init__(self, nc, n_dma_sems=24):
        self.nc = nc
        self.sem = {e: nc.alloc_semaphore('s_' + e) for e in self.ENG}
        self.cnt = {e: 0 for e in self.ENG}
        self.dsem = [nc.alloc_semaphore('d%d' % i) for i in range(n_dma_sems)]
        self.dcnt = [0] * n_dma_sems
        self.drr = {'sp': 0, 'pool': 0}
        self.dpool = {'sp': list(range(0, n_dma_sems // 2)), 'pool': list(range(n_dma_sems // 2, n_dma_sems))}
        self.waited = {}
        self.last_w = {}
        self.readers = {}
        self.q = {e: [] for e in self.ENG}

    def barrier(self):
        if NO_BARRIER is True:
            return
        self.epoch = {}
        scr = self.bar_scratch
        d_sp = {('dma', i): self.dcnt[i] for i in self.dpool['sp'] if self.dcnt[i] > 0}
        d_pl = {('dma', i): self.dcnt[i] for i in self.dpool['pool'] if self.dcnt[i] > 0}
        self._token('act', d_sp, lambda e: e.memzero(scr[:, 0:1]))
        t1 = ('act', self.cnt['act'])
        d_pl[t1[0]] = t1[1]
        self._token('pool', d_pl, lambda e: e.memset(scr[:, 1:2], 0.0))
        d3 = {e: c for e, c in self.cnt.items() if c > 0}
        self._token('dve', d3, lambda e: e.memset(scr[:, 2:3], 0.0))
        self.epoch = {'dve': self.cnt['dve']}

    def _token(self, e, deps, emit):
        waits = self._waits(e, deps)
        self.cnt[e] += 1
        sem = self.sem[e]

        def thunk(eng):
            for s, v in waits:
                eng.wait_ge(s, v)
            emit(eng).then_inc(sem, 1)
        self.q[e].append(thunk)

    def _deps(self, reads, writes):
        deps = dict(getattr(self, 'epoch', {}))

        def add(src, idx):
            if deps.get(src, 0) < idx:
                deps[src] = idx
        for k in reads:
            lw = self.last_w.get(k)
            if lw is not None:
                add(*lw)
        for k in writes:
            lw = self.last_w.get(k)
            if lw is not None:
                add(*lw)
            for src, idx in self.readers.get(k, {}).items():
                add(src, idx)
        return deps

    def _waits(self, e, deps):
        out = []
        self.maxw = getattr(self, 'maxw', {})
        for src, idx in deps.items():
            if self.waited.get((e, src), 0) >= idx:
                continue
            self.waited[(e, src)] = idx
            if isinstance(src, tuple):
                out.append((self.dsem[src[1]], 16 * idx))
            else:
                out.append((self.sem[src], idx))
        self.maxw[len(out)] = self.maxw.get(len(out), 0) + 1
        return out

    def _record(self, token, reads, writes):
        src, idx = token
        for k in reads:
            r = self.readers.setdefault(k, {})
            if r.get(src, 0) < idx:
                r[src] = idx
        for k in writes:
            self.last_w[k] = (src, idx)
            self.readers[k] = {}

    @staticmethod
    def _bank(k):
        if isinstance(k, tuple) and k and k[0] == 'PA':
            return ('BANK', k[1])
        if isinstance(k, tuple) and k and k[0] == 'PB':
            return ('BANK', 4 + k[1])
        if k in ('PC', 'PCg', 'PCo'):
            return ('BANK', 6)
        if k == 'PD':
            return ('BANK', 7)
        return None

    def _canon(self, reads, writes):
        r2, w2 = [], []
        for k in reads:
            b = self._bank(k)
            if b is None:
                r2.append(k)
            elif b not in w2:
                w2.append(b)
        for k in writes:
            b = self._bank(k)
            if b is None:
                w2.append(k)
            elif b not in w2:
                w2.append(b)
        return r2, w2

    def op(self, e, reads, writes, emit):
        reads, writes = self._canon(reads, writes)
        deps = self._deps(reads, writes)
        waits = self._waits(e, deps)
        self.cnt[e] += 1
        idx = self.cnt[e]
        sem = self.sem[e]

        def thunk(eng):
            for s, v in waits:
                eng.wait_ge(s, v)
            emit(eng).then_inc(sem, 1)
        self.q[e].append(thunk)
        self._record((e, idx), reads, writes)

    def dma(self, e, reads, writes, emit):
        pool = self.dpool[e]
        s = pool[self.drr[e] % len(pool)]
        self.drr[e] += 1
        src = ('dma', s)
        deps = self._deps(reads, writes)
        if self.dcnt[s] > 0 and deps.get(src, 0) < self.dcnt[s]:
            deps[src] = self.dcnt[s]
        waits = self._waits(e, deps)
        self.dcnt[s] += 1
        idx = self.dcnt[s]
        sem = self.dsem[s]

        def thunk(eng):
            for sm, v in waits:
                eng.wait_ge(sm, v)
            emit(eng).then_inc(sem, 16)
        self.q[e].append(thunk)
        self._record((src, idx), reads, writes)

    def alias(self, old_keys, new_keys):
        acc = {}
        for k in old_keys:
            lw = self.last_w.get(k)
            if lw is not None and acc.get(lw[0], 0) < lw[1]:
                acc[lw[0]] = lw[1]
            for src, idx in self.readers.get(k, {}).items():
                if acc.get(src, 0) < idx:
                    acc[src] = idx
        for k in new_keys:
            r = self.readers.setdefault(k, {})
            for src, idx in acc.items():
                if r.get(src, 0) < idx:
                    r[src] = idx

    def finish(self, final_keys):
        self.barrier()
        deps = self._deps(final_keys, [])
        waits = self._waits('sp', deps)

        def thunk(eng):
            for s, v in waits:
                eng.wait_ge(s, v)
        self.q['sp'].append(thunk)
        nc = self.nc
        q = self.q
        with nc.Block() as block:
            @block.sync
            def _(eng):
                for t in q['sp']:
                    t(eng)

            @block.tensor
            def _(eng):
                for t in q['pe']:
                    t(eng)

            @block.scalar
            def _(eng):
                for t in q['act']:
                    t(eng)

            @block.vector
            def _(eng):
                for t in q['dve']:
                    t(eng)

            @block.gpsimd
            def _(eng):
                for t in q['pool']:
                    t(eng)


C_ID, C_TRI, C_ONES, C_RM, C_INVF, C_SGN, C_NEGP, C_FLOOR, NC_CONST = 0, 128, 256, 384, 512, 513, 514, 578, 648


def make_consts():
    c = np.zeros((128, NC_CONST), np.float32)
    c[:, C_ID:C_ID + 128] = np.eye(128, dtype=np.float32)
    c[:, C_TRI:C_TRI + 128] = np.triu(np.ones((128, 128), np.float32))
    c[:, C_ONES:C_ONES + 128] = 1.0
    rm = np.zeros((128, 128), np.float32)
    inv = (500000.0 ** (-np.arange(0, 16, 2, dtype=np.float32) / 16)).astype(np.float32)
    for blk in range(2):
        for d in range(16):
            src = d + 8 if d < 8 else d - 8
            rm[blk * 64 + src, blk * 64 + d] = 1.0
    c[:, C_RM:C_RM + 128] = rm
    for p in range(128):
        d = p % 64
        if d < 16:
            c[p, C_INVF] = inv[d % 8]
            c[p, C_SGN] = -1.0 if d < 8 else 1.0
    for qt in range(8):
        blk = (8 + qt) // 2
        for j in range(8):
            c[:, C_NEGP + qt * 8 + j] = 0.0 if j < blk else -1.0e30
            c[:, C_FLOOR + qt * 8 + j] = NEG if j < blk else 0.0
    e8 = np.zeros((8, SEQ), np.float32)
    for j in range(8):
        e8[j, j * 256:(j + 1) * 256] = 1.0
    return c, e8


def prep_weights(inp):
    out = {}
    L = DEPTH

    def kmaj(w):
        K, N = w.shape
        return w.reshape(K // 128, 128, N).transpose(1, 0, 2)

    w_in = inp['w_in']
    wqkv = np.empty((L, 8, 128, 8, 384), np.float32)
    wssm = np.empty((L, 8, 128, 8, 768), np.float32)
    wdt = np.empty((L, 128, 8, 32), np.float32)
    wc1 = np.empty((L, 8, 128, 5120), np.float32)
    wout = np.empty((L, 128, 8, 1024), np.float32)
    wup = np.empty((L, 8, 128, 8, 512), np.float32)
    wdn = np.empty((L, 8, 128, 4, 1024), np.float32)
    cw = np.empty((L, 128, 8, 4, 5), np.float32)
    for l in range(L):
        wi = kmaj(w_in[l])
        for hp in range(8):
            wqkv[l, hp, :, :, 0:128] = wi[:, :, hp * 128:(hp + 1) * 128]
            wqkv[l, hp, :, :, 128:256] = wi[:, :, 1024 + hp * 128:1024 + (hp + 1) * 128]
            wqkv[l, hp, :, :, 256:384] = wi[:, :, 2048 + hp * 128:2048 + (hp + 1) * 128]
        for g in range(8):
            wssm[l, g, :, :, 0:256] = wi[:, :, 3072 + g * 256:3072 + (g + 1) * 256]
            wssm[l, g, :, :, 256:512] = wi[:, :, 5120 + g * 256:5120 + (g + 1) * 256]
            wssm[l, g, :, :, 512:640] = wi[:, :, 7168 + g * 128:7168 + (g + 1) * 128]
            wssm[l, g, :, :, 640:768] = wi[:, :, 8192 + g * 128:8192 + (g + 1) * 128]
        wdt[l] = wi[:, :, 9216:9248]
        wap = kmaj(inp['w_attn_proj'][l])
        wsp = kmaj(inp['w_ssm_proj'][l])
        for fc in range(8):
            blk = np.empty((128, 8, 256), np.float32)
            blk[:, :, 0:128] = wi[:, :, 9248 + fc * 128:9248 + (fc + 1) * 128]
            blk[:, :, 128:256] = wi[:, :, 10272 + fc * 128:10272 + (fc + 1) * 128]
            wc1[l, fc, :, 0:2048] = blk.reshape(128, 2048)
            wc1[l, fc, :, 2048:3072] = wap[:, :, fc * 128:(fc + 1) * 128].reshape(128, 1024)
            wc1[l, fc, :, 3072:5120] = wsp[:, :, fc * 128:(fc + 1) * 128].reshape(128, 2048)
        wout[l] = kmaj(inp['w_out'][l])
        wu = kmaj(inp['w_up'][l])
        wd = kmaj(inp['w_down'][l])
        for gI in range(8):
            wup[l, gI] = wu[:, :, gI * 512:(gI + 1) * 512]
            wdn[l, gI] = wd[:, gI * 4:(gI + 1) * 4, :]
        cwl = inp['conv_w'][l]
        cbl = inp['conv_b'][l]
        for g in range(8):
            offs = [g * 256, g * 256 + 128, 2048 + g * 128, 3072 + g * 128]
            for ci, o in enumerate(offs):
                cw[l, :, g, ci, 0:4] = cwl[:, o:o + 128].T
                cw[l, :, g, ci, 4] = cbl[o:o + 128]
    out['wqkv'] = wqkv.reshape(L, 8, 128, 3072)
    out['wssm'] = wssm.reshape(L, 8, 128, 6144)
    out['wdt'] = wdt.reshape(L, 128, 256)
    out['wc1'] = wc1
    out['wout'] = wout.reshape(L, 128, 8192)
    out['wup'] = wup.reshape(L, 8, 128, 4096)
    out['wdn'] = wdn.reshape(L, 8, 128, 4096)
    out['cw'] = cw.reshape(L, 128, 160)
    small = np.concatenate([inp['dt_bias'], inp['a_log'], inp['d_skip']], axis=1)
    out['small'] = np.ascontiguousarray(small.reshape(L, 1, 96))
    out['normw'] = np.ascontiguousarray(inp['ssm_norm_w'].reshape(L, 1, 2048))
    lnp = np.stack([inp['ln1_g'], inp['ln1_b'], inp['ln2_g'], inp['ln2_b']], axis=1)
    out['lnp'] = np.ascontiguousarray(lnp.reshape(L, 1, 4096))
    return {k: np.ascontiguousarray(v, dtype=np.float32) for k, v in out.items()}


class Arena:
    def __init__(self, base_ap, segments):
        self.base = base_ap
        self.segs = [list(x) for x in segments]

    def alloc(self, shape, dtype=F32):
        P = shape[0]
        n = 1
        for d in shape[1:]:
            n *= d
        per = 4 if dtype in (F32, I32) else 2
        ncols = (n * per + 3) // 4
        ncols = (ncols + 7) // 8 * 8
        for sg in self.segs:
            if sg[1] - sg[0] >= ncols:
                off = sg[0]
                sg[0] += ncols
                ap = self.base[0:P, off:off + (n * per + 3) // 4]
                if dtype != F32:
                    ap = ap.bitcast(dtype)
                if len(shape) == 3:
                    ap = ap.rearrange("p (a b) -> p a b", a=shape[1])
                elif len(shape) == 4:
                    ap = ap.rearrange("p (a b c) -> p a b c", a=shape[1], b=shape[2])
                return ap
        raise RuntimeError("arena out of memory for %s" % (shape,))


class _Stop(Exception):
    pass


class Builder:
    def __init__(self):
        nc = bass.Bass("TRN2", target_bir_lowering=False)
        self.nc = nc
        self.S = Sched(nc)
        self.rec = None
        S_ = self.S
        S_._op, S_._dma = S_.op, S_.dma

        def _rop(*a):
            if self.rec is not None:
                self.rec.append(lambda: S_._op(*a))
            else:
                S_._op(*a)

        def _rdma(*a):
            if self.rec is not None:
                self.rec.append(lambda: S_._dma(*a))
            else:
                S_._dma(*a)
        S_.op, S_.dma = _rop, _rdma
        L = DEPTH
        dt = lambda n, s, d=F32, k="ExternalInput": nc.dram_tensor(n, s, d, kind=k).ap()
        self.x_in = dt("x", [SEQ, D])
        self.pos_in = dt("pos", [1, SEQ], I32)
        self.consts_in = dt("consts", [128, NC_CONST])
        self.e8_in = dt("e8", [8, SEQ])
        self.wqkv = dt("wqkv", [L, 8, 128, 3072])
        self.wssm = dt("wssm", [L, 8, 128, 6144])
        self.wdt = dt("wdt", [L, 128, 256])
        self.wc1 = dt("wc1", [L, 8, 128, 5120])
        self.wout = dt("wout", [L, 128, 8192])
        self.wup = dt("wup", [L, 8, 128, 4096])
        self.wdn = dt("wdn", [L, 8, 128, 4096])
        self.cw = dt("cw", [L, 128, 160])
        self.small = dt("small", [L, 1, 96])
        self.normw = dt("normw", [L, 1, 2048])
        self.lnp = dt("lnp", [L, 1, 4096])
        self.out = dt("out", [SEQ, D], F32, "ExternalOutput")
        self.x1 = dt("x1s", [SEQ, D], F32, "Internal")
        self.hres = dt("hres", [SEQ, D], F32, "Internal")
        self.dbg = {}
        for name, shape in DEBUG.items():
            self.dbg[name] = dt("dbg_" + name, list(shape), F32, "ExternalOutput")
        self.final_keys = []

    def sb(self, name, shape, dtype=F32):
        return self.ar.alloc(list(shape), dtype)

    def phase(self, segs):
        self.S.barrier()
        self.ar = Arena(self.arena, segs)

    def mm(self, reads, writes, out, pairs, transpose=False):
        def emit(e):
            n = len(pairs)
            ins = None
            for i, (l, r) in enumerate(pairs):
                ins = e.matmul(out, lhsT=l, rhs=r, start=(i == 0), stop=(i == n - 1))
            return ins
        self.S.op('pe', reads, writes, emit)

    def mmseq(self, reads, writes, groups):
        def emit(e):
            ins = None
            for out, pairs in groups:
                n = len(pairs)
                for i, (l, r) in enumerate(pairs):
                    ins = e.matmul(out, lhsT=l, rhs=r, start=(i == 0), stop=(i == n - 1))
            return ins
        self.S.op('pe', reads, writes, emit)

    def tps(self, reads, writes, items, ident):
        def emit(e):
            ins = None
            for o, i in items:
                ins = e.transpose(o, i, ident)
            return ins
        self.S.op('pe', reads, writes, emit)

    def dump(self, name, ap, keys):
        if name in self.dbg:
            if not isinstance(keys, list):
                keys = [keys]
            self.S.dma('sp', keys, [('dbg', name)], lambda e: e.dma_start(out=self.dbg[name], in_=ap))
            self.final_keys.append(('dbg', name))

    def build(self):
        nc, S = self.nc, self.S
        sb = self.sb
        nbytes = int(nc.sbuf_bytes_remaining) - 256
        NA = nbytes // 4 // 8 * 8
        self.arena = nc.alloc_sbuf_tensor("arena", [128, NA], F32)[:]
        self.R_P = (0, 11264)
        self.R_B1 = (11264, 19456)
        self.R_B2 = (19456, 35840)
        self.R_B3 = (35840, 44032)
        self.R_W = (44032, NA)
        assert NA - 44032 > 8000, NA
        self.ar = Arena(self.arena, [self.R_P])
        self.S.bar_scratch = sb("barscr", [128, 8])
        cst = sb("cst", [128, NC_CONST])
        self.cst = cst
        S.dma('sp', [], ['cst'], lambda e: e.dma_start(out=cst[:], in_=self.consts_in))
        self.ident_f = cst[:, C_ID:C_ID + 128]
        self.tri_f = cst[:, C_TRI:C_TRI + 128]
        self.ones_f = cst[:, C_ONES:C_ONES + 128]
        cbf = sb("cbf", [128, 512], BF16)
        S.op('dve', ['cst'], ['cbf'], lambda e: e.tensor_copy(cbf[:], cst[:, 0:512]))
        self.ident_b = cbf[:, 0:128]
        self.tri_b = cbf[:, 128:256]
        self.rm_b = cbf[:, 384:512]
        self.nh = sb("nh", [128, 8])
        S.op('pool', [], ['nh'], lambda e: e.memset(self.nh[:], -0.5))
        self.ropeC = sb("ropeC", [128, SEQ], BF16)
        self.ropeS = sb("ropeS", [128, SEQ], BF16)
        self.xT = sb("xT", [128, 8, SEQ], BF16)
        A = self.arena
        self.B1 = A[:, self.R_B1[0]:self.R_B1[1]].bitcast(BF16).rearrange("p (c t) -> p c t", c=8)
        self.B2 = A[:, self.R_B2[0]:self.R_B2[1]]
        self.B3 = A[:, self.R_B3[0]:self.R_B3[1]].bitcast(BF16).rearrange("p (c t) -> p c t", c=8)
        self.yT = self.B2.bitcast(BF16).rearrange("p (c t) -> p c t", c=16)
        self.acc = self.B2.rearrange("p (t f) -> p t f", t=16)
        self.ar = Arena(self.arena, [self.R_B2])
        self.PA = nc.alloc_psum_tensor("PA", [128, 2048], F32)
        self.PB = nc.alloc_psum_tensor("PB", [128, 1024], F32)
        self.PC = nc.alloc_psum_tensor("PC", [128, 512], F32)
        self.PD = nc.alloc_psum_tensor("PD", [128, 512], F32)
        self.rope_tables()
        if STOP_AFTER == 'rope':
            return self.finish()
        self.load_xT()
        if STOP_AFTER == 'xT':
            self.dbg_tile('xT0', self.xT[:, 3, 1024:1536], self.xT_keys(), 512)
            return self.finish()
        try:
            for l in range(DEPTH):
                self.layer(l)
                if STOP_AFTER == 'layer0':
                    break
        except _Stop:
            pass
        return self.finish()

    def finish(self):
        self.S.finish(self.final_keys)
        return self.nc

    def rope_tables(self):
        S, sb = self.S, self.sb
        cst = None
        posi = sb("posi", [128, SEQ], I32)
        S.dma('sp', [], ['posi'], lambda e: e.dma_start(out=posi[:], in_=self.pos_in.broadcast_to([128, SEQ])))
        ang = sb("ang", [128, SEQ])
        tmp = sb("rtmp", [128, SEQ])
        tmi = sb("rtmi", [128, SEQ], I32)
        rc = sb("rc", [128, SEQ])
        rs = sb("rs", [128, SEQ])
        invf = self.cst[:, C_INVF:C_INVF + 1]
        sgn = self.cst[:, C_SGN:C_SGN + 1]
        S.op('dve', ['posi'], ['ang'], lambda e: e.tensor_copy(ang[:], posi[:]))
        S.op('dve', ['ang', 'cst'], ['ang'], lambda e: e.tensor_scalar(ang[:], ang[:], invf, None, ALU.mult))
        for which, dst in ((0, rs), (1, rc)):
            off = 0.0 if which == 0 else PI / 2
            S.op('dve', ['ang'], ['rtmp'], lambda e, off=off: e.tensor_scalar(tmp[:], ang[:], off, 1.0 / (2 * PI), ALU.add, ALU.mult))
            S.op('dve', ['rtmp'], ['rtmi'], lambda e: e.tensor_copy(tmi[:], tmp[:]))
            S.op('dve', ['rtmi'], ['rtmp'], lambda e: e.tensor_copy(tmp[:], tmi[:]))
            S.op('dve', ['rtmp', 'ang'], ['rtmp'], lambda e: e.scalar_tensor_tensor(tmp[:], in0=tmp[:], scalar=-2 * PI, in1=ang[:], op0=ALU.mult, op1=ALU.add))
            S.op('dve', ['rtmp'], [('rope', which)], lambda e, off=off, dst=dst: e.tensor_scalar(dst[:], tmp[:], off, None, ALU.add))
            S.op('dve', [('rope', which)], ['rtmp'], lambda e, dst=dst: e.tensor_scalar(tmp[:], dst[:], PI, -2 * PI, ALU.is_gt, ALU.mult))
            S.op('dve', ['rtmp', ('rope', which)], [('rope', which)], lambda e, dst=dst: e.tensor_tensor(dst[:], dst[:], tmp[:], ALU.add))
            S.op('dve', [('rope', which)], ['rtmp'], lambda e, dst=dst: e.tensor_scalar(tmp[:], dst[:], -PI, 2 * PI, ALU.is_lt, ALU.mult))
            S.op('dve', ['rtmp', ('rope', which)], [('rope', which)], lambda e, dst=dst: e.tensor_tensor(dst[:], dst[:], tmp[:], ALU.add))
            S.op('dve', [('rope', which)], [('rope', which)], lambda e, dst=dst: e.tensor_scalar(dst[:], dst[:], 3.1415925, -3.1415925, ALU.min, ALU.max))
            S.op('act', [('rope', which)], [('rope', which)], lambda e, dst=dst: e.activation(dst[:], dst[:], AF.Sin))
        S.op('dve', [('rope', 0), 'cst'], [('rope', 0)], lambda e: e.tensor_scalar(self.ropeS[:], rs[:], sgn, None, ALU.mult))
        S.op('dve', [('rope', 1)], [('rope', 1)], lambda e: e.tensor_copy(self.ropeC[:], rc[:]))
        self.dump('ropeC', rc[:], ('rope', 1))
        self.dump('ropeS', rs[:], ('rope', 0))

    def load_xT(self):
        S, sb = self.S, self.sb
        self.xt_buf = [sb("xtile%d" % i, [128, D]) for i in range(2)]
        for tt in range(16):
            xt = self.xt_buf[tt % 2]
            k = ('xtile', tt % 2)
            S.dma('sp', [], [k], lambda e, xt=xt, tt=tt: e.dma_start(out=xt[:], in_=self.x_in[tt * 128:(tt + 1) * 128, :]))
            self.to_featmajor(xt, k, self.xT, 'xT', tt)

    def to_featmajor(self, tile, tkey, dstT, dkey, tt):
        S = self.S
        for half in range(2):
            ps = self.PA[:, half * 512:(half + 1) * 512]
            pk = ('PA', half)
            self.tps([tkey, 'cst'], [pk], [(ps[:, j * 128:(j + 1) * 128], tile[:, (half * 4 + j) * 128:(half * 4 + j + 1) * 128]) for j in range(4)], self.ident_f)
            eng = 'act' if half == 0 else 'dve'
            if eng == 'act':
                S.op('act', [pk], [(dkey, tt)], lambda e, ps=ps, half=half: e.activation(
                    dstT[:, half * 4:(half + 1) * 4, tt * 128:(tt + 1) * 128], ps.rearrange("p (c t) -> p c t", c=4), AF.Copy))
            else:
                S.op('dve', [pk], [(dkey, tt)], lambda e, ps=ps, half=half: e.tensor_copy(
                    dstT[:, half * 4:(half + 1) * 4, tt * 128:(tt + 1) * 128], ps.rearrange("p (c t) -> p c t", c=4)))

    def layer(self, l):
        self.attention(l)
        if STOP_AFTER == 'att':
            raise _Stop()
        self.ssd(l)
        if STOP_AFTER == 'ssd':
            raise _Stop()
        self.merge(l)
        if STOP_AFTER == 'merge':
            raise _Stop()
        self.mix_ln1(l)
        if STOP_AFTER == 'ln1':
            raise _Stop()
        self.ffn(l)

    def psum_epoch(self):
        keys = [('PA', i) for i in range(4)] + [('PB', 0), ('PB', 1), ('PB', 0, 0), ('PB', 0, 1)] + \
               [('PB', 1, h) for h in range(4)] + ['PC', 'PCg', 'PCo', 'PD']
        self.S.alias(keys, keys)

    def dbg_qk(self, h):
        S = self.S
        if 'qa0' in self.dbg:
            qa, ka = self.qa[h], self.ka[h]
            dq = self.sb("dbgq", [128, SEQ])
            dk = self.sb("dbgk", [128, SEQ])
            S.op('dve', [('qa', h)], ['dbgq'], lambda e: e.tensor_copy(dq[0:72, :], qa[0:72, :]))
            S.op('dve', [('ka', h)], ['dbgk'], lambda e: e.tensor_copy(dk[0:72, :], ka[0:72, :]))
            self.dump('qa0', dq[0:72, :], 'dbgq')
            self.dump('ka0', dk[0:72, :], 'dbgk')

    def dbg_tile(self, name, ap, keys, ncols):
        if name in self.dbg:
            t = self.sb("dbg_" + name, [128, ncols])
            self.S.op('dve', keys, ['dbgt_' + name], lambda e: e.tensor_copy(t[:], ap))
            self.dump(name, t[:], 'dbgt_' + name)

    def xT_keys(self):
        return [('xT', tt) for tt in range(16)]

    def attention(self, l):
        self.psum_epoch()
        S, sb, nc = self.S, self.sb, self.nc
        self.phase([self.R_W, self.R_B2, (self.R_B3[0] + 7168, self.R_B3[1])])
        if True:
            self.wqkv_b = [sb("wqkv%d" % i, [128, 8, 384], BF16) for i in range(2)]
            self.qa = [sb("qa%d" % i, [72, SEQ], BF16) for i in range(4)]
            self.ka = [sb("ka%d" % i, [72, SEQ], BF16) for i in range(4)]
            self.va = [sb("va%d" % i, [128, 16, 65], BF16) for i in range(4)]
            self.qbf = [sb("qbf%d" % i, [128, 512], BF16) for i in range(2)]
            self.t1 = [sb("t1_%d" % i, [128, 512]) for i in range(2)]
            self.t2 = [sb("t2_%d" % i, [128, 512]) for i in range(2)]
            self.atm = [sb("atm%d" % i, [128, 16, 128], BF16) for i in range(2)]
            self.km = sb("km", [64, 8])
            self.kmh = sb("kmh", [64, 8], BF16)
            self.kml = sb("kml", [64, 8], BF16)
            self.gm = sb("gm", [128, 64])
            self.m8 = sb("m8", [128, 8, 8])
            self.lt = sb("lt", [128, 64])
            self.bpad = sb("bpad", [128, 8, 72], BF16)
            self.rden = sb("rden", [128, 4])
            self.NPT = 28
            for i in range(4):
                S.op('pool', [], [('qa', i)], lambda e, i=i: e.memset(self.qa[i][64:72, :], 0.0))
                S.dma('pool', [], [('ka', i)], lambda e, i=i: e.dma_start(out=self.ka[i][64:72, :], in_=self.e8_in))
                S.op('pool', [], [('va', i)], lambda e, i=i: e.memset(self.va[i][:, :, 64:65], 1.0))
            S.op('pool', [], ['bpad'], lambda e: e.memset(self.bpad[:], 0.0))
            if STOP_AFTER == 'att_init':
                self.dbg_qk(0)
                raise _Stop()
        PT = [self.B3[:, i // 4, (i % 4) * 512:(i % 4 + 1) * 512] for i in range(self.NPT)]
        attT = self.B1
        negp = self.cst[:, C_NEGP:C_NEGP + 64]
        floorb = self.cst[:, C_FLOOR:C_FLOOR + 64]
        pt_rr = [0]
        PDb = self.PD[:].bitcast(BF16)
        xk = self.xT_keys()

        def load_w(hp):
            wb = self.wqkv_b[hp % 2]
            S.dma('pool', [], [('wqkv', hp % 2, 0)], lambda e, wb=wb, hp=hp: e.dma_start(
                out=wb[:], in_=self.wqkv[l, hp].rearrange("p (k n) -> p k n", k=8), max_dma_last_dim=4096))
        load_w(0)
        Wl = [[] for _ in range(9)]
        Pl = [[] for _ in range(9)]
        Cl = [[] for _ in range(9)]
        for hp in range(8):
            wb = self.wqkv_b[hp % 2]
            wk = ('wqkv', hp % 2, 0)
            wk1 = wk2 = wk3 = wk
            self.rec = Wl[hp + 1]
            if STOP_AFTER == 'att_w0':
                self.dbg_tile('wb0', wb[:].rearrange("p a b -> p (a b)"), [wk, wk1, wk2, wk3], 3072)
                raise _Stop()
            if hp + 1 < 8:
                load_w(hp + 1)
            if STOP_AFTER == 'att_w1':
                self.dbg_tile('wb0', wb[:].rearrange("p a b -> p (a b)"), [wk, wk1, wk2, wk3], 3072)
                raise _Stop()
            par = (hp % 2) * 2
            hA, hB = par, par + 1
            self.rec = Pl[hp]
            cnt = 0
            for which in (1, 0):
                dst = self.ka if which == 1 else self.qa
                dn = 'ka' if which == 1 else 'qa'
                for tc in range(4):
                    bank = cnt % 4
                    cnt += 1
                    ps = self.PA[:, bank * 512:(bank + 1) * 512]
                    pk = ('PA', bank)
                    cols = slice(tc * 512, (tc + 1) * 512)
                    def stopat(ch, ap=None, keys=None):
                        if STOP_AFTER == 'att_qk1' + ch:
                            if ap is not None:
                                self.dbg_tile('probe', ap, keys, 512)
                            raise _Stop()
                    self.mm([wk, wk1, wk2, wk3] + xk[tc * 4:(tc + 1) * 4], [pk], ps,
                            [(wb[:, kc, which * 128:(which + 1) * 128], self.xT[:, kc, cols]) for kc in range(8)])
                    stopat('a', ps, [pk])
                    qb = self.qbf[cnt % 2]
                    qk_ = ('qbf', cnt % 2)
                    if VARIANT != 6:
                        S.op('act', [pk], [qk_], lambda e, qb=qb, ps=ps: e.activation(qb[:], ps, AF.Copy))
                    stopat('b', qb[:], [qk_])
                    t1 = self.t1[cnt % 2]
                    t2 = self.t2[cnt % 2]
                    k1 = ('t1', cnt % 2)
                    k2 = ('t2', cnt % 2)
                    if VARIANT == 4:
                        t1 = self.sb("t1x", [128, 512])
                    if VARIANT in (5, 8):
                        S.op('dve', [('rope', 1)], [k1], lambda e, t1=t1, ps=ps, cols=cols: e.tensor_copy(t1[:], self.ropeC[:, cols]))
                    elif VARIANT in (1, 4):
                        S.op('dve', [pk, ('rope', 1)], [k1], lambda e, t1=t1, ps=ps, cols=cols: e.tensor_copy(t1[:], ps))
                    elif VARIANT == 2:
                        S.op('dve', [pk, ('rope', 1)], [k1], lambda e, t1=t1, ps=ps, cols=cols: e.tensor_copy(t1[:], self.ropeC[:, cols]))
                    elif VARIANT == 3:
                        S.op('dve', [pk, ('rope', 1)], [k1], lambda e, t1=t1, ps=ps, cols=cols: e.tensor_tensor(t1[:], ps, self.cst[:, 0:512], ALU.mult))
                    else:
                        S.op('dve', [pk, ('rope', 1)], [k1], lambda e, t1=t1, ps=ps, cols=cols: e.tensor_tensor(t1[:], ps, self.ropeC[:, cols], ALU.mult))
                    stopat('c', qb[:] if VARIANT == 7 else t1[:], [qk_, k1] if VARIANT in (7, 8) else [k1])
                    rb = (bank + 2) % 4
                    pr = self.PA[:, rb * 512:(rb + 1) * 512]
                    prk = ('PA', rb)
                    self.mm([qk_, 'cbf'], [prk], pr, [(self.rm_b, qb[:])])
                    stopat('d', pr, [prk])
                    S.op('dve', [prk, ('rope', 0)], [k2], lambda e, t2=t2, pr=pr, cols=cols: e.tensor_tensor(t2[:], pr, self.ropeS[:, cols], ALU.mult))
                    stopat('e', t2[:], [k2])
                    S.op('dve', [k1, k2], [(dn, hA)], lambda e, t1=t1, t2=t2, dst=dst, cols=cols, hA=hA: e.tensor_tensor(dst[hA][0:64, cols], t1[0:64, :], t2[0:64, :], ALU.add))
                    stopat('f', dst[hA][0:64, cols], [(dn, hA)])
                    S.op('dve', [k1, k2], [(dn, hB)], lambda e, t1=t1, t2=t2, dst=dst, cols=cols, hB=hB: e.tensor_tensor(dst[hB][0:64, cols], t1[64:128, :], t2[64:128, :], ALU.add))
                    stopat('g', dst[hB][0:64, cols], [(dn, hB)])
            if STOP_AFTER == 'att_qk':
                raise _Stop()
            for tq in range(4):
                bank = cnt % 4
                cnt += 1
                ps = self.PA[:, bank * 512:(bank + 1) * 512]
                pk = ('PA', bank)
                self.mmseq([wk, wk1, wk2, wk3] + xk[tq * 4:(tq + 1) * 4], [pk],
                           [(ps[:, j * 128:(j + 1) * 128],
                             [(self.xT[:, kc, (tq * 4 + j) * 128:(tq * 4 + j + 1) * 128], wb[:, kc, 256:384]) for kc in range(8)])
                            for j in range(4)])
                psv = ps.rearrange("p (t c) -> p t c", t=4)
                if hp == 0 and tq == 0:
                    self.dbg_tile('psv', ps, [pk], 512)
                    self.dbg_tile('wb0', wb[:].rearrange("p a b -> p (a b)"), [wk, wk1, wk2, wk3], 3072)
                S.op('act', [pk], [('va', hA)], lambda e, psv=psv, tq=tq, hA=hA: e.activation(self.va[hA][:, tq * 4:(tq + 1) * 4, 0:64], psv[:, :, 0:64], AF.Copy))
                S.op('act', [pk], [('va', hB)], lambda e, psv=psv, tq=tq, hB=hB: e.activation(self.va[hB][:, tq * 4:(tq + 1) * 4, 0:64], psv[:, :, 64:128], AF.Copy))
            if STOP_AFTER == 'att_proj':
                self.dbg_tile('va0', self.va[0][:].rearrange("p a b -> p (a b)"), [('va', 0)], 1040)
                self.dbg_qk(hA)
                raise _Stop()
            self.rec = Cl[hp]
            atm = self.atm[hp % 2]
            ak = ('atm', hp % 2)
            for hh, hbuf in ((0, hA), (1, hB)):
                qa, ka, va = self.qa[hbuf], self.ka[hbuf], self.va[hbuf]
                qk, kk, vk = ('qa', hbuf), ('ka', hbuf), ('va', hbuf)
                S.op('dve', [kk], ['km'], lambda e, ka=ka: e.tensor_reduce(self.km[:], ka[0:64, :].rearrange("p (b t) -> p b t", b=8), AX.X, ALU.add))
                S.op('dve', ['km'], ['kmh'], lambda e: e.tensor_scalar(self.kmh[:], self.km[:], 1.0 / 256, None, ALU.mult))
                S.op('dve', ['km', 'kmh'], ['kml'], lambda e: e.scalar_tensor_tensor(self.kml[:], in0=self.km[:], scalar=1.0 / 256, in1=self.kmh[:], op0=ALU.mult, op1=ALU.subtract))
                pg = self.PC[:, 448:512]
                self.mmseq([qk, 'kmh', 'kml'], ['PCg'],
                           [(pg[:, t * 8:(t + 1) * 8], [(qa[0:64, (8 + t) * 128:(9 + t) * 128], self.kmh[:]), (qa[0:64, (8 + t) * 128:(9 + t) * 128], self.kml[:])]) for t in range(8)])
                S.op('dve', ['PCg', 'cst'], ['gm'], lambda e, pg=pg: e.tensor_tensor(self.gm[:], pg, negp, ALU.add))

                S.op('dve', ['gm'], ['m8'], lambda e: sel_max(e, self))
                S.op('dve', ['gm', 'm8'], ['lt'], lambda e: sel_lt(e, self))
                S.op('dve', ['lt', 'cst'], ['bpad'], lambda e: e.tensor_tensor(self.bpad[:, :, 64:72], self.lt[:].rearrange("p (t j) -> p t j", t=8), floorb.rearrange("p (t j) -> p t j", t=8), ALU.max))
                self.tps(['bpad', 'cbf'], ['PD'], [(PDb[0:72, t * 128:(t + 1) * 128], self.bpad[:, t, :]) for t in range(8)], self.ident_b)
                S.op('act', ['PD'], [qk], lambda e, qa=qa: e.activation(qa[64:72, 1024:2048], PDb[64:72, :], AF.Copy))
                if STOP_AFTER == 'att_gate':
                    self.dbg_qk(hA)
                    raise _Stop()
                slots = {}

                def stageA(qc):
                    sl = []
                    for kt in range(4 * qc + 4):
                        c0 = max(0, kt * 128 - qc * 512)
                        bank = kt % 2
                        ps = self.PB[:, bank * 512:(bank + 1) * 512]
                        pk = ('PB', bank)
                        self.mm([kk, qk], [pk], ps[:, c0:512], [(ka[0:72, kt * 128:(kt + 1) * 128], qa[0:72, qc * 512 + c0:(qc + 1) * 512])])
                        si = pt_rr[0]
                        pt_rr[0] = (pt_rr[0] + 1) % self.NPT
                        sl.append(si)
                        pt = PT[si]
                        S.op('act', [pk], [('PT', si)], lambda e, pt=pt, ps=ps, c0=c0: e.activation(pt[:, c0:512], ps[:, c0:512], AF.Exp, scale=0.125))
                        if kt >= 4 * qc:
                            S.op('dve', [('PT', si), 'cbf'], [('PT', si)], lambda e, pt=pt, c0=c0: e.tensor_tensor(pt[:, c0:c0 + 128], pt[:, c0:c0 + 128], self.tri_b, ALU.mult))
                    slots[qc] = sl

                def stageB(qc):
                    sl = slots[qc]
                    groups = []
                    for qi in range(4):
                        qt = 4 * qc + qi
                        groups.append((self.PC[:, qi * 65:(qi + 1) * 65],
                                       [(PT[sl[kt]][:, qi * 128:(qi + 1) * 128], va[:, kt, :]) for kt in range(qt + 1)]))
                    self.mmseq([('PT', s_) for s_ in sl] + [vk], ['PCo'], groups)
                    pv = self.PC[:, 0:260].rearrange("p (q c) -> p q c", q=4)
                    S.op('dve', ['PCo'], ['rden'], lambda e, pv=pv: e.reciprocal(self.rden[:], pv[:, :, 64]))

                    def norm(e, atm=atm, hh=hh, qc=qc):
                        ins = None
                        for qi in range(4):
                            ins = e.tensor_scalar(atm[:, 4 * qc + qi, hh * 64:(hh + 1) * 64], self.PC[:, qi * 65:qi * 65 + 64], self.rden[:, qi:qi + 1], None, ALU.mult)
                        return ins
                    S.op('dve', ['PCo', 'rden'], [ak], norm)

                stageA(0)
                for qc in range(4):
                    if qc + 1 < 4:
                        stageA(qc + 1)
                    stageB(qc)
            if hp == 0:
                self.dbg_tile('va0', self.va[0][:].rearrange("p a b -> p (a b)"), [('va', 0)], 1040)
                self.dbg_tile('atm0', atm[:].rearrange("p a b -> p (a b)"), [ak], 2048)
            for half in range(2):
                self.tps([ak, 'cbf'], ['PD'], [(PDb[:, t * 128:(t + 1) * 128], atm[:, half * 8 + t, :]) for t in range(8)], self.ident_b)
                S.op('act', ['PD'], [('attT', hp, half)], lambda e, half=half, hp=hp: e.activation(attT[:, hp, half * 1024:(half + 1) * 1024], PDb, AF.Copy))
            self.rec = None
        for t in Pl[0]:
            t()
        for hp in range(8):
            for t in Wl[hp + 1]:
                t()
            A, Bn = Cl[hp], Pl[hp + 1]
            n1, n2 = len(A), len(Bn)
            i1 = i2 = 0
            while i1 < n1 or i2 < n2:
                if i2 >= n2 or (i1 < n1 and i1 * n2 <= i2 * n1):
                    A[i1]()
                    i1 += 1
                else:
                    Bn[i2]()
                    i2 += 1
        if 'attT' in self.dbg:
            self.dbgf = self.sb("dbgf", [128, 512])
            for hp in range(8):
                for q4 in range(4):
                    S.op('dve', [('attT', hp, 0), ('attT', hp, 1), ('dbg', 'attT')], ['dbgf'], lambda e, hp=hp, q4=q4: e.tensor_copy(self.dbgf[:, 0:512], attT[:, hp, q4 * 512:(q4 + 1) * 512]))
                    S.dma('sp', ['dbgf'], [('dbg', 'attT')], lambda e, hp=hp, q4=q4: e.dma_start(out=self.dbg['attT'][hp * 128:(hp + 1) * 128, q4 * 512:(q4 + 1) * 512], in_=self.dbgf[:, 0:512]))
            self.final_keys.append(('dbg', 'attT'))

    def ssd(self, l):
        self.psum_epoch()
        S, sb, nc = self.S, self.sb, self.nc
        self.phase([self.R_W, self.R_B3])
        if True:
            self.wssm_b = [sb("wssm0", [128, 8, 768], BF16)] * 2
            self.wdt_b = sb("wdtb", [128, 8, 32], BF16)
            self.smallb = sb("smallb", [128, 96])
            self.cwb = sb("cwb", [128, 8, 4, 5])
            self.dt_all = sb("dt_all", [128, 16, 32])
            self.adt_all = sb("adt_all", [128, 16, 32])
            self.acs_all = sb("acs_all", [128, 16, 32])
            self.dst_all = sb("dst_all", [128, 16, 32])
            self.ea_all = sb("ea_all", [128, 16, 32])
            self.lastb = sb("lastb", [128, 8, 32])
            self.cdb = sb("cdb", [128, 8, 32])
            self.a_b = sb("a_b", [128, 32])
            self.uext = [sb("uext%d" % i, [128, 4, 259], BF16) for i in range(2)]
            self.dg = sb("dg", [128, 4, 4, 128], BF16)
            self.xh = sb("xh", [128, 4, 256])
            self.tnh = sb("tnh", [128, 4, 256])
            self.bcb2 = [sb("bcb%d" % i, [128, 2, 256], BF16) for i in range(2)]
            self.bcf = sb("bcf", [128, 256])
            self.xs_tok2 = [sb("xs_tok%d" % i, [128, 2, 256], BF16) for i in range(2)]
            self.xdt2 = [sb("xdt%d" % i, [128, 2, 256], BF16) for i in range(2)]
            self.xw2 = [sb("xw%d" % i, [128, 2, 256], BF16) for i in range(2)]
            self.btok2 = [sb("btok%d" % i, [128, 2, 128], BF16) for i in range(2)]
            self.cb32 = [sb("cb3_%d" % i, [128, 3, 128]) for i in range(2)]
            self.dtmp = [sb("dtmp%d" % i, [128, 3, 128]) for i in range(2)]
            self.mt = [sb("mt%d" % i, [128, 3, 128], BF16) for i in range(2)]
            self.prev_f = sb("prev_f", [128, 256])
            self.prev_b = sb("prev_b", [128, 256], BF16)
            self.tz = sb("tz", [128, 2, 256])
            self.yv = sb("yv", [128, 2, 256])
            self.ss = sb("ss", [128, 2])
            self.rstd = sb("rstd", [128, 2])
            self.nwb = [sb("nwb%d" % i, [128, 256]) for i in range(2)]
            self.xsf = self.xh[:, 0:2, :]
            self.zs = self.tz
            self.h2 = self.yv
            self.yn = self.yv
            self.junk = sb("junk", [128, 256], BF16)
        xk = self.xT_keys()
        S.dma('sp', [], ['smallb'], lambda e: e.dma_start(out=self.smallb[:], in_=self.small[l].broadcast_to([128, 96])))
        S.dma('sp', [], ['cwb'], lambda e: e.dma_start(out=self.cwb[:], in_=self.cw[l].rearrange("p (g c k) -> p g c k", g=8, c=4)))
        S.dma('pool', [], ['wdtb'], lambda e: e.dma_start(out=self.wdt_b[:], in_=self.wdt[l].rearrange("p (k n) -> p k n", k=8)))
        S.op('pool', ['cwb'], ['cwb'], lambda e: e.tensor_scalar(self.cwb[:], self.cwb[:], 0.5, None, ALU.mult))
        dtb = self.smallb[:, 0:32]
        alog = self.smallb[:, 32:64]
        dsk = self.smallb[:, 64:96]
        S.op('act', ['smallb'], ['a_b'], lambda e: e.activation(self.a_b[:], alog, AF.Exp))
        S.op('dve', ['a_b'], ['a_b'], lambda e: e.tensor_scalar(self.a_b[:], self.a_b[:], -1.0, None, ALU.mult))
        psd = self.PA[:, 0:512]
        self.mmseq(['wdtb'] + xk, [('PA', 0)],
                   [(psd[:, tt * 32:(tt + 1) * 32], [(self.xT[:, kc, tt * 128:(tt + 1) * 128], self.wdt_b[:, kc, :]) for kc in range(8)]) for tt in range(16)])
        dtf = self.dt_all[:].rearrange("p t h -> p (t h)")
        psd3 = psd.rearrange("p (t h) -> p t h", t=16)
        S.op('dve', [('PA', 0), 'smallb'], ['dt_all'], lambda e: e.tensor_tensor(self.dt_all[:], psd3, dtb.unsqueeze(1).broadcast_to([128, 16, 32]), ALU.add))
        S.op('act', ['dt_all'], ['dt_all'], lambda e: e.activation(dtf, dtf, AF.Exp))
        S.op('act', ['dt_all'], ['dt_all'], lambda e: e.activation(dtf, dtf, AF.Ln, bias=1.0))
        S.op('dve', ['dt_all', 'a_b'], ['adt_all'], lambda e: e.tensor_tensor(self.adt_all[:], self.dt_all[:], self.a_b[:].unsqueeze(1).broadcast_to([128, 16, 32]), ALU.mult))
        psa = self.PA[:, 512:1024]
        psl = self.PA[:, 1024:1280]
        groups = []
        for c in range(8):
            a0 = self.adt_all[:, 2 * c, :]
            a1 = self.adt_all[:, 2 * c + 1, :]
            groups.append((psa[:, (2 * c) * 32:(2 * c + 1) * 32], [(self.tri_f, a0)]))
            groups.append((psa[:, (2 * c + 1) * 32:(2 * c + 2) * 32], [(self.ones_f, a0), (self.tri_f, a1)]))
        self.mmseq(['adt_all', 'cst'], [('PA', 1)], groups)
        self.mmseq(['adt_all', 'cst'], [('PA', 2)],
                   [(psl[:, c * 32:(c + 1) * 32], [(self.ones_f, self.adt_all[:, 2 * c, :]), (self.ones_f, self.adt_all[:, 2 * c + 1, :])]) for c in range(8)])
        acsf = self.acs_all[:].rearrange("p t h -> p (t h)")
        S.op('dve', [('PA', 1)], ['acs_all'], lambda e: e.tensor_copy(acsf, psa))
        S.op('dve', [('PA', 2)], ['lastb'], lambda e: e.tensor_copy(self.lastb[:].rearrange("p c h -> p (c h)"), psl))
        S.op('dve', ['lastb', 'acs_all'], ['dst_all'], lambda e: e.tensor_tensor(
            self.dst_all[:].rearrange("p (c s) h -> p c s h", c=8), self.lastb[:].unsqueeze(2).broadcast_to([128, 8, 2, 32]),
            self.acs_all[:].rearrange("p (c s) h -> p c s h", c=8), ALU.subtract))
        dstf = self.dst_all[:].rearrange("p t h -> p (t h)")
        S.op('act', ['dst_all'], ['dst_all'], lambda e: e.activation(dstf, dstf, AF.Exp))
        S.op('dve', ['dst_all', 'dt_all'], ['dst_all'], lambda e: e.tensor_tensor(self.dst_all[:], self.dst_all[:], self.dt_all[:], ALU.mult))
        S.op('act', ['acs_all'], ['ea_all'], lambda e: e.activation(self.ea_all[:].rearrange("p t h -> p (t h)"), acsf, AF.Exp))
        S.op('act', ['lastb'], ['cdb'], lambda e: e.activation(self.cdb[:].rearrange("p c h -> p (c h)"), self.lastb[:].rearrange("p c h -> p (c h)"), AF.Exp))
        self.dump('dt_all', self.dt_all[:].rearrange("p t h -> p (t h)"), 'dt_all')
        self.dump('acs_all', acsf, 'acs_all')

        P0 = self.PA[:, 0:512]
        P1 = self.PA[:, 512:1024]
        P2 = self.PA[:, 1024:1536]
        P3 = self.PA[:, 1536:2048]
        P4 = self.PB[:, 0:512]
        P5 = self.PB[:, 512:1024]
        P6 = self.PC[:, 0:512]
        P7 = self.PD[:, 0:512]
        uprev = None
        def load_xbc(g):
            wb = self.wssm_b[0]
            S.dma('pool', [], [('wssm', 'xbc')], lambda e, wb=wb, g=g: e.dma_start(
                out=wb[:, :, 256:768], in_=self.wssm[l, g].rearrange("p (k n) -> p k n", k=8)[:, :, 256:768], max_dma_last_dim=2048))
            nw = self.nwb[g % 2]
            S.dma('sp', [], [('nwb', g % 2)], lambda e, nw=nw, g=g: e.dma_start(out=nw[:], in_=self.normw[l][:, g * 256:(g + 1) * 256].broadcast_to([128, 256])))

        def load_z(g):
            wb = self.wssm_b[0]
            S.dma('pool', [], [('wssm', 'z')], lambda e, wb=wb, g=g: e.dma_start(
                out=wb[:, :, 0:256], in_=self.wssm[l, g].rearrange("p (k n) -> p k n", k=8)[:, :, 0:256], max_dma_last_dim=1024))
        load_xbc(0)
        load_z(0)
        prevB = None
        for g in range(8):
            wb = self.wssm_b[g % 2]
            wk = ('wssm', 0, 0)
            wkall = [('wssm', 'xbc')]
            nw = self.nwb[g % 2]
            nk = ('nwb', g % 2)
            if g > 0:
                load_xbc(g)
            hs = slice(4 * g, 4 * g + 4)

            def build_dg(e, g=g):
                ins = None
                for ci in range(4):
                    for k in range(4):
                        ins = e.tensor_scalar(self.dg[:, ci, k, :], self.ident_b, self.cwb[:, g, ci, k:k + 1], None, ALU.mult)
                return ins
            S.op('dve', ['cwb', 'cbf'], ['dg'], build_dg)
            for c in range(8):
                cols = slice(c * 256, (c + 1) * 256)
                sl = c % 2
                bcb, xs_tok, xw, btok = self.bcb2[sl], self.xs_tok2[sl], self.xw2[sl], self.btok2[sl]
                kbcb, kxs, kxw, kbt = ('bcb', sl), ('xs_tok', sl), ('xw', sl), ('btok', sl)
                xdt, cb3 = self.xdt2[sl], self.cb32[sl]
                kxdt, kcb3 = ('xdt', sl), ('cb3', sl)
                F1, F2, Bq = [], [], []
                self.rec = F1
                xkc = xk[2 * c:2 * c + 2]
                ue = self.uext[c % 2]
                uk = ('uext', c % 2)
                self.mmseq(wkall + xkc, [('PA', 0)],
                           [(P0[:, j * 256:(j + 1) * 256], [(wb[:, kc, 256 + j * 128:256 + (j + 1) * 128], self.xT[:, kc, cols]) for kc in range(8)]) for j in range(2)])
                self.mmseq(wkall + xkc, [('PA', 1)],
                           [(P1[:, j * 256:(j + 1) * 256], [(wb[:, kc, 512 + j * 128:512 + (j + 1) * 128], self.xT[:, kc, cols]) for kc in range(8)]) for j in range(2)])
                if c == 0:
                    S.op('pool', [], [uk], lambda e, bcb=bcb, xs_tok=xs_tok, xw=xw, btok=btok, xdt=xdt, cb3=cb3, ue=ue: e.memset(ue[:, :, 0:3], 0.0))
                else:
                    up = self.uext[(c - 1) % 2]
                    S.op('pool', [('uext', (c - 1) % 2)], [uk], lambda e, bcb=bcb, xs_tok=xs_tok, xw=xw, btok=btok, xdt=xdt, cb3=cb3, ue=ue, up=up: e.tensor_copy(ue[:, :, 0:3], up[:, :, 256:259]))
                S.op('act', [('PA', 0)], [uk], lambda e, bcb=bcb, xs_tok=xs_tok, xw=xw, btok=btok, xdt=xdt, cb3=cb3, ue=ue: e.activation(ue[:, 0:2, 3:259], P0.rearrange("p (j t) -> p j t", j=2), AF.Copy))
                S.op('act', [('PA', 1)], [uk], lambda e, bcb=bcb, xs_tok=xs_tok, xw=xw, btok=btok, xdt=xdt, cb3=cb3, ue=ue: e.activation(ue[:, 2:4, 3:259], P1.rearrange("p (j t) -> p j t", j=2), AF.Copy))
                self.mmseq([uk, 'dg'], [('PA', 0)],
                           [(P0[:, j * 256:(j + 1) * 256], [(self.dg[:, j, k, :], ue[:, j, k:k + 256]) for k in range(4)]) for j in range(2)])
                self.mmseq([uk, 'dg'], [('PA', 1)],
                           [(P1[:, j * 256:(j + 1) * 256], [(self.dg[:, 2 + j, k, :], ue[:, 2 + j, k:k + 256]) for k in range(4)]) for j in range(2)])
                for ci in range(4):
                    w = self.cwb[:, g, ci, :]
                    pc = (P0 if ci < 2 else P1)[:, (ci % 2) * 256:(ci % 2 + 1) * 256]
                    pk = ('PA', 0 if ci < 2 else 1)
                    S.op('act', [pk, 'cwb'], [('xh', ci)], lambda e, ci=ci, w=w, pc=pc: e.activation(self.xh[:, ci, :], pc, AF.Identity, bias=w[:, 4:5]))
                    S.op('act', [pk, 'cwb'], [('tnh', ci)], lambda e, ci=ci, w=w, pc=pc: e.activation(self.tnh[:, ci, :], pc, AF.Tanh, bias=w[:, 4:5]))
                S.op('dve', [('tnh', 0), ('tnh', 1), ('xh', 0), ('xh', 1)], [('xh', 0), ('xh', 1)], lambda e, bcb=bcb, xs_tok=xs_tok, xw=xw, btok=btok, xdt=xdt, cb3=cb3: e.scalar_tensor_tensor(self.xsf[:], in0=self.tnh[:, 0:2, :], scalar=1.0, in1=self.xh[:, 0:2, :], op0=ALU.add, op1=ALU.mult))
                S.op('dve', [('tnh', 2), ('tnh', 3), ('xh', 2), ('xh', 3)], [kbcb], lambda e, bcb=bcb, xs_tok=xs_tok, xw=xw, btok=btok, xdt=xdt, cb3=cb3: e.scalar_tensor_tensor(bcb[:], in0=self.tnh[:, 2:4, :], scalar=1.0, in1=self.xh[:, 2:4, :], op0=ALU.add, op1=ALU.mult))
                S.op('dve', [('tnh', 2), ('xh', 2)], ['bcf'], lambda e, bcb=bcb, xs_tok=xs_tok, xw=xw, btok=btok, xdt=xdt, cb3=cb3: e.scalar_tensor_tensor(self.bcf[:], in0=self.tnh[:, 2, :], scalar=1.0, in1=self.xh[:, 2, :], op0=ALU.add, op1=ALU.mult))
                self.tps([('xh', 0), ('xh', 1), 'cst'], [('PA', 0)],
                         [(P0[:, si * 256 + j * 128:si * 256 + (j + 1) * 128], self.xsf[:, j, si * 128:(si + 1) * 128]) for si in range(2) for j in range(2)], self.ident_f)
                self.tps(['bcf', 'cst'], [('PA', 1)],
                         [(P1[:, si * 128:(si + 1) * 128], self.bcf[:, si * 128:(si + 1) * 128]) for si in range(2)], self.ident_f)
                P0v = P0.rearrange("p (s h d) -> p s h d", s=2, h=4)
                S.op('act', [('PA', 0)], [kxs], lambda e, bcb=bcb, xs_tok=xs_tok, xw=xw, btok=btok, xdt=xdt, cb3=cb3: e.activation(xs_tok[:].rearrange("p s c -> p (s c)"), P0, AF.Copy))
                S.op('dve', [('PA', 0), 'dt_all'], [kxdt], lambda e, bcb=bcb, xs_tok=xs_tok, xw=xw, btok=btok, xdt=xdt, cb3=cb3, c=c, hs=hs: e.tensor_tensor(
                    xdt[:].rearrange("p s (h d) -> p s h d", h=4), P0v,
                    self.dt_all[:, 2 * c:2 * c + 2, hs].unsqueeze(3).broadcast_to([128, 2, 4, 64]), ALU.mult))
                S.op('dve', [('PA', 0), 'dst_all'], [kxw], lambda e, bcb=bcb, xs_tok=xs_tok, xw=xw, btok=btok, xdt=xdt, cb3=cb3, c=c, hs=hs: e.tensor_tensor(
                    xw[:].rearrange("p s (h d) -> p s h d", h=4), P0v,
                    self.dst_all[:, 2 * c:2 * c + 2, hs].unsqueeze(3).broadcast_to([128, 2, 4, 64]), ALU.mult))
                S.op('act', [('PA', 1)], [kbt], lambda e, bcb=bcb, xs_tok=xs_tok, xw=xw, btok=btok, xdt=xdt, cb3=cb3: e.activation(btok[:].rearrange("p s n -> p (s n)"), P1[:, 0:256], AF.Copy))
                self.mmseq([kbcb], [('PA', 3)],
                           [(P3[:, 0:256], [(bcb[:, 0, 0:128], bcb[:, 1, :])]),
                            (P3[:, 384:512], [(bcb[:, 0, 128:256], bcb[:, 1, 128:256])])])
                S.op('dve', [('PA', 3), 'cst'], [kcb3], lambda e, bcb=bcb, xs_tok=xs_tok, xw=xw, btok=btok, xdt=xdt, cb3=cb3: e.tensor_tensor(cb3[:, 0, :], P3[:, 0:128], self.tri_f, ALU.mult))
                S.op('act', [('PA', 3)], [kcb3], lambda e, bcb=bcb, xs_tok=xs_tok, xw=xw, btok=btok, xdt=xdt, cb3=cb3: e.activation(cb3[:, 1, :], P3[:, 128:256], AF.Copy))
                S.op('dve', [('PA', 3), 'cst', kcb3], [kcb3], lambda e, bcb=bcb, xs_tok=xs_tok, xw=xw, btok=btok, xdt=xdt, cb3=cb3: e.tensor_tensor(cb3[:, 2, :], P3[:, 384:512], self.tri_f, ALU.mult))
                self.rec = F2
                def bc_mm(hh):
                    hg = 4 * g + hh
                    pb = (P4 if hh % 2 == 0 else P6)[:, 0:256]
                    pbk = ('PB', 0, 0) if hh % 2 == 0 else 'PC'
                    a0 = self.adt_all[:, 2 * c, hg:hg + 1].broadcast_to([128, 128])
                    a1 = self.adt_all[:, 2 * c + 1, hg:hg + 1].broadcast_to([128, 128])
                    self.mmseq(['adt_all', 'cst'], [pbk],
                               [(pb[:, 0:128], [(a0, self.tri_f)]),
                                (pb[:, 128:256], [(a0, self.ones_f), (a1, self.tri_f)])])
                    return pb, pbk
                nxt = bc_mm(0)
                for hh in range(4):
                    hg = 4 * g + hh
                    pb, pbk = nxt
                    if hh + 1 < 4:
                        nxt = bc_mm(hh + 1)
                    dtm = self.dtmp[hh % 2]
                    dk = ('dtmp', hh % 2)
                    mt = self.mt[hh % 2]
                    mk = ('mt', hh % 2)
                    ac0 = self.acs_all[:, 2 * c, hg:hg + 1]
                    ac1 = self.acs_all[:, 2 * c + 1, hg:hg + 1]

                    def dec(e, dtm=dtm, pb=pb, ac0=ac0, ac1=ac1):
                        e.tensor_scalar(dtm[:, 0, :], pb[:, 0:128], ac0, 0.0, ALU.subtract, ALU.min)
                        e.tensor_scalar(dtm[:, 1, :], pb[:, 128:256], ac0, 0.0, ALU.subtract, ALU.min)
                        return e.tensor_scalar(dtm[:, 2, :], pb[:, 128:256], ac1, 0.0, ALU.subtract, ALU.min)
                    S.op('dve', [pbk, 'acs_all'], [dk], dec)
                    S.op('act', [dk], [dk], lambda e, dtm=dtm: e.activation(dtm[:].rearrange("p a b -> p (a b)"), dtm[:].rearrange("p a b -> p (a b)"), AF.Exp))
                    S.op('dve', [dk, kcb3], [mk], lambda e, dtm=dtm, mt=mt, cb3=cb3: e.tensor_tensor(mt[:], dtm[:], cb3[:], ALU.mult))
                    self.mmseq([mk, kxdt], [('PB', 1, hh)],
                               [(P5[:, hh * 64:(hh + 1) * 64], [(mt[:, 0, :], xdt[:, 0, hh * 64:(hh + 1) * 64])]),
                                (P5[:, 256 + hh * 64:256 + (hh + 1) * 64], [(mt[:, 1, :], xdt[:, 0, hh * 64:(hh + 1) * 64]), (mt[:, 2, :], xdt[:, 1, hh * 64:(hh + 1) * 64])])])
                self.rec = Bq
                self.mmseq([('wssm', 'z')] + xkc, [('PA', 2)],
                           [(P2[:, si * 256:(si + 1) * 256], [(self.xT[:, kc, c * 256 + si * 128:c * 256 + (si + 1) * 128], wb[:, kc, 0:256]) for kc in range(8)]) for si in range(2)])
                S.op('act', [('PA', 2)], ['tz'], lambda e, bcb=bcb, xs_tok=xs_tok, xw=xw, btok=btok, xdt=xdt, cb3=cb3: e.activation(self.tz[:].rearrange("p s c -> p (s c)"), P2, AF.Tanh, scale=0.5))
                S.op('dve', ['tz', ('PA', 2)], ['tz'], lambda e, bcb=bcb, xs_tok=xs_tok, xw=xw, btok=btok, xdt=xdt, cb3=cb3: e.scalar_tensor_tensor(self.zs[:].rearrange("p s c -> p (s c)"), in0=self.tz[:].rearrange("p s c -> p (s c)"), scalar=1.0, in1=P2, op0=ALU.add, op1=ALU.mult))
                p5k = [('PB', 1, hh) for hh in range(4)]
                if c > 0:
                    self.mmseq([kbcb, 'prev_b'], [('PA', 2)],
                               [(P2[:, si * 256:(si + 1) * 256], [(bcb[:, 1, si * 128:(si + 1) * 128], self.prev_b[:])]) for si in range(2)])
                if c < 7:
                    self.mm([kbt, kxw], [('PB', 0, 0)], P4[:, 0:256], [(btok[:, si, :], xw[:, si, :]) for si in range(2)])
                    if c == 0:
                        S.op('act', [('PB', 0, 0)], ['prev_f'], lambda e, bcb=bcb, xs_tok=xs_tok, xw=xw, btok=btok, xdt=xdt, cb3=cb3: e.activation(self.prev_f[:], P4[:, 0:256], AF.Copy))
                    else:
                        S.op('dve', ['prev_f', 'cdb'], ['prev_f'], lambda e, bcb=bcb, xs_tok=xs_tok, xw=xw, btok=btok, xdt=xdt, cb3=cb3, c=c, hs=hs: e.tensor_tensor(
                            self.prev_f[:].rearrange("p (h d) -> p h d", h=4), self.prev_f[:].rearrange("p (h d) -> p h d", h=4),
                            self.cdb[:, c, hs].unsqueeze(2).broadcast_to([128, 4, 64]), ALU.mult))
                        S.op('dve', ['prev_f', ('PB', 0, 0)], ['prev_f'], lambda e, bcb=bcb, xs_tok=xs_tok, xw=xw, btok=btok, xdt=xdt, cb3=cb3: e.tensor_tensor(self.prev_f[:], self.prev_f[:], P4[:, 0:256], ALU.add))
                    S.op('act', ['prev_f'], ['prev_b'], lambda e, bcb=bcb, xs_tok=xs_tok, xw=xw, btok=btok, xdt=xdt, cb3=cb3: e.activation(self.prev_b[:], self.prev_f[:], AF.Copy))
                S.op('pool', [kxs, 'smallb'], ['yv'], lambda e, bcb=bcb, xs_tok=xs_tok, xw=xw, btok=btok, xdt=xdt, cb3=cb3, hs=hs: e.tensor_tensor(
                    self.yv[:].rearrange("p s (h d) -> p s h d", h=4), xs_tok[:].rearrange("p s (h d) -> p s h d", h=4),
                    dsk[:, hs].unsqueeze(1).unsqueeze(3).broadcast_to([128, 2, 4, 64]), ALU.mult))
                S.op('dve', p5k + ['yv'], ['yv'], lambda e, bcb=bcb, xs_tok=xs_tok, xw=xw, btok=btok, xdt=xdt, cb3=cb3: e.tensor_tensor(self.yv[:].rearrange("p s c -> p (s c)"), P5, self.yv[:].rearrange("p s c -> p (s c)"), ALU.add))
                if c > 0:
                    def yoff(e, c=c, g=g):
                        ins = None
                        for si in range(2):
                            for hh in range(4):
                                o = self.yv[:, si, hh * 64:(hh + 1) * 64]
                                ins = e.scalar_tensor_tensor(o, in0=P2[:, si * 256 + hh * 64:si * 256 + (hh + 1) * 64],
                                                             scalar=self.ea_all[:, 2 * c + si, 4 * g + hh:4 * g + hh + 1], in1=o, op0=ALU.mult, op1=ALU.add)
                        return ins
                    S.op('dve', [('PA', 2), 'ea_all', 'yv'], ['yv'], yoff)
                S.op('dve', ['yv', 'tz'], ['yv'], lambda e, bcb=bcb, xs_tok=xs_tok, xw=xw, btok=btok, xdt=xdt, cb3=cb3: e.tensor_tensor(self.h2[:], self.yv[:], self.zs[:], ALU.mult))
                for si in range(2):
                    S.op('act', ['yv'], ['junk', ('ss', si)], lambda e, bcb=bcb, xs_tok=xs_tok, xw=xw, btok=btok, xdt=xdt, cb3=cb3, si=si: e.activation(self.junk[:], self.h2[:, si, :], AF.Square, accum_out=self.ss[:, si:si + 1]))
                S.op('dve', [('ss', 0), ('ss', 1)], ['rstd'], lambda e, bcb=bcb, xs_tok=xs_tok, xw=xw, btok=btok, xdt=xdt, cb3=cb3: e.tensor_scalar(self.rstd[:], self.ss[:], 1.0 / 256, 4 * RMS_EPS, ALU.mult, ALU.add))
                S.op('pool', ['rstd', 'nh'], ['rstd'], lambda e, bcb=bcb, xs_tok=xs_tok, xw=xw, btok=btok, xdt=xdt, cb3=cb3: e.tensor_tensor(self.rstd[:], self.rstd[:], self.nh[:, 0:2], ALU.pow))
                for si in range(2):
                    S.op('dve', ['yv', 'rstd', nk], ['yv'], lambda e, bcb=bcb, xs_tok=xs_tok, xw=xw, btok=btok, xdt=xdt, cb3=cb3, si=si, nw=nw: e.scalar_tensor_tensor(
                        self.yn[:, si, :], in0=self.h2[:, si, :], scalar=self.rstd[:, si:si + 1], in1=nw[:], op0=ALU.mult, op1=ALU.mult))
                self.tps(['yv', 'cst'], ['PD'],
                         [(P7[:, j * 256 + si * 128:j * 256 + (si + 1) * 128], self.yn[:, si, j * 128:(j + 1) * 128]) for j in range(2) for si in range(2)], self.ident_f)
                S.op('act', ['PD'], [('yT', g, c)], lambda e, bcb=bcb, xs_tok=xs_tok, xw=xw, btok=btok, xdt=xdt, cb3=cb3, g=g, cols=cols: e.activation(self.yT[:, 2 * g:2 * g + 2, cols], P7.rearrange("p (j t) -> p j t", j=2), AF.Copy))
                self.rec = None
                X = F2 + Bq
                if prevB is None:
                    for t in F1:
                        t()
                else:
                    n1, n2 = len(F1), len(prevB)
                    i1 = i2 = 0
                    while i1 < n1 or i2 < n2:
                        if i2 >= n2 or (i1 < n1 and i1 * n2 <= i2 * n1):
                            F1[i1]()
                            i1 += 1
                        else:
                            prevB[i2]()
                            i2 += 1
                prevB = X
                if c == 0 and g > 0:
                    load_z(g)
        for t in prevB:
            t()
        if 'yT' in self.dbg:
            if True:
                self.dbgf = self.sb("dbgf", [128, 512])
            for kc in range(16):
                for q4 in range(4):
                    S.op('dve', [('yT', g, c) for g in range(8) for c in range(8)] + [('dbg', 'yT')], ['dbgf'], lambda e, kc=kc, q4=q4: e.tensor_copy(self.dbgf[:, 0:512], self.yT[:, kc, q4 * 512:(q4 + 1) * 512]))
                    S.dma('sp', ['dbgf'], [('dbg', 'yT')], lambda e, kc=kc, q4=q4: e.dma_start(out=self.dbg['yT'][kc * 128:(kc + 1) * 128, q4 * 512:(q4 + 1) * 512], in_=self.dbgf[:, 0:512]))
            self.final_keys.append(('dbg', 'yT'))

    def merge(self, l):
        self.psum_epoch()
        S, sb = self.S, self.sb
        self.phase([self.R_W])
        if True:
            self.wc1_b = [sb("wc1b%d" % i, [128, 5120], BF16) for i in range(2)]
            self.ta = [sb("ta%d" % i, [128, 512]) for i in range(2)]
            self.tb = [sb("tb%d" % i, [128, 512]) for i in range(2)]
        mT = self.B3
        xk = self.xT_keys()
        attk = [('attT', hp, h) for hp in range(8) for h in range(2)]
        yk = [('yT', g, c) for g in range(8) for c in range(8)]
        it = 0
        def load_c(fc):
            wb = self.wc1_b[fc % 2]
            S.dma('pool', [], [('wc1', fc % 2, 0)], lambda e, wb=wb, fc=fc: e.dma_start(
                out=wb[:].rearrange("p (a b) -> p a b", a=5), in_=self.wc1[l, fc].rearrange("p (a b) -> p a b", a=5), max_dma_last_dim=4096))
        load_c(0)
        for fc in range(8):
            wb = self.wc1_b[fc % 2]
            wks = [('wc1', fc % 2, 0)]
            if fc + 1 < 8:
                load_c(fc + 1)
            wg = wb[:, 0:2048].rearrange("p (k n) -> p k n", k=8)
            wa = wb[:, 2048:3072].rearrange("p (k n) -> p k n", k=8)
            ws = wb[:, 3072:5120].rearrange("p (k n) -> p k n", k=16)
            for tc in range(4):
                cols = slice(tc * 512, (tc + 1) * 512)
                if it % 2 == 0:
                    pga, pgs, ppa, pps = (self.PA[:, i * 512:(i + 1) * 512] for i in range(4))
                    bk = [('PA', 0), ('PA', 1), ('PA', 2), ('PA', 3)]
                else:
                    pga, pgs, ppa, pps = self.PB[:, 0:512], self.PB[:, 512:1024], self.PC[:, 0:512], self.PD[:, 0:512]
                    bk = [('PB', 0), ('PB', 1), 'PC', 'PD']
                self.mm(wks + xk[tc * 4:(tc + 1) * 4], [bk[0]], pga, [(wg[:, kc, 0:128], self.xT[:, kc, cols]) for kc in range(8)])
                self.mm(wks + xk[tc * 4:(tc + 1) * 4], [bk[1]], pgs, [(wg[:, kc, 128:256], self.xT[:, kc, cols]) for kc in range(8)])
                self.mm(wks + attk, [bk[2]], ppa, [(wa[:, kc, :], self.B1[:, kc, cols]) for kc in range(8)])
                self.mm(wks + yk, [bk[3]], pps, [(ws[:, kc, :], self.yT[:, kc, cols]) for kc in range(16)])
                ta, tb = self.ta[it % 2], self.tb[it % 2]
                tak, tbk = ('ta', it % 2), ('tb', it % 2)
                it += 1
                S.op('act', [bk[0]], [tak], lambda e, ta=ta, pga=pga: e.activation(ta[:], pga, AF.Tanh, scale=0.5))
                S.op('act', [bk[1]], [tbk], lambda e, tb=tb, pgs=pgs: e.activation(tb[:], pgs, AF.Tanh, scale=0.5))
                S.op('dve', [tak, bk[2]], [tak], lambda e, ta=ta, ppa=ppa: e.scalar_tensor_tensor(ta[:], in0=ta[:], scalar=1.0, in1=ppa, op0=ALU.add, op1=ALU.mult))
                S.op('dve', [tbk, bk[3]], [tbk], lambda e, tb=tb, pps=pps: e.scalar_tensor_tensor(tb[:], in0=tb[:], scalar=1.0, in1=pps, op0=ALU.add, op1=ALU.mult))
                S.op('pool', [tak, tbk], [('mergedT', fc, tc)], lambda e, ta=ta, tb=tb, fc=fc, cols=cols: e.tensor_tensor(mT[:, fc, cols], ta[:], tb[:], ALU.add))

    def layernorm(self, t, tk, g_b, b_b, gk, i):
        S = self.S
        st = self.ln_st[i % 2]
        mv = self.ln_mv[i % 2]
        sk = ('lnst', i % 2)

        def stats(e):
            e.bn_stats(st[:, 0, :], t[:, 0:512])
            return e.bn_stats(st[:, 1, :], t[:, 512:1024])
        S.op('dve', [tk], [sk], stats)
        S.op('dve', [sk], [sk], lambda e: e.bn_aggr(mv[:, 0:2], st[:].rearrange("p a b -> p (a b)")))
        S.op('dve', [sk], [sk], lambda e: e.tensor_scalar(mv[:, 2:3], mv[:, 1:2], LN_EPS, None, ALU.add))
        S.op('pool', [sk, 'nh'], [sk], lambda e: e.tensor_tensor(mv[:, 3:4], mv[:, 2:3], self.nh[:, 0:1], ALU.pow))
        S.op('dve', [tk, sk], [tk], lambda e: e.tensor_scalar(t[:], t[:], mv[:, 0:1], mv[:, 3:4], ALU.subtract, ALU.mult))
        S.op('pool', [tk, gk], [tk], lambda e: e.tensor_tensor(t[:], t[:], g_b, ALU.mult))
        S.op('dve', [tk, gk], [tk], lambda e: e.tensor_tensor(t[:], t[:], b_b, ALU.add))

    def mix_ln1(self, l):
        self.psum_epoch()
        S, sb = self.S, self.sb
        self.phase([self.R_W, self.R_B2])
        if True:
            self.wout_b = sb("woutb", [128, 8, 1024], BF16)
            self.lnpb = sb("lnpb", [128, 4096])
            self.res = [sb("res%d" % i, [128, D]) for i in range(2)]
            self.ln_st = [sb("lnst%d" % i, [128, 2, 6]) for i in range(2)]
            self.ln_mv = [sb("lnmv%d" % i, [128, 4]) for i in range(2)]
        S.dma('pool', [], [('wout', 0)], lambda e: e.dma_start(
            out=self.wout_b[:], in_=self.wout[l].rearrange("p (k n) -> p k n", k=8), max_dma_last_dim=4096))
        lnpb1 = self.lnpb
        S.dma('sp', [], ['lnpb'], lambda e: e.dma_start(out=lnpb1[:], in_=self.lnp[l].broadcast_to([128, 4096])))
        src = self.x_in if l == 0 else self.x1
        srck = 'x_in' if l == 0 else 'x1'
        mk = [('mergedT', fc, tc) for fc in range(8) for tc in range(4)]
        wk = [('wout', 0)]
        for tt in range(16):
            r = self.res[tt % 2]
            rk = ('res', tt % 2)
            S.dma('sp', [(srck, tt)], [rk], lambda e, r=r, tt=tt: e.dma_start(out=r[:], in_=src[tt * 128:(tt + 1) * 128, :]))
            S.op('act', [rk], [rk], lambda e, r=r: e.activation(r[:], r[:], AF.Copy, scale=float(ALPHA)))
            if tt % 2 == 0:
                pm = self.PB[:, 0:1024]
                rkeys = [('PB', 0), ('PB', 1)]
            else:
                pm = self.PA[:, 0:1024]
                rkeys = [('PA', 0), ('PA', 1)]
            self.mmseq(mk + wk, rkeys,
                       [(pm[:, half * 512:(half + 1) * 512], [(self.B3[:, kc, tt * 128:(tt + 1) * 128], self.wout_b[:, kc, half * 512:(half + 1) * 512]) for kc in range(8)]) for half in range(2)])
            S.op('dve', rkeys + [rk], [rk], lambda e, r=r, pm=pm: e.scalar_tensor_tensor(r[:], in0=pm, scalar=0.5, in1=r[:], op0=ALU.mult, op1=ALU.add))
            self.layernorm(r, rk, self.lnpb[:, 0:1024], self.lnpb[:, 1024:2048], 'lnpb', tt)
            S.dma('sp', [rk], [('hres', tt)], lambda e, r=r, tt=tt: e.dma_start(out=self.hres[tt * 128:(tt + 1) * 128, :], in_=r[:]))
            self.to_featmajor2(r, rk, self.B1, 'hT', tt)
        if 'hT' in self.dbg:
            if True:
                self.dbgf = self.sb("dbgf", [128, 512])
            for kc in range(8):
                for q4 in range(4):
                    S.op('dve', [('hT', tt) for tt in range(16)] + [('dbg', 'hT')], ['dbgf'], lambda e, kc=kc, q4=q4: e.tensor_copy(self.dbgf[:, 0:512], self.B1[:, kc, q4 * 512:(q4 + 1) * 512]))
                    S.dma('sp', ['dbgf'], [('dbg', 'hT')], lambda e, kc=kc, q4=q4: e.dma_start(out=self.dbg['hT'][kc * 128:(kc + 1) * 128, q4 * 512:(q4 + 1) * 512], in_=self.dbgf[:, 0:512]))
            self.final_keys.append(('dbg', 'hT'))

    def to_featmajor2(self, tile, tkey, dstT, dkey, tt):
        S = self.S
        for half in range(2):
            ps = self.PC[:, 0:512] if half == 0 else self.PD[:, 0:512]
            pk = 'PC' if half == 0 else 'PD'
            self.tps([tkey, 'cst'], [pk], [(ps[:, j * 128:(j + 1) * 128], tile[:, (half * 4 + j) * 128:(half * 4 + j + 1) * 128]) for j in range(4)], self.ident_f)
            S.op('act', [pk], [(dkey, tt)], lambda e, ps=ps, half=half: e.activation(
                dstT[:, half * 4:(half + 1) * 4, tt * 128:(tt + 1) * 128], ps.rearrange("p (c t) -> p c t", c=4), AF.Copy))

    def ffn(self, l):
        self.psum_epoch()
        S, sb = self.S, self.sb
        self.phase([self.R_W, (self.R_B3[0] + 4096, self.R_B3[1])])
        if True:
            self.wup_b = [sb("wupb%d" % i, [128, 8, 512], BF16) for i in range(2)]
            self.wdn_b = [sb("wdnb%d" % i, [128, 4, 1024], BF16) for i in range(2)]
            self.rl = [sb("rl%d" % i, [128, 512]) for i in range(2)]
            self.ln_st = [sb("lnst%d" % i, [128, 2, 6]) for i in range(2)]
            self.ln_mv = [sb("lnmv%d" % i, [128, 4]) for i in range(2)]
        uT = self.B3
        hk = [('hT', tt) for tt in range(16)]
        it = 0
        def load_f(gi):
            wu = self.wup_b[gi % 2]
            wd = self.wdn_b[gi % 2]
            wuk, wdk = ('wup', gi % 2), ('wdn', gi % 2)
            S.dma('pool', [], [(wuk, 0)], lambda e, wu=wu, gi=gi: e.dma_start(
                out=wu[:], in_=self.wup[l, gi].rearrange("p (k n) -> p k n", k=8), max_dma_last_dim=4096))
            S.dma('pool', [], [(wdk, 0)], lambda e, wd=wd, gi=gi: e.dma_start(
                out=wd[:], in_=self.wdn[l, gi].rearrange("p (k n) -> p k n", k=4), max_dma_last_dim=4096))
        load_f(0)
        for gi in range(8):
            wu = self.wup_b[gi % 2]
            wd = self.wdn_b[gi % 2]
            wuk, wdk = ('wup', gi % 2), ('wdn', gi % 2)
            if gi + 1 < 8:
                load_f(gi + 1)
            wuks = [(wuk, 0)]
            wdks = [(wdk, 0)]
            for j in range(4):
                for tc in range(4):
                    bank = it % 4
                    ps = [self.PA[:, 0:512], self.PA[:, 512:1024], self.PC[:, 0:512], self.PD[:, 0:512]][bank]
                    pk = [('PA', 0), ('PA', 1), 'PC', 'PD'][bank]
                    cols = slice(tc * 512, (tc + 1) * 512)
                    self.mm(wuks + hk[tc * 4:(tc + 1) * 4], [pk], ps, [(wu[:, kc, j * 128:(j + 1) * 128], self.B1[:, kc, cols]) for kc in range(8)])
                    rl = self.rl[it % 2]
                    rlk = ('rl', it % 2)
                    S.op('act', [pk], [rlk], lambda e, rl=rl, ps=ps: e.activation(rl[:], ps, AF.Relu))
                    eng = 'dve'
                    S.op(eng, [rlk], [('uT', j)], lambda e, rl=rl, j=j, cols=cols: e.tensor_tensor(uT[:, j, cols], rl[:], rl[:], ALU.mult))
                    it += 1
            for tt in range(16):
                if tt % 2 == 0:
                    pm = self.PB[:, 0:1024]
                    rkeys = [('PB', 0), ('PB', 1)]
                else:
                    pm = self.PA[:, 1024:2048]
                    rkeys = [('PA', 2), ('PA', 3)]
                self.mmseq([('uT', j) for j in range(4)] + wdks, rkeys,
                           [(pm[:, half * 512:(half + 1) * 512], [(uT[:, j, tt * 128:(tt + 1) * 128], wd[:, j, half * 512:(half + 1) * 512]) for j in range(4)]) for half in range(2)])
                if gi == 0:
                    S.op('act', rkeys, [('acc', tt)], lambda e, tt=tt, pm=pm: e.activation(self.acc[:, tt, :], pm, AF.Copy))
                else:
                    S.op('dve', rkeys + [('acc', tt)], [('acc', tt)], lambda e, tt=tt, pm=pm: e.tensor_tensor(self.acc[:, tt, :], pm, self.acc[:, tt, :], ALU.add))
        S.barrier()
        self.res = [self.wup_b[i][:, 0:4, :].rearrange("p a b -> p (a b)").bitcast(F32) for i in range(2)]
        self.lnpb = self.wdn_b[0][:].rearrange("p a b -> p (a b)").bitcast(F32)
        lnpb2 = self.lnpb
        S.dma('sp', [], ['lnpb'], lambda e: e.dma_start(out=lnpb2, in_=self.lnp[l][:, 2048:4096].broadcast_to([128, 2048])))
        dst = self.out if l == DEPTH - 1 else self.x1
        dstk = 'out' if l == DEPTH - 1 else 'x1'
        for tt in range(16):
            r = self.res[tt % 2]
            rk = ('res', tt % 2)
            S.dma('sp', [('hres', tt)], [rk], lambda e, r=r, tt=tt: e.dma_start(out=r[:], in_=self.hres[tt * 128:(tt + 1) * 128, :]))
            S.op('dve', [rk, ('acc', tt)], [rk], lambda e, r=r, tt=tt: e.scalar_tensor_tensor(r[:], in0=r[:], scalar=float(ALPHA), in1=self.acc[:, tt, :], op0=ALU.mult, op1=ALU.add))
            self.layernorm(r, rk, self.lnpb[:, 0:1024], self.lnpb[:, 1024:2048], 'lnpb', tt)
            S.dma('sp', [rk], [(dstk, tt)], lambda e, r=r, tt=tt: e.dma_start(out=dst[tt * 128:(tt + 1) * 128, :], in_=r[:]))
            if l == DEPTH - 1:
                self.final_keys.append((dstk, tt))
            else:
                self.to_featmajor2(r, rk, self.xT, 'xT', tt)


def sel_max(e, b):
    ins = None
    for t in range(8):
        ins = e.max(b.m8[:, t, :], b.gm[:, t * 8:(t + 1) * 8])
    return ins


def sel_lt(e, b):
    ins = None
    for t in range(8):
        ins = e.tensor_scalar(b.lt[:, t * 8:(t + 1) * 8], b.gm[:, t * 8:(t + 1) * 8], b.m8[:, t, 2:3], NEG, ALU.is_lt, ALU.mult)
    return ins


_CACHE = {}


def kernel(**inputs):
    inp = {k: np.asarray(v) for k, v in inputs.items()}
    w = prep_weights(inp)
    consts, e8 = make_consts()
    x = np.ascontiguousarray(inp['x'], dtype=np.float32)
    pos = np.ascontiguousarray(inp['positions'], dtype=np.int32)
    nc = Builder().build()
    in_maps = []
    for b in range(NCORES):
        m = {"x": x[b], "pos": pos[b:b + 1], "consts": consts, "e8": e8}
        m.update(w)
        in_maps.append(m)
    res = run_bass_kernel_spmd(nc, in_maps, core_ids=list(range(NCORES)))
    out = np.stack([np.asarray(r["out"], dtype=np.float32) for r in res.results], axis=0)
    return out
```

```python
import math
import numpy as np
import concourse.bass as bass
import concourse.mybir as mybir
from concourse.bass_utils import run_bass_kernel_spmd

F32 = mybir.dt.float32
BF16 = mybir.dt.bfloat16
I32 = mybir.dt.int32
AF = mybir.ActivationFunctionType
ALU = mybir.AluOpType
AX = mybir.AxisListType

D = 1024
SEQ = 2048
DEPTH = 2
NCORES = 8
IN_W = 11296
ALPHA = (2 * DEPTH) ** 0.25
LN_EPS = 1e-5
RMS_EPS = 1e-5
NEG = -1.0e5
PI = math.pi

DEBUG = {}
STOP_AFTER = None
NO_BARRIER = False
VARIANT = 0


class Sched:
    ENG = ('pe', 'act', 'dve', 'pool', 'sp')

    def __init__(self, nc, n_dma_sems=24):
        self.nc = nc
        self.sem = {e: nc.alloc_semaphore('s_' + e) for e in self.ENG}
        self.cnt = {e: 0 for e in self.ENG}
        self.dsem = [nc.alloc_semaphore('d%d' % i) for i in range(n_dma_sems)]
        self.dcnt = [0] * n_dma_sems
        self.drr = {'sp': 0, 'pool': 0}
        self.dpool = {'sp': list(range(0, n_dma_sems // 2)), 'pool': list(range(n_dma_sems // 2, n_dma_sems))}
        self.waited = {}
        self.last_w = {}
        self.readers = {}
        self.q = {e: [] for e in self.ENG}

    def barrier(self):
        if NO_BARRIER is True:
            return
        self.epoch = {}
        scr = self.bar_scratch
        d_sp = {('dma', i): self.dcnt[i] for i in self.dpool['sp'] if self.dcnt[i] > 0}
        d_pl = {('dma', i): self.dcnt[i] for i in self.dpool['pool'] if self.dcnt[i] > 0}
        self._token('act', d_sp, lambda e: e.memzero(scr[:, 0:1]))
        t1 = ('act', self.cnt['act'])
        d_pl[t1[0]] = t1[1]
        self._token('pool', d_pl, lambda e: e.memset(scr[:, 1:2], 0.0))
        d3 = {e: c for e, c in self.cnt.items() if c > 0}
        self._token('dve', d3, lambda e: e.memset(scr[:, 2:3], 0.0))
        self.epoch = {'dve': self.cnt['dve']}

    def _token(self, e, deps, emit):
        waits = self._waits(e, deps)
        self.cnt[e] += 1
        sem = self.sem[e]

        def thunk(eng):
            for s, v in waits:
                eng.wait_ge(s, v)
            emit(eng).then_inc(sem, 1)
        self.q[e].append(thunk)

    def _deps(self, reads, writes):
        deps = dict(getattr(self, 'epoch', {}))

        def add(src, idx):
            if deps.get(src, 0) < idx:
                deps[src] = idx
        for k in reads:
            lw = self.last_w.get(k)
            if lw is not None:
                add(*lw)
        for k in writes:
            lw = self.last_w.get(k)
            if lw is not None:
                add(*lw)
            for src, idx in self.readers.get(k, {}).items():
                add(src, idx)
        return deps

    def _waits(self, e, deps):
        out = []
        self.maxw = getattr(self, 'maxw', {})
        for src, idx in deps.items():
            if self.waited.get((e, src), 0) >= idx:
                continue
            self.waited[(e, src)] = idx
            if isinstance(src, tuple):
                out.append((self.dsem[src[1]], 16 * idx))
            else:
                out.append((self.sem[src], idx))
        self.maxw[len(out)] = self.maxw.get(len(out), 0) + 1
        return out

    def _record(self, token, reads, writes):
        src, idx = token
        for k in reads:
            r = self.readers.setdefault(k, {})
            if r.get(src, 0) < idx:
                r[src] = idx
        for k in writes:
            self.last_w[k] = (src, idx)
            self.readers[k] = {}

    @staticmethod
    def _bank(k):
        if isinstance(k, tuple) and k and k[0] == 'PA':
            return ('BANK', k[1])
        if isinstance(k, tuple) and k and k[0] == 'PB':
            return ('BANK', 4 + k[1])
        if k in ('PC', 'PCg', 'PCo'):
            return ('BANK', 6)
        if k == 'PD':
            return ('BANK', 7)
        return None

    def _canon(self, reads, writes):
        r2, w2 = [], []
        for k in reads:
            b = self._bank(k)
            if b is None:
                r2.append(k)
            elif b not in w2:
                w2.append(b)
        for k in writes:
            b = self._bank(k)
            if b is None:
                w2.append(k)
            elif b not in w2:
                w2.append(b)
        return r2, w2

    def op(self, e, reads, writes, emit):
        reads, writes = self._canon(reads, writes)
        deps = self._deps(reads, writes)
        waits = self._waits(e, deps)
        self.cnt[e] += 1
        idx = self.cnt[e]
        sem = self.sem[e]

        def thunk(eng):
            for s, v in waits:
                eng.wait_ge(s, v)
            emit(eng).then_inc(sem, 1)
        self.q[e].append(thunk)
        self._record((e, idx), reads, writes)

    def dma(self, e, reads, writes, emit):
        pool = self.dpool[e]
        s = pool[self.drr[e] % len(pool)]
        self.drr[e] += 1
        src = ('dma', s)
        deps = self._deps(reads, writes)
        if self.dcnt[s] > 0 and deps.get(src, 0) < self.dcnt[s]:
            deps[src] = self.dcnt[s]
        waits = self._waits(e, deps)
        self.dcnt[s] += 1
        idx = self.dcnt[s]
        sem = self.dsem[s]

        def thunk(eng):
            for sm, v in waits:
                eng.wait_ge(sm, v)
            emit(eng).then_inc(sem, 16)
        self.q[e].append(thunk)
        self._record((src, idx), reads, writes)

    def alias(self, old_keys, new_keys):
        acc = {}
        for k in old_keys:
            lw = self.last_w.get(k)
            if lw is not None and acc.get(lw[0], 0) < lw[1]:
                acc[lw[0]] = lw[1]
            for src, idx in self.readers.get(k, {}).items():
                if acc.get(src, 0) < idx:
                    acc[src] = idx
        for k in new_keys:
            r = self.readers.setdefault(k, {})
            for src, idx in acc.items():
                if r.get(src, 0) < idx:
                    r[src] = idx

    def finish(self, final_keys):
        self.barrier()
        deps = self._deps(final_keys, [])
        waits = self._waits('sp', deps)

        def thunk(eng):
            for s, v in waits:
                eng.wait_ge(s, v)
        self.q['sp'].append(thunk)
        nc = self.nc
        q = self.q
        with nc.Block() as block:
            @block.sync
            def _(eng):
                for t in q['sp']:
                    t(eng)

            @block.tensor
            def _(eng):
                for t in q['pe']:
                    t(eng)

            @block.scalar
            def _(eng):
                for t in q['act']:
                    t(eng)

            @block.vector
            def _(eng):
                for t in q['dve']:
                    t(eng)

            @block.gpsimd
            def _(eng):
                for t in q['pool']:
                    t(eng)


C_ID, C_TRI, C_ONES, C_RM, C_INVF, C_SGN, C_NEGP, C_FLOOR, NC_CONST = 0, 128, 256, 384, 512, 513, 514, 578, 648


def make_consts():
    c = np.zeros((128, NC_CONST), np.float32)
    c[:, C_ID:C_ID + 128] = np.eye(128, dtype=np.float32)
    c[:, C_TRI:C_TRI + 128] = np.triu(np.ones((128, 128), np.float32))
    c[:, C_ONES:C_ONES + 128] = 1.0
    rm = np.zeros((128, 128), np.float32)
    inv = (500000.0 ** (-np.arange(0, 16, 2, dtype=np.float32) / 16)).astype(np.float32)
    for blk in range(2):
        for d in range(16):
            src = d + 8 if d < 8 else d - 8
            rm[blk * 64 + src, blk * 64 + d] = 1.0
    c[:, C_RM:C_RM + 128] = rm
    for p in range(128):
        d = p % 64
        if d < 16:
            c[p, C_INVF] = inv[d % 8]
            c[p, C_SGN] = -1.0 if d < 8 else 1.0
    for qt in range(8):
        blk = (8 + qt) // 2
        for j in range(8):
            c[:, C_NEGP + qt * 8 + j] = 0.0 if j < blk else -1.0e30
            c[:, C_FLOOR + qt * 8 + j] = NEG if j < blk else 0.0
    e8 = np.zeros((8, SEQ), np.float32)
    for j in range(8):
        e8[j, j * 256:(j + 1) * 256] = 1.0
    return c, e8


def prep_weights(inp):
    out = {}
    L = DEPTH

    def kmaj(w):
        K, N = w.shape
        return w.reshape(K // 128, 128, N).transpose(1, 0, 2)

    w_in = inp['w_in']
    wqkv = np.empty((L, 8, 128, 8, 384), np.float32)
    wssm = np.empty((L, 8, 128, 8, 768), np.float32)
    wdt = np.empty((L, 128, 8, 32), np.float32)
    wc1 = np.empty((L, 8, 128, 5120), np.float32)
    wout = np.empty((L, 128, 8, 1024), np.float32)
    wup = np.empty((L, 8, 128, 8, 512), np.float32)
    wdn = np.empty((L, 8, 128, 4, 1024), np.float32)
    cw = np.empty((L, 128, 8, 4, 5), np.float32)
    for l in range(L):
        wi = kmaj(w_in[l])
        for hp in range(8):
            wqkv[l, hp, :, :, 0:128] = wi[:, :, hp * 128:(hp + 1) * 128]
            wqkv[l, hp, :, :, 128:256] = wi[:, :, 1024 + hp * 128:1024 + (hp + 1) * 128]
            wqkv[l, hp, :, :, 256:384] = wi[:, :, 2048 + hp * 128:2048 + (hp + 1) * 128]
        for g in range(8):
            wssm[l, g, :, :, 0:256] = wi[:, :, 3072 + g * 256:3072 + (g + 1) * 256]
            wssm[l, g, :, :, 256:512] = wi[:, :, 5120 + g * 256:5120 + (g + 1) * 256]
            wssm[l, g, :, :, 512:640] = wi[:, :, 7168 + g * 128:7168 + (g + 1) * 128]
            wssm[l, g, :, :, 640:768] = wi[:, :, 8192 + g * 128:8192 + (g + 1) * 128]
        wdt[l] = wi[:, :, 9216:9248]
        wap = kmaj(inp['w_attn_proj'][l])
        wsp = kmaj(inp['w_ssm_proj'][l])
        for fc in range(8):
            blk = np.empty((128, 8, 256), np.float32)
            blk[:, :, 0:128] = wi[:, :, 9248 + fc * 128:9248 + (fc + 1) * 128]
            blk[:, :, 128:256] = wi[:, :, 10272 + fc * 128:10272 + (fc + 1) * 128]
            wc1[l, fc, :, 0:2048] = blk.reshape(128, 2048)
            wc1[l, fc, :, 2048:3072] = wap[:, :, fc * 128:(fc + 1) * 128].reshape(128, 1024)
            wc1[l, fc, :, 3072:5120] = wsp[:, :, fc * 128:(fc + 1) * 128].reshape(128, 2048)
        wout[l] = kmaj(inp['w_out'][l])
        wu = kmaj(inp['w_up'][l])
        wd = kmaj(inp['w_down'][l])
        for gI in range(8):
            wup[l, gI] = wu[:, :, gI * 512:(gI + 1) * 512]
            wdn[l, gI] = wd[:, gI * 4:(gI + 1) * 4, :]
        cwl = inp['conv_w'][l]
        cbl = inp['conv_b'][l]
        for g in range(8):
            offs = [g * 256, g * 256 + 128, 2048 + g * 128, 3072 + g * 128]
            for ci, o in enumerate(offs):
                cw[l, :, g, ci, 0:4] = cwl[:, o:o + 128].T
                cw[l, :, g, ci, 4] = cbl[o:o + 128]
    out['wqkv'] = wqkv.reshape(L, 8, 128, 3072)
    out['wssm'] = wssm.reshape(L, 8, 128, 6144)
    out['wdt'] = wdt.reshape(L, 128, 256)
    out['wc1'] = wc1
    out['wout'] = wout.reshape(L, 128, 8192)
    out['wup'] = wup.reshape(L, 8, 128, 4096)
    out['wdn'] = wdn.reshape(L, 8, 128, 4096)
    out['cw'] = cw.reshape(L, 128, 160)
    small = np.concatenate([inp['dt_bias'], inp['a_log'], inp['d_skip']], axis=1)
    out['small'] = np.ascontiguousarray(small.reshape(L, 1, 96))
    out['normw'] = np.ascontiguousarray(inp['ssm_norm_w'].reshape(L, 1, 2048))
    lnp = np.stack([inp['ln1_g'], inp['ln1_b'], inp['ln2_g'], inp['ln2_b']], axis=1)
    out['lnp'] = np.ascontiguousarray(lnp.reshape(L, 1, 4096))
    return {k: np.ascontiguousarray(v, dtype=np.float32) for k, v in out.items()}


class Arena:
    def __init__(self, base_ap, segments):
        self.base = base_ap
        self.segs = [list(x) for x in segments]

    def alloc(self, shape, dtype=F32):
        P = shape[0]
        n = 1
        for d in shape[1:]:
            n *= d
        per = 4 if dtype in (F32, I32) else 2
        ncols = (n * per + 3) // 4
        ncols = (ncols + 7) // 8 * 8
        for sg in self.segs:
            if sg[1] - sg[0] >= ncols:
                off = sg[0]
                sg[0] += ncols
                ap = self.base[0:P, off:off + (n * per + 3) // 4]
                if dtype != F32:
                    ap = ap.bitcast(dtype)
                if len(shape) == 3:
                    ap = ap.rearrange("p (a b) -> p a b", a=shape[1])
                elif len(shape) == 4:
                    ap = ap.rearrange("p (a b c) -> p a b c", a=shape[1], b=shape[2])
                return ap
        raise RuntimeError("arena out of memory for %s" % (shape,))


class _Stop(Exception):
    pass


class Builder:
    def __init__(self):
        nc = bass.Bass("TRN2", target_bir_lowering=False)
        self.nc = nc
        self.S = Sched(nc)
        self.rec = None
        S_ = self.S
        S_._op, S_._dma = S_.op, S_.dma

        def _rop(*a):
            if self.rec is not None:
                self.rec.append(lambda: S_._op(*a))
            else:
                S_._op(*a)

        def _rdma(*a):
            if self.rec is not None:
                self.rec.append(lambda: S_._dma(*a))
            else:
                S_._dma(*a)
        S_.op, S_.dma = _rop, _rdma
        L = DEPTH
        dt = lambda n, s, d=F32, k="ExternalInput": nc.dram_tensor(n, s, d, kind=k).ap()
        self.x_in = dt("x", [SEQ, D])
        self.pos_in = dt("pos", [1, SEQ], I32)
        self.consts_in = dt("consts", [128, NC_CONST])
        self.e8_in = dt("e8", [8, SEQ])
        self.wqkv = dt("wqkv", [L, 8, 128, 3072])
        self.wssm = dt("wssm", [L, 8, 128, 6144])
        self.wdt = dt("wdt", [L, 128, 256])
        self.wc1 = dt("wc1", [L, 8, 128, 5120])
        self.wout = dt("wout", [L, 128, 8192])
        self.wup = dt("wup", [L, 8, 128, 4096])
        self.wdn = dt("wdn", [L, 8, 128, 4096])
        self.cw = dt("cw", [L, 128, 160])
        self.small = dt("small", [L, 1, 96])
        self.normw = dt("normw", [L, 1, 2048])
        self.lnp = dt("lnp", [L, 1, 4096])
        self.out = dt("out", [SEQ, D], F32, "ExternalOutput")
        self.x1 = dt("x1s", [SEQ, D], F32, "Internal")
        self.hres = dt("hres", [SEQ, D], F32, "Internal")
        self.dbg = {}
        for name, shape in DEBUG.items():
            self.dbg[name] = dt("dbg_" + name, list(shape), F32, "ExternalOutput")
        self.final_keys = []

    def sb(self, name, shape, dtype=F32):
        return self.ar.alloc(list(shape), dtype)

    def phase(self, segs):
        self.S.barrier()
        self.ar = Arena(self.arena, segs)

    def mm(self, reads, writes, out, pairs, transpose=False):
        def emit(e):
            n = len(pairs)
            ins = None
            for i, (l, r) in enumerate(pairs):
                ins = e.matmul(out, lhsT=l, rhs=r, start=(i == 0), stop=(i == n - 1))
            return ins
        self.S.op('pe', reads, writes, emit)

    def mmseq(self, reads, writes, groups):
        def emit(e):
            ins = None
            for out, pairs in groups:
                n = len(pairs)
                for i, (l, r) in enumerate(pairs):
                    ins = e.matmul(out, lhsT=l, rhs=r, start=(i == 0), stop=(i == n - 1))
            return ins
        self.S.op('pe', reads, writes, emit)

    def tps(self, reads, writes, items, ident):
        def emit(e):
            ins = None
            for o, i in items:
                ins = e.transpose(o, i, ident)
            return ins
        self.S.op('pe', reads, writes, emit)

    def dump(self, name, ap, keys):
        if name in self.dbg:
            if not isinstance(keys, list):
                keys = [keys]
            self.S.dma('sp', keys, [('dbg', name)], lambda e: e.dma_start(out=self.dbg[name], in_=ap))
            self.final_keys.append(('dbg', name))

    def build(self):
        nc, S = self.nc, self.S
        sb = self.sb
        nbytes = int(nc.sbuf_bytes_remaining) - 256
        NA = nbytes // 4 // 8 * 8
        self.arena = nc.alloc_sbuf_tensor("arena", [128, NA], F32)[:]
        self.R_P = (0, 11264)
        self.R_B1 = (11264, 19456)
        self.R_B2 = (19456, 35840)
        self.R_B3 = (35840, 44032)
        self.R_W = (44032, NA)
        assert NA - 44032 > 8000, NA
        self.ar = Arena(self.arena, [self.R_P])
        self.S.bar_scratch = sb("barscr", [128, 8])
        cst = sb("cst", [128, NC_CONST])
        self.cst = cst
        S.dma('sp', [], ['cst'], lambda e: e.dma_start(out=cst[:], in_=self.consts_in))
        self.ident_f = cst[:, C_ID:C_ID + 128]
        self.tri_f = cst[:, C_TRI:C_TRI + 128]
        self.ones_f = cst[:, C_ONES:C_ONES + 128]
        cbf = sb("cbf", [128, 512], BF16)
        S.op('dve', ['cst'], ['cbf'], lambda e: e.tensor_copy(cbf[:], cst[:, 0:512]))
        self.ident_b = cbf[:, 0:128]
        self.tri_b = cbf[:, 128:256]
        self.rm_b = cbf[:, 384:512]
        self.nh = sb("nh", [128, 8])
        S.op('pool', [], ['nh'], lambda e: e.memset(self.nh[:], -0.5))
        self.ropeC = sb("ropeC", [128, SEQ], BF16)
        self.ropeS = sb("ropeS", [128, SEQ], BF16)
        self.xT = sb("xT", [128, 8, SEQ], BF16)
        A = self.arena
        self.B1 = A[:, self.R_B1[0]:self.R_B1[1]].bitcast(BF16).rearrange("p (c t) -> p c t", c=8)
        self.B2 = A[:, self.R_B2[0]:self.R_B2[1]]
        self.B3 = A[:, self.R_B3[0]:self.R_B3[1]].bitcast(BF16).rearrange("p (c t) -> p c t", c=8)
        self.yT = self.B2.bitcast(BF16).rearrange("p (c t) -> p c t", c=16)
        self.acc = self.B2.rearrange("p (t f) -> p t f", t=16)
        self.ar = Arena(self.arena, [self.R_B2])
        self.PA = nc.alloc_psum_tensor("PA", [128, 2048], F32)
        self.PB = nc.alloc_psum_tensor("PB", [128, 1024], F32)
        self.PC = nc.alloc_psum_tensor("PC", [128, 512], F32)
        self.PD = nc.alloc_psum_tensor("PD", [128, 512], F32)
        self.rope_tables()
        if STOP_AFTER == 'rope':
            return self.finish()
        self.load_xT()
        if STOP_AFTER == 'xT':
            self.dbg_tile('xT0', self.xT[:, 3, 1024:1536], self.xT_keys(), 512)
            return self.finish()
        try:
            for l in range(DEPTH):
                self.layer(l)
                if STOP_AFTER == 'layer0':
                    break
        except _Stop:
            pass
        return self.finish()

    def finish(self):
        self.S.finish(self.final_keys)
        return self.nc

    def rope_tables(self):
        S, sb = self.S, self.sb
        cst = None
        posi = sb("posi", [128, SEQ], I32)
        S.dma('sp', [], ['posi'], lambda e: e.dma_start(out=posi[:], in_=self.pos_in.broadcast_to([128, SEQ])))
        ang = sb("ang", [128, SEQ])
        tmp = sb("rtmp", [128, SEQ])
        tmi = sb("rtmi", [128, SEQ], I32)
        rc = sb("rc", [128, SEQ])
        rs = sb("rs", [128, SEQ])
        invf = self.cst[:, C_INVF:C_INVF + 1]
        sgn = self.cst[:, C_SGN:C_SGN + 1]
        S.op('dve', ['posi'], ['ang'], lambda e: e.tensor_copy(ang[:], posi[:]))
        S.op('dve', ['ang', 'cst'], ['ang'], lambda e: e.tensor_scalar(ang[:], ang[:], invf, None, ALU.mult))
        for which, dst in ((0, rs), (1, rc)):
            off = 0.0 if which == 0 else PI / 2
            S.op('dve', ['ang'], ['rtmp'], lambda e, off=off: e.tensor_scalar(tmp[:], ang[:], off, 1.0 / (2 * PI), ALU.add, ALU.mult))
            S.op('dve', ['rtmp'], ['rtmi'], lambda e: e.tensor_copy(tmi[:], tmp[:]))
            S.op('dve', ['rtmi'], ['rtmp'], lambda e: e.tensor_copy(tmp[:], tmi[:]))
            S.op('dve', ['rtmp', 'ang'], ['rtmp'], lambda e: e.scalar_tensor_tensor(tmp[:], in0=tmp[:], scalar=-2 * PI, in1=ang[:], op0=ALU.mult, op1=ALU.add))
            S.op('dve', ['rtmp'], [('rope', which)], lambda e, off=off, dst=dst: e.tensor_scalar(dst[:], tmp[:], off, None, ALU.add))
            S.op('dve', [('rope', which)], ['rtmp'], lambda e, dst=dst: e.tensor_scalar(tmp[:], dst[:], PI, -2 * PI, ALU.is_gt, ALU.mult))
            S.op('dve', ['rtmp', ('rope', which)], [('rope', which)], lambda e, dst=dst: e.tensor_tensor(dst[:], dst[:], tmp[:], ALU.add))
            S.op('dve', [('rope', which)], ['rtmp'], lambda e, dst=dst: e.tensor_scalar(tmp[:], dst[:], -PI, 2 * PI, ALU.is_lt, ALU.mult))
            S.op('dve', ['rtmp', ('rope', which)], [('rope', which)], lambda e, dst=dst: e.tensor_tensor(dst[:], dst[:], tmp[:], ALU.add))
            S.op('dve', [('rope', which)], [('rope', which)], lambda e, dst=dst: e.tensor_scalar(dst[:], dst[:], 3.1415925, -3.1415925, ALU.min, ALU.max))
            S.op('act', [('rope', which)], [('rope', which)], lambda e, dst=dst: e.activation(dst[:], dst[:], AF.Sin))
        S.op('dve', [('rope', 0), 'cst'], [('rope', 0)], lambda e: e.tensor_scalar(self.ropeS[:], rs[:], sgn, None, ALU.mult))
        S.op('dve', [('rope', 1)], [('rope', 1)], lambda e: e.tensor_copy(self.ropeC[:], rc[:]))
        self.dump('ropeC', rc[:], ('rope', 1))
        self.dump('ropeS', rs[:], ('rope', 0))

    def load_xT(self):
        S, sb = self.S, self.sb
        self.xt_buf = [sb("xtile%d" % i, [128, D]) for i in range(2)]
        for tt in range(16):
            xt = self.xt_buf[tt % 2]
            k = ('xtile', tt % 2)
            S.dma('sp', [], [k], lambda e, xt=xt, tt=tt: e.dma_start(out=xt[:], in_=self.x_in[tt * 128:(tt + 1) * 128, :]))
            self.to_featmajor(xt, k, self.xT, 'xT', tt)

    def to_featmajor(self, tile, tkey, dstT, dkey, tt):
        S = self.S
        for half in range(2):
            ps = self.PA[:, half * 512:(half + 1) * 512]
            pk = ('PA', half)
            self.tps([tkey, 'cst'], [pk], [(ps[:, j * 128:(j + 1) * 128], tile[:, (half * 4 + j) * 128:(half * 4 + j + 1) * 128]) for j in range(4)], self.ident_f)
            eng = 'act' if half == 0 else 'dve'
            if eng == 'act':
                S.op('act', [pk], [(dkey, tt)], lambda e, ps=ps, half=half: e.activation(
                    dstT[:, half * 4:(half + 1) * 4, tt * 128:(tt + 1) * 128], ps.rearrange("p (c t) -> p c t", c=4), AF.Copy))
            else:
                S.op('dve', [pk], [(dkey, tt)], lambda e, ps=ps, half=half: e.tensor_copy(
                    dstT[:, half * 4:(half + 1) * 4, tt * 128:(tt + 1) * 128], ps.rearrange("p (c t) -> p c t", c=4)))

    def layer(self, l):
        self.attention(l)
        if STOP_AFTER == 'att':
            raise _Stop()
        self.ssd(l)
        if STOP_AFTER == 'ssd':
            raise _Stop()
        self.merge(l)
        if STOP_AFTER == 'merge':
            raise _Stop()
        self.mix_ln1(l)
        if STOP_AFTER == 'ln1':
            raise _Stop()
        self.ffn(l)

    def psum_epoch(self):
        keys = [('PA', i) for i in range(4)] + [('PB', 0), ('PB', 1), ('PB', 0, 0), ('PB', 0, 1)] + \
               [('PB', 1, h) for h in range(4)] + ['PC', 'PCg', 'PCo', 'PD']
        self.S.alias(keys, keys)

    def dbg_qk(self, h):
        S = self.S
        if 'qa0' in self.dbg:
            qa, ka = self.qa[h], self.ka[h]
            dq = self.sb("dbgq", [128, SEQ])
            dk = self.sb("dbgk", [128, SEQ])
            S.op('dve', [('qa', h)], ['dbgq'], lambda e: e.tensor_copy(dq[0:72, :], qa[0:72, :]))
            S.op('dve', [('ka', h)], ['dbgk'], lambda e: e.tensor_copy(dk[0:72, :], ka[0:72, :]))
            self.dump('qa0', dq[0:72, :], 'dbgq')
            self.dump('ka0', dk[0:72, :], 'dbgk')

    def dbg_tile(self, name, ap, keys, ncols):
        if name in self.dbg:
            t = self.sb("dbg_" + name, [128, ncols])
            self.S.op('dve', keys, ['dbgt_' + name], lambda e: e.tensor_copy(t[:], ap))
            self.dump(name, t[:], 'dbgt_' + name)

    def xT_keys(self):
        return [('xT', tt) for tt in range(16)]

    def attention(self, l):
        self.psum_epoch()
        S, sb, nc = self.S, self.sb, self.nc
        self.phase([self.R_W, self.R_B2, (self.R_B3[0] + 7168, self.R_B3[1])])
        if True:
            self.wqkv_b = [sb("wqkv%d" % i, [128, 8, 384], BF16) for i in range(2)]
            self.qa = [sb("qa%d" % i, [72, SEQ], BF16) for i in range(4)]
            self.ka = [sb("ka%d" % i, [72, SEQ], BF16) for i in range(4)]
            self.va = [sb("va%d" % i, [128, 16, 65], BF16) for i in range(4)]
            self.qbf = [sb("qbf%d" % i, [128, 512], BF16) for i in range(2)]
            self.t1 = [sb("t1_%d" % i, [128, 512]) for i in range(2)]
            self.t2 = [sb("t2_%d" % i, [128, 512]) for i in range(2)]
            self.atm = [sb("atm%d" % i, [128, 16, 128], BF16) for i in range(2)]
            self.km = sb("km", [64, 8])
            self.kmh = sb("kmh", [64, 8], BF16)
            self.kml = sb("kml", [64, 8], BF16)
            self.gm = sb("gm", [128, 64])
            self.m8 = sb("m8", [128, 8, 8])
            self.lt = sb("lt", [128, 64])
            self.bpad = sb("bpad", [128, 8, 72], BF16)
            self.rden = sb("rden", [128, 4])
            self.NPT = 28
            for i in range(4):
                S.op('pool', [], [('qa', i)], lambda e, i=i: e.memset(self.qa[i][64:72, :], 0.0))
                S.dma('pool', [], [('ka', i)], lambda e, i=i: e.dma_start(out=self.ka[i][64:72, :], in_=self.e8_in))
                S.op('pool', [], [('va', i)], lambda e, i=i: e.memset(self.va[i][:, :, 64:65], 1.0))
            S.op('pool', [], ['bpad'], lambda e: e.memset(self.bpad[:], 0.0))
            if STOP_AFTER == 'att_init':
                self.dbg_qk(0)
                raise _Stop()
        PT = [self.B3[:, i // 4, (i % 4) * 512:(i % 4 + 1) * 512] for i in range(self.NPT)]
        attT = self.B1
        negp = self.cst[:, C_NEGP:C_NEGP + 64]
        floorb = self.cst[:, C_FLOOR:C_FLOOR + 64]
        pt_rr = [0]
        PDb = self.PD[:].bitcast(BF16)
        xk = self.xT_keys()

        def load_w(hp):
            wb = self.wqkv_b[hp % 2]
            S.dma('pool', [], [('wqkv', hp % 2, 0)], lambda e, wb=wb, hp=hp: e.dma_start(
                out=wb[:], in_=self.wqkv[l, hp].rearrange("p (k n) -> p k n", k=8), max_dma_last_dim=4096))
        load_w(0)
        Wl = [[] for _ in range(9)]
        Pl = [[] for _ in range(9)]
        Cl = [[] for _ in range(9)]
        for hp in range(8):
            wb = self.wqkv_b[hp % 2]
            wk = ('wqkv', hp % 2, 0)
            wk1 = wk2 = wk3 = wk
            self.rec = Wl[hp + 1]
            if STOP_AFTER == 'att_w0':
                self.dbg_tile('wb0', wb[:].rearrange("p a b -> p (a b)"), [wk, wk1, wk2, wk3], 3072)
                raise _Stop()
            if hp + 1 < 8:
                load_w(hp + 1)
            if STOP_AFTER == 'att_w1':
                self.dbg_tile('wb0', wb[:].rearrange("p a b -> p (a b)"), [wk, wk1, wk2, wk3], 3072)
                raise _Stop()
            par = (hp % 2) * 2
            hA, hB = par, par + 1
            self.rec = Pl[hp]
            cnt = 0
            for which in (1, 0):
                dst = self.ka if which == 1 else self.qa
                dn = 'ka' if which == 1 else 'qa'
                for tc in range(4):
                    bank = cnt % 4
                    cnt += 1
                    ps = self.PA[:, bank * 512:(bank + 1) * 512]
                    pk = ('PA', bank)
                    cols = slice(tc * 512, (tc + 1) * 512)
                    def stopat(ch, ap=None, keys=None):
                        if STOP_AFTER == 'att_qk1' + ch:
                            if ap is not None:
                                self.dbg_tile('probe', ap, keys, 512)
                            raise _Stop()
                    self.mm([wk, wk1, wk2, wk3] + xk[tc * 4:(tc + 1) * 4], [pk], ps,
                            [(wb[:, kc, which * 128:(which + 1) * 128], self.xT[:, kc, cols]) for kc in range(8)])
                    stopat('a', ps, [pk])
                    qb = self.qbf[cnt % 2]
                    qk_ = ('qbf', cnt % 2)
                    if VARIANT != 6:
                        S.op('act', [pk], [qk_], lambda e, qb=qb, ps=ps: e.activation(qb[:], ps, AF.Copy))
                    stopat('b', qb[:], [qk_])
                    t1 = self.t1[cnt % 2]
                    t2 = self.t2[cnt % 2]
                    k1 = ('t1', cnt % 2)
                    k2 = ('t2', cnt % 2)
                    if VARIANT == 4:
                        t1 = self.sb("t1x", [128, 512])
                    if VARIANT in (5, 8):
                        S.op('dve', [('rope', 1)], [k1], lambda e, t1=t1, ps=ps, cols=cols: e.tensor_copy(t1[:], self.ropeC[:, cols]))
                    elif VARIANT in (1, 4):
                        S.op('dve', [pk, ('rope', 1)], [k1], lambda e, t1=t1, ps=ps, cols=cols: e.tensor_copy(t1[:], ps))
                    elif VARIANT == 2:
                        S.op('dve', [pk, ('rope', 1)], [k1], lambda e, t1=t1, ps=ps, cols=cols: e.tensor_copy(t1[:], self.ropeC[:, cols]))
                    elif VARIANT == 3:
                        S.op('dve', [pk, ('rope', 1)], [k1], lambda e, t1=t1, ps=ps, cols=cols: e.tensor_tensor(t1[:], ps, self.cst[:, 0:512], ALU.mult))
                    else:
                        S.op('dve', [pk, ('rope', 1)], [k1], lambda e, t1=t1, ps=ps, cols=cols: e.tensor_tensor(t1[:], ps, self.ropeC[:, cols], ALU.mult))
                    stopat('c', qb[:] if VARIANT == 7 else t1[:], [qk_, k1] if VARIANT in (7, 8) else [k1])
                    rb = (bank + 2) % 4
                    pr = self.PA[:, rb * 512:(rb + 1) * 512]
                    prk = ('PA', rb)
                    self.mm([qk_, 'cbf'], [prk], pr, [(self.rm_b, qb[:])])
                    stopat('d', pr, [prk])
                    S.op('dve', [prk, ('rope', 0)], [k2], lambda e, t2=t2, pr=pr, cols=cols: e.tensor_tensor(t2[:], pr, self.ropeS[:, cols], ALU.mult))
                    stopat('e', t2[:], [k2])
                    S.op('dve', [k1, k2], [(dn, hA)], lambda e, t1=t1, t2=t2, dst=dst, cols=cols, hA=hA: e.tensor_tensor(dst[hA][0:64, cols], t1[0:64, :], t2[0:64, :], ALU.add))
                    stopat('f', dst[hA][0:64, cols], [(dn, hA)])
                    S.op('dve', [k1, k2], [(dn, hB)], lambda e, t1=t1, t2=t2, dst=dst, cols=cols, hB=hB: e.tensor_tensor(dst[hB][0:64, cols], t1[64:128, :], t2[64:128, :], ALU.add))
                    stopat('g', dst[hB][0:64, cols], [(dn, hB)])
            if STOP_AFTER == 'att_qk':
                raise _Stop()
            for tq in range(4):
                bank = cnt % 4
                cnt += 1
                ps = self.PA[:, bank * 512:(bank + 1) * 512]
                pk = ('PA', bank)
                self.mmseq([wk, wk1, wk2, wk3] + xk[tq * 4:(tq + 1) * 4], [pk],
                           [(ps[:, j * 128:(j + 1) * 128],
                             [(self.xT[:, kc, (tq * 4 + j) * 128:(tq * 4 + j + 1) * 128], wb[:, kc, 256:384]) for kc in range(8)])
                            for j in range(4)])
                psv = ps.rearrange("p (t c) -> p t c", t=4)
                if hp == 0 and tq == 0:
                    self.dbg_tile('psv', ps, [pk], 512)
                    self.dbg_tile('wb0', wb[:].rearrange("p a b -> p (a b)"), [wk, wk1, wk2, wk3], 3072)
                S.op('act', [pk], [('va', hA)], lambda e, psv=psv, tq=tq, hA=hA: e.activation(self.va[hA][:, tq * 4:(tq + 1) * 4, 0:64], psv[:, :, 0:64], AF.Copy))
                S.op('act', [pk], [('va', hB)], lambda e, psv=psv, tq=tq, hB=hB: e.activation(self.va[hB][:, tq * 4:(tq + 1) * 4, 0:64], psv[:, :, 64:128], AF.Copy))
            if STOP_AFTER == 'att_proj':
                self.dbg_tile('va0', self.va[0][:].rearrange("p a b -> p (a b)"), [('va', 0)], 1040)
                self.dbg_qk(hA)
                raise _Stop()
            self.rec = Cl[hp]
            atm = self.atm[hp % 2]
            ak = ('atm', hp % 2)
            for hh, hbuf in ((0, hA), (1, hB)):
                qa, ka, va = self.qa[hbuf], self.ka[hbuf], self.va[hbuf]
                qk, kk, vk = ('qa', hbuf), ('ka', hbuf), ('va', hbuf)
                S.op('dve', [kk], ['km'], lambda e, ka=ka: e.tensor_reduce(self.km[:], ka[0:64, :].rearrange("p (b t) -> p b t", b=8), AX.X, ALU.add))
                S.op('dve', ['km'], ['kmh'], lambda e: e.tensor_scalar(self.kmh[:], self.km[:], 1.0 / 256, None, ALU.mult))
                S.op('dve', ['km', 'kmh'], ['kml'], lambda e: e.scalar_tensor_tensor(self.kml[:], in0=self.km[:], scalar=1.0 / 256, in1=self.kmh[:], op0=ALU.mult, op1=ALU.subtract))
                pg = self.PC[:, 448:512]
                self.mmseq([qk, 'kmh', 'kml'], ['PCg'],
                           [(pg[:, t * 8:(t + 1) * 8], [(qa[0:64, (8 + t) * 128:(9 + t) * 128], self.kmh[:]), (qa[0:64, (8 + t) * 128:(9 + t) * 128], self.kml[:])]) for t in range(8)])
                S.op('dve', ['PCg', 'cst'], ['gm'], lambda e, pg=pg: e.tensor_tensor(self.gm[:], pg, negp, ALU.add))

                S.op('dve', ['gm'], ['m8'], lambda e: sel_max(e, self))
                S.op('dve', ['gm', 'm8'], ['lt'], lambda e: sel_lt(e, self))
                S.op('dve', ['lt', 'cst'], ['bpad'], lambda e: e.tensor_tensor(self.bpad[:, :, 64:72], self.lt[:].rearrange("p (t j) -> p t j", t=8), floorb.rearrange("p (t j) -> p t j", t=8), ALU.max))
                self.tps(['bpad', 'cbf'], ['PD'], [(PDb[0:72, t * 128:(t + 1) * 128], self.bpad[:, t, :]) for t in range(8)], self.ident_b)
                S.op('act', ['PD'], [qk], lambda e, qa=qa: e.activation(qa[64:72, 1024:2048], PDb[64:72, :], AF.Copy))
                if STOP_AFTER == 'att_gate':
                    self.dbg_qk(hA)
                    raise _Stop()
                slots = {}

                def stageA(qc):
                    sl = []
                    for kt in range(4 * qc + 4):
                        c0 = max(0, kt * 128 - qc * 512)
                        bank = kt % 2
                        ps = self.PB[:, bank * 512:(bank + 1) * 512]
                        pk = ('PB', bank)
                        self.mm([kk, qk], [pk], ps[:, c0:512], [(ka[0:72, kt * 128:(kt + 1) * 128], qa[0:72, qc * 512 + c0:(qc + 1) * 512])])
                        si = pt_rr[0]
                        pt_rr[0] = (pt_rr[0] + 1) % self.NPT
                        sl.append(si)
                        pt = PT[si]
                        S.op('act', [pk], [('PT', si)], lambda e, pt=pt, ps=ps, c0=c0: e.activation(pt[:, c0:512], ps[:, c0:512], AF.Exp, scale=0.125))
                        if kt >= 4 * qc:
                            S.op('dve', [('PT', si), 'cbf'], [('PT', si)], lambda e, pt=pt, c0=c0: e.tensor_tensor(pt[:, c0:c0 + 128], pt[:, c0:c0 + 128], self.tri_b, ALU.mult))
                    slots[qc] = sl

                def stageB(qc):
                    sl = slots[qc]
                    groups = []
                    for qi in range(4):
                        qt = 4 * qc + qi
                        groups.append((self.PC[:, qi * 65:(qi + 1) * 65],
                                       [(PT[sl[kt]][:, qi * 128:(qi + 1) * 128], va[:, kt, :]) for kt in range(qt + 1)]))
                    self.mmseq([('PT', s_) for s_ in sl] + [vk], ['PCo'], groups)
                    pv = self.PC[:, 0:260].rearrange("p (q c) -> p q c", q=4)
                    S.op('dve', ['PCo'], ['rden'], lambda e, pv=pv: e.reciprocal(self.rden[:], pv[:, :, 64]))

                    def norm(e, atm=atm, hh=hh, qc=qc):
                        ins = None
                        for qi in range(4):
                            ins = e.tensor_scalar(atm[:, 4 * qc + qi, hh * 64:(hh + 1) * 64], self.PC[:, qi * 65:qi * 65 + 64], self.rden[:, qi:qi + 1], None, ALU.mult)
                        return ins
                    S.op('dve', ['PCo', 'rden'], [ak], norm)

                stageA(0)
                for qc in range(4):
                    if qc + 1 < 4:
                        stageA(qc + 1)
                    stageB(qc)
            if hp == 0:
                self.dbg_tile('va0', self.va[0][:].rearrange("p a b -> p (a b)"), [('va', 0)], 1040)
                self.dbg_tile('atm0', atm[:].rearrange("p a b -> p (a b)"), [ak], 2048)
            for half in range(2):
                self.tps([ak, 'cbf'], ['PD'], [(PDb[:, t * 128:(t + 1) * 128], atm[:, half * 8 + t, :]) for t in range(8)], self.ident_b)
                S.op('act', ['PD'], [('attT', hp, half)], lambda e, half=half, hp=hp: e.activation(attT[:, hp, half * 1024:(half + 1) * 1024], PDb, AF.Copy))
            self.rec = None
        for t in Pl[0]:
            t()
        for hp in range(8):
            for t in Wl[hp + 1]:
                t()
            A, Bn = Cl[hp], Pl[hp + 1]
            n1, n2 = len(A), len(Bn)
            i1 = i2 = 0
            while i1 < n1 or i2 < n2:
                if i2 >= n2 or (i1 < n1 and i1 * n2 <= i2 * n1):
                    A[i1]()
                    i1 += 1
                else:
                    Bn[i2]()
                    i2 += 1
        if 'attT' in self.dbg:
            self.dbgf = self.sb("dbgf", [128, 512])
            for hp in range(8):
                for q4 in range(4):
                    S.op('dve', [('attT', hp, 0), ('attT', hp, 1), ('dbg', 'attT')], ['dbgf'], lambda e, hp=hp, q4=q4: e.tensor_copy(self.dbgf[:, 0:512], attT[:, hp, q4 * 512:(q4 + 1) * 512]))
                    S.dma('sp', ['dbgf'], [('dbg', 'attT')], lambda e, hp=hp, q4=q4: e.dma_start(out=self.dbg['attT'][hp * 128:(hp + 1) * 128, q4 * 512:(q4 + 1) * 512], in_=self.dbgf[:, 0:512]))
            self.final_keys.append(('dbg', 'attT'))

    def ssd(self, l):
        self.psum_epoch()
        S, sb, nc = self.S, self.sb, self.nc
        self.phase([self.R_W, self.R_B3])
        if True:
            self.wssm_b = [sb("wssm0", [128, 8, 768], BF16)] * 2
            self.wdt_b = sb("wdtb", [128, 8, 32], BF16)
            self.smallb = sb("smallb", [128, 96])
            self.cwb = sb("cwb", [128, 8, 4, 5])
            self.dt_all = sb("dt_all", [128, 16, 32])
            self.adt_all = sb("adt_all", [128, 16, 32])
            self.acs_all = sb("acs_all", [128, 16, 32])
            self.dst_all = sb("dst_all", [128, 16, 32])
            self.ea_all = sb("ea_all", [128, 16, 32])
            self.lastb = sb("lastb", [128, 8, 32])
            self.cdb = sb("cdb", [128, 8, 32])
            self.a_b = sb("a_b", [128, 32])
            self.uext = [sb("uext%d" % i, [128, 4, 259], BF16) for i in range(2)]
            self.dg = sb("dg", [128, 4, 4, 128], BF16)
            self.xh = sb("xh", [128, 4, 256])
            self.tnh = sb("tnh", [128, 4, 256])
            self.bcb2 = [sb("bcb%d" % i, [128, 2, 256], BF16) for i in range(2)]
            self.bcf = sb("bcf", [128, 256])
            self.xs_tok2 = [sb("xs_tok%d" % i, [128, 2, 256], BF16) for i in range(2)]
            self.xdt2 = [sb("xdt%d" % i, [128, 2, 256], BF16) for i in range(2)]
            self.xw2 = [sb("xw%d" % i, [128, 2, 256], BF16) for i in range(2)]
            self.btok2 = [sb("btok%d" % i, [128, 2, 128], BF16) for i in range(2)]
            self.cb32 = [sb("cb3_%d" % i, [128, 3, 128]) for i in range(2)]
            self.dtmp = [sb("dtmp%d" % i, [128, 3, 128]) for i in range(2)]
            self.mt = [sb("mt%d" % i, [128, 3, 128], BF16) for i in range(2)]
            self.prev_f = sb("prev_f", [128, 256])
            self.prev_b = sb("prev_b", [128, 256], BF16)
            self.tz = sb("tz", [128, 2, 256])
            self.yv = sb("yv", [128, 2, 256])
            self.ss = sb("ss", [128, 2])
            self.rstd = sb("rstd", [128, 2])
            self.nwb = [sb("nwb%d" % i, [128, 256]) for i in range(2)]
            self.xsf = self.xh[:, 0:2, :]
            self.zs = self.tz
            self.h2 = self.yv
            self.yn = self.yv
            self.junk = sb("junk", [128, 256], BF16)
        xk = self.xT_keys()
        S.dma('sp', [], ['smallb'], lambda e: e.dma_start(out=self.smallb[:], in_=self.small[l].broadcast_to([128, 96])))
        S.dma('sp', [], ['cwb'], lambda e: e.dma_start(out=self.cwb[:], in_=self.cw[l].rearrange("p (g c k) -> p g c k", g=8, c=4)))
        S.dma('pool', [], ['wdtb'], lambda e: e.dma_start(out=self.wdt_b[:], in_=self.wdt[l].rearrange("p (k n) -> p k n", k=8)))
        S.op('pool', ['cwb'], ['cwb'], lambda e: e.tensor_scalar(self.cwb[:], self.cwb[:], 0.5, None, ALU.mult))
        dtb = self.smallb[:, 0:32]
        alog = self.smallb[:, 32:64]
        dsk = self.smallb[:, 64:96]
        S.op('act', ['smallb'], ['a_b'], lambda e: e.activation(self.a_b[:], alog, AF.Exp))
        S.op('dve', ['a_b'], ['a_b'], lambda e: e.tensor_scalar(self.a_b[:], self.a_b[:], -1.0, None, ALU.mult))
        psd = self.PA[:, 0:512]
        self.mmseq(['wdtb'] + xk, [('PA', 0)],
                   [(psd[:, tt * 32:(tt + 1) * 32], [(self.xT[:, kc, tt * 128:(tt + 1) * 128], self.wdt_b[:, kc, :]) for kc in range(8)]) for tt in range(16)])
        dtf = self.dt_all[:].rearrange("p t h -> p (t h)")
        psd3 = psd.rearrange("p (t h) -> p t h", t=16)
        S.op('dve', [('PA', 0), 'smallb'], ['dt_all'], lambda e: e.tensor_tensor(self.dt_all[:], psd3, dtb.unsqueeze(1).broadcast_to([128, 16, 32]), ALU.add))
        S.op('act', ['dt_all'], ['dt_all'], lambda e: e.activation(dtf, dtf, AF.Exp))
        S.op('act', ['dt_all'], ['dt_all'], lambda e: e.activation(dtf, dtf, AF.Ln, bias=1.0))
        S.op('dve', ['dt_all', 'a_b'], ['adt_all'], lambda e: e.tensor_tensor(self.adt_all[:], self.dt_all[:], self.a_b[:].unsqueeze(1).broadcast_to([128, 16, 32]), ALU.mult))
        psa = self.PA[:, 512:1024]
        psl = self.PA[:, 1024:1280]
        groups = []
        for c in range(8):
            a0 = self.adt_all[:, 2 * c, :]
            a1 = self.adt_all[:, 2 * c + 1, :]
            groups.append((psa[:, (2 * c) * 32:(2 * c + 1) * 32], [(self.tri_f, a0)]))
            groups.append((psa[:, (2 * c + 1) * 32:(2 * c + 2) * 32], [(self.ones_f, a0), (self.tri_f, a1)]))
        self.mmseq(['adt_all', 'cst'], [('PA', 1)], groups)
        self.mmseq(['adt_all', 'cst'], [('PA', 2)],
                   [(psl[:, c * 32:(c + 1) * 32], [(self.ones_f, self.adt_all[:, 2 * c, :]), (self.ones_f, self.adt_all[:, 2 * c + 1, :])]) for c in range(8)])
        acsf = self.acs_all[:].rearrange("p t h -> p (t h)")
        S.op('dve', [('PA', 1)], ['acs_all'], lambda e: e.tensor_copy(acsf, psa))
        S.op('dve', [('PA', 2)], ['lastb'], lambda e: e.tensor_copy(self.lastb[:].rearrange("p c h -> p (c h)"), psl))
        S.op('dve', ['lastb', 'acs_all'], ['dst_all'], lambda e: e.tensor_tensor(
            self.dst_all[:].rearrange("p (c s) h -> p c s h", c=8), self.lastb[:].unsqueeze(2).broadcast_to([128, 8, 2, 32]),
            self.acs_all[:].rearrange("p (c s) h -> p c s h", c=8), ALU.subtract))
        dstf = self.dst_all[:].rearrange("p t h -> p (t h)")
        S.op('act', ['dst_all'], ['dst_all'], lambda e: e.activation(dstf, dstf, AF.Exp))
        S.op('dve', ['dst_all', 'dt_all'], ['dst_all'], lambda e: e.tensor_tensor(self.dst_all[:], self.dst_all[:], self.dt_all[:], ALU.mult))
        S.op('act', ['acs_all'], ['ea_all'], lambda e: e.activation(self.ea_all[:].rearrange("p t h -> p (t h)"), acsf, AF.Exp))
        S.op('act', ['lastb'], ['cdb'], lambda e: e.activation(self.cdb[:].rearrange("p c h -> p (c h)"), self.lastb[:].rearrange("p c h -> p (c h)"), AF.Exp))
        self.dump('dt_all', self.dt_all[:].rearrange("p t h -> p (t h)"), 'dt_all')
        self.dump('acs_all', acsf, 'acs_all')

        P0 = self.PA[:, 0:512]
        P1 = self.PA[:, 512:1024]
        P2 = self.PA[:, 1024:1536]
        P3 = self.PA[:, 1536:2048]
        P4 = self.PB[:, 0:512]
        P5 = self.PB[:, 512:1024]
        P6 = self.PC[:, 0:512]
        P7 = self.PD[:, 0:512]
        uprev = None
        def load_xbc(g):
            wb = self.wssm_b[0]
            S.dma('pool', [], [('wssm', 'xbc')], lambda e, wb=wb, g=g: e.dma_start(
                out=wb[:, :, 256:768], in_=self.wssm[l, g].rearrange("p (k n) -> p k n", k=8)[:, :, 256:768], max_dma_last_dim=2048))
            nw = self.nwb[g % 2]
            S.dma('sp', [], [('nwb', g % 2)], lambda e, nw=nw, g=g: e.dma_start(out=nw[:], in_=self.normw[l][:, g * 256:(g + 1) * 256].broadcast_to([128, 256])))

        def load_z(g):
            wb = self.wssm_b[0]
            S.dma('pool', [], [('wssm', 'z')], lambda e, wb=wb, g=g: e.dma_start(
                out=wb[:, :, 0:256], in_=self.wssm[l, g].rearrange("p (k n) -> p k n", k=8)[:, :, 0:256], max_dma_last_dim=1024))
        load_xbc(0)
        load_z(0)
        prevB = None
        for g in range(8):
            wb = self.wssm_b[g % 2]
            wk = ('wssm', 0, 0)
            wkall = [('wssm', 'xbc')]
            nw = self.nwb[g % 2]
            nk = ('nwb', g % 2)
            if g > 0:
                load_xbc(g)
            hs = slice(4 * g, 4 * g + 4)

            def build_dg(e, g=g):
                ins = None
                for ci in range(4):
                    for k in range(4):
                        ins = e.tensor_scalar(self.dg[:, ci, k, :], self.ident_b, self.cwb[:, g, ci, k:k + 1], None, ALU.mult)
                return ins
            S.op('dve', ['cwb', 'cbf'], ['dg'], build_dg)
            for c in range(8):
                cols = slice(c * 256, (c + 1) * 256)
                sl = c % 2
                bcb, xs_tok, xw, btok = self.bcb2[sl], self.xs_tok2[sl], self.xw2[sl], self.btok2[sl]
                kbcb, kxs, kxw, kbt = ('bcb', sl), ('xs_tok', sl), ('xw', sl), ('btok', sl)
                xdt, cb3 = self.xdt2[sl], self.cb32[sl]
                kxdt, kcb3 = ('xdt', sl), ('cb3', sl)
                F1, F2, Bq = [], [], []
                self.rec = F1
                xkc = xk[2 * c:2 * c + 2]
                ue = self.uext[c % 2]
                uk = ('uext', c % 2)
                self.mmseq(wkall + xkc, [('PA', 0)],
                           [(P0[:, j * 256:(j + 1) * 256], [(wb[:, kc, 256 + j * 128:256 + (j + 1) * 128], self.xT[:, kc, cols]) for kc in range(8)]) for j in range(2)])
                self.mmseq(wkall + xkc, [('PA', 1)],
                           [(P1[:, j * 256:(j + 1) * 256], [(wb[:, kc, 512 + j * 128:512 + (j + 1) * 128], self.xT[:, kc, cols]) for kc in range(8)]) for j in range(2)])
                if c == 0:
                    S.op('pool', [], [uk], lambda e, bcb=bcb, xs_tok=xs_tok, xw=xw, btok=btok, xdt=xdt, cb3=cb3, ue=ue: e.memset(ue[:, :, 0:3], 0.0))
                else:
                    up = self.uext[(c - 1) % 2]
                    S.op('pool', [('uext', (c - 1) % 2)], [uk], lambda e, bcb=bcb, xs_tok=xs_tok, xw=xw, btok=btok, xdt=xdt, cb3=cb3, ue=ue, up=up: e.tensor_copy(ue[:, :, 0:3], up[:, :, 256:259]))
                S.op('act', [('PA', 0)], [uk], lambda e, bcb=bcb, xs_tok=xs_tok, xw=xw, btok=btok, xdt=xdt, cb3=cb3, ue=ue: e.activation(ue[:, 0:2, 3:259], P0.rearrange("p (j t) -> p j t", j=2), AF.Copy))
                S.op('act', [('PA', 1)], [uk], lambda e, bcb=bcb, xs_tok=xs_tok, xw=xw, btok=btok, xdt=xdt, cb3=cb3, ue=ue: e.activation(ue[:, 2:4, 3:259], P1.rearrange("p (j t) -> p j t", j=2), AF.Copy))
                self.mmseq([uk, 'dg'], [('PA', 0)],
                           [(P0[:, j * 256:(j + 1) * 256], [(self.dg[:, j, k, :], ue[:, j, k:k + 256]) for k in range(4)]) for j in range(2)])
                self.mmseq([uk, 'dg'], [('PA', 1)],
                           [(P1[:, j * 256:(j + 1) * 256], [(self.dg[:, 2 + j, k, :], ue[:, 2 + j, k:k + 256]) for k in range(4)]) for j in range(2)])
                for ci in range(4):
                    w = self.cwb[:, g, ci, :]
                    pc = (P0 if ci < 2 else P1)[:, (ci % 2) * 256:(ci % 2 + 1) * 256]
                    pk = ('PA', 0 if ci < 2 else 1)
                    S.op('act', [pk, 'cwb'], [('xh', ci)], lambda e, ci=ci, w=w, pc=pc: e.activation(self.xh[:, ci, :], pc, AF.Identity, bias=w[:, 4:5]))
                    S.op('act', [pk, 'cwb'], [('tnh', ci)], lambda e, ci=ci, w=w, pc=pc: e.activation(self.tnh[:, ci, :], pc, AF.Tanh, bias=w[:, 4:5]))
                S.op('dve', [('tnh', 0), ('tnh', 1), ('xh', 0), ('xh', 1)], [('xh', 0), ('xh', 1)], lambda e, bcb=bcb, xs_tok=xs_tok, xw=xw, btok=btok, xdt=xdt, cb3=cb3: e.scalar_tensor_tensor(self.xsf[:], in0=self.tnh[:, 0:2, :], scalar=1.0, in1=self.xh[:, 0:2, :], op0=ALU.add, op1=ALU.mult))
                S.op('dve', [('tnh', 2), ('tnh', 3), ('xh', 2), ('xh', 3)], [kbcb], lambda e, bcb=bcb, xs_tok=xs_tok, xw=xw, btok=btok, xdt=xdt, cb3=cb3: e.scalar_tensor_tensor(bcb[:], in0=self.tnh[:, 2:4, :], scalar=1.0, in1=self.xh[:, 2:4, :], op0=ALU.add, op1=ALU.mult))
                S.op('dve', [('tnh', 2), ('xh', 2)], ['bcf'], lambda e, bcb=bcb, xs_tok=xs_tok, xw=xw, btok=btok, xdt=xdt, cb3=cb3: e.scalar_tensor_tensor(self.bcf[:], in0=self.tnh[:, 2, :], scalar=1.0, in1=self.xh[:, 2, :], op0=ALU.add, op1=ALU.mult))
                self.tps([('xh', 0), ('xh', 1), 'cst'], [('PA', 0)],
                         [(P0[:, si * 256 + j * 128:si * 256 + (j + 1) * 128], self.xsf[:, j, si * 128:(si + 1) * 128]) for si in range(2) for j in range(2)], self.ident_f)
                self.tps(['bcf', 'cst'], [('PA', 1)],
                         [(P1[:, si * 128:(si + 1) * 128], self.bcf[:, si * 128:(si + 1) * 128]) for si in range(2)], self.ident_f)
                P0v = P0.rearrange("p (s h d) -> p s h d", s=2, h=4)
                S.op('act', [('PA', 0)], [kxs], lambda e, bcb=bcb, xs_tok=xs_tok, xw=xw, btok=btok, xdt=xdt, cb3=cb3: e.activation(xs_tok[:].rearrange("p s c -> p (s c)"), P0, AF.Copy))
                S.op('dve', [('PA', 0), 'dt_all'], [kxdt], lambda e, bcb=bcb, xs_tok=xs_tok, xw=xw, btok=btok, xdt=xdt, cb3=cb3, c=c, hs=hs: e.tensor_tensor(
                    xdt[:].rearrange("p s (h d) -> p s h d", h=4), P0v,
                    self.dt_all[:, 2 * c:2 * c + 2, hs].unsqueeze(3).broadcast_to([128, 2, 4, 64]), ALU.mult))
                S.op('dve', [('PA', 0), 'dst_all'], [kxw], lambda e, bcb=bcb, xs_tok=xs_tok, xw=xw, btok=btok, xdt=xdt, cb3=cb3, c=c, hs=hs: e.tensor_tensor(
                    xw[:].rearrange("p s (h d) -> p s h d", h=4), P0v,
                    self.dst_all[:, 2 * c:2 * c + 2, hs].unsqueeze(3).broadcast_to([128, 2, 4, 64]), ALU.mult))
                S.op('act', [('PA', 1)], [kbt], lambda e, bcb=bcb, xs_tok=xs_tok, xw=xw, btok=btok, xdt=xdt, cb3=cb3: e.activation(btok[:].rearrange("p s n -> p (s n)"), P1[:, 0:256], AF.Copy))
                self.mmseq([kbcb], [('PA', 3)],
                           [(P3[:, 0:256], [(bcb[:, 0, 0:128], bcb[:, 1, :])]),
                            (P3[:, 384:512], [(bcb[:, 0, 128:256], bcb[:, 1, 128:256])])])
                S.op('dve', [('PA', 3), 'cst'], [kcb3], lambda e, bcb=bcb, xs_tok=xs_tok, xw=xw, btok=btok, xdt=xdt, cb3=cb3: e.tensor_tensor(cb3[:, 0, :], P3[:, 0:128], self.tri_f, ALU.mult))
                S.op('act', [('PA', 3)], [kcb3], lambda e, bcb=bcb, xs_tok=xs_tok, xw=xw, btok=btok, xdt=xdt, cb3=cb3: e.activation(cb3[:, 1, :], P3[:, 128:256], AF.Copy))
                S.op('dve', [('PA', 3), 'cst', kcb3], [kcb3], lambda e, bcb=bcb, xs_tok=xs_tok, xw=xw, btok=btok, xdt=xdt, cb3=cb3: e.tensor_tensor(cb3[:, 2, :], P3[:, 384:512], self.tri_f, ALU.mult))
                self.rec = F2
                def bc_mm(hh):
                    hg = 4 * g + hh
                    pb = (P4 if hh % 2 == 0 else P6)[:, 0:256]
                    pbk = ('PB', 0, 0) if hh % 2 == 0 else 'PC'
                    a0 = self.adt_all[:, 2 * c, hg:hg + 1].broadcast_to([128, 128])
                    a1 = self.adt_all[:, 2 * c + 1, hg:hg + 1].broadcast_to([128, 128])
                    self.mmseq(['adt_all', 'cst'], [pbk],
                               [(pb[:, 0:128], [(a0, self.tri_f)]),
                                (pb[:, 128:256], [(a0, self.ones_f), (a1, self.tri_f)])])
                    return pb, pbk
                nxt = bc_mm(0)
                for hh in range(4):
                    hg = 4 * g + hh
                    pb, pbk = nxt
                    if hh + 1 < 4:
                        nxt = bc_mm(hh + 1)
                    dtm = self.dtmp[hh % 2]
                    dk = ('dtmp', hh % 2)
                    mt = self.mt[hh % 2]
                    mk = ('mt', hh % 2)
                    ac0 = self.acs_all[:, 2 * c, hg:hg + 1]
                    ac1 = self.acs_all[:, 2 * c + 1, hg:hg + 1]

                    def dec(e, dtm=dtm, pb=pb, ac0=ac0, ac1=ac1):
                        e.tensor_scalar(dtm[:, 0:2, :].rearrange("p a b -> p (a b)"), pb[:, 0:256], ac0, 0.0, ALU.subtract, ALU.min)
                        return e.tensor_scalar(dtm[:, 2, :], pb[:, 128:256], ac1, 0.0, ALU.subtract, ALU.min)
                    S.op('dve', [pbk, 'acs_all'], [dk], dec)
                    S.op('act', [dk], [dk], lambda e, dtm=dtm: e.activation(dtm[:].rearrange("p a b -> p (a b)"), dtm[:].rearrange("p a b -> p (a b)"), AF.Exp))
                    S.op('dve', [dk, kcb3], [mk], lambda e, dtm=dtm, mt=mt, cb3=cb3: e.tensor_tensor(mt[:], dtm[:], cb3[:], ALU.mult))
                    self.mmseq([mk, kxdt], [('PB', 1, hh)],
                               [(P5[:, hh * 64:(hh + 1) * 64], [(mt[:, 0, :], xdt[:, 0, hh * 64:(hh + 1) * 64])]),
                                (P5[:, 256 + hh * 64:256 + (hh + 1) * 64], [(mt[:, 1, :], xdt[:, 0, hh * 64:(hh + 1) * 64]), (mt[:, 2, :], xdt[:, 1, hh * 64:(hh + 1) * 64])])])
                self.rec = Bq
                self.mmseq([('wssm', 'z')] + xkc, [('PA', 2)],
                           [(P2[:, si * 256:(si + 1) * 256], [(self.xT[:, kc, c * 256 + si * 128:c * 256 + (si + 1) * 128], wb[:, kc, 0:256]) for kc in range(8)]) for si in range(2)])
                S.op('act', [('PA', 2)], ['tz'], lambda e, bcb=bcb, xs_tok=xs_tok, xw=xw, btok=btok, xdt=xdt, cb3=cb3: e.activation(self.tz[:].rearrange("p s c -> p (s c)"), P2, AF.Tanh, scale=0.5))
                S.op('dve', ['tz', ('PA', 2)], ['tz'], lambda e, bcb=bcb, xs_tok=xs_tok, xw=xw, btok=btok, xdt=xdt, cb3=cb3: e.scalar_tensor_tensor(self.zs[:].rearrange("p s c -> p (s c)"), in0=self.tz[:].rearrange("p s c -> p (s c)"), scalar=1.0, in1=P2, op0=ALU.add, op1=ALU.mult))
                p5k = [('PB', 1, hh) for hh in range(4)]
                if c > 0:
                    self.mmseq([kbcb, 'prev_b'], [('PA', 2)],
                               [(P2[:, si * 256:(si + 1) * 256], [(bcb[:, 1, si * 128:(si + 1) * 128], self.prev_b[:])]) for si in range(2)])
                if c < 7:
                    self.mm([kbt, kxw], [('PB', 0, 0)], P4[:, 0:256], [(btok[:, si, :], xw[:, si, :]) for si in range(2)])
                    if c == 0:
                        S.op('act', [('PB', 0, 0)], ['prev_f'], lambda e, bcb=bcb, xs_tok=xs_tok, xw=xw, btok=btok, xdt=xdt, cb3=cb3: e.activation(self.prev_f[:], P4[:, 0:256], AF.Copy))
                    else:
                        S.op('dve', ['prev_f', 'cdb'], ['prev_f'], lambda e, bcb=bcb, xs_tok=xs_tok, xw=xw, btok=btok, xdt=xdt, cb3=cb3, c=c, hs=hs: e.tensor_tensor(
                            self.prev_f[:].rearrange("p (h d) -> p h d", h=4), self.prev_f[:].rearrange("p (h d) -> p h d", h=4),
                            self.cdb[:, c, hs].unsqueeze(2).broadcast_to([128, 4, 64]), ALU.mult))
                        S.op('dve', ['prev_f', ('PB', 0, 0)], ['prev_f'], lambda e, bcb=bcb, xs_tok=xs_tok, xw=xw, btok=btok, xdt=xdt, cb3=cb3: e.tensor_tensor(self.prev_f[:], self.prev_f[:], P4[:, 0:256], ALU.add))
                    S.op('act', ['prev_f'], ['prev_b'], lambda e, bcb=bcb, xs_tok=xs_tok, xw=xw, btok=btok, xdt=xdt, cb3=cb3: e.activation(self.prev_b[:], self.prev_f[:], AF.Copy))
                S.op('pool', [kxs, 'smallb'], ['yv'], lambda e, bcb=bcb, xs_tok=xs_tok, xw=xw, btok=btok, xdt=xdt, cb3=cb3, hs=hs: e.tensor_tensor(
                    self.yv[:].rearrange("p s (h d) -> p s h d", h=4), xs_tok[:].rearrange("p s (h d) -> p s h d", h=4),
                    dsk[:, hs].unsqueeze(1).unsqueeze(3).broadcast_to([128, 2, 4, 64]), ALU.mult))
                S.op('dve', p5k + ['yv'], ['yv'], lambda e, bcb=bcb, xs_tok=xs_tok, xw=xw, btok=btok, xdt=xdt, cb3=cb3: e.tensor_tensor(self.yv[:].rearrange("p s c -> p (s c)"), P5, self.yv[:].rearrange("p s c -> p (s c)"), ALU.add))
                if c > 0:
                    def yoff(e, c=c, g=g):
                        ins = None
                        for si in range(2):
                            for hh in range(4):
                                o = self.yv[:, si, hh * 64:(hh + 1) * 64]
                                ins = e.scalar_tensor_tensor(o, in0=P2[:, si * 256 + hh * 64:si * 256 + (hh + 1) * 64],
                                                             scalar=self.ea_all[:, 2 * c + si, 4 * g + hh:4 * g + hh + 1], in1=o, op0=ALU.mult, op1=ALU.add)
                        return ins
                    S.op('dve', [('PA', 2), 'ea_all', 'yv'], ['yv'], yoff)
                S.op('dve', ['yv', 'tz'], ['yv'], lambda e, bcb=bcb, xs_tok=xs_tok, xw=xw, btok=btok, xdt=xdt, cb3=cb3: e.tensor_tensor(self.h2[:], self.yv[:], self.zs[:], ALU.mult))
                for si in range(2):
                    S.op('act', ['yv'], ['junk', ('ss', si)], lambda e, bcb=bcb, xs_tok=xs_tok, xw=xw, btok=btok, xdt=xdt, cb3=cb3, si=si: e.activation(self.junk[:], self.h2[:, si, :], AF.Square, accum_out=self.ss[:, si:si + 1]))
                S.op('dve', [('ss', 0), ('ss', 1)], ['rstd'], lambda e, bcb=bcb, xs_tok=xs_tok, xw=xw, btok=btok, xdt=xdt, cb3=cb3: e.tensor_scalar(self.rstd[:], self.ss[:], 1.0 / 256, 4 * RMS_EPS, ALU.mult, ALU.add))
                S.op('pool', ['rstd', 'nh'], ['rstd'], lambda e, bcb=bcb, xs_tok=xs_tok, xw=xw, btok=btok, xdt=xdt, cb3=cb3: e.tensor_tensor(self.rstd[:], self.rstd[:], self.nh[:, 0:2], ALU.pow))
                for si in range(2):
                    S.op('dve', ['yv', 'rstd', nk], ['yv'], lambda e, bcb=bcb, xs_tok=xs_tok, xw=xw, btok=btok, xdt=xdt, cb3=cb3, si=si, nw=nw: e.scalar_tensor_tensor(
                        self.yn[:, si, :], in0=self.h2[:, si, :], scalar=self.rstd[:, si:si + 1], in1=nw[:], op0=ALU.mult, op1=ALU.mult))
                self.tps(['yv', 'cst'], ['PD'],
                         [(P7[:, j * 256 + si * 128:j * 256 + (si + 1) * 128], self.yn[:, si, j * 128:(j + 1) * 128]) for j in range(2) for si in range(2)], self.ident_f)
                S.op('act', ['PD'], [('yT', g, c)], lambda e, bcb=bcb, xs_tok=xs_tok, xw=xw, btok=btok, xdt=xdt, cb3=cb3, g=g, cols=cols: e.activation(self.yT[:, 2 * g:2 * g + 2, cols], P7.rearrange("p (j t) -> p j t", j=2), AF.Copy))
                self.rec = None
                X = F2 + Bq
                if prevB is None:
                    for t in F1:
                        t()
                else:
                    n1, n2 = len(F1), len(prevB)
                    i1 = i2 = 0
                    while i1 < n1 or i2 < n2:
                        if i2 >= n2 or (i1 < n1 and i1 * n2 <= i2 * n1):
                            F1[i1]()
                            i1 += 1
                        else:
                            prevB[i2]()
                            i2 += 1
                prevB = X
                if c == 0 and g > 0:
                    load_z(g)
        for t in prevB:
            t()
        if 'yT' in self.dbg:
            if True:
                self.dbgf = self.sb("dbgf", [128, 512])
            for kc in range(16):
                for q4 in range(4):
                    S.op('dve', [('yT', g, c) for g in range(8) for c in range(8)] + [('dbg', 'yT')], ['dbgf'], lambda e, kc=kc, q4=q4: e.tensor_copy(self.dbgf[:, 0:512], self.yT[:, kc, q4 * 512:(q4 + 1) * 512]))
                    S.dma('sp', ['dbgf'], [('dbg', 'yT')], lambda e, kc=kc, q4=q4: e.dma_start(out=self.dbg['yT'][kc * 128:(kc + 1) * 128, q4 * 512:(q4 + 1) * 512], in_=self.dbgf[:, 0:512]))
            self.final_keys.append(('dbg', 'yT'))

    def merge(self, l):
        self.psum_epoch()
        S, sb = self.S, self.sb
        self.phase([self.R_W])
        if True:
            self.wc1_b = [sb("wc1b%d" % i, [128, 5120], BF16) for i in range(2)]
            self.ta = [sb("ta%d" % i, [128, 512]) for i in range(2)]
            self.tb = [sb("tb%d" % i, [128, 512]) for i in range(2)]
        mT = self.B3
        xk = self.xT_keys()
        attk = [('attT', hp, h) for hp in range(8) for h in range(2)]
        yk = [('yT', g, c) for g in range(8) for c in range(8)]
        it = 0
        def load_c(fc):
            wb = self.wc1_b[fc % 2]
            S.dma('pool', [], [('wc1', fc % 2, 0)], lambda e, wb=wb, fc=fc: e.dma_start(
                out=wb[:].rearrange("p (a b) -> p a b", a=5), in_=self.wc1[l, fc].rearrange("p (a b) -> p a b", a=5), max_dma_last_dim=4096))
        load_c(0)
        for fc in range(8):
            wb = self.wc1_b[fc % 2]
            wks = [('wc1', fc % 2, 0)]
            if fc + 1 < 8:
                load_c(fc + 1)
            wg = wb[:, 0:2048].rearrange("p (k n) -> p k n", k=8)
            wa = wb[:, 2048:3072].rearrange("p (k n) -> p k n", k=8)
            ws = wb[:, 3072:5120].rearrange("p (k n) -> p k n", k=16)
            for tc in range(4):
                cols = slice(tc * 512, (tc + 1) * 512)
                if it % 2 == 0:
                    pga, pgs, ppa, pps = (self.PA[:, i * 512:(i + 1) * 512] for i in range(4))
                    bk = [('PA', 0), ('PA', 1), ('PA', 2), ('PA', 3)]
                else:
                    pga, pgs, ppa, pps = self.PB[:, 0:512], self.PB[:, 512:1024], self.PC[:, 0:512], self.PD[:, 0:512]
                    bk = [('PB', 0), ('PB', 1), 'PC', 'PD']
                self.mm(wks + xk[tc * 4:(tc + 1) * 4], [bk[0]], pga, [(wg[:, kc, 0:128], self.xT[:, kc, cols]) for kc in range(8)])
                self.mm(wks + xk[tc * 4:(tc + 1) * 4], [bk[1]], pgs, [(wg[:, kc, 128:256], self.xT[:, kc, cols]) for kc in range(8)])
                self.mm(wks + attk, [bk[2]], ppa, [(wa[:, kc, :], self.B1[:, kc, cols]) for kc in range(8)])
                self.mm(wks + yk, [bk[3]], pps, [(ws[:, kc, :], self.yT[:, kc, cols]) for kc in range(16)])
                ta, tb = self.ta[it % 2], self.tb[it % 2]
                tak, tbk = ('ta', it % 2), ('tb', it % 2)
                it += 1
                S.op('act', [bk[0]], [tak], lambda e, ta=ta, pga=pga: e.activation(ta[:], pga, AF.Tanh, scale=0.5))
                S.op('act', [bk[1]], [tbk], lambda e, tb=tb, pgs=pgs: e.activation(tb[:], pgs, AF.Tanh, scale=0.5))
                S.op('dve', [tak, bk[2]], [tak], lambda e, ta=ta, ppa=ppa: e.scalar_tensor_tensor(ta[:], in0=ta[:], scalar=1.0, in1=ppa, op0=ALU.add, op1=ALU.mult))
                S.op('dve', [tbk, bk[3]], [tbk], lambda e, tb=tb, pps=pps: e.scalar_tensor_tensor(tb[:], in0=tb[:], scalar=1.0, in1=pps, op0=ALU.add, op1=ALU.mult))
                S.op('pool', [tak, tbk], [('mergedT', fc, tc)], lambda e, ta=ta, tb=tb, fc=fc, cols=cols: e.tensor_tensor(mT[:, fc, cols], ta[:], tb[:], ALU.add))

    def layernorm(self, t, tk, g_b, b_b, gk, i):
        S = self.S
        st = self.ln_st[i % 2]
        mv = self.ln_mv[i % 2]
        sk = ('lnst', i % 2)

        def stats(e):
            e.bn_stats(st[:, 0, :], t[:, 0:512])
            return e.bn_stats(st[:, 1, :], t[:, 512:1024])
        S.op('dve', [tk], [sk], stats)
        S.op('dve', [sk], [sk], lambda e: e.bn_aggr(mv[:, 0:2], st[:].rearrange("p a b -> p (a b)")))
        S.op('dve', [sk], [sk], lambda e: e.tensor_scalar(mv[:, 2:3], mv[:, 1:2], LN_EPS, None, ALU.add))
        S.op('pool', [sk, 'nh'], [sk], lambda e: e.tensor_tensor(mv[:, 3:4], mv[:, 2:3], self.nh[:, 0:1], ALU.pow))
        S.op('dve', [tk, sk], [tk], lambda e: e.tensor_scalar(t[:], t[:], mv[:, 0:1], mv[:, 3:4], ALU.subtract, ALU.mult))
        S.op('pool', [tk, gk], [tk], lambda e: e.tensor_tensor(t[:], t[:], g_b, ALU.mult))
        S.op('dve', [tk, gk], [tk], lambda e: e.tensor_tensor(t[:], t[:], b_b, ALU.add))

    def mix_ln1(self, l):
        self.psum_epoch()
        S, sb = self.S, self.sb
        self.phase([self.R_W, self.R_B2])
        if True:
            self.wout_b = sb("woutb", [128, 8, 1024], BF16)
            self.lnpb = sb("lnpb", [128, 4096])
            self.res = [sb("res%d" % i, [128, D]) for i in range(2)]
            self.ln_st = [sb("lnst%d" % i, [128, 2, 6]) for i in range(2)]
            self.ln_mv = [sb("lnmv%d" % i, [128, 4]) for i in range(2)]
        S.dma('pool', [], [('wout', 0)], lambda e: e.dma_start(
            out=self.wout_b[:], in_=self.wout[l].rearrange("p (k n) -> p k n", k=8), max_dma_last_dim=4096))
        lnpb1 = self.lnpb
        S.dma('sp', [], ['lnpb'], lambda e: e.dma_start(out=lnpb1[:], in_=self.lnp[l].broadcast_to([128, 4096])))
        src = self.x_in if l == 0 else self.x1
        srck = 'x_in' if l == 0 else 'x1'
        mk = [('mergedT', fc, tc) for fc in range(8) for tc in range(4)]
        wk = [('wout', 0)]
        for tt in range(16):
            r = self.res[tt % 2]
            rk = ('res', tt % 2)
            S.dma('sp', [(srck, tt)], [rk], lambda e, r=r, tt=tt: e.dma_start(out=r[:], in_=src[tt * 128:(tt + 1) * 128, :]))
            S.op('act', [rk], [rk], lambda e, r=r: e.activation(r[:], r[:], AF.Copy, scale=float(ALPHA)))
            if tt % 2 == 0:
                pm = self.PB[:, 0:1024]
                rkeys = [('PB', 0), ('PB', 1)]
            else:
                pm = self.PA[:, 0:1024]
                rkeys = [('PA', 0), ('PA', 1)]
            self.mmseq(mk + wk, rkeys,
                       [(pm[:, half * 512:(half + 1) * 512], [(self.B3[:, kc, tt * 128:(tt + 1) * 128], self.wout_b[:, kc, half * 512:(half + 1) * 512]) for kc in range(8)]) for half in range(2)])
            S.op('dve', rkeys + [rk], [rk], lambda e, r=r, pm=pm: e.scalar_tensor_tensor(r[:], in0=pm, scalar=0.5, in1=r[:], op0=ALU.mult, op1=ALU.add))
            self.layernorm(r, rk, self.lnpb[:, 0:1024], self.lnpb[:, 1024:2048], 'lnpb', tt)
            S.dma('sp', [rk], [('hres', tt)], lambda e, r=r, tt=tt: e.dma_start(out=self.hres[tt * 128:(tt + 1) * 128, :], in_=r[:]))
            self.to_featmajor2(r, rk, self.B1, 'hT', tt)
        if 'hT' in self.dbg:
            if True:
                self.dbgf = self.sb("dbgf", [128, 512])
            for kc in range(8):
                for q4 in range(4):
                    S.op('dve', [('hT', tt) for tt in range(16)] + [('dbg', 'hT')], ['dbgf'], lambda e, kc=kc, q4=q4: e.tensor_copy(self.dbgf[:, 0:512], self.B1[:, kc, q4 * 512:(q4 + 1) * 512]))
                    S.dma('sp', ['dbgf'], [('dbg', 'hT')], lambda e, kc=kc, q4=q4: e.dma_start(out=self.dbg['hT'][kc * 128:(kc + 1) * 128, q4 * 512:(q4 + 1) * 512], in_=self.dbgf[:, 0:512]))
            self.final_keys.append(('dbg', 'hT'))

    def to_featmajor2(self, tile, tkey, dstT, dkey, tt):
        S = self.S
        for half in range(2):
            ps = self.PC[:, 0:512] if half == 0 else self.PD[:, 0:512]
            pk = 'PC' if half == 0 else 'PD'
            self.tps([tkey, 'cst'], [pk], [(ps[:, j * 128:(j + 1) * 128], tile[:, (half * 4 + j) * 128:(half * 4 + j + 1) * 128]) for j in range(4)], self.ident_f)
            S.op('act', [pk], [(dkey, tt)], lambda e, ps=ps, half=half: e.activation(
                dstT[:, half * 4:(half + 1) * 4, tt * 128:(tt + 1) * 128], ps.rearrange("p (c t) -> p c t", c=4), AF.Copy))

    def ffn(self, l):
        self.psum_epoch()
        S, sb = self.S, self.sb
        self.phase([self.R_W, (self.R_B3[0] + 4096, self.R_B3[1])])
        if True:
            self.wup_b = [sb("wupb%d" % i, [128, 8, 512], BF16) for i in range(2)]
            self.wdn_b = [sb("wdnb%d" % i, [128, 4, 1024], BF16) for i in range(2)]
            self.rl = [sb("rl%d" % i, [128, 512]) for i in range(2)]
            self.ln_st = [sb("lnst%d" % i, [128, 2, 6]) for i in range(2)]
            self.ln_mv = [sb("lnmv%d" % i, [128, 4]) for i in range(2)]
        uT = self.B3
        hk = [('hT', tt) for tt in range(16)]
        it = 0
        def load_f(gi):
            wu = self.wup_b[gi % 2]
            wd = self.wdn_b[gi % 2]
            wuk, wdk = ('wup', gi % 2), ('wdn', gi % 2)
            S.dma('pool', [], [(wuk, 0)], lambda e, wu=wu, gi=gi: e.dma_start(
                out=wu[:], in_=self.wup[l, gi].rearrange("p (k n) -> p k n", k=8), max_dma_last_dim=4096))
            S.dma('pool', [], [(wdk, 0)], lambda e, wd=wd, gi=gi: e.dma_start(
                out=wd[:], in_=self.wdn[l, gi].rearrange("p (k n) -> p k n", k=4), max_dma_last_dim=4096))
        load_f(0)
        for gi in range(8):
            wu = self.wup_b[gi % 2]
            wd = self.wdn_b[gi % 2]
            wuk, wdk = ('wup', gi % 2), ('wdn', gi % 2)
            if gi + 1 < 8:
                load_f(gi + 1)
            wuks = [(wuk, 0)]
            wdks = [(wdk, 0)]
            for j in range(4):
                for tc in range(4):
                    bank = it % 4
                    ps = [self.PA[:, 0:512], self.PA[:, 512:1024], self.PC[:, 0:512], self.PD[:, 0:512]][bank]
                    pk = [('PA', 0), ('PA', 1), 'PC', 'PD'][bank]
                    cols = slice(tc * 512, (tc + 1) * 512)
                    self.mm(wuks + hk[tc * 4:(tc + 1) * 4], [pk], ps, [(wu[:, kc, j * 128:(j + 1) * 128], self.B1[:, kc, cols]) for kc in range(8)])
                    rl = self.rl[it % 2]
                    rlk = ('rl', it % 2)
                    S.op('act', [pk], [rlk], lambda e, rl=rl, ps=ps: e.activation(rl[:], ps, AF.Relu))
                    eng = 'dve'
                    S.op(eng, [rlk], [('uT', j)], lambda e, rl=rl, j=j, cols=cols: e.tensor_tensor(uT[:, j, cols], rl[:], rl[:], ALU.mult))
                    it += 1
            for tt in range(16):
                if tt % 2 == 0:
                    pm = self.PB[:, 0:1024]
                    rkeys = [('PB', 0), ('PB', 1)]
                else:
                    pm = self.PA[:, 1024:2048]
                    rkeys = [('PA', 2), ('PA', 3)]
                self.mmseq([('uT', j) for j in range(4)] + wdks, rkeys,
                           [(pm[:, half * 512:(half + 1) * 512], [(uT[:, j, tt * 128:(tt + 1) * 128], wd[:, j, half * 512:(half + 1) * 512]) for j in range(4)]) for half in range(2)])
                if gi == 0:
                    S.op('act', rkeys, [('acc', tt)], lambda e, tt=tt, pm=pm: e.activation(self.acc[:, tt, :], pm, AF.Copy))
                else:
                    S.op('dve', rkeys + [('acc', tt)], [('acc', tt)], lambda e, tt=tt, pm=pm: e.tensor_tensor(self.acc[:, tt, :], pm, self.acc[:, tt, :], ALU.add))
        S.barrier()
        self.res = [self.wup_b[i][:, 0:4, :].rearrange("p a b -> p (a b)").bitcast(F32) for i in range(2)]
        self.lnpb = self.wdn_b[0][:].rearrange("p a b -> p (a b)").bitcast(F32)
        lnpb2 = self.lnpb
        S.dma('sp', [], ['lnpb'], lambda e: e.dma_start(out=lnpb2, in_=self.lnp[l][:, 2048:4096].broadcast_to([128, 2048])))
        dst = self.out if l == DEPTH - 1 else self.x1
        dstk = 'out' if l == DEPTH - 1 else 'x1'
        for tt in range(16):
            r = self.res[tt % 2]
            rk = ('res', tt % 2)
            S.dma('sp', [('hres', tt)], [rk], lambda e, r=r, tt=tt: e.dma_start(out=r[:], in_=self.hres[tt * 128:(tt + 1) * 128, :]))
            S.op('dve', [rk, ('acc', tt)], [rk], lambda e, r=r, tt=tt: e.scalar_tensor_tensor(r[:], in0=r[:], scalar=float(ALPHA), in1=self.acc[:, tt, :], op0=ALU.mult, op1=ALU.add))
            self.layernorm(r, rk, self.lnpb[:, 0:1024], self.lnpb[:, 1024:2048], 'lnpb', tt)
            S.dma('sp', [rk], [(dstk, tt)], lambda e, r=r, tt=tt: e.dma_start(out=dst[tt * 128:(tt + 1) * 128, :], in_=r[:]))
            if l == DEPTH - 1:
                self.final_keys.append((dstk, tt))
            else:
                self.to_featmajor2(r, rk, self.xT, 'xT', tt)


def sel_max(e, b):
    ins = None
    for t in range(8):
        ins = e.max(b.m8[:, t, :], b.gm[:, t * 8:(t + 1) * 8])
    return ins


def sel_lt(e, b):
    ins = None
    for t in range(8):
        ins = e.tensor_scalar(b.lt[:, t * 8:(t + 1) * 8], b.gm[:, t * 8:(t + 1) * 8], b.m8[:, t, 2:3], NEG, ALU.is_lt, ALU.mult)
    return ins


_CACHE = {}


def kernel(**inputs):
    inp = {k: np.asarray(v) for k, v in inputs.items()}
    w = prep_weights(inp)
    consts, e8 = make_consts()
    x = np.ascontiguousarray(inp['x'], dtype=np.float32)
    pos = np.ascontiguousarray(inp['positions'], dtype=np.int32)
    nc = Builder().build()
    in_maps = []
    for b in range(NCORES):
        m = {"x": x[b], "pos": pos[b:b + 1], "consts": consts, "e8": e8}
        m.update(w)
        in_maps.append(m)
    res = run_bass_kernel_spmd(nc, in_maps, core_ids=list(range(NCORES)))
    out = np.stack([np.asarray(r["out"], dtype=np.float32) for r in res.results], axis=0)
    return out
```

```python
import math
import numpy as np
import concourse.bass as bass
import concourse.mybir as mybir
from concourse.bass_utils import run_bass_kernel_spmd

F32 = mybir.dt.float32
BF16 = mybir.dt.bfloat16
I32 = mybir.dt.int32
AF = mybir.ActivationFunctionType
ALU = mybir.AluOpType
AX = mybir.AxisListType

D = 1024
SEQ = 2048
DEPTH = 2
NCORES = 8
IN_W = 11296
ALPHA = (2 * DEPTH) ** 0.25
LN_EPS = 1e-5
RMS_EPS = 1e-5
NEG = -1.0e5
PI = math.pi

DEBUG = {}
STOP_AFTER = None
NO_BARRIER = False
VARIANT = 0


class Sched:
    ENG = ('pe', 'act', 'dve', 'pool', 'sp')

    def __init__(self, nc, n_dma_sems=24):
        self.nc = nc
        self.sem = {e: nc.alloc_semaphore('s_' + e) for e in self.ENG}
        self.cnt = {e: 0 for e in self.ENG}
        self.dsem = [nc.alloc_semaphore('d%d' % i) for i in range(n_dma_sems)]
        self.dcnt = [0] * n_dma_sems
        self.drr = {'sp': 0, 'pool': 0}
        self.dpool = {'sp': list(range(0, n_dma_sems // 2)), 'pool': list(range(n_dma_sems // 2, n_dma_sems))}
        self.waited = {}
        self.last_w = {}
        self.readers = {}
        self.q = {e: [] for e in self.ENG}

    def barrier(self):
        if NO_BARRIER is True:
            return
        self.epoch = {}
        scr = self.bar_scratch
        d_sp = {('dma', i): self.dcnt[i] for i in self.dpool['sp'] if self.dcnt[i] > 0}
        d_pl = {('dma', i): self.dcnt[i] for i in self.dpool['pool'] if self.dcnt[i] > 0}
        self._token('act', d_sp, lambda e: e.memzero(scr[:, 0:1]))
        t1 = ('act', self.cnt['act'])
        d_pl[t1[0]] = t1[1]
        self._token('pool', d_pl, lambda e: e.memset(scr[:, 1:2], 0.0))
        d3 = {e: c for e, c in self.cnt.items() if c > 0}
        self._token('dve', d3, lambda e: e.memset(scr[:, 2:3], 0.0))
        self.epoch = {'dve': self.cnt['dve']}

    def _token(self, e, deps, emit):
        waits = self._waits(e, deps)
        self.cnt[e] += 1
        sem = self.sem[e]

        def thunk(eng):
            for s, v in waits:
                eng.wait_ge(s, v)
            emit(eng).then_inc(sem, 1)
        self.q[e].append(thunk)

    def _deps(self, reads, writes):
        deps = dict(getattr(self, 'epoch', {}))

        def add(src, idx):
            if deps.get(src, 0) < idx:
                deps[src] = idx
        for k in reads:
            lw = self.last_w.get(k)
            if lw is not None:
                add(*lw)
        for k in writes:
            lw = self.last_w.get(k)
            if lw is not None:
                add(*lw)
            for src, idx in self.readers.get(k, {}).items():
                add(src, idx)
        return deps

    def _waits(self, e, deps):
        out = []
        self.maxw = getattr(self, 'maxw', {})
        for src, idx in deps.items():
            if self.waited.get((e, src), 0) >= idx:
                continue
            self.waited[(e, src)] = idx
            if isinstance(src, tuple):
                out.append((self.dsem[src[1]], 16 * idx))
            else:
                out.append((self.sem[src], idx))
        self.maxw[len(out)] = self.maxw.get(len(out), 0) + 1
        return out

    def _record(self, token, reads, writes):
        src, idx = token
        for k in reads:
            r = self.readers.setdefault(k, {})
            if r.get(src, 0) < idx:
                r[src] = idx
        for k in writes:
            self.last_w[k] = (src, idx)
            self.readers[k] = {}

    @staticmethod
    def _bank(k):
        if isinstance(k, tuple) and k and k[0] == 'PA':
            return ('BANK', k[1])
        if isinstance(k, tuple) and k and k[0] == 'PB':
            return ('BANK', 4 + k[1])
        if k in ('PC', 'PCg', 'PCo'):
            return ('BANK', 6)
        if k == 'PD':
            return ('BANK', 7)
        return None

    def _canon(self, reads, writes):
        r2, w2 = [], []
        for k in reads:
            b = self._bank(k)
            if b is None:
                r2.append(k)
            elif b not in w2:
                w2.append(b)
        for k in writes:
            b = self._bank(k)
            if b is None:
                w2.append(k)
            elif b not in w2:
                w2.append(b)
        return r2, w2

    def op(self, e, reads, writes, emit):
        reads, writes = self._canon(reads, writes)
        deps = self._deps(reads, writes)
        waits = self._waits(e, deps)
        self.cnt[e] += 1
        idx = self.cnt[e]
        sem = self.sem[e]

        def thunk(eng):
            for s, v in waits:
                eng.wait_ge(s, v)
            emit(eng).then_inc(sem, 1)
        self.q[e].append(thunk)
        self._record((e, idx), reads, writes)

    def dma(self, e, reads, writes, emit):
        pool = self.dpool[e]
        s = pool[self.drr[e] % len(pool)]
        self.drr[e] += 1
        src = ('dma', s)
        deps = self._deps(reads, writes)
        if self.dcnt[s] > 0 and deps.get(src, 0) < self.dcnt[s]:
            deps[src] = self.dcnt[s]
        waits = self._waits(e, deps)
        self.dcnt[s] += 1
        idx = self.dcnt[s]
        sem = self.dsem[s]

        def thunk(eng):
            for sm, v in waits:
                eng.wait_ge(sm, v)
            emit(eng).then_inc(sem, 16)
        self.q[e].append(thunk)
        self._record((src, idx), reads, writes)

    def alias(self, old_keys, new_keys):
        acc = {}
        for k in old_keys:
            lw = self.last_w.get(k)
            if lw is not None and acc.get(lw[0], 0) < lw[1]:
                acc[lw[0]] = lw[1]
            for src, idx in self.readers.get(k, {}).items():
                if acc.get(src, 0) < idx:
                    acc[src] = idx
        for k in new_keys:
            r = self.readers.setdefault(k, {})
            for src, idx in acc.items():
                if r.get(src, 0) < idx:
                    r[src] = idx

    def finish(self, final_keys):
        self.barrier()
        deps = self._deps(final_keys, [])
        waits = self._waits('sp', deps)

        def thunk(eng):
            for s, v in waits:
                eng.wait_ge(s, v)
        self.q['sp'].append(thunk)
        nc = self.nc
        q = self.q
        with nc.Block() as block:
            @block.sync
            def _(eng):
                for t in q['sp']:
                    t(eng)

            @block.tensor
            def _(eng):
                for t in q['pe']:
                    t(eng)

            @block.scalar
            def _(eng):
                for t in q['act']:
                    t(eng)

            @block.vector
            def _(eng):
                for t in q['dve']:
                    t(eng)

            @block.gpsimd
            def _(eng):
                for t in q['pool']:
                    t(eng)


C_ID, C_TRI, C_ONES, C_RM, C_INVF, C_SGN, C_NEGP, C_FLOOR, NC_CONST = 0, 128, 256, 384, 512, 513, 514, 578, 648


def make_consts():
    c = np.zeros((128, NC_CONST), np.float32)
    c[:, C_ID:C_ID + 128] = np.eye(128, dtype=np.float32)
    c[:, C_TRI:C_TRI + 128] = np.triu(np.ones((128, 128), np.float32))
    c[:, C_ONES:C_ONES + 128] = 1.0
    rm = np.zeros((128, 128), np.float32)
    inv = (500000.0 ** (-np.arange(0, 16, 2, dtype=np.float32) / 16)).astype(np.float32)
    for blk in range(2):
        for d in range(16):
            src = d + 8 if d < 8 else d - 8
            rm[blk * 64 + src, blk * 64 + d] = 1.0
    c[:, C_RM:C_RM + 128] = rm
    for p in range(128):
        d = p % 64
        if d < 16:
            c[p, C_INVF] = inv[d % 8]
            c[p, C_SGN] = -1.0 if d < 8 else 1.0
    for qt in range(8):
        blk = (8 + qt) // 2
        for j in range(8):
            c[:, C_NEGP + qt * 8 + j] = 0.0 if j < blk else -1.0e30
            c[:, C_FLOOR + qt * 8 + j] = NEG if j < blk else 0.0
    e8 = np.zeros((8, SEQ), np.float32)
    for j in range(8):
        e8[j, j * 256:(j + 1) * 256] = 1.0
    return c, e8


def prep_weights(inp):
    out = {}
    L = DEPTH

    def kmaj(w):
        K, N = w.shape
        return w.reshape(K // 128, 128, N).transpose(1, 0, 2)

    w_in = inp['w_in']
    wqkv = np.empty((L, 8, 128, 8, 384), np.float32)
    wssm = np.empty((L, 8, 128, 8, 768), np.float32)
    wdt = np.empty((L, 128, 8, 32), np.float32)
    wc1 = np.empty((L, 8, 128, 5120), np.float32)
    wout = np.empty((L, 128, 8, 1024), np.float32)
    wup = np.empty((L, 8, 128, 8, 512), np.float32)
    wdn = np.empty((L, 8, 128, 4, 1024), np.float32)
    cw = np.empty((L, 128, 8, 4, 5), np.float32)
    for l in range(L):
        wi = kmaj(w_in[l])
        for hp in range(8):
            wqkv[l, hp, :, :, 0:128] = wi[:, :, hp * 128:(hp + 1) * 128]
            wqkv[l, hp, :, :, 128:256] = wi[:, :, 1024 + hp * 128:1024 + (hp + 1) * 128]
            wqkv[l, hp, :, :, 256:384] = wi[:, :, 2048 + hp * 128:2048 + (hp + 1) * 128]
        for g in range(8):
            wssm[l, g, :, :, 0:256] = wi[:, :, 3072 + g * 256:3072 + (g + 1) * 256]
            wssm[l, g, :, :, 256:512] = wi[:, :, 5120 + g * 256:5120 + (g + 1) * 256]
            wssm[l, g, :, :, 512:640] = wi[:, :, 7168 + g * 128:7168 + (g + 1) * 128]
            wssm[l, g, :, :, 640:768] = wi[:, :, 8192 + g * 128:8192 + (g + 1) * 128]
        wdt[l] = wi[:, :, 9216:9248]
        wap = kmaj(inp['w_attn_proj'][l])
        wsp = kmaj(inp['w_ssm_proj'][l])
        for fc in range(8):
            blk = np.empty((128, 8, 256), np.float32)
            blk[:, :, 0:128] = wi[:, :, 9248 + fc * 128:9248 + (fc + 1) * 128]
            blk[:, :, 128:256] = wi[:, :, 10272 + fc * 128:10272 + (fc + 1) * 128]
            wc1[l, fc, :, 0:2048] = blk.reshape(128, 2048)
            wc1[l, fc, :, 2048:3072] = wap[:, :, fc * 128:(fc + 1) * 128].reshape(128, 1024)
            wc1[l, fc, :, 3072:5120] = wsp[:, :, fc * 128:(fc + 1) * 128].reshape(128, 2048)
        wout[l] = kmaj(inp['w_out'][l])
        wu = kmaj(inp['w_up'][l])
        wd = kmaj(inp['w_down'][l])
        for gI in range(8):
            wup[l, gI] = wu[:, :, gI * 512:(gI + 1) * 512]
            wdn[l, gI] = wd[:, gI * 4:(gI + 1) * 4, :]
        cwl = inp['conv_w'][l]
        cbl = inp['conv_b'][l]
        for g in range(8):
            offs = [g * 256, g * 256 + 128, 2048 + g * 128, 3072 + g * 128]
            for ci, o in enumerate(offs):
                cw[l, :, g, ci, 0:4] = cwl[:, o:o + 128].T
                cw[l, :, g, ci, 4] = cbl[o:o + 128]
    out['wqkv'] = wqkv.reshape(L, 8, 128, 3072)
    out['wssm'] = wssm.reshape(L, 8, 128, 6144)
    out['wdt'] = wdt.reshape(L, 128, 256)
    out['wc1'] = wc1
    out['wout'] = wout.reshape(L, 128, 8192)
    out['wup'] = wup.reshape(L, 8, 128, 4096)
    out['wdn'] = wdn.reshape(L, 8, 128, 4096)
    out['cw'] = cw.reshape(L, 128, 160)
    small = np.concatenate([inp['dt_bias'], inp['a_log'], inp['d_skip']], axis=1)
    out['small'] = np.ascontiguousarray(small.reshape(L, 1, 96))
    out['normw'] = np.ascontiguousarray(inp['ssm_norm_w'].reshape(L, 1, 2048))
    lnp = np.stack([inp['ln1_g'], inp['ln1_b'], inp['ln2_g'], inp['ln2_b']], axis=1)
    out['lnp'] = np.ascontiguousarray(lnp.reshape(L, 1, 4096))
    return {k: np.ascontiguousarray(v, dtype=np.float32) for k, v in out.items()}


class Arena:
    def __init__(self, base_ap, segments):
        self.base = base_ap
        self.segs = [list(x) for x in segments]

    def alloc(self, shape, dtype=F32):
        P = shape[0]
        n = 1
        for d in shape[1:]:
            n *= d
        per = 4 if dtype in (F32, I32) else 2
        ncols = (n * per + 3) // 4
        ncols = (ncols + 7) // 8 * 8
        for sg in self.segs:
            if sg[1] - sg[0] >= ncols:
                off = sg[0]
                sg[0] += ncols
                ap = self.base[0:P, off:off + (n * per + 3) // 4]
                if dtype != F32:
                    ap = ap.bitcast(dtype)
                if len(shape) == 3:
                    ap = ap.rearrange("p (a b) -> p a b", a=shape[1])
                elif len(shape) == 4:
                    ap = ap.rearrange("p (a b c) -> p a b c", a=shape[1], b=shape[2])
                return ap
        raise RuntimeError("arena out of memory for %s" % (shape,))


class _Stop(Exception):
    pass


class Builder:
    def __init__(self):
        nc = bass.Bass("TRN2", target_bir_lowering=False)
        self.nc = nc
        self.S = Sched(nc)
        self.rec = None
        S_ = self.S
        S_._op, S_._dma = S_.op, S_.dma

        def _rop(*a):
            if self.rec is not None:
                self.rec.append(lambda: S_._op(*a))
            else:
                S_._op(*a)

        def _rdma(*a):
            if self.rec is not None:
                self.rec.append(lambda: S_._dma(*a))
            else:
                S_._dma(*a)
        S_.op, S_.dma = _rop, _rdma
        L = DEPTH
        dt = lambda n, s, d=F32, k="ExternalInput": nc.dram_tensor(n, s, d, kind=k).ap()
        self.x_in = dt("x", [SEQ, D])
        self.pos_in = dt("pos", [1, SEQ], I32)
        self.consts_in = dt("consts", [128, NC_CONST])
        self.e8_in = dt("e8", [8, SEQ])
        self.wqkv = dt("wqkv", [L, 8, 128, 3072])
        self.wssm = dt("wssm", [L, 8, 128, 6144])
        self.wdt = dt("wdt", [L, 128, 256])
        self.wc1 = dt("wc1", [L, 8, 128, 5120])
        self.wout = dt("wout", [L, 128, 8192])
        self.wup = dt("wup", [L, 8, 128, 4096])
        self.wdn = dt("wdn", [L, 8, 128, 4096])
        self.cw = dt("cw", [L, 128, 160])
        self.small = dt("small", [L, 1, 96])
        self.normw = dt("normw", [L, 1, 2048])
        self.lnp = dt("lnp", [L, 1, 4096])
        self.out = dt("out", [SEQ, D], F32, "ExternalOutput")
        self.x1 = dt("x1s", [SEQ, D], F32, "Internal")
        self.hres = dt("hres", [SEQ, D], F32, "Internal")
        self.dbg = {}
        for name, shape in DEBUG.items():
            self.dbg[name] = dt("dbg_" + name, list(shape), F32, "ExternalOutput")
        self.final_keys = []

    def sb(self, name, shape, dtype=F32):
        return self.ar.alloc(list(shape), dtype)

    def phase(self, segs):
        self.S.barrier()
        self.ar = Arena(self.arena, segs)

    def mm(self, reads, writes, out, pairs, transpose=False):
        def emit(e):
            n = len(pairs)
            ins = None
            for i, (l, r) in enumerate(pairs):
                ins = e.matmul(out, lhsT=l, rhs=r, start=(i == 0), stop=(i == n - 1))
            return ins
        self.S.op('pe', reads, writes, emit)

    def mmseq(self, reads, writes, groups):
        def emit(e):
            ins = None
            for out, pairs in groups:
                n = len(pairs)
                for i, (l, r) in enumerate(pairs):
                    ins = e.matmul(out, lhsT=l, rhs=r, start=(i == 0), stop=(i == n - 1))
            return ins
        self.S.op('pe', reads, writes, emit)

    def tps(self, reads, writes, items, ident):
        def emit(e):
            ins = None
            for o, i in items:
                ins = e.transpose(o, i, ident)
            return ins
        self.S.op('pe', reads, writes, emit)

    def dump(self, name, ap, keys):
        if name in self.dbg:
            if not isinstance(keys, list):
                keys = [keys]
            self.S.dma('sp', keys, [('dbg', name)], lambda e: e.dma_start(out=self.dbg[name], in_=ap))
            self.final_keys.append(('dbg', name))

    def build(self):
        nc, S = self.nc, self.S
        sb = self.sb
        nbytes = int(nc.sbuf_bytes_remaining) - 256
        NA = nbytes // 4 // 8 * 8
        self.arena = nc.alloc_sbuf_tensor("arena", [128, NA], F32)[:]
        self.R_P = (0, 11264)
        self.R_B1 = (11264, 19456)
        self.R_B2 = (19456, 35840)
        self.R_B3 = (35840, 44032)
        self.R_W = (44032, NA)
        assert NA - 44032 > 8000, NA
        self.ar = Arena(self.arena, [self.R_P])
        self.S.bar_scratch = sb("barscr", [128, 8])
        cst = sb("cst", [128, NC_CONST])
        self.cst = cst
        S.dma('sp', [], ['cst'], lambda e: e.dma_start(out=cst[:], in_=self.consts_in))
        self.ident_f = cst[:, C_ID:C_ID + 128]
        self.tri_f = cst[:, C_TRI:C_TRI + 128]
        self.ones_f = cst[:, C_ONES:C_ONES + 128]
        cbf = sb("cbf", [128, 512], BF16)
        S.op('dve', ['cst'], ['cbf'], lambda e: e.tensor_copy(cbf[:], cst[:, 0:512]))
        self.ident_b = cbf[:, 0:128]
        self.tri_b = cbf[:, 128:256]
        self.rm_b = cbf[:, 384:512]
        self.nh = sb("nh", [128, 8])
        S.op('pool', [], ['nh'], lambda e: e.memset(self.nh[:], -0.5))
        self.ropeC = sb("ropeC", [128, SEQ], BF16)
        self.ropeS = sb("ropeS", [128, SEQ], BF16)
        self.xT = sb("xT", [128, 8, SEQ], BF16)
        A = self.arena
        self.B1 = A[:, self.R_B1[0]:self.R_B1[1]].bitcast(BF16).rearrange("p (c t) -> p c t", c=8)
        self.B2 = A[:, self.R_B2[0]:self.R_B2[1]]
        self.B3 = A[:, self.R_B3[0]:self.R_B3[1]].bitcast(BF16).rearrange("p (c t) -> p c t", c=8)
        self.yT = self.B2.bitcast(BF16).rearrange("p (c t) -> p c t", c=16)
        self.acc = self.B2.rearrange("p (t f) -> p t f", t=16)
        self.ar = Arena(self.arena, [self.R_B2])
        self.PA = nc.alloc_psum_tensor("PA", [128, 2048], F32)
        self.PB = nc.alloc_psum_tensor("PB", [128, 1024], F32)
        self.PC = nc.alloc_psum_tensor("PC", [128, 512], F32)
        self.PD = nc.alloc_psum_tensor("PD", [128, 512], F32)
        self.rope_tables()
        if STOP_AFTER == 'rope':
            return self.finish()
        self.load_xT()
        if STOP_AFTER == 'xT':
            self.dbg_tile('xT0', self.xT[:, 3, 1024:1536], self.xT_keys(), 512)
            return self.finish()
        try:
            for l in range(DEPTH):
                self.layer(l)
                if STOP_AFTER == 'layer0':
                    break
        except _Stop:
            pass
        return self.finish()

    def finish(self):
        self.S.finish(self.final_keys)
        return self.nc

    def rope_tables(self):
        S, sb = self.S, self.sb
        cst = None
        posi = sb("posi", [128, SEQ], I32)
        S.dma('sp', [], ['posi'], lambda e: e.dma_start(out=posi[:], in_=self.pos_in.broadcast_to([128, SEQ])))
        ang = sb("ang", [128, SEQ])
        tmp = sb("rtmp", [128, SEQ])
        tmi = sb("rtmi", [128, SEQ], I32)
        rc = sb("rc", [128, SEQ])
        rs = sb("rs", [128, SEQ])
        invf = self.cst[:, C_INVF:C_INVF + 1]
        sgn = self.cst[:, C_SGN:C_SGN + 1]
        S.op('dve', ['posi'], ['ang'], lambda e: e.tensor_copy(ang[:], posi[:]))
        S.op('dve', ['ang', 'cst'], ['ang'], lambda e: e.tensor_scalar(ang[:], ang[:], invf, None, ALU.mult))
        for which, dst in ((0, rs), (1, rc)):
            off = 0.0 if which == 0 else PI / 2
            S.op('dve', ['ang'], ['rtmp'], lambda e, off=off: e.tensor_scalar(tmp[:], ang[:], off, 1.0 / (2 * PI), ALU.add, ALU.mult))
            S.op('dve', ['rtmp'], ['rtmi'], lambda e: e.tensor_copy(tmi[:], tmp[:]))
            S.op('dve', ['rtmi'], ['rtmp'], lambda e: e.tensor_copy(tmp[:], tmi[:]))
            S.op('dve', ['rtmp', 'ang'], ['rtmp'], lambda e: e.scalar_tensor_tensor(tmp[:], in0=tmp[:], scalar=-2 * PI, in1=ang[:], op0=ALU.mult, op1=ALU.add))
            S.op('dve', ['rtmp'], [('rope', which)], lambda e, off=off, dst=dst: e.tensor_scalar(dst[:], tmp[:], off, None, ALU.add))
            S.op('dve', [('rope', which)], ['rtmp'], lambda e, dst=dst: e.tensor_scalar(tmp[:], dst[:], PI, -2 * PI, ALU.is_gt, ALU.mult))
            S.op('dve', ['rtmp', ('rope', which)], [('rope', which)], lambda e, dst=dst: e.tensor_tensor(dst[:], dst[:], tmp[:], ALU.add))
            S.op('dve', [('rope', which)], ['rtmp'], lambda e, dst=dst: e.tensor_scalar(tmp[:], dst[:], -PI, 2 * PI, ALU.is_lt, ALU.mult))
            S.op('dve', ['rtmp', ('rope', which)], [('rope', which)], lambda e, dst=dst: e.tensor_tensor(dst[:], dst[:], tmp[:], ALU.add))
            S.op('dve', [('rope', which)], [('rope', which)], lambda e, dst=dst: e.tensor_scalar(dst[:], dst[:], 3.1415925, -3.1415925, ALU.min, ALU.max))
            S.op('act', [('rope', which)], [('rope', which)], lambda e, dst=dst: e.activation(dst[:], dst[:], AF.Sin))
        S.op('dve', [('rope', 0), 'cst'], [('rope', 0)], lambda e: e.tensor_scalar(self.ropeS[:], rs[:], sgn, None, ALU.mult))
        S.op('dve', [('rope', 1)], [('rope', 1)], lambda e: e.tensor_copy(self.ropeC[:], rc[:]))
        self.dump('ropeC', rc[:], ('rope', 1))
        self.dump('ropeS', rs[:], ('rope', 0))

    def load_xT(self):
        S, sb = self.S, self.sb
        self.xt_buf = [sb("xtile%d" % i, [128, D]) for i in range(2)]
        for tt in range(16):
            xt = self.xt_buf[tt % 2]
            k = ('xtile', tt % 2)
            S.dma('sp', [], [k], lambda e, xt=xt, tt=tt: e.dma_start(out=xt[:], in_=self.x_in[tt * 128:(tt + 1) * 128, :]))
            self.to_featmajor(xt, k, self.xT, 'xT', tt)

    def to_featmajor(self, tile, tkey, dstT, dkey, tt):
        S = self.S
        for half in range(2):
            ps = self.PA[:, half * 512:(half + 1) * 512]
            pk = ('PA', half)
            self.tps([tkey, 'cst'], [pk], [(ps[:, j * 128:(j + 1) * 128], tile[:, (half * 4 + j) * 128:(half * 4 + j + 1) * 128]) for j in range(4)], self.ident_f)
            eng = 'act' if half == 0 else 'dve'
            if eng == 'act':
                S.op('act', [pk], [(dkey, tt)], lambda e, ps=ps, half=half: e.activation(
                    dstT[:, half * 4:(half + 1) * 4, tt * 128:(tt + 1) * 128], ps.rearrange("p (c t) -> p c t", c=4), AF.Copy))
            else:
                S.op('dve', [pk], [(dkey, tt)], lambda e, ps=ps, half=half: e.tensor_copy(
                    dstT[:, half * 4:(half + 1) * 4, tt * 128:(tt + 1) * 128], ps.rearrange("p (c t) -> p c t", c=4)))

    def layer(self, l):
        self.attention(l)
        if STOP_AFTER == 'att':
            raise _Stop()
        self.ssd(l)
        if STOP_AFTER == 'ssd':
            raise _Stop()
        self.merge(l)
        if STOP_AFTER == 'merge':
            raise _Stop()
        self.mix_ln1(l)
        if STOP_AFTER == 'ln1':
            raise _Stop()
        self.ffn(l)

    def psum_epoch(self):
        keys = [('PA', i) for i in range(4)] + [('PB', 0), ('PB', 1), ('PB', 0, 0), ('PB', 0, 1)] + \
               [('PB', 1, h) for h in range(4)] + ['PC', 'PCg', 'PCo', 'PD']
        self.S.alias(keys, keys)

    def dbg_qk(self, h):
        S = self.S
        if 'qa0' in self.dbg:
            qa, ka = self.qa[h], self.ka[h]
            dq = self.sb("dbgq", [128, SEQ])
            dk = self.sb("dbgk", [128, SEQ])
            S.op('dve', [('qa', h)], ['dbgq'], lambda e: e.tensor_copy(dq[0:72, :], qa[0:72, :]))
            S.op('dve', [('ka', h)], ['dbgk'], lambda e: e.tensor_copy(dk[0:72, :], ka[0:72, :]))
            self.dump('qa0', dq[0:72, :], 'dbgq')
            self.dump('ka0', dk[0:72, :], 'dbgk')

    def dbg_tile(self, name, ap, keys, ncols):
        if name in self.dbg:
            t = self.sb("dbg_" + name, [128, ncols])
            self.S.op('dve', keys, ['dbgt_' + name], lambda e: e.tensor_copy(t[:], ap))
            self.dump(name, t[:], 'dbgt_' + name)

    def xT_keys(self):
        return [('xT', tt) for tt in range(16)]

    def attention(self, l):
        self.psum_epoch()
        S, sb, nc = self.S, self.sb, self.nc
        self.phase([self.R_W, self.R_B2, (self.R_B3[0] + 7168, self.R_B3[1])])
        if True:
            self.wqkv_b = [sb("wqkv%d" % i, [128, 8, 384], BF16) for i in range(2)]
            self.qa = [sb("qa%d" % i, [72, SEQ], BF16) for i in range(4)]
            self.ka = [sb("ka%d" % i, [72, SEQ], BF16) for i in range(4)]
            self.va = [sb("va%d" % i, [128, 16, 65], BF16) for i in range(4)]
            self.qbf = [sb("qbf%d" % i, [128, 512], BF16) for i in range(2)]
            self.t1 = [sb("t1_%d" % i, [128, 512]) for i in range(2)]
            self.t2 = [sb("t2_%d" % i, [128, 512]) for i in range(2)]
            self.atm = [sb("atm%d" % i, [128, 16, 128], BF16) for i in range(2)]
            self.km = sb("km", [64, 8])
            self.kmh = sb("kmh", [64, 8], BF16)
            self.kml = sb("kml", [64, 8], BF16)
            self.gm = sb("gm", [128, 64])
            self.m8 = sb("m8", [128, 8, 8])
            self.lt = sb("lt", [128, 64])
            self.bpad = sb("bpad", [128, 8, 72], BF16)
            self.rden = sb("rden", [128, 4])
            self.NPT = 28
            for i in range(4):
                S.op('pool', [], [('qa', i)], lambda e, i=i: e.memset(self.qa[i][64:72, :], 0.0))
                S.dma('pool', [], [('ka', i)], lambda e, i=i: e.dma_start(out=self.ka[i][64:72, :], in_=self.e8_in))
                S.op('pool', [], [('va', i)], lambda e, i=i: e.memset(self.va[i][:, :, 64:65], 1.0))
            S.op('pool', [], ['bpad'], lambda e: e.memset(self.bpad[:], 0.0))
            if STOP_AFTER == 'att_init':
                self.dbg_qk(0)
                raise _Stop()
        PT = [self.B3[:, i // 4, (i % 4) * 512:(i % 4 + 1) * 512] for i in range(self.NPT)]
        attT = self.B1
        negp = self.cst[:, C_NEGP:C_NEGP + 64]
        floorb = self.cst[:, C_FLOOR:C_FLOOR + 64]
        pt_rr = [0]
        PDb = self.PD[:].bitcast(BF16)
        xk = self.xT_keys()

        def load_w(hp):
            wb = self.wqkv_b[hp % 2]
            S.dma('pool', [], [('wqkv', hp % 2, 0)], lambda e, wb=wb, hp=hp: e.dma_start(
                out=wb[:], in_=self.wqkv[l, hp].rearrange("p (k n) -> p k n", k=8), max_dma_last_dim=4096))
        load_w(0)
        Wl = [[] for _ in range(9)]
        Pl = [[] for _ in range(9)]
        Cl = [[] for _ in range(9)]
        for hp in range(8):
            wb = self.wqkv_b[hp % 2]
            wk = ('wqkv', hp % 2, 0)
            wk1 = wk2 = wk3 = wk
            self.rec = Wl[hp + 1]
            if STOP_AFTER == 'att_w0':
                self.dbg_tile('wb0', wb[:].rearrange("p a b -> p (a b)"), [wk, wk1, wk2, wk3], 3072)
                raise _Stop()
            if hp + 1 < 8:
                load_w(hp + 1)
            if STOP_AFTER == 'att_w1':
                self.dbg_tile('wb0', wb[:].rearrange("p a b -> p (a b)"), [wk, wk1, wk2, wk3], 3072)
                raise _Stop()
            par = (hp % 2) * 2
            hA, hB = par, par + 1
            self.rec = Pl[hp]
            cnt = 0
            for which in (1, 0):
                dst = self.ka if which == 1 else self.qa
                dn = 'ka' if which == 1 else 'qa'
                for tc in range(4):
                    bank = cnt % 4
                    cnt += 1
                    ps = self.PA[:, bank * 512:(bank + 1) * 512]
                    pk = ('PA', bank)
                    cols = slice(tc * 512, (tc + 1) * 512)
                    def stopat(ch, ap=None, keys=None):
                        if STOP_AFTER == 'att_qk1' + ch:
                            if ap is not None:
                                self.dbg_tile('probe', ap, keys, 512)
                            raise _Stop()
                    self.mm([wk, wk1, wk2, wk3] + xk[tc * 4:(tc + 1) * 4], [pk], ps,
                            [(wb[:, kc, which * 128:(which + 1) * 128], self.xT[:, kc, cols]) for kc in range(8)])
                    stopat('a', ps, [pk])
                    qb = self.qbf[cnt % 2]
                    qk_ = ('qbf', cnt % 2)
                    if VARIANT != 6:
                        S.op('act', [pk], [qk_], lambda e, qb=qb, ps=ps: e.activation(qb[:], ps, AF.Copy))
                    stopat('b', qb[:], [qk_])
                    t1 = self.t1[cnt % 2]
                    t2 = self.t2[cnt % 2]
                    k1 = ('t1', cnt % 2)
                    k2 = ('t2', cnt % 2)
                    if VARIANT == 4:
                        t1 = self.sb("t1x", [128, 512])
                    if VARIANT in (5, 8):
                        S.op('dve', [('rope', 1)], [k1], lambda e, t1=t1, ps=ps, cols=cols: e.tensor_copy(t1[:], self.ropeC[:, cols]))
                    elif VARIANT in (1, 4):
                        S.op('dve', [pk, ('rope', 1)], [k1], lambda e, t1=t1, ps=ps, cols=cols: e.tensor_copy(t1[:], ps))
                    elif VARIANT == 2:
                        S.op('dve', [pk, ('rope', 1)], [k1], lambda e, t1=t1, ps=ps, cols=cols: e.tensor_copy(t1[:], self.ropeC[:, cols]))
                    elif VARIANT == 3:
                        S.op('dve', [pk, ('rope', 1)], [k1], lambda e, t1=t1, ps=ps, cols=cols: e.tensor_tensor(t1[:], ps, self.cst[:, 0:512], ALU.mult))
                    else:
                        S.op('dve', [pk, ('rope', 1)], [k1], lambda e, t1=t1, ps=ps, cols=cols: e.tensor_tensor(t1[:], ps, self.ropeC[:, cols], ALU.mult))
                    stopat('c', qb[:] if VARIANT == 7 else t1[:], [qk_, k1] if VARIANT in (7, 8) else [k1])
                    rb = (bank + 2) % 4
                    pr = self.PA[:, rb * 512:(rb + 1) * 512]
                    prk = ('PA', rb)
                    self.mm([qk_, 'cbf'], [prk], pr, [(self.rm_b, qb[:])])
                    stopat('d', pr, [prk])
                    S.op('dve', [prk, ('rope', 0)], [k2], lambda e, t2=t2, pr=pr, cols=cols: e.tensor_tensor(t2[:], pr, self.ropeS[:, cols], ALU.mult))
                    stopat('e', t2[:], [k2])
                    S.op('dve', [k1, k2], [(dn, hA)], lambda e, t1=t1, t2=t2, dst=dst, cols=cols, hA=hA: e.tensor_tensor(dst[hA][0:64, cols], t1[0:64, :], t2[0:64, :], ALU.add))
                    stopat('f', dst[hA][0:64, cols], [(dn, hA)])
                    S.op('dve', [k1, k2], [(dn, hB)], lambda e, t1=t1, t2=t2, dst=dst, cols=cols, hB=hB: e.tensor_tensor(dst[hB][0:64, cols], t1[64:128, :], t2[64:128, :], ALU.add))
                    stopat('g', dst[hB][0:64, cols], [(dn, hB)])
            if STOP_AFTER == 'att_qk':
                raise _Stop()
            for tq in range(4):
                bank = cnt % 4
                cnt += 1
                ps = self.PA[:, bank * 512:(bank + 1) * 512]
                pk = ('PA', bank)
                self.mmseq([wk, wk1, wk2, wk3] + xk[tq * 4:(tq + 1) * 4], [pk],
                           [(ps[:, j * 128:(j + 1) * 128],
                             [(self.xT[:, kc, (tq * 4 + j) * 128:(tq * 4 + j + 1) * 128], wb[:, kc, 256:384]) for kc in range(8)])
                            for j in range(4)])
                psv = ps.rearrange("p (t c) -> p t c", t=4)
                if hp == 0 and tq == 0:
                    self.dbg_tile('psv', ps, [pk], 512)
                    self.dbg_tile('wb0', wb[:].rearrange("p a b -> p (a b)"), [wk, wk1, wk2, wk3], 3072)
                S.op('act', [pk], [('va', hA)], lambda e, psv=psv, tq=tq, hA=hA: e.activation(self.va[hA][:, tq * 4:(tq + 1) * 4, 0:64], psv[:, :, 0:64], AF.Copy))
                S.op('act', [pk], [('va', hB)], lambda e, psv=psv, tq=tq, hB=hB: e.activation(self.va[hB][:, tq * 4:(tq + 1) * 4, 0:64], psv[:, :, 64:128], AF.Copy))
            if STOP_AFTER == 'att_proj':
                self.dbg_tile('va0', self.va[0][:].rearrange("p a b -> p (a b)"), [('va', 0)], 1040)
                self.dbg_qk(hA)
                raise _Stop()
            self.rec = Cl[hp]
            atm = self.atm[hp % 2]
            ak = ('atm', hp % 2)
            for hh, hbuf in ((0, hA), (1, hB)):
                qa, ka, va = self.qa[hbuf], self.ka[hbuf], self.va[hbuf]
                qk, kk, vk = ('qa', hbuf), ('ka', hbuf), ('va', hbuf)
                S.op('dve', [kk], ['km'], lambda e, ka=ka: e.tensor_reduce(self.km[:], ka[0:64, :].rearrange("p (b t) -> p b t", b=8), AX.X, ALU.add))
                S.op('dve', ['km'], ['kmh'], lambda e: e.tensor_scalar(self.kmh[:], self.km[:], 1.0 / 256, None, ALU.mult))
                S.op('dve', ['km', 'kmh'], ['kml'], lambda e: e.scalar_tensor_tensor(self.kml[:], in0=self.km[:], scalar=1.0 / 256, in1=self.kmh[:], op0=ALU.mult, op1=ALU.subtract))
                pg = self.PC[:, 448:512]
                self.mmseq([qk, 'kmh', 'kml'], ['PCg'],
                           [(pg[:, t * 8:(t + 1) * 8], [(qa[0:64, (8 + t) * 128:(9 + t) * 128], self.kmh[:]), (qa[0:64, (8 + t) * 128:(9 + t) * 128], self.kml[:])]) for t in range(8)])
                S.op('dve', ['PCg', 'cst'], ['gm'], lambda e, pg=pg: e.tensor_tensor(self.gm[:], pg, negp, ALU.add))

                S.op('dve', ['gm'], ['m8'], lambda e: sel_max(e, self))
                S.op('dve', ['gm', 'm8'], ['lt'], lambda e: sel_lt(e, self))
                S.op('dve', ['lt', 'cst'], ['bpad'], lambda e: e.tensor_tensor(self.bpad[:, :, 64:72], self.lt[:].rearrange("p (t j) -> p t j", t=8), floorb.rearrange("p (t j) -> p t j", t=8), ALU.max))
                self.tps(['bpad', 'cbf'], ['PD'], [(PDb[0:72, t * 128:(t + 1) * 128], self.bpad[:, t, :]) for t in range(8)], self.ident_b)
                S.op('act', ['PD'], [qk], lambda e, qa=qa: e.activation(qa[64:72, 1024:2048], PDb[64:72, :], AF.Copy))
                if STOP_AFTER == 'att_gate':
                    self.dbg_qk(hA)
                    raise _Stop()
                slots = {}

                def stageA(qc):
                    sl = []
                    for kt in range(4 * qc + 4):
                        c0 = max(0, kt * 128 - qc * 512)
                        bank = kt % 2
                        ps = self.PB[:, bank * 512:(bank + 1) * 512]
                        pk = ('PB', bank)
                        self.mm([kk, qk], [pk], ps[:, c0:512], [(ka[0:72, kt * 128:(kt + 1) * 128], qa[0:72, qc * 512 + c0:(qc + 1) * 512])])
                        si = pt_rr[0]
                        pt_rr[0] = (pt_rr[0] + 1) % self.NPT
                        sl.append(si)
                        pt = PT[si]
                        S.op('act', [pk], [('PT', si)], lambda e, pt=pt, ps=ps, c0=c0: e.activation(pt[:, c0:512], ps[:, c0:512], AF.Exp, scale=0.125))
                        if kt >= 4 * qc:
                            S.op('dve', [('PT', si), 'cbf'], [('PT', si)], lambda e, pt=pt, c0=c0: e.tensor_tensor(pt[:, c0:c0 + 128], pt[:, c0:c0 + 128], self.tri_b, ALU.mult))
                    slots[qc] = sl

                def stageB(qc):
                    sl = slots[qc]
                    groups = []
                    for qi in range(4):
                        qt = 4 * qc + qi
                        groups.append((self.PC[:, qi * 65:(qi + 1) * 65],
                                       [(PT[sl[kt]][:, qi * 128:(qi + 1) * 128], va[:, kt, :]) for kt in range(qt + 1)]))
                    self.mmseq([('PT', s_) for s_ in sl] + [vk], ['PCo'], groups)
                    pv = self.PC[:, 0:260].rearrange("p (q c) -> p q c", q=4)
                    S.op('dve', ['PCo'], ['rden'], lambda e, pv=pv: e.reciprocal(self.rden[:], pv[:, :, 64]))

                    def norm(e, atm=atm, hh=hh, qc=qc):
                        ins = None
                        for qi in range(4):
                            ins = e.tensor_scalar(atm[:, 4 * qc + qi, hh * 64:(hh + 1) * 64], self.PC[:, qi * 65:qi * 65 + 64], self.rden[:, qi:qi + 1], None, ALU.mult)
                        return ins
                    S.op('dve', ['PCo', 'rden'], [ak], norm)

                stageA(0)
                for qc in range(4):
                    if qc + 1 < 4:
                        stageA(qc + 1)
                    stageB(qc)
            if hp == 0:
                self.dbg_tile('va0', self.va[0][:].rearrange("p a b -> p (a b)"), [('va', 0)], 1040)
                self.dbg_tile('atm0', atm[:].rearrange("p a b -> p (a b)"), [ak], 2048)
            for half in range(2):
                self.tps([ak, 'cbf'], ['PD'], [(PDb[:, t * 128:(t + 1) * 128], atm[:, half * 8 + t, :]) for t in range(8)], self.ident_b)
                S.op('act', ['PD'], [('attT', hp, half)], lambda e, half=half, hp=hp: e.activation(attT[:, hp, half * 1024:(half + 1) * 1024], PDb, AF.Copy))
            self.rec = None
        for t in Pl[0]:
            t()
        for hp in range(8):
            for t in Wl[hp + 1]:
                t()
            A, Bn = Cl[hp], Pl[hp + 1]
            n1, n2 = len(A), len(Bn)
            i1 = i2 = 0
            while i1 < n1 or i2 < n2:
                if i2 >= n2 or (i1 < n1 and i1 * n2 <= i2 * n1):
                    A[i1]()
                    i1 += 1
                else:
                    Bn[i2]()
                    i2 += 1
        if 'attT' in self.dbg:
            self.dbgf = self.sb("dbgf", [128, 512])
            for hp in range(8):
                for q4 in range(4):
                    S.op('dve', [('attT', hp, 0), ('attT', hp, 1), ('dbg', 'attT')], ['dbgf'], lambda e, hp=hp, q4=q4: e.tensor_copy(self.dbgf[:, 0:512], attT[:, hp, q4 * 512:(q4 + 1) * 512]))
                    S.dma('sp', ['dbgf'], [('dbg', 'attT')], lambda e, hp=hp, q4=q4: e.dma_start(out=self.dbg['attT'][hp * 128:(hp + 1) * 128, q4 * 512:(q4 + 1) * 512], in_=self.dbgf[:, 0:512]))
            self.final_keys.append(('dbg', 'attT'))

    def ssd(self, l):
        self.psum_epoch()
        S, sb, nc = self.S, self.sb, self.nc
        self.phase([self.R_W, self.R_B3])
        if True:
            self.wssm_b = [sb("wssm0", [128, 8, 768], BF16)] * 2
            self.wdt_b = sb("wdtb", [128, 8, 32], BF16)
            self.smallb = sb("smallb", [128, 96])
            self.cwb = sb("cwb", [128, 8, 4, 5])
            self.dt_all = sb("dt_all", [128, 16, 32])
            self.adt_all = sb("adt_all", [128, 16, 32])
            self.acs_all = sb("acs_all", [128, 16, 32])
            self.dst_all = sb("dst_all", [128, 16, 32])
            self.ea_all = sb("ea_all", [128, 16, 32])
            self.lastb = sb("lastb", [128, 8, 32])
            self.cdb = sb("cdb", [128, 8, 32])
            self.a_b = sb("a_b", [128, 32])
            self.uext = [sb("uext%d" % i, [128, 4, 259], BF16) for i in range(2)]
            self.dg = sb("dg", [128, 4, 4, 128], BF16)
            self.xh = sb("xh", [128, 4, 256])
            self.tnh = sb("tnh", [128, 4, 256])
            self.bcb2 = [sb("bcb%d" % i, [128, 2, 256], BF16) for i in range(2)]
            self.bcf = sb("bcf", [128, 256])
            self.xs_tok2 = [sb("xs_tok%d" % i, [128, 2, 256], BF16) for i in range(2)]
            self.xdt2 = [sb("xdt%d" % i, [128, 2, 256], BF16) for i in range(2)]
            self.xw2 = [sb("xw%d" % i, [128, 2, 256], BF16) for i in range(2)]
            self.btok2 = [sb("btok%d" % i, [128, 2, 128], BF16) for i in range(2)]
            self.cb32 = [sb("cb3_%d" % i, [128, 3, 128]) for i in range(2)]
            self.dtmp = [sb("dtmp%d" % i, [128, 3, 128]) for i in range(2)]
            self.mt = [sb("mt%d" % i, [128, 3, 128], BF16) for i in range(2)]
            self.prev_f = sb("prev_f", [128, 256])
            self.prev_b = sb("prev_b", [128, 256], BF16)
            self.tz = sb("tz", [128, 2, 256])
            self.yv = sb("yv", [128, 2, 256])
            self.ss = sb("ss", [128, 2])
            self.rstd = sb("rstd", [128, 2])
            self.nwb = [sb("nwb%d" % i, [128, 256]) for i in range(2)]
            self.xsf = self.xh[:, 0:2, :]
            self.zs = self.tz
            self.h2 = self.yv
            self.yn = self.yv
            self.junk = sb("junk", [128, 256], BF16)
        xk = self.xT_keys()
        S.dma('sp', [], ['smallb'], lambda e: e.dma_start(out=self.smallb[:], in_=self.small[l].broadcast_to([128, 96])))
        S.dma('sp', [], ['cwb'], lambda e: e.dma_start(out=self.cwb[:], in_=self.cw[l].rearrange("p (g c k) -> p g c k", g=8, c=4)))
        S.dma('pool', [], ['wdtb'], lambda e: e.dma_start(out=self.wdt_b[:], in_=self.wdt[l].rearrange("p (k n) -> p k n", k=8)))
        S.op('pool', ['cwb'], ['cwb'], lambda e: e.tensor_scalar(self.cwb[:], self.cwb[:], 0.5, None, ALU.mult))
        dtb = self.smallb[:, 0:32]
        alog = self.smallb[:, 32:64]
        dsk = self.smallb[:, 64:96]
        S.op('act', ['smallb'], ['a_b'], lambda e: e.activation(self.a_b[:], alog, AF.Exp))
        S.op('dve', ['a_b'], ['a_b'], lambda e: e.tensor_scalar(self.a_b[:], self.a_b[:], -1.0, None, ALU.mult))
        psd = self.PA[:, 0:512]
        self.mmseq(['wdtb'] + xk, [('PA', 0)],
                   [(psd[:, tt * 32:(tt + 1) * 32], [(self.xT[:, kc, tt * 128:(tt + 1) * 128], self.wdt_b[:, kc, :]) for kc in range(8)]) for tt in range(16)])
        dtf = self.dt_all[:].rearrange("p t h -> p (t h)")
        psd3 = psd.rearrange("p (t h) -> p t h", t=16)
        S.op('dve', [('PA', 0), 'smallb'], ['dt_all'], lambda e: e.tensor_tensor(self.dt_all[:], psd3, dtb.unsqueeze(1).broadcast_to([128, 16, 32]), ALU.add))
        S.op('act', ['dt_all'], ['dt_all'], lambda e: e.activation(dtf, dtf, AF.Exp))
        S.op('act', ['dt_all'], ['dt_all'], lambda e: e.activation(dtf, dtf, AF.Ln, bias=1.0))
        S.op('dve', ['dt_all', 'a_b'], ['adt_all'], lambda e: e.tensor_tensor(self.adt_all[:], self.dt_all[:], self.a_b[:].unsqueeze(1).broadcast_to([128, 16, 32]), ALU.mult))
        psa = self.PA[:, 512:1024]
        psl = self.PA[:, 1024:1280]
        groups = []
        for c in range(8):
            a0 = self.adt_all[:, 2 * c, :]
            a1 = self.adt_all[:, 2 * c + 1, :]
            groups.append((psa[:, (2 * c) * 32:(2 * c + 1) * 32], [(self.tri_f, a0)]))
            groups.append((psa[:, (2 * c + 1) * 32:(2 * c + 2) * 32], [(self.ones_f, a0), (self.tri_f, a1)]))
        self.mmseq(['adt_all', 'cst'], [('PA', 1)], groups)
        self.mmseq(['adt_all', 'cst'], [('PA', 2)],
                   [(psl[:, c * 32:(c + 1) * 32], [(self.ones_f, self.adt_all[:, 2 * c, :]), (self.ones_f, self.adt_all[:, 2 * c + 1, :])]) for c in range(8)])
        acsf = self.acs_all[:].rearrange("p t h -> p (t h)")
        S.op('dve', [('PA', 1)], ['acs_all'], lambda e: e.tensor_copy(acsf, psa))
        S.op('dve', [('PA', 2)], ['lastb'], lambda e: e.tensor_copy(self.lastb[:].rearrange("p c h -> p (c h)"), psl))
        S.op('dve', ['lastb', 'acs_all'], ['dst_all'], lambda e: e.tensor_tensor(
            self.dst_all[:].rearrange("p (c s) h -> p c s h", c=8), self.lastb[:].unsqueeze(2).broadcast_to([128, 8, 2, 32]),
            self.acs_all[:].rearrange("p (c s) h -> p c s h", c=8), ALU.subtract))
        dstf = self.dst_all[:].rearrange("p t h -> p (t h)")
        S.op('act', ['dst_all'], ['dst_all'], lambda e: e.activation(dstf, dstf, AF.Exp))
        S.op('dve', ['dst_all', 'dt_all'], ['dst_all'], lambda e: e.tensor_tensor(self.dst_all[:], self.dst_all[:], self.dt_all[:], ALU.mult))
        S.op('act', ['acs_all'], ['ea_all'], lambda e: e.activation(self.ea_all[:].rearrange("p t h -> p (t h)"), acsf, AF.Exp))
        S.op('act', ['lastb'], ['cdb'], lambda e: e.activation(self.cdb[:].rearrange("p c h -> p (c h)"), self.lastb[:].rearrange("p c h -> p (c h)"), AF.Exp))
        self.dump('dt_all', self.dt_all[:].rearrange("p t h -> p (t h)"), 'dt_all')
        self.dump('acs_all', acsf, 'acs_all')

        P0 = self.PA[:, 0:512]
        P1 = self.PA[:, 512:1024]
        P2 = self.PA[:, 1024:1536]
        P3 = self.PA[:, 1536:2048]
        P4 = self.PB[:, 0:512]
        P5 = self.PB[:, 512:1024]
        P6 = self.PC[:, 0:512]
        P7 = self.PD[:, 0:512]
        uprev = None
        def load_xbc(g):
            wb = self.wssm_b[0]
            S.dma('pool', [], [('wssm', 'xbc')], lambda e, wb=wb, g=g: e.dma_start(
                out=wb[:, :, 256:768], in_=self.wssm[l, g].rearrange("p (k n) -> p k n", k=8)[:, :, 256:768], max_dma_last_dim=2048))
            nw = self.nwb[g % 2]
            S.dma('sp', [], [('nwb', g % 2)], lambda e, nw=nw, g=g: e.dma_start(out=nw[:], in_=self.normw[l][:, g * 256:(g + 1) * 256].broadcast_to([128, 256])))

        def load_z(g):
            wb = self.wssm_b[0]
            S.dma('pool', [], [('wssm', 'z')], lambda e, wb=wb, g=g: e.dma_start(
                out=wb[:, :, 0:256], in_=self.wssm[l, g].rearrange("p (k n) -> p k n", k=8)[:, :, 0:256], max_dma_last_dim=1024))
        load_xbc(0)
        load_z(0)
        prevB = None
        for g in range(8):
            wb = self.wssm_b[g % 2]
            wk = ('wssm', 0, 0)
            wkall = [('wssm', 'xbc')]
            nw = self.nwb[g % 2]
            nk = ('nwb', g % 2)
            if g > 0:
                load_xbc(g)
            hs = slice(4 * g, 4 * g + 4)

            def build_dg(e, g=g):
                ins = None
                for ci in range(4):
                    for k in range(4):
                        ins = e.tensor_scalar(self.dg[:, ci, k, :], self.ident_b, self.cwb[:, g, ci, k:k + 1], None, ALU.mult)
                return ins
            S.op('dve', ['cwb', 'cbf'], ['dg'], build_dg)
            for c in range(8):
                cols = slice(c * 256, (c + 1) * 256)
                sl = c % 2
                bcb, xs_tok, xw, btok = self.bcb2[sl], self.xs_tok2[sl], self.xw2[sl], self.btok2[sl]
                kbcb, kxs, kxw, kbt = ('bcb', sl), ('xs_tok', sl), ('xw', sl), ('btok', sl)
                xdt, cb3 = self.xdt2[sl], self.cb32[sl]
                kxdt, kcb3 = ('xdt', sl), ('cb3', sl)
                F1, F2, Bq = [], [], []
                self.rec = F1
                xkc = xk[2 * c:2 * c + 2]
                ue = self.uext[c % 2]
                uk = ('uext', c % 2)
                self.mmseq(wkall + xkc, [('PA', 0)],
                           [(P0[:, j * 256:(j + 1) * 256], [(wb[:, kc, 256 + j * 128:256 + (j + 1) * 128], self.xT[:, kc, cols]) for kc in range(8)]) for j in range(2)])
                self.mmseq(wkall + xkc, [('PA', 1)],
                           [(P1[:, j * 256:(j + 1) * 256], [(wb[:, kc, 512 + j * 128:512 + (j + 1) * 128], self.xT[:, kc, cols]) for kc in range(8)]) for j in range(2)])
                if c == 0:
                    S.op('pool', [], [uk], lambda e, bcb=bcb, xs_tok=xs_tok, xw=xw, btok=btok, xdt=xdt, cb3=cb3, ue=ue: e.memset(ue[:, :, 0:3], 0.0))
                else:
                    up = self.uext[(c - 1) % 2]
                    S.op('pool', [('uext', (c - 1) % 2)], [uk], lambda e, bcb=bcb, xs_tok=xs_tok, xw=xw, btok=btok, xdt=xdt, cb3=cb3, ue=ue, up=up: e.tensor_copy(ue[:, :, 0:3], up[:, :, 256:259]))
                S.op('act', [('PA', 0)], [uk], lambda e, bcb=bcb, xs_tok=xs_tok, xw=xw, btok=btok, xdt=xdt, cb3=cb3, ue=ue: e.activation(ue[:, 0:2, 3:259], P0.rearrange("p (j t) -> p j t", j=2), AF.Copy))
                S.op('act', [('PA', 1)], [uk], lambda e, bcb=bcb, xs_tok=xs_tok, xw=xw, btok=btok, xdt=xdt, cb3=cb3, ue=ue: e.activation(ue[:, 2:4, 3:259], P1.rearrange("p (j t) -> p j t", j=2), AF.Copy))
                self.mmseq([uk, 'dg'], [('PA', 0)],
                           [(P0[:, j * 256:(j + 1) * 256], [(self.dg[:, j, k, :], ue[:, j, k:k + 256]) for k in range(4)]) for j in range(2)])
                self.mmseq([uk, 'dg'], [('PA', 1)],
                           [(P1[:, j * 256:(j + 1) * 256], [(self.dg[:, 2 + j, k, :], ue[:, 2 + j, k:k + 256]) for k in range(4)]) for j in range(2)])
                for ci in range(4):
                    w = self.cwb[:, g, ci, :]
                    pc = (P0 if ci < 2 else P1)[:, (ci % 2) * 256:(ci % 2 + 1) * 256]
                    pk = ('PA', 0 if ci < 2 else 1)
                    S.op('act', [pk, 'cwb'], [('xh', ci)], lambda e, ci=ci, w=w, pc=pc: e.activation(self.xh[:, ci, :], pc, AF.Identity, bias=w[:, 4:5]))
                    S.op('act', [pk, 'cwb'], [('tnh', ci)], lambda e, ci=ci, w=w, pc=pc: e.activation(self.tnh[:, ci, :], pc, AF.Tanh, bias=w[:, 4:5]))
                S.op('dve', [('tnh', 0), ('tnh', 1), ('xh', 0), ('xh', 1)], [('xh', 0), ('xh', 1)], lambda e, bcb=bcb, xs_tok=xs_tok, xw=xw, btok=btok, xdt=xdt, cb3=cb3: e.scalar_tensor_tensor(self.xsf[:], in0=self.tnh[:, 0:2, :], scalar=1.0, in1=self.xh[:, 0:2, :], op0=ALU.add, op1=ALU.mult))
                S.op('dve', [('tnh', 2), ('tnh', 3), ('xh', 2), ('xh', 3)], [kbcb], lambda e, bcb=bcb, xs_tok=xs_tok, xw=xw, btok=btok, xdt=xdt, cb3=cb3: e.scalar_tensor_tensor(bcb[:], in0=self.tnh[:, 2:4, :], scalar=1.0, in1=self.xh[:, 2:4, :], op0=ALU.add, op1=ALU.mult))
                S.op('dve', [('tnh', 2), ('xh', 2)], ['bcf'], lambda e, bcb=bcb, xs_tok=xs_tok, xw=xw, btok=btok, xdt=xdt, cb3=cb3: e.scalar_tensor_tensor(self.bcf[:], in0=self.tnh[:, 2, :], scalar=1.0, in1=self.xh[:, 2, :], op0=ALU.add, op1=ALU.mult))
                self.tps([('xh', 0), ('xh', 1), 'cst'], [('PA', 0)],
                         [(P0[:, si * 256 + j * 128:si * 256 + (j + 1) * 128], self.xsf[:, j, si * 128:(si + 1) * 128]) for si in range(2) for j in range(2)], self.ident_f)
                self.tps(['bcf', 'cst'], [('PA', 1)],
                         [(P1[:, si * 128:(si + 1) * 128], self.bcf[:, si * 128:(si + 1) * 128]) for si in range(2)], self.ident_f)
                P0v = P0.rearrange("p (s h d) -> p s h d", s=2, h=4)
                def evac_xs(e, xs_tok=xs_tok, xdt=xdt, xw=xw, c=c, hs=hs):
                    e.tensor_copy(xs_tok[:].rearrange("p s c -> p (s c)"), P0)
                    e.tensor_tensor(xdt[:].rearrange("p s (h d) -> p s h d", h=4), P0v,
                                    self.dt_all[:, 2 * c:2 * c + 2, hs].unsqueeze(3).broadcast_to([128, 2, 4, 64]), ALU.mult)
                    return e.tensor_tensor(xw[:].rearrange("p s (h d) -> p s h d", h=4), P0v,
                                           self.dst_all[:, 2 * c:2 * c + 2, hs].unsqueeze(3).broadcast_to([128, 2, 4, 64]), ALU.mult)
                S.op('dve', [('PA', 0), 'dt_all', 'dst_all'], [kxs, kxdt, kxw], evac_xs)
                S.op('act', [('PA', 1)], [kbt], lambda e, bcb=bcb, xs_tok=xs_tok, xw=xw, btok=btok, xdt=xdt, cb3=cb3: e.activation(btok[:].rearrange("p s n -> p (s n)"), P1[:, 0:256], AF.Copy))
                self.mmseq([kbcb], [('PA', 3)],
                           [(P3[:, 0:256], [(bcb[:, 0, 0:128], bcb[:, 1, :])]),
                            (P3[:, 384:512], [(bcb[:, 0, 128:256], bcb[:, 1, 128:256])])])
                def evac_cb(e, cb3=cb3):
                    e.tensor_tensor(cb3[:, 0:2, :].rearrange("p a b -> p (a b)"), P3[:, 0:256], self.cst[:, C_TRI:C_TRI + 256], ALU.mult)
                    return e.tensor_tensor(cb3[:, 2, :], P3[:, 384:512], self.tri_f, ALU.mult)
                S.op('dve', [('PA', 3), 'cst'], [kcb3], evac_cb)
                self.rec = F2
                def bc_mm(hh):
                    hg = 4 * g + hh
                    pb = (P4 if hh % 2 == 0 else P6)[:, 0:256]
                    pbk = ('PB', 0, 0) if hh % 2 == 0 else 'PC'
                    a0 = self.adt_all[:, 2 * c, hg:hg + 1].broadcast_to([128, 128])
                    a1 = self.adt_all[:, 2 * c + 1, hg:hg + 1].broadcast_to([128, 128])
                    self.mmseq(['adt_all', 'cst'], [pbk],
                               [(pb[:, 0:128], [(a0, self.tri_f)]),
                                (pb[:, 128:256], [(a0, self.ones_f), (a1, self.tri_f)])])
                    return pb, pbk
                nxt = bc_mm(0)
                for hh in range(4):
                    hg = 4 * g + hh
                    pb, pbk = nxt
                    if hh + 1 < 4:
                        nxt = bc_mm(hh + 1)
                    dtm = self.dtmp[hh % 2]
                    dk = ('dtmp', hh % 2)
                    mt = self.mt[hh % 2]
                    mk = ('mt', hh % 2)
                    ac0 = self.acs_all[:, 2 * c, hg:hg + 1]
                    ac1 = self.acs_all[:, 2 * c + 1, hg:hg + 1]

                    def dec(e, dtm=dtm, pb=pb, ac0=ac0, ac1=ac1):
                        e.tensor_scalar(dtm[:, 0:2, :].rearrange("p a b -> p (a b)"), pb[:, 0:256], ac0, 0.0, ALU.subtract, ALU.min)
                        return e.tensor_scalar(dtm[:, 2, :], pb[:, 128:256], ac1, 0.0, ALU.subtract, ALU.min)
                    S.op('dve', [pbk, 'acs_all'], [dk], dec)
                    S.op('act', [dk], [dk], lambda e, dtm=dtm: e.activation(dtm[:].rearrange("p a b -> p (a b)"), dtm[:].rearrange("p a b -> p (a b)"), AF.Exp))
                    S.op('dve', [dk, kcb3], [mk], lambda e, dtm=dtm, mt=mt, cb3=cb3: e.tensor_tensor(mt[:], dtm[:], cb3[:], ALU.mult))
                    self.mmseq([mk, kxdt], [('PB', 1, hh)],
                               [(P5[:, hh * 64:(hh + 1) * 64], [(mt[:, 0, :], xdt[:, 0, hh * 64:(hh + 1) * 64])]),
                                (P5[:, 256 + hh * 64:256 + (hh + 1) * 64], [(mt[:, 1, :], xdt[:, 0, hh * 64:(hh + 1) * 64]), (mt[:, 2, :], xdt[:, 1, hh * 64:(hh + 1) * 64])])])
                self.rec = Bq
                self.mmseq([('wssm', 'z')] + xkc, [('PA', 2)],
                           [(P2[:, si * 256:(si + 1) * 256], [(self.xT[:, kc, c * 256 + si * 128:c * 256 + (si + 1) * 128], wb[:, kc, 0:256]) for kc in range(8)]) for si in range(2)])
                S.op('act', [('PA', 2)], ['tz'], lambda e, bcb=bcb, xs_tok=xs_tok, xw=xw, btok=btok, xdt=xdt, cb3=cb3: e.activation(self.tz[:].rearrange("p s c -> p (s c)"), P2, AF.Tanh, scale=0.5))
                S.op('dve', ['tz', ('PA', 2)], ['tz'], lambda e, bcb=bcb, xs_tok=xs_tok, xw=xw, btok=btok, xdt=xdt, cb3=cb3: e.scalar_tensor_tensor(self.zs[:].rearrange("p s c -> p (s c)"), in0=self.tz[:].rearrange("p s c -> p (s c)"), scalar=1.0, in1=P2, op0=ALU.add, op1=ALU.mult))
                p5k = [('PB', 1, hh) for hh in range(4)]
                if c > 0:
                    self.mmseq([kbcb, 'prev_b'], [('PA', 2)],
                               [(P2[:, si * 256:(si + 1) * 256], [(bcb[:, 1, si * 128:(si + 1) * 128], self.prev_b[:])]) for si in range(2)])
                if c < 7:
                    self.mm([kbt, kxw], [('PB', 0, 0)], P4[:, 0:256], [(btok[:, si, :], xw[:, si, :]) for si in range(2)])
                    if c == 0:
                        S.op('act', [('PB', 0, 0)], ['prev_f'], lambda e, bcb=bcb, xs_tok=xs_tok, xw=xw, btok=btok, xdt=xdt, cb3=cb3: e.activation(self.prev_f[:], P4[:, 0:256], AF.Copy))
                    else:
                        S.op('dve', ['prev_f', 'cdb'], ['prev_f'], lambda e, bcb=bcb, xs_tok=xs_tok, xw=xw, btok=btok, xdt=xdt, cb3=cb3, c=c, hs=hs: e.tensor_tensor(
                            self.prev_f[:].rearrange("p (h d) -> p h d", h=4), self.prev_f[:].rearrange("p (h d) -> p h d", h=4),
                            self.cdb[:, c, hs].unsqueeze(2).broadcast_to([128, 4, 64]), ALU.mult))
                        S.op('dve', ['prev_f', ('PB', 0, 0)], ['prev_f'], lambda e, bcb=bcb, xs_tok=xs_tok, xw=xw, btok=btok, xdt=xdt, cb3=cb3: e.tensor_tensor(self.prev_f[:], self.prev_f[:], P4[:, 0:256], ALU.add))
                    S.op('act', ['prev_f'], ['prev_b'], lambda e, bcb=bcb, xs_tok=xs_tok, xw=xw, btok=btok, xdt=xdt, cb3=cb3: e.activation(self.prev_b[:], self.prev_f[:], AF.Copy))
                S.op('pool', [kxs, 'smallb'], ['yv'], lambda e, bcb=bcb, xs_tok=xs_tok, xw=xw, btok=btok, xdt=xdt, cb3=cb3, hs=hs: e.tensor_tensor(
                    self.yv[:].rearrange("p s (h d) -> p s h d", h=4), xs_tok[:].rearrange("p s (h d) -> p s h d", h=4),
                    dsk[:, hs].unsqueeze(1).unsqueeze(3).broadcast_to([128, 2, 4, 64]), ALU.mult))
                S.op('dve', p5k + ['yv'], ['yv'], lambda e, bcb=bcb, xs_tok=xs_tok, xw=xw, btok=btok, xdt=xdt, cb3=cb3: e.tensor_tensor(self.yv[:].rearrange("p s c -> p (s c)"), P5, self.yv[:].rearrange("p s c -> p (s c)"), ALU.add))
                if c > 0:
                    def yoff(e, c=c, g=g):
                        ins = None
                        for si in range(2):
                            for hh in range(4):
                                o = self.yv[:, si, hh * 64:(hh + 1) * 64]
                                ins = e.scalar_tensor_tensor(o, in0=P2[:, si * 256 + hh * 64:si * 256 + (hh + 1) * 64],
                                                             scalar=self.ea_all[:, 2 * c + si, 4 * g + hh:4 * g + hh + 1], in1=o, op0=ALU.mult, op1=ALU.add)
                        return ins
                    S.op('dve', [('PA', 2), 'ea_all', 'yv'], ['yv'], yoff)
                S.op('dve', ['yv', 'tz'], ['yv'], lambda e, bcb=bcb, xs_tok=xs_tok, xw=xw, btok=btok, xdt=xdt, cb3=cb3: e.tensor_tensor(self.h2[:], self.yv[:], self.zs[:], ALU.mult))
                for si in range(2):
                    S.op('act', ['yv'], ['junk', ('ss', si)], lambda e, bcb=bcb, xs_tok=xs_tok, xw=xw, btok=btok, xdt=xdt, cb3=cb3, si=si: e.activation(self.junk[:], self.h2[:, si, :], AF.Square, accum_out=self.ss[:, si:si + 1]))
                S.op('dve', [('ss', 0), ('ss', 1)], ['rstd'], lambda e, bcb=bcb, xs_tok=xs_tok, xw=xw, btok=btok, xdt=xdt, cb3=cb3: e.tensor_scalar(self.rstd[:], self.ss[:], 1.0 / 256, 4 * RMS_EPS, ALU.mult, ALU.add))
                S.op('pool', ['rstd', 'nh'], ['rstd'], lambda e, bcb=bcb, xs_tok=xs_tok, xw=xw, btok=btok, xdt=xdt, cb3=cb3: e.tensor_tensor(self.rstd[:], self.rstd[:], self.nh[:, 0:2], ALU.pow))
                for si in range(2):
                    S.op('dve', ['yv', 'rstd', nk], ['yv'], lambda e, bcb=bcb, xs_tok=xs_tok, xw=xw, btok=btok, xdt=xdt, cb3=cb3, si=si, nw=nw: e.scalar_tensor_tensor(
                        self.yn[:, si, :], in0=self.h2[:, si, :], scalar=self.rstd[:, si:si + 1], in1=nw[:], op0=ALU.mult, op1=ALU.mult))
                self.tps(['yv', 'cst'], ['PD'],
                         [(P7[:, j * 256 + si * 128:j * 256 + (si + 1) * 128], self.yn[:, si, j * 128:(j + 1) * 128]) for j in range(2) for si in range(2)], self.ident_f)
                S.op('act', ['PD'], [('yT', g, c)], lambda e, bcb=bcb, xs_tok=xs_tok, xw=xw, btok=btok, xdt=xdt, cb3=cb3, g=g, cols=cols: e.activation(self.yT[:, 2 * g:2 * g + 2, cols], P7.rearrange("p (j t) -> p j t", j=2), AF.Copy))
                self.rec = None
                X = F2 + Bq
                if prevB is None:
                    for t in F1:
                        t()
                else:
                    n1, n2 = len(F1), len(prevB)
                    i1 = i2 = 0
                    while i1 < n1 or i2 < n2:
                        if i2 >= n2 or (i1 < n1 and i1 * n2 <= i2 * n1):
                            F1[i1]()
                            i1 += 1
                        else:
                            prevB[i2]()
                            i2 += 1
                prevB = X
                if c == 0 and g > 0:
                    load_z(g)
        for t in prevB:
            t()
        if 'yT' in self.dbg:
            if True:
                self.dbgf = self.sb("dbgf", [128, 512])
            for kc in range(16):
                for q4 in range(4):
                    S.op('dve', [('yT', g, c) for g in range(8) for c in range(8)] + [('dbg', 'yT')], ['dbgf'], lambda e, kc=kc, q4=q4: e.tensor_copy(self.dbgf[:, 0:512], self.yT[:, kc, q4 * 512:(q4 + 1) * 512]))
                    S.dma('sp', ['dbgf'], [('dbg', 'yT')], lambda e, kc=kc, q4=q4: e.dma_start(out=self.dbg['yT'][kc * 128:(kc + 1) * 128, q4 * 512:(q4 + 1) * 512], in_=self.dbgf[:, 0:512]))
            self.final_keys.append(('dbg', 'yT'))

    def merge(self, l):
        self.psum_epoch()
        S, sb = self.S, self.sb
        self.phase([self.R_W])
        if True:
            self.wc1_b = [sb("wc1b%d" % i, [128, 5120], BF16) for i in range(2)]
            self.ta = [sb("ta%d" % i, [128, 512]) for i in range(2)]
            self.tb = [sb("tb%d" % i, [128, 512]) for i in range(2)]
        mT = self.B3
        xk = self.xT_keys()
        attk = [('attT', hp, h) for hp in range(8) for h in range(2)]
        yk = [('yT', g, c) for g in range(8) for c in range(8)]
        it = 0
        def load_c(fc):
            wb = self.wc1_b[fc % 2]
            S.dma('pool', [], [('wc1', fc % 2, 0)], lambda e, wb=wb, fc=fc: e.dma_start(
                out=wb[:].rearrange("p (a b) -> p a b", a=5), in_=self.wc1[l, fc].rearrange("p (a b) -> p a b", a=5), max_dma_last_dim=4096))
        load_c(0)
        for fc in range(8):
            wb = self.wc1_b[fc % 2]
            wks = [('wc1', fc % 2, 0)]
            if fc + 1 < 8:
                load_c(fc + 1)
            wg = wb[:, 0:2048].rearrange("p (k n) -> p k n", k=8)
            wa = wb[:, 2048:3072].rearrange("p (k n) -> p k n", k=8)
            ws = wb[:, 3072:5120].rearrange("p (k n) -> p k n", k=16)
            for tc in range(4):
                cols = slice(tc * 512, (tc + 1) * 512)
                if it % 2 == 0:
                    pga, pgs, ppa, pps = (self.PA[:, i * 512:(i + 1) * 512] for i in range(4))
                    bk = [('PA', 0), ('PA', 1), ('PA', 2), ('PA', 3)]
                else:
                    pga, pgs, ppa, pps = self.PB[:, 0:512], self.PB[:, 512:1024], self.PC[:, 0:512], self.PD[:, 0:512]
                    bk = [('PB', 0), ('PB', 1), 'PC', 'PD']
                self.mm(wks + xk[tc * 4:(tc + 1) * 4], [bk[0]], pga, [(wg[:, kc, 0:128], self.xT[:, kc, cols]) for kc in range(8)])
                self.mm(wks + xk[tc * 4:(tc + 1) * 4], [bk[1]], pgs, [(wg[:, kc, 128:256], self.xT[:, kc, cols]) for kc in range(8)])
                self.mm(wks + attk, [bk[2]], ppa, [(wa[:, kc, :], self.B1[:, kc, cols]) for kc in range(8)])
                self.mm(wks + yk, [bk[3]], pps, [(ws[:, kc, :], self.yT[:, kc, cols]) for kc in range(16)])
                ta, tb = self.ta[it % 2], self.tb[it % 2]
                tak, tbk = ('ta', it % 2), ('tb', it % 2)
                it += 1
                S.op('act', [bk[0]], [tak], lambda e, ta=ta, pga=pga: e.activation(ta[:], pga, AF.Tanh, scale=0.5))
                S.op('act', [bk[1]], [tbk], lambda e, tb=tb, pgs=pgs: e.activation(tb[:], pgs, AF.Tanh, scale=0.5))
                S.op('dve', [tak, bk[2]], [tak], lambda e, ta=ta, ppa=ppa: e.scalar_tensor_tensor(ta[:], in0=ta[:], scalar=1.0, in1=ppa, op0=ALU.add, op1=ALU.mult))
                S.op('dve', [tbk, bk[3]], [tbk], lambda e, tb=tb, pps=pps: e.scalar_tensor_tensor(tb[:], in0=tb[:], scalar=1.0, in1=pps, op0=ALU.add, op1=ALU.mult))
                S.op('pool', [tak, tbk], [('mergedT', fc, tc)], lambda e, ta=ta, tb=tb, fc=fc, cols=cols: e.tensor_tensor(mT[:, fc, cols], ta[:], tb[:], ALU.add))

    def layernorm(self, t, tk, g_b, b_b, gk, i):
        S = self.S
        st = self.ln_st[i % 2]
        mv = self.ln_mv[i % 2]
        sk = ('lnst', i % 2)

        def stats(e):
            e.bn_stats(st[:, 0, :], t[:, 0:512])
            return e.bn_stats(st[:, 1, :], t[:, 512:1024])
        S.op('dve', [tk], [sk], stats)
        S.op('dve', [sk], [sk], lambda e: e.bn_aggr(mv[:, 0:2], st[:].rearrange("p a b -> p (a b)")))
        S.op('dve', [sk], [sk], lambda e: e.tensor_scalar(mv[:, 2:3], mv[:, 1:2], LN_EPS, None, ALU.add))
        S.op('pool', [sk, 'nh'], [sk], lambda e: e.tensor_tensor(mv[:, 3:4], mv[:, 2:3], self.nh[:, 0:1], ALU.pow))
        S.op('dve', [tk, sk], [tk], lambda e: e.tensor_scalar(t[:], t[:], mv[:, 0:1], mv[:, 3:4], ALU.subtract, ALU.mult))
        S.op('pool', [tk, gk], [tk], lambda e: e.tensor_tensor(t[:], t[:], g_b, ALU.mult))
        S.op('dve', [tk, gk], [tk], lambda e: e.tensor_tensor(t[:], t[:], b_b, ALU.add))

    def mix_ln1(self, l):
        self.psum_epoch()
        S, sb = self.S, self.sb
        self.phase([self.R_W, self.R_B2])
        if True:
            self.wout_b = sb("woutb", [128, 8, 1024], BF16)
            self.lnpb = sb("lnpb", [128, 4096])
            self.res = [sb("res%d" % i, [128, D]) for i in range(2)]
            self.ln_st = [sb("lnst%d" % i, [128, 2, 6]) for i in range(2)]
            self.ln_mv = [sb("lnmv%d" % i, [128, 4]) for i in range(2)]
        S.dma('pool', [], [('wout', 0)], lambda e: e.dma_start(
            out=self.wout_b[:], in_=self.wout[l].rearrange("p (k n) -> p k n", k=8), max_dma_last_dim=4096))
        lnpb1 = self.lnpb
        S.dma('sp', [], ['lnpb'], lambda e: e.dma_start(out=lnpb1[:], in_=self.lnp[l].broadcast_to([128, 4096])))
        src = self.x_in if l == 0 else self.x1
        srck = 'x_in' if l == 0 else 'x1'
        mk = [('mergedT', fc, tc) for fc in range(8) for tc in range(4)]
        wk = [('wout', 0)]
        for tt in range(16):
            r = self.res[tt % 2]
            rk = ('res', tt % 2)
            S.dma('sp', [(srck, tt)], [rk], lambda e, r=r, tt=tt: e.dma_start(out=r[:], in_=src[tt * 128:(tt + 1) * 128, :]))
            S.op('act', [rk], [rk], lambda e, r=r: e.activation(r[:], r[:], AF.Copy, scale=float(ALPHA)))
            if tt % 2 == 0:
                pm = self.PB[:, 0:1024]
                rkeys = [('PB', 0), ('PB', 1)]
            else:
                pm = self.PA[:, 0:1024]
                rkeys = [('PA', 0), ('PA', 1)]
            self.mmseq(mk + wk, rkeys,
                       [(pm[:, half * 512:(half + 1) * 512], [(self.B3[:, kc, tt * 128:(tt + 1) * 128], self.wout_b[:, kc, half * 512:(half + 1) * 512]) for kc in range(8)]) for half in range(2)])
            S.op('dve', rkeys + [rk], [rk], lambda e, r=r, pm=pm: e.scalar_tensor_tensor(r[:], in0=pm, scalar=0.5, in1=r[:], op0=ALU.mult, op1=ALU.add))
            self.layernorm(r, rk, self.lnpb[:, 0:1024], self.lnpb[:, 1024:2048], 'lnpb', tt)
            S.dma('sp', [rk], [('hres', tt)], lambda e, r=r, tt=tt: e.dma_start(out=self.hres[tt * 128:(tt + 1) * 128, :], in_=r[:]))
            self.to_featmajor2(r, rk, self.B1, 'hT', tt)
        if 'hT' in self.dbg:
            if True:
                self.dbgf = self.sb("dbgf", [128, 512])
            for kc in range(8):
                for q4 in range(4):
                    S.op('dve', [('hT', tt) for tt in range(16)] + [('dbg', 'hT')], ['dbgf'], lambda e, kc=kc, q4=q4: e.tensor_copy(self.dbgf[:, 0:512], self.B1[:, kc, q4 * 512:(q4 + 1) * 512]))
                    S.dma('sp', ['dbgf'], [('dbg', 'hT')], lambda e, kc=kc, q4=q4: e.dma_start(out=self.dbg['hT'][kc * 128:(kc + 1) * 128, q4 * 512:(q4 + 1) * 512], in_=self.dbgf[:, 0:512]))
            self.final_keys.append(('dbg', 'hT'))

    def to_featmajor2(self, tile, tkey, dstT, dkey, tt):
        S = self.S
        for half in range(2):
            ps = self.PC[:, 0:512] if half == 0 else self.PD[:, 0:512]
            pk = 'PC' if half == 0 else 'PD'
            self.tps([tkey, 'cst'], [pk], [(ps[:, j * 128:(j + 1) * 128], tile[:, (half * 4 + j) * 128:(half * 4 + j + 1) * 128]) for j in range(4)], self.ident_f)
            S.op('act', [pk], [(dkey, tt)], lambda e, ps=ps, half=half: e.activation(
                dstT[:, half * 4:(half + 1) * 4, tt * 128:(tt + 1) * 128], ps.rearrange("p (c t) -> p c t", c=4), AF.Copy))

    def ffn(self, l):
        self.psum_epoch()
        S, sb = self.S, self.sb
        self.phase([self.R_W, (self.R_B3[0] + 4096, self.R_B3[1])])
        if True:
            self.wup_b = [sb("wupb%d" % i, [128, 8, 512], BF16) for i in range(2)]
            self.wdn_b = [sb("wdnb%d" % i, [128, 4, 1024], BF16) for i in range(2)]
            self.rl = [sb("rl%d" % i, [128, 512]) for i in range(2)]
            self.ln_st = [sb("lnst%d" % i, [128, 2, 6]) for i in range(2)]
            self.ln_mv = [sb("lnmv%d" % i, [128, 4]) for i in range(2)]
        uT = self.B3
        hk = [('hT', tt) for tt in range(16)]
        it = 0
        def load_f(gi):
            wu = self.wup_b[gi % 2]
            wd = self.wdn_b[gi % 2]
            wuk, wdk = ('wup', gi % 2), ('wdn', gi % 2)
            S.dma('pool', [], [(wuk, 0)], lambda e, wu=wu, gi=gi: e.dma_start(
                out=wu[:], in_=self.wup[l, gi].rearrange("p (k n) -> p k n", k=8), max_dma_last_dim=4096))
            S.dma('pool', [], [(wdk, 0)], lambda e, wd=wd, gi=gi: e.dma_start(
                out=wd[:], in_=self.wdn[l, gi].rearrange("p (k n) -> p k n", k=4), max_dma_last_dim=4096))
        load_f(0)
        for gi in range(8):
            wu = self.wup_b[gi % 2]
            wd = self.wdn_b[gi % 2]
            wuk, wdk = ('wup', gi % 2), ('wdn', gi % 2)
            if gi + 1 < 8:
                load_f(gi + 1)
            wuks = [(wuk, 0)]
            wdks = [(wdk, 0)]
            for j in range(4):
                for tc in range(4):
                    bank = it % 4
                    ps = [self.PA[:, 0:512], self.PA[:, 512:1024], self.PC[:, 0:512], self.PD[:, 0:512]][bank]
                    pk = [('PA', 0), ('PA', 1), 'PC', 'PD'][bank]
                    cols = slice(tc * 512, (tc + 1) * 512)
                    self.mm(wuks + hk[tc * 4:(tc + 1) * 4], [pk], ps, [(wu[:, kc, j * 128:(j + 1) * 128], self.B1[:, kc, cols]) for kc in range(8)])
                    rl = self.rl[it % 2]
                    rlk = ('rl', it % 2)
                    S.op('act', [pk], [rlk], lambda e, rl=rl, ps=ps: e.activation(rl[:], ps, AF.Relu))
                    eng = 'dve'
                    S.op(eng, [rlk], [('uT', j)], lambda e, rl=rl, j=j, cols=cols: e.tensor_tensor(uT[:, j, cols], rl[:], rl[:], ALU.mult))
                    it += 1
            for tt in range(16):
                if tt % 2 == 0:
                    pm = self.PB[:, 0:1024]
                    rkeys = [('PB', 0), ('PB', 1)]
                else:
                    pm = self.PA[:, 1024:2048]
                    rkeys = [('PA', 2), ('PA', 3)]
                self.mmseq([('uT', j) for j in range(4)] + wdks, rkeys,
                           [(pm[:, half * 512:(half + 1) * 512], [(uT[:, j, tt * 128:(tt + 1) * 128], wd[:, j, half * 512:(half + 1) * 512]) for j in range(4)]) for half in range(2)])
                if gi == 0:
                    S.op('act', rkeys, [('acc', tt)], lambda e, tt=tt, pm=pm: e.activation(self.acc[:, tt, :], pm, AF.Copy))
                else:
                    S.op('dve', rkeys + [('acc', tt)], [('acc', tt)], lambda e, tt=tt, pm=pm: e.tensor_tensor(self.acc[:, tt, :], pm, self.acc[:, tt, :], ALU.add))
        S.barrier()
        self.res = [self.wup_b[i][:, 0:4, :].rearrange("p a b -> p (a b)").bitcast(F32) for i in range(2)]
        self.lnpb = self.wdn_b[0][:].rearrange("p a b -> p (a b)").bitcast(F32)
        lnpb2 = self.lnpb
        S.dma('sp', [], ['lnpb'], lambda e: e.dma_start(out=lnpb2, in_=self.lnp[l][:, 2048:4096].broadcast_to([128, 2048])))
        dst = self.out if l == DEPTH - 1 else self.x1
        dstk = 'out' if l == DEPTH - 1 else 'x1'
        for tt in range(16):
            r = self.res[tt % 2]
            rk = ('res', tt % 2)
            S.dma('sp', [('hres', tt)], [rk], lambda e, r=r, tt=tt: e.dma_start(out=r[:], in_=self.hres[tt * 128:(tt + 1) * 128, :]))
            S.op('dve', [rk, ('acc', tt)], [rk], lambda e, r=r, tt=tt: e.scalar_tensor_tensor(r[:], in0=r[:], scalar=float(ALPHA), in1=self.acc[:, tt, :], op0=ALU.mult, op1=ALU.add))
            self.layernorm(r, rk, self.lnpb[:, 0:1024], self.lnpb[:, 1024:2048], 'lnpb', tt)
            S.dma('sp', [rk], [(dstk, tt)], lambda e, r=r, tt=tt: e.dma_start(out=dst[tt * 128:(tt + 1) * 128, :], in_=r[:]))
            if l == DEPTH - 1:
                self.final_keys.append((dstk, tt))
            else:
                self.to_featmajor2(r, rk, self.xT, 'xT', tt)


def sel_max(e, b):
    ins = None
    for t in range(8):
        ins = e.max(b.m8[:, t, :], b.gm[:, t * 8:(t + 1) * 8])
    return ins


def sel_lt(e, b):
    ins = None
    for t in range(8):
        ins = e.tensor_scalar(b.lt[:, t * 8:(t + 1) * 8], b.gm[:, t * 8:(t + 1) * 8], b.m8[:, t, 2:3], NEG, ALU.is_lt, ALU.mult)
    return ins


_CACHE = {}


def kernel(**inputs):
    inp = {k: np.asarray(v) for k, v in inputs.items()}
    w = prep_weights(inp)
    consts, e8 = make_consts()
    x = np.ascontiguousarray(inp['x'], dtype=np.float32)
    pos = np.ascontiguousarray(inp['positions'], dtype=np.int32)
    nc = Builder().build()
    in_maps = []
    for b in range(NCORES):
        m = {"x": x[b], "pos": pos[b:b + 1], "consts": consts, "e8": e8}
        m.update(w)
        in_maps.append(m)
    res = run_bass_kernel_spmd(nc, in_maps, core_ids=list(range(NCORES)))
    out = np.stack([np.asarray(r["out"], dtype=np.float32) for r in res.results], axis=0)
    return out
```

```python
import math
import numpy as np
import concourse.bass as bass
import concourse.mybir as mybir
from concourse.bass_utils import run_bass_kernel_spmd

F32 = mybir.dt.float32
BF16 = mybir.dt.bfloat16
I32 = mybir.dt.int32
AF = mybir.ActivationFunctionType
ALU = mybir.AluOpType
AX = mybir.AxisListType

D = 1024
SEQ = 2048
DEPTH = 2
NCORES = 8
IN_W = 11296
ALPHA = (2 * DEPTH) ** 0.25
LN_EPS = 1e-5
RMS_EPS = 1e-5
NEG = -1.0e5
PI = math.pi

DEBUG = {}
STOP_AFTER = None
NO_BARRIER = False
VARIANT = 0


class Sched:
    ENG = ('pe', 'act', 'dve', 'pool', 'sp')

    def __init__(self, nc, n_dma_sems=24):
        self.nc = nc
        self.sem = {e: nc.alloc_semaphore('s_' + e) for e in self.ENG}
        self.cnt = {e: 0 for e in self.ENG}
        self.dsem = [nc.alloc_semaphore('d%d' % i) for i in range(n_dma_sems)]
        self.dcnt = [0] * n_dma_sems
        self.drr = {'sp': 0, 'pool': 0}
        self.dpool = {'sp': list(range(0, n_dma_sems // 2)), 'pool': list(range(n_dma_sems // 2, n_dma_sems))}
        self.waited = {}
        self.last_w = {}
        self.readers = {}
        self.q = {e: [] for e in self.ENG}

    def barrier(self):
        if NO_BARRIER is True:
            return
        self.epoch = {}
        scr = self.bar_scratch
        d_sp = {('dma', i): self.dcnt[i] for i in self.dpool['sp'] if self.dcnt[i] > 0}
        d_pl = {('dma', i): self.dcnt[i] for i in self.dpool['pool'] if self.dcnt[i] > 0}
        self._token('act', d_sp, lambda e: e.memzero(scr[:, 0:1]))
        t1 = ('act', self.cnt['act'])
        d_pl[t1[0]] = t1[1]
        self._token('pool', d_pl, lambda e: e.memset(scr[:, 1:2], 0.0))
        d3 = {e: c for e, c in self.cnt.items() if c > 0}
        self._token('dve', d3, lambda e: e.memset(scr[:, 2:3], 0.0))
        self.epoch = {'dve': self.cnt['dve']}

    def _token(self, e, deps, emit):
        waits = self._waits(e, deps)
        self.cnt[e] += 1
        sem = self.sem[e]

        def thunk(eng):
            for s, v in waits:
                eng.wait_ge(s, v)
            emit(eng).then_inc(sem, 1)
        self.q[e].append(thunk)

    def _deps(self, reads, writes):
        deps = dict(getattr(self, 'epoch', {}))

        def add(src, idx):
            if deps.get(src, 0) < idx:
                deps[src] = idx
        for k in reads:
            lw = self.last_w.get(k)
            if lw is not None:
                add(*lw)
        for k in writes:
            lw = self.last_w.get(k)
            if lw is not None:
                add(*lw)
            for src, idx in self.readers.get(k, {}).items():
                add(src, idx)
        return deps

    def _waits(self, e, deps):
        out = []
        self.maxw = getattr(self, 'maxw', {})
        for src, idx in deps.items():
            if self.waited.get((e, src), 0) >= idx:
                continue
            self.waited[(e, src)] = idx
            if isinstance(src, tuple):
                out.append((self.dsem[src[1]], 16 * idx))
            else:
                out.append((self.sem[src], idx))
        self.maxw[len(out)] = self.maxw.get(len(out), 0) + 1
        return out

    def _record(self, token, reads, writes):
        src, idx = token
        for k in reads:
            r = self.readers.setdefault(k, {})
            if r.get(src, 0) < idx:
                r[src] = idx
        for k in writes:
            self.last_w[k] = (src, idx)
            self.readers[k] = {}

    @staticmethod
    def _bank(k):
        if isinstance(k, tuple) and k and k[0] == 'PA':
            return ('BANK', k[1])
        if isinstance(k, tuple) and k and k[0] == 'PB':
            return ('BANK', 4 + k[1])
        if k in ('PC', 'PCg', 'PCo'):
            return ('BANK', 6)
        if k == 'PD':
            return ('BANK', 7)
        return None

    def _canon(self, reads, writes):
        r2, w2 = [], []
        for k in reads:
            b = self._bank(k)
            if b is None:
                r2.append(k)
            elif b not in w2:
                w2.append(b)
        for k in writes:
            b = self._bank(k)
            if b is None:
                w2.append(k)
            elif b not in w2:
                w2.append(b)
        return r2, w2

    def op(self, e, reads, writes, emit):
        reads, writes = self._canon(reads, writes)
        deps = self._deps(reads, writes)
        waits = self._waits(e, deps)
        self.cnt[e] += 1
        idx = self.cnt[e]
        sem = self.sem[e]

        def thunk(eng):
            for s, v in waits:
                eng.wait_ge(s, v)
            emit(eng).then_inc(sem, 1)
        self.q[e].append(thunk)
        self._record((e, idx), reads, writes)

    def dma(self, e, reads, writes, emit):
        pool = self.dpool[e]
        s = pool[self.drr[e] % len(pool)]
        self.drr[e] += 1
        src = ('dma', s)
        deps = self._deps(reads, writes)
        if self.dcnt[s] > 0 and deps.get(src, 0) < self.dcnt[s]:
            deps[src] = self.dcnt[s]
        waits = self._waits(e, deps)
        self.dcnt[s] += 1
        idx = self.dcnt[s]
        sem = self.dsem[s]

        def thunk(eng):
            for sm, v in waits:
                eng.wait_ge(sm, v)
            emit(eng).then_inc(sem, 16)
        self.q[e].append(thunk)
        self._record((src, idx), reads, writes)

    def alias(self, old_keys, new_keys):
        acc = {}
        for k in old_keys:
            lw = self.last_w.get(k)
            if lw is not None and acc.get(lw[0], 0) < lw[1]:
                acc[lw[0]] = lw[1]
            for src, idx in self.readers.get(k, {}).items():
                if acc.get(src, 0) < idx:
                    acc[src] = idx
        for k in new_keys:
            r = self.readers.setdefault(k, {})
            for src, idx in acc.items():
                if r.get(src, 0) < idx:
                    r[src] = idx

    def finish(self, final_keys):
        self.barrier()
        deps = self._deps(final_keys, [])
        waits = self._waits('sp', deps)

        def thunk(eng):
            for s, v in waits:
                eng.wait_ge(s, v)
        self.q['sp'].append(thunk)
        nc = self.nc
        q = self.q
        with nc.Block() as block:
            @block.sync
            def _(eng):
                for t in q['sp']:
                    t(eng)

            @block.tensor
            def _(eng):
                for t in q['pe']:
                    t(eng)

            @block.scalar
            def _(eng):
                for t in q['act']:
                    t(eng)

            @block.vector
            def _(eng):
                for t in q['dve']:
                    t(eng)

            @block.gpsimd
            def _(eng):
                for t in q['pool']:
                    t(eng)


C_ID, C_TRI, C_ONES, C_RM, C_INVF, C_SGN, C_NEGP, C_FLOOR, NC_CONST = 0, 128, 256, 384, 512, 513, 514, 578, 648


def make_consts():
    c = np.zeros((128, NC_CONST), np.float32)
    c[:, C_ID:C_ID + 128] = np.eye(128, dtype=np.float32)
    c[:, C_TRI:C_TRI + 128] = np.triu(np.ones((128, 128), np.float32))
    c[:, C_ONES:C_ONES + 128] = 1.0
    rm = np.zeros((128, 128), np.float32)
    inv = (500000.0 ** (-np.arange(0, 16, 2, dtype=np.float32) / 16)).astype(np.float32)
    for blk in range(2):
        for d in range(16):
            src = d + 8 if d < 8 else d - 8
            rm[blk * 64 + src, blk * 64 + d] = 1.0
    c[:, C_RM:C_RM + 128] = rm
    for p in range(128):
        d = p % 64
        if d < 16:
            c[p, C_INVF] = inv[d % 8]
            c[p, C_SGN] = -1.0 if d < 8 else 1.0
    for qt in range(8):
        blk = (8 + qt) // 2
        for j in range(8):
            c[:, C_NEGP + qt * 8 + j] = 0.0 if j < blk else -1.0e30
            c[:, C_FLOOR + qt * 8 + j] = NEG if j < blk else 0.0
    e8 = np.zeros((8, SEQ), np.float32)
    for j in range(8):
        e8[j, j * 256:(j + 1) * 256] = 1.0
    return c, e8


def prep_weights(inp):
    out = {}
    L = DEPTH

    def kmaj(w):
        K, N = w.shape
        return w.reshape(K // 128, 128, N).transpose(1, 0, 2)

    w_in = inp['w_in']
    wqkv = np.empty((L, 8, 128, 8, 384), np.float32)
    wssm = np.empty((L, 8, 128, 8, 768), np.float32)
    wdt = np.empty((L, 128, 8, 32), np.float32)
    wc1 = np.empty((L, 8, 128, 5120), np.float32)
    wout = np.empty((L, 128, 8, 1024), np.float32)
    wup = np.empty((L, 8, 128, 8, 512), np.float32)
    wdn = np.empty((L, 8, 128, 4, 1024), np.float32)
    cw = np.empty((L, 128, 8, 4, 5), np.float32)
    for l in range(L):
        wi = kmaj(w_in[l])
        for hp in range(8):
            wqkv[l, hp, :, :, 0:128] = wi[:, :, hp * 128:(hp + 1) * 128]
            wqkv[l, hp, :, :, 128:256] = wi[:, :, 1024 + hp * 128:1024 + (hp + 1) * 128]
            wqkv[l, hp, :, :, 256:384] = wi[:, :, 2048 + hp * 128:2048 + (hp + 1) * 128]
        for g in range(8):
            wssm[l, g, :, :, 0:256] = wi[:, :, 3072 + g * 256:3072 + (g + 1) * 256]
            wssm[l, g, :, :, 256:512] = wi[:, :, 5120 + g * 256:5120 + (g + 1) * 256]
            wssm[l, g, :, :, 512:640] = wi[:, :, 7168 + g * 128:7168 + (g + 1) * 128]
            wssm[l, g, :, :, 640:768] = wi[:, :, 8192 + g * 128:8192 + (g + 1) * 128]
        wdt[l] = wi[:, :, 9216:9248]
        wap = kmaj(inp['w_attn_proj'][l])
        wsp = kmaj(inp['w_ssm_proj'][l])
        for fc in range(8):
            blk = np.empty((128, 8, 256), np.float32)
            blk[:, :, 0:128] = wi[:, :, 9248 + fc * 128:9248 + (fc + 1) * 128]
            blk[:, :, 128:256] = wi[:, :, 10272 + fc * 128:10272 + (fc + 1) * 128]
            wc1[l, fc, :, 0:2048] = blk.reshape(128, 2048)
            wc1[l, fc, :, 2048:3072] = wap[:, :, fc * 128:(fc + 1) * 128].reshape(128, 1024)
            wc1[l, fc, :, 3072:5120] = wsp[:, :, fc * 128:(fc + 1) * 128].reshape(128, 2048)
        wout[l] = kmaj(inp['w_out'][l])
        wu = kmaj(inp['w_up'][l])
        wd = kmaj(inp['w_down'][l])
        for gI in range(8):
            wup[l, gI] = wu[:, :, gI * 512:(gI + 1) * 512]
            wdn[l, gI] = wd[:, gI * 4:(gI + 1) * 4, :]
        cwl = inp['conv_w'][l]
        cbl = inp['conv_b'][l]
        for g in range(8):
            offs = [g * 256, g * 256 + 128, 2048 + g * 128, 3072 + g * 128]
            for ci, o in enumerate(offs):
                cw[l, :, g, ci, 0:4] = cwl[:, o:o + 128].T
                cw[l, :, g, ci, 4] = cbl[o:o + 128]
    out['wqkv'] = wqkv.reshape(L, 8, 128, 3072)
    out['wssm'] = wssm.reshape(L, 8, 128, 6144)
    out['wdt'] = wdt.reshape(L, 128, 256)
    out['wc1'] = wc1
    out['wout'] = wout.reshape(L, 128, 8192)
    out['wup'] = wup.reshape(L, 8, 128, 4096)
    out['wdn'] = wdn.reshape(L, 8, 128, 4096)
    out['cw'] = cw.reshape(L, 128, 160)
    small = np.concatenate([inp['dt_bias'], inp['a_log'], inp['d_skip']], axis=1)
    out['small'] = np.ascontiguousarray(small.reshape(L, 1, 96))
    out['normw'] = np.ascontiguousarray(inp['ssm_norm_w'].reshape(L, 1, 2048))
    lnp = np.stack([inp['ln1_g'], inp['ln1_b'], inp['ln2_g'], inp['ln2_b']], axis=1)
    out['lnp'] = np.ascontiguousarray(lnp.reshape(L, 1, 4096))
    return {k: np.ascontiguousarray(v, dtype=np.float32) for k, v in out.items()}


class Arena:
    def __init__(self, base_ap, segments):
        self.base = base_ap
        self.segs = [list(x) for x in segments]

    def alloc(self, shape, dtype=F32):
        P = shape[0]
        n = 1
        for d in shape[1:]:
            n *= d
        per = 4 if dtype in (F32, I32) else 2
        ncols = (n * per + 3) // 4
        ncols = (ncols + 7) // 8 * 8
        for sg in self.segs:
            if sg[1] - sg[0] >= ncols:
                off = sg[0]
                sg[0] += ncols
                ap = self.base[0:P, off:off + (n * per + 3) // 4]
                if dtype != F32:
                    ap = ap.bitcast(dtype)
                if len(shape) == 3:
                    ap = ap.rearrange("p (a b) -> p a b", a=shape[1])
                elif len(shape) == 4:
                    ap = ap.rearrange("p (a b c) -> p a b c", a=shape[1], b=shape[2])
                return ap
        raise RuntimeError("arena out of memory for %s" % (shape,))


class _Stop(Exception):
    pass


class Builder:
    def __init__(self):
        nc = bass.Bass("TRN2", target_bir_lowering=False)
        self.nc = nc
        self.S = Sched(nc)
        self.rec = None
        S_ = self.S
        S_._op, S_._dma = S_.op, S_.dma

        def _rop(*a):
            if self.rec is not None:
                self.rec.append(lambda: S_._op(*a))
            else:
                S_._op(*a)

        def _rdma(*a):
            if self.rec is not None:
                self.rec.append(lambda: S_._dma(*a))
            else:
                S_._dma(*a)
        S_.op, S_.dma = _rop, _rdma
        L = DEPTH
        dt = lambda n, s, d=F32, k="ExternalInput": nc.dram_tensor(n, s, d, kind=k).ap()
        self.x_in = dt("x", [SEQ, D])
        self.pos_in = dt("pos", [1, SEQ], I32)
        self.consts_in = dt("consts", [128, NC_CONST])
        self.e8_in = dt("e8", [8, SEQ])
        self.wqkv = dt("wqkv", [L, 8, 128, 3072])
        self.wssm = dt("wssm", [L, 8, 128, 6144])
        self.wdt = dt("wdt", [L, 128, 256])
        self.wc1 = dt("wc1", [L, 8, 128, 5120])
        self.wout = dt("wout", [L, 128, 8192])
        self.wup = dt("wup", [L, 8, 128, 4096])
        self.wdn = dt("wdn", [L, 8, 128, 4096])
        self.cw = dt("cw", [L, 128, 160])
        self.small = dt("small", [L, 1, 96])
        self.normw = dt("normw", [L, 1, 2048])
        self.lnp = dt("lnp", [L, 1, 4096])
        self.out = dt("out", [SEQ, D], F32, "ExternalOutput")
        self.x1 = dt("x1s", [SEQ, D], F32, "Internal")
        self.hres = dt("hres", [SEQ, D], F32, "Internal")
        self.dbg = {}
        for name, shape in DEBUG.items():
            self.dbg[name] = dt("dbg_" + name, list(shape), F32, "ExternalOutput")
        self.final_keys = []

    def sb(self, name, shape, dtype=F32):
        return self.ar.alloc(list(shape), dtype)

    def phase(self, segs):
        self.S.barrier()
        self.ar = Arena(self.arena, segs)

    def mm(self, reads, writes, out, pairs, transpose=False):
        def emit(e):
            n = len(pairs)
            ins = None
            for i, (l, r) in enumerate(pairs):
                ins = e.matmul(out, lhsT=l, rhs=r, start=(i == 0), stop=(i == n - 1))
            return ins
        self.S.op('pe', reads, writes, emit)

    def mmseq(self, reads, writes, groups):
        def emit(e):
            ins = None
            for out, pairs in groups:
                n = len(pairs)
                for i, (l, r) in enumerate(pairs):
                    ins = e.matmul(out, lhsT=l, rhs=r, start=(i == 0), stop=(i == n - 1))
            return ins
        self.S.op('pe', reads, writes, emit)

    def tps(self, reads, writes, items, ident):
        def emit(e):
            ins = None
            for o, i in items:
                ins = e.transpose(o, i, ident)
            return ins
        self.S.op('pe', reads, writes, emit)

    def dump(self, name, ap, keys):
        if name in self.dbg:
            if not isinstance(keys, list):
                keys = [keys]
            self.S.dma('sp', keys, [('dbg', name)], lambda e: e.dma_start(out=self.dbg[name], in_=ap))
            self.final_keys.append(('dbg', name))

    def build(self):
        nc, S = self.nc, self.S
        sb = self.sb
        nbytes = int(nc.sbuf_bytes_remaining) - 256
        NA = nbytes // 4 // 8 * 8
        self.arena = nc.alloc_sbuf_tensor("arena", [128, NA], F32)[:]
        self.R_P = (0, 11264)
        self.R_B1 = (11264, 19456)
        self.R_B2 = (19456, 35840)
        self.R_B3 = (35840, 44032)
        self.R_W = (44032, NA)
        assert NA - 44032 > 8000, NA
        self.ar = Arena(self.arena, [self.R_P])
        self.S.bar_scratch = sb("barscr", [128, 8])
        cst = sb("cst", [128, NC_CONST])
        self.cst = cst
        S.dma('sp', [], ['cst'], lambda e: e.dma_start(out=cst[:], in_=self.consts_in))
        self.ident_f = cst[:, C_ID:C_ID + 128]
        self.tri_f = cst[:, C_TRI:C_TRI + 128]
        self.ones_f = cst[:, C_ONES:C_ONES + 128]
        cbf = sb("cbf", [128, 512], BF16)
        S.op('dve', ['cst'], ['cbf'], lambda e: e.tensor_copy(cbf[:], cst[:, 0:512]))
        self.ident_b = cbf[:, 0:128]
        self.tri_b = cbf[:, 128:256]
        self.rm_b = cbf[:, 384:512]
        self.nh = sb("nh", [128, 8])
        S.op('pool', [], ['nh'], lambda e: e.memset(self.nh[:], -0.5))
        self.ropeC = sb("ropeC", [128, SEQ], BF16)
        self.ropeS = sb("ropeS", [128, SEQ], BF16)
        self.xT = sb("xT", [128, 8, SEQ], BF16)
        A = self.arena
        self.B1 = A[:, self.R_B1[0]:self.R_B1[1]].bitcast(BF16).rearrange("p (c t) -> p c t", c=8)
        self.B2 = A[:, self.R_B2[0]:self.R_B2[1]]
        self.B3 = A[:, self.R_B3[0]:self.R_B3[1]].bitcast(BF16).rearrange("p (c t) -> p c t", c=8)
        self.yT = self.B2.bitcast(BF16).rearrange("p (c t) -> p c t", c=16)
        self.acc = self.B2.rearrange("p (t f) -> p t f", t=16)
        self.ar = Arena(self.arena, [self.R_B2])
        self.PA = nc.alloc_psum_tensor("PA", [128, 2048], F32)
        self.PB = nc.alloc_psum_tensor("PB", [128, 1024], F32)
        self.PC = nc.alloc_psum_tensor("PC", [128, 512], F32)
        self.PD = nc.alloc_psum_tensor("PD", [128, 512], F32)
        self.rope_tables()
        if STOP_AFTER == 'rope':
            return self.finish()
        self.load_xT()
        if STOP_AFTER == 'xT':
            self.dbg_tile('xT0', self.xT[:, 3, 1024:1536], self.xT_keys(), 512)
            return self.finish()
        try:
            for l in range(DEPTH):
                self.layer(l)
                if STOP_AFTER == 'layer0':
                    break
        except _Stop:
            pass
        return self.finish()

    def finish(self):
        self.S.finish(self.final_keys)
        return self.nc

    def rope_tables(self):
        S, sb = self.S, self.sb
        cst = None
        posi = sb("posi", [128, SEQ], I32)
        S.dma('sp', [], ['posi'], lambda e: e.dma_start(out=posi[:], in_=self.pos_in.broadcast_to([128, SEQ])))
        ang = sb("ang", [128, SEQ])
        tmp = sb("rtmp", [128, SEQ])
        tmi = sb("rtmi", [128, SEQ], I32)
        rc = sb("rc", [128, SEQ])
        rs = sb("rs", [128, SEQ])
        invf = self.cst[:, C_INVF:C_INVF + 1]
        sgn = self.cst[:, C_SGN:C_SGN + 1]
        S.op('dve', ['posi'], ['ang'], lambda e: e.tensor_copy(ang[:], posi[:]))
        S.op('dve', ['ang', 'cst'], ['ang'], lambda e: e.tensor_scalar(ang[:], ang[:], invf, None, ALU.mult))
        for which, dst in ((0, rs), (1, rc)):
            off = 0.0 if which == 0 else PI / 2
            S.op('dve', ['ang'], ['rtmp'], lambda e, off=off: e.tensor_scalar(tmp[:], ang[:], off, 1.0 / (2 * PI), ALU.add, ALU.mult))
            S.op('dve', ['rtmp'], ['rtmi'], lambda e: e.tensor_copy(tmi[:], tmp[:]))
            S.op('dve', ['rtmi'], ['rtmp'], lambda e: e.tensor_copy(tmp[:], tmi[:]))
            S.op('dve', ['rtmp', 'ang'], ['rtmp'], lambda e: e.scalar_tensor_tensor(tmp[:], in0=tmp[:], scalar=-2 * PI, in1=ang[:], op0=ALU.mult, op1=ALU.add))
            S.op('dve', ['rtmp'], [('rope', which)], lambda e, off=off, dst=dst: e.tensor_scalar(dst[:], tmp[:], off, None, ALU.add))
            S.op('dve', [('rope', which)], ['rtmp'], lambda e, dst=dst: e.tensor_scalar(tmp[:], dst[:], PI, -2 * PI, ALU.is_gt, ALU.mult))
            S.op('dve', ['rtmp', ('rope', which)], [('rope', which)], lambda e, dst=dst: e.tensor_tensor(dst[:], dst[:], tmp[:], ALU.add))
            S.op('dve', [('rope', which)], ['rtmp'], lambda e, dst=dst: e.tensor_scalar(tmp[:], dst[:], -PI, 2 * PI, ALU.is_lt, ALU.mult))
            S.op('dve', ['rtmp', ('rope', which)], [('rope', which)], lambda e, dst=dst: e.tensor_tensor(dst[:], dst[:], tmp[:], ALU.add))
            S.op('dve', [('rope', which)], [('rope', which)], lambda e, dst=dst: e.tensor_scalar(dst[:], dst[:], 3.1415925, -3.1415925, ALU.min, ALU.max))
            S.op('act', [('rope', which)], [('rope', which)], lambda e, dst=dst: e.activation(dst[:], dst[:], AF.Sin))
        S.op('dve', [('rope', 0), 'cst'], [('rope', 0)], lambda e: e.tensor_scalar(self.ropeS[:], rs[:], sgn, None, ALU.mult))
        S.op('dve', [('rope', 1)], [('rope', 1)], lambda e: e.tensor_copy(self.ropeC[:], rc[:]))
        self.dump('ropeC', rc[:], ('rope', 1))
        self.dump('ropeS', rs[:], ('rope', 0))

    def load_xT(self):
        S, sb = self.S, self.sb
        self.xt_buf = [sb("xtile%d" % i, [128, D]) for i in range(2)]
        for tt in range(16):
            xt = self.xt_buf[tt % 2]
            k = ('xtile', tt % 2)
            S.dma('sp', [], [k], lambda e, xt=xt, tt=tt: e.dma_start(out=xt[:], in_=self.x_in[tt * 128:(tt + 1) * 128, :]))
            self.to_featmajor(xt, k, self.xT, 'xT', tt)

    def to_featmajor(self, tile, tkey, dstT, dkey, tt):
        S = self.S
        for half in range(2):
            ps = self.PA[:, half * 512:(half + 1) * 512]
            pk = ('PA', half)
            self.tps([tkey, 'cst'], [pk], [(ps[:, j * 128:(j + 1) * 128], tile[:, (half * 4 + j) * 128:(half * 4 + j + 1) * 128]) for j in range(4)], self.ident_f)
            eng = 'act' if half == 0 else 'dve'
            if eng == 'act':
                S.op('act', [pk], [(dkey, tt)], lambda e, ps=ps, half=half: e.activation(
                    dstT[:, half * 4:(half + 1) * 4, tt * 128:(tt + 1) * 128], ps.rearrange("p (c t) -> p c t", c=4), AF.Copy))
            else:
                S.op('dve', [pk], [(dkey, tt)], lambda e, ps=ps, half=half: e.tensor_copy(
                    dstT[:, half * 4:(half + 1) * 4, tt * 128:(tt + 1) * 128], ps.rearrange("p (c t) -> p c t", c=4)))

    def layer(self, l):
        self.attention(l)
        if STOP_AFTER == 'att':
            raise _Stop()
        self.ssd(l)
        if STOP_AFTER == 'ssd':
            raise _Stop()
        self.merge(l)
        if STOP_AFTER == 'merge':
            raise _Stop()
        self.mix_ln1(l)
        if STOP_AFTER == 'ln1':
            raise _Stop()
        self.ffn(l)

    def psum_epoch(self):
        keys = [('PA', i) for i in range(4)] + [('PB', 0), ('PB', 1), ('PB', 0, 0), ('PB', 0, 1)] + \
               [('PB', 1, h) for h in range(4)] + ['PC', 'PCg', 'PCo', 'PD']
        self.S.alias(keys, keys)

    def dbg_qk(self, h):
        S = self.S
        if 'qa0' in self.dbg:
            qa, ka = self.qa[h], self.ka[h]
            dq = self.sb("dbgq", [128, SEQ])
            dk = self.sb("dbgk", [128, SEQ])
            S.op('dve', [('qa', h)], ['dbgq'], lambda e: e.tensor_copy(dq[0:72, :], qa[0:72, :]))
            S.op('dve', [('ka', h)], ['dbgk'], lambda e: e.tensor_copy(dk[0:72, :], ka[0:72, :]))
            self.dump('qa0', dq[0:72, :], 'dbgq')
            self.dump('ka0', dk[0:72, :], 'dbgk')

    def dbg_tile(self, name, ap, keys, ncols):
        if name in self.dbg:
            t = self.sb("dbg_" + name, [128, ncols])
            self.S.op('dve', keys, ['dbgt_' + name], lambda e: e.tensor_copy(t[:], ap))
            self.dump(name, t[:], 'dbgt_' + name)

    def xT_keys(self):
        return [('xT', tt) for tt in range(16)]

    def attention(self, l):
        self.psum_epoch()
        S, sb, nc = self.S, self.sb, self.nc
        self.phase([self.R_W, self.R_B2, (self.R_B3[0] + 7168, self.R_B3[1])])
        if True:
            self.wqkv_b = [sb("wqkv%d" % i, [128, 8, 384], BF16) for i in range(2)]
            self.qa = [sb("qa%d" % i, [72, SEQ], BF16) for i in range(4)]
            self.ka = [sb("ka%d" % i, [72, SEQ], BF16) for i in range(4)]
            self.va = [sb("va%d" % i, [128, 16, 65], BF16) for i in range(4)]
            self.qbf = [sb("qbf%d" % i, [128, 512], BF16) for i in range(2)]
            self.t1 = [sb("t1_%d" % i, [128, 512]) for i in range(2)]
            self.t2 = [sb("t2_%d" % i, [128, 512]) for i in range(2)]
            self.atm = [sb("atm%d" % i, [128, 16, 128], BF16) for i in range(2)]
            self.km = sb("km", [64, 8])
            self.kmh = sb("kmh", [64, 8], BF16)
            self.kml = sb("kml", [64, 8], BF16)
            self.gm = sb("gm", [128, 64])
            self.m8 = sb("m8", [128, 8, 8])
            self.lt = sb("lt", [128, 64])
            self.bpad = sb("bpad", [128, 8, 72], BF16)
            self.rden = sb("rden", [128, 4])
            self.NPT = 28
            for i in range(4):
                S.op('pool', [], [('qa', i)], lambda e, i=i: e.memset(self.qa[i][64:72, :], 0.0))
                S.dma('pool', [], [('ka', i)], lambda e, i=i: e.dma_start(out=self.ka[i][64:72, :], in_=self.e8_in))
                S.op('pool', [], [('va', i)], lambda e, i=i: e.memset(self.va[i][:, :, 64:65], 1.0))
            S.op('pool', [], ['bpad'], lambda e: e.memset(self.bpad[:], 0.0))
            if STOP_AFTER == 'att_init':
                self.dbg_qk(0)
                raise _Stop()
        PT = [self.B3[:, i // 4, (i % 4) * 512:(i % 4 + 1) * 512] for i in range(self.NPT)]
        attT = self.B1
        negp = self.cst[:, C_NEGP:C_NEGP + 64]
        floorb = self.cst[:, C_FLOOR:C_FLOOR + 64]
        pt_rr = [0]
        PDb = self.PD[:].bitcast(BF16)
        xk = self.xT_keys()

        def load_w(hp):
            wb = self.wqkv_b[hp % 2]
            S.dma('pool', [], [('wqkv', hp % 2, 0)], lambda e, wb=wb, hp=hp: e.dma_start(
                out=wb[:], in_=self.wqkv[l, hp].rearrange("p (k n) -> p k n", k=8), max_dma_last_dim=4096))
        load_w(0)
        Wl = [[] for _ in range(9)]
        Pl = [[] for _ in range(9)]
        Cl = [[] for _ in range(9)]
        for hp in range(8):
            wb = self.wqkv_b[hp % 2]
            wk = ('wqkv', hp % 2, 0)
            wk1 = wk2 = wk3 = wk
            self.rec = Wl[hp + 1]
            if STOP_AFTER == 'att_w0':
                self.dbg_tile('wb0', wb[:].rearrange("p a b -> p (a b)"), [wk, wk1, wk2, wk3], 3072)
                raise _Stop()
            if hp + 1 < 8:
                load_w(hp + 1)
            if STOP_AFTER == 'att_w1':
                self.dbg_tile('wb0', wb[:].rearrange("p a b -> p (a b)"), [wk, wk1, wk2, wk3], 3072)
                raise _Stop()
            par = (hp % 2) * 2
            hA, hB = par, par + 1
            self.rec = Pl[hp]
            cnt = 0
            for which in (1, 0):
                dst = self.ka if which == 1 else self.qa
                dn = 'ka' if which == 1 else 'qa'
                for tc in range(4):
                    bank = cnt % 4
                    cnt += 1
                    ps = self.PA[:, bank * 512:(bank + 1) * 512]
                    pk = ('PA', bank)
                    cols = slice(tc * 512, (tc + 1) * 512)
                    def stopat(ch, ap=None, keys=None):
                        if STOP_AFTER == 'att_qk1' + ch:
                            if ap is not None:
                                self.dbg_tile('probe', ap, keys, 512)
                            raise _Stop()
                    self.mm([wk, wk1, wk2, wk3] + xk[tc * 4:(tc + 1) * 4], [pk], ps,
                            [(wb[:, kc, which * 128:(which + 1) * 128], self.xT[:, kc, cols]) for kc in range(8)])
                    stopat('a', ps, [pk])
                    qb = self.qbf[cnt % 2]
                    qk_ = ('qbf', cnt % 2)
                    if VARIANT == 99:
                        S.op('act', [pk], [qk_], lambda e, qb=qb, ps=ps: e.activation(qb[:], ps, AF.Copy))
                    stopat('b', qb[:], [qk_])
                    t1 = self.t1[cnt % 2]
                    t2 = self.t2[cnt % 2]
                    k1 = ('t1', cnt % 2)
                    k2 = ('t2', cnt % 2)
                    if VARIANT == 4:
                        t1 = self.sb("t1x", [128, 512])
                    if VARIANT in (5, 8):
                        S.op('dve', [('rope', 1)], [k1], lambda e, t1=t1, ps=ps, cols=cols: e.tensor_copy(t1[:], self.ropeC[:, cols]))
                    elif VARIANT in (1, 4):
                        S.op('dve', [pk, ('rope', 1)], [k1], lambda e, t1=t1, ps=ps, cols=cols: e.tensor_copy(t1[:], ps))
                    elif VARIANT == 2:
                        S.op('dve', [pk, ('rope', 1)], [k1], lambda e, t1=t1, ps=ps, cols=cols: e.tensor_copy(t1[:], self.ropeC[:, cols]))
                    elif VARIANT == 3:
                        S.op('dve', [pk, ('rope', 1)], [k1], lambda e, t1=t1, ps=ps, cols=cols: e.tensor_tensor(t1[:], ps, self.cst[:, 0:512], ALU.mult))
                    else:
                        def evac_qk(e, qb=qb, t1=t1, ps=ps, cols=cols):
                            e.tensor_copy(qb[:], ps)
                            return e.tensor_tensor(t1[:], ps, self.ropeC[:, cols], ALU.mult)
                        S.op('dve', [pk, ('rope', 1)], [qk_, k1], evac_qk)
                    stopat('c', qb[:] if VARIANT == 7 else t1[:], [qk_, k1] if VARIANT in (7, 8) else [k1])
                    rb = (bank + 2) % 4
                    pr = self.PA[:, rb * 512:(rb + 1) * 512]
                    prk = ('PA', rb)
                    self.mm([qk_, 'cbf'], [prk], pr, [(self.rm_b, qb[:])])
                    stopat('d', pr, [prk])
                    S.op('dve', [prk, ('rope', 0)], [k2], lambda e, t2=t2, pr=pr, cols=cols: e.tensor_tensor(t2[:], pr, self.ropeS[:, cols], ALU.mult))
                    stopat('e', t2[:], [k2])
                    S.op('dve', [k1, k2], [(dn, hA)], lambda e, t1=t1, t2=t2, dst=dst, cols=cols, hA=hA: e.tensor_tensor(dst[hA][0:64, cols], t1[0:64, :], t2[0:64, :], ALU.add))
                    stopat('f', dst[hA][0:64, cols], [(dn, hA)])
                    S.op('dve', [k1, k2], [(dn, hB)], lambda e, t1=t1, t2=t2, dst=dst, cols=cols, hB=hB: e.tensor_tensor(dst[hB][0:64, cols], t1[64:128, :], t2[64:128, :], ALU.add))
                    stopat('g', dst[hB][0:64, cols], [(dn, hB)])
            if STOP_AFTER == 'att_qk':
                raise _Stop()
            for tq in range(4):
                bank = cnt % 4
                cnt += 1
                ps = self.PA[:, bank * 512:(bank + 1) * 512]
                pk = ('PA', bank)
                self.mmseq([wk, wk1, wk2, wk3] + xk[tq * 4:(tq + 1) * 4], [pk],
                           [(ps[:, j * 128:(j + 1) * 128],
                             [(self.xT[:, kc, (tq * 4 + j) * 128:(tq * 4 + j + 1) * 128], wb[:, kc, 256:384]) for kc in range(8)])
                            for j in range(4)])
                psv = ps.rearrange("p (t c) -> p t c", t=4)
                if hp == 0 and tq == 0:
                    self.dbg_tile('psv', ps, [pk], 512)
                    self.dbg_tile('wb0', wb[:].rearrange("p a b -> p (a b)"), [wk, wk1, wk2, wk3], 3072)
                S.op('act', [pk], [('va', hA)], lambda e, psv=psv, tq=tq, hA=hA: e.activation(self.va[hA][:, tq * 4:(tq + 1) * 4, 0:64], psv[:, :, 0:64], AF.Copy))
                S.op('act', [pk], [('va', hB)], lambda e, psv=psv, tq=tq, hB=hB: e.activation(self.va[hB][:, tq * 4:(tq + 1) * 4, 0:64], psv[:, :, 64:128], AF.Copy))
            if STOP_AFTER == 'att_proj':
                self.dbg_tile('va0', self.va[0][:].rearrange("p a b -> p (a b)"), [('va', 0)], 1040)
                self.dbg_qk(hA)
                raise _Stop()
            self.rec = Cl[hp]
            atm = self.atm[hp % 2]
            ak = ('atm', hp % 2)
            for hh, hbuf in ((0, hA), (1, hB)):
                qa, ka, va = self.qa[hbuf], self.ka[hbuf], self.va[hbuf]
                qk, kk, vk = ('qa', hbuf), ('ka', hbuf), ('va', hbuf)
                S.op('dve', [kk], ['km'], lambda e, ka=ka: e.tensor_reduce(self.km[:], ka[0:64, :].rearrange("p (b t) -> p b t", b=8), AX.X, ALU.add))
                S.op('dve', ['km'], ['kmh'], lambda e: e.tensor_scalar(self.kmh[:], self.km[:], 1.0 / 256, None, ALU.mult))
                S.op('dve', ['km', 'kmh'], ['kml'], lambda e: e.scalar_tensor_tensor(self.kml[:], in0=self.km[:], scalar=1.0 / 256, in1=self.kmh[:], op0=ALU.mult, op1=ALU.subtract))
                pg = self.PC[:, 448:512]
                self.mmseq([qk, 'kmh', 'kml'], ['PCg'],
                           [(pg[:, t * 8:(t + 1) * 8], [(qa[0:64, (8 + t) * 128:(9 + t) * 128], self.kmh[:]), (qa[0:64, (8 + t) * 128:(9 + t) * 128], self.kml[:])]) for t in range(8)])
                S.op('dve', ['PCg', 'cst'], ['gm'], lambda e, pg=pg: e.tensor_tensor(self.gm[:], pg, negp, ALU.add))

                S.op('dve', ['gm'], ['m8'], lambda e: sel_max(e, self))
                S.op('dve', ['gm', 'm8'], ['lt'], lambda e: sel_lt(e, self))
                S.op('dve', ['lt', 'cst'], ['bpad'], lambda e: e.tensor_tensor(self.bpad[:, :, 64:72], self.lt[:].rearrange("p (t j) -> p t j", t=8), floorb.rearrange("p (t j) -> p t j", t=8), ALU.max))
                self.tps(['bpad', 'cbf'], ['PD'], [(PDb[0:72, t * 128:(t + 1) * 128], self.bpad[:, t, :]) for t in range(8)], self.ident_b)
                S.op('act', ['PD'], [qk], lambda e, qa=qa: e.activation(qa[64:72, 1024:2048], PDb[64:72, :], AF.Copy))
                if STOP_AFTER == 'att_gate':
                    self.dbg_qk(hA)
                    raise _Stop()
                slots = {}

                def stageA(qc):
                    sl = []
                    for kt in range(4 * qc + 4):
                        c0 = max(0, kt * 128 - qc * 512)
                        bank = kt % 2
                        ps = self.PB[:, bank * 512:(bank + 1) * 512]
                        pk = ('PB', bank)
                        self.mm([kk, qk], [pk], ps[:, c0:512], [(ka[0:72, kt * 128:(kt + 1) * 128], qa[0:72, qc * 512 + c0:(qc + 1) * 512])])
                        si = pt_rr[0]
                        pt_rr[0] = (pt_rr[0] + 1) % self.NPT
                        sl.append(si)
                        pt = PT[si]
                        S.op('act', [pk], [('PT', si)], lambda e, pt=pt, ps=ps, c0=c0: e.activation(pt[:, c0:512], ps[:, c0:512], AF.Exp, scale=0.125))
                        if kt >= 4 * qc:
                            S.op('dve', [('PT', si), 'cbf'], [('PT', si)], lambda e, pt=pt, c0=c0: e.tensor_tensor(pt[:, c0:c0 + 128], pt[:, c0:c0 + 128], self.tri_b, ALU.mult))
                    slots[qc] = sl

                def stageB(qc):
                    sl = slots[qc]
                    groups = []
                    for qi in range(4):
                        qt = 4 * qc + qi
                        groups.append((self.PC[:, qi * 65:(qi + 1) * 65],
                                       [(PT[sl[kt]][:, qi * 128:(qi + 1) * 128], va[:, kt, :]) for kt in range(qt + 1)]))
                    self.mmseq([('PT', s_) for s_ in sl] + [vk], ['PCo'], groups)
                    pv = self.PC[:, 0:260].rearrange("p (q c) -> p q c", q=4)
                    S.op('dve', ['PCo'], ['rden'], lambda e, pv=pv: e.reciprocal(self.rden[:], pv[:, :, 64]))

                    def norm(e, atm=atm, hh=hh, qc=qc):
                        ins = None
                        for qi in range(4):
                            ins = e.tensor_scalar(atm[:, 4 * qc + qi, hh * 64:(hh + 1) * 64], self.PC[:, qi * 65:qi * 65 + 64], self.rden[:, qi:qi + 1], None, ALU.mult)
                        return ins
                    S.op('dve', ['PCo', 'rden'], [ak], norm)

                stageA(0)
                for qc in range(4):
                    if qc + 1 < 4:
                        stageA(qc + 1)
                    stageB(qc)
            if hp == 0:
                self.dbg_tile('va0', self.va[0][:].rearrange("p a b -> p (a b)"), [('va', 0)], 1040)
                self.dbg_tile('atm0', atm[:].rearrange("p a b -> p (a b)"), [ak], 2048)
            for half in range(2):
                self.tps([ak, 'cbf'], ['PD'], [(PDb[:, t * 128:(t + 1) * 128], atm[:, half * 8 + t, :]) for t in range(8)], self.ident_b)
                S.op('act', ['PD'], [('attT', hp, half)], lambda e, half=half, hp=hp: e.activation(attT[:, hp, half * 1024:(half + 1) * 1024], PDb, AF.Copy))
            self.rec = None
        for t in Pl[0]:
            t()
        for hp in range(8):
            for t in Wl[hp + 1]:
                t()
            A, Bn = Cl[hp], Pl[hp + 1]
            n1, n2 = len(A), len(Bn)
            i1 = i2 = 0
            while i1 < n1 or i2 < n2:
                if i2 >= n2 or (i1 < n1 and i1 * n2 <= i2 * n1):
                    A[i1]()
                    i1 += 1
                else:
                    Bn[i2]()
                    i2 += 1
        if 'attT' in self.dbg:
            self.dbgf = self.sb("dbgf", [128, 512])
            for hp in range(8):
                for q4 in range(4):
                    S.op('dve', [('attT', hp, 0), ('attT', hp, 1), ('dbg', 'attT')], ['dbgf'], lambda e, hp=hp, q4=q4: e.tensor_copy(self.dbgf[:, 0:512], attT[:, hp, q4 * 512:(q4 + 1) * 512]))
                    S.dma('sp', ['dbgf'], [('dbg', 'attT')], lambda e, hp=hp, q4=q4: e.dma_start(out=self.dbg['attT'][hp * 128:(hp + 1) * 128, q4 * 512:(q4 + 1) * 512], in_=self.dbgf[:, 0:512]))
            self.final_keys.append(('dbg', 'attT'))

    def ssd(self, l):
        self.psum_epoch()
        S, sb, nc = self.S, self.sb, self.nc
        self.phase([self.R_W, self.R_B3])
        if True:
            self.wssm_b = [sb("wssm0", [128, 8, 768], BF16)] * 2
            self.wdt_b = sb("wdtb", [128, 8, 32], BF16)
            self.smallb = sb("smallb", [128, 96])
            self.cwb = sb("cwb", [128, 8, 4, 5])
            self.dt_all = sb("dt_all", [128, 16, 32])
            self.adt_all = sb("adt_all", [128, 16, 32])
            self.acs_all = sb("acs_all", [128, 16, 32])
            self.dst_all = sb("dst_all", [128, 16, 32])
            self.ea_all = sb("ea_all", [128, 16, 32])
            self.lastb = sb("lastb", [128, 8, 32])
            self.cdb = sb("cdb", [128, 8, 32])
            self.a_b = sb("a_b", [128, 32])
            self.uext = [sb("uext%d" % i, [128, 4, 259], BF16) for i in range(2)]
            self.dg = sb("dg", [128, 4, 4, 128], BF16)
            self.xh = sb("xh", [128, 4, 256])
            self.tnh = sb("tnh", [128, 4, 256])
            self.bcb2 = [sb("bcb%d" % i, [128, 2, 256], BF16) for i in range(2)]
            self.bcf = sb("bcf", [128, 256])
            self.xs_tok2 = [sb("xs_tok%d" % i, [128, 2, 256], BF16) for i in range(2)]
            self.xdt2 = [sb("xdt%d" % i, [128, 2, 256], BF16) for i in range(2)]
            self.xw2 = [sb("xw%d" % i, [128, 2, 256], BF16) for i in range(2)]
            self.btok2 = [sb("btok%d" % i, [128, 2, 128], BF16) for i in range(2)]
            self.cb32 = [sb("cb3_%d" % i, [128, 3, 128]) for i in range(2)]
            self.dtmp = [sb("dtmp%d" % i, [128, 3, 128]) for i in range(2)]
            self.mt = [sb("mt%d" % i, [128, 3, 128], BF16) for i in range(2)]
            self.prev_f = sb("prev_f", [128, 256])
            self.prev_b = sb("prev_b", [128, 256], BF16)
            self.tz = sb("tz", [128, 2, 256])
            self.yv = sb("yv", [128, 2, 256])
            self.ss = sb("ss", [128, 2])
            self.rstd = sb("rstd", [128, 2])
            self.nwb = [sb("nwb%d" % i, [128, 256]) for i in range(2)]
            self.xsf = self.xh[:, 0:2, :]
            self.zs = self.tz
            self.h2 = self.yv
            self.yn = self.yv
            self.junk = sb("junk", [128, 256], BF16)
        xk = self.xT_keys()
        S.dma('sp', [], ['smallb'], lambda e: e.dma_start(out=self.smallb[:], in_=self.small[l].broadcast_to([128, 96])))
        S.dma('sp', [], ['cwb'], lambda e: e.dma_start(out=self.cwb[:], in_=self.cw[l].rearrange("p (g c k) -> p g c k", g=8, c=4)))
        S.dma('pool', [], ['wdtb'], lambda e: e.dma_start(out=self.wdt_b[:], in_=self.wdt[l].rearrange("p (k n) -> p k n", k=8)))
        S.op('pool', ['cwb'], ['cwb'], lambda e: e.tensor_scalar(self.cwb[:], self.cwb[:], 0.5, None, ALU.mult))
        dtb = self.smallb[:, 0:32]
        alog = self.smallb[:, 32:64]
        dsk = self.smallb[:, 64:96]
        S.op('act', ['smallb'], ['a_b'], lambda e: e.activation(self.a_b[:], alog, AF.Exp))
        S.op('dve', ['a_b'], ['a_b'], lambda e: e.tensor_scalar(self.a_b[:], self.a_b[:], -1.0, None, ALU.mult))
        psd = self.PA[:, 0:512]
        self.mmseq(['wdtb'] + xk, [('PA', 0)],
                   [(psd[:, tt * 32:(tt + 1) * 32], [(self.xT[:, kc, tt * 128:(tt + 1) * 128], self.wdt_b[:, kc, :]) for kc in range(8)]) for tt in range(16)])
        dtf = self.dt_all[:].rearrange("p t h -> p (t h)")
        psd3 = psd.rearrange("p (t h) -> p t h", t=16)
        S.op('dve', [('PA', 0), 'smallb'], ['dt_all'], lambda e: e.tensor_tensor(self.dt_all[:], psd3, dtb.unsqueeze(1).broadcast_to([128, 16, 32]), ALU.add))
        S.op('act', ['dt_all'], ['dt_all'], lambda e: e.activation(dtf, dtf, AF.Exp))
        S.op('act', ['dt_all'], ['dt_all'], lambda e: e.activation(dtf, dtf, AF.Ln, bias=1.0))
        S.op('dve', ['dt_all', 'a_b'], ['adt_all'], lambda e: e.tensor_tensor(self.adt_all[:], self.dt_all[:], self.a_b[:].unsqueeze(1).broadcast_to([128, 16, 32]), ALU.mult))
        psa = self.PA[:, 512:1024]
        psl = self.PA[:, 1024:1280]
        groups = []
        for c in range(8):
            a0 = self.adt_all[:, 2 * c, :]
            a1 = self.adt_all[:, 2 * c + 1, :]
            groups.append((psa[:, (2 * c) * 32:(2 * c + 1) * 32], [(self.tri_f, a0)]))
            groups.append((psa[:, (2 * c + 1) * 32:(2 * c + 2) * 32], [(self.ones_f, a0), (self.tri_f, a1)]))
        self.mmseq(['adt_all', 'cst'], [('PA', 1)], groups)
        self.mmseq(['adt_all', 'cst'], [('PA', 2)],
                   [(psl[:, c * 32:(c + 1) * 32], [(self.ones_f, self.adt_all[:, 2 * c, :]), (self.ones_f, self.adt_all[:, 2 * c + 1, :])]) for c in range(8)])
        acsf = self.acs_all[:].rearrange("p t h -> p (t h)")
        S.op('dve', [('PA', 1)], ['acs_all'], lambda e: e.tensor_copy(acsf, psa))
        S.op('dve', [('PA', 2)], ['lastb'], lambda e: e.tensor_copy(self.lastb[:].rearrange("p c h -> p (c h)"), psl))
        S.op('dve', ['lastb', 'acs_all'], ['dst_all'], lambda e: e.tensor_tensor(
            self.dst_all[:].rearrange("p (c s) h -> p c s h", c=8), self.lastb[:].unsqueeze(2).broadcast_to([128, 8, 2, 32]),
            self.acs_all[:].rearrange("p (c s) h -> p c s h", c=8), ALU.subtract))
        dstf = self.dst_all[:].rearrange("p t h -> p (t h)")
        S.op('act', ['dst_all'], ['dst_all'], lambda e: e.activation(dstf, dstf, AF.Exp))
        S.op('dve', ['dst_all', 'dt_all'], ['dst_all'], lambda e: e.tensor_tensor(self.dst_all[:], self.dst_all[:], self.dt_all[:], ALU.mult))
        S.op('act', ['acs_all'], ['ea_all'], lambda e: e.activation(self.ea_all[:].rearrange("p t h -> p (t h)"), acsf, AF.Exp))
        S.op('act', ['lastb'], ['cdb'], lambda e: e.activation(self.cdb[:].rearrange("p c h -> p (c h)"), self.lastb[:].rearrange("p c h -> p (c h)"), AF.Exp))
        self.dump('dt_all', self.dt_all[:].rearrange("p t h -> p (t h)"), 'dt_all')
        self.dump('acs_all', acsf, 'acs_all')

        P0 = self.PA[:, 0:512]
        P1 = self.PA[:, 512:1024]
        P2 = self.PA[:, 1024:1536]
        P3 = self.PA[:, 1536:2048]
        P4 = self.PB[:, 0:512]
        P5 = self.PB[:, 512:1024]
        P6 = self.PC[:, 0:512]
        P7 = self.PD[:, 0:512]
        uprev = None
        def load_xbc(g):
            wb = self.wssm_b[0]
            S.dma('pool', [], [('wssm', 'xbc')], lambda e, wb=wb, g=g: e.dma_start(
                out=wb[:, :, 256:768], in_=self.wssm[l, g].rearrange("p (k n) -> p k n", k=8)[:, :, 256:768], max_dma_last_dim=2048))
            nw = self.nwb[g % 2]
            S.dma('sp', [], [('nwb', g % 2)], lambda e, nw=nw, g=g: e.dma_start(out=nw[:], in_=self.normw[l][:, g * 256:(g + 1) * 256].broadcast_to([128, 256])))

        def load_z(g):
            wb = self.wssm_b[0]
            S.dma('pool', [], [('wssm', 'z')], lambda e, wb=wb, g=g: e.dma_start(
                out=wb[:, :, 0:256], in_=self.wssm[l, g].rearrange("p (k n) -> p k n", k=8)[:, :, 0:256], max_dma_last_dim=1024))
        load_xbc(0)
        load_z(0)
        prevB = None
        for g in range(8):
            wb = self.wssm_b[g % 2]
            wk = ('wssm', 0, 0)
            wkall = [('wssm', 'xbc')]
            nw = self.nwb[g % 2]
            nk = ('nwb', g % 2)
            if g > 0:
                load_xbc(g)
            hs = slice(4 * g, 4 * g + 4)

            def build_dg(e, g=g):
                ins = None
                for ci in range(4):
                    for k in range(4):
                        ins = e.tensor_scalar(self.dg[:, ci, k, :], self.ident_b, self.cwb[:, g, ci, k:k + 1], None, ALU.mult)
                return ins
            S.op('dve', ['cwb', 'cbf'], ['dg'], build_dg)
            for c in range(8):
                cols = slice(c * 256, (c + 1) * 256)
                sl = c % 2
                bcb, xs_tok, xw, btok = self.bcb2[sl], self.xs_tok2[sl], self.xw2[sl], self.btok2[sl]
                kbcb, kxs, kxw, kbt = ('bcb', sl), ('xs_tok', sl), ('xw', sl), ('btok', sl)
                xdt, cb3 = self.xdt2[sl], self.cb32[sl]
                kxdt, kcb3 = ('xdt', sl), ('cb3', sl)
                F1, F2, Bq = [], [], []
                self.rec = F1
                xkc = xk[2 * c:2 * c + 2]
                ue = self.uext[c % 2]
                uk = ('uext', c % 2)
                self.mmseq(wkall + xkc, [('PA', 0)],
                           [(P0[:, j * 256:(j + 1) * 256], [(wb[:, kc, 256 + j * 128:256 + (j + 1) * 128], self.xT[:, kc, cols]) for kc in range(8)]) for j in range(2)])
                self.mmseq(wkall + xkc, [('PA', 1)],
                           [(P1[:, j * 256:(j + 1) * 256], [(wb[:, kc, 512 + j * 128:512 + (j + 1) * 128], self.xT[:, kc, cols]) for kc in range(8)]) for j in range(2)])
                if c == 0:
                    S.op('pool', [], [uk], lambda e, bcb=bcb, xs_tok=xs_tok, xw=xw, btok=btok, xdt=xdt, cb3=cb3, ue=ue: e.memset(ue[:, :, 0:3], 0.0))
                else:
                    up = self.uext[(c - 1) % 2]
                    S.op('pool', [('uext', (c - 1) % 2)], [uk], lambda e, bcb=bcb, xs_tok=xs_tok, xw=xw, btok=btok, xdt=xdt, cb3=cb3, ue=ue, up=up: e.tensor_copy(ue[:, :, 0:3], up[:, :, 256:259]))
                S.op('act', [('PA', 0)], [uk], lambda e, bcb=bcb, xs_tok=xs_tok, xw=xw, btok=btok, xdt=xdt, cb3=cb3, ue=ue: e.activation(ue[:, 0:2, 3:259], P0.rearrange("p (j t) -> p j t", j=2), AF.Copy))
                S.op('act', [('PA', 1)], [uk], lambda e, bcb=bcb, xs_tok=xs_tok, xw=xw, btok=btok, xdt=xdt, cb3=cb3, ue=ue: e.activation(ue[:, 2:4, 3:259], P1.rearrange("p (j t) -> p j t", j=2), AF.Copy))
                self.mmseq([uk, 'dg'], [('PA', 0)],
                           [(P0[:, j * 256:(j + 1) * 256], [(self.dg[:, j, k, :], ue[:, j, k:k + 256]) for k in range(4)]) for j in range(2)])
                self.mmseq([uk, 'dg'], [('PA', 1)],
                           [(P1[:, j * 256:(j + 1) * 256], [(self.dg[:, 2 + j, k, :], ue[:, 2 + j, k:k + 256]) for k in range(4)]) for j in range(2)])
                for ci in range(4):
                    w = self.cwb[:, g, ci, :]
                    pc = (P0 if ci < 2 else P1)[:, (ci % 2) * 256:(ci % 2 + 1) * 256]
                    pk = ('PA', 0 if ci < 2 else 1)
                    S.op('act', [pk, 'cwb'], [('xh', ci)], lambda e, ci=ci, w=w, pc=pc: e.activation(self.xh[:, ci, :], pc, AF.Identity, bias=w[:, 4:5]))
                    S.op('act', [pk, 'cwb'], [('tnh', ci)], lambda e, ci=ci, w=w, pc=pc: e.activation(self.tnh[:, ci, :], pc, AF.Tanh, bias=w[:, 4:5]))
                S.op('dve', [('tnh', 0), ('tnh', 1), ('xh', 0), ('xh', 1)], [('xh', 0), ('xh', 1)], lambda e, bcb=bcb, xs_tok=xs_tok, xw=xw, btok=btok, xdt=xdt, cb3=cb3: e.scalar_tensor_tensor(self.xsf[:], in0=self.tnh[:, 0:2, :], scalar=1.0, in1=self.xh[:, 0:2, :], op0=ALU.add, op1=ALU.mult))
                S.op('dve', [('tnh', 2), ('tnh', 3), ('xh', 2), ('xh', 3)], [kbcb], lambda e, bcb=bcb, xs_tok=xs_tok, xw=xw, btok=btok, xdt=xdt, cb3=cb3: e.scalar_tensor_tensor(bcb[:], in0=self.tnh[:, 2:4, :], scalar=1.0, in1=self.xh[:, 2:4, :], op0=ALU.add, op1=ALU.mult))
                S.op('dve', [('tnh', 2), ('xh', 2)], ['bcf'], lambda e, bcb=bcb, xs_tok=xs_tok, xw=xw, btok=btok, xdt=xdt, cb3=cb3: e.scalar_tensor_tensor(self.bcf[:], in0=self.tnh[:, 2, :], scalar=1.0, in1=self.xh[:, 2, :], op0=ALU.add, op1=ALU.mult))
                self.tps([('xh', 0), ('xh', 1), 'cst'], [('PA', 0)],
                         [(P0[:, si * 256 + j * 128:si * 256 + (j + 1) * 128], self.xsf[:, j, si * 128:(si + 1) * 128]) for si in range(2) for j in range(2)], self.ident_f)
                self.tps(['bcf', 'cst'], [('PA', 1)],
                         [(P1[:, si * 128:(si + 1) * 128], self.bcf[:, si * 128:(si + 1) * 128]) for si in range(2)], self.ident_f)
                P0v = P0.rearrange("p (s h d) -> p s h d", s=2, h=4)
                def evac_xs(e, xs_tok=xs_tok, xdt=xdt, xw=xw, c=c, hs=hs):
                    e.tensor_copy(xs_tok[:].rearrange("p s c -> p (s c)"), P0)
                    e.tensor_tensor(xdt[:].rearrange("p s (h d) -> p s h d", h=4), P0v,
                                    self.dt_all[:, 2 * c:2 * c + 2, hs].unsqueeze(3).broadcast_to([128, 2, 4, 64]), ALU.mult)
                    return e.tensor_tensor(xw[:].rearrange("p s (h d) -> p s h d", h=4), P0v,
                                           self.dst_all[:, 2 * c:2 * c + 2, hs].unsqueeze(3).broadcast_to([128, 2, 4, 64]), ALU.mult)
                S.op('dve', [('PA', 0), 'dt_all', 'dst_all'], [kxs, kxdt, kxw], evac_xs)
                S.op('act', [('PA', 1)], [kbt], lambda e, bcb=bcb, xs_tok=xs_tok, xw=xw, btok=btok, xdt=xdt, cb3=cb3: e.activation(btok[:].rearrange("p s n -> p (s n)"), P1[:, 0:256], AF.Copy))
                self.mmseq([kbcb], [('PA', 3)],
                           [(P3[:, 0:256], [(bcb[:, 0, 0:128], bcb[:, 1, :])]),
                            (P3[:, 384:512], [(bcb[:, 0, 128:256], bcb[:, 1, 128:256])])])
                def evac_cb(e, cb3=cb3):
                    e.tensor_tensor(cb3[:, 0:2, :].rearrange("p a b -> p (a b)"), P3[:, 0:256], self.cst[:, C_TRI:C_TRI + 256], ALU.mult)
                    return e.tensor_tensor(cb3[:, 2, :], P3[:, 384:512], self.tri_f, ALU.mult)
                S.op('dve', [('PA', 3), 'cst'], [kcb3], evac_cb)
                self.rec = F2
                def bc_mm(hh):
                    hg = 4 * g + hh
                    pb = (P4 if hh % 2 == 0 else P6)[:, 0:256]
                    pbk = ('PB', 0, 0) if hh % 2 == 0 else 'PC'
                    a0 = self.adt_all[:, 2 * c, hg:hg + 1].broadcast_to([128, 128])
                    a1 = self.adt_all[:, 2 * c + 1, hg:hg + 1].broadcast_to([128, 128])
                    self.mmseq(['adt_all', 'cst'], [pbk],
                               [(pb[:, 0:128], [(a0, self.tri_f)]),
                                (pb[:, 128:256], [(a0, self.ones_f), (a1, self.tri_f)])])
                    return pb, pbk
                nxt = bc_mm(0)
                for hh in range(4):
                    hg = 4 * g + hh
                    pb, pbk = nxt
                    if hh + 1 < 4:
                        nxt = bc_mm(hh + 1)
                    dtm = self.dtmp[hh % 2]
                    dk = ('dtmp', hh % 2)
                    mt = self.mt[hh % 2]
                    mk = ('mt', hh % 2)
                    ac0 = self.acs_all[:, 2 * c, hg:hg + 1]
                    ac1 = self.acs_all[:, 2 * c + 1, hg:hg + 1]

                    def dec(e, dtm=dtm, pb=pb, ac0=ac0, ac1=ac1):
                        e.tensor_scalar(dtm[:, 0:2, :].rearrange("p a b -> p (a b)"), pb[:, 0:256], ac0, 0.0, ALU.subtract, ALU.min)
                        return e.tensor_scalar(dtm[:, 2, :], pb[:, 128:256], ac1, 0.0, ALU.subtract, ALU.min)
                    S.op('dve', [pbk, 'acs_all'], [dk], dec)
                    S.op('act', [dk], [dk], lambda e, dtm=dtm: e.activation(dtm[:].rearrange("p a b -> p (a b)"), dtm[:].rearrange("p a b -> p (a b)"), AF.Exp))
                    S.op('dve', [dk, kcb3], [mk], lambda e, dtm=dtm, mt=mt, cb3=cb3: e.tensor_tensor(mt[:], dtm[:], cb3[:], ALU.mult))
                    self.mmseq([mk, kxdt], [('PB', 1, hh)],
                               [(P5[:, hh * 64:(hh + 1) * 64], [(mt[:, 0, :], xdt[:, 0, hh * 64:(hh + 1) * 64])]),
                                (P5[:, 256 + hh * 64:256 + (hh + 1) * 64], [(mt[:, 1, :], xdt[:, 0, hh * 64:(hh + 1) * 64]), (mt[:, 2, :], xdt[:, 1, hh * 64:(hh + 1) * 64])])])
                self.rec = Bq
                self.mmseq([('wssm', 'z')] + xkc, [('PA', 2)],
                           [(P2[:, si * 256:(si + 1) * 256], [(self.xT[:, kc, c * 256 + si * 128:c * 256 + (si + 1) * 128], wb[:, kc, 0:256]) for kc in range(8)]) for si in range(2)])
                S.op('act', [('PA', 2)], ['tz'], lambda e, bcb=bcb, xs_tok=xs_tok, xw=xw, btok=btok, xdt=xdt, cb3=cb3: e.activation(self.tz[:].rearrange("p s c -> p (s c)"), P2, AF.Tanh, scale=0.5))
                S.op('dve', ['tz', ('PA', 2)], ['tz'], lambda e, bcb=bcb, xs_tok=xs_tok, xw=xw, btok=btok, xdt=xdt, cb3=cb3: e.scalar_tensor_tensor(self.zs[:].rearrange("p s c -> p (s c)"), in0=self.tz[:].rearrange("p s c -> p (s c)"), scalar=1.0, in1=P2, op0=ALU.add, op1=ALU.mult))
                p5k = [('PB', 1, hh) for hh in range(4)]
                if c > 0:
                    self.mmseq([kbcb, 'prev_b'], [('PA', 2)],
                               [(P2[:, si * 256:(si + 1) * 256], [(bcb[:, 1, si * 128:(si + 1) * 128], self.prev_b[:])]) for si in range(2)])
                if c < 7:
                    self.mm([kbt, kxw], [('PB', 0, 0)], P4[:, 0:256], [(btok[:, si, :], xw[:, si, :]) for si in range(2)])
                    if c == 0:
                        S.op('act', [('PB', 0, 0)], ['prev_f'], lambda e, bcb=bcb, xs_tok=xs_tok, xw=xw, btok=btok, xdt=xdt, cb3=cb3: e.activation(self.prev_f[:], P4[:, 0:256], AF.Copy))
                    else:
                        S.op('dve', ['prev_f', 'cdb'], ['prev_f'], lambda e, bcb=bcb, xs_tok=xs_tok, xw=xw, btok=btok, xdt=xdt, cb3=cb3, c=c, hs=hs: e.tensor_tensor(
                            self.prev_f[:].rearrange("p (h d) -> p h d", h=4), self.prev_f[:].rearrange("p (h d) -> p h d", h=4),
                            self.cdb[:, c, hs].unsqueeze(2).broadcast_to([128, 4, 64]), ALU.mult))
                        S.op('dve', ['prev_f', ('PB', 0, 0)], ['prev_f'], lambda e, bcb=bcb, xs_tok=xs_tok, xw=xw, btok=btok, xdt=xdt, cb3=cb3: e.tensor_tensor(self.prev_f[:], self.prev_f[:], P4[:, 0:256], ALU.add))
                    S.op('act', ['prev_f'], ['prev_b'], lambda e, bcb=bcb, xs_tok=xs_tok, xw=xw, btok=btok, xdt=xdt, cb3=cb3: e.activation(self.prev_b[:], self.prev_f[:], AF.Copy))
                S.op('pool', [kxs, 'smallb'], ['yv'], lambda e, bcb=bcb, xs_tok=xs_tok, xw=xw, btok=btok, xdt=xdt, cb3=cb3, hs=hs: e.tensor_tensor(
                    self.yv[:].rearrange("p s (h d) -> p s h d", h=4), xs_tok[:].rearrange("p s (h d) -> p s h d", h=4),
                    dsk[:, hs].unsqueeze(1).unsqueeze(3).broadcast_to([128, 2, 4, 64]), ALU.mult))
                S.op('dve', p5k + ['yv'], ['yv'], lambda e, bcb=bcb, xs_tok=xs_tok, xw=xw, btok=btok, xdt=xdt, cb3=cb3: e.tensor_tensor(self.yv[:].rearrange("p s c -> p (s c)"), P5, self.yv[:].rearrange("p s c -> p (s c)"), ALU.add))
                if c > 0:
                    def yoff(e, c=c, g=g):
                        ins = None
                        for si in range(2):
                            for hh in range(4):
                                o = self.yv[:, si, hh * 64:(hh + 1) * 64]
                                ins = e.scalar_tensor_tensor(o, in0=P2[:, si * 256 + hh * 64:si * 256 + (hh + 1) * 64],
                                                             scalar=self.ea_all[:, 2 * c + si, 4 * g + hh:4 * g + hh + 1], in1=o, op0=ALU.mult, op1=ALU.add)
                        return ins
                    S.op('dve', [('PA', 2), 'ea_all', 'yv'], ['yv'], yoff)
                S.op('dve', ['yv', 'tz'], ['yv'], lambda e, bcb=bcb, xs_tok=xs_tok, xw=xw, btok=btok, xdt=xdt, cb3=cb3: e.tensor_tensor(self.h2[:], self.yv[:], self.zs[:], ALU.mult))
                for si in range(2):
                    S.op('act', ['yv'], ['junk', ('ss', si)], lambda e, bcb=bcb, xs_tok=xs_tok, xw=xw, btok=btok, xdt=xdt, cb3=cb3, si=si: e.activation(self.junk[:], self.h2[:, si, :], AF.Square, accum_out=self.ss[:, si:si + 1]))
                S.op('dve', [('ss', 0), ('ss', 1)], ['rstd'], lambda e, bcb=bcb, xs_tok=xs_tok, xw=xw, btok=btok, xdt=xdt, cb3=cb3: e.tensor_scalar(self.rstd[:], self.ss[:], 1.0 / 256, 4 * RMS_EPS, ALU.mult, ALU.add))
                S.op('pool', ['rstd', 'nh'], ['rstd'], lambda e, bcb=bcb, xs_tok=xs_tok, xw=xw, btok=btok, xdt=xdt, cb3=cb3: e.tensor_tensor(self.rstd[:], self.rstd[:], self.nh[:, 0:2], ALU.pow))
                for si in range(2):
                    S.op('dve', ['yv', 'rstd', nk], ['yv'], lambda e, bcb=bcb, xs_tok=xs_tok, xw=xw, btok=btok, xdt=xdt, cb3=cb3, si=si, nw=nw: e.scalar_tensor_tensor(
                        self.yn[:, si, :], in0=self.h2[:, si, :], scalar=self.rstd[:, si:si + 1], in1=nw[:], op0=ALU.mult, op1=ALU.mult))
                self.tps(['yv', 'cst'], ['PD'],
                         [(P7[:, j * 256 + si * 128:j * 256 + (si + 1) * 128], self.yn[:, si, j * 128:(j + 1) * 128]) for j in range(2) for si in range(2)], self.ident_f)
                S.op('act', ['PD'], [('yT', g, c)], lambda e, bcb=bcb, xs_tok=xs_tok, xw=xw, btok=btok, xdt=xdt, cb3=cb3, g=g, cols=cols: e.activation(self.yT[:, 2 * g:2 * g + 2, cols], P7.rearrange("p (j t) -> p j t", j=2), AF.Copy))
                self.rec = None
                X = F2 + Bq
                if prevB is None:
                    for t in F1:
                        t()
                else:
                    n1, n2 = len(F1), len(prevB)
                    i1 = i2 = 0
                    while i1 < n1 or i2 < n2:
                        if i2 >= n2 or (i1 < n1 and i1 * n2 <= i2 * n1):
                            F1[i1]()
                            i1 += 1
                        else:
                            prevB[i2]()
                            i2 += 1
                prevB = X
                if c == 0 and g > 0:
                    load_z(g)
        for t in prevB:
            t()
        if 'yT' in self.dbg:
            if True:
                self.dbgf = self.sb("dbgf", [128, 512])
            for kc in range(16):
                for q4 in range(4):
                    S.op('dve', [('yT', g, c) for g in range(8) for c in range(8)] + [('dbg', 'yT')], ['dbgf'], lambda e, kc=kc, q4=q4: e.tensor_copy(self.dbgf[:, 0:512], self.yT[:, kc, q4 * 512:(q4 + 1) * 512]))
                    S.dma('sp', ['dbgf'], [('dbg', 'yT')], lambda e, kc=kc, q4=q4: e.dma_start(out=self.dbg['yT'][kc * 128:(kc + 1) * 128, q4 * 512:(q4 + 1) * 512], in_=self.dbgf[:, 0:512]))
            self.final_keys.append(('dbg', 'yT'))

    def merge(self, l):
        self.psum_epoch()
        S, sb = self.S, self.sb
        self.phase([self.R_W])
        if True:
            self.wc1_b = [sb("wc1b%d" % i, [128, 5120], BF16) for i in range(2)]
            self.ta = [sb("ta%d" % i, [128, 512]) for i in range(2)]
            self.tb = [sb("tb%d" % i, [128, 512]) for i in range(2)]
        mT = self.B3
        xk = self.xT_keys()
        attk = [('attT', hp, h) for hp in range(8) for h in range(2)]
        yk = [('yT', g, c) for g in range(8) for c in range(8)]
        it = 0
        def load_c(fc):
            wb = self.wc1_b[fc % 2]
            S.dma('pool', [], [('wc1', fc % 2, 0)], lambda e, wb=wb, fc=fc: e.dma_start(
                out=wb[:].rearrange("p (a b) -> p a b", a=5), in_=self.wc1[l, fc].rearrange("p (a b) -> p a b", a=5), max_dma_last_dim=4096))
        load_c(0)
        for fc in range(8):
            wb = self.wc1_b[fc % 2]
            wks = [('wc1', fc % 2, 0)]
            if fc + 1 < 8:
                load_c(fc + 1)
            wg = wb[:, 0:2048].rearrange("p (k n) -> p k n", k=8)
            wa = wb[:, 2048:3072].rearrange("p (k n) -> p k n", k=8)
            ws = wb[:, 3072:5120].rearrange("p (k n) -> p k n", k=16)
            for tc in range(4):
                cols = slice(tc * 512, (tc + 1) * 512)
                if it % 2 == 0:
                    pga, pgs, ppa, pps = (self.PA[:, i * 512:(i + 1) * 512] for i in range(4))
                    bk = [('PA', 0), ('PA', 1), ('PA', 2), ('PA', 3)]
                else:
                    pga, pgs, ppa, pps = self.PB[:, 0:512], self.PB[:, 512:1024], self.PC[:, 0:512], self.PD[:, 0:512]
                    bk = [('PB', 0), ('PB', 1), 'PC', 'PD']
                self.mm(wks + xk[tc * 4:(tc + 1) * 4], [bk[0]], pga, [(wg[:, kc, 0:128], self.xT[:, kc, cols]) for kc in range(8)])
                self.mm(wks + xk[tc * 4:(tc + 1) * 4], [bk[1]], pgs, [(wg[:, kc, 128:256], self.xT[:, kc, cols]) for kc in range(8)])
                self.mm(wks + attk, [bk[2]], ppa, [(wa[:, kc, :], self.B1[:, kc, cols]) for kc in range(8)])
                self.mm(wks + yk, [bk[3]], pps, [(ws[:, kc, :], self.yT[:, kc, cols]) for kc in range(16)])
                ta, tb = self.ta[it % 2], self.tb[it % 2]
                tak, tbk = ('ta', it % 2), ('tb', it % 2)
                it += 1
                S.op('act', [bk[0]], [tak], lambda e, ta=ta, pga=pga: e.activation(ta[:], pga, AF.Tanh, scale=0.5))
                S.op('act', [bk[1]], [tbk], lambda e, tb=tb, pgs=pgs: e.activation(tb[:], pgs, AF.Tanh, scale=0.5))
                S.op('dve', [tak, bk[2]], [tak], lambda e, ta=ta, ppa=ppa: e.scalar_tensor_tensor(ta[:], in0=ta[:], scalar=1.0, in1=ppa, op0=ALU.add, op1=ALU.mult))
                S.op('dve', [tbk, bk[3]], [tbk], lambda e, tb=tb, pps=pps: e.scalar_tensor_tensor(tb[:], in0=tb[:], scalar=1.0, in1=pps, op0=ALU.add, op1=ALU.mult))
                S.op('pool', [tak, tbk], [('mergedT', fc, tc)], lambda e, ta=ta, tb=tb, fc=fc, cols=cols: e.tensor_tensor(mT[:, fc, cols], ta[:], tb[:], ALU.add))

    def layernorm(self, t, tk, g_b, b_b, gk, i):
        S = self.S
        st = self.ln_st[i % 2]
        mv = self.ln_mv[i % 2]
        sk = ('lnst', i % 2)

        def stats(e):
            e.bn_stats(st[:, 0, :], t[:, 0:512])
            return e.bn_stats(st[:, 1, :], t[:, 512:1024])
        S.op('dve', [tk], [sk], stats)
        S.op('dve', [sk], [sk], lambda e: e.bn_aggr(mv[:, 0:2], st[:].rearrange("p a b -> p (a b)")))
        S.op('dve', [sk], [sk], lambda e: e.tensor_scalar(mv[:, 2:3], mv[:, 1:2], LN_EPS, None, ALU.add))
        S.op('pool', [sk, 'nh'], [sk], lambda e: e.tensor_tensor(mv[:, 3:4], mv[:, 2:3], self.nh[:, 0:1], ALU.pow))
        S.op('dve', [tk, sk], [tk], lambda e: e.tensor_scalar(t[:], t[:], mv[:, 0:1], mv[:, 3:4], ALU.subtract, ALU.mult))
        S.op('pool', [tk, gk], [tk], lambda e: e.tensor_tensor(t[:], t[:], g_b, ALU.mult))
        S.op('dve', [tk, gk], [tk], lambda e: e.tensor_tensor(t[:], t[:], b_b, ALU.add))

    def mix_ln1(self, l):
        self.psum_epoch()
        S, sb = self.S, self.sb
        self.phase([self.R_W, self.R_B2])
        if True:
            self.wout_b = sb("woutb", [128, 8, 1024], BF16)
            self.lnpb = sb("lnpb", [128, 4096])
            self.res = [sb("res%d" % i, [128, D]) for i in range(2)]
            self.ln_st = [sb("lnst%d" % i, [128, 2, 6]) for i in range(2)]
            self.ln_mv = [sb("lnmv%d" % i, [128, 4]) for i in range(2)]
        S.dma('pool', [], [('wout', 0)], lambda e: e.dma_start(
            out=self.wout_b[:], in_=self.wout[l].rearrange("p (k n) -> p k n", k=8), max_dma_last_dim=4096))
        lnpb1 = self.lnpb
        S.dma('sp', [], ['lnpb'], lambda e: e.dma_start(out=lnpb1[:], in_=self.lnp[l].broadcast_to([128, 4096])))
        src = self.x_in if l == 0 else self.x1
        srck = 'x_in' if l == 0 else 'x1'
        mk = [('mergedT', fc, tc) for fc in range(8) for tc in range(4)]
        wk = [('wout', 0)]
        for tt in range(16):
            r = self.res[tt % 2]
            rk = ('res', tt % 2)
            S.dma('sp', [(srck, tt)], [rk], lambda e, r=r, tt=tt: e.dma_start(out=r[:], in_=src[tt * 128:(tt + 1) * 128, :]))
            S.op('act', [rk], [rk], lambda e, r=r: e.activation(r[:], r[:], AF.Copy, scale=float(ALPHA)))
            if tt % 2 == 0:
                pm = self.PB[:, 0:1024]
                rkeys = [('PB', 0), ('PB', 1)]
            else:
                pm = self.PA[:, 0:1024]
                rkeys = [('PA', 0), ('PA', 1)]
            self.mmseq(mk + wk, rkeys,
                       [(pm[:, half * 512:(half + 1) * 512], [(self.B3[:, kc, tt * 128:(tt + 1) * 128], self.wout_b[:, kc, half * 512:(half + 1) * 512]) for kc in range(8)]) for half in range(2)])
            S.op('dve', rkeys + [rk], [rk], lambda e, r=r, pm=pm: e.scalar_tensor_tensor(r[:], in0=pm, scalar=0.5, in1=r[:], op0=ALU.mult, op1=ALU.add))
            self.layernorm(r, rk, self.lnpb[:, 0:1024], self.lnpb[:, 1024:2048], 'lnpb', tt)
            S.dma('sp', [rk], [('hres', tt)], lambda e, r=r, tt=tt: e.dma_start(out=self.hres[tt * 128:(tt + 1) * 128, :], in_=r[:]))
            self.to_featmajor2(r, rk, self.B1, 'hT', tt)
        if 'hT' in self.dbg:
            if True:
                self.dbgf = self.sb("dbgf", [128, 512])
            for kc in range(8):
                for q4 in range(4):
                    S.op('dve', [('hT', tt) for tt in range(16)] + [('dbg', 'hT')], ['dbgf'], lambda e, kc=kc, q4=q4: e.tensor_copy(self.dbgf[:, 0:512], self.B1[:, kc, q4 * 512:(q4 + 1) * 512]))
                    S.dma('sp', ['dbgf'], [('dbg', 'hT')], lambda e, kc=kc, q4=q4: e.dma_start(out=self.dbg['hT'][kc * 128:(kc + 1) * 128, q4 * 512:(q4 + 1) * 512], in_=self.dbgf[:, 0:512]))
            self.final_keys.append(('dbg', 'hT'))

    def to_featmajor2(self, tile, tkey, dstT, dkey, tt):
        S = self.S
        for half in range(2):
            ps = self.PC[:, 0:512] if half == 0 else self.PD[:, 0:512]
            pk = 'PC' if half == 0 else 'PD'
            self.tps([tkey, 'cst'], [pk], [(ps[:, j * 128:(j + 1) * 128], tile[:, (half * 4 + j) * 128:(half * 4 + j + 1) * 128]) for j in range(4)], self.ident_f)
            S.op('act', [pk], [(dkey, tt)], lambda e, ps=ps, half=half: e.activation(
                dstT[:, half * 4:(half + 1) * 4, tt * 128:(tt + 1) * 128], ps.rearrange("p (c t) -> p c t", c=4), AF.Copy))

    def ffn(self, l):
        self.psum_epoch()
        S, sb = self.S, self.sb
        self.phase([self.R_W, (self.R_B3[0] + 4096, self.R_B3[1])])
        if True:
            self.wup_b = [sb("wupb%d" % i, [128, 8, 512], BF16) for i in range(2)]
            self.wdn_b = [sb("wdnb%d" % i, [128, 4, 1024], BF16) for i in range(2)]
            self.rl = [sb("rl%d" % i, [128, 512]) for i in range(2)]
            self.ln_st = [sb("lnst%d" % i, [128, 2, 6]) for i in range(2)]
            self.ln_mv = [sb("lnmv%d" % i, [128, 4]) for i in range(2)]
        uT = self.B3
        hk = [('hT', tt) for tt in range(16)]
        it = 0
        def load_f(gi):
            wu = self.wup_b[gi % 2]
            wd = self.wdn_b[gi % 2]
            wuk, wdk = ('wup', gi % 2), ('wdn', gi % 2)
            S.dma('pool', [], [(wuk, 0)], lambda e, wu=wu, gi=gi: e.dma_start(
                out=wu[:], in_=self.wup[l, gi].rearrange("p (k n) -> p k n", k=8), max_dma_last_dim=4096))
            S.dma('pool', [], [(wdk, 0)], lambda e, wd=wd, gi=gi: e.dma_start(
                out=wd[:], in_=self.wdn[l, gi].rearrange("p (k n) -> p k n", k=4), max_dma_last_dim=4096))
        load_f(0)
        for gi in range(8):
            wu = self.wup_b[gi % 2]
            wd = self.wdn_b[gi % 2]
            wuk, wdk = ('wup', gi % 2), ('wdn', gi % 2)
            if gi + 1 < 8:
                load_f(gi + 1)
            wuks = [(wuk, 0)]
            wdks = [(wdk, 0)]
            for j in range(4):
                for tc in range(4):
                    bank = it % 4
                    ps = [self.PA[:, 0:512], self.PA[:, 512:1024], self.PC[:, 0:512], self.PD[:, 0:512]][bank]
                    pk = [('PA', 0), ('PA', 1), 'PC', 'PD'][bank]
                    cols = slice(tc * 512, (tc + 1) * 512)
                    self.mm(wuks + hk[tc * 4:(tc + 1) * 4], [pk], ps, [(wu[:, kc, j * 128:(j + 1) * 128], self.B1[:, kc, cols]) for kc in range(8)])
                    rl = self.rl[it % 2]
                    rlk = ('rl', it % 2)
                    S.op('act', [pk], [rlk], lambda e, rl=rl, ps=ps: e.activation(rl[:], ps, AF.Relu))
                    eng = 'dve'
                    S.op(eng, [rlk], [('uT', j)], lambda e, rl=rl, j=j, cols=cols: e.tensor_tensor(uT[:, j, cols], rl[:], rl[:], ALU.mult))
                    it += 1
            for tt in range(16):
                if tt % 2 == 0:
                    pm = self.PB[:, 0:1024]
                    rkeys = [('PB', 0), ('PB', 1)]
                else:
                    pm = self.PA[:, 1024:2048]
                    rkeys = [('PA', 2), ('PA', 3)]
                self.mmseq([('uT', j) for j in range(4)] + wdks, rkeys,
                           [(pm[:, half * 512:(half + 1) * 512], [(uT[:, j, tt * 128:(tt + 1) * 128], wd[:, j, half * 512:(half + 1) * 512]) for j in range(4)]) for half in range(2)])
                if gi == 0:
                    S.op('act', rkeys, [('acc', tt)], lambda e, tt=tt, pm=pm: e.activation(self.acc[:, tt, :], pm, AF.Copy))
                else:
                    S.op('dve', rkeys + [('acc', tt)], [('acc', tt)], lambda e, tt=tt, pm=pm: e.tensor_tensor(self.acc[:, tt, :], pm, self.acc[:, tt, :], ALU.add))
        S.barrier()
        self.res = [self.wup_b[i][:, 0:4, :].rearrange("p a b -> p (a b)").bitcast(F32) for i in range(2)]
        self.lnpb = self.wdn_b[0][:].rearrange("p a b -> p (a b)").bitcast(F32)
        lnpb2 = self.lnpb
        S.dma('sp', [], ['lnpb'], lambda e: e.dma_start(out=lnpb2, in_=self.lnp[l][:, 2048:4096].broadcast_to([128, 2048])))
        dst = self.out if l == DEPTH - 1 else self.x1
        dstk = 'out' if l == DEPTH - 1 else 'x1'
        for tt in range(16):
            r = self.res[tt % 2]
            rk = ('res', tt % 2)
            S.dma('sp', [('hres', tt)], [rk], lambda e, r=r, tt=tt: e.dma_start(out=r[:], in_=self.hres[tt * 128:(tt + 1) * 128, :]))
            S.op('dve', [rk, ('acc', tt)], [rk], lambda e, r=r, tt=tt: e.scalar_tensor_tensor(r[:], in0=r[:], scalar=float(ALPHA), in1=self.acc[:, tt, :], op0=ALU.mult, op1=ALU.add))
            self.layernorm(r, rk, self.lnpb[:, 0:1024], self.lnpb[:, 1024:2048], 'lnpb', tt)
            S.dma('sp', [rk], [(dstk, tt)], lambda e, r=r, tt=tt: e.dma_start(out=dst[tt * 128:(tt + 1) * 128, :], in_=r[:]))
            if l == DEPTH - 1:
                self.final_keys.append((dstk, tt))
            else:
                self.to_featmajor2(r, rk, self.xT, 'xT', tt)


def sel_max(e, b):
    ins = None
    for t in range(8):
        ins = e.max(b.m8[:, t, :], b.gm[:, t * 8:(t + 1) * 8])
    return ins


def sel_lt(e, b):
    ins = None
    for t in range(8):
        ins = e.tensor_scalar(b.lt[:, t * 8:(t + 1) * 8], b.gm[:, t * 8:(t + 1) * 8], b.m8[:, t, 2:3], NEG, ALU.is_lt, ALU.mult)
    return ins


_CACHE = {}


def kernel(**inputs):
    inp = {k: np.asarray(v) for k, v in inputs.items()}
    w = prep_weights(inp)
    consts, e8 = make_consts()
    x = np.ascontiguousarray(inp['x'], dtype=np.float32)
    pos = np.ascontiguousarray(inp['positions'], dtype=np.int32)
    nc = Builder().build()
    in_maps = []
    for b in range(NCORES):
        m = {"x": x[b], "pos": pos[b:b + 1], "consts": consts, "e8": e8}
        m.update(w)
        in_maps.append(m)
    res = run_bass_kernel_spmd(nc, in_maps, core_ids=list(range(NCORES)))
    out = np.stack([np.asarray(r["out"], dtype=np.float32) for r in res.results], axis=0)
    return out
```
